# Optimizing a Trainium2 kernel written in Bass

```python
import jax, jax.numpy as jnp
from jax import lax
import numpy as np

D_MODEL = 1024
BATCH = 16
SEQ = 256
DEPTH = 2
DEC_BATCH = 8
DEC_SEQ = 2048
PAST_LEN = 512

GRID_W = 64
EPS = 1e-6
ROPE_BASE = 10000.0
N_BRANCH = 4
BRANCH_WIDTH = D_MODEL // 4
N_MOD = 9
D_FF = 2816
S5_WIDTH = BRANCH_WIDTH
S5_GROUP = 16
S5_GROUPS = S5_WIDTH // S5_GROUP
S5_STATE = 64
RET_HEADS = 4
RET_DIM = BRANCH_WIDTH // RET_HEADS
RET_WIDTH = RET_HEADS * RET_DIM
RET_CHUNK = 128
CONV_WIDTH = BRANCH_WIDTH
CONV_K = 3
MLA_HEADS = 4
MLA_Q_LORA = 192
MLA_KV_LORA = 128
MLA_NOPE = 64
MLA_ROPE = 32
MLA_V = BRANCH_WIDTH // MLA_HEADS
Q_BLOCK = 128
IN_SIZES = (S5_WIDTH, RET_WIDTH, RET_WIDTH, RET_WIDTH, RET_WIDTH, CONV_WIDTH, CONV_WIDTH, CONV_WIDTH, MLA_Q_LORA, MLA_KV_LORA, MLA_ROPE)
IN_COLS = sum(IN_SIZES)

kernel_name = 'hybrid_diffusion_prefix_trunk_step'


def rmsnorm(x, g):
    xf = x.astype(jnp.float32)
    y = xf * lax.rsqrt(jnp.mean(xf * xf, axis=-1, keepdims=True) + EPS)
    return (y * g.astype(jnp.float32)).astype(x.dtype)


def swiglu(h, w1, w3, w2):
    return jnp.dot(jax.nn.silu(jnp.dot(h, w1)) * jnp.dot(h, w3), w2)


def axial_angles(T, dim):
    rows = T // GRID_W
    row = jnp.repeat(jnp.arange(rows, dtype=jnp.float32), GRID_W)
    col = jnp.tile(jnp.arange(GRID_W, dtype=jnp.float32), rows)
    quarter = dim // 4
    inv = ROPE_BASE ** (-jnp.arange(quarter, dtype=jnp.float32) / quarter)
    ang = jnp.concatenate([row[:, None] * inv, col[:, None] * inv], axis=-1)
    return jnp.cos(ang), jnp.sin(ang)


def apply_rope(x, cos, sin):
    half = x.shape[-1] // 2
    c = cos[None, :, None, :]
    s = sin[None, :, None, :]
    x1 = x[..., :half].astype(jnp.float32)
    x2 = x[..., half:].astype(jnp.float32)
    return jnp.concatenate([x1 * c - x2 * s, x1 * s + x2 * c], axis=-1).astype(x.dtype)


def _complex_affine_combine(e1, e2):
    a1r, a1i, b1r, b1i = e1
    a2r, a2i, b2r, b2i = e2
    ar = a2r * a1r - a2i * a1i
    ai = a2r * a1i + a2i * a1r
    br = a2r * b1r - a2i * b1i + b2r
    bi = a2r * b1i + a2i * b1r + b2i
    return ar, ai, br, bi


def s5_discretize(lam_re, lam_im, log_dt, b_re, b_im):
    lam_re = lam_re.astype(jnp.float32)
    lam_im = lam_im.astype(jnp.float32)
    dt = jnp.exp(log_dt.astype(jnp.float32))[:, None]
    mag = jnp.exp(lam_re * dt)
    ab_re = mag * jnp.cos(lam_im * dt)
    ab_im = mag * jnp.sin(lam_im * dt)
    den = lam_re * lam_re + lam_im * lam_im
    f_re = ((ab_re - 1.0) * lam_re + ab_im * lam_im) / den
    f_im = (ab_im * lam_re - (ab_re - 1.0) * lam_im) / den
    b_re = b_re.astype(jnp.float32)
    b_im = b_im.astype(jnp.float32)
    bb_re = f_re[..., None] * b_re - f_im[..., None] * b_im
    bb_im = f_re[..., None] * b_im + f_im[..., None] * b_re
    return ab_re, ab_im, bb_re, bb_im


def s5_scan_dir(u, lam_re, lam_im, log_dt, b_re, b_im, s0_re, s0_im, reverse):
    ab_re, ab_im, bb_re, bb_im = s5_discretize(lam_re, lam_im, log_dt, b_re, b_im)
    bu_re = jnp.einsum('btgh,gph->btgp', u, bb_re)
    bu_im = jnp.einsum('btgh,gph->btgp', u, bb_im)
    t0 = -1 if reverse else 0
    bu_re = bu_re.at[:, t0].add(ab_re * s0_re - ab_im * s0_im)
    bu_im = bu_im.at[:, t0].add(ab_re * s0_im + ab_im * s0_re)
    a_re = jnp.broadcast_to(ab_re, bu_re.shape)
    a_im = jnp.broadcast_to(ab_im, bu_im.shape)
    _, _, s_re, s_im = lax.associative_scan(_complex_affine_combine, (a_re, a_im, bu_re, bu_im), reverse=reverse, axis=1)
    return s_re, s_im


def s5_mixer(u, s0, lam_re, lam_im, log_dt, b_re, b_im, c_re, c_im, d_skip, w_glu):
    B_, T = u.shape[:2]
    uf = u.astype(jnp.float32).reshape(B_, T, S5_GROUPS, S5_GROUP)
    s0 = s0.astype(jnp.float32)
    tot_re, tot_im, finals = 0.0, 0.0, []
    for d, rev in enumerate((False, True)):
        s_re, s_im = s5_scan_dir(uf, lam_re[d], lam_im[d], log_dt[d], b_re[d], b_im[d], s0[:, d, ..., 0], s0[:, d, ..., 1], rev)
        tot_re = tot_re + s_re
        tot_im = tot_im + s_im
        end = 0 if rev else -1
        finals.append(jnp.stack([s_re[:, end], s_im[:, end]], axis=-1))
    y = jnp.einsum('ghp,btgp->btgh', c_re.astype(jnp.float32), tot_re) - jnp.einsum('ghp,btgp->btgh', c_im.astype(jnp.float32), tot_im)
    y = y.reshape(B_, T, S5_WIDTH) + d_skip.astype(jnp.float32) * uf.reshape(B_, T, S5_WIDTH)
    z = jax.nn.gelu(y)
    out = z * jax.nn.sigmoid(jnp.dot(z, w_glu.astype(jnp.float32)))
    return out.astype(u.dtype), jnp.stack(finals, axis=1)


def retention_dir(q, k, v, log_g, s0, strict):
    B_, T, H, d = q.shape
    n = T // RET_CHUNK

    def chunks(a):
        return a.reshape(B_, n, RET_CHUNK, H, d).transpose(1, 0, 3, 2, 4)

    idx = jnp.arange(RET_CHUNK, dtype=jnp.float32)
    diff = idx[:, None] - idx[None, :]
    mask = (diff > 0) if strict else (diff >= 0)
    dmat = jnp.where(mask, jnp.exp(log_g[:, None, None] * jnp.maximum(diff, 0.0)), 0.0)
    q_dec = jnp.exp(log_g[:, None] * (idx + 1.0))[None, :, :, None]
    k_dec = jnp.exp(log_g[:, None] * (RET_CHUNK - 1.0 - idx))[None, :, :, None]
    c_dec = jnp.exp(log_g * RET_CHUNK)[None, :, None, None]

    def step(s, blk):
        qb, kb, vb = blk
        att = jnp.einsum('bhid,bhjd->bhij', qb, kb) * dmat
        o = jnp.einsum('bhij,bhjd->bhid', att, vb) + jnp.einsum('bhid,bhde->bhie', qb, s) * q_dec
        s = s * c_dec + jnp.einsum('bhjd,bhje->bhde', kb * k_dec, vb)
        return s, o

    s_fin, o = lax.scan(step, s0, (chunks(q), chunks(k), chunks(v)))
    return o.transpose(1, 0, 3, 2, 4).reshape(B_, T, H, d), s_fin


def retention_mixer(rq, rk, rv, rg, s0, decay_logit, gn, rope):
    B_, T = rq.shape[:2]
    shp = (B_, T, RET_HEADS, RET_DIM)
    q = rq.reshape(shp)
    k = rk.reshape(shp)
    if rope is not None:
        q = apply_rope(q, rope[0], rope[1])
        k = apply_rope(k, rope[0], rope[1])
    q = q.astype(jnp.float32)
    k = k.astype(jnp.float32) * (RET_DIM ** -0.5)
    v = rv.reshape(shp).astype(jnp.float32)
    s0 = s0.astype(jnp.float32)
    log_g = jax.nn.log_sigmoid(decay_logit.astype(jnp.float32))
    o_f, s_f = retention_dir(q, k, v, log_g[0], s0[:, 0], False)
    o_b, s_b = retention_dir(jnp.flip(q, 1), jnp.flip(k, 1), jnp.flip(v, 1), log_g[1], s0[:, 1], True)
    o = o_f + jnp.flip(o_b, 1)
    mu = jnp.mean(o, axis=-1, keepdims=True)
    var = jnp.mean(jnp.square(o - mu), axis=-1, keepdims=True)
    o = ((o - mu) * lax.rsqrt(var + EPS)).reshape(B_, T, RET_WIDTH) * gn.astype(jnp.float32)
    out = jax.nn.silu(rg.astype(jnp.float32)) * o
    return out.astype(rg.dtype), jnp.stack([s_f, s_b], axis=1)


def short_conv_mixer(cx, cb, cc, w, b):
    z = cc * cx
    T = z.shape[1]
    pad = CONV_K // 2
    zp = jnp.pad(z, ((0, 0), (pad, pad), (0, 0)))
    y = b
    for j in range(CONV_K):
        y = y + zp[:, j:j + T] * w[j]
    return cb * y


def mla_expand(cache, w_ukv):
    B_, S = cache.shape[:2]
    kv = jnp.dot(cache[..., :MLA_KV_LORA], w_ukv).reshape(B_, S, MLA_HEADS, MLA_NOPE + MLA_V)
    return kv[..., :MLA_NOPE], kv[..., MLA_NOPE:], cache[..., MLA_KV_LORA:]


def mla_attend(q_nope, q_rope, k_nope, k_rope, v):
    B_, T = q_nope.shape[:2]
    n = T // Q_BLOCK
    scale = (MLA_NOPE + MLA_ROPE) ** -0.5

    def block(args):
        qn, qr = args
        s = jnp.einsum('bqhd,bkhd->bhqk', qn, k_nope) + jnp.einsum('bqhd,bkd->bhqk', qr, k_rope)
        p = jax.nn.softmax(s.astype(jnp.float32) * scale, axis=-1).astype(v.dtype)
        return jnp.einsum('bhqk,bkhd->bqhd', p, v)

    qn = q_nope.reshape(B_, n, Q_BLOCK, MLA_HEADS, MLA_NOPE).swapaxes(0, 1)
    qr = q_rope.reshape(B_, n, Q_BLOCK, MLA_HEADS, MLA_ROPE).swapaxes(0, 1)
    o = lax.map(block, (qn, qr))
    return o.swapaxes(0, 1).reshape(B_, T, MLA_HEADS * MLA_V)


def mla_mixer(cq, ckv, kr, q_norm, w_uq, kv_norm, w_ukv, rope, ctx_cache):
    B_, T = cq.shape[:2]
    q = jnp.dot(rmsnorm(cq, q_norm), w_uq).reshape(B_, T, MLA_HEADS, MLA_NOPE + MLA_ROPE)
    q_nope, q_rope = q[..., :MLA_NOPE], q[..., MLA_NOPE:]
    cache = jnp.concatenate([rmsnorm(ckv, kv_norm), kr], axis=-1)
    k_nope, v, k_rope = mla_expand(cache, w_ukv)
    if ctx_cache is not None:
        cos, sin = rope
        q_rope = apply_rope(q_rope, cos, sin)
        k_rope = apply_rope(k_rope[:, :, None, :], cos, sin)[:, :, 0, :]
        kn_c, v_c, kr_c = mla_expand(ctx_cache, w_ukv)
        k_nope = jnp.concatenate([k_nope, kn_c], axis=1)
        v = jnp.concatenate([v, v_c], axis=1)
        k_rope = jnp.concatenate([k_rope, kr_c], axis=1)
    return mla_attend(q_nope, q_rope, k_nope, k_rope, v), cache


def parallel_mixers(h, p, ctx):
    B_, T, _ = h.shape
    splits = np.cumsum(IN_SIZES)[:-1].tolist()
    u, rq, rk, rv, rg, cx, cb, cc, cq, ckv, kr = jnp.split(jnp.dot(h, p['w_in']), splits, axis=-1)
    if ctx is None:
        s5_0 = jnp.zeros((B_, 2, S5_GROUPS, S5_STATE, 2), jnp.float32)
        ret_0 = jnp.zeros((B_, 2, RET_HEADS, RET_DIM, RET_DIM), jnp.float32)
        mla_ctx, rope_ret, rope_mla = None, None, None
    else:
        s5_0, ret_0, mla_ctx = ctx
        rope_ret = axial_angles(T, RET_DIM)
        rope_mla = axial_angles(T, MLA_ROPE)
    y_s5, s5_fin = s5_mixer(u, s5_0, p['s5_lam_re'], p['s5_lam_im'], p['s5_log_dt'], p['s5_b_re'], p['s5_b_im'], p['s5_c_re'], p['s5_c_im'], p['s5_d'], p['s5_w_glu'])
    y_ret, ret_fin = retention_mixer(rq, rk, rv, rg, ret_0, p['ret_decay'], p['ret_gn'], rope_ret)
    y_conv = short_conv_mixer(cx, cb, cc, p['conv_w'], p['conv_b'])
    y_mla, mla_cache = mla_mixer(cq, ckv, kr, p['mla_q_norm'], p['mla_w_uq'], p['mla_kv_norm'], p['mla_w_ukv'], rope_mla, mla_ctx)
    branches = jnp.stack([y_s5, y_ret, y_conv.astype(h.dtype), y_mla.astype(h.dtype)], axis=2)
    proj = jnp.einsum('btnw,nwd->btnd', branches, p['w_branch'])
    gates = jax.nn.sigmoid(jnp.dot(h, p['w_gate']) + p['b_gate']).reshape(B_, T, N_BRANCH, D_MODEL)
    merged = jnp.einsum('btnd,btnd->btd', gates, proj)
    return jnp.dot(merged, p['w_o']), (s5_fin, ret_fin, mla_cache)


def trunk_layer(x, cond, p, ctx):
    B_ = x.shape[0]
    mod = (jnp.dot(jax.nn.silu(cond), p['w_mod']) + p['b_mod']).reshape(B_, N_MOD, 1, D_MODEL)
    sh1, sc1, g1, sh2, sc2, g2, sh3, sc3, g3 = [mod[:, i] for i in range(N_MOD)]
    h = rmsnorm(x, p['norm_pre'][0]) * (1.0 + sc1) + sh1
    f = swiglu(h, p['ffn_w1'][0], p['ffn_w3'][0], p['ffn_w2'][0])
    x = x + 0.5 * g1 * rmsnorm(f, p['norm_post'][0])
    h = rmsnorm(x, p['norm_pre'][1]) * (1.0 + sc2) + sh2
    m, state = parallel_mixers(h, p, ctx)
    x = x + g2 * rmsnorm(m, p['norm_post'][1])
    h = rmsnorm(x, p['norm_pre'][2]) * (1.0 + sc3) + sh3
    f = swiglu(h, p['ffn_w1'][1], p['ffn_w3'][1], p['ffn_w2'][1])
    x = x + 0.5 * g3 * rmsnorm(f, p['norm_post'][2])
    return x, state


def setup_inputs(seed: int = 0) -> dict:
    key = jax.random.key(seed)
    ks = iter(jax.random.split(key, 48))
    f32 = jnp.float32

    def nrm(shape, scale):
        return scale * jax.random.normal(next(ks), shape, f32)

    def gain(shape):
        return 1.0 + nrm(shape, 0.02)

    lam_im0 = jnp.pi * jnp.arange(S5_STATE, dtype=f32)
    ret_logit0 = jnp.log(2.0 ** (5.0 + jnp.arange(RET_HEADS, dtype=f32)) - 1.0)
    return {
        'x_prompt': nrm((BATCH, SEQ, D_MODEL), 1.0),
        'x_sample': nrm((DEC_BATCH, DEC_SEQ, D_MODEL), 1.0),
        'state_s5': nrm((DEC_BATCH, DEPTH, 2, S5_GROUPS, S5_STATE, 2), 0.5),
        'state_ret': nrm((DEC_BATCH, DEPTH, 2, RET_HEADS, RET_DIM, RET_DIM), 0.1),
        'cache_mla': nrm((DEC_BATCH, DEPTH, PAST_LEN, MLA_KV_LORA + MLA_ROPE), 1.0),
        'c': nrm((DEC_BATCH, D_MODEL), 1.0),
        'c_ctx': nrm((D_MODEL,), 1.0),
        'w_mod': nrm((DEPTH, D_MODEL, N_MOD * D_MODEL), 0.5 * D_MODEL ** -0.5),
        'b_mod': nrm((DEPTH, N_MOD * D_MODEL), 0.01),
        'norm_pre': gain((DEPTH, 3, D_MODEL)),
        'norm_post': gain((DEPTH, 3, D_MODEL)),
        'ffn_w1': nrm((DEPTH, 2, D_MODEL, D_FF), D_MODEL ** -0.5),
        'ffn_w3': nrm((DEPTH, 2, D_MODEL, D_FF), D_MODEL ** -0.5),
        'ffn_w2': nrm((DEPTH, 2, D_FF, D_MODEL), D_FF ** -0.5),
        'w_in': nrm((DEPTH, D_MODEL, IN_COLS), D_MODEL ** -0.5),
        's5_lam_re': -0.5 + nrm((DEPTH, 2, S5_GROUPS, S5_STATE), 0.01),
        's5_lam_im': lam_im0 + nrm((DEPTH, 2, S5_GROUPS, S5_STATE), 0.01),
        's5_log_dt': jax.random.uniform(next(ks), (DEPTH, 2, S5_GROUPS), f32, float(np.log(1e-3)), float(np.log(1e-1))),
        's5_b_re': nrm((DEPTH, 2, S5_GROUPS, S5_STATE, S5_GROUP), (2.0 * S5_GROUP) ** -0.5),
        's5_b_im': nrm((DEPTH, 2, S5_GROUPS, S5_STATE, S5_GROUP), (2.0 * S5_GROUP) ** -0.5),
        's5_c_re': nrm((DEPTH, S5_GROUPS, S5_GROUP, S5_STATE), (2.0 * S5_STATE) ** -0.5),
        's5_c_im': nrm((DEPTH, S5_GROUPS, S5_GROUP, S5_STATE), (2.0 * S5_STATE) ** -0.5),
        's5_d': nrm((DEPTH, S5_WIDTH), 1.0),
        's5_w_glu': nrm((DEPTH, S5_WIDTH, S5_WIDTH), S5_WIDTH ** -0.5),
        'ret_decay': ret_logit0 + nrm((DEPTH, 2, RET_HEADS), 0.1),
        'ret_gn': gain((DEPTH, RET_WIDTH)),
        'conv_w': nrm((DEPTH, CONV_K, CONV_WIDTH), CONV_K ** -0.5),
        'conv_b': nrm((DEPTH, CONV_WIDTH), 0.01),
        'mla_q_norm': gain((DEPTH, MLA_Q_LORA)),
        'mla_w_uq': nrm((DEPTH, MLA_Q_LORA, MLA_HEADS * (MLA_NOPE + MLA_ROPE)), MLA_Q_LORA ** -0.5),
        'mla_kv_norm': gain((DEPTH, MLA_KV_LORA)),
        'mla_w_ukv': nrm((DEPTH, MLA_KV_LORA, MLA_HEADS * (MLA_NOPE + MLA_V)), MLA_KV_LORA ** -0.5),
        'w_branch': nrm((DEPTH, N_BRANCH, BRANCH_WIDTH, D_MODEL), BRANCH_WIDTH ** -0.5),
        'w_gate': nrm((DEPTH, D_MODEL, N_BRANCH * D_MODEL), D_MODEL ** -0.5),
        'b_gate': nrm((DEPTH, N_BRANCH * D_MODEL), 0.01),
        'w_o': nrm((DEPTH, D_MODEL, D_MODEL), D_MODEL ** -0.5),
    }


def reference(x_prompt, x_sample, state_s5, state_ret, cache_mla, c, c_ctx, w_mod, b_mod, norm_pre, norm_post, ffn_w1, ffn_w3, ffn_w2, w_in, s5_lam_re, s5_lam_im, s5_log_dt, s5_b_re, s5_b_im, s5_c_re, s5_c_im, s5_d, s5_w_glu, ret_decay, ret_gn, conv_w, conv_b, mla_q_norm, mla_w_uq, mla_kv_norm, mla_w_ukv, w_branch, w_gate, b_gate, w_o):
    y_prompt = x_prompt
    y_sample = x_sample
    cond_ctx = jnp.broadcast_to(c_ctx, (x_prompt.shape[0], D_MODEL))
    s5_list, ret_list, mla_list = [], [], []
    for l in range(DEPTH):
        p = dict(w_mod=w_mod[l], b_mod=b_mod[l], norm_pre=norm_pre[l], norm_post=norm_post[l],
                 ffn_w1=ffn_w1[l], ffn_w3=ffn_w3[l], ffn_w2=ffn_w2[l], w_in=w_in[l],
                 s5_lam_re=s5_lam_re[l], s5_lam_im=s5_lam_im[l], s5_log_dt=s5_log_dt[l],
                 s5_b_re=s5_b_re[l], s5_b_im=s5_b_im[l], s5_c_re=s5_c_re[l], s5_c_im=s5_c_im[l],
                 s5_d=s5_d[l], s5_w_glu=s5_w_glu[l], ret_decay=ret_decay[l], ret_gn=ret_gn[l],
                 conv_w=conv_w[l], conv_b=conv_b[l], mla_q_norm=mla_q_norm[l], mla_w_uq=mla_w_uq[l],
                 mla_kv_norm=mla_kv_norm[l], mla_w_ukv=mla_w_ukv[l], w_branch=w_branch[l],
                 w_gate=w_gate[l], b_gate=b_gate[l], w_o=w_o[l])
        y_prompt, (s5_s, ret_s, mla_c) = trunk_layer(y_prompt, cond_ctx, p, None)
        s5_list.append(s5_s)
        ret_list.append(ret_s)
        mla_list.append(mla_c)
        y_sample, _ = trunk_layer(y_sample, c, p, (state_s5[:, l], state_ret[:, l], cache_mla[:, l]))
    new_state_s5 = jnp.stack(s5_list, axis=1)
    new_state_ret = jnp.stack(ret_list, axis=1)
    new_cache_mla = jnp.stack(mla_list, axis=1)
    return (y_prompt, y_sample, new_state_s5, new_state_ret, new_cache_mla)
```

```python
import numpy as np
import concourse.bass as bass
import concourse.mybir as mybir
from concourse.bass_utils import run_bass_kernel_spmd
from contextlib import ExitStack

F32 = mybir.dt.float32
BF16 = mybir.dt.bfloat16
AF = mybir.ActivationFunctionType
ALU = mybir.AluOpType
AX = mybir.AxisListType

D = 1024
DFF = 2816
NFF = 22
TOK = 2560
NT = 20
EPS = 1e-6
NCORES = 8
INC = 2400

SAME_ENG_SYNC = True


_REGS = {}


def Reg(name):
    if name not in _REGS:
        _REGS[name] = _Reg(name)
    return _REGS[name]


class _Reg:
    __slots__ = ("name", "w", "r", "dsem", "dcnt")

    def __init__(self, name):
        self.name = name
        self.w = None
        self.r = []
        self.dsem = None
        self.dcnt = 0


class Sched:
    def __init__(self, nc, es):
        self.nc = nc
        self.es = es
        self.eng = {"pe": nc.tensor, "act": nc.scalar, "dve": nc.vector, "pool": nc.gpsimd, "sp": nc.sync}
        self.sem = {e: es.enter_context(nc.semaphore("s_" + e)) for e in self.eng}
        self.cnt = {e: 0 for e in self.eng}
        self.seen = {e: {} for e in self.eng}
        self.seen_d = {e: {} for e in self.eng}
        self.nsem = 0
        self.out_events = []
        self.pending_reads = {}

    def _wait(self, e, deps, raw=None):
        best = {}
        bestd = {}
        for d in deps:
            if d[0] == "e":
                _, e2, c = d
                if e2 == e and (e == "pe" or not SAME_ENG_SYNC or (raw is not None and d not in raw)):
                    continue
                if c > best.get(e2, 0):
                    best[e2] = c
            else:
                _, sem, tgt, key = d
                if tgt > bestd.get(key, (None, 0))[1]:
                    bestd[key] = (sem, tgt)
        E = self.eng[e]
        for e2, c in best.items():
            if self.seen[e].get(e2, 0) >= c:
                continue
            E.wait_ge(self.sem[e2], c)
            self.seen[e][e2] = c
        for key, (sem, tgt) in bestd.items():
            if self.seen_d[e].get(key, 0) >= tgt:
                continue
            E.wait_ge(sem, tgt)
            self.seen_d[e][key] = tgt

    def _deps(self, reads, writes):
        deps = set()
        for r in reads:
            if r.w is not None:
                deps.add(r.w)
        for w in writes:
            if w.w is not None:
                deps.add(w.w)
            deps.update(w.r)
        return deps

    def op(self, e, emit, reads=(), writes=()):
        raw = set(r.w for r in reads if r.w is not None)
        self._wait(e, self._deps(reads, writes), raw)
        inst = emit()
        self.cnt[e] += 1
        inst.then_inc(self.sem[e], 1)
        ev = ("e", e, self.cnt[e])
        for r in reads:
            r.r.append(ev)
        for w in writes:
            w.w = ev
            w.r = []
        return ev

    def dma(self, e, out, in_, reads=(), writes=(), key=None, slow=False):
        key = key or (list(writes) + list(reads))[0]
        if key.dsem is None:
            key.dsem = self.es.enter_context(self.nc.semaphore("d%d" % self.nsem))
            self.nsem += 1
        deps = set(d for d in self._deps(reads, writes) if not (d[0] == "d" and d[3] == key.name))
        self._wait(e, deps)
        inst = self.eng[e].dma_start(out=out, in_=in_, allow_slow_non_contiguous=True) if slow else self.eng[e].dma_start(out=out, in_=in_)
        key.dcnt += 16
        inst.then_inc(key.dsem, 16)
        ev = ("d", key.dsem, key.dcnt, key.name)
        if reads:
            self.pending_reads[key.name] = ev
        for r in reads:
            r.r.append(ev)
        for w in writes:
            w.w = ev
            w.r = []
        return ev

    def barrier(self):
        pend = set(self.pending_reads.values())
        for e in self.eng:
            deps = set(("e", e2, self.cnt[e2]) for e2 in self.eng if e2 != e and self.cnt[e2] > 0)
            self._wait(e, deps | pend)
        self.pending_reads = {}


def build(cfg):
    _REGS.clear()
    nc = bass.Bass("TRN2", target_bir_lowering=False)
    LAYERS = cfg.get("layers", 2)

    def din(name, shape):
        return nc.dram_tensor(name, list(shape), F32, kind="ExternalInput").ap()

    def dout(name, shape):
        return nc.dram_tensor(name, list(shape), F32, kind="ExternalOutput").ap()

    xin = din("xin", [TOK, D])
    cond2 = din("cond2", [2, D])
    st_s5 = din("st_s5", [2, 2, 16, 64, 2])
    st_ret = din("st_ret", [2, 2, 4, 64, 64])
    ctx_mla = din("ctx_mla", [2, 512, 160])
    w_mod = din("w_mod", [2, D, 9 * D])
    b_mod = din("b_mod", [2, 9 * D])
    norm_pre = din("norm_pre", [2, 3, D])
    norm_post = din("norm_post", [2, 3, D])
    ffn_w1 = din("ffn_w1", [2, 2, D, DFF])
    ffn_w3 = din("ffn_w3", [2, 2, D, DFF])
    ffn_w2 = din("ffn_w2", [2, 2, DFF, D])
    w_in = din("w_in", [2, D, INC])
    s5_lam_re = din("s5_lam_re", [2, 2, 16, 64])
    s5_lam_im = din("s5_lam_im", [2, 2, 16, 64])
    s5_log_dt = din("s5_log_dt", [2, 2, 16])
    s5_b_re = din("s5_b_re", [2, 2, 16, 64, 16])
    s5_b_im = din("s5_b_im", [2, 2, 16, 64, 16])
    s5_c_re = din("s5_c_re", [2, 16, 16, 64])
    s5_c_im = din("s5_c_im", [2, 16, 16, 64])
    s5_d = din("s5_d", [2, 256])
    s5_w_glu = din("s5_w_glu", [2, 256, 256])
    ret_decay = din("ret_decay", [2, 2, 4])
    ret_gn = din("ret_gn", [2, 256])
    conv_w = din("conv_w", [2, 3, 256])
    conv_b = din("conv_b", [2, 256])
    mla_q_norm = din("mla_q_norm", [2, 192])
    mla_w_uq = din("mla_w_uq", [2, 192, 384])
    mla_kv_norm = din("mla_kv_norm", [2, 128])
    mla_w_ukv = din("mla_w_ukv", [2, 128, 512])
    w_branch = din("w_branch", [2, 4, 256, D])
    w_gate = din("w_gate", [2, D, 4 * D])
    b_gate = din("b_gate", [2, 4 * D])
    w_o = din("w_o", [2, D, D])
    c_ident = din("c_ident", [128, 128])
    c_rope_mla = din("c_rope_mla", [2, 32, 2048])
    c_rope_ret = din("c_rope_ret", [2, 2048, 32])
    c_ret = din("c_ret", [6, 128, 128])
    c_pidx = din("c_pidx", [128, 2])

    y = dout("y", [TOK, D])
    o_s5 = dout("o_s5", [2, 2, 2, 16, 64, 2])
    o_ret = dout("o_ret", [2, 2, 2, 4, 64, 64])
    o_mla = dout("o_mla", [2, 2, 256, 160])

    es = ExitStack()
    with es:
        S = Sched(nc, es)

        def sb(name, shape, dt=F32):
            return es.enter_context(nc.sbuf_tensor(name, list(shape), dt))

        X = sb("X", [128, NT, D])
        RX = [Reg("X%d" % t) for t in range(NT)]
        PS = es.enter_context(nc.psum_tensor("PS", [128, 8, 512], F32))
        RPS = [Reg("PS%d" % b) for b in range(8)]
        ident = sb("ident", [128, 128])
        identb = sb("identb", [128, 128], BF16)
        Rid = Reg("ident")
        VEC = sb("VEC", [128, 2, 72])
        Rvec = Reg("VEC")
        NRM = sb("NRM", [128, 48])
        Rnrm = Reg("NRM")
        SV = sb("SV", [128, 2, 3, 8])
        Rsv = Reg("SV")
        GV = sb("GV", [128, 2, 3, 8])
        Rgv = Reg("GV")
        GB = sb("GB", [128, 2, D])
        Rgb = [Reg("GB0"), Reg("GB1")]
        DG = sb("DG", [128, 2, 128])
        Rdg = [Reg("DG0"), Reg("DG1")]
        small = sb("small", [128, 4, 4])
        Rsmall = [Reg("sm%d" % i) for i in range(4)]
        junk = sb("junk", [128, D], BF16)
        Rjunk = Reg("junk")
        ACOLS = 28672
        ARENA = sb("ARENA", [128, ACOLS])

        def view(off, shape, dt=F32):
            n = int(np.prod(shape[1:]))
            nbytes = n * (2 if dt == BF16 else 4)
            assert off % 4 == 0 and nbytes % 4 == 0 and off + nbytes <= ACOLS * 4, (off, shape)
            ap = ARENA[:, off // 4:(off + nbytes) // 4]
            if dt == BF16:
                ap = ap.bitcast(BF16)
            if len(shape) > 2:
                names = "abcdef"[:len(shape) - 1]
                pat = "p (" + " ".join(names) + ") -> p " + " ".join(names)
                ap = ap.rearrange(pat, **{names[i]: shape[i + 1] for i in range(len(names) - 1)})
            return ap, off + nbytes

        HTB, _o = view(0, [128, 2, 8, 512], BF16)
        GT, _o = view(_o, [128, NFF, 512], BF16)
        W13, _o = view(_o, [128, 3, 2, 8, 256], BF16)
        W2, _o = view(_o, [128, 3, 2, D], BF16)
        SIL, _o = view(_o, [128, 2, 512], BF16)
        XN, _o = view(_o, [128, 2, D])
        TMP, _o = view(_o, [128, 2, D])
        WM, _ = view(0, [128, 2, 8, 512], BF16)
        Rxn = [Reg("XN0"), Reg("XN1")]

        xv = xin.rearrange("(t p) d -> p t d", p=128)
        Rxall = Reg("xall")
        for t in range(NT):
            S.dma("sp", X[:, t, :], xv[:, t, :], writes=[RX[t]], key=Rxall)
        for t in range(NT):
            RX[t].w = ("d", Rxall.dsem, Rxall.dcnt, Rxall.name)
        S.dma("sp", ident[:], c_ident[:, :], writes=[Rid])
        S.op("dve", lambda: nc.vector.tensor_copy(out=identb[:], in_=ident[:]), reads=[Rid], writes=[Rid])

        rot = {"ps": 0, "sm": 0, "xn": 0, "dg": 0}
        OUT_EVS = []

        stage = sb("stage", [128, 128])
        Rstage = Reg("stage")

        def load_T(dst_ap, src_ap, rows, dst_reg, bank=7):
            S.dma("sp", stage[0:rows, :], src_ap, writes=[Rstage])
            S.op("pe", lambda: nc.tensor.transpose(out=PS[:, bank, 0:rows], in_=stage[0:rows, :], identity=ident[0:rows, 0:rows]),
                 reads=[Rstage, Rid], writes=[RPS[bank]])
            S.op("dve", lambda: nc.vector.tensor_copy(out=dst_ap, in_=PS[:, bank, 0:rows]), reads=[RPS[bank]], writes=[dst_reg])

        SCT = sb("SCT", [128, 8, 2], BF16)
        Rsct = Reg("SCT")
        sct32 = sb("sct32", [128, 16])
        load_T(sct32[:, :], cond2.rearrange("c (k p) -> (c k) p", p=128), 16, Rsct)
        S.op("act", lambda: nc.scalar.activation(out=SCT[:].rearrange("p k c -> p c k"), in_=sct32[:].rearrange("p (c k) -> p c k", c=2), func=AF.Silu),
             reads=[Rsct], writes=[Rsct])

        Rwm = [Reg("WM0"), Reg("WM1")]
        BM = sb("BM", [128, 72])
        Rbm = Reg("BM")

        def compute_mod(l):
            load_T(BM[:, :], b_mod[l].rearrange("(c p) -> c p", p=128), 72, Rbm)
            load_T(NRM[:, 0:24], norm_pre[l].rearrange("s (c p) -> (s c) p", p=128), 24, Rnrm)
            load_T(NRM[:, 24:48], norm_post[l].rearrange("s (c p) -> (s c) p", p=128), 24, Rnrm)
            wv = w_mod[l].rearrange("(kc p) n -> p kc n", p=128)
            for cb in range(18):
                sl = cb % 2
                S.dma("pool", WM[:, sl, :, :], wv[:, :, cb * 512:(cb + 1) * 512], writes=[Rwm[sl]])
                bank = 6

                def emit(cb=cb, sl=sl):
                    inst = None
                    for cc in range(4):
                        for kc in range(8):
                            inst = nc.tensor.matmul(PS[:, bank, cc * 2:cc * 2 + 2], lhsT=WM[:, sl, kc, cc * 128:(cc + 1) * 128],
                                                    rhs=SCT[:, kc, :], start=(kc == 0), stop=(kc == 7))
                    return inst
                S.op("pe", emit, reads=[Rwm[sl], Rsct], writes=[RPS[bank]])
                S.op("dve", lambda cb=cb: nc.vector.tensor_tensor(
                    out=VEC[:, :, cb * 4:(cb + 1) * 4].rearrange("p c j -> p j c"),
                    in0=PS[:, bank, 0:8].rearrange("p (j c) -> p j c", c=2),
                    in1=BM[:, cb * 4:(cb + 1) * 4].unsqueeze(2).broadcast_to([128, 4, 2]), op=ALU.add),
                    reads=[RPS[bank], Rbm], writes=[Rvec])
            for ci in range(2):
                for s in range(3):
                    S.op("dve", lambda ci=ci, s=s: nc.vector.scalar_tensor_tensor(
                        out=SV[:, ci, s, :], in0=VEC[:, ci, (3 * s + 1) * 8:(3 * s + 2) * 8], scalar=1.0,
                        in1=NRM[:, s * 8:(s + 1) * 8], op0=ALU.add, op1=ALU.mult), reads=[Rvec, Rnrm], writes=[Rsv])
                    fac = 1.0 if s == 1 else 0.5
                    S.op("dve", lambda ci=ci, s=s, fac=fac: nc.vector.scalar_tensor_tensor(
                        out=GV[:, ci, s, :], in0=VEC[:, ci, (3 * s + 2) * 8:(3 * s + 3) * 8], scalar=fac,
                        in1=NRM[:, 24 + s * 8:24 + (s + 1) * 8], op0=ALU.mult, op1=ALU.mult), reads=[Rvec, Rnrm], writes=[Rgv])

        def make_gate_bcast(s):
            for ci in range(2):
                bank0 = 4 + 2 * ci
                for c in range(8):
                    dgi = rot["dg"] % 2
                    rot["dg"] += 1
                    S.op("dve", lambda c=c, ci=ci, dgi=dgi: nc.vector.tensor_scalar(
                        out=DG[:, dgi, :], in0=ident[:], scalar1=GV[:, ci, s, c:c + 1], scalar2=None, op0=ALU.mult),
                        reads=[Rid, Rgv], writes=[Rdg[dgi]])
                    bank = bank0 + c // 4
                    S.op("pe", lambda c=c, dgi=dgi, bank=bank: nc.tensor.matmul(
                        PS[:, bank, (c % 4) * 128:(c % 4 + 1) * 128], lhsT=ones32[:], rhs=DG[:, dgi, :], start=True, stop=True),
                        reads=[Rdg[dgi], Rones], writes=[RPS[bank]])
                S.op("dve", lambda ci=ci, bank0=bank0: nc.vector.tensor_copy(
                    out=GB[:, ci, :], in_=PS[:, bank0:bank0 + 2, :].rearrange("p b n -> p (b n)")),
                    reads=[RPS[bank0], RPS[bank0 + 1]], writes=[Rgb[ci]])

        ones32 = sb("ones32", [128, 128])
        Rones = Reg("ones")
        S.op("dve", lambda: nc.vector.memset(ones32[:], 1.0), writes=[Rones])

        cur = {"XN": XN, "TMP": TMP}

        def prenorm_tile(t, ci, s, dst, dst_reg, banks):
            XN = cur["XN"]
            smi = rot["sm"] % 4
            rot["sm"] += 1
            xi = rot["xn"] % 2
            rot["xn"] += 1
            sm = small[:, smi, :]
            S.op("act", lambda: nc.scalar.activation(out=junk[:], in_=X[:, t, :], func=AF.Square, accum_out=sm[:, 0:1]),
                 reads=[RX[t]], writes=[Rjunk, Rsmall[smi]])
            S.op("act", lambda: nc.scalar.activation(out=sm[:, 1:2], in_=sm[:, 0:1], func=AF.Sqrt, scale=1.0 / D, bias=epsb[:, 0:1]),
                 reads=[Rsmall[smi], Reps], writes=[Rsmall[smi]])
            S.op("dve", lambda: nc.vector.reciprocal(out=sm[:, 2:3], in_=sm[:, 1:2]), reads=[Rsmall[smi]], writes=[Rsmall[smi]])
            S.op("dve", lambda: nc.vector.tensor_scalar(out=XN[:, xi, :], in0=X[:, t, :], scalar1=sm[:, 2:3], scalar2=None, op0=ALU.mult),
                 reads=[RX[t], Rsmall[smi]], writes=[Rxn[xi]])
            b0, b1 = banks

            def emit():
                inst = None
                for c in range(8):
                    bk = b0 if c < 4 else b1
                    inst = nc.tensor.transpose(out=PS[:, bk, (c % 4) * 128:(c % 4 + 1) * 128], in_=XN[:, xi, c * 128:(c + 1) * 128], identity=ident[:])
                return inst
            S.op("pe", emit, reads=[Rxn[xi], Rid], writes=[RPS[b0], RPS[b1]])
            for c in range(8):
                bk = b0 if c < 4 else b1
                S.op("act", lambda c=c, bk=bk: nc.scalar.activation(
                    out=dst[:, c, :], in_=PS[:, bk, (c % 4) * 128:(c % 4 + 1) * 128], func=AF.Identity,
                    scale=SV[:, ci, s, c:c + 1], bias=VEC[:, ci, 3 * s * 8 + c:3 * s * 8 + c + 1]),
                    reads=[RPS[bk], Rsv, Rvec], writes=[dst_reg])

        epsb = sb("epsb", [128, 1])
        halfpi = sb("halfpi", [128, 1])
        Reps = Reg("eps")
        S.op("dve", lambda: nc.vector.memset(epsb[:], EPS), writes=[Reps])
        S.op("dve", lambda: nc.vector.memset(halfpi[:], float(np.pi / 2)), writes=[Reps])

        Rtmp = [Reg("TMP0"), Reg("TMP1")]

        def postnorm_tile(t, ci, b0):
            TMP = cur["TMP"]
            smi = rot["sm"] % 4
            rot["sm"] += 1
            ti = rot["xn"] % 2
            rot["xn"] += 1
            sm = small[:, smi, :]
            fin = PS[:, b0:b0 + 2, :].rearrange("p b n -> p (b n)")
            S.op("act", lambda: nc.scalar.activation(out=junk[:], in_=fin, func=AF.Square, accum_out=sm[:, 0:1]),
                 reads=[RPS[b0], RPS[b0 + 1]], writes=[Rjunk, Rsmall[smi]])
            S.op("act", lambda: nc.scalar.activation(out=sm[:, 1:2], in_=sm[:, 0:1], func=AF.Sqrt, scale=1.0 / D, bias=epsb[:, 0:1]),
                 reads=[Rsmall[smi], Reps], writes=[Rsmall[smi]])
            S.op("dve", lambda: nc.vector.reciprocal(out=sm[:, 2:3], in_=sm[:, 1:2]), reads=[Rsmall[smi]], writes=[Rsmall[smi]])
            S.op("dve", lambda: nc.vector.scalar_tensor_tensor(out=TMP[:, ti, :], in0=fin, scalar=sm[:, 2:3], in1=GB[:, ci, :],
                                                               op0=ALU.mult, op1=ALU.mult),
                 reads=[RPS[b0], RPS[b0 + 1], Rsmall[smi], Rgb[ci]], writes=[Rtmp[ti]])
            S.op("dve", lambda: nc.vector.tensor_tensor(out=X[:, t, :], in0=X[:, t, :], in1=TMP[:, ti, :], op=ALU.add),
                 reads=[RX[t], Rtmp[ti]], writes=[RX[t]])

        Rhtb = [Reg("HTB0"), Reg("HTB1")]
        Rgt = [Reg("GT%d" % j) for j in range(NFF)]
        Rw13 = [Reg("W13_%d" % i) for i in range(3)]
        Rw2 = [Reg("W2_%d" % i) for i in range(3)]
        Rsil = [Reg("SIL0"), Reg("SIL1")]
        cnts = {"w13": 0, "w2": 0, "htb": 0, "sil": 0, "pa": 0}

        def ffn(l, f, s):
            make_gate_bcast(s)
            w1v = ffn_w1[l, f].rearrange("(kc p) n -> p kc n", p=128)
            w3v = ffn_w3[l, f].rearrange("(kc p) n -> p kc n", p=128)
            w2v = ffn_w2[l, f].rearrange("(j p) n -> p j n", p=128)
            w13base = cnts["w13"]
            w13issued = [0]

            def issue_w13(upto):
                while w13issued[0] < min(upto, 5 * (NFF // 2)):
                    i = w13issued[0]
                    j2_ = i % (NFF // 2)
                    sl_ = (w13base + i) % 3
                    S.dma("pool", W13[:, sl_, 0, :, :], w1v[:, :, j2_ * 256:(j2_ + 1) * 256], writes=[Rw13[sl_]])
                    S.dma("pool", W13[:, sl_, 1, :, :], w3v[:, :, j2_ * 256:(j2_ + 1) * 256], writes=[Rw13[sl_]])
                    w13issued[0] += 1
                    cnts["w13"] += 1
            for blk in range(5):
                ci = 0 if blk < 4 else 1
                hb = cnts["htb"] % 2
                cnts["htb"] += 1
                for tt in range(4):
                    t = blk * 4 + tt
                    prenorm_tile(t, ci, s, HTB[:, hb, :, tt * 128:(tt + 1) * 128], Rhtb[hb], (4 + 2 * (tt % 2), 5 + 2 * (tt % 2)))
                for j2 in range(NFF // 2):
                    idx = blk * (NFF // 2) + j2
                    issue_w13(idx + 1)
                    sl = (w13base + idx) % 3
                    for jj in range(2):
                        j = 2 * j2 + jj
                        pa = cnts["pa"] % 2
                        cnts["pa"] += 1
                        b1, b3 = 2 * pa, 2 * pa + 1
                        for (m, bk) in ((0, b1), (1, b3)):
                            def emit(m=m, bk=bk, jj=jj, sl=sl):
                                inst = None
                                for kc in range(8):
                                    inst = nc.tensor.matmul(PS[:, bk, :], lhsT=W13[:, sl, m, kc, jj * 128:(jj + 1) * 128],
                                                            rhs=HTB[:, hb, kc, :], start=(kc == 0), stop=(kc == 7))
                                return inst
                            S.op("pe", emit, reads=[Rw13[sl], Rhtb[hb]], writes=[RPS[bk]])
                        si = cnts["sil"] % 2
                        cnts["sil"] += 1
                        S.op("act", lambda b1=b1, si=si: nc.scalar.activation(out=SIL[:, si, :], in_=PS[:, b1, :], func=AF.Silu),
                             reads=[RPS[b1]], writes=[Rsil[si]])
                        S.op("dve", lambda b3=b3, si=si, j=j: nc.vector.tensor_tensor(out=GT[:, j, :], in0=PS[:, b3, :], in1=SIL[:, si, :], op=ALU.mult),
                             reads=[RPS[b3], Rsil[si]], writes=[Rgt[j]])
                for j2 in range(NFF // 2):
                    if j2 == 3:
                        issue_w13((blk + 1) * (NFF // 2) + 3)
                    sl = cnts["w2"] % 3
                    cnts["w2"] += 1
                    S.dma("pool", W2[:, sl, :, :], w2v[:, 2 * j2:2 * j2 + 2, :], writes=[Rw2[sl]])
                    for jj in range(2):
                        j = 2 * j2 + jj

                        def emit(j=j, jj=jj, sl=sl):
                            inst = None
                            for tt in range(4):
                                for half in range(2):
                                    inst = nc.tensor.matmul(PS[:, 2 * tt + half, :], lhsT=GT[:, j, tt * 128:(tt + 1) * 128],
                                                            rhs=W2[:, sl, jj, half * 512:(half + 1) * 512], start=(j == 0), stop=(j == NFF - 1))
                            return inst
                        S.op("pe", emit, reads=[Rgt[j], Rw2[sl]], writes=RPS)
                for tt in range(4):
                    postnorm_tile(blk * 4 + tt, ci, 2 * tt)

        MOFF = 0
        HT, MOFF = view(MOFF, [128, 8, 2048], BF16)
        BR, MOFF = view(MOFF, [128, 8, 2048], BF16)
        WIN0 = MOFF
        WIN, MOFF = view(MOFF, [128, 2, 8, 256], BF16)
        BS0 = MOFF
        Rht = Reg("HT")
        Rbr = [Reg("BR%d" % i) for i in range(8)]
        Rwin = [Reg("WIN0"), Reg("WIN1")]
        PV = sb("PV", [128, 64])
        Rpv = Reg("PV")
        BG = sb("BG", [128, 32])
        Rbg = Reg("BG")
        mc = {"win": 0, "pb": 0, "wg": 0, "wo": 0, "sg": 0}
        SEQS = [(0, 16, 0, "sample"), (16, 2, 1, "pA"), (18, 2, 1, "pB")]

        def mixer_params(l):
            load_T(PV[:, 0:6], conv_w[l].rearrange("j (c p) -> (j c) p", p=128), 6, Rpv)
            load_T(PV[:, 6:8], conv_b[l].rearrange("(c p) -> c p", p=128), 2, Rpv)
            load_T(PV[:, 8:10], ret_gn[l].rearrange("(c p) -> c p", p=128), 2, Rpv)
            load_T(PV[:, 10:12], s5_d[l].rearrange("(c p) -> c p", p=128), 2, Rpv)
            load_T(PV[:, 12:13], mla_kv_norm[l].rearrange("(c p) -> c p", p=128), 1, Rpv)
            load_T(BG[:, :], b_gate[l].rearrange("(c p) -> c p", p=128), 32, Rbg)

        def proj_fm(l, col0, ncols, NB, BW, evac, extra_reads=()):
            winv = w_in[l].rearrange("(kc p) n -> p kc n", p=128)
            sl = mc["win"] % 2
            mc["win"] += 1
            S.dma("pool", WIN[:, sl, :, 0:ncols], winv[:, :, col0:col0 + ncols], writes=[Rwin[sl]])
            for b in range(NB):
                bank = mc["pb"] % 4
                mc["pb"] += 1

                def emit(b=b, bank=bank):
                    inst = None
                    for kc in range(8):
                        inst = nc.tensor.matmul(PS[0:ncols, bank, 0:BW], lhsT=WIN[:, sl, kc, 0:ncols], rhs=HT[:, kc, b * BW:(b + 1) * BW],
                                                start=(kc == 0), stop=(kc == 7))
                    return inst
                S.op("pe", emit, reads=[Rwin[sl], Rht], writes=[RPS[bank]])
                evac(b, bank)

        def branch_conv(l, T, NB, BW):
            o = BS0
            Z, o = view(o, [128, 2056])
            CX, o = view(o, [128, 2048])
            Y, o = view(o, [128, 2048])
            CB, o = view(o, [128, 2048], BF16)
            Rz, Rcx, Ry, Rcb = Reg("Z"), Reg("CX"), Reg("Y"), Reg("CB")
            for cc in range(2):
                S.op("dve", lambda: nc.vector.memset(Z[:, 0:1], 0.0), writes=[Rz])
                S.op("dve", lambda: nc.vector.memset(Z[:, T + 1:T + 2], 0.0), writes=[Rz])
                proj_fm(l, 1280 + cc * 128, 128, NB, BW, lambda b, bank: S.op(
                    "act", lambda: nc.scalar.copy(out=CX[:, b * BW:(b + 1) * BW], in_=PS[:, bank, 0:BW]), reads=[RPS[bank]], writes=[Rcx]))
                proj_fm(l, 1792 + cc * 128, 128, NB, BW, lambda b, bank: S.op(
                    "dve", lambda: nc.vector.tensor_tensor(out=Z[:, 1 + b * BW:1 + (b + 1) * BW], in0=PS[:, bank, 0:BW], in1=CX[:, b * BW:(b + 1) * BW], op=ALU.mult),
                    reads=[RPS[bank], Rcx], writes=[Rz]))
                proj_fm(l, 1536 + cc * 128, 128, NB, BW, lambda b, bank: S.op(
                    "act", lambda: nc.scalar.copy(out=CB[:, b * BW:(b + 1) * BW], in_=PS[:, bank, 0:BW]), reads=[RPS[bank]], writes=[Rcb]))
                S.op("dve", lambda: nc.vector.tensor_scalar(out=Y[:, 0:T], in0=Z[:, 1:T + 1], scalar1=PV[:, 2 + cc:3 + cc], scalar2=PV[:, 6 + cc:7 + cc],
                                                            op0=ALU.mult, op1=ALU.add), reads=[Rz, Rpv], writes=[Ry])
                S.op("dve", lambda: nc.vector.scalar_tensor_tensor(out=Y[:, 0:T], in0=Z[:, 0:T], scalar=PV[:, 0 + cc:1 + cc], in1=Y[:, 0:T],
                                                                   op0=ALU.mult, op1=ALU.add), reads=[Rz, Rpv, Ry], writes=[Ry])
                S.op("dve", lambda: nc.vector.scalar_tensor_tensor(out=Y[:, 0:T], in0=Z[:, 2:T + 2], scalar=PV[:, 4 + cc:5 + cc], in1=Y[:, 0:T],
                                                                   op0=ALU.mult, op1=ALU.add), reads=[Rz, Rpv, Ry], writes=[Ry])
                S.op("dve", lambda: nc.vector.tensor_tensor(out=BR[:, 4 + cc, 0:T], in0=Y[:, 0:T], in1=CB[:, 0:T], op=ALU.mult),
                     reads=[Ry, Rcb], writes=[Rbr[4 + cc]])

        def gate_stage(l, t0, ntile, ci, T, NB, BW):
            o = WIN0
            WG, o = view(o, [128, 2, 8, 4, 128], BF16)
            WB, o = view(o, [128, 2, 2, 4, 128], BF16)
            MG, o = view(o, [128, 8, 512], BF16)
            SG, o = view(o, [128, 4, 512], BF16)
            WO, o = view(o, [128, 2, D], BF16)
            ACC, o = view(o, [128, 2, 512])
            cur["TMP"], o = view(o, [128, 2, D])
            Rwg = [Reg("WG0"), Reg("WG1")]
            Rmg = [Reg("MG%d" % c) for c in range(8)]
            Rsg = [Reg("SG%d" % n) for n in range(4)]
            Rwo = [Reg("WO0"), Reg("WO1")]
            Racc = [Reg("ACC0"), Reg("ACC1")]
            wgv = w_gate[l].rearrange("(kc p) (n d) -> p kc n d", p=128, n=4)
            wbv = w_branch[l].rearrange("n (kc p) d -> p kc n d", p=128)
            tpb = BW // 128
            for b in range(NB):
                for c in range(8):
                    sl = mc["wg"] % 2
                    mc["wg"] += 1
                    for n in range(4):
                        S.dma("pool", WG[:, sl, :, n, :], wgv[:, :, n, c * 128:(c + 1) * 128], writes=[Rwg[sl]])
                    for n in range(4):
                        S.dma("pool", WB[:, sl, :, n, :], wbv[:, :, n, c * 128:(c + 1) * 128], writes=[Rwg[sl]])
                    for n in range(4):
                        def emit_g(n=n, sl=sl):
                            inst = None
                            for kc in range(8):
                                inst = nc.tensor.matmul(PS[:, n, 0:BW], lhsT=WG[:, sl, kc, n, :], rhs=HT[:, kc, b * BW:(b + 1) * BW],
                                                        start=(kc == 0), stop=(kc == 7))
                            return inst
                        S.op("pe", emit_g, reads=[Rwg[sl], Rht], writes=[RPS[n]])

                        def emit_p(n=n, sl=sl):
                            inst = None
                            for kc in range(2):
                                inst = nc.tensor.matmul(PS[:, 4 + n, 0:BW], lhsT=WB[:, sl, kc, n, :], rhs=BR[:, 2 * n + kc, b * BW:(b + 1) * BW],
                                                        start=(kc == 0), stop=(kc == 1))
                            return inst
                        S.op("pe", emit_p, reads=[Rwg[sl], Rbr[2 * n], Rbr[2 * n + 1]], writes=[RPS[4 + n]])
                        S.op("act", lambda n=n: nc.scalar.activation(out=SG[:, n, 0:BW], in_=PS[:, n, 0:BW], func=AF.Sigmoid,
                                                                     bias=BG[:, n * 8 + c:n * 8 + c + 1]), reads=[RPS[n], Rbg], writes=[Rsg[n]])
                    S.op("dve", lambda: nc.vector.tensor_tensor(out=ACC[:, 0, 0:BW], in0=PS[:, 4, 0:BW], in1=SG[:, 0, 0:BW], op=ALU.mult),
                         reads=[RPS[4], Rsg[0]], writes=[Racc[0]])
                    for n in range(1, 4):
                        S.op("dve", lambda n=n: nc.vector.tensor_tensor(out=ACC[:, 1, 0:BW], in0=PS[:, 4 + n, 0:BW], in1=SG[:, n, 0:BW], op=ALU.mult),
                             reads=[RPS[4 + n], Rsg[n]], writes=[Racc[1]])
                        if n < 3:
                            S.op("dve", lambda: nc.vector.tensor_tensor(out=ACC[:, 0, 0:BW], in0=ACC[:, 0, 0:BW], in1=ACC[:, 1, 0:BW], op=ALU.add),
                                 reads=[Racc[0], Racc[1]], writes=[Racc[0]])
                        else:
                            S.op("dve", lambda: nc.vector.tensor_tensor(out=MG[:, c, 0:BW], in0=ACC[:, 0, 0:BW], in1=ACC[:, 1, 0:BW], op=ALU.add),
                                 reads=[Racc[0], Racc[1]], writes=[Rmg[c]])
                for c in range(8):
                    sl = mc["wo"] % 2
                    mc["wo"] += 1
                    S.dma("pool", WO[:, sl, :], w_o[l, c * 128:(c + 1) * 128, :], writes=[Rwo[sl]])

                    def emit_o(c=c, sl=sl):
                        inst = None
                        for tt in range(tpb):
                            for half in range(2):
                                inst = nc.tensor.matmul(PS[:, 2 * tt + half, :], lhsT=MG[:, c, tt * 128:(tt + 1) * 128],
                                                        rhs=WO[:, sl, half * 512:(half + 1) * 512], start=(c == 0), stop=(c == 7))
                        return inst
                    S.op("pe", emit_o, reads=[Rmg[c], Rwo[sl]], writes=RPS[0:2 * tpb])
                for tt in range(tpb):
                    postnorm_tile(t0 + b * tpb + tt, ci, 2 * tt)

        onesb = sb("onesb", [128, 128], BF16)
        S.op("dve", lambda: nc.vector.memset(onesb[:], 1.0), writes=[Rones])
        ATT_SCALE = float(96 ** -0.5)

        def branch_mla(l, t0, T, NB, BW, kind):
            sample = kind == "sample"
            Skeys = T + (512 if sample else 0)
            NKT = Skeys // 128
            o = BS0
            CQ, o = view(o, [128, 2, 2048], BF16)
            CKVN, o = view(o, [128, 2560], BF16)
            KR, o = view(o, [128, 2560], BF16)
            WUQ, o = view(o, [128, 2, 384], BF16)
            WUQS, o = view(o, [128, 2, 4, 32], BF16)
            WUKV, o = view(o, [128, 512], BF16)
            WKRS, o = view(o, [128, 8, 32], BF16)
            QNV, o = view(o, [128, 2])
            oB = o
            Rcq, Rckvn, Rkr, Rw = Reg("m_CQ"), Reg("m_CKVN"), Reg("m_KR"), Reg("m_W")
            W32, oo = view(oB, [128, 2, 384])
            Rw32 = Reg("m_W32")
            S.dma("sp", W32[:, 0, :], mla_w_uq[l, 0:128, :], writes=[Rw32])
            S.dma("sp", W32[0:64, 1, :], mla_w_uq[l, 128:192, :], writes=[Rw32])
            S.dma("sp", QNV[:, 0:1], mla_q_norm[l, 0:128].unsqueeze(1), writes=[Rw])
            S.dma("sp", QNV[0:64, 1:2], mla_q_norm[l, 128:192].unsqueeze(1), writes=[Rw])
            S.dma("pool", WUKV[:, :], mla_w_ukv[l, :, :], writes=[Rw])
            for kc, np_ in ((0, 128), (1, 64)):
                S.op("dve", lambda kc=kc, np_=np_: nc.vector.tensor_scalar(out=WUQ[0:np_, kc, :], in0=W32[0:np_, kc, :], scalar1=QNV[0:np_, kc:kc + 1],
                                                                          scalar2=None, op0=ALU.mult), reads=[Rw32, Rw], writes=[Rw])
                if sample:
                    wv = WUQ[0:np_, kc, :].rearrange("p (h e) -> p h e", h=4)
                    S.op("dve", lambda wv=wv, kc=kc, np_=np_: nc.vector.tensor_scalar(out=WUQS[0:np_, kc, :, 0:16], in0=wv[:, :, 80:96], scalar1=-1.0,
                                                                                   scalar2=None, op0=ALU.mult), reads=[Rw], writes=[Rw])
                    S.op("dve", lambda wv=wv, kc=kc, np_=np_: nc.vector.tensor_copy(out=WUQS[0:np_, kc, :, 16:32], in_=wv[:, :, 64:80]), reads=[Rw], writes=[Rw])
            if sample:
                winv = w_in[l].rearrange("(kc p) n -> p kc n", p=128)
                S.dma("pool", WKRS[:, :, 0:16], winv[:, :, 2384:2400], writes=[Rw])
                S.dma("pool", WKRS[:, :, 16:32], winv[:, :, 2368:2384], writes=[Rw])
                S.op("dve", lambda: nc.vector.tensor_scalar(out=WKRS[:, :, 0:16], in0=WKRS[:, :, 0:16], scalar1=-1.0, scalar2=None, op0=ALU.mult),
                     reads=[Rw], writes=[Rw])
            SQ, oo = view(oo, [128, 2, 512], BF16)
            RST, oo = view(oo, [128, 512])
            TB, oo = view(oo, [128, 2, 512])
            T1, oo = view(oo, [128, 2, 512])
            Rsq, Rrst, Rtb, Rt1 = Reg("m_SQ"), Reg("m_RST"), Reg("m_TB"), Reg("m_T1")

            def rstd_from_ps(bank, parts):
                S.op("act", lambda: nc.scalar.activation(out=RST[:, 0:BW], in_=PS[:, bank, 0:BW], func=AF.Sqrt, scale=1.0 / parts, bias=epsb[:, 0:1]),
                     reads=[RPS[bank], Reps], writes=[Rrst])
                S.op("dve", lambda: nc.vector.reciprocal(out=RST[:, 0:BW], in_=RST[:, 0:BW]), reads=[Rrst], writes=[Rrst])

            winv = w_in[l].rearrange("(kc p) n -> p kc n", p=128)
            WQ = WIN
            S.dma("pool", WQ[:, 0, :, 0:192], winv[:, :, 2048:2240], writes=[Rwin[0]])
            S.dma("pool", WQ[:, 1, :, 0:160], winv[:, :, 2240:2400], writes=[Rwin[1]])
            for b in range(NB):
                cols = slice(b * BW, (b + 1) * BW)
                for kc2, np_, bank in ((0, 128, 0), (1, 64, 1)):
                    def emit(kc2=kc2, np_=np_, bank=bank):
                        inst = None
                        for kc in range(8):
                            inst = nc.tensor.matmul(PS[0:np_, bank, 0:BW], lhsT=WQ[:, 0, kc, kc2 * 128:kc2 * 128 + np_], rhs=HT[:, kc, cols],
                                                    start=(kc == 0), stop=(kc == 7))
                        return inst
                    S.op("pe", emit, reads=[Rwin[0], Rht], writes=[RPS[bank]])
                    S.op("act", lambda kc2=kc2, np_=np_, bank=bank: nc.scalar.activation(out=SQ[0:np_, kc2, 0:BW], in_=PS[0:np_, bank, 0:BW], func=AF.Square),
                         reads=[RPS[bank]], writes=[Rsq])

                def emit_ss():
                    nc.tensor.matmul(PS[:, 2, 0:BW], lhsT=onesb[:, :], rhs=SQ[:, 0, 0:BW], start=True, stop=False)
                    return nc.tensor.matmul(PS[:, 2, 0:BW], lhsT=onesb[0:64, :], rhs=SQ[0:64, 1, 0:BW], start=False, stop=True)
                S.op("pe", emit_ss, reads=[Rsq, Rones], writes=[RPS[2]])
                rstd_from_ps(2, 192.0)
                for kc2, np_, bank in ((0, 128, 0), (1, 64, 1)):
                    S.op("dve", lambda kc2=kc2, np_=np_, bank=bank: nc.vector.tensor_tensor(out=CQ[0:np_, kc2, cols], in0=PS[0:np_, bank, 0:BW], in1=RST[0:np_, 0:BW], op=ALU.mult),
                         reads=[RPS[bank], Rrst], writes=[Rcq])
                def emit_kv():
                    inst = None
                    for kc in range(8):
                        inst = nc.tensor.matmul(PS[:, 3, 0:BW], lhsT=WQ[:, 1, kc, 0:128], rhs=HT[:, kc, cols], start=(kc == 0), stop=(kc == 7))
                    return inst
                S.op("pe", emit_kv, reads=[Rwin[1], Rht], writes=[RPS[3]])
                S.op("act", lambda: nc.scalar.activation(out=SQ[:, 0, 0:BW], in_=PS[:, 3, 0:BW], func=AF.Square), reads=[RPS[3]], writes=[Rsq])
                S.op("pe", lambda: nc.tensor.matmul(PS[:, 2, 0:BW], lhsT=onesb[:, :], rhs=SQ[:, 0, 0:BW], start=True, stop=True), reads=[Rsq, Rones], writes=[RPS[2]])
                rstd_from_ps(2, 128.0)
                S.op("dve", lambda: nc.vector.scalar_tensor_tensor(out=CKVN[:, cols], in0=PS[:, 3, 0:BW], scalar=PV[:, 12:13], in1=RST[:, 0:BW], op0=ALU.mult, op1=ALU.mult),
                     reads=[RPS[3], Rpv, Rrst], writes=[Rckvn])
                def emit_kr():
                    inst = None
                    for kc in range(8):
                        inst = nc.tensor.matmul(PS[0:32, 4, 0:BW], lhsT=WQ[:, 1, kc, 128:160], rhs=HT[:, kc, cols], start=(kc == 0), stop=(kc == 7))
                    return inst
                S.op("pe", emit_kr, reads=[Rwin[1], Rht], writes=[RPS[4]])
                if sample:
                    def emit_krs():
                        inst = None
                        for kc in range(8):
                            inst = nc.tensor.matmul(PS[0:32, 5, 0:BW], lhsT=WKRS[:, kc, :], rhs=HT[:, kc, cols], start=(kc == 0), stop=(kc == 7))
                        return inst
                    S.op("pe", emit_krs, reads=[Rw, Rht], writes=[RPS[5]])
                    S.dma("sp", TB[0:32, 0, 0:BW], c_rope_mla[0, :, cols], writes=[Rtb])
                    S.dma("sp", TB[0:32, 1, 0:BW], c_rope_mla[1, :, cols], writes=[Rtb])
                    S.op("dve", lambda: nc.vector.tensor_tensor(out=T1[0:32, 0, 0:BW], in0=PS[0:32, 4, 0:BW], in1=TB[0:32, 0, 0:BW], op=ALU.mult),
                         reads=[RPS[4], Rtb], writes=[Rt1])
                    S.op("dve", lambda: nc.vector.tensor_tensor(out=T1[0:32, 1, 0:BW], in0=PS[0:32, 5, 0:BW], in1=TB[0:32, 1, 0:BW], op=ALU.mult),
                         reads=[RPS[5], Rtb], writes=[Rt1])
                    S.op("dve", lambda: nc.vector.tensor_tensor(out=KR[0:32, cols], in0=T1[0:32, 0, 0:BW], in1=T1[0:32, 1, 0:BW], op=ALU.add),
                         reads=[Rt1], writes=[Rkr])
                else:
                    S.op("act", lambda: nc.scalar.copy(out=KR[0:32, cols], in_=PS[0:32, 4, 0:BW]), reads=[RPS[4]], writes=[Rkr])
            if sample:
                CT, _ = view(oB + 3072, [128, 4, 160])
                Rct = Reg("m_CT")
                S.barrier()
                S.dma("sp", CT[:, :, :], ctx_mla[l].rearrange("(i p) f -> p i f", p=128), writes=[Rct])
                for i in range(4):
                    S.op("pe", lambda i=i: nc.tensor.transpose(out=PS[:, 6, 0:128], in_=CT[:, i, 0:128], identity=ident[:]), reads=[Rct, Rid], writes=[RPS[6]])
                    S.op("act", lambda i=i: nc.scalar.copy(out=CKVN[:, T + i * 128:T + (i + 1) * 128], in_=PS[:, 6, 0:128]), reads=[RPS[6]], writes=[Rckvn])
                    S.op("pe", lambda i=i: nc.tensor.transpose(out=PS[0:32, 7, 0:128], in_=CT[:, i, 128:160], identity=ident[:]), reads=[Rct, Rid], writes=[RPS[7]])
                    S.op("act", lambda i=i: nc.scalar.copy(out=KR[0:32, T + i * 128:T + (i + 1) * 128], in_=PS[0:32, 7, 0:128]), reads=[RPS[7]], writes=[Rkr])
            S.barrier()
            o = oB
            KN, o = view(o, [128, 2560], BF16)
            QN, o = view(o, [128, 2048], BF16)
            QR, o = view(o, [128, 2048], BF16)
            VA, o = view(o, [128, 20, 66], BF16)
            Rkn, Rqn, Rqr, Rva = Reg("m_KN"), Reg("m_QN"), Reg("m_QR"), Reg("m_VA")
            ow = WIN0
            TB2, ow2 = view(ow, [128, 2, 512])
            T2, ow2 = view(ow2, [128, 2, 512])
            PT, ow3 = view(ow, [128, 2, 512], BF16)
            OS, ow3 = view(ow3, [128, 512])
            OT, ow3 = view(ow3, [128, 512], BF16)
            Rtb2, Rt2, Rpt, Ros, Rot = Reg("m_TB2"), Reg("m_T2"), [Reg("m_PT0"), Reg("m_PT1")], Reg("m_OS"), Reg("m_OT")
            S.op("dve", lambda: nc.vector.memset(VA[:, :, 64:66], 1.0), writes=[Rva])
            KBW = 512
            for h in range(4):
                for kb in range((Skeys + KBW - 1) // KBW):
                    w = min(KBW, Skeys - kb * KBW)
                    bank = mc["pb"] % 4
                    mc["pb"] += 1
                    S.op("pe", lambda kb=kb, w=w, bank=bank: nc.tensor.matmul(PS[0:64, bank, 0:w], lhsT=WUKV[:, h * 128:h * 128 + 64], rhs=CKVN[:, kb * KBW:kb * KBW + w],
                                                                              start=True, stop=True), reads=[Rw, Rckvn], writes=[RPS[bank]])
                    S.op("act", lambda kb=kb, w=w, bank=bank: nc.scalar.copy(out=KN[0:64, kb * KBW:kb * KBW + w], in_=PS[0:64, bank, 0:w]), reads=[RPS[bank]], writes=[Rkn])
                for kt in range(NKT):
                    bank = mc["pb"] % 4
                    mc["pb"] += 1
                    S.op("pe", lambda kt=kt, bank=bank: nc.tensor.matmul(PS[:, bank, 0:64], lhsT=CKVN[:, kt * 128:(kt + 1) * 128], rhs=WUKV[:, h * 128 + 64:h * 128 + 128],
                                                                          start=True, stop=True), reads=[Rw, Rckvn], writes=[RPS[bank]])
                    S.op("dve", lambda kt=kt, bank=bank: nc.vector.tensor_copy(out=VA[:, kt, 0:64], in_=PS[:, bank, 0:64]), reads=[RPS[bank]], writes=[Rva])
                for b in range(NB):
                    cols = slice(b * BW, (b + 1) * BW)
                    bank = mc["pb"] % 4
                    mc["pb"] += 1

                    def emit_q(c0, m, bank, wt=None):
                        def f():
                            if wt is None:
                                nc.tensor.matmul(PS[0:m, bank, 0:BW], lhsT=WUQ[:, 0, c0:c0 + m], rhs=CQ[:, 0, cols], start=True, stop=False)
                                return nc.tensor.matmul(PS[0:m, bank, 0:BW], lhsT=WUQ[0:64, 1, c0:c0 + m], rhs=CQ[0:64, 1, cols], start=False, stop=True)
                            nc.tensor.matmul(PS[0:m, bank, 0:BW], lhsT=WUQS[:, 0, h, :], rhs=CQ[:, 0, cols], start=True, stop=False)
                            return nc.tensor.matmul(PS[0:m, bank, 0:BW], lhsT=WUQS[0:64, 1, h, :], rhs=CQ[0:64, 1, cols], start=False, stop=True)
                        return f
                    S.op("pe", emit_q(h * 96, 64, bank), reads=[Rw, Rcq], writes=[RPS[bank]])
                    S.op("act", lambda bank=bank: nc.scalar.copy(out=QN[0:64, cols], in_=PS[0:64, bank, 0:BW]), reads=[RPS[bank]], writes=[Rqn])
                    bank2 = mc["pb"] % 4
                    mc["pb"] += 1
                    S.op("pe", emit_q(h * 96 + 64, 32, bank2), reads=[Rw, Rcq], writes=[RPS[bank2]])
                    if sample:
                        bank3 = mc["pb"] % 4
                        mc["pb"] += 1
                        S.op("pe", emit_q(0, 32, bank3, wt=1), reads=[Rw, Rcq], writes=[RPS[bank3]])
                        S.dma("sp", TB2[0:32, 0, 0:BW], c_rope_mla[0, :, cols], writes=[Rtb2])
                        S.dma("sp", TB2[0:32, 1, 0:BW], c_rope_mla[1, :, cols], writes=[Rtb2])
                        S.op("dve", lambda: nc.vector.tensor_tensor(out=T2[0:32, 0, 0:BW], in0=PS[0:32, bank2, 0:BW], in1=TB2[0:32, 0, 0:BW], op=ALU.mult),
                             reads=[RPS[bank2], Rtb2], writes=[Rt2])
                        S.op("dve", lambda: nc.vector.tensor_tensor(out=T2[0:32, 1, 0:BW], in0=PS[0:32, bank3, 0:BW], in1=TB2[0:32, 1, 0:BW], op=ALU.mult),
                             reads=[RPS[bank3], Rtb2], writes=[Rt2])
                        S.op("dve", lambda: nc.vector.tensor_tensor(out=QR[0:32, cols], in0=T2[0:32, 0, 0:BW], in1=T2[0:32, 1, 0:BW], op=ALU.add),
                             reads=[Rt2], writes=[Rqr])
                    else:
                        S.op("act", lambda: nc.scalar.copy(out=QR[0:32, cols], in_=PS[0:32, bank2, 0:BW]), reads=[RPS[bank2]], writes=[Rqr])
                S.barrier()
                for b in range(NB):
                    cols = slice(b * BW, (b + 1) * BW)
                    ob = 4 + (b % 2)
                    for kt in range(NKT):
                        bank = mc["pb"] % 4
                        mc["pb"] += 1
                        pi = kt % 2

                        def emit_s(kt=kt, bank=bank):
                            nc.tensor.matmul(PS[:, bank, 0:BW], lhsT=KN[0:64, kt * 128:(kt + 1) * 128], rhs=QN[0:64, cols], start=True, stop=False)
                            return nc.tensor.matmul(PS[:, bank, 0:BW], lhsT=KR[0:32, kt * 128:(kt + 1) * 128], rhs=QR[0:32, cols], start=False, stop=True)
                        S.op("pe", emit_s, reads=[Rkn, Rqn, Rkr, Rqr], writes=[RPS[bank]])
                        S.op("act", lambda bank=bank, pi=pi: nc.scalar.activation(out=PT[:, pi, 0:BW], in_=PS[:, bank, 0:BW], func=AF.Exp, scale=ATT_SCALE),
                             reads=[RPS[bank]], writes=[Rpt[pi]])
                        S.op("pe", lambda kt=kt, pi=pi: nc.tensor.matmul(PS[0:65, ob, 0:BW], lhsT=VA[:, kt, 0:65], rhs=PT[:, pi, 0:BW], start=(kt == 0), stop=(kt == NKT - 1)),
                             reads=[Rva, Rpt[pi]], writes=[RPS[ob]])
                    S.op("act", lambda: nc.scalar.copy(out=OS[0:65, 0:BW], in_=PS[0:65, ob, 0:BW]), reads=[RPS[ob]], writes=[Ros])
                    S.op("dve", lambda: nc.vector.reciprocal(out=OS[64:65, 0:BW], in_=OS[64:65, 0:BW]), reads=[Ros], writes=[Ros])
                    S.op("pe", lambda: nc.tensor.matmul(PS[0:64, 6, 0:BW], lhsT=ones32[64:65, 0:64], rhs=OS[64:65, 0:BW], start=True, stop=True),
                         reads=[Ros, Rones], writes=[RPS[6]])
                    S.op("dve", lambda: nc.vector.tensor_tensor(out=OT[0:64, 0:BW], in0=PS[0:64, 6, 0:BW], in1=OS[0:64, 0:BW], op=ALU.mult),
                         reads=[RPS[6], Ros], writes=[Rot])
                    S.dma("sp", BR[(h % 2) * 64:(h % 2) * 64 + 64, 6 + h // 2, cols], OT[0:64, 0:BW], reads=[Rot], writes=[Rbr[6 + h // 2]], key=Reg("m_OTd"))
                S.barrier()

        def mla_cache_out(l, t0, ntile, pi):
            o = BS0
            CA, o = view(o, [128, 2, 160])
            KVB, o = view(o, [128, 128])
            Rca, Rkvb = [Reg("m_CA0"), Reg("m_CA1")], Reg("m_KVB")
            winv = w_in[l].rearrange("(kc p) n -> p kc n", p=128)
            S.dma("pool", WIN[:, 0, :, 0:160], winv[:, :, 2240:2400], writes=[Rwin[0]])
            S.dma("sp", KVB[:, :], mla_kv_norm[l:l + 1, :].broadcast_to([128, 128]), writes=[Rkvb])
            for tt in range(ntile):
                bank = mc["pb"] % 4
                mc["pb"] += 1
                smi = rot["sm"] % 4
                rot["sm"] += 1
                sm = small[:, smi, :]

                def emit(tt=tt, bank=bank):
                    inst = None
                    for kc in range(8):
                        inst = nc.tensor.matmul(PS[:, bank, 0:160], lhsT=HT[:, kc, tt * 128:(tt + 1) * 128], rhs=WIN[:, 0, kc, 0:160], start=(kc == 0), stop=(kc == 7))
                    return inst
                S.op("pe", emit, reads=[Rwin[0], Rht], writes=[RPS[bank]])
                S.op("act", lambda: nc.scalar.activation(out=junk[:, 0:128], in_=PS[:, bank, 0:128], func=AF.Square, accum_out=sm[:, 0:1]),
                     reads=[RPS[bank]], writes=[Rjunk, Rsmall[smi]])
                S.op("act", lambda: nc.scalar.activation(out=sm[:, 1:2], in_=sm[:, 0:1], func=AF.Sqrt, scale=1.0 / 128, bias=epsb[:, 0:1]),
                     reads=[Rsmall[smi], Reps], writes=[Rsmall[smi]])
                S.op("dve", lambda: nc.vector.reciprocal(out=sm[:, 2:3], in_=sm[:, 1:2]), reads=[Rsmall[smi]], writes=[Rsmall[smi]])
                ci_ = tt % 2
                S.op("dve", lambda: nc.vector.scalar_tensor_tensor(out=CA[:, ci_, 0:128], in0=PS[:, bank, 0:128], scalar=sm[:, 2:3], in1=KVB[:, :], op0=ALU.mult, op1=ALU.mult),
                     reads=[RPS[bank], Rsmall[smi], Rkvb], writes=[Rca[ci_]])
                S.op("dve", lambda: nc.vector.tensor_copy(out=CA[:, ci_, 128:160], in_=PS[:, bank, 128:160]), reads=[RPS[bank]], writes=[Rca[ci_]])
                OUT_EVS.append(S.dma("sp", o_mla[pi, l, tt * 128:(tt + 1) * 128, :], CA[:, ci_, :], reads=[Rca[ci_]], key=Reg("o_mla_d%d" % ci_)))

        def branch_ret(l, t0, T, kind):
            sample = kind == "sample"
            n = T // 128
            o = BS0
            WRb, o = view(o, [128, 8, 512], BF16)
            SBst, o = view(o, [128, 16, 256], BF16)
            DM, o = view(o, [128, 4, 128], BF16)
            QD, o = view(o, [128, 2, 4, 128], BF16)
            CD, o = view(o, [128, 2, 256])
            LG, o = view(o, [128, 8])
            KD, o = view(o, [128, 2, 4])
            SF, o = view(o, [128, 256])
            SB, o = view(o, [128, 256])
            SFb, o = view(o, [128, 256], BF16)
            oT = o
            WRa, _ = view(WIN0, [128, 8, 512], BF16)
            Rwr, Rsbst, Rtab, Rsf, Rsb, Rsfb = Reg("r_WR"), Reg("r_SBst"), Reg("r_TAB"), Reg("r_SF"), Reg("r_SB"), Reg("r_SFb")
            winv = w_in[l].rearrange("(kc p) n -> p kc n", p=128)
            S.dma("pool", WRa[:, :, :], winv[:, :, 256:768], writes=[Rwr])
            S.dma("pool", WRb[:, :, :], winv[:, :, 768:1280], writes=[Rwr])
            CT6, o2 = view(oT, [128, 6, 128])
            E1, o2 = view(o2, [128, 2, 128])
            PIDX, o2 = view(o2, [128, 2])
            C128, o2 = view(o2, [128, 64])
            Rc6, Re1 = Reg("r_C6"), Reg("r_E1")
            S.dma("sp", CT6[:, :, :], c_ret.rearrange("k p i -> p k i"), writes=[Rc6])
            S.dma("sp", PIDX[:, :], c_pidx[:, :], writes=[Rc6])
            S.dma("sp", LG[:, :], ret_decay[l:l + 1].rearrange("o d h -> o (d h)").broadcast_to([128, 8]), writes=[Rtab])
            S.op("dve", lambda: nc.vector.memset(C128[:, :], 128.0), writes=[Rc6])
            S.op("act", lambda: nc.scalar.activation(out=LG[:, :], in_=LG[:, :], func=AF.Sigmoid), reads=[Rtab], writes=[Rtab])
            S.op("act", lambda: nc.scalar.activation(out=LG[:, :], in_=LG[:, :], func=AF.Ln), reads=[Rtab], writes=[Rtab])
            for h in range(4):
                S.op("act", lambda h=h: nc.scalar.activation(out=E1[:, 0, :], in_=CT6[:, 0, :], func=AF.Exp, scale=LG[:, h:h + 1]), reads=[Rc6, Rtab], writes=[Re1])
                S.op("act", lambda h=h: nc.scalar.activation(out=E1[:, 1, :], in_=CT6[:, 1, :], func=AF.Exp, scale=LG[:, 4 + h:5 + h]), reads=[Rc6, Rtab], writes=[Re1])
                S.op("dve", lambda h=h: nc.vector.tensor_tensor(out=E1[:, :, :], in0=E1[:, :, :], in1=CT6[:, 2:4, :], op=ALU.mult), reads=[Re1, Rc6], writes=[Re1])
                S.op("dve", lambda h=h: nc.vector.tensor_tensor(out=DM[:, h, :], in0=E1[:, 0, :], in1=E1[:, 1, :], op=ALU.add), reads=[Re1], writes=[Rtab])
                for d in range(2):
                    S.op("act", lambda h=h, d=d: nc.scalar.activation(out=QD[:, d, h, :], in_=CT6[:, 4 + d, :], func=AF.Exp, scale=LG[:, d * 4 + h:d * 4 + h + 1]),
                         reads=[Rc6, Rtab], writes=[Rtab])
                    S.op("act", lambda h=h, d=d: nc.scalar.activation(out=CD[:, d, h * 64:(h + 1) * 64], in_=C128[:, :], func=AF.Exp, scale=LG[:, d * 4 + h:d * 4 + h + 1]),
                         reads=[Rc6, Rtab], writes=[Rtab])
                    S.op("act", lambda h=h, d=d: nc.scalar.activation(out=KD[:, d, h:h + 1], in_=PIDX[:, d:d + 1], func=AF.Exp, scale=LG[:, d * 4 + h:d * 4 + h + 1]),
                         reads=[Rc6, Rtab], writes=[Rtab])
            S.op("dve", lambda: nc.vector.tensor_scalar(out=KD[:, :, :], in0=KD[:, :, :], scalar1=0.125, scalar2=None, op0=ALU.mult), reads=[Rtab], writes=[Rtab])
            if sample:
                S.dma("sp", SF[0:64, :].rearrange("d (h e) -> d h e", h=4), st_ret[l, 0].rearrange("h d e -> d h e"), writes=[Rsf])
                S.dma("sp", SB[0:64, :].rearrange("d (h e) -> d h e", h=4), st_ret[l, 1].rearrange("h d e -> d h e"), writes=[Rsb])
            else:
                S.op("dve", lambda: nc.vector.memset(SF[0:64, :], 0.0), writes=[Rsf])
                S.op("dve", lambda: nc.vector.memset(SB[0:64, :], 0.0), writes=[Rsb])
            S.barrier()
            if cfg.get("ret_stop") == "tables":
                return
            o3 = oT
            QK, o3 = view(o3, [128, 512], BF16)
            TA, o3 = view(o3, [128, 256])
            TBt, o3 = view(o3, [128, 256])
            RT, o3 = view(o3, [128, 2, 32])
            KDt, o3 = view(o3, [128, 256], BF16)
            VTc, o3 = view(o3, [128, 256], BF16)
            SRG, o3 = view(o3, [128, 256], BF16)
            QT, o3 = view(o3, [128, 3, 512], BF16)
            KT, o3 = view(o3, [128, 512], BF16)
            AM, o3 = view(o3, [128, 512], BF16)
            CEN, o3 = view(o3, [128, 256])
            SQr, o3 = view(o3, [128, 256])
            NRo, o3 = view(o3, [128, 256], BF16)
            MS, o3 = view(o3, [128, 8])
            Rqk, Rta, Rrt, Rkd, Rvt, Rsrg, Rqt, Rkt, Ram, Rcen, Rsq, Rnro, Rms = (Reg("r_" + x) for x in
                ("QK", "TA", "RT", "KDt", "VTc", "SRG", "QT", "KT", "AM", "CEN", "SQ", "NRo", "MS"))
            PSb = lambda bank: PS[:, bank, :].bitcast(BF16)

            def proj(c, bank, WRx, c0, ncol):
                def emit():
                    inst = None
                    for kc in range(8):
                        inst = nc.tensor.matmul(PS[:, bank, 0:ncol], lhsT=HT[:, kc, c * 128:(c + 1) * 128], rhs=WRx[:, kc, c0:c0 + ncol], start=(kc == 0), stop=(kc == 7))
                    return inst
                S.op("pe", emit, reads=[Rwr, Rht], writes=[RPS[bank]])

            def rope(c, bank, col0, ng, dst):
                src = PS[:, bank, col0:col0 + ng * 64].rearrange("p (g t e) -> p g t e", g=ng, t=2)
                dv = dst.rearrange("p (g t e) -> p g t e", g=ng, t=2)
                if not sample:
                    S.op("act", lambda: nc.scalar.copy(out=dst, in_=PS[:, bank, col0:col0 + ng * 64]), reads=[RPS[bank]], writes=[Rqk])
                    return
                S.dma("sp", RT[:, 0, :], c_rope_ret[0, c * 128:(c + 1) * 128, :], writes=[Rrt])
                S.dma("sp", RT[:, 1, :], c_rope_ret[1, c * 128:(c + 1) * 128, :], writes=[Rrt])
                cosb = RT[:, 0, :].unsqueeze(1).broadcast_to([128, ng, 32])
                sinb = RT[:, 1, :].unsqueeze(1).broadcast_to([128, ng, 32])
                ta = TA[:, 0:ng * 32].rearrange("p (g e) -> p g e", g=ng)
                tb = TBt[:, 0:ng * 32].rearrange("p (g e) -> p g e", g=ng)
                S.op("dve", lambda: nc.vector.tensor_tensor(out=ta, in0=src[:, :, 0, :], in1=cosb, op=ALU.mult), reads=[RPS[bank], Rrt], writes=[Rta])
                S.op("dve", lambda: nc.vector.tensor_tensor(out=tb, in0=src[:, :, 1, :], in1=sinb, op=ALU.mult), reads=[RPS[bank], Rrt], writes=[Rta])
                S.op("dve", lambda: nc.vector.tensor_tensor(out=dv[:, :, 0, :], in0=ta, in1=tb, op=ALU.subtract), reads=[Rta], writes=[Rqk])
                S.op("dve", lambda: nc.vector.tensor_tensor(out=ta, in0=src[:, :, 0, :], in1=sinb, op=ALU.mult), reads=[RPS[bank], Rrt], writes=[Rta])
                S.op("dve", lambda: nc.vector.tensor_tensor(out=tb, in0=src[:, :, 1, :], in1=cosb, op=ALU.mult), reads=[RPS[bank], Rrt], writes=[Rta])
                S.op("dve", lambda: nc.vector.tensor_tensor(out=dv[:, :, 1, :], in0=ta, in1=tb, op=ALU.add), reads=[Rta], writes=[Rqk])

            def kdec_mul(d, ksrc):
                S.op("dve", lambda: nc.vector.tensor_tensor(out=KDt[:, :].rearrange("p (h e) -> p h e", h=4), in0=ksrc.rearrange("p (h e) -> p h e", h=4),
                                                            in1=KD[:, d, :].unsqueeze(2).broadcast_to([128, 4, 64]), op=ALU.mult), reads=[Rqk, Rtab], writes=[Rkd])

            def umat(bank):
                def emit():
                    inst = None
                    for h in range(4):
                        inst = nc.tensor.matmul(PS[0:64, bank, h * 64:(h + 1) * 64], lhsT=KDt[:, h * 64:(h + 1) * 64], rhs=VTc[:, h * 64:(h + 1) * 64], start=True, stop=True)
                    return inst
                S.op("pe", emit, reads=[Rkd, Rvt], writes=[RPS[bank]])

            def state_update(St, Rst, d, bank):
                S.op("dve", lambda: nc.vector.tensor_tensor(out=St[0:64, :], in0=St[0:64, :], in1=CD[0:64, d, :], op=ALU.mult), reads=[Rst, Rtab], writes=[Rst])
                S.op("dve", lambda: nc.vector.tensor_tensor(out=St[0:64, :], in0=St[0:64, :], in1=PS[0:64, bank, 0:256], op=ALU.add), reads=[Rst, RPS[bank]], writes=[Rst])

            for c in range(n - 1, -1, -1):
                proj(c, 0, WRa, 256, 256)
                proj(c, 1, WRb, 0, 256)
                rope(c, 0, 0, 4, QK[:, 0:256])
                S.op("act", lambda: nc.scalar.copy(out=VTc[:, :], in_=PS[:, 1, 0:256]), reads=[RPS[1]], writes=[Rvt])
                kdec_mul(1, QK[:, 0:256])
                umat(5)
                S.op("act", lambda c=c: nc.scalar.copy(out=SBst[0:64, c, :], in_=SB[0:64, :]), reads=[Rsb], writes=[Rsbst])
                state_update(SB, Rsb, 1, 5)
            S.op("act", lambda: nc.scalar.copy(out=SFb[0:64, :], in_=SF[0:64, :]), reads=[Rsf], writes=[Rsfb])
            if cfg.get("ret_stop") == "pass1":
                return
            for c in range(n):
                proj(c, 0, WRa, 0, 512)
                proj(c, 1, WRb, 0, 512)
                rope(c, 0, 0, 8, QK[:, :])
                S.op("act", lambda: nc.scalar.copy(out=VTc[:, :], in_=PS[:, 1, 0:256]), reads=[RPS[1]], writes=[Rvt])
                S.op("act", lambda: nc.scalar.activation(out=SRG[:, :], in_=PS[:, 1, 256:512], func=AF.Silu), reads=[RPS[1]], writes=[Rsrg])
                kdec_mul(0, QK[:, 256:512])

                def emit_t():
                    inst = None
                    for g in range(8):
                        inst = nc.tensor.transpose(out=PSb(2)[0:64, g * 128:(g + 1) * 128], in_=QK[:, g * 64:(g + 1) * 64], identity=identb[:])
                    return inst
                S.op("pe", emit_t, reads=[Rqk, Rid], writes=[RPS[2]])
                S.op("dve", lambda: nc.vector.tensor_copy(out=QT[0:64, 0, :], in_=PSb(2)[0:64, 0:512]), reads=[RPS[2]], writes=[Rqt])
                for d in range(2):
                    S.op("dve", lambda d=d: nc.vector.tensor_tensor(out=QT[0:64, 1 + d, :], in0=PSb(2)[0:64, 0:512], in1=QD[0:64, d, :, :].rearrange("p h i -> p (h i)"), op=ALU.mult),
                         reads=[RPS[2], Rtab], writes=[Rqt])
                S.op("dve", lambda: nc.vector.tensor_copy(out=KT[0:64, :], in_=PSb(2)[0:64, 512:1024]), reads=[RPS[2]], writes=[Rkt])
                if cfg.get("ret_stop") == "p2a":
                    continue

                def emit_a():
                    inst = None
                    for h in range(4):
                        inst = nc.tensor.matmul(PS[:, 3, h * 128:(h + 1) * 128], lhsT=KT[0:64, h * 128:(h + 1) * 128], rhs=QT[0:64, 0, h * 128:(h + 1) * 128], start=True, stop=True)
                    return inst
                S.op("pe", emit_a, reads=[Rkt, Rqt], writes=[RPS[3]])
                S.op("dve", lambda: nc.vector.tensor_tensor(out=AM[:, :], in0=PS[:, 3, :], in1=DM[:, :, :].rearrange("p h i -> p (h i)"), op=ALU.mult),
                     reads=[RPS[3], Rtab], writes=[Ram])
                if cfg.get("ret_stop") == "p2b":
                    continue

                def emit_o(c=c):
                    inst = None
                    for h in range(4):
                        oc = PS[:, 4, h * 64:(h + 1) * 64]
                        nc.tensor.matmul(oc, lhsT=AM[:, h * 128:(h + 1) * 128], rhs=VTc[:, h * 64:(h + 1) * 64], start=True, stop=False)
                        nc.tensor.matmul(oc, lhsT=QT[0:64, 1, h * 128:(h + 1) * 128], rhs=SFb[0:64, h * 64:(h + 1) * 64], start=False, stop=False)
                        inst = nc.tensor.matmul(oc, lhsT=QT[0:64, 2, h * 128:(h + 1) * 128], rhs=SBst[0:64, c, h * 64:(h + 1) * 64], start=False, stop=True)
                    return inst
                S.op("pe", emit_o, reads=[Ram, Rvt, Rqt, Rsfb, Rsbst], writes=[RPS[4]])
                umat(5)
                state_update(SF, Rsf, 0, 5)
                S.op("act", lambda: nc.scalar.copy(out=SFb[0:64, :], in_=SF[0:64, :]), reads=[Rsf], writes=[Rsfb])
                if cfg.get("ret_stop") == "p2c":
                    continue
                ov = PS[:, 4, 0:256].rearrange("p (h e) -> p h e", h=4)
                S.op("dve", lambda: nc.vector.tensor_reduce(out=MS[:, 0:4], in_=ov, axis=AX.X, op=ALU.add), reads=[RPS[4]], writes=[Rms])
                S.op("dve", lambda: nc.vector.tensor_scalar(out=MS[:, 0:4], in0=MS[:, 0:4], scalar1=-1.0 / 64, scalar2=None, op0=ALU.mult), reads=[Rms], writes=[Rms])
                cv = CEN[:, :].rearrange("p (h e) -> p h e", h=4)
                S.op("dve", lambda: nc.vector.tensor_tensor(out=cv, in0=ov, in1=MS[:, 0:4].unsqueeze(2).broadcast_to([128, 4, 64]), op=ALU.add),
                     reads=[RPS[4], Rms], writes=[Rcen])
                S.op("dve", lambda: nc.vector.tensor_tensor(out=SQr[:, :], in0=CEN[:, :], in1=CEN[:, :], op=ALU.mult), reads=[Rcen], writes=[Rsq])
                S.op("dve", lambda: nc.vector.tensor_reduce(out=MS[:, 4:8], in_=SQr[:, :].rearrange("p (h e) -> p h e", h=4), axis=AX.X, op=ALU.add), reads=[Rsq], writes=[Rms])
                S.op("act", lambda: nc.scalar.activation(out=MS[:, 4:8], in_=MS[:, 4:8], func=AF.Sqrt, scale=1.0 / 64, bias=epsb[:, 0:1]), reads=[Rms, Reps], writes=[Rms])
                S.op("dve", lambda: nc.vector.reciprocal(out=MS[:, 4:8], in_=MS[:, 4:8]), reads=[Rms], writes=[Rms])
                S.op("dve", lambda: nc.vector.tensor_tensor(out=cv, in0=cv, in1=MS[:, 4:8].unsqueeze(2).broadcast_to([128, 4, 64]), op=ALU.mult), reads=[Rcen, Rms], writes=[Rcen])
                S.op("dve", lambda: nc.vector.tensor_tensor(out=NRo[:, :], in0=CEN[:, :], in1=SRG[:, :], op=ALU.mult), reads=[Rcen, Rsrg], writes=[Rnro])

                if cfg.get("ret_stop") == "p2d":
                    continue

                def emit_t2():
                    inst = None
                    for cc in range(2):
                        inst = nc.tensor.transpose(out=PSb(6)[:, cc * 128:(cc + 1) * 128], in_=NRo[:, cc * 128:(cc + 1) * 128], identity=identb[:])
                    return inst
                S.op("pe", emit_t2, reads=[Rnro, Rid], writes=[RPS[6]])
                for cc in range(2):
                    S.op("dve", lambda cc=cc, c=c: nc.vector.tensor_scalar(out=BR[:, 2 + cc, c * 128:(c + 1) * 128], in0=PSb(6)[:, cc * 128:(cc + 1) * 128],
                                                                          scalar1=PV[:, 8 + cc:9 + cc], scalar2=None, op0=ALU.mult), reads=[RPS[6], Rpv], writes=[Rbr[2 + cc]])
            if not sample:
                pi = 0 if kind == "pA" else 1
                OUT_EVS.append(S.dma("sp", o_ret[pi, l, 0].rearrange("h d e -> d h e"), SF[0:64, :].rearrange("d (h e) -> d h e", h=4), reads=[Rsf], key=Reg("o_ret_d")))
                OUT_EVS.append(S.dma("sp", o_ret[pi, l, 1].rearrange("h d e -> d h e"), SB[0:64, :].rearrange("d (h e) -> d h e", h=4), reads=[Rsb], key=Reg("o_ret_d")))

        def branch_s5(l, t0, T, NB, BW, kind):
            sample = kind == "sample"
            o = BS0
            UT, o = view(o, [128, 2, 2048], BF16)
            YS, o = view(o, [128, 2, 2048], BF16)
            YFp, o = view(o, [128, 2048], BF16)
            oZ = o
            TRI, o = view(o, [128, 2, 512])
            TRIb, o = view(o, [128, 2, 512], BF16)
            BZ, o = view(o, [128, 2, 512], BF16)
            oTT = o
            TT, o = view(o, [128, 2, 1024], BF16)
            TTf, _ = view(oTT, [128, 2, 512])
            oS = o
            ow = WIN0
            SBb, ow = view(ow, [128, 2, 512], BF16)
            OTs, ow = view(ow, [128, 512], BF16)
            BW_, ow = view(ow, [128, 2, 2, 128], BF16)
            BBR, ow = view(ow, [128, 16, 16])
            BBI, ow = view(ow, [128, 16, 16])
            CW, ow = view(ow, [128, 8, 2, 32], BF16)
            WGL, ow = view(ow, [128, 2, 256], BF16)
            YV, _ = view(WIN0, [128, 512])
            Rut, Rys, Rsfs, Rtri, Rbz, Rtt, Rsbb, Rots, Rrb, Rbw = (Reg("s_" + x) for x in ("UT", "YS", "YFp", "TRI", "BZ", "TT", "SBb", "OTs", "RB", "BW"))
            def sm_(shape, dt=F32):
                nonlocal o
                v, o = view(o, shape, dt)
                return v
            o1 = [oTT]

            def ot_(shape, dt=F32):
                v, o1[0] = view(o1[0], shape, dt)
                return v
            LRE, LIM, LDT, AR, AI, FR, FI = (ot_([128, 16]) for _ in range(7))
            BRE, BIM = ot_([128, 16, 16]), ot_([128, 16, 16])
            CNAT = ot_([128, 2, 64])
            MAG, UR, UI, W1, W2_, W3 = (sm_([128, 16]) for _ in range(6))
            UBR, UBI = sm_([128, 16]), sm_([128, 16])
            TA_, TBs = sm_([128, 2, 16, 16]), sm_([128, 2, 16, 32])
            PW = sm_([128, 2, 16])
            S0t = sm_([128, 16, 2])
            INI = sm_([128, 2, 2])
            FIN = sm_([128, 16, 2])
            WP = sm_([128, 128])
            Rsu = Reg("s_setup")
            Rini, Rfin, Rwp, Rcn = Reg("s_INI"), Reg("s_FIN"), Reg("s_WP"), Reg("s_CN")
            Rbu = Reg("s_BU")
            V = nc.vector
            dbgon = cfg.get("s5dbg") == kind
            if dbgon:
                dbg2 = nc.dram_tensor("dbg2", [128, 4096], F32, kind="ExternalOutput").ap()

            def dbg(ap, c0, n, regs):
                if dbgon:
                    S.dma("sp", dbg2[:, c0:c0 + n], ap, reads=regs, key=Reg("dbg2"))

            def dv(fn, reads, writes):
                S.op("dve", fn, reads=reads, writes=writes)

            def tt(out, a, b, op, reads=(Rsu,), writes=(Rsu,)):
                dv(lambda: V.tensor_tensor(out=out, in0=a, in1=b, op=op), list(reads), list(writes))

            def cmul(orr, oi, ar, ai, br, bi, t1, t2, reads=(Rsu,), writes=(Rsu,)):
                tt(t1, ar, br, ALU.mult, reads, writes)
                tt(t2, ai, bi, ALU.mult, reads, writes)
                tt(t2, t1, t2, ALU.subtract, reads, writes)
                tt(t1, ar, bi, ALU.mult, reads, writes)
                tt(oi, ai, br, ALU.mult, reads, writes)
                tt(oi, t1, oi, ALU.add, reads, writes)
                tt(orr, t2, t2, ALU.max, reads, writes)

            for cc in range(2):
                proj_fm(l, cc * 128, 128, NB, BW, lambda b, bank, cc=cc: S.op(
                    "act", lambda: nc.scalar.copy(out=UT[:, cc, b * BW:(b + 1) * BW], in_=PS[:, bank, 0:BW]), reads=[RPS[bank]], writes=[Rut]))
            S.barrier()
            for d in range(2):
                for dst, src in ((LRE, s5_lam_re), (LIM, s5_lam_im)):
                    S.dma("sp", dst[:, d::2], src[l, d].rearrange("(m g) p -> (g p) m", g=2), writes=[Rsu], slow=True)
                for g2 in range(2):
                    S.dma("sp", LDT[g2 * 64:(g2 + 1) * 64, d::2], s5_log_dt[l, d:d + 1, g2::2].broadcast_to([64, 8]), writes=[Rsu], slow=True)
                for dst, src in ((BRE, s5_b_re), (BIM, s5_b_im)):
                    S.dma("sp", dst[:, d::2, :], src[l, d].rearrange("(m g) p h -> (g p) m h", g=2), writes=[Rsu])
                if sample:
                    S.dma("sp", S0t[:, d::2, :], st_s5[l, d].rearrange("(m g) p r -> (g p) m r", g=2), writes=[Rsu], slow=True)
            S.dma("pool", WGL[:, :, :], s5_w_glu[l].rearrange("(kc p) n -> p kc n", p=128), writes=[Rsu])
            S.op("act", lambda: nc.scalar.activation(out=LDT[:, :], in_=LDT[:, :], func=AF.Exp), reads=[Rsu], writes=[Rsu])
            tt(W1[:, :], LRE[:, :], LDT[:, :], ALU.mult)
            S.op("act", lambda: nc.scalar.activation(out=MAG[:, :], in_=W1[:, :], func=AF.Exp), reads=[Rsu], writes=[Rsu])
            tt(W1[:, :], LIM[:, :], LDT[:, :], ALU.mult)
            S.op("act", lambda: nc.scalar.activation(out=UI[:, :], in_=W1[:, :], func=AF.Sin, scale=1.0 / 64), reads=[Rsu], writes=[Rsu])
            S.op("act", lambda: nc.scalar.activation(out=UR[:, :], in_=W1[:, :], func=AF.Sin, scale=1.0 / 64, bias=halfpi[:, 0:1]), reads=[Rsu, Reps], writes=[Rsu])
            for _ in range(6):
                tt(W1[:, :], UR[:, :], UR[:, :], ALU.mult)
                tt(W2_[:, :], UI[:, :], UI[:, :], ALU.mult)
                tt(W3[:, :], UR[:, :], UI[:, :], ALU.mult)
                tt(UR[:, :], W1[:, :], W2_[:, :], ALU.subtract)
                tt(UI[:, :], W3[:, :], W3[:, :], ALU.add)
            tt(AR[:, :], MAG[:, :], UR[:, :], ALU.mult)
            tt(AI[:, :], MAG[:, :], UI[:, :], ALU.mult)
            tt(W1[:, :], LRE[:, :], LRE[:, :], ALU.mult)
            tt(W2_[:, :], LIM[:, :], LIM[:, :], ALU.mult)
            tt(W1[:, :], W1[:, :], W2_[:, :], ALU.add)
            dv(lambda: V.reciprocal(out=W1[:, :], in_=W1[:, :]), [Rsu], [Rsu])
            dv(lambda: V.tensor_scalar(out=W2_[:, :], in0=AR[:, :], scalar1=-1.0, scalar2=None, op0=ALU.add), [Rsu], [Rsu])
            tt(FR[:, :], W2_[:, :], LRE[:, :], ALU.mult)
            tt(W3[:, :], AI[:, :], LIM[:, :], ALU.mult)
            tt(FR[:, :], FR[:, :], W3[:, :], ALU.add)
            tt(FR[:, :], FR[:, :], W1[:, :], ALU.mult)
            tt(FI[:, :], AI[:, :], LRE[:, :], ALU.mult)
            tt(W3[:, :], W2_[:, :], LIM[:, :], ALU.mult)
            tt(FI[:, :], FI[:, :], W3[:, :], ALU.subtract)
            tt(FI[:, :], FI[:, :], W1[:, :], ALU.mult)
            dbg(MAG[:, :], 0, 16, [Rsu]); dbg(UR[:, :], 16, 16, [Rsu]); dbg(UI[:, :], 32, 16, [Rsu]); dbg(FR[:, :], 48, 16, [Rsu]); dbg(FI[:, :], 64, 16, [Rsu])
            frb = FR[:, :].unsqueeze(2).broadcast_to([128, 16, 16])
            fib = FI[:, :].unsqueeze(2).broadcast_to([128, 16, 16])
            tt(BBR[:, :, :], BRE[:, :, :], frb, ALU.mult)
            tt(BBI[:, :, :], BIM[:, :, :], fib, ALU.mult)
            tt(BBR[:, :, :], BBR[:, :, :], BBI[:, :, :], ALU.subtract)
            tt(BBI[:, :, :], BRE[:, :, :], fib, ALU.mult)
            tt(BRE[:, :, :], BIM[:, :, :], frb, ALU.mult)
            tt(BBI[:, :, :], BBI[:, :, :], BRE[:, :, :], ALU.add)
            dv(lambda: V.memset(CW[:, :, :, :], 0.0), [], [Rsu])
            for ri, src in ((0, s5_c_re), (1, s5_c_im)):
                S.dma("sp", CNAT[:, :, :], src[l].rearrange("(c g) h p -> (g h) c p", c=2), writes=[Rcn])
                CNB = TTf[:, 1, 0:64].bitcast(BF16)
                dv(lambda: V.tensor_copy(out=CNB.rearrange("p (c k) -> p c k", c=2), in_=CNAT[:, :, :]), [Rcn, Rtt], [Rtt])
                for c in range(2):
                    for half in range(2):
                        S.op("pe", lambda c=c, half=half: nc.tensor.matmul(PS[half * 64:(half + 1) * 64, 6, c * 128:(c + 1) * 128], lhsT=CNB[:, c * 64:(c + 1) * 64], rhs=identb[:, :],
                                                                           start=True, stop=True), reads=[Rtt, Rid], writes=[RPS[6]])
                ctv = PS[:, 6, 0:256].rearrange("q (m g h) -> q m g h", m=8, g=2)
                sc = 1.0 if ri == 0 else -1.0
                dv(lambda ri=ri, sc=sc: V.tensor_scalar(out=CW[0:64, :, ri, 0:16], in0=ctv[0:64, :, 0, :], scalar1=sc, scalar2=None, op0=ALU.mult), [RPS[6]], [Rsu])
                dv(lambda ri=ri, sc=sc: V.tensor_scalar(out=CW[64:128, :, ri, 16:32], in0=ctv[64:128, :, 1, :], scalar1=sc, scalar2=None, op0=ALU.mult), [RPS[6]], [Rsu])
            S.barrier()
            def build_pows(TAB, nent, base_r, base_i):
                dv(lambda: V.memset(TAB[:, 0, :, 0:1], 1.0), [], [Rsu])
                dv(lambda: V.memset(TAB[:, 1, :, 0:1], 0.0), [], [Rsu])
                tt(PW[:, 0, :], base_r, base_r, ALU.max)
                tt(PW[:, 1, :], base_i, base_i, ALU.max)
                nn = 1
                while nn < nent:
                    pr = PW[:, 0, :].unsqueeze(2).broadcast_to([128, 16, nn])
                    pi_ = PW[:, 1, :].unsqueeze(2).broadcast_to([128, 16, nn])
                    t1 = TTf[:, 0, 0:16 * nn].rearrange("p (k j) -> p k j", k=16)
                    t2 = TTf[:, 1, 0:16 * nn].rearrange("p (k j) -> p k j", k=16)
                    rr, ri = Rsu, Rtt
                    tt(t1, TAB[:, 0, :, 0:nn], pr, ALU.mult, (rr, ri), (ri,))
                    tt(t2, TAB[:, 1, :, 0:nn], pi_, ALU.mult, (rr, ri), (ri,))
                    tt(TAB[:, 0, :, nn:2 * nn], t1, t2, ALU.subtract, (rr, ri), (rr,))
                    tt(t1, TAB[:, 0, :, 0:nn], pi_, ALU.mult, (rr, ri), (ri,))
                    tt(t2, TAB[:, 1, :, 0:nn], pr, ALU.mult, (rr, ri), (ri,))
                    tt(TAB[:, 1, :, nn:2 * nn], t1, t2, ALU.add, (rr, ri), (rr,))
                    tt(W1[:, :], PW[:, 0, :], PW[:, 0, :], ALU.mult)
                    tt(W2_[:, :], PW[:, 1, :], PW[:, 1, :], ALU.mult)
                    tt(W3[:, :], PW[:, 0, :], PW[:, 1, :], ALU.mult)
                    tt(PW[:, 0, :], W1[:, :], W2_[:, :], ALU.subtract)
                    tt(PW[:, 1, :], W3[:, :], W3[:, :], ALU.add)
                    nn *= 2
            build_pows(TBs, 32, UR[:, :], UI[:, :])
            tt(W1[:, :], PW[:, 0, :], PW[:, 0, :], ALU.max)
            tt(W2_[:, :], PW[:, 1, :], PW[:, 1, :], ALU.max)
            tt(UBR[:, :], PW[:, 0, :], PW[:, 0, :], ALU.max)
            tt(UBI[:, :], PW[:, 1, :], PW[:, 1, :], ALU.max)
            build_pows(TA_, 16, UBR[:, :], UBI[:, :])
            if BW == 512:
                tt(UBR[:, :], PW[:, 0, :], PW[:, 0, :], ALU.max)
                tt(UBI[:, :], PW[:, 1, :], PW[:, 1, :], ALU.max)
            else:
                tt(UBR[:, :], TA_[:, 0, :, 8], TA_[:, 0, :, 8], ALU.max)
                tt(UBI[:, :], TA_[:, 1, :, 8], TA_[:, 1, :, 8], ALU.max)
            S.barrier()
            mcb = [0]
            for m in range(8):
                cc, m4 = m // 4, m % 4
                for d in range(2):
                    k = m * 2 + d
                    for ri, BB in ((0, BBR), (1, BBI)):
                        dv(lambda: V.memset(WP[:, :], 0.0), [Rwp], [Rwp])
                        dv(lambda BB=BB, k=k: V.tensor_copy(out=WP[0:64, m4 * 32:m4 * 32 + 16], in_=BB[0:64, k, :]), [Rsu, Rwp], [Rwp])
                        dv(lambda BB=BB, k=k: V.tensor_copy(out=WP[64:128, m4 * 32 + 16:m4 * 32 + 32], in_=BB[64:128, k, :]), [Rsu, Rwp], [Rwp])
                        S.op("pe", lambda: nc.tensor.transpose(out=PS[:, 7, 0:128], in_=WP[:, :], identity=ident[:]), reads=[Rwp, Rid], writes=[RPS[7]])
                        S.op("act", lambda d=d, ri=ri: nc.scalar.copy(out=BW_[:, d, ri, :], in_=PS[:, 7, 0:128]), reads=[RPS[7]], writes=[Rbw])
                for d in range(2):
                    k = m * 2 + d
                    rev = d == 1
                    ar = TA_[:, 0, k, :].unsqueeze(2).broadcast_to([128, 16, 32])
                    ai = TA_[:, 1, k, :].unsqueeze(2).broadcast_to([128, 16, 32])
                    br = TBs[:, 0, k, :].unsqueeze(1).broadcast_to([128, 16, 32])
                    bi = TBs[:, 1, k, :].unsqueeze(1).broadcast_to([128, 16, 32])
                    trv = TRI[:, 0, :].rearrange("p (q j) -> p q j", q=16)
                    tiv = TRI[:, 1, :].rearrange("p (q j) -> p q j", q=16)
                    t1 = TTf[:, 0, :].rearrange("p (q j) -> p q j", q=16)
                    t2 = TTf[:, 1, :].rearrange("p (q j) -> p q j", q=16)
                    rw = (Rsu, Rtt, Rtri)
                    tt(t1, ar, br, ALU.mult, rw, (Rtt,))
                    tt(t2, ai, bi, ALU.mult, rw, (Rtt,))
                    tt(trv, t1, t2, ALU.subtract, rw, (Rtri,))
                    tt(t1, ar, bi, ALU.mult, rw, (Rtt,))
                    tt(t2, ai, br, ALU.mult, rw, (Rtt,))
                    tt(tiv, t1, t2, ALU.add, rw, (Rtri,))
                    dv(lambda: V.tensor_copy(out=TRIb[:, :, :], in_=TRI[:, :, :]), [Rtri], [Rtri])
                    if k == 0:
                        dbg(TRI[:, 0, :], 128, 512, [Rtri]); dbg(TRI[:, 1, :], 640, 512, [Rtri])
                    ib = 0
                    if sample:
                        cmul(INI[:, 0, ib:ib + 1], INI[:, 1, ib:ib + 1], UR[:, k:k + 1], UI[:, k:k + 1], S0t[:, k, 0:1], S0t[:, k, 1:2], W1[:, 0:1], W2_[:, 0:1], (Rsu, Rini), (Rsu, Rini))
                    else:
                        dv(lambda: V.memset(INI[:, :, 0:1], 0.0), [Rini], [Rini])
                    blocks = list(range(NB - 1, -1, -1)) if rev else list(range(NB))
                    for bi_, b in enumerate(blocks):
                        cols = slice(b * BW, (b + 1) * BW)
                        bk = 2 * (mcb[0] % 2)
                        mcb[0] += 1
                        for ri in range(2):
                            S.op("pe", lambda ri=ri: nc.tensor.matmul(PS[:, bk + ri, 0:BW], lhsT=BW_[:, d, ri, :], rhs=UT[:, cc, cols], start=True, stop=True),
                                 reads=[Rbw, Rut], writes=[RPS[bk + ri]])
                        if rev:
                            trr, tri = TRI[:, 0, BW - 1::-1] if BW == 512 else TRI[:, 0, BW - 1::-1], TRI[:, 1, BW - 1::-1]
                            trr = TRI[:, 0, 0:BW][:, ::-1]
                            tri = TRI[:, 1, 0:BW][:, ::-1]
                            trrb = TRIb[:, 0, 0:BW][:, ::-1]
                            trib = TRIb[:, 1, 0:BW][:, ::-1]
                        else:
                            trr, tri = TRI[:, 0, 0:BW], TRI[:, 1, 0:BW]
                            trrb, trib = TRIb[:, 0, 0:BW], TRIb[:, 1, 0:BW]
                        for ri in range(2):
                            S.op("act", lambda ri=ri: nc.scalar.copy(out=SBb[:, ri, 0:BW], in_=PS[:, bk + ri, 0:BW]), reads=[RPS[bk + ri]], writes=[Rsbb])
                        pre, pim = SBb[:, 0, 0:BW], SBb[:, 1, 0:BW]
                        rr = (Rtri, Rsbb, Rtt, Rbz)
                        tt(TT[:, 0, 0:BW], pre, trrb, ALU.mult, rr, (Rtt,))
                        tt(TT[:, 1, 0:BW], pim, trib, ALU.mult, rr, (Rtt,))
                        tt(BZ[:, 0, 0:BW], TT[:, 0, 0:BW], TT[:, 1, 0:BW], ALU.add, rr, (Rbz,))
                        tt(TT[:, 0, 0:BW], pim, trrb, ALU.mult, rr, (Rtt,))
                        tt(TT[:, 1, 0:BW], pre, trib, ALU.mult, rr, (Rtt,))
                        tt(BZ[:, 1, 0:BW], TT[:, 0, 0:BW], TT[:, 1, 0:BW], ALU.subtract, rr, (Rbz,))
                        if k == 0 and bi_ == 0:
                            dbg(BZ[:, 0, 0:BW], 1152, BW, [Rbz]); dbg(BZ[:, 1, 0:BW], 1664, BW, [Rbz])
                        for ri in range(2):
                            zo = TT[:, ri, 0:BW]
                            zin = BZ[:, ri, 0:BW]
                            if rev:
                                zo, zin = zo[:, ::-1], zin[:, ::-1]
                            dv(lambda zo=zo, zin=zin, ri=ri: V.tensor_tensor_scan(out=zo, data0=MAG[:, k:k + 1].broadcast_to([128, BW]), data1=zin, initial=INI[:, ri, ib:ib + 1], op0=ALU.mult, op1=ALU.add),
                               [Rsu, Rbz, Rini, Rtt], [Rtt])
                        if k == 0 and bi_ == 0:
                            dbg(TT[:, 0, 0:BW], 2176, BW, [Rtt]); dbg(TT[:, 1, 0:BW], 2688, BW, [Rtt])
                        zl = 0 if rev else BW - 1
                        zr_, zi_ = TT[:, 0, zl:zl + 1], TT[:, 1, zl:zl + 1]
                        last_blk = bi_ == NB - 1
                        if not last_blk:
                            ib2 = 1 - ib
                            rc, wc = [Rsu, Rini, Rtt], [Rsu, Rini]
                            dv(lambda: V.tensor_scalar(out=W1[:, 0:1], in0=zi_, scalar1=UBI[:, k:k + 1], scalar2=None, op0=ALU.mult), rc, wc)
                            dv(lambda: V.scalar_tensor_tensor(out=INI[:, 0, ib2:ib2 + 1], in0=zr_, scalar=UBR[:, k:k + 1], in1=W1[:, 0:1], op0=ALU.mult, op1=ALU.subtract), rc, wc)
                            dv(lambda: V.tensor_scalar(out=W2_[:, 0:1], in0=zr_, scalar1=UBI[:, k:k + 1], scalar2=None, op0=ALU.mult), rc, wc)
                            dv(lambda: V.scalar_tensor_tensor(out=INI[:, 1, ib2:ib2 + 1], in0=zi_, scalar=UBR[:, k:k + 1], in1=W2_[:, 0:1], op0=ALU.mult, op1=ALU.add), rc, wc)
                            ib = ib2
                        elif not sample:
                            cmul(FIN[:, k, 0:1], FIN[:, k, 1:2], TRI[:, 0, BW - 1:BW], TRI[:, 1, BW - 1:BW], zr_, zi_, W1[:, 0:1], W2_[:, 0:1], (Rsu, Rtri, Rtt, Rfin), (Rsu, Rfin))
                        dstS = SBb[:, :, 0:BW]
                        Rdst = Rsbb
                        rr2 = (Rtri, Rtt, Rbz)
                        tt(BZ[:, 0, 0:BW], TT[:, 0, 0:BW], trrb, ALU.mult, rr2, (Rbz,))
                        tt(BZ[:, 1, 0:BW], TT[:, 1, 0:BW], trib, ALU.mult, rr2, (Rbz,))
                        tt(dstS[:, 0, :], BZ[:, 0, 0:BW], BZ[:, 1, 0:BW], ALU.subtract, (Rbz, Rdst), (Rdst,))
                        tt(BZ[:, 0, 0:BW], TT[:, 1, 0:BW], trrb, ALU.mult, rr2, (Rbz,))
                        tt(BZ[:, 1, 0:BW], TT[:, 0, 0:BW], trib, ALU.mult, rr2, (Rbz,))
                        tt(dstS[:, 1, :], BZ[:, 0, 0:BW], BZ[:, 1, 0:BW], ALU.add, (Rbz, Rdst), (Rdst,))
                        def emit_y():
                            nc.tensor.matmul(PS[0:32, 4, 0:BW], lhsT=CW[:, m, 0, :], rhs=SBb[:, 0, 0:BW], start=True, stop=False)
                            return nc.tensor.matmul(PS[0:32, 4, 0:BW], lhsT=CW[:, m, 1, :], rhs=SBb[:, 1, 0:BW], start=False, stop=True)
                        S.op("pe", emit_y, reads=[Rsu, Rsbb], writes=[RPS[4]])
                        if not rev:
                            S.op("act", lambda: nc.scalar.copy(out=YFp[0:32, cols], in_=PS[0:32, 4, 0:BW]), reads=[RPS[4]], writes=[Rsfs])
                        else:
                            tt(OTs[0:32, 0:BW], PS[0:32, 4, 0:BW], YFp[0:32, cols], ALU.add, (RPS[4], Rsfs, Rots), (Rots,))
                            S.dma("sp", YS[m4 * 32:(m4 + 1) * 32, cc, cols], OTs[0:32, 0:BW], reads=[Rots], writes=[Rys], key=Reg("s_OTd"))
            if not sample:
                pi = 0 if kind == "pA" else 1
                for d in range(2):
                    OUT_EVS.append(S.dma("sp", o_s5[pi, l, d].rearrange("(m g) p r -> (g p) m r", g=2), FIN[:, d::2, :], reads=[Rfin], key=Reg("o_s5_d"), slow=True))
            S.barrier()
            Z, _ = view(oZ, [128, 2, 2048], BF16)
            Rz_ = Reg("s_Z")
            for b in range(NB):
                cols = slice(b * BW, (b + 1) * BW)
                for cc in range(2):
                    yv_ = YV[:, 0:BW]
                    dv(lambda: V.scalar_tensor_tensor(out=yv_, in0=UT[:, cc, cols], scalar=PV[:, 10 + cc:11 + cc], in1=YS[:, cc, cols], op0=ALU.mult, op1=ALU.add),
                       [Rut, Rpv, Rys, Rbz], [Rbz])
                    tt(TTf[:, 0, 0:BW], yv_, yv_, ALU.mult, (Rbz, Rtt), (Rtt,))
                    dv(lambda: V.tensor_scalar(out=TTf[:, 0, 0:BW], in0=TTf[:, 0, 0:BW], scalar1=0.044715, scalar2=1.0, op0=ALU.mult, op1=ALU.add), [Rtt], [Rtt])
                    tt(TTf[:, 0, 0:BW], TTf[:, 0, 0:BW], yv_, ALU.mult, (Rbz, Rtt), (Rtt,))
                    S.op("act", lambda: nc.scalar.activation(out=TTf[:, 1, 0:BW], in_=TTf[:, 0, 0:BW], func=AF.Sigmoid, scale=1.5957691216057308), reads=[Rtt], writes=[Rtt])
                    tt(Z[:, cc, cols], TTf[:, 1, 0:BW], yv_, ALU.mult, (Rbz, Rtt, Rz_), (Rz_,))
                for co in range(2):
                    def emit_g(co=co):
                        nc.tensor.matmul(PS[:, 5, 0:BW], lhsT=WGL[:, 0, co * 128:(co + 1) * 128], rhs=Z[:, 0, cols], start=True, stop=False)
                        return nc.tensor.matmul(PS[:, 5, 0:BW], lhsT=WGL[:, 1, co * 128:(co + 1) * 128], rhs=Z[:, 1, cols], start=False, stop=True)
                    S.op("pe", emit_g, reads=[Rsu, Rz_], writes=[RPS[5]])
                    S.op("act", lambda: nc.scalar.activation(out=OTs[:, 0:BW], in_=PS[:, 5, 0:BW], func=AF.Sigmoid), reads=[RPS[5]], writes=[Rots])
                    tt(BR[:, co, cols], Z[:, co, cols], OTs[:, 0:BW], ALU.mult, (Rz_, Rots), (Rbr[co],))

        DBG = {}

        def mixer(l):
            make_gate_bcast(1)
            mixer_params(l)
            for (t0, ntile, ci, kind) in SEQS:
                T = ntile * 128
                BW = min(512, T)
                NB = T // BW
                S.barrier()
                cur["XN"], _ = view(BS0, [128, 2, D])
                for tt in range(ntile):
                    prenorm_tile(t0 + tt, ci, 1, HT[:, :, tt * 128:(tt + 1) * 128], Rht, (4 + 2 * (tt % 2), 5 + 2 * (tt % 2)))
                S.barrier()
                if kind != "sample":
                    mla_cache_out(l, t0, ntile, 0 if kind == "pA" else 1)
                    S.barrier()
                zero = []
                for nm, chs in (("s5", (0, 1)), ("ret", (2, 3)), ("conv", (4, 5)), ("mla", (6, 7))):
                    if not cfg.get(nm, True):
                        zero += list(chs)
                for i in zero:
                    S.op("dve", lambda i=i: nc.vector.memset(BR[:, i, 0:T], 0.0), writes=[Rbr[i]])
                if cfg.get("conv", True):
                    branch_conv(l, T, NB, BW)
                    S.barrier()
                if cfg.get("mla", True):
                    branch_mla(l, t0, T, NB, BW, kind)
                    S.barrier()
                if cfg.get("ret", True):
                    branch_ret(l, t0, T, kind)
                    S.barrier()
                if cfg.get("s5", True):
                    branch_s5(l, t0, T, NB, BW, kind)
                    S.barrier()
                if tuple(cfg.get("dump_br", ())) == (l, kind):
                    dbg = nc.dram_tensor("dbg_br", [128, 8, 2048], BF16, kind="ExternalOutput").ap()
                    S.dma("sp", dbg[:, :, 0:T], BR[:, :, 0:T], reads=Rbr, key=Reg("dbg"))
                gate_stage(l, t0, ntile, ci, T, NB, BW)
                S.barrier()
            cur["XN"], cur["TMP"] = XN, TMP

        for l in range(LAYERS):
            S.barrier()
            compute_mod(l)
            S.barrier()
            if cfg.get("ffn1", True):
                ffn(l, 0, 0)
            S.barrier()
            if cfg.get("mixer", True):
                mixer(l)
            S.barrier()
            if cfg.get("ffn2", True):
                ffn(l, 1, 2)

        yv = y.rearrange("(t p) d -> p t d", p=128)
        Ryout = Reg("yout")
        evs = []
        for t in range(NT):
            evs.append(S.dma("sp", yv[:, t, :], X[:, t, :], reads=[RX[t]], key=Ryout))
        S._wait("sp", set([evs[-1]] + OUT_EVS))
        S.barrier()
    return nc


def _axial(T, dim):
    rows = T // 64
    row = np.repeat(np.arange(rows, dtype=np.float32), 64)
    col = np.tile(np.arange(64, dtype=np.float32), rows)
    quarter = dim // 4
    inv = (np.float32(10000.0) ** (-np.arange(quarter, dtype=np.float32) / np.float32(quarter))).astype(np.float32)
    ang = np.concatenate([row[:, None] * inv, col[:, None] * inv], axis=-1).astype(np.float32)
    return np.cos(ang).astype(np.float32), np.sin(ang).astype(np.float32)


def _rope_tables():
    c, s = _axial(2048, 32)
    mla = np.stack([np.concatenate([c.T, c.T], 0), np.concatenate([s.T, s.T], 0)], 0)
    c2, s2 = _axial(2048, 64)
    ret = np.stack([c2, s2], 0)
    return np.ascontiguousarray(mla, dtype=np.float32), np.ascontiguousarray(ret, dtype=np.float32)


def _prep_inputs(inputs):
    f = lambda a: np.ascontiguousarray(np.asarray(a, dtype=np.float32))
    shared = {k: f(inputs[k]) for k in (
        "w_mod", "b_mod", "norm_pre", "norm_post", "ffn_w1", "ffn_w3", "ffn_w2", "w_in", "s5_lam_re", "s5_lam_im", "s5_log_dt",
        "s5_b_re", "s5_b_im", "s5_c_re", "s5_c_im", "s5_d", "s5_w_glu", "ret_decay", "ret_gn", "conv_w", "conv_b",
        "mla_q_norm", "mla_w_uq", "mla_kv_norm", "mla_w_ukv", "w_branch", "w_gate", "b_gate", "w_o")}
    shared["c_ident"] = np.eye(128, dtype=np.float32)
    shared["c_rope_mla"], shared["c_rope_ret"] = _rope_tables()
    jj = np.arange(128, dtype=np.float32)[:, None]
    ii = np.arange(128, dtype=np.float32)[None, :]
    diff = ii - jj
    shared["c_ret"] = np.ascontiguousarray(np.stack([
        np.maximum(diff, 0), np.maximum(-diff, 0), 0.125 * (diff >= 0), 0.125 * (diff < 0),
        np.broadcast_to(ii + 1.0, (128, 128)), np.broadcast_to(128.0 - ii, (128, 128))], 0), dtype=np.float32)
    shared["c_pidx"] = np.ascontiguousarray(np.concatenate([127.0 - jj, jj], 1), dtype=np.float32)
    xp = f(inputs["x_prompt"])
    xs = f(inputs["x_sample"])
    c = f(inputs["c"])
    cctx = f(inputs["c_ctx"])
    maps = []
    for i in range(NCORES):
        m = dict(shared)
        m["xin"] = np.ascontiguousarray(np.concatenate([xs[i], xp[2 * i], xp[2 * i + 1]], axis=0))
        m["cond2"] = np.ascontiguousarray(np.stack([c[i], cctx], axis=0))
        m["st_s5"] = f(inputs["state_s5"][i])
        m["st_ret"] = f(inputs["state_ret"][i])
        m["ctx_mla"] = f(inputs["cache_mla"][i])
        maps.append(m)
    return maps


def _gather(res):
    ys = np.stack([r["y"][:2048] for r in res], axis=0)
    yp = np.stack([r["y"][2048 + 256 * j:2048 + 256 * (j + 1)] for r in res for j in range(2)], axis=0)
    s5 = np.concatenate([r["o_s5"] for r in res], axis=0)
    ret = np.concatenate([r["o_ret"] for r in res], axis=0)
    mla = np.concatenate([r["o_mla"] for r in res], axis=0)
    return (yp.astype(np.float32), ys.astype(np.float32), s5.astype(np.float32), ret.astype(np.float32), mla.astype(np.float32))


CFG = {}


def kernel(**inputs):
    nc = build(CFG)
    maps = _prep_inputs(inputs)
    res = run_bass_kernel_spmd(nc, maps, core_ids=list(range(NCORES)))
    return _gather(res.results)
```

```python
import numpy as np
import concourse.bass as bass
import concourse.mybir as mybir
from concourse.bass_utils import run_bass_kernel_spmd
from contextlib import ExitStack

F32 = mybir.dt.float32
BF16 = mybir.dt.bfloat16
AF = mybir.ActivationFunctionType
ALU = mybir.AluOpType
AX = mybir.AxisListType

D = 1024
DFF = 2816
NFF = 22
TOK = 2560
NT = 20
EPS = 1e-6
NCORES = 8
INC = 2400

SAME_ENG_SYNC = True


_REGS = {}


def Reg(name):
    if name not in _REGS:
        _REGS[name] = _Reg(name)
    return _REGS[name]


class _Reg:
    __slots__ = ("name", "w", "r", "dsem", "dcnt")

    def __init__(self, name):
        self.name = name
        self.w = None
        self.r = []
        self.dsem = None
        self.dcnt = 0


class Sched:
    def __init__(self, nc, es):
        self.nc = nc
        self.es = es
        self.eng = {"pe": nc.tensor, "act": nc.scalar, "dve": nc.vector, "pool": nc.gpsimd, "sp": nc.sync}
        self.sem = {e: es.enter_context(nc.semaphore("s_" + e)) for e in self.eng}
        self.cnt = {e: 0 for e in self.eng}
        self.seen = {e: {} for e in self.eng}
        self.seen_d = {e: {} for e in self.eng}
        self.nsem = 0
        self.out_events = []
        self.pending_reads = {}

    def _wait(self, e, deps, raw=None):
        best = {}
        bestd = {}
        for d in deps:
            if d[0] == "e":
                _, e2, c = d
                if e2 == e and (e == "pe" or not SAME_ENG_SYNC or (raw is not None and d not in raw)):
                    continue
                if c > best.get(e2, 0):
                    best[e2] = c
            else:
                _, sem, tgt, key = d
                if tgt > bestd.get(key, (None, 0))[1]:
                    bestd[key] = (sem, tgt)
        E = self.eng[e]
        for e2, c in best.items():
            if self.seen[e].get(e2, 0) >= c:
                continue
            E.wait_ge(self.sem[e2], c)
            self.seen[e][e2] = c
        for key, (sem, tgt) in bestd.items():
            if self.seen_d[e].get(key, 0) >= tgt:
                continue
            E.wait_ge(sem, tgt)
            self.seen_d[e][key] = tgt

    def _deps(self, reads, writes):
        deps = set()
        for r in reads:
            if r.w is not None:
                deps.add(r.w)
        for w in writes:
            if w.w is not None:
                deps.add(w.w)
            deps.update(w.r)
        return deps

    def op(self, e, emit, reads=(), writes=()):
        raw = set(r.w for r in reads if r.w is not None)
        self._wait(e, self._deps(reads, writes), raw)
        inst = emit()
        self.cnt[e] += 1
        inst.then_inc(self.sem[e], 1)
        ev = ("e", e, self.cnt[e])
        for r in reads:
            r.r.append(ev)
        for w in writes:
            w.w = ev
            w.r = []
        return ev

    def dma(self, e, out, in_, reads=(), writes=(), key=None, slow=False):
        key = key or (list(writes) + list(reads))[0]
        if key.dsem is None:
            key.dsem = self.es.enter_context(self.nc.semaphore("d%d" % self.nsem))
            self.nsem += 1
        deps = set(d for d in self._deps(reads, writes) if not (d[0] == "d" and d[3] == key.name))
        self._wait(e, deps)
        inst = self.eng[e].dma_start(out=out, in_=in_, allow_slow_non_contiguous=True) if slow else self.eng[e].dma_start(out=out, in_=in_)
        key.dcnt += 16
        inst.then_inc(key.dsem, 16)
        ev = ("d", key.dsem, key.dcnt, key.name)
        if reads:
            self.pending_reads[key.name] = ev
        for r in reads:
            r.r.append(ev)
        for w in writes:
            w.w = ev
            w.r = []
        return ev

    def barrier(self):
        pend = set(self.pending_reads.values())
        for e in self.eng:
            deps = set(("e", e2, self.cnt[e2]) for e2 in self.eng if e2 != e and self.cnt[e2] > 0)
            self._wait(e, deps | pend)
        self.pending_reads = {}


def build(cfg):
    _REGS.clear()
    nc = bass.Bass("TRN2", target_bir_lowering=False)
    LAYERS = cfg.get("layers", 2)

    def din(name, shape):
        return nc.dram_tensor(name, list(shape), F32, kind="ExternalInput").ap()

    def dout(name, shape):
        return nc.dram_tensor(name, list(shape), F32, kind="ExternalOutput").ap()

    xin = din("xin", [TOK, D])
    cond2 = din("cond2", [2, D])
    st_s5 = din("st_s5", [2, 2, 16, 64, 2])
    st_ret = din("st_ret", [2, 2, 4, 64, 64])
    ctx_mla = din("ctx_mla", [2, 512, 160])
    w_mod = din("w_mod", [2, D, 9 * D])
    b_mod = din("b_mod", [2, 9 * D])
    norm_pre = din("norm_pre", [2, 3, D])
    norm_post = din("norm_post", [2, 3, D])
    ffn_w1 = din("ffn_w1", [2, 2, D, DFF])
    ffn_w3 = din("ffn_w3", [2, 2, D, DFF])
    ffn_w2 = din("ffn_w2", [2, 2, DFF, D])
    w_in = din("w_in", [2, D, INC])
    s5_lam_re = din("s5_lam_re", [2, 2, 16, 64])
    s5_lam_im = din("s5_lam_im", [2, 2, 16, 64])
    s5_log_dt = din("s5_log_dt", [2, 2, 16])
    s5_b_re = din("s5_b_re", [2, 2, 16, 64, 16])
    s5_b_im = din("s5_b_im", [2, 2, 16, 64, 16])
    s5_c_re = din("s5_c_re", [2, 16, 16, 64])
    s5_c_im = din("s5_c_im", [2, 16, 16, 64])
    s5_d = din("s5_d", [2, 256])
    s5_w_glu = din("s5_w_glu", [2, 256, 256])
    ret_decay = din("ret_decay", [2, 2, 4])
    ret_gn = din("ret_gn", [2, 256])
    conv_w = din("conv_w", [2, 3, 256])
    conv_b = din("conv_b", [2, 256])
    mla_q_norm = din("mla_q_norm", [2, 192])
    mla_w_uq = din("mla_w_uq", [2, 192, 384])
    mla_kv_norm = din("mla_kv_norm", [2, 128])
    mla_w_ukv = din("mla_w_ukv", [2, 128, 512])
    w_branch = din("w_branch", [2, 4, 256, D])
    w_gate = din("w_gate", [2, D, 4 * D])
    b_gate = din("b_gate", [2, 4 * D])
    w_o = din("w_o", [2, D, D])
    c_ident = din("c_ident", [128, 128])
    c_rope_mla = din("c_rope_mla", [2, 32, 2048])
    c_rope_ret = din("c_rope_ret", [2, 2048, 32])
    c_ret = din("c_ret", [6, 128, 128])
    c_pidx = din("c_pidx", [128, 2])

    y = dout("y", [TOK, D])
    o_s5 = dout("o_s5", [2, 2, 2, 16, 64, 2])
    o_ret = dout("o_ret", [2, 2, 2, 4, 64, 64])
    o_mla = dout("o_mla", [2, 2, 256, 160])

    es = ExitStack()
    with es:
        S = Sched(nc, es)

        def sb(name, shape, dt=F32):
            return es.enter_context(nc.sbuf_tensor(name, list(shape), dt))

        X = sb("X", [128, NT, D])
        RX = [Reg("X%d" % t) for t in range(NT)]
        PS = es.enter_context(nc.psum_tensor("PS", [128, 8, 512], F32))
        RPS = [Reg("PS%d" % b) for b in range(8)]
        ident = sb("ident", [128, 128])
        identb = sb("identb", [128, 128], BF16)
        Rid = Reg("ident")
        VEC = sb("VEC", [128, 2, 72])
        Rvec = Reg("VEC")
        NRM = sb("NRM", [128, 48])
        Rnrm = Reg("NRM")
        SV = sb("SV", [128, 2, 3, 8])
        Rsv = Reg("SV")
        GV = sb("GV", [128, 2, 3, 8])
        Rgv = Reg("GV")
        GB = sb("GB", [128, 2, D])
        Rgb = [Reg("GB0"), Reg("GB1")]
        DG = sb("DG", [128, 2, 128])
        Rdg = [Reg("DG0"), Reg("DG1")]
        small = sb("small", [128, 4, 4])
        Rsmall = [Reg("sm%d" % i) for i in range(4)]
        junk = sb("junk", [128, D], BF16)
        Rjunk = Reg("junk")
        ACOLS = 28672
        ARENA = sb("ARENA", [128, ACOLS])

        def view(off, shape, dt=F32):
            n = int(np.prod(shape[1:]))
            nbytes = n * (2 if dt == BF16 else 4)
            assert off % 4 == 0 and nbytes % 4 == 0 and off + nbytes <= ACOLS * 4, (off, shape)
            ap = ARENA[:, off // 4:(off + nbytes) // 4]
            if dt == BF16:
                ap = ap.bitcast(BF16)
            if len(shape) > 2:
                names = "abcdef"[:len(shape) - 1]
                pat = "p (" + " ".join(names) + ") -> p " + " ".join(names)
                ap = ap.rearrange(pat, **{names[i]: shape[i + 1] for i in range(len(names) - 1)})
            return ap, off + nbytes

        HTB, _o = view(0, [128, 2, 8, 512], BF16)
        GT, _o = view(_o, [128, NFF, 512], BF16)
        W13, _o = view(_o, [128, 3, 2, 8, 256], BF16)
        W2, _o = view(_o, [128, 3, 2, D], BF16)
        SIL, _o = view(_o, [128, 2, 512], BF16)
        XN, _o = view(_o, [128, 2, D])
        TMP, _o = view(_o, [128, 2, D])
        WM, _ = view(0, [128, 2, 8, 512], BF16)
        Rxn = [Reg("XN0"), Reg("XN1")]

        xv = xin.rearrange("(t p) d -> p t d", p=128)
        Rxall = Reg("xall")
        for t in range(NT):
            S.dma("sp", X[:, t, :], xv[:, t, :], writes=[RX[t]], key=Rxall)
        for t in range(NT):
            RX[t].w = ("d", Rxall.dsem, Rxall.dcnt, Rxall.name)
        S.dma("sp", ident[:], c_ident[:, :], writes=[Rid])
        S.op("dve", lambda: nc.vector.tensor_copy(out=identb[:], in_=ident[:]), reads=[Rid], writes=[Rid])

        rot = {"ps": 0, "sm": 0, "xn": 0, "dg": 0}
        OUT_EVS = []

        stage = sb("stage", [128, 128])
        Rstage = Reg("stage")

        def load_T(dst_ap, src_ap, rows, dst_reg, bank=7):
            S.dma("sp", stage[0:rows, :], src_ap, writes=[Rstage])
            S.op("pe", lambda: nc.tensor.transpose(out=PS[:, bank, 0:rows], in_=stage[0:rows, :], identity=ident[0:rows, 0:rows]),
                 reads=[Rstage, Rid], writes=[RPS[bank]])
            S.op("dve", lambda: nc.vector.tensor_copy(out=dst_ap, in_=PS[:, bank, 0:rows]), reads=[RPS[bank]], writes=[dst_reg])

        SCT = sb("SCT", [128, 8, 2], BF16)
        Rsct = Reg("SCT")
        sct32 = sb("sct32", [128, 16])
        load_T(sct32[:, :], cond2.rearrange("c (k p) -> (c k) p", p=128), 16, Rsct)
        S.op("act", lambda: nc.scalar.activation(out=SCT[:].rearrange("p k c -> p c k"), in_=sct32[:].rearrange("p (c k) -> p c k", c=2), func=AF.Silu),
             reads=[Rsct], writes=[Rsct])

        Rwm = [Reg("WM0"), Reg("WM1")]
        BM = sb("BM", [128, 72])
        Rbm = Reg("BM")

        def compute_mod(l):
            load_T(BM[:, :], b_mod[l].rearrange("(c p) -> c p", p=128), 72, Rbm)
            load_T(NRM[:, 0:24], norm_pre[l].rearrange("s (c p) -> (s c) p", p=128), 24, Rnrm)
            load_T(NRM[:, 24:48], norm_post[l].rearrange("s (c p) -> (s c) p", p=128), 24, Rnrm)
            wv = w_mod[l].rearrange("(kc p) n -> p kc n", p=128)
            for cb in range(18):
                sl = cb % 2
                S.dma("pool", WM[:, sl, :, :], wv[:, :, cb * 512:(cb + 1) * 512], writes=[Rwm[sl]])
                bank = 6

                def emit(cb=cb, sl=sl):
                    inst = None
                    for cc in range(4):
                        for kc in range(8):
                            inst = nc.tensor.matmul(PS[:, bank, cc * 2:cc * 2 + 2], lhsT=WM[:, sl, kc, cc * 128:(cc + 1) * 128],
                                                    rhs=SCT[:, kc, :], start=(kc == 0), stop=(kc == 7))
                    return inst
                S.op("pe", emit, reads=[Rwm[sl], Rsct], writes=[RPS[bank]])
                S.op("dve", lambda cb=cb: nc.vector.tensor_tensor(
                    out=VEC[:, :, cb * 4:(cb + 1) * 4].rearrange("p c j -> p j c"),
                    in0=PS[:, bank, 0:8].rearrange("p (j c) -> p j c", c=2),
                    in1=BM[:, cb * 4:(cb + 1) * 4].unsqueeze(2).broadcast_to([128, 4, 2]), op=ALU.add),
                    reads=[RPS[bank], Rbm], writes=[Rvec])
            for ci in range(2):
                for s in range(3):
                    S.op("dve", lambda ci=ci, s=s: nc.vector.scalar_tensor_tensor(
                        out=SV[:, ci, s, :], in0=VEC[:, ci, (3 * s + 1) * 8:(3 * s + 2) * 8], scalar=1.0,
                        in1=NRM[:, s * 8:(s + 1) * 8], op0=ALU.add, op1=ALU.mult), reads=[Rvec, Rnrm], writes=[Rsv])
                    fac = 1.0 if s == 1 else 0.5
                    S.op("dve", lambda ci=ci, s=s, fac=fac: nc.vector.scalar_tensor_tensor(
                        out=GV[:, ci, s, :], in0=VEC[:, ci, (3 * s + 2) * 8:(3 * s + 3) * 8], scalar=fac,
                        in1=NRM[:, 24 + s * 8:24 + (s + 1) * 8], op0=ALU.mult, op1=ALU.mult), reads=[Rvec, Rnrm], writes=[Rgv])

        def make_gate_bcast(s):
            for ci in range(2):
                bank0 = 4 + 2 * ci
                for c in range(8):
                    dgi = rot["dg"] % 2
                    rot["dg"] += 1
                    S.op("dve", lambda c=c, ci=ci, dgi=dgi: nc.vector.tensor_scalar(
                        out=DG[:, dgi, :], in0=ident[:], scalar1=GV[:, ci, s, c:c + 1], scalar2=None, op0=ALU.mult),
                        reads=[Rid, Rgv], writes=[Rdg[dgi]])
                    bank = bank0 + c // 4
                    S.op("pe", lambda c=c, dgi=dgi, bank=bank: nc.tensor.matmul(
                        PS[:, bank, (c % 4) * 128:(c % 4 + 1) * 128], lhsT=ones32[:], rhs=DG[:, dgi, :], start=True, stop=True),
                        reads=[Rdg[dgi], Rones], writes=[RPS[bank]])
                S.op("dve", lambda ci=ci, bank0=bank0: nc.vector.tensor_copy(
                    out=GB[:, ci, :], in_=PS[:, bank0:bank0 + 2, :].rearrange("p b n -> p (b n)")),
                    reads=[RPS[bank0], RPS[bank0 + 1]], writes=[Rgb[ci]])

        ones32 = sb("ones32", [128, 128])
        Rones = Reg("ones")
        S.op("dve", lambda: nc.vector.memset(ones32[:], 1.0), writes=[Rones])

        cur = {"XN": XN, "TMP": TMP}

        def prenorm_tile(t, ci, s, dst, dst_reg, banks):
            XN = cur["XN"]
            smi = rot["sm"] % 4
            rot["sm"] += 1
            xi = rot["xn"] % 2
            rot["xn"] += 1
            sm = small[:, smi, :]
            S.op("act", lambda: nc.scalar.activation(out=junk[:], in_=X[:, t, :], func=AF.Square, accum_out=sm[:, 0:1]),
                 reads=[RX[t]], writes=[Rjunk, Rsmall[smi]])
            S.op("act", lambda: nc.scalar.activation(out=sm[:, 1:2], in_=sm[:, 0:1], func=AF.Sqrt, scale=1.0 / D, bias=epsb[:, 0:1]),
                 reads=[Rsmall[smi], Reps], writes=[Rsmall[smi]])
            S.op("dve", lambda: nc.vector.reciprocal(out=sm[:, 2:3], in_=sm[:, 1:2]), reads=[Rsmall[smi]], writes=[Rsmall[smi]])
            S.op("dve", lambda: nc.vector.tensor_scalar(out=XN[:, xi, :], in0=X[:, t, :], scalar1=sm[:, 2:3], scalar2=None, op0=ALU.mult),
                 reads=[RX[t], Rsmall[smi]], writes=[Rxn[xi]])
            b0, b1 = banks

            def emit():
                inst = None
                for c in range(8):
                    bk = b0 if c < 4 else b1
                    inst = nc.tensor.transpose(out=PS[:, bk, (c % 4) * 128:(c % 4 + 1) * 128], in_=XN[:, xi, c * 128:(c + 1) * 128], identity=ident[:])
                return inst
            S.op("pe", emit, reads=[Rxn[xi], Rid], writes=[RPS[b0], RPS[b1]])
            for c in range(8):
                bk = b0 if c < 4 else b1
                S.op("act", lambda c=c, bk=bk: nc.scalar.activation(
                    out=dst[:, c, :], in_=PS[:, bk, (c % 4) * 128:(c % 4 + 1) * 128], func=AF.Identity,
                    scale=SV[:, ci, s, c:c + 1], bias=VEC[:, ci, 3 * s * 8 + c:3 * s * 8 + c + 1]),
                    reads=[RPS[bk], Rsv, Rvec], writes=[dst_reg])

        epsb = sb("epsb", [128, 1])
        halfpi = sb("halfpi", [128, 1])
        Reps = Reg("eps")
        S.op("dve", lambda: nc.vector.memset(epsb[:], EPS), writes=[Reps])
        S.op("dve", lambda: nc.vector.memset(halfpi[:], float(np.pi / 2)), writes=[Reps])

        Rtmp = [Reg("TMP0"), Reg("TMP1")]

        def postnorm_tile(t, ci, b0):
            TMP = cur["TMP"]
            smi = rot["sm"] % 4
            rot["sm"] += 1
            ti = rot["xn"] % 2
            rot["xn"] += 1
            sm = small[:, smi, :]
            fin = PS[:, b0:b0 + 2, :].rearrange("p b n -> p (b n)")
            S.op("act", lambda: nc.scalar.activation(out=junk[:], in_=fin, func=AF.Square, accum_out=sm[:, 0:1]),
                 reads=[RPS[b0], RPS[b0 + 1]], writes=[Rjunk, Rsmall[smi]])
            S.op("act", lambda: nc.scalar.activation(out=sm[:, 1:2], in_=sm[:, 0:1], func=AF.Sqrt, scale=1.0 / D, bias=epsb[:, 0:1]),
                 reads=[Rsmall[smi], Reps], writes=[Rsmall[smi]])
            S.op("dve", lambda: nc.vector.reciprocal(out=sm[:, 2:3], in_=sm[:, 1:2]), reads=[Rsmall[smi]], writes=[Rsmall[smi]])
            S.op("dve", lambda: nc.vector.scalar_tensor_tensor(out=TMP[:, ti, :], in0=fin, scalar=sm[:, 2:3], in1=GB[:, ci, :],
                                                               op0=ALU.mult, op1=ALU.mult),
                 reads=[RPS[b0], RPS[b0 + 1], Rsmall[smi], Rgb[ci]], writes=[Rtmp[ti]])
            S.op("dve", lambda: nc.vector.tensor_tensor(out=X[:, t, :], in0=X[:, t, :], in1=TMP[:, ti, :], op=ALU.add),
                 reads=[RX[t], Rtmp[ti]], writes=[RX[t]])

        Rhtb = [Reg("HTB0"), Reg("HTB1")]
        Rgt = [Reg("GT%d" % j) for j in range(NFF)]
        Rw13 = [Reg("W13_%d" % i) for i in range(3)]
        Rw2 = [Reg("W2_%d" % i) for i in range(3)]
        Rsil = [Reg("SIL0"), Reg("SIL1")]
        cnts = {"w13": 0, "w2": 0, "htb": 0, "sil": 0, "pa": 0}

        def ffn(l, f, s):
            make_gate_bcast(s)
            w1v = ffn_w1[l, f].rearrange("(kc p) n -> p kc n", p=128)
            w3v = ffn_w3[l, f].rearrange("(kc p) n -> p kc n", p=128)
            w2v = ffn_w2[l, f].rearrange("(j p) n -> p j n", p=128)
            hb0 = cnts["htb"]
            cnts["htb"] += 5

            def prenorm_block(blk_):
                hb_ = (hb0 + blk_) % 2
                ci_ = 0 if blk_ < 4 else 1
                for tt in range(4):
                    prenorm_tile(blk_ * 4 + tt, ci_, s, HTB[:, hb_, :, tt * 128:(tt + 1) * 128], Rhtb[hb_], (4 + 2 * (tt % 2), 5 + 2 * (tt % 2)))
            prenorm_block(0)
            for blk in range(5):
                ci = 0 if blk < 4 else 1
                hb = (hb0 + blk) % 2
                for j2 in range(NFF // 2):
                    if j2 == 4 and blk + 1 < 5:
                        prenorm_block(blk + 1)
                    sl = cnts["w13"] % 3
                    cnts["w13"] += 1
                    S.dma("pool", W13[:, sl, 0, :, :], w1v[:, :, j2 * 256:(j2 + 1) * 256], writes=[Rw13[sl]])
                    S.dma("pool", W13[:, sl, 1, :, :], w3v[:, :, j2 * 256:(j2 + 1) * 256], writes=[Rw13[sl]])
                    for jj in range(2):
                        j = 2 * j2 + jj
                        pa = cnts["pa"] % 2
                        cnts["pa"] += 1
                        b1, b3 = 2 * pa, 2 * pa + 1
                        for (m, bk) in ((0, b1), (1, b3)):
                            def emit(m=m, bk=bk, jj=jj, sl=sl):
                                inst = None
                                for kc in range(8):
                                    inst = nc.tensor.matmul(PS[:, bk, :], lhsT=W13[:, sl, m, kc, jj * 128:(jj + 1) * 128],
                                                            rhs=HTB[:, hb, kc, :], start=(kc == 0), stop=(kc == 7))
                                return inst
                            S.op("pe", emit, reads=[Rw13[sl], Rhtb[hb]], writes=[RPS[bk]])
                        si = cnts["sil"] % 2
                        cnts["sil"] += 1
                        S.op("act", lambda b1=b1, si=si: nc.scalar.activation(out=SIL[:, si, :], in_=PS[:, b1, :], func=AF.Silu),
                             reads=[RPS[b1]], writes=[Rsil[si]])
                        S.op("dve", lambda b3=b3, si=si, j=j: nc.vector.tensor_tensor(out=GT[:, j, :], in0=PS[:, b3, :], in1=SIL[:, si, :], op=ALU.mult),
                             reads=[RPS[b3], Rsil[si]], writes=[Rgt[j]])
                for j2 in range(NFF // 2):
                    sl = cnts["w2"] % 3
                    cnts["w2"] += 1
                    S.dma("pool", W2[:, sl, :, :], w2v[:, 2 * j2:2 * j2 + 2, :], writes=[Rw2[sl]])
                    for jj in range(2):
                        j = 2 * j2 + jj

                        def emit(j=j, jj=jj, sl=sl):
                            inst = None
                            for tt in range(4):
                                for half in range(2):
                                    inst = nc.tensor.matmul(PS[:, 2 * tt + half, :], lhsT=GT[:, j, tt * 128:(tt + 1) * 128],
                                                            rhs=W2[:, sl, jj, half * 512:(half + 1) * 512], start=(j == 0), stop=(j == NFF - 1))
                            return inst
                        S.op("pe", emit, reads=[Rgt[j], Rw2[sl]], writes=RPS)
                for tt in range(4):
                    postnorm_tile(blk * 4 + tt, ci, 2 * tt)

        MOFF = 0
        HT, MOFF = view(MOFF, [128, 8, 2048], BF16)
        BR, MOFF = view(MOFF, [128, 8, 2048], BF16)
        WIN0 = MOFF
        WIN, MOFF = view(MOFF, [128, 2, 8, 256], BF16)
        BS0 = MOFF
        Rht = Reg("HT")
        Rbr = [Reg("BR%d" % i) for i in range(8)]
        Rwin = [Reg("WIN0"), Reg("WIN1")]
        PV = sb("PV", [128, 64])
        Rpv = Reg("PV")
        BG = sb("BG", [128, 32])
        Rbg = Reg("BG")
        mc = {"win": 0, "pb": 0, "wg": 0, "wo": 0, "sg": 0}
        SEQS = [(0, 16, 0, "sample"), (16, 2, 1, "pA"), (18, 2, 1, "pB")]

        def mixer_params(l):
            load_T(PV[:, 0:6], conv_w[l].rearrange("j (c p) -> (j c) p", p=128), 6, Rpv)
            load_T(PV[:, 6:8], conv_b[l].rearrange("(c p) -> c p", p=128), 2, Rpv)
            load_T(PV[:, 8:10], ret_gn[l].rearrange("(c p) -> c p", p=128), 2, Rpv)
            load_T(PV[:, 10:12], s5_d[l].rearrange("(c p) -> c p", p=128), 2, Rpv)
            load_T(PV[:, 12:13], mla_kv_norm[l].rearrange("(c p) -> c p", p=128), 1, Rpv)
            load_T(BG[:, :], b_gate[l].rearrange("(c p) -> c p", p=128), 32, Rbg)

        def proj_fm(l, col0, ncols, NB, BW, evac, extra_reads=()):
            winv = w_in[l].rearrange("(kc p) n -> p kc n", p=128)
            sl = mc["win"] % 2
            mc["win"] += 1
            S.dma("pool", WIN[:, sl, :, 0:ncols], winv[:, :, col0:col0 + ncols], writes=[Rwin[sl]])
            for b in range(NB):
                bank = mc["pb"] % 4
                mc["pb"] += 1

                def emit(b=b, bank=bank):
                    inst = None
                    for kc in range(8):
                        inst = nc.tensor.matmul(PS[0:ncols, bank, 0:BW], lhsT=WIN[:, sl, kc, 0:ncols], rhs=HT[:, kc, b * BW:(b + 1) * BW],
                                                start=(kc == 0), stop=(kc == 7))
                    return inst
                S.op("pe", emit, reads=[Rwin[sl], Rht], writes=[RPS[bank]])
                evac(b, bank)

        def branch_conv(l, T, NB, BW):
            o = BS0
            Z, o = view(o, [128, 2056])
            CX, o = view(o, [128, 2048])
            Y, o = view(o, [128, 2048])
            CB, o = view(o, [128, 2048], BF16)
            Rz, Rcx, Ry, Rcb = Reg("Z"), Reg("CX"), Reg("Y"), Reg("CB")
            for cc in range(2):
                S.op("dve", lambda: nc.vector.memset(Z[:, 0:1], 0.0), writes=[Rz])
                S.op("dve", lambda: nc.vector.memset(Z[:, T + 1:T + 2], 0.0), writes=[Rz])
                proj_fm(l, 1280 + cc * 128, 128, NB, BW, lambda b, bank: S.op(
                    "act", lambda: nc.scalar.copy(out=CX[:, b * BW:(b + 1) * BW], in_=PS[:, bank, 0:BW]), reads=[RPS[bank]], writes=[Rcx]))
                proj_fm(l, 1792 + cc * 128, 128, NB, BW, lambda b, bank: S.op(
                    "dve", lambda: nc.vector.tensor_tensor(out=Z[:, 1 + b * BW:1 + (b + 1) * BW], in0=PS[:, bank, 0:BW], in1=CX[:, b * BW:(b + 1) * BW], op=ALU.mult),
                    reads=[RPS[bank], Rcx], writes=[Rz]))
                proj_fm(l, 1536 + cc * 128, 128, NB, BW, lambda b, bank: S.op(
                    "act", lambda: nc.scalar.copy(out=CB[:, b * BW:(b + 1) * BW], in_=PS[:, bank, 0:BW]), reads=[RPS[bank]], writes=[Rcb]))
                S.op("dve", lambda: nc.vector.tensor_scalar(out=Y[:, 0:T], in0=Z[:, 1:T + 1], scalar1=PV[:, 2 + cc:3 + cc], scalar2=PV[:, 6 + cc:7 + cc],
                                                            op0=ALU.mult, op1=ALU.add), reads=[Rz, Rpv], writes=[Ry])
                S.op("dve", lambda: nc.vector.scalar_tensor_tensor(out=Y[:, 0:T], in0=Z[:, 0:T], scalar=PV[:, 0 + cc:1 + cc], in1=Y[:, 0:T],
                                                                   op0=ALU.mult, op1=ALU.add), reads=[Rz, Rpv, Ry], writes=[Ry])
                S.op("dve", lambda: nc.vector.scalar_tensor_tensor(out=Y[:, 0:T], in0=Z[:, 2:T + 2], scalar=PV[:, 4 + cc:5 + cc], in1=Y[:, 0:T],
                                                                   op0=ALU.mult, op1=ALU.add), reads=[Rz, Rpv, Ry], writes=[Ry])
                S.op("dve", lambda: nc.vector.tensor_tensor(out=BR[:, 4 + cc, 0:T], in0=Y[:, 0:T], in1=CB[:, 0:T], op=ALU.mult),
                     reads=[Ry, Rcb], writes=[Rbr[4 + cc]])

        def gate_stage(l, t0, ntile, ci, T, NB, BW):
            o = WIN0
            WG, o = view(o, [128, 2, 8, 4, 128], BF16)
            WB, o = view(o, [128, 2, 2, 4, 128], BF16)
            MG, o = view(o, [128, 8, 512], BF16)
            SG, o = view(o, [128, 4, 512], BF16)
            WO, o = view(o, [128, 2, D], BF16)
            ACC, o = view(o, [128, 2, 512])
            cur["TMP"], o = view(o, [128, 2, D])
            Rwg = [Reg("WG0"), Reg("WG1")]
            Rmg = [Reg("MG%d" % c) for c in range(8)]
            Rsg = [Reg("SG%d" % n) for n in range(4)]
            Rwo = [Reg("WO0"), Reg("WO1")]
            Racc = [Reg("ACC0"), Reg("ACC1")]
            wgv = w_gate[l].rearrange("(kc p) (n d) -> p kc n d", p=128, n=4)
            wbv = w_branch[l].rearrange("n (kc p) d -> p kc n d", p=128)
            tpb = BW // 128
            for b in range(NB):
                for c in range(8):
                    sl = mc["wg"] % 2
                    mc["wg"] += 1
                    for n in range(4):
                        S.dma("pool", WG[:, sl, :, n, :], wgv[:, :, n, c * 128:(c + 1) * 128], writes=[Rwg[sl]])
                    for n in range(4):
                        S.dma("pool", WB[:, sl, :, n, :], wbv[:, :, n, c * 128:(c + 1) * 128], writes=[Rwg[sl]])
                    for n in range(4):
                        def emit_g(n=n, sl=sl):
                            inst = None
                            for kc in range(8):
                                inst = nc.tensor.matmul(PS[:, n, 0:BW], lhsT=WG[:, sl, kc, n, :], rhs=HT[:, kc, b * BW:(b + 1) * BW],
                                                        start=(kc == 0), stop=(kc == 7))
                            return inst
                        S.op("pe", emit_g, reads=[Rwg[sl], Rht], writes=[RPS[n]])

                        def emit_p(n=n, sl=sl):
                            inst = None
                            for kc in range(2):
                                inst = nc.tensor.matmul(PS[:, 4 + n, 0:BW], lhsT=WB[:, sl, kc, n, :], rhs=BR[:, 2 * n + kc, b * BW:(b + 1) * BW],
                                                        start=(kc == 0), stop=(kc == 1))
                            return inst
                        S.op("pe", emit_p, reads=[Rwg[sl], Rbr[2 * n], Rbr[2 * n + 1]], writes=[RPS[4 + n]])
                        S.op("act", lambda n=n: nc.scalar.activation(out=SG[:, n, 0:BW], in_=PS[:, n, 0:BW], func=AF.Sigmoid,
                                                                     bias=BG[:, n * 8 + c:n * 8 + c + 1]), reads=[RPS[n], Rbg], writes=[Rsg[n]])
                    S.op("dve", lambda: nc.vector.tensor_tensor(out=ACC[:, 0, 0:BW], in0=PS[:, 4, 0:BW], in1=SG[:, 0, 0:BW], op=ALU.mult),
                         reads=[RPS[4], Rsg[0]], writes=[Racc[0]])
                    for n in range(1, 4):
                        S.op("dve", lambda n=n: nc.vector.tensor_tensor(out=ACC[:, 1, 0:BW], in0=PS[:, 4 + n, 0:BW], in1=SG[:, n, 0:BW], op=ALU.mult),
                             reads=[RPS[4 + n], Rsg[n]], writes=[Racc[1]])
                        if n < 3:
                            S.op("dve", lambda: nc.vector.tensor_tensor(out=ACC[:, 0, 0:BW], in0=ACC[:, 0, 0:BW], in1=ACC[:, 1, 0:BW], op=ALU.add),
                                 reads=[Racc[0], Racc[1]], writes=[Racc[0]])
                        else:
                            S.op("dve", lambda: nc.vector.tensor_tensor(out=MG[:, c, 0:BW], in0=ACC[:, 0, 0:BW], in1=ACC[:, 1, 0:BW], op=ALU.add),
                                 reads=[Racc[0], Racc[1]], writes=[Rmg[c]])
                for c in range(8):
                    sl = mc["wo"] % 2
                    mc["wo"] += 1
                    S.dma("pool", WO[:, sl, :], w_o[l, c * 128:(c + 1) * 128, :], writes=[Rwo[sl]])

                    def emit_o(c=c, sl=sl):
                        inst = None
                        for tt in range(tpb):
                            for half in range(2):
                                inst = nc.tensor.matmul(PS[:, 2 * tt + half, :], lhsT=MG[:, c, tt * 128:(tt + 1) * 128],
                                                        rhs=WO[:, sl, half * 512:(half + 1) * 512], start=(c == 0), stop=(c == 7))
                        return inst
                    S.op("pe", emit_o, reads=[Rmg[c], Rwo[sl]], writes=RPS[0:2 * tpb])
                for tt in range(tpb):
                    postnorm_tile(t0 + b * tpb + tt, ci, 2 * tt)

        onesb = sb("onesb", [128, 128], BF16)
        S.op("dve", lambda: nc.vector.memset(onesb[:], 1.0), writes=[Rones])
        ATT_SCALE = float(96 ** -0.5)

        def branch_mla(l, t0, T, NB, BW, kind):
            sample = kind == "sample"
            Skeys = T + (512 if sample else 0)
            NKT = Skeys // 128
            o = BS0
            CQ, o = view(o, [128, 2, 2048], BF16)
            CKVN, o = view(o, [128, 2560], BF16)
            KR, o = view(o, [128, 2560], BF16)
            WUQ, o = view(o, [128, 2, 384], BF16)
            WUQS, o = view(o, [128, 2, 4, 32], BF16)
            WUKV, o = view(o, [128, 512], BF16)
            WKRS, o = view(o, [128, 8, 32], BF16)
            QNV, o = view(o, [128, 2])
            oB = o
            Rcq, Rckvn, Rkr, Rw = Reg("m_CQ"), Reg("m_CKVN"), Reg("m_KR"), Reg("m_W")
            W32, oo = view(oB, [128, 2, 384])
            Rw32 = Reg("m_W32")
            S.dma("sp", W32[:, 0, :], mla_w_uq[l, 0:128, :], writes=[Rw32])
            S.dma("sp", W32[0:64, 1, :], mla_w_uq[l, 128:192, :], writes=[Rw32])
            S.dma("sp", QNV[:, 0:1], mla_q_norm[l, 0:128].unsqueeze(1), writes=[Rw])
            S.dma("sp", QNV[0:64, 1:2], mla_q_norm[l, 128:192].unsqueeze(1), writes=[Rw])
            S.dma("pool", WUKV[:, :], mla_w_ukv[l, :, :], writes=[Rw])
            for kc, np_ in ((0, 128), (1, 64)):
                S.op("dve", lambda kc=kc, np_=np_: nc.vector.tensor_scalar(out=WUQ[0:np_, kc, :], in0=W32[0:np_, kc, :], scalar1=QNV[0:np_, kc:kc + 1],
                                                                          scalar2=None, op0=ALU.mult), reads=[Rw32, Rw], writes=[Rw])
                if sample:
                    wv = WUQ[0:np_, kc, :].rearrange("p (h e) -> p h e", h=4)
                    S.op("dve", lambda wv=wv, kc=kc, np_=np_: nc.vector.tensor_scalar(out=WUQS[0:np_, kc, :, 0:16], in0=wv[:, :, 80:96], scalar1=-1.0,
                                                                                   scalar2=None, op0=ALU.mult), reads=[Rw], writes=[Rw])
                    S.op("dve", lambda wv=wv, kc=kc, np_=np_: nc.vector.tensor_copy(out=WUQS[0:np_, kc, :, 16:32], in_=wv[:, :, 64:80]), reads=[Rw], writes=[Rw])
            if sample:
                winv = w_in[l].rearrange("(kc p) n -> p kc n", p=128)
                S.dma("pool", WKRS[:, :, 0:16], winv[:, :, 2384:2400], writes=[Rw])
                S.dma("pool", WKRS[:, :, 16:32], winv[:, :, 2368:2384], writes=[Rw])
                S.op("dve", lambda: nc.vector.tensor_scalar(out=WKRS[:, :, 0:16], in0=WKRS[:, :, 0:16], scalar1=-1.0, scalar2=None, op0=ALU.mult),
                     reads=[Rw], writes=[Rw])
            SQ, oo = view(oo, [128, 2, 512], BF16)
            RST, oo = view(oo, [128, 512])
            TB, oo = view(oo, [128, 2, 512])
            T1, oo = view(oo, [128, 2, 512])
            Rsq, Rrst, Rtb, Rt1 = Reg("m_SQ"), Reg("m_RST"), Reg("m_TB"), Reg("m_T1")

            def rstd_from_ps(bank, parts):
                S.op("act", lambda: nc.scalar.activation(out=RST[:, 0:BW], in_=PS[:, bank, 0:BW], func=AF.Sqrt, scale=1.0 / parts, bias=epsb[:, 0:1]),
                     reads=[RPS[bank], Reps], writes=[Rrst])
                S.op("dve", lambda: nc.vector.reciprocal(out=RST[:, 0:BW], in_=RST[:, 0:BW]), reads=[Rrst], writes=[Rrst])

            winv = w_in[l].rearrange("(kc p) n -> p kc n", p=128)
            WQ = WIN
            S.dma("pool", WQ[:, 0, :, 0:192], winv[:, :, 2048:2240], writes=[Rwin[0]])
            S.dma("pool", WQ[:, 1, :, 0:160], winv[:, :, 2240:2400], writes=[Rwin[1]])
            for b in range(NB):
                cols = slice(b * BW, (b + 1) * BW)
                for kc2, np_, bank in ((0, 128, 0), (1, 64, 1)):
                    def emit(kc2=kc2, np_=np_, bank=bank):
                        inst = None
                        for kc in range(8):
                            inst = nc.tensor.matmul(PS[0:np_, bank, 0:BW], lhsT=WQ[:, 0, kc, kc2 * 128:kc2 * 128 + np_], rhs=HT[:, kc, cols],
                                                    start=(kc == 0), stop=(kc == 7))
                        return inst
                    S.op("pe", emit, reads=[Rwin[0], Rht], writes=[RPS[bank]])
                    S.op("act", lambda kc2=kc2, np_=np_, bank=bank: nc.scalar.activation(out=SQ[0:np_, kc2, 0:BW], in_=PS[0:np_, bank, 0:BW], func=AF.Square),
                         reads=[RPS[bank]], writes=[Rsq])

                def emit_ss():
                    nc.tensor.matmul(PS[:, 2, 0:BW], lhsT=onesb[:, :], rhs=SQ[:, 0, 0:BW], start=True, stop=False)
                    return nc.tensor.matmul(PS[:, 2, 0:BW], lhsT=onesb[0:64, :], rhs=SQ[0:64, 1, 0:BW], start=False, stop=True)
                S.op("pe", emit_ss, reads=[Rsq, Rones], writes=[RPS[2]])
                rstd_from_ps(2, 192.0)
                for kc2, np_, bank in ((0, 128, 0), (1, 64, 1)):
                    S.op("dve", lambda kc2=kc2, np_=np_, bank=bank: nc.vector.tensor_tensor(out=CQ[0:np_, kc2, cols], in0=PS[0:np_, bank, 0:BW], in1=RST[0:np_, 0:BW], op=ALU.mult),
                         reads=[RPS[bank], Rrst], writes=[Rcq])
                def emit_kv():
                    inst = None
                    for kc in range(8):
                        inst = nc.tensor.matmul(PS[:, 3, 0:BW], lhsT=WQ[:, 1, kc, 0:128], rhs=HT[:, kc, cols], start=(kc == 0), stop=(kc == 7))
                    return inst
                S.op("pe", emit_kv, reads=[Rwin[1], Rht], writes=[RPS[3]])
                S.op("act", lambda: nc.scalar.activation(out=SQ[:, 0, 0:BW], in_=PS[:, 3, 0:BW], func=AF.Square), reads=[RPS[3]], writes=[Rsq])
                S.op("pe", lambda: nc.tensor.matmul(PS[:, 2, 0:BW], lhsT=onesb[:, :], rhs=SQ[:, 0, 0:BW], start=True, stop=True), reads=[Rsq, Rones], writes=[RPS[2]])
                rstd_from_ps(2, 128.0)
                S.op("dve", lambda: nc.vector.scalar_tensor_tensor(out=CKVN[:, cols], in0=PS[:, 3, 0:BW], scalar=PV[:, 12:13], in1=RST[:, 0:BW], op0=ALU.mult, op1=ALU.mult),
                     reads=[RPS[3], Rpv, Rrst], writes=[Rckvn])
                def emit_kr():
                    inst = None
                    for kc in range(8):
                        inst = nc.tensor.matmul(PS[0:32, 4, 0:BW], lhsT=WQ[:, 1, kc, 128:160], rhs=HT[:, kc, cols], start=(kc == 0), stop=(kc == 7))
                    return inst
                S.op("pe", emit_kr, reads=[Rwin[1], Rht], writes=[RPS[4]])
                if sample:
                    def emit_krs():
                        inst = None
                        for kc in range(8):
                            inst = nc.tensor.matmul(PS[0:32, 5, 0:BW], lhsT=WKRS[:, kc, :], rhs=HT[:, kc, cols], start=(kc == 0), stop=(kc == 7))
                        return inst
                    S.op("pe", emit_krs, reads=[Rw, Rht], writes=[RPS[5]])
                    S.dma("sp", TB[0:32, 0, 0:BW], c_rope_mla[0, :, cols], writes=[Rtb])
                    S.dma("sp", TB[0:32, 1, 0:BW], c_rope_mla[1, :, cols], writes=[Rtb])
                    S.op("dve", lambda: nc.vector.tensor_tensor(out=T1[0:32, 0, 0:BW], in0=PS[0:32, 4, 0:BW], in1=TB[0:32, 0, 0:BW], op=ALU.mult),
                         reads=[RPS[4], Rtb], writes=[Rt1])
                    S.op("dve", lambda: nc.vector.tensor_tensor(out=T1[0:32, 1, 0:BW], in0=PS[0:32, 5, 0:BW], in1=TB[0:32, 1, 0:BW], op=ALU.mult),
                         reads=[RPS[5], Rtb], writes=[Rt1])
                    S.op("dve", lambda: nc.vector.tensor_tensor(out=KR[0:32, cols], in0=T1[0:32, 0, 0:BW], in1=T1[0:32, 1, 0:BW], op=ALU.add),
                         reads=[Rt1], writes=[Rkr])
                else:
                    S.op("act", lambda: nc.scalar.copy(out=KR[0:32, cols], in_=PS[0:32, 4, 0:BW]), reads=[RPS[4]], writes=[Rkr])
            if sample:
                CT, _ = view(oB + 3072, [128, 4, 160])
                Rct = Reg("m_CT")
                S.barrier()
                S.dma("sp", CT[:, :, :], ctx_mla[l].rearrange("(i p) f -> p i f", p=128), writes=[Rct])
                for i in range(4):
                    S.op("pe", lambda i=i: nc.tensor.transpose(out=PS[:, 6, 0:128], in_=CT[:, i, 0:128], identity=ident[:]), reads=[Rct, Rid], writes=[RPS[6]])
                    S.op("act", lambda i=i: nc.scalar.copy(out=CKVN[:, T + i * 128:T + (i + 1) * 128], in_=PS[:, 6, 0:128]), reads=[RPS[6]], writes=[Rckvn])
                    S.op("pe", lambda i=i: nc.tensor.transpose(out=PS[0:32, 7, 0:128], in_=CT[:, i, 128:160], identity=ident[:]), reads=[Rct, Rid], writes=[RPS[7]])
                    S.op("act", lambda i=i: nc.scalar.copy(out=KR[0:32, T + i * 128:T + (i + 1) * 128], in_=PS[0:32, 7, 0:128]), reads=[RPS[7]], writes=[Rkr])
            S.barrier()
            o = oB
            KN, o = view(o, [128, 2560], BF16)
            QN, o = view(o, [128, 2048], BF16)
            QR, o = view(o, [128, 2048], BF16)
            VA, o = view(o, [128, 20, 66], BF16)
            Rkn, Rqn, Rqr, Rva = Reg("m_KN"), Reg("m_QN"), Reg("m_QR"), Reg("m_VA")
            ow = WIN0
            TB2, ow2 = view(ow, [128, 2, 512])
            T2, ow2 = view(ow2, [128, 2, 512])
            PT, ow3 = view(ow, [128, 2, 512], BF16)
            OS, ow3 = view(ow3, [128, 512])
            OT, ow3 = view(ow3, [128, 512], BF16)
            Rtb2, Rt2, Rpt, Ros, Rot = Reg("m_TB2"), Reg("m_T2"), [Reg("m_PT0"), Reg("m_PT1")], Reg("m_OS"), Reg("m_OT")
            S.op("dve", lambda: nc.vector.memset(VA[:, :, 64:66], 1.0), writes=[Rva])
            KBW = 512
            for h in range(4):
                for kb in range((Skeys + KBW - 1) // KBW):
                    w = min(KBW, Skeys - kb * KBW)
                    bank = mc["pb"] % 4
                    mc["pb"] += 1
                    S.op("pe", lambda kb=kb, w=w, bank=bank: nc.tensor.matmul(PS[0:64, bank, 0:w], lhsT=WUKV[:, h * 128:h * 128 + 64], rhs=CKVN[:, kb * KBW:kb * KBW + w],
                                                                              start=True, stop=True), reads=[Rw, Rckvn], writes=[RPS[bank]])
                    S.op("act", lambda kb=kb, w=w, bank=bank: nc.scalar.copy(out=KN[0:64, kb * KBW:kb * KBW + w], in_=PS[0:64, bank, 0:w]), reads=[RPS[bank]], writes=[Rkn])
                for kt in range(NKT):
                    bank = mc["pb"] % 4
                    mc["pb"] += 1
                    S.op("pe", lambda kt=kt, bank=bank: nc.tensor.matmul(PS[:, bank, 0:64], lhsT=CKVN[:, kt * 128:(kt + 1) * 128], rhs=WUKV[:, h * 128 + 64:h * 128 + 128],
                                                                          start=True, stop=True), reads=[Rw, Rckvn], writes=[RPS[bank]])
                    S.op("dve", lambda kt=kt, bank=bank: nc.vector.tensor_copy(out=VA[:, kt, 0:64], in_=PS[:, bank, 0:64]), reads=[RPS[bank]], writes=[Rva])
                for b in range(NB):
                    cols = slice(b * BW, (b + 1) * BW)
                    bank = mc["pb"] % 4
                    mc["pb"] += 1

                    def emit_q(c0, m, bank, wt=None):
                        def f():
                            if wt is None:
                                nc.tensor.matmul(PS[0:m, bank, 0:BW], lhsT=WUQ[:, 0, c0:c0 + m], rhs=CQ[:, 0, cols], start=True, stop=False)
                                return nc.tensor.matmul(PS[0:m, bank, 0:BW], lhsT=WUQ[0:64, 1, c0:c0 + m], rhs=CQ[0:64, 1, cols], start=False, stop=True)
                            nc.tensor.matmul(PS[0:m, bank, 0:BW], lhsT=WUQS[:, 0, h, :], rhs=CQ[:, 0, cols], start=True, stop=False)
                            return nc.tensor.matmul(PS[0:m, bank, 0:BW], lhsT=WUQS[0:64, 1, h, :], rhs=CQ[0:64, 1, cols], start=False, stop=True)
                        return f
                    S.op("pe", emit_q(h * 96, 64, bank), reads=[Rw, Rcq], writes=[RPS[bank]])
                    S.op("act", lambda bank=bank: nc.scalar.copy(out=QN[0:64, cols], in_=PS[0:64, bank, 0:BW]), reads=[RPS[bank]], writes=[Rqn])
                    bank2 = mc["pb"] % 4
                    mc["pb"] += 1
                    S.op("pe", emit_q(h * 96 + 64, 32, bank2), reads=[Rw, Rcq], writes=[RPS[bank2]])
                    if sample:
                        bank3 = mc["pb"] % 4
                        mc["pb"] += 1
                        S.op("pe", emit_q(0, 32, bank3, wt=1), reads=[Rw, Rcq], writes=[RPS[bank3]])
                        S.dma("sp", TB2[0:32, 0, 0:BW], c_rope_mla[0, :, cols], writes=[Rtb2])
                        S.dma("sp", TB2[0:32, 1, 0:BW], c_rope_mla[1, :, cols], writes=[Rtb2])
                        S.op("dve", lambda: nc.vector.tensor_tensor(out=T2[0:32, 0, 0:BW], in0=PS[0:32, bank2, 0:BW], in1=TB2[0:32, 0, 0:BW], op=ALU.mult),
                             reads=[RPS[bank2], Rtb2], writes=[Rt2])
                        S.op("dve", lambda: nc.vector.tensor_tensor(out=T2[0:32, 1, 0:BW], in0=PS[0:32, bank3, 0:BW], in1=TB2[0:32, 1, 0:BW], op=ALU.mult),
                             reads=[RPS[bank3], Rtb2], writes=[Rt2])
                        S.op("dve", lambda: nc.vector.tensor_tensor(out=QR[0:32, cols], in0=T2[0:32, 0, 0:BW], in1=T2[0:32, 1, 0:BW], op=ALU.add),
                             reads=[Rt2], writes=[Rqr])
                    else:
                        S.op("act", lambda: nc.scalar.copy(out=QR[0:32, cols], in_=PS[0:32, bank2, 0:BW]), reads=[RPS[bank2]], writes=[Rqr])
                S.barrier()
                for b in range(NB):
                    cols = slice(b * BW, (b + 1) * BW)
                    ob = 4 + (b % 2)
                    for kt in range(NKT):
                        bank = mc["pb"] % 4
                        mc["pb"] += 1
                        pi = kt % 2

                        def emit_s(kt=kt, bank=bank):
                            nc.tensor.matmul(PS[:, bank, 0:BW], lhsT=KN[0:64, kt * 128:(kt + 1) * 128], rhs=QN[0:64, cols], start=True, stop=False)
                            return nc.tensor.matmul(PS[:, bank, 0:BW], lhsT=KR[0:32, kt * 128:(kt + 1) * 128], rhs=QR[0:32, cols], start=False, stop=True)
                        S.op("pe", emit_s, reads=[Rkn, Rqn, Rkr, Rqr], writes=[RPS[bank]])
                        S.op("act", lambda bank=bank, pi=pi: nc.scalar.activation(out=PT[:, pi, 0:BW], in_=PS[:, bank, 0:BW], func=AF.Exp, scale=ATT_SCALE),
                             reads=[RPS[bank]], writes=[Rpt[pi]])
                        S.op("pe", lambda kt=kt, pi=pi: nc.tensor.matmul(PS[0:65, ob, 0:BW], lhsT=VA[:, kt, 0:65], rhs=PT[:, pi, 0:BW], start=(kt == 0), stop=(kt == NKT - 1)),
                             reads=[Rva, Rpt[pi]], writes=[RPS[ob]])
                    S.op("act", lambda: nc.scalar.copy(out=OS[0:65, 0:BW], in_=PS[0:65, ob, 0:BW]), reads=[RPS[ob]], writes=[Ros])
                    S.op("dve", lambda: nc.vector.reciprocal(out=OS[64:65, 0:BW], in_=OS[64:65, 0:BW]), reads=[Ros], writes=[Ros])
                    S.op("pe", lambda: nc.tensor.matmul(PS[0:64, 6, 0:BW], lhsT=ones32[64:65, 0:64], rhs=OS[64:65, 0:BW], start=True, stop=True),
                         reads=[Ros, Rones], writes=[RPS[6]])
                    S.op("dve", lambda: nc.vector.tensor_tensor(out=OT[0:64, 0:BW], in0=PS[0:64, 6, 0:BW], in1=OS[0:64, 0:BW], op=ALU.mult),
                         reads=[RPS[6], Ros], writes=[Rot])
                    S.dma("sp", BR[(h % 2) * 64:(h % 2) * 64 + 64, 6 + h // 2, cols], OT[0:64, 0:BW], reads=[Rot], writes=[Rbr[6 + h // 2]], key=Reg("m_OTd"))
                S.barrier()

        def mla_cache_out(l, t0, ntile, pi):
            o = BS0
            CA, o = view(o, [128, 2, 160])
            KVB, o = view(o, [128, 128])
            Rca, Rkvb = [Reg("m_CA0"), Reg("m_CA1")], Reg("m_KVB")
            winv = w_in[l].rearrange("(kc p) n -> p kc n", p=128)
            S.dma("pool", WIN[:, 0, :, 0:160], winv[:, :, 2240:2400], writes=[Rwin[0]])
            S.dma("sp", KVB[:, :], mla_kv_norm[l:l + 1, :].broadcast_to([128, 128]), writes=[Rkvb])
            for tt in range(ntile):
                bank = mc["pb"] % 4
                mc["pb"] += 1
                smi = rot["sm"] % 4
                rot["sm"] += 1
                sm = small[:, smi, :]

                def emit(tt=tt, bank=bank):
                    inst = None
                    for kc in range(8):
                        inst = nc.tensor.matmul(PS[:, bank, 0:160], lhsT=HT[:, kc, tt * 128:(tt + 1) * 128], rhs=WIN[:, 0, kc, 0:160], start=(kc == 0), stop=(kc == 7))
                    return inst
                S.op("pe", emit, reads=[Rwin[0], Rht], writes=[RPS[bank]])
                S.op("act", lambda: nc.scalar.activation(out=junk[:, 0:128], in_=PS[:, bank, 0:128], func=AF.Square, accum_out=sm[:, 0:1]),
                     reads=[RPS[bank]], writes=[Rjunk, Rsmall[smi]])
                S.op("act", lambda: nc.scalar.activation(out=sm[:, 1:2], in_=sm[:, 0:1], func=AF.Sqrt, scale=1.0 / 128, bias=epsb[:, 0:1]),
                     reads=[Rsmall[smi], Reps], writes=[Rsmall[smi]])
                S.op("dve", lambda: nc.vector.reciprocal(out=sm[:, 2:3], in_=sm[:, 1:2]), reads=[Rsmall[smi]], writes=[Rsmall[smi]])
                ci_ = tt % 2
                S.op("dve", lambda: nc.vector.scalar_tensor_tensor(out=CA[:, ci_, 0:128], in0=PS[:, bank, 0:128], scalar=sm[:, 2:3], in1=KVB[:, :], op0=ALU.mult, op1=ALU.mult),
                     reads=[RPS[bank], Rsmall[smi], Rkvb], writes=[Rca[ci_]])
                S.op("dve", lambda: nc.vector.tensor_copy(out=CA[:, ci_, 128:160], in_=PS[:, bank, 128:160]), reads=[RPS[bank]], writes=[Rca[ci_]])
                OUT_EVS.append(S.dma("sp", o_mla[pi, l, tt * 128:(tt + 1) * 128, :], CA[:, ci_, :], reads=[Rca[ci_]], key=Reg("o_mla_d%d" % ci_)))

        def branch_ret(l, t0, T, kind):
            sample = kind == "sample"
            n = T // 128
            o = BS0
            WRb, o = view(o, [128, 8, 512], BF16)
            SBst, o = view(o, [128, 16, 256], BF16)
            DM, o = view(o, [128, 4, 128], BF16)
            QD, o = view(o, [128, 2, 4, 128], BF16)
            CD, o = view(o, [128, 2, 256])
            LG, o = view(o, [128, 8])
            KD, o = view(o, [128, 2, 4])
            SF, o = view(o, [128, 256])
            SB, o = view(o, [128, 256])
            SFb, o = view(o, [128, 256], BF16)
            oT = o
            WRa, _ = view(WIN0, [128, 8, 512], BF16)
            Rwr, Rsbst, Rtab, Rsf, Rsb, Rsfb = Reg("r_WR"), Reg("r_SBst"), Reg("r_TAB"), Reg("r_SF"), Reg("r_SB"), Reg("r_SFb")
            winv = w_in[l].rearrange("(kc p) n -> p kc n", p=128)
            S.dma("pool", WRa[:, :, :], winv[:, :, 256:768], writes=[Rwr])
            S.dma("pool", WRb[:, :, :], winv[:, :, 768:1280], writes=[Rwr])
            CT6, o2 = view(oT, [128, 6, 128])
            E1, o2 = view(o2, [128, 2, 128])
            PIDX, o2 = view(o2, [128, 2])
            C128, o2 = view(o2, [128, 64])
            Rc6, Re1 = Reg("r_C6"), Reg("r_E1")
            S.dma("sp", CT6[:, :, :], c_ret.rearrange("k p i -> p k i"), writes=[Rc6])
            S.dma("sp", PIDX[:, :], c_pidx[:, :], writes=[Rc6])
            S.dma("sp", LG[:, :], ret_decay[l:l + 1].rearrange("o d h -> o (d h)").broadcast_to([128, 8]), writes=[Rtab])
            S.op("dve", lambda: nc.vector.memset(C128[:, :], 128.0), writes=[Rc6])
            S.op("act", lambda: nc.scalar.activation(out=LG[:, :], in_=LG[:, :], func=AF.Sigmoid), reads=[Rtab], writes=[Rtab])
            S.op("act", lambda: nc.scalar.activation(out=LG[:, :], in_=LG[:, :], func=AF.Ln), reads=[Rtab], writes=[Rtab])
            for h in range(4):
                S.op("act", lambda h=h: nc.scalar.activation(out=E1[:, 0, :], in_=CT6[:, 0, :], func=AF.Exp, scale=LG[:, h:h + 1]), reads=[Rc6, Rtab], writes=[Re1])
                S.op("act", lambda h=h: nc.scalar.activation(out=E1[:, 1, :], in_=CT6[:, 1, :], func=AF.Exp, scale=LG[:, 4 + h:5 + h]), reads=[Rc6, Rtab], writes=[Re1])
                S.op("dve", lambda h=h: nc.vector.tensor_tensor(out=E1[:, :, :], in0=E1[:, :, :], in1=CT6[:, 2:4, :], op=ALU.mult), reads=[Re1, Rc6], writes=[Re1])
                S.op("dve", lambda h=h: nc.vector.tensor_tensor(out=DM[:, h, :], in0=E1[:, 0, :], in1=E1[:, 1, :], op=ALU.add), reads=[Re1], writes=[Rtab])
                for d in range(2):
                    S.op("act", lambda h=h, d=d: nc.scalar.activation(out=QD[:, d, h, :], in_=CT6[:, 4 + d, :], func=AF.Exp, scale=LG[:, d * 4 + h:d * 4 + h + 1]),
                         reads=[Rc6, Rtab], writes=[Rtab])
                    S.op("act", lambda h=h, d=d: nc.scalar.activation(out=CD[:, d, h * 64:(h + 1) * 64], in_=C128[:, :], func=AF.Exp, scale=LG[:, d * 4 + h:d * 4 + h + 1]),
                         reads=[Rc6, Rtab], writes=[Rtab])
                    S.op("act", lambda h=h, d=d: nc.scalar.activation(out=KD[:, d, h:h + 1], in_=PIDX[:, d:d + 1], func=AF.Exp, scale=LG[:, d * 4 + h:d * 4 + h + 1]),
                         reads=[Rc6, Rtab], writes=[Rtab])
            S.op("dve", lambda: nc.vector.tensor_scalar(out=KD[:, :, :], in0=KD[:, :, :], scalar1=0.125, scalar2=None, op0=ALU.mult), reads=[Rtab], writes=[Rtab])
            if sample:
                S.dma("sp", SF[0:64, :].rearrange("d (h e) -> d h e", h=4), st_ret[l, 0].rearrange("h d e -> d h e"), writes=[Rsf])
                S.dma("sp", SB[0:64, :].rearrange("d (h e) -> d h e", h=4), st_ret[l, 1].rearrange("h d e -> d h e"), writes=[Rsb])
            else:
                S.op("dve", lambda: nc.vector.memset(SF[0:64, :], 0.0), writes=[Rsf])
                S.op("dve", lambda: nc.vector.memset(SB[0:64, :], 0.0), writes=[Rsb])
            S.barrier()
            if cfg.get("ret_stop") == "tables":
                return
            o3 = oT
            QK, o3 = view(o3, [128, 512], BF16)
            TA, o3 = view(o3, [128, 256])
            TBt, o3 = view(o3, [128, 256])
            RT, o3 = view(o3, [128, 2, 32])
            KDt, o3 = view(o3, [128, 256], BF16)
            VTc, o3 = view(o3, [128, 256], BF16)
            SRG, o3 = view(o3, [128, 256], BF16)
            QT, o3 = view(o3, [128, 3, 512], BF16)
            KT, o3 = view(o3, [128, 512], BF16)
            AM, o3 = view(o3, [128, 512], BF16)
            CEN, o3 = view(o3, [128, 256])
            SQr, o3 = view(o3, [128, 256])
            NRo, o3 = view(o3, [128, 256], BF16)
            MS, o3 = view(o3, [128, 8])
            Rqk, Rta, Rrt, Rkd, Rvt, Rsrg, Rqt, Rkt, Ram, Rcen, Rsq, Rnro, Rms = (Reg("r_" + x) for x in
                ("QK", "TA", "RT", "KDt", "VTc", "SRG", "QT", "KT", "AM", "CEN", "SQ", "NRo", "MS"))
            PSb = lambda bank: PS[:, bank, :].bitcast(BF16)

            def proj(c, bank, WRx, c0, ncol):
                def emit():
                    inst = None
                    for kc in range(8):
                        inst = nc.tensor.matmul(PS[:, bank, 0:ncol], lhsT=HT[:, kc, c * 128:(c + 1) * 128], rhs=WRx[:, kc, c0:c0 + ncol], start=(kc == 0), stop=(kc == 7))
                    return inst
                S.op("pe", emit, reads=[Rwr, Rht], writes=[RPS[bank]])

            def rope(c, bank, col0, ng, dst):
                src = PS[:, bank, col0:col0 + ng * 64].rearrange("p (g t e) -> p g t e", g=ng, t=2)
                dv = dst.rearrange("p (g t e) -> p g t e", g=ng, t=2)
                if not sample:
                    S.op("act", lambda: nc.scalar.copy(out=dst, in_=PS[:, bank, col0:col0 + ng * 64]), reads=[RPS[bank]], writes=[Rqk])
                    return
                S.dma("sp", RT[:, 0, :], c_rope_ret[0, c * 128:(c + 1) * 128, :], writes=[Rrt])
                S.dma("sp", RT[:, 1, :], c_rope_ret[1, c * 128:(c + 1) * 128, :], writes=[Rrt])
                cosb = RT[:, 0, :].unsqueeze(1).broadcast_to([128, ng, 32])
                sinb = RT[:, 1, :].unsqueeze(1).broadcast_to([128, ng, 32])
                ta = TA[:, 0:ng * 32].rearrange("p (g e) -> p g e", g=ng)
                tb = TBt[:, 0:ng * 32].rearrange("p (g e) -> p g e", g=ng)
                S.op("dve", lambda: nc.vector.tensor_tensor(out=ta, in0=src[:, :, 0, :], in1=cosb, op=ALU.mult), reads=[RPS[bank], Rrt], writes=[Rta])
                S.op("dve", lambda: nc.vector.tensor_tensor(out=tb, in0=src[:, :, 1, :], in1=sinb, op=ALU.mult), reads=[RPS[bank], Rrt], writes=[Rta])
                S.op("dve", lambda: nc.vector.tensor_tensor(out=dv[:, :, 0, :], in0=ta, in1=tb, op=ALU.subtract), reads=[Rta], writes=[Rqk])
                S.op("dve", lambda: nc.vector.tensor_tensor(out=ta, in0=src[:, :, 0, :], in1=sinb, op=ALU.mult), reads=[RPS[bank], Rrt], writes=[Rta])
                S.op("dve", lambda: nc.vector.tensor_tensor(out=tb, in0=src[:, :, 1, :], in1=cosb, op=ALU.mult), reads=[RPS[bank], Rrt], writes=[Rta])
                S.op("dve", lambda: nc.vector.tensor_tensor(out=dv[:, :, 1, :], in0=ta, in1=tb, op=ALU.add), reads=[Rta], writes=[Rqk])

            def kdec_mul(d, ksrc):
                S.op("dve", lambda: nc.vector.tensor_tensor(out=KDt[:, :].rearrange("p (h e) -> p h e", h=4), in0=ksrc.rearrange("p (h e) -> p h e", h=4),
                                                            in1=KD[:, d, :].unsqueeze(2).broadcast_to([128, 4, 64]), op=ALU.mult), reads=[Rqk, Rtab], writes=[Rkd])

            def umat(bank):
                def emit():
                    inst = None
                    for h in range(4):
                        inst = nc.tensor.matmul(PS[0:64, bank, h * 64:(h + 1) * 64], lhsT=KDt[:, h * 64:(h + 1) * 64], rhs=VTc[:, h * 64:(h + 1) * 64], start=True, stop=True)
                    return inst
                S.op("pe", emit, reads=[Rkd, Rvt], writes=[RPS[bank]])

            def state_update(St, Rst, d, bank):
                S.op("dve", lambda: nc.vector.tensor_tensor(out=St[0:64, :], in0=St[0:64, :], in1=CD[0:64, d, :], op=ALU.mult), reads=[Rst, Rtab], writes=[Rst])
                S.op("dve", lambda: nc.vector.tensor_tensor(out=St[0:64, :], in0=St[0:64, :], in1=PS[0:64, bank, 0:256], op=ALU.add), reads=[Rst, RPS[bank]], writes=[Rst])

            for c in range(n - 1, -1, -1):
                proj(c, 0, WRa, 256, 256)
                proj(c, 1, WRb, 0, 256)
                rope(c, 0, 0, 4, QK[:, 0:256])
                S.op("act", lambda: nc.scalar.copy(out=VTc[:, :], in_=PS[:, 1, 0:256]), reads=[RPS[1]], writes=[Rvt])
                kdec_mul(1, QK[:, 0:256])
                umat(5)
                S.op("act", lambda c=c: nc.scalar.copy(out=SBst[0:64, c, :], in_=SB[0:64, :]), reads=[Rsb], writes=[Rsbst])
                state_update(SB, Rsb, 1, 5)
            S.op("act", lambda: nc.scalar.copy(out=SFb[0:64, :], in_=SF[0:64, :]), reads=[Rsf], writes=[Rsfb])
            if cfg.get("ret_stop") == "pass1":
                return
            for c in range(n):
                proj(c, 0, WRa, 0, 512)
                proj(c, 1, WRb, 0, 512)
                rope(c, 0, 0, 8, QK[:, :])
                S.op("act", lambda: nc.scalar.copy(out=VTc[:, :], in_=PS[:, 1, 0:256]), reads=[RPS[1]], writes=[Rvt])
                S.op("act", lambda: nc.scalar.activation(out=SRG[:, :], in_=PS[:, 1, 256:512], func=AF.Silu), reads=[RPS[1]], writes=[Rsrg])
                kdec_mul(0, QK[:, 256:512])

                def emit_t():
                    inst = None
                    for g in range(8):
                        inst = nc.tensor.transpose(out=PSb(2)[0:64, g * 128:(g + 1) * 128], in_=QK[:, g * 64:(g + 1) * 64], identity=identb[:])
                    return inst
                S.op("pe", emit_t, reads=[Rqk, Rid], writes=[RPS[2]])
                S.op("dve", lambda: nc.vector.tensor_copy(out=QT[0:64, 0, :], in_=PSb(2)[0:64, 0:512]), reads=[RPS[2]], writes=[Rqt])
                for d in range(2):
                    S.op("dve", lambda d=d: nc.vector.tensor_tensor(out=QT[0:64, 1 + d, :], in0=PSb(2)[0:64, 0:512], in1=QD[0:64, d, :, :].rearrange("p h i -> p (h i)"), op=ALU.mult),
                         reads=[RPS[2], Rtab], writes=[Rqt])
                S.op("dve", lambda: nc.vector.tensor_copy(out=KT[0:64, :], in_=PSb(2)[0:64, 512:1024]), reads=[RPS[2]], writes=[Rkt])
                if cfg.get("ret_stop") == "p2a":
                    continue

                def emit_a():
                    inst = None
                    for h in range(4):
                        inst = nc.tensor.matmul(PS[:, 3, h * 128:(h + 1) * 128], lhsT=KT[0:64, h * 128:(h + 1) * 128], rhs=QT[0:64, 0, h * 128:(h + 1) * 128], start=True, stop=True)
                    return inst
                S.op("pe", emit_a, reads=[Rkt, Rqt], writes=[RPS[3]])
                S.op("dve", lambda: nc.vector.tensor_tensor(out=AM[:, :], in0=PS[:, 3, :], in1=DM[:, :, :].rearrange("p h i -> p (h i)"), op=ALU.mult),
                     reads=[RPS[3], Rtab], writes=[Ram])
                if cfg.get("ret_stop") == "p2b":
                    continue

                def emit_o(c=c):
                    inst = None
                    for h in range(4):
                        oc = PS[:, 4, h * 64:(h + 1) * 64]
                        nc.tensor.matmul(oc, lhsT=AM[:, h * 128:(h + 1) * 128], rhs=VTc[:, h * 64:(h + 1) * 64], start=True, stop=False)
                        nc.tensor.matmul(oc, lhsT=QT[0:64, 1, h * 128:(h + 1) * 128], rhs=SFb[0:64, h * 64:(h + 1) * 64], start=False, stop=False)
                        inst = nc.tensor.matmul(oc, lhsT=QT[0:64, 2, h * 128:(h + 1) * 128], rhs=SBst[0:64, c, h * 64:(h + 1) * 64], start=False, stop=True)
                    return inst
                S.op("pe", emit_o, reads=[Ram, Rvt, Rqt, Rsfb, Rsbst], writes=[RPS[4]])
                umat(5)
                state_update(SF, Rsf, 0, 5)
                S.op("act", lambda: nc.scalar.copy(out=SFb[0:64, :], in_=SF[0:64, :]), reads=[Rsf], writes=[Rsfb])
                if cfg.get("ret_stop") == "p2c":
                    continue
                ov = PS[:, 4, 0:256].rearrange("p (h e) -> p h e", h=4)
                S.op("dve", lambda: nc.vector.tensor_reduce(out=MS[:, 0:4], in_=ov, axis=AX.X, op=ALU.add), reads=[RPS[4]], writes=[Rms])
                S.op("dve", lambda: nc.vector.tensor_scalar(out=MS[:, 0:4], in0=MS[:, 0:4], scalar1=-1.0 / 64, scalar2=None, op0=ALU.mult), reads=[Rms], writes=[Rms])
                cv = CEN[:, :].rearrange("p (h e) -> p h e", h=4)
                S.op("dve", lambda: nc.vector.tensor_tensor(out=cv, in0=ov, in1=MS[:, 0:4].unsqueeze(2).broadcast_to([128, 4, 64]), op=ALU.add),
                     reads=[RPS[4], Rms], writes=[Rcen])
                S.op("dve", lambda: nc.vector.tensor_tensor(out=SQr[:, :], in0=CEN[:, :], in1=CEN[:, :], op=ALU.mult), reads=[Rcen], writes=[Rsq])
                S.op("dve", lambda: nc.vector.tensor_reduce(out=MS[:, 4:8], in_=SQr[:, :].rearrange("p (h e) -> p h e", h=4), axis=AX.X, op=ALU.add), reads=[Rsq], writes=[Rms])
                S.op("act", lambda: nc.scalar.activation(out=MS[:, 4:8], in_=MS[:, 4:8], func=AF.Sqrt, scale=1.0 / 64, bias=epsb[:, 0:1]), reads=[Rms, Reps], writes=[Rms])
                S.op("dve", lambda: nc.vector.reciprocal(out=MS[:, 4:8], in_=MS[:, 4:8]), reads=[Rms], writes=[Rms])
                S.op("dve", lambda: nc.vector.tensor_tensor(out=cv, in0=cv, in1=MS[:, 4:8].unsqueeze(2).broadcast_to([128, 4, 64]), op=ALU.mult), reads=[Rcen, Rms], writes=[Rcen])
                S.op("dve", lambda: nc.vector.tensor_tensor(out=NRo[:, :], in0=CEN[:, :], in1=SRG[:, :], op=ALU.mult), reads=[Rcen, Rsrg], writes=[Rnro])

                if cfg.get("ret_stop") == "p2d":
                    continue

                def emit_t2():
                    inst = None
                    for cc in range(2):
                        inst = nc.tensor.transpose(out=PSb(6)[:, cc * 128:(cc + 1) * 128], in_=NRo[:, cc * 128:(cc + 1) * 128], identity=identb[:])
                    return inst
                S.op("pe", emit_t2, reads=[Rnro, Rid], writes=[RPS[6]])
                for cc in range(2):
                    S.op("dve", lambda cc=cc, c=c: nc.vector.tensor_scalar(out=BR[:, 2 + cc, c * 128:(c + 1) * 128], in0=PSb(6)[:, cc * 128:(cc + 1) * 128],
                                                                          scalar1=PV[:, 8 + cc:9 + cc], scalar2=None, op0=ALU.mult), reads=[RPS[6], Rpv], writes=[Rbr[2 + cc]])
            if not sample:
                pi = 0 if kind == "pA" else 1
                OUT_EVS.append(S.dma("sp", o_ret[pi, l, 0].rearrange("h d e -> d h e"), SF[0:64, :].rearrange("d (h e) -> d h e", h=4), reads=[Rsf], key=Reg("o_ret_d")))
                OUT_EVS.append(S.dma("sp", o_ret[pi, l, 1].rearrange("h d e -> d h e"), SB[0:64, :].rearrange("d (h e) -> d h e", h=4), reads=[Rsb], key=Reg("o_ret_d")))

        def branch_s5(l, t0, T, NB, BW, kind):
            sample = kind == "sample"
            o = BS0
            UT, o = view(o, [128, 2, 2048], BF16)
            YS, o = view(o, [128, 2, 2048], BF16)
            YFp, o = view(o, [128, 2048], BF16)
            oZ = o
            TRI, o = view(o, [128, 2, 512])
            TRIb, o = view(o, [128, 2, 512], BF16)
            BZ, o = view(o, [128, 2, 512], BF16)
            oTT = o
            TT, o = view(o, [128, 2, 1024], BF16)
            TTf, _ = view(oTT, [128, 2, 512])
            oS = o
            ow = WIN0
            SBb, ow = view(ow, [128, 2, 512], BF16)
            OTs, ow = view(ow, [128, 512], BF16)
            BW_, ow = view(ow, [128, 2, 2, 128], BF16)
            BBR, ow = view(ow, [128, 16, 16])
            BBI, ow = view(ow, [128, 16, 16])
            CW, ow = view(ow, [128, 8, 2, 32], BF16)
            WGL, ow = view(ow, [128, 2, 256], BF16)
            YV, _ = view(WIN0, [128, 512])
            Rut, Rys, Rsfs, Rtri, Rbz, Rtt, Rsbb, Rots, Rrb, Rbw = (Reg("s_" + x) for x in ("UT", "YS", "YFp", "TRI", "BZ", "TT", "SBb", "OTs", "RB", "BW"))
            def sm_(shape, dt=F32):
                nonlocal o
                v, o = view(o, shape, dt)
                return v
            o1 = [oTT]

            def ot_(shape, dt=F32):
                v, o1[0] = view(o1[0], shape, dt)
                return v
            LRE, LIM, LDT, AR, AI, FR, FI = (ot_([128, 16]) for _ in range(7))
            BRE, BIM = ot_([128, 16, 16]), ot_([128, 16, 16])
            CNAT = ot_([128, 2, 64])
            MAG, UR, UI, W1, W2_, W3 = (sm_([128, 16]) for _ in range(6))
            UBR, UBI = sm_([128, 16]), sm_([128, 16])
            TA_, TBs = sm_([128, 2, 16, 16]), sm_([128, 2, 16, 32])
            PW = sm_([128, 2, 16])
            S0t = sm_([128, 16, 2])
            INI = sm_([128, 2, 2])
            FIN = sm_([128, 16, 2])
            WP = sm_([128, 128])
            Rsu = Reg("s_setup")
            Rini, Rfin, Rwp, Rcn = Reg("s_INI"), Reg("s_FIN"), Reg("s_WP"), Reg("s_CN")
            Rbu = Reg("s_BU")
            V = nc.vector
            dbgon = cfg.get("s5dbg") == kind
            if dbgon:
                dbg2 = nc.dram_tensor("dbg2", [128, 4096], F32, kind="ExternalOutput").ap()

            def dbg(ap, c0, n, regs):
                if dbgon:
                    S.dma("sp", dbg2[:, c0:c0 + n], ap, reads=regs, key=Reg("dbg2"))

            def dv(fn, reads, writes):
                S.op("dve", fn, reads=reads, writes=writes)

            def tt(out, a, b, op, reads=(Rsu,), writes=(Rsu,)):
                dv(lambda: V.tensor_tensor(out=out, in0=a, in1=b, op=op), list(reads), list(writes))

            def cmul(orr, oi, ar, ai, br, bi, t1, t2, reads=(Rsu,), writes=(Rsu,)):
                tt(t1, ar, br, ALU.mult, reads, writes)
                tt(t2, ai, bi, ALU.mult, reads, writes)
                tt(t2, t1, t2, ALU.subtract, reads, writes)
                tt(t1, ar, bi, ALU.mult, reads, writes)
                tt(oi, ai, br, ALU.mult, reads, writes)
                tt(oi, t1, oi, ALU.add, reads, writes)
                tt(orr, t2, t2, ALU.max, reads, writes)

            for cc in range(2):
                proj_fm(l, cc * 128, 128, NB, BW, lambda b, bank, cc=cc: S.op(
                    "act", lambda: nc.scalar.copy(out=UT[:, cc, b * BW:(b + 1) * BW], in_=PS[:, bank, 0:BW]), reads=[RPS[bank]], writes=[Rut]))
            S.barrier()
            for d in range(2):
                for dst, src in ((LRE, s5_lam_re), (LIM, s5_lam_im)):
                    S.dma("sp", dst[:, d::2], src[l, d].rearrange("(m g) p -> (g p) m", g=2), writes=[Rsu], slow=True)
                for g2 in range(2):
                    S.dma("sp", LDT[g2 * 64:(g2 + 1) * 64, d::2], s5_log_dt[l, d:d + 1, g2::2].broadcast_to([64, 8]), writes=[Rsu], slow=True)
                for dst, src in ((BRE, s5_b_re), (BIM, s5_b_im)):
                    S.dma("sp", dst[:, d::2, :], src[l, d].rearrange("(m g) p h -> (g p) m h", g=2), writes=[Rsu])
                if sample:
                    S.dma("sp", S0t[:, d::2, :], st_s5[l, d].rearrange("(m g) p r -> (g p) m r", g=2), writes=[Rsu], slow=True)
            S.dma("pool", WGL[:, :, :], s5_w_glu[l].rearrange("(kc p) n -> p kc n", p=128), writes=[Rsu])
            S.op("act", lambda: nc.scalar.activation(out=LDT[:, :], in_=LDT[:, :], func=AF.Exp), reads=[Rsu], writes=[Rsu])
            tt(W1[:, :], LRE[:, :], LDT[:, :], ALU.mult)
            S.op("act", lambda: nc.scalar.activation(out=MAG[:, :], in_=W1[:, :], func=AF.Exp), reads=[Rsu], writes=[Rsu])
            tt(W1[:, :], LIM[:, :], LDT[:, :], ALU.mult)
            S.op("act", lambda: nc.scalar.activation(out=UI[:, :], in_=W1[:, :], func=AF.Sin, scale=1.0 / 64), reads=[Rsu], writes=[Rsu])
            S.op("act", lambda: nc.scalar.activation(out=UR[:, :], in_=W1[:, :], func=AF.Sin, scale=1.0 / 64, bias=halfpi[:, 0:1]), reads=[Rsu, Reps], writes=[Rsu])
            for _ in range(6):
                tt(W1[:, :], UR[:, :], UR[:, :], ALU.mult)
                tt(W2_[:, :], UI[:, :], UI[:, :], ALU.mult)
                tt(W3[:, :], UR[:, :], UI[:, :], ALU.mult)
                tt(UR[:, :], W1[:, :], W2_[:, :], ALU.subtract)
                tt(UI[:, :], W3[:, :], W3[:, :], ALU.add)
            tt(AR[:, :], MAG[:, :], UR[:, :], ALU.mult)
            tt(AI[:, :], MAG[:, :], UI[:, :], ALU.mult)
            tt(W1[:, :], LRE[:, :], LRE[:, :], ALU.mult)
            tt(W2_[:, :], LIM[:, :], LIM[:, :], ALU.mult)
            tt(W1[:, :], W1[:, :], W2_[:, :], ALU.add)
            dv(lambda: V.reciprocal(out=W1[:, :], in_=W1[:, :]), [Rsu], [Rsu])
            dv(lambda: V.tensor_scalar(out=W2_[:, :], in0=AR[:, :], scalar1=-1.0, scalar2=None, op0=ALU.add), [Rsu], [Rsu])
            tt(FR[:, :], W2_[:, :], LRE[:, :], ALU.mult)
            tt(W3[:, :], AI[:, :], LIM[:, :], ALU.mult)
            tt(FR[:, :], FR[:, :], W3[:, :], ALU.add)
            tt(FR[:, :], FR[:, :], W1[:, :], ALU.mult)
            tt(FI[:, :], AI[:, :], LRE[:, :], ALU.mult)
            tt(W3[:, :], W2_[:, :], LIM[:, :], ALU.mult)
            tt(FI[:, :], FI[:, :], W3[:, :], ALU.subtract)
            tt(FI[:, :], FI[:, :], W1[:, :], ALU.mult)
            dbg(MAG[:, :], 0, 16, [Rsu]); dbg(UR[:, :], 16, 16, [Rsu]); dbg(UI[:, :], 32, 16, [Rsu]); dbg(FR[:, :], 48, 16, [Rsu]); dbg(FI[:, :], 64, 16, [Rsu])
            frb = FR[:, :].unsqueeze(2).broadcast_to([128, 16, 16])
            fib = FI[:, :].unsqueeze(2).broadcast_to([128, 16, 16])
            tt(BBR[:, :, :], BRE[:, :, :], frb, ALU.mult)
            tt(BBI[:, :, :], BIM[:, :, :], fib, ALU.mult)
            tt(BBR[:, :, :], BBR[:, :, :], BBI[:, :, :], ALU.subtract)
            tt(BBI[:, :, :], BRE[:, :, :], fib, ALU.mult)
            tt(BRE[:, :, :], BIM[:, :, :], frb, ALU.mult)
            tt(BBI[:, :, :], BBI[:, :, :], BRE[:, :, :], ALU.add)
            dv(lambda: V.memset(CW[:, :, :, :], 0.0), [], [Rsu])
            for ri, src in ((0, s5_c_re), (1, s5_c_im)):
                S.dma("sp", CNAT[:, :, :], src[l].rearrange("(c g) h p -> (g h) c p", c=2), writes=[Rcn])
                CNB = TTf[:, 1, 0:64].bitcast(BF16)
                dv(lambda: V.tensor_copy(out=CNB.rearrange("p (c k) -> p c k", c=2), in_=CNAT[:, :, :]), [Rcn, Rtt], [Rtt])
                for c in range(2):
                    for half in range(2):
                        S.op("pe", lambda c=c, half=half: nc.tensor.matmul(PS[half * 64:(half + 1) * 64, 6, c * 128:(c + 1) * 128], lhsT=CNB[:, c * 64:(c + 1) * 64], rhs=identb[:, :],
                                                                           start=True, stop=True), reads=[Rtt, Rid], writes=[RPS[6]])
                ctv = PS[:, 6, 0:256].rearrange("q (m g h) -> q m g h", m=8, g=2)
                sc = 1.0 if ri == 0 else -1.0
                dv(lambda ri=ri, sc=sc: V.tensor_scalar(out=CW[0:64, :, ri, 0:16], in0=ctv[0:64, :, 0, :], scalar1=sc, scalar2=None, op0=ALU.mult), [RPS[6]], [Rsu])
                dv(lambda ri=ri, sc=sc: V.tensor_scalar(out=CW[64:128, :, ri, 16:32], in0=ctv[64:128, :, 1, :], scalar1=sc, scalar2=None, op0=ALU.mult), [RPS[6]], [Rsu])
            S.barrier()
            def build_pows(TAB, nent, base_r, base_i):
                dv(lambda: V.memset(TAB[:, 0, :, 0:1], 1.0), [], [Rsu])
                dv(lambda: V.memset(TAB[:, 1, :, 0:1], 0.0), [], [Rsu])
                tt(PW[:, 0, :], base_r, base_r, ALU.max)
                tt(PW[:, 1, :], base_i, base_i, ALU.max)
                nn = 1
                while nn < nent:
                    pr = PW[:, 0, :].unsqueeze(2).broadcast_to([128, 16, nn])
                    pi_ = PW[:, 1, :].unsqueeze(2).broadcast_to([128, 16, nn])
                    t1 = TTf[:, 0, 0:16 * nn].rearrange("p (k j) -> p k j", k=16)
                    t2 = TTf[:, 1, 0:16 * nn].rearrange("p (k j) -> p k j", k=16)
                    rr, ri = Rsu, Rtt
                    tt(t1, TAB[:, 0, :, 0:nn], pr, ALU.mult, (rr, ri), (ri,))
                    tt(t2, TAB[:, 1, :, 0:nn], pi_, ALU.mult, (rr, ri), (ri,))
                    tt(TAB[:, 0, :, nn:2 * nn], t1, t2, ALU.subtract, (rr, ri), (rr,))
                    tt(t1, TAB[:, 0, :, 0:nn], pi_, ALU.mult, (rr, ri), (ri,))
                    tt(t2, TAB[:, 1, :, 0:nn], pr, ALU.mult, (rr, ri), (ri,))
                    tt(TAB[:, 1, :, nn:2 * nn], t1, t2, ALU.add, (rr, ri), (rr,))
                    tt(W1[:, :], PW[:, 0, :], PW[:, 0, :], ALU.mult)
                    tt(W2_[:, :], PW[:, 1, :], PW[:, 1, :], ALU.mult)
                    tt(W3[:, :], PW[:, 0, :], PW[:, 1, :], ALU.mult)
                    tt(PW[:, 0, :], W1[:, :], W2_[:, :], ALU.subtract)
                    tt(PW[:, 1, :], W3[:, :], W3[:, :], ALU.add)
                    nn *= 2
            build_pows(TBs, 32, UR[:, :], UI[:, :])
            tt(W1[:, :], PW[:, 0, :], PW[:, 0, :], ALU.max)
            tt(W2_[:, :], PW[:, 1, :], PW[:, 1, :], ALU.max)
            tt(UBR[:, :], PW[:, 0, :], PW[:, 0, :], ALU.max)
            tt(UBI[:, :], PW[:, 1, :], PW[:, 1, :], ALU.max)
            build_pows(TA_, 16, UBR[:, :], UBI[:, :])
            if BW == 512:
                tt(UBR[:, :], PW[:, 0, :], PW[:, 0, :], ALU.max)
                tt(UBI[:, :], PW[:, 1, :], PW[:, 1, :], ALU.max)
            else:
                tt(UBR[:, :], TA_[:, 0, :, 8], TA_[:, 0, :, 8], ALU.max)
                tt(UBI[:, :], TA_[:, 1, :, 8], TA_[:, 1, :, 8], ALU.max)
            S.barrier()
            mcb = [0]
            for m in range(8):
                cc, m4 = m // 4, m % 4
                for d in range(2):
                    k = m * 2 + d
                    for ri, BB in ((0, BBR), (1, BBI)):
                        dv(lambda: V.memset(WP[:, :], 0.0), [Rwp], [Rwp])
                        dv(lambda BB=BB, k=k: V.tensor_copy(out=WP[0:64, m4 * 32:m4 * 32 + 16], in_=BB[0:64, k, :]), [Rsu, Rwp], [Rwp])
                        dv(lambda BB=BB, k=k: V.tensor_copy(out=WP[64:128, m4 * 32 + 16:m4 * 32 + 32], in_=BB[64:128, k, :]), [Rsu, Rwp], [Rwp])
                        S.op("pe", lambda: nc.tensor.transpose(out=PS[:, 7, 0:128], in_=WP[:, :], identity=ident[:]), reads=[Rwp, Rid], writes=[RPS[7]])
                        S.op("act", lambda d=d, ri=ri: nc.scalar.copy(out=BW_[:, d, ri, :], in_=PS[:, 7, 0:128]), reads=[RPS[7]], writes=[Rbw])
                for d in range(2):
                    k = m * 2 + d
                    rev = d == 1
                    ar = TA_[:, 0, k, :].unsqueeze(2).broadcast_to([128, 16, 32])
                    ai = TA_[:, 1, k, :].unsqueeze(2).broadcast_to([128, 16, 32])
                    br = TBs[:, 0, k, :].unsqueeze(1).broadcast_to([128, 16, 32])
                    bi = TBs[:, 1, k, :].unsqueeze(1).broadcast_to([128, 16, 32])
                    trv = TRI[:, 0, :].rearrange("p (q j) -> p q j", q=16)
                    tiv = TRI[:, 1, :].rearrange("p (q j) -> p q j", q=16)
                    t1 = TTf[:, 0, :].rearrange("p (q j) -> p q j", q=16)
                    t2 = TTf[:, 1, :].rearrange("p (q j) -> p q j", q=16)
                    rw = (Rsu, Rtt, Rtri)
                    tt(t1, ar, br, ALU.mult, rw, (Rtt,))
                    tt(t2, ai, bi, ALU.mult, rw, (Rtt,))
                    tt(trv, t1, t2, ALU.subtract, rw, (Rtri,))
                    tt(t1, ar, bi, ALU.mult, rw, (Rtt,))
                    tt(t2, ai, br, ALU.mult, rw, (Rtt,))
                    tt(tiv, t1, t2, ALU.add, rw, (Rtri,))
                    dv(lambda: V.tensor_copy(out=TRIb[:, :, :], in_=TRI[:, :, :]), [Rtri], [Rtri])
                    if k == 0:
                        dbg(TRI[:, 0, :], 128, 512, [Rtri]); dbg(TRI[:, 1, :], 640, 512, [Rtri])
                    ib = 0
                    if sample:
                        cmul(INI[:, 0, ib:ib + 1], INI[:, 1, ib:ib + 1], UR[:, k:k + 1], UI[:, k:k + 1], S0t[:, k, 0:1], S0t[:, k, 1:2], W1[:, 0:1], W2_[:, 0:1], (Rsu, Rini), (Rsu, Rini))
                    else:
                        dv(lambda: V.memset(INI[:, :, 0:1], 0.0), [Rini], [Rini])
                    blocks = list(range(NB - 1, -1, -1)) if rev else list(range(NB))
                    for bi_, b in enumerate(blocks):
                        cols = slice(b * BW, (b + 1) * BW)
                        bk = 2 * (mcb[0] % 2)
                        mcb[0] += 1
                        for ri in range(2):
                            S.op("pe", lambda ri=ri: nc.tensor.matmul(PS[:, bk + ri, 0:BW], lhsT=BW_[:, d, ri, :], rhs=UT[:, cc, cols], start=True, stop=True),
                                 reads=[Rbw, Rut], writes=[RPS[bk + ri]])
                        if rev:
                            trr, tri = TRI[:, 0, BW - 1::-1] if BW == 512 else TRI[:, 0, BW - 1::-1], TRI[:, 1, BW - 1::-1]
                            trr = TRI[:, 0, 0:BW][:, ::-1]
                            tri = TRI[:, 1, 0:BW][:, ::-1]
                            trrb = TRIb[:, 0, 0:BW][:, ::-1]
                            trib = TRIb[:, 1, 0:BW][:, ::-1]
                        else:
                            trr, tri = TRI[:, 0, 0:BW], TRI[:, 1, 0:BW]
                            trrb, trib = TRIb[:, 0, 0:BW], TRIb[:, 1, 0:BW]
                        for ri in range(2):
                            S.op("act", lambda ri=ri: nc.scalar.copy(out=SBb[:, ri, 0:BW], in_=PS[:, bk + ri, 0:BW]), reads=[RPS[bk + ri]], writes=[Rsbb])
                        pre, pim = SBb[:, 0, 0:BW], SBb[:, 1, 0:BW]
                        rr = (Rtri, Rsbb, Rtt, Rbz)
                        tt(TT[:, 0, 0:BW], pre, trrb, ALU.mult, rr, (Rtt,))
                        tt(TT[:, 1, 0:BW], pim, trib, ALU.mult, rr, (Rtt,))
                        tt(BZ[:, 0, 0:BW], TT[:, 0, 0:BW], TT[:, 1, 0:BW], ALU.add, rr, (Rbz,))
                        tt(TT[:, 0, 0:BW], pim, trrb, ALU.mult, rr, (Rtt,))
                        tt(TT[:, 1, 0:BW], pre, trib, ALU.mult, rr, (Rtt,))
                        tt(BZ[:, 1, 0:BW], TT[:, 0, 0:BW], TT[:, 1, 0:BW], ALU.subtract, rr, (Rbz,))
                        if k == 0 and bi_ == 0:
                            dbg(BZ[:, 0, 0:BW], 1152, BW, [Rbz]); dbg(BZ[:, 1, 0:BW], 1664, BW, [Rbz])
                        for ri in range(2):
                            zo = TT[:, ri, 0:BW]
                            zin = BZ[:, ri, 0:BW]
                            if rev:
                                zo, zin = zo[:, ::-1], zin[:, ::-1]
                            dv(lambda zo=zo, zin=zin, ri=ri: V.tensor_tensor_scan(out=zo, data0=MAG[:, k:k + 1].broadcast_to([128, BW]), data1=zin, initial=INI[:, ri, ib:ib + 1], op0=ALU.mult, op1=ALU.add),
                               [Rsu, Rbz, Rini, Rtt], [Rtt])
                        if k == 0 and bi_ == 0:
                            dbg(TT[:, 0, 0:BW], 2176, BW, [Rtt]); dbg(TT[:, 1, 0:BW], 2688, BW, [Rtt])
                        zl = 0 if rev else BW - 1
                        zr_, zi_ = TT[:, 0, zl:zl + 1], TT[:, 1, zl:zl + 1]
                        last_blk = bi_ == NB - 1
                        if not last_blk:
                            ib2 = 1 - ib
                            rc, wc = [Rsu, Rini, Rtt], [Rsu, Rini]
                            dv(lambda: V.tensor_scalar(out=W1[:, 0:1], in0=zi_, scalar1=UBI[:, k:k + 1], scalar2=None, op0=ALU.mult), rc, wc)
                            dv(lambda: V.scalar_tensor_tensor(out=INI[:, 0, ib2:ib2 + 1], in0=zr_, scalar=UBR[:, k:k + 1], in1=W1[:, 0:1], op0=ALU.mult, op1=ALU.subtract), rc, wc)
                            dv(lambda: V.tensor_scalar(out=W2_[:, 0:1], in0=zr_, scalar1=UBI[:, k:k + 1], scalar2=None, op0=ALU.mult), rc, wc)
                            dv(lambda: V.scalar_tensor_tensor(out=INI[:, 1, ib2:ib2 + 1], in0=zi_, scalar=UBR[:, k:k + 1], in1=W2_[:, 0:1], op0=ALU.mult, op1=ALU.add), rc, wc)
                            ib = ib2
                        elif not sample:
                            cmul(FIN[:, k, 0:1], FIN[:, k, 1:2], TRI[:, 0, BW - 1:BW], TRI[:, 1, BW - 1:BW], zr_, zi_, W1[:, 0:1], W2_[:, 0:1], (Rsu, Rtri, Rtt, Rfin), (Rsu, Rfin))
                        dstS = SBb[:, :, 0:BW]
                        Rdst = Rsbb
                        rr2 = (Rtri, Rtt, Rbz)
                        tt(BZ[:, 0, 0:BW], TT[:, 0, 0:BW], trrb, ALU.mult, rr2, (Rbz,))
                        tt(BZ[:, 1, 0:BW], TT[:, 1, 0:BW], trib, ALU.mult, rr2, (Rbz,))
                        tt(dstS[:, 0, :], BZ[:, 0, 0:BW], BZ[:, 1, 0:BW], ALU.subtract, (Rbz, Rdst), (Rdst,))
                        tt(BZ[:, 0, 0:BW], TT[:, 1, 0:BW], trrb, ALU.mult, rr2, (Rbz,))
                        tt(BZ[:, 1, 0:BW], TT[:, 0, 0:BW], trib, ALU.mult, rr2, (Rbz,))
                        tt(dstS[:, 1, :], BZ[:, 0, 0:BW], BZ[:, 1, 0:BW], ALU.add, (Rbz, Rdst), (Rdst,))
                        def emit_y():
                            nc.tensor.matmul(PS[0:32, 4, 0:BW], lhsT=CW[:, m, 0, :], rhs=SBb[:, 0, 0:BW], start=True, stop=False)
                            return nc.tensor.matmul(PS[0:32, 4, 0:BW], lhsT=CW[:, m, 1, :], rhs=SBb[:, 1, 0:BW], start=False, stop=True)
                        S.op("pe", emit_y, reads=[Rsu, Rsbb], writes=[RPS[4]])
                        if not rev:
                            S.op("act", lambda: nc.scalar.copy(out=YFp[0:32, cols], in_=PS[0:32, 4, 0:BW]), reads=[RPS[4]], writes=[Rsfs])
                        else:
                            tt(OTs[0:32, 0:BW], PS[0:32, 4, 0:BW], YFp[0:32, cols], ALU.add, (RPS[4], Rsfs, Rots), (Rots,))
                            S.dma("sp", YS[m4 * 32:(m4 + 1) * 32, cc, cols], OTs[0:32, 0:BW], reads=[Rots], writes=[Rys], key=Reg("s_OTd"))
            if not sample:
                pi = 0 if kind == "pA" else 1
                for d in range(2):
                    OUT_EVS.append(S.dma("sp", o_s5[pi, l, d].rearrange("(m g) p r -> (g p) m r", g=2), FIN[:, d::2, :], reads=[Rfin], key=Reg("o_s5_d"), slow=True))
            S.barrier()
            Z, _ = view(oZ, [128, 2, 2048], BF16)
            Rz_ = Reg("s_Z")
            for b in range(NB):
                cols = slice(b * BW, (b + 1) * BW)
                for cc in range(2):
                    yv_ = YV[:, 0:BW]
                    dv(lambda: V.scalar_tensor_tensor(out=yv_, in0=UT[:, cc, cols], scalar=PV[:, 10 + cc:11 + cc], in1=YS[:, cc, cols], op0=ALU.mult, op1=ALU.add),
                       [Rut, Rpv, Rys, Rbz], [Rbz])
                    tt(TTf[:, 0, 0:BW], yv_, yv_, ALU.mult, (Rbz, Rtt), (Rtt,))
                    dv(lambda: V.tensor_scalar(out=TTf[:, 0, 0:BW], in0=TTf[:, 0, 0:BW], scalar1=0.044715, scalar2=1.0, op0=ALU.mult, op1=ALU.add), [Rtt], [Rtt])
                    tt(TTf[:, 0, 0:BW], TTf[:, 0, 0:BW], yv_, ALU.mult, (Rbz, Rtt), (Rtt,))
                    S.op("act", lambda: nc.scalar.activation(out=TTf[:, 1, 0:BW], in_=TTf[:, 0, 0:BW], func=AF.Sigmoid, scale=1.5957691216057308), reads=[Rtt], writes=[Rtt])
                    tt(Z[:, cc, cols], TTf[:, 1, 0:BW], yv_, ALU.mult, (Rbz, Rtt, Rz_), (Rz_,))
                for co in range(2):
                    def emit_g(co=co):
                        nc.tensor.matmul(PS[:, 5, 0:BW], lhsT=WGL[:, 0, co * 128:(co + 1) * 128], rhs=Z[:, 0, cols], start=True, stop=False)
                        return nc.tensor.matmul(PS[:, 5, 0:BW], lhsT=WGL[:, 1, co * 128:(co + 1) * 128], rhs=Z[:, 1, cols], start=False, stop=True)
                    S.op("pe", emit_g, reads=[Rsu, Rz_], writes=[RPS[5]])
                    S.op("act", lambda: nc.scalar.activation(out=OTs[:, 0:BW], in_=PS[:, 5, 0:BW], func=AF.Sigmoid), reads=[RPS[5]], writes=[Rots])
                    tt(BR[:, co, cols], Z[:, co, cols], OTs[:, 0:BW], ALU.mult, (Rz_, Rots), (Rbr[co],))

        DBG = {}

        def mixer(l):
            make_gate_bcast(1)
            mixer_params(l)
            for (t0, ntile, ci, kind) in SEQS:
                T = ntile * 128
                BW = min(512, T)
                NB = T // BW
                S.barrier()
                cur["XN"], _ = view(BS0, [128, 2, D])
                for tt in range(ntile):
                    prenorm_tile(t0 + tt, ci, 1, HT[:, :, tt * 128:(tt + 1) * 128], Rht, (4 + 2 * (tt % 2), 5 + 2 * (tt % 2)))
                S.barrier()
                if kind != "sample":
                    mla_cache_out(l, t0, ntile, 0 if kind == "pA" else 1)
                    S.barrier()
                zero = []
                for nm, chs in (("s5", (0, 1)), ("ret", (2, 3)), ("conv", (4, 5)), ("mla", (6, 7))):
                    if not cfg.get(nm, True):
                        zero += list(chs)
                for i in zero:
                    S.op("dve", lambda i=i: nc.vector.memset(BR[:, i, 0:T], 0.0), writes=[Rbr[i]])
                if cfg.get("conv", True):
                    branch_conv(l, T, NB, BW)
                    S.barrier()
                if cfg.get("mla", True):
                    branch_mla(l, t0, T, NB, BW, kind)
                    S.barrier()
                if cfg.get("ret", True):
                    branch_ret(l, t0, T, kind)
                    S.barrier()
                if cfg.get("s5", True):
                    branch_s5(l, t0, T, NB, BW, kind)
                    S.barrier()
                if tuple(cfg.get("dump_br", ())) == (l, kind):
                    dbg = nc.dram_tensor("dbg_br", [128, 8, 2048], BF16, kind="ExternalOutput").ap()
                    S.dma("sp", dbg[:, :, 0:T], BR[:, :, 0:T], reads=Rbr, key=Reg("dbg"))
                gate_stage(l, t0, ntile, ci, T, NB, BW)
                S.barrier()
            cur["XN"], cur["TMP"] = XN, TMP

        for l in range(LAYERS):
            S.barrier()
            compute_mod(l)
            S.barrier()
            if cfg.get("ffn1", True):
                ffn(l, 0, 0)
            S.barrier()
            if cfg.get("mixer", True):
                mixer(l)
            S.barrier()
            if cfg.get("ffn2", True):
                ffn(l, 1, 2)

        yv = y.rearrange("(t p) d -> p t d", p=128)
        Ryout = Reg("yout")
        evs = []
        for t in range(NT):
            evs.append(S.dma("sp", yv[:, t, :], X[:, t, :], reads=[RX[t]], key=Ryout))
        S._wait("sp", set([evs[-1]] + OUT_EVS))
        S.barrier()
    return nc


def _axial(T, dim):
    rows = T // 64
    row = np.repeat(np.arange(rows, dtype=np.float32), 64)
    col = np.tile(np.arange(64, dtype=np.float32), rows)
    quarter = dim // 4
    inv = (np.float32(10000.0) ** (-np.arange(quarter, dtype=np.float32) / np.float32(quarter))).astype(np.float32)
    ang = np.concatenate([row[:, None] * inv, col[:, None] * inv], axis=-1).astype(np.float32)
    return np.cos(ang).astype(np.float32), np.sin(ang).astype(np.float32)


def _rope_tables():
    c, s = _axial(2048, 32)
    mla = np.stack([np.concatenate([c.T, c.T], 0), np.concatenate([s.T, s.T], 0)], 0)
    c2, s2 = _axial(2048, 64)
    ret = np.stack([c2, s2], 0)
    return np.ascontiguousarray(mla, dtype=np.float32), np.ascontiguousarray(ret, dtype=np.float32)


def _prep_inputs(inputs):
    f = lambda a: np.ascontiguousarray(np.asarray(a, dtype=np.float32))
    shared = {k: f(inputs[k]) for k in (
        "w_mod", "b_mod", "norm_pre", "norm_post", "ffn_w1", "ffn_w3", "ffn_w2", "w_in", "s5_lam_re", "s5_lam_im", "s5_log_dt",
        "s5_b_re", "s5_b_im", "s5_c_re", "s5_c_im", "s5_d", "s5_w_glu", "ret_decay", "ret_gn", "conv_w", "conv_b",
        "mla_q_norm", "mla_w_uq", "mla_kv_norm", "mla_w_ukv", "w_branch", "w_gate", "b_gate", "w_o")}
    shared["c_ident"] = np.eye(128, dtype=np.float32)
    shared["c_rope_mla"], shared["c_rope_ret"] = _rope_tables()
    jj = np.arange(128, dtype=np.float32)[:, None]
    ii = np.arange(128, dtype=np.float32)[None, :]
    diff = ii - jj
    shared["c_ret"] = np.ascontiguousarray(np.stack([
        np.maximum(diff, 0), np.maximum(-diff, 0), 0.125 * (diff >= 0), 0.125 * (diff < 0),
        np.broadcast_to(ii + 1.0, (128, 128)), np.broadcast_to(128.0 - ii, (128, 128))], 0), dtype=np.float32)
    shared["c_pidx"] = np.ascontiguousarray(np.concatenate([127.0 - jj, jj], 1), dtype=np.float32)
    xp = f(inputs["x_prompt"])
    xs = f(inputs["x_sample"])
    c = f(inputs["c"])
    cctx = f(inputs["c_ctx"])
    maps = []
    for i in range(NCORES):
        m = dict(shared)
        m["xin"] = np.ascontiguousarray(np.concatenate([xs[i], xp[2 * i], xp[2 * i + 1]], axis=0))
        m["cond2"] = np.ascontiguousarray(np.stack([c[i], cctx], axis=0))
        m["st_s5"] = f(inputs["state_s5"][i])
        m["st_ret"] = f(inputs["state_ret"][i])
        m["ctx_mla"] = f(inputs["cache_mla"][i])
        maps.append(m)
    return maps


def _gather(res):
    ys = np.stack([r["y"][:2048] for r in res], axis=0)
    yp = np.stack([r["y"][2048 + 256 * j:2048 + 256 * (j + 1)] for r in res for j in range(2)], axis=0)
    s5 = np.concatenate([r["o_s5"] for r in res], axis=0)
    ret = np.concatenate([r["o_ret"] for r in res], axis=0)
    mla = np.concatenate([r["o_mla"] for r in res], axis=0)
    return (yp.astype(np.float32), ys.astype(np.float32), s5.astype(np.float32), ret.astype(np.float32), mla.astype(np.float32))


CFG = {}


def kernel(**inputs):
    nc = build(CFG)
    maps = _prep_inputs(inputs)
    res = run_bass_kernel_spmd(nc, maps, core_ids=list(range(NCORES)))
    return _gather(res.results)
```

```python
import numpy as np
import concourse.bass as bass
import concourse.mybir as mybir
from concourse.bass_utils import run_bass_kernel_spmd
from contextlib import ExitStack

F32 = mybir.dt.float32
BF16 = mybir.dt.bfloat16
AF = mybir.ActivationFunctionType
ALU = mybir.AluOpType
AX = mybir.AxisListType

D = 1024
DFF = 2816
NFF = 22
TOK = 2560
NT = 20
EPS = 1e-6
NCORES = 8
INC = 2400

SAME_ENG_SYNC = True


_REGS = {}


def Reg(name):
    if name not in _REGS:
        _REGS[name] = _Reg(name)
    return _REGS[name]


class _Reg:
    __slots__ = ("name", "w", "r", "dsem", "dcnt")

    def __init__(self, name):
        self.name = name
        self.w = None
        self.r = []
        self.dsem = None
        self.dcnt = 0


class Sched:
    def __init__(self, nc, es):
        self.nc = nc
        self.es = es
        self.eng = {"pe": nc.tensor, "act": nc.scalar, "dve": nc.vector, "pool": nc.gpsimd, "sp": nc.sync}
        self.sem = {e: es.enter_context(nc.semaphore("s_" + e)) for e in self.eng}
        self.cnt = {e: 0 for e in self.eng}
        self.seen = {e: {} for e in self.eng}
        self.seen_d = {e: {} for e in self.eng}
        self.nsem = 0
        self.out_events = []
        self.pending_reads = {}

    def _wait(self, e, deps, raw=None):
        best = {}
        bestd = {}
        for d in deps:
            if d[0] == "e":
                _, e2, c = d
                if e2 == e and (e == "pe" or not SAME_ENG_SYNC or (raw is not None and d not in raw)):
                    continue
                if c > best.get(e2, 0):
                    best[e2] = c
            else:
                _, sem, tgt, key = d
                if tgt > bestd.get(key, (None, 0))[1]:
                    bestd[key] = (sem, tgt)
        E = self.eng[e]
        for e2, c in best.items():
            if self.seen[e].get(e2, 0) >= c:
                continue
            E.wait_ge(self.sem[e2], c)
            self.seen[e][e2] = c
        for key, (sem, tgt) in bestd.items():
            if self.seen_d[e].get(key, 0) >= tgt:
                continue
            E.wait_ge(sem, tgt)
            self.seen_d[e][key] = tgt

    def _deps(self, reads, writes):
        deps = set()
        for r in reads:
            if r.w is not None:
                deps.add(r.w)
        for w in writes:
            if w.w is not None:
                deps.add(w.w)
            deps.update(w.r)
        return deps

    def op(self, e, emit, reads=(), writes=()):
        raw = set(r.w for r in reads if r.w is not None)
        self._wait(e, self._deps(reads, writes), raw)
        inst = emit()
        self.cnt[e] += 1
        inst.then_inc(self.sem[e], 1)
        ev = ("e", e, self.cnt[e])
        for r in reads:
            r.r.append(ev)
        for w in writes:
            w.w = ev
            w.r = []
        return ev

    def dma(self, e, out, in_, reads=(), writes=(), key=None, slow=False):
        key = key or (list(writes) + list(reads))[0]
        if key.dsem is None:
            key.dsem = self.es.enter_context(self.nc.semaphore("d%d" % self.nsem))
            self.nsem += 1
        deps = set(d for d in self._deps(reads, writes) if not (d[0] == "d" and d[3] == key.name))
        self._wait(e, deps)
        inst = self.eng[e].dma_start(out=out, in_=in_, allow_slow_non_contiguous=True) if slow else self.eng[e].dma_start(out=out, in_=in_)
        key.dcnt += 16
        inst.then_inc(key.dsem, 16)
        ev = ("d", key.dsem, key.dcnt, key.name)
        if reads:
            self.pending_reads[key.name] = ev
        for r in reads:
            r.r.append(ev)
        for w in writes:
            w.w = ev
            w.r = []
        return ev

    def barrier(self):
        pend = set(self.pending_reads.values())
        for e in self.eng:
            deps = set(("e", e2, self.cnt[e2]) for e2 in self.eng if e2 != e and self.cnt[e2] > 0)
            self._wait(e, deps | pend)
        self.pending_reads = {}


def build(cfg):
    _REGS.clear()
    nc = bass.Bass("TRN2", target_bir_lowering=False)
    LAYERS = cfg.get("layers", 2)

    def din(name, shape):
        return nc.dram_tensor(name, list(shape), F32, kind="ExternalInput").ap()

    def dout(name, shape):
        return nc.dram_tensor(name, list(shape), F32, kind="ExternalOutput").ap()

    xin = din("xin", [TOK, D])
    cond2 = din("cond2", [2, D])
    st_s5 = din("st_s5", [2, 2, 16, 64, 2])
    st_ret = din("st_ret", [2, 2, 4, 64, 64])
    ctx_mla = din("ctx_mla", [2, 512, 160])
    w_mod = din("w_mod", [2, D, 9 * D])
    b_mod = din("b_mod", [2, 9 * D])
    norm_pre = din("norm_pre", [2, 3, D])
    norm_post = din("norm_post", [2, 3, D])
    ffn_w1 = din("ffn_w1", [2, 2, D, DFF])
    ffn_w3 = din("ffn_w3", [2, 2, D, DFF])
    ffn_w2 = din("ffn_w2", [2, 2, DFF, D])
    w_in = din("w_in", [2, D, INC])
    s5_lam_re = din("s5_lam_re", [2, 2, 16, 64])
    s5_lam_im = din("s5_lam_im", [2, 2, 16, 64])
    s5_log_dt = din("s5_log_dt", [2, 2, 16])
    s5_b_re = din("s5_b_re", [2, 2, 16, 64, 16])
    s5_b_im = din("s5_b_im", [2, 2, 16, 64, 16])
    s5_c_re = din("s5_c_re", [2, 16, 16, 64])
    s5_c_im = din("s5_c_im", [2, 16, 16, 64])
    s5_d = din("s5_d", [2, 256])
    s5_w_glu = din("s5_w_glu", [2, 256, 256])
    ret_decay = din("ret_decay", [2, 2, 4])
    ret_gn = din("ret_gn", [2, 256])
    conv_w = din("conv_w", [2, 3, 256])
    conv_b = din("conv_b", [2, 256])
    mla_q_norm = din("mla_q_norm", [2, 192])
    mla_w_uq = din("mla_w_uq", [2, 192, 384])
    mla_kv_norm = din("mla_kv_norm", [2, 128])
    mla_w_ukv = din("mla_w_ukv", [2, 128, 512])
    w_branch = din("w_branch", [2, 4, 256, D])
    w_gate = din("w_gate", [2, D, 4 * D])
    b_gate = din("b_gate", [2, 4 * D])
    w_o = din("w_o", [2, D, D])
    c_ident = din("c_ident", [128, 128])
    c_rope_mla = din("c_rope_mla", [2, 32, 2048])
    c_rope_ret = din("c_rope_ret", [2, 2048, 32])
    c_ret = din("c_ret", [6, 128, 128])
    c_pidx = din("c_pidx", [128, 2])

    y = dout("y", [TOK, D])
    o_s5 = dout("o_s5", [2, 2, 2, 16, 64, 2])
    o_ret = dout("o_ret", [2, 2, 2, 4, 64, 64])
    o_mla = dout("o_mla", [2, 2, 256, 160])

    es = ExitStack()
    with es:
        S = Sched(nc, es)

        def sb(name, shape, dt=F32):
            return es.enter_context(nc.sbuf_tensor(name, list(shape), dt))

        X = sb("X", [128, NT, D])
        RX = [Reg("X%d" % t) for t in range(NT)]
        PS = es.enter_context(nc.psum_tensor("PS", [128, 8, 512], F32))
        RPS = [Reg("PS%d" % b) for b in range(8)]
        ident = sb("ident", [128, 128])
        identb = sb("identb", [128, 128], BF16)
        Rid = Reg("ident")
        VEC = sb("VEC", [128, 2, 72])
        Rvec = Reg("VEC")
        NRM = sb("NRM", [128, 48])
        Rnrm = Reg("NRM")
        SV = sb("SV", [128, 2, 3, 8])
        Rsv = Reg("SV")
        GV = sb("GV", [128, 2, 3, 8])
        Rgv = Reg("GV")
        GB = sb("GB", [128, 2, D])
        Rgb = [Reg("GB0"), Reg("GB1")]
        DG = sb("DG", [128, 2, 128])
        Rdg = [Reg("DG0"), Reg("DG1")]
        small = sb("small", [128, 4, 4])
        Rsmall = [Reg("sm%d" % i) for i in range(4)]
        junk = sb("junk", [128, D], BF16)
        Rjunk = Reg("junk")
        ACOLS = 28672
        ARENA = sb("ARENA", [128, ACOLS])

        def view(off, shape, dt=F32):
            n = int(np.prod(shape[1:]))
            nbytes = n * (2 if dt == BF16 else 4)
            assert off % 4 == 0 and nbytes % 4 == 0 and off + nbytes <= ACOLS * 4, (off, shape)
            ap = ARENA[:, off // 4:(off + nbytes) // 4]
            if dt == BF16:
                ap = ap.bitcast(BF16)
            if len(shape) > 2:
                names = "abcdef"[:len(shape) - 1]
                pat = "p (" + " ".join(names) + ") -> p " + " ".join(names)
                ap = ap.rearrange(pat, **{names[i]: shape[i + 1] for i in range(len(names) - 1)})
            return ap, off + nbytes

        HTB, _o = view(0, [128, 2, 8, 512], BF16)
        GT, _o = view(_o, [128, NFF, 512], BF16)
        W13, _o = view(_o, [128, 3, 2, 8, 256], BF16)
        W2, _o = view(_o, [128, 3, 2, D], BF16)
        SIL, _o = view(_o, [128, 2, 512], BF16)
        XN, _o = view(_o, [128, 2, D])
        TMP, _o = view(_o, [128, 2, D])
        WM, _ = view(0, [128, 2, 8, 512], BF16)
        Rxn = [Reg("XN0"), Reg("XN1")]

        xv = xin.rearrange("(t p) d -> p t d", p=128)
        Rxall = Reg("xall")
        for t in range(NT):
            S.dma("sp", X[:, t, :], xv[:, t, :], writes=[RX[t]], key=Rxall)
        for t in range(NT):
            RX[t].w = ("d", Rxall.dsem, Rxall.dcnt, Rxall.name)
        S.dma("sp", ident[:], c_ident[:, :], writes=[Rid])
        S.op("dve", lambda: nc.vector.tensor_copy(out=identb[:], in_=ident[:]), reads=[Rid], writes=[Rid])

        rot = {"ps": 0, "sm": 0, "xn": 0, "dg": 0}
        OUT_EVS = []

        stage = sb("stage", [128, 128])
        Rstage = Reg("stage")

        def load_T(dst_ap, src_ap, rows, dst_reg, bank=7):
            S.dma("sp", stage[0:rows, :], src_ap, writes=[Rstage])
            S.op("pe", lambda: nc.tensor.transpose(out=PS[:, bank, 0:rows], in_=stage[0:rows, :], identity=ident[0:rows, 0:rows]),
                 reads=[Rstage, Rid], writes=[RPS[bank]])
            S.op("dve", lambda: nc.vector.tensor_copy(out=dst_ap, in_=PS[:, bank, 0:rows]), reads=[RPS[bank]], writes=[dst_reg])

        SCT = sb("SCT", [128, 8, 2], BF16)
        Rsct = Reg("SCT")
        sct32 = sb("sct32", [128, 16])
        load_T(sct32[:, :], cond2.rearrange("c (k p) -> (c k) p", p=128), 16, Rsct)
        S.op("act", lambda: nc.scalar.activation(out=SCT[:].rearrange("p k c -> p c k"), in_=sct32[:].rearrange("p (c k) -> p c k", c=2), func=AF.Silu),
             reads=[Rsct], writes=[Rsct])

        Rwm = [Reg("WM0"), Reg("WM1")]
        BM = sb("BM", [128, 72])
        Rbm = Reg("BM")

        def compute_mod(l):
            load_T(BM[:, :], b_mod[l].rearrange("(c p) -> c p", p=128), 72, Rbm)
            load_T(NRM[:, 0:24], norm_pre[l].rearrange("s (c p) -> (s c) p", p=128), 24, Rnrm)
            load_T(NRM[:, 24:48], norm_post[l].rearrange("s (c p) -> (s c) p", p=128), 24, Rnrm)
            wv = w_mod[l].rearrange("(kc p) n -> p kc n", p=128)
            for cb in range(18):
                sl = cb % 2
                S.dma("pool", WM[:, sl, :, :], wv[:, :, cb * 512:(cb + 1) * 512], writes=[Rwm[sl]])
                bank = 6

                def emit(cb=cb, sl=sl):
                    inst = None
                    for cc in range(4):
                        for kc in range(8):
                            inst = nc.tensor.matmul(PS[:, bank, cc * 2:cc * 2 + 2], lhsT=WM[:, sl, kc, cc * 128:(cc + 1) * 128],
                                                    rhs=SCT[:, kc, :], start=(kc == 0), stop=(kc == 7))
                    return inst
                S.op("pe", emit, reads=[Rwm[sl], Rsct], writes=[RPS[bank]])
                S.op("dve", lambda cb=cb: nc.vector.tensor_tensor(
                    out=VEC[:, :, cb * 4:(cb + 1) * 4].rearrange("p c j -> p j c"),
                    in0=PS[:, bank, 0:8].rearrange("p (j c) -> p j c", c=2),
                    in1=BM[:, cb * 4:(cb + 1) * 4].unsqueeze(2).broadcast_to([128, 4, 2]), op=ALU.add),
                    reads=[RPS[bank], Rbm], writes=[Rvec])
            for ci in range(2):
                for s in range(3):
                    S.op("dve", lambda ci=ci, s=s: nc.vector.scalar_tensor_tensor(
                        out=SV[:, ci, s, :], in0=VEC[:, ci, (3 * s + 1) * 8:(3 * s + 2) * 8], scalar=1.0,
                        in1=NRM[:, s * 8:(s + 1) * 8], op0=ALU.add, op1=ALU.mult), reads=[Rvec, Rnrm], writes=[Rsv])
                    fac = 1.0 if s == 1 else 0.5
                    S.op("dve", lambda ci=ci, s=s, fac=fac: nc.vector.scalar_tensor_tensor(
                        out=GV[:, ci, s, :], in0=VEC[:, ci, (3 * s + 2) * 8:(3 * s + 3) * 8], scalar=fac,
                        in1=NRM[:, 24 + s * 8:24 + (s + 1) * 8], op0=ALU.mult, op1=ALU.mult), reads=[Rvec, Rnrm], writes=[Rgv])

        def make_gate_bcast(s):
            for ci in range(2):
                bank0 = 4 + 2 * ci
                for c in range(8):
                    dgi = rot["dg"] % 2
                    rot["dg"] += 1
                    S.op("dve", lambda c=c, ci=ci, dgi=dgi: nc.vector.tensor_scalar(
                        out=DG[:, dgi, :], in0=ident[:], scalar1=GV[:, ci, s, c:c + 1], scalar2=None, op0=ALU.mult),
                        reads=[Rid, Rgv], writes=[Rdg[dgi]])
                    bank = bank0 + c // 4
                    S.op("pe", lambda c=c, dgi=dgi, bank=bank: nc.tensor.matmul(
                        PS[:, bank, (c % 4) * 128:(c % 4 + 1) * 128], lhsT=ones32[:], rhs=DG[:, dgi, :], start=True, stop=True),
                        reads=[Rdg[dgi], Rones], writes=[RPS[bank]])
                S.op("dve", lambda ci=ci, bank0=bank0: nc.vector.tensor_copy(
                    out=GB[:, ci, :], in_=PS[:, bank0:bank0 + 2, :].rearrange("p b n -> p (b n)")),
                    reads=[RPS[bank0], RPS[bank0 + 1]], writes=[Rgb[ci]])

        ones32 = sb("ones32", [128, 128])
        Rones = Reg("ones")
        S.op("dve", lambda: nc.vector.memset(ones32[:], 1.0), writes=[Rones])

        cur = {"XN": XN, "TMP": TMP}

        def prenorm_p1(t):
            XN = cur["XN"]
            smi = rot["sm"] % 4
            rot["sm"] += 1
            xi = rot["xn"] % 2
            rot["xn"] += 1
            sm = small[:, smi, :]
            S.op("act", lambda: nc.scalar.activation(out=junk[:], in_=X[:, t, :], func=AF.Square, accum_out=sm[:, 0:1]),
                 reads=[RX[t]], writes=[Rjunk, Rsmall[smi]])
            S.op("act", lambda: nc.scalar.activation(out=sm[:, 1:2], in_=sm[:, 0:1], func=AF.Sqrt, scale=1.0 / D, bias=epsb[:, 0:1]),
                 reads=[Rsmall[smi], Reps], writes=[Rsmall[smi]])
            S.op("dve", lambda: nc.vector.reciprocal(out=sm[:, 2:3], in_=sm[:, 1:2]), reads=[Rsmall[smi]], writes=[Rsmall[smi]])
            S.op("dve", lambda: nc.vector.tensor_scalar(out=XN[:, xi, :], in0=X[:, t, :], scalar1=sm[:, 2:3], scalar2=None, op0=ALU.mult),
                 reads=[RX[t], Rsmall[smi]], writes=[Rxn[xi]])
            return xi

        def prenorm_p2(xi, ci, s, dst, dst_reg, banks):
            XN = cur["XN"]
            b0, b1 = banks

            def emit():
                inst = None
                for c in range(8):
                    bk = b0 if c < 4 else b1
                    inst = nc.tensor.transpose(out=PS[:, bk, (c % 4) * 128:(c % 4 + 1) * 128], in_=XN[:, xi, c * 128:(c + 1) * 128], identity=ident[:])
                return inst
            S.op("pe", emit, reads=[Rxn[xi], Rid], writes=[RPS[b0], RPS[b1]])
            for c in range(8):
                bk = b0 if c < 4 else b1
                S.op("act", lambda c=c, bk=bk: nc.scalar.activation(
                    out=dst[:, c, :], in_=PS[:, bk, (c % 4) * 128:(c % 4 + 1) * 128], func=AF.Identity,
                    scale=SV[:, ci, s, c:c + 1], bias=VEC[:, ci, 3 * s * 8 + c:3 * s * 8 + c + 1]),
                    reads=[RPS[bk], Rsv, Rvec], writes=[dst_reg])

        def prenorm_tile(t, ci, s, dst, dst_reg, banks):
            prenorm_p2(prenorm_p1(t), ci, s, dst, dst_reg, banks)

        epsb = sb("epsb", [128, 1])
        halfpi = sb("halfpi", [128, 1])
        Reps = Reg("eps")
        S.op("dve", lambda: nc.vector.memset(epsb[:], EPS), writes=[Reps])
        S.op("dve", lambda: nc.vector.memset(halfpi[:], float(np.pi / 2)), writes=[Reps])

        Rtmp = [Reg("TMP0"), Reg("TMP1")]

        def postnorm_tile(t, ci, b0):
            TMP = cur["TMP"]
            smi = rot["sm"] % 4
            rot["sm"] += 1
            ti = rot["xn"] % 2
            rot["xn"] += 1
            sm = small[:, smi, :]
            fin = PS[:, b0:b0 + 2, :].rearrange("p b n -> p (b n)")
            S.op("act", lambda: nc.scalar.activation(out=junk[:], in_=fin, func=AF.Square, accum_out=sm[:, 0:1]),
                 reads=[RPS[b0], RPS[b0 + 1]], writes=[Rjunk, Rsmall[smi]])
            S.op("act", lambda: nc.scalar.activation(out=sm[:, 1:2], in_=sm[:, 0:1], func=AF.Sqrt, scale=1.0 / D, bias=epsb[:, 0:1]),
                 reads=[Rsmall[smi], Reps], writes=[Rsmall[smi]])
            S.op("dve", lambda: nc.vector.reciprocal(out=sm[:, 2:3], in_=sm[:, 1:2]), reads=[Rsmall[smi]], writes=[Rsmall[smi]])
            S.op("dve", lambda: nc.vector.scalar_tensor_tensor(out=TMP[:, ti, :], in0=fin, scalar=sm[:, 2:3], in1=GB[:, ci, :],
                                                               op0=ALU.mult, op1=ALU.mult),
                 reads=[RPS[b0], RPS[b0 + 1], Rsmall[smi], Rgb[ci]], writes=[Rtmp[ti]])
            S.op("dve", lambda: nc.vector.tensor_tensor(out=X[:, t, :], in0=X[:, t, :], in1=TMP[:, ti, :], op=ALU.add),
                 reads=[RX[t], Rtmp[ti]], writes=[RX[t]])

        Rhtb = [Reg("HTB0"), Reg("HTB1")]
        Rgt = [Reg("GT%d" % j) for j in range(NFF)]
        Rw13 = [Reg("W13_%d" % i) for i in range(3)]
        Rw2 = [Reg("W2_%d" % i) for i in range(3)]
        Rsil = [Reg("SIL0"), Reg("SIL1")]
        cnts = {"w13": 0, "w2": 0, "htb": 0, "sil": 0, "pa": 0}

        def ffn(l, f, s):
            make_gate_bcast(s)
            w1v = ffn_w1[l, f].rearrange("(kc p) n -> p kc n", p=128)
            w3v = ffn_w3[l, f].rearrange("(kc p) n -> p kc n", p=128)
            w2v = ffn_w2[l, f].rearrange("(j p) n -> p j n", p=128)
            hb0 = cnts["htb"]
            cnts["htb"] += 5

            def prenorm_block(blk_):
                hb_ = (hb0 + blk_) % 2
                ci_ = 0 if blk_ < 4 else 1
                for tt in range(4):
                    prenorm_tile(blk_ * 4 + tt, ci_, s, HTB[:, hb_, :, tt * 128:(tt + 1) * 128], Rhtb[hb_], (4 + 2 * (tt % 2), 5 + 2 * (tt % 2)))
            prenorm_block(0)
            pend = [None] * 4
            for blk in range(5):
                ci = 0 if blk < 4 else 1
                hb = (hb0 + blk) % 2
                for j2 in range(NFF // 2):
                    if blk + 1 < 5:
                        nb_, hbn, cin = blk + 1, (hb0 + blk + 1) % 2, (0 if blk + 1 < 4 else 1)
                        if 4 <= j2 <= 7:
                            tt_ = j2 - 4
                            prenorm_p2(pend[tt_], cin, s, HTB[:, hbn, :, tt_ * 128:(tt_ + 1) * 128], Rhtb[hbn], (4 + 2 * (tt_ % 2), 5 + 2 * (tt_ % 2)))
                        if 2 <= j2 <= 5:
                            pend[j2 - 2] = prenorm_p1(nb_ * 4 + (j2 - 2))
                    sl = cnts["w13"] % 3
                    cnts["w13"] += 1
                    S.dma("pool", W13[:, sl, 0, :, :], w1v[:, :, j2 * 256:(j2 + 1) * 256], writes=[Rw13[sl]])
                    S.dma("pool", W13[:, sl, 1, :, :], w3v[:, :, j2 * 256:(j2 + 1) * 256], writes=[Rw13[sl]])
                    for jj in range(2):
                        j = 2 * j2 + jj
                        pa = cnts["pa"] % 2
                        cnts["pa"] += 1
                        b1, b3 = 2 * pa, 2 * pa + 1
                        for (m, bk) in ((0, b1), (1, b3)):
                            def emit(m=m, bk=bk, jj=jj, sl=sl):
                                inst = None
                                for kc in range(8):
                                    inst = nc.tensor.matmul(PS[:, bk, :], lhsT=W13[:, sl, m, kc, jj * 128:(jj + 1) * 128],
                                                            rhs=HTB[:, hb, kc, :], start=(kc == 0), stop=(kc == 7))
                                return inst
                            S.op("pe", emit, reads=[Rw13[sl], Rhtb[hb]], writes=[RPS[bk]])
                        si = cnts["sil"] % 2
                        cnts["sil"] += 1
                        S.op("act", lambda b1=b1, si=si: nc.scalar.activation(out=SIL[:, si, :], in_=PS[:, b1, :], func=AF.Silu),
                             reads=[RPS[b1]], writes=[Rsil[si]])
                        S.op("dve", lambda b3=b3, si=si, j=j: nc.vector.tensor_tensor(out=GT[:, j, :], in0=PS[:, b3, :], in1=SIL[:, si, :], op=ALU.mult),
                             reads=[RPS[b3], Rsil[si]], writes=[Rgt[j]])
                for j2 in range(NFF // 2):
                    sl = cnts["w2"] % 3
                    cnts["w2"] += 1
                    S.dma("pool", W2[:, sl, :, :], w2v[:, 2 * j2:2 * j2 + 2, :], writes=[Rw2[sl]])
                    for jj in range(2):
                        j = 2 * j2 + jj

                        def emit(j=j, jj=jj, sl=sl):
                            inst = None
                            for tt in range(4):
                                for half in range(2):
                                    inst = nc.tensor.matmul(PS[:, 2 * tt + half, :], lhsT=GT[:, j, tt * 128:(tt + 1) * 128],
                                                            rhs=W2[:, sl, jj, half * 512:(half + 1) * 512], start=(j == 0), stop=(j == NFF - 1))
                            return inst
                        S.op("pe", emit, reads=[Rgt[j], Rw2[sl]], writes=RPS)
                for tt in range(4):
                    postnorm_tile(blk * 4 + tt, ci, 2 * tt)

        MOFF = 0
        HT, MOFF = view(MOFF, [128, 8, 2048], BF16)
        BR, MOFF = view(MOFF, [128, 8, 2048], BF16)
        WIN0 = MOFF
        WIN, MOFF = view(MOFF, [128, 2, 8, 256], BF16)
        BS0 = MOFF
        Rht = Reg("HT")
        Rbr = [Reg("BR%d" % i) for i in range(8)]
        Rwin = [Reg("WIN0"), Reg("WIN1")]
        PV = sb("PV", [128, 64])
        Rpv = Reg("PV")
        BG = sb("BG", [128, 32])
        Rbg = Reg("BG")
        mc = {"win": 0, "pb": 0, "wg": 0, "wo": 0, "sg": 0}
        SEQS = [(0, 16, 0, "sample"), (16, 2, 1, "pA"), (18, 2, 1, "pB")]

        def mixer_params(l):
            load_T(PV[:, 0:6], conv_w[l].rearrange("j (c p) -> (j c) p", p=128), 6, Rpv)
            load_T(PV[:, 6:8], conv_b[l].rearrange("(c p) -> c p", p=128), 2, Rpv)
            load_T(PV[:, 8:10], ret_gn[l].rearrange("(c p) -> c p", p=128), 2, Rpv)
            load_T(PV[:, 10:12], s5_d[l].rearrange("(c p) -> c p", p=128), 2, Rpv)
            load_T(PV[:, 12:13], mla_kv_norm[l].rearrange("(c p) -> c p", p=128), 1, Rpv)
            load_T(BG[:, :], b_gate[l].rearrange("(c p) -> c p", p=128), 32, Rbg)

        def proj_fm(l, col0, ncols, NB, BW, evac, extra_reads=()):
            winv = w_in[l].rearrange("(kc p) n -> p kc n", p=128)
            sl = mc["win"] % 2
            mc["win"] += 1
            S.dma("pool", WIN[:, sl, :, 0:ncols], winv[:, :, col0:col0 + ncols], writes=[Rwin[sl]])
            for b in range(NB):
                bank = mc["pb"] % 4
                mc["pb"] += 1

                def emit(b=b, bank=bank):
                    inst = None
                    for kc in range(8):
                        inst = nc.tensor.matmul(PS[0:ncols, bank, 0:BW], lhsT=WIN[:, sl, kc, 0:ncols], rhs=HT[:, kc, b * BW:(b + 1) * BW],
                                                start=(kc == 0), stop=(kc == 7))
                    return inst
                S.op("pe", emit, reads=[Rwin[sl], Rht], writes=[RPS[bank]])
                evac(b, bank)

        def branch_conv(l, T, NB, BW):
            o = BS0
            Z, o = view(o, [128, 2056])
            CX, o = view(o, [128, 2048])
            Y, o = view(o, [128, 2048])
            CB, o = view(o, [128, 2048], BF16)
            Rz, Rcx, Ry, Rcb = Reg("Z"), Reg("CX"), Reg("Y"), Reg("CB")
            for cc in range(2):
                S.op("dve", lambda: nc.vector.memset(Z[:, 0:1], 0.0), writes=[Rz])
                S.op("dve", lambda: nc.vector.memset(Z[:, T + 1:T + 2], 0.0), writes=[Rz])
                proj_fm(l, 1280 + cc * 128, 128, NB, BW, lambda b, bank: S.op(
                    "act", lambda: nc.scalar.copy(out=CX[:, b * BW:(b + 1) * BW], in_=PS[:, bank, 0:BW]), reads=[RPS[bank]], writes=[Rcx]))
                proj_fm(l, 1792 + cc * 128, 128, NB, BW, lambda b, bank: S.op(
                    "dve", lambda: nc.vector.tensor_tensor(out=Z[:, 1 + b * BW:1 + (b + 1) * BW], in0=PS[:, bank, 0:BW], in1=CX[:, b * BW:(b + 1) * BW], op=ALU.mult),
                    reads=[RPS[bank], Rcx], writes=[Rz]))
                proj_fm(l, 1536 + cc * 128, 128, NB, BW, lambda b, bank: S.op(
                    "act", lambda: nc.scalar.copy(out=CB[:, b * BW:(b + 1) * BW], in_=PS[:, bank, 0:BW]), reads=[RPS[bank]], writes=[Rcb]))
                S.op("dve", lambda: nc.vector.tensor_scalar(out=Y[:, 0:T], in0=Z[:, 1:T + 1], scalar1=PV[:, 2 + cc:3 + cc], scalar2=PV[:, 6 + cc:7 + cc],
                                                            op0=ALU.mult, op1=ALU.add), reads=[Rz, Rpv], writes=[Ry])
                S.op("dve", lambda: nc.vector.scalar_tensor_tensor(out=Y[:, 0:T], in0=Z[:, 0:T], scalar=PV[:, 0 + cc:1 + cc], in1=Y[:, 0:T],
                                                                   op0=ALU.mult, op1=ALU.add), reads=[Rz, Rpv, Ry], writes=[Ry])
                S.op("dve", lambda: nc.vector.scalar_tensor_tensor(out=Y[:, 0:T], in0=Z[:, 2:T + 2], scalar=PV[:, 4 + cc:5 + cc], in1=Y[:, 0:T],
                                                                   op0=ALU.mult, op1=ALU.add), reads=[Rz, Rpv, Ry], writes=[Ry])
                S.op("dve", lambda: nc.vector.tensor_tensor(out=BR[:, 4 + cc, 0:T], in0=Y[:, 0:T], in1=CB[:, 0:T], op=ALU.mult),
                     reads=[Ry, Rcb], writes=[Rbr[4 + cc]])

        def gate_stage(l, t0, ntile, ci, T, NB, BW):
            o = WIN0
            WG, o = view(o, [128, 2, 8, 4, 128], BF16)
            WB, o = view(o, [128, 2, 2, 4, 128], BF16)
            MG, o = view(o, [128, 8, 512], BF16)
            SG, o = view(o, [128, 4, 512], BF16)
            WO, o = view(o, [128, 2, D], BF16)
            ACC, o = view(o, [128, 2, 512])
            cur["TMP"], o = view(o, [128, 2, D])
            Rwg = [Reg("WG0"), Reg("WG1")]
            Rmg = [Reg("MG%d" % c) for c in range(8)]
            Rsg = [Reg("SG%d" % n) for n in range(4)]
            Rwo = [Reg("WO0"), Reg("WO1")]
            Racc = [Reg("ACC0"), Reg("ACC1")]
            wgv = w_gate[l].rearrange("(kc p) (n d) -> p kc n d", p=128, n=4)
            wbv = w_branch[l].rearrange("n (kc p) d -> p kc n d", p=128)
            tpb = BW // 128
            for b in range(NB):
                for c in range(8):
                    sl = mc["wg"] % 2
                    mc["wg"] += 1
                    for n in range(4):
                        S.dma("pool", WG[:, sl, :, n, :], wgv[:, :, n, c * 128:(c + 1) * 128], writes=[Rwg[sl]])
                    for n in range(4):
                        S.dma("pool", WB[:, sl, :, n, :], wbv[:, :, n, c * 128:(c + 1) * 128], writes=[Rwg[sl]])
                    for n in range(4):
                        def emit_g(n=n, sl=sl):
                            inst = None
                            for kc in range(8):
                                inst = nc.tensor.matmul(PS[:, n, 0:BW], lhsT=WG[:, sl, kc, n, :], rhs=HT[:, kc, b * BW:(b + 1) * BW],
                                                        start=(kc == 0), stop=(kc == 7))
                            return inst
                        S.op("pe", emit_g, reads=[Rwg[sl], Rht], writes=[RPS[n]])

                        def emit_p(n=n, sl=sl):
                            inst = None
                            for kc in range(2):
                                inst = nc.tensor.matmul(PS[:, 4 + n, 0:BW], lhsT=WB[:, sl, kc, n, :], rhs=BR[:, 2 * n + kc, b * BW:(b + 1) * BW],
                                                        start=(kc == 0), stop=(kc == 1))
                            return inst
                        S.op("pe", emit_p, reads=[Rwg[sl], Rbr[2 * n], Rbr[2 * n + 1]], writes=[RPS[4 + n]])
                        S.op("act", lambda n=n: nc.scalar.activation(out=SG[:, n, 0:BW], in_=PS[:, n, 0:BW], func=AF.Sigmoid,
                                                                     bias=BG[:, n * 8 + c:n * 8 + c + 1]), reads=[RPS[n], Rbg], writes=[Rsg[n]])
                    S.op("dve", lambda: nc.vector.tensor_tensor(out=ACC[:, 0, 0:BW], in0=PS[:, 4, 0:BW], in1=SG[:, 0, 0:BW], op=ALU.mult),
                         reads=[RPS[4], Rsg[0]], writes=[Racc[0]])
                    for n in range(1, 4):
                        S.op("dve", lambda n=n: nc.vector.tensor_tensor(out=ACC[:, 1, 0:BW], in0=PS[:, 4 + n, 0:BW], in1=SG[:, n, 0:BW], op=ALU.mult),
                             reads=[RPS[4 + n], Rsg[n]], writes=[Racc[1]])
                        if n < 3:
                            S.op("dve", lambda: nc.vector.tensor_tensor(out=ACC[:, 0, 0:BW], in0=ACC[:, 0, 0:BW], in1=ACC[:, 1, 0:BW], op=ALU.add),
                                 reads=[Racc[0], Racc[1]], writes=[Racc[0]])
                        else:
                            S.op("dve", lambda: nc.vector.tensor_tensor(out=MG[:, c, 0:BW], in0=ACC[:, 0, 0:BW], in1=ACC[:, 1, 0:BW], op=ALU.add),
                                 reads=[Racc[0], Racc[1]], writes=[Rmg[c]])
                for c in range(8):
                    sl = mc["wo"] % 2
                    mc["wo"] += 1
                    S.dma("pool", WO[:, sl, :], w_o[l, c * 128:(c + 1) * 128, :], writes=[Rwo[sl]])

                    def emit_o(c=c, sl=sl):
                        inst = None
                        for tt in range(tpb):
                            for half in range(2):
                                inst = nc.tensor.matmul(PS[:, 2 * tt + half, :], lhsT=MG[:, c, tt * 128:(tt + 1) * 128],
                                                        rhs=WO[:, sl, half * 512:(half + 1) * 512], start=(c == 0), stop=(c == 7))
                        return inst
                    S.op("pe", emit_o, reads=[Rmg[c], Rwo[sl]], writes=RPS[0:2 * tpb])
                for tt in range(tpb):
                    postnorm_tile(t0 + b * tpb + tt, ci, 2 * tt)

        onesb = sb("onesb", [128, 128], BF16)
        S.op("dve", lambda: nc.vector.memset(onesb[:], 1.0), writes=[Rones])
        ATT_SCALE = float(96 ** -0.5)

        def branch_mla(l, t0, T, NB, BW, kind):
            sample = kind == "sample"
            Skeys = T + (512 if sample else 0)
            NKT = Skeys // 128
            o = BS0
            CQ, o = view(o, [128, 2, 2048], BF16)
            CKVN, o = view(o, [128, 2560], BF16)
            KR, o = view(o, [128, 2560], BF16)
            WUQ, o = view(o, [128, 2, 384], BF16)
            WUQS, o = view(o, [128, 2, 4, 32], BF16)
            WUKV, o = view(o, [128, 512], BF16)
            WKRS, o = view(o, [128, 8, 32], BF16)
            QNV, o = view(o, [128, 2])
            oB = o
            Rcq, Rckvn, Rkr, Rw = Reg("m_CQ"), Reg("m_CKVN"), Reg("m_KR"), Reg("m_W")
            W32, oo = view(oB, [128, 2, 384])
            Rw32 = Reg("m_W32")
            S.dma("sp", W32[:, 0, :], mla_w_uq[l, 0:128, :], writes=[Rw32])
            S.dma("sp", W32[0:64, 1, :], mla_w_uq[l, 128:192, :], writes=[Rw32])
            S.dma("sp", QNV[:, 0:1], mla_q_norm[l, 0:128].unsqueeze(1), writes=[Rw])
            S.dma("sp", QNV[0:64, 1:2], mla_q_norm[l, 128:192].unsqueeze(1), writes=[Rw])
            S.dma("pool", WUKV[:, :], mla_w_ukv[l, :, :], writes=[Rw])
            for kc, np_ in ((0, 128), (1, 64)):
                S.op("dve", lambda kc=kc, np_=np_: nc.vector.tensor_scalar(out=WUQ[0:np_, kc, :], in0=W32[0:np_, kc, :], scalar1=QNV[0:np_, kc:kc + 1],
                                                                          scalar2=None, op0=ALU.mult), reads=[Rw32, Rw], writes=[Rw])
                if sample:
                    wv = WUQ[0:np_, kc, :].rearrange("p (h e) -> p h e", h=4)
                    S.op("dve", lambda wv=wv, kc=kc, np_=np_: nc.vector.tensor_scalar(out=WUQS[0:np_, kc, :, 0:16], in0=wv[:, :, 80:96], scalar1=-1.0,
                                                                                   scalar2=None, op0=ALU.mult), reads=[Rw], writes=[Rw])
                    S.op("dve", lambda wv=wv, kc=kc, np_=np_: nc.vector.tensor_copy(out=WUQS[0:np_, kc, :, 16:32], in_=wv[:, :, 64:80]), reads=[Rw], writes=[Rw])
            if sample:
                winv = w_in[l].rearrange("(kc p) n -> p kc n", p=128)
                S.dma("pool", WKRS[:, :, 0:16], winv[:, :, 2384:2400], writes=[Rw])
                S.dma("pool", WKRS[:, :, 16:32], winv[:, :, 2368:2384], writes=[Rw])
                S.op("dve", lambda: nc.vector.tensor_scalar(out=WKRS[:, :, 0:16], in0=WKRS[:, :, 0:16], scalar1=-1.0, scalar2=None, op0=ALU.mult),
                     reads=[Rw], writes=[Rw])
            SQ, oo = view(oo, [128, 2, 512], BF16)
            RST, oo = view(oo, [128, 512])
            TB, oo = view(oo, [128, 2, 512])
            T1, oo = view(oo, [128, 2, 512])
            Rsq, Rrst, Rtb, Rt1 = Reg("m_SQ"), Reg("m_RST"), Reg("m_TB"), Reg("m_T1")

            def rstd_from_ps(bank, parts):
                S.op("act", lambda: nc.scalar.activation(out=RST[:, 0:BW], in_=PS[:, bank, 0:BW], func=AF.Sqrt, scale=1.0 / parts, bias=epsb[:, 0:1]),
                     reads=[RPS[bank], Reps], writes=[Rrst])
                S.op("dve", lambda: nc.vector.reciprocal(out=RST[:, 0:BW], in_=RST[:, 0:BW]), reads=[Rrst], writes=[Rrst])

            winv = w_in[l].rearrange("(kc p) n -> p kc n", p=128)
            WQ = WIN
            S.dma("pool", WQ[:, 0, :, 0:192], winv[:, :, 2048:2240], writes=[Rwin[0]])
            S.dma("pool", WQ[:, 1, :, 0:160], winv[:, :, 2240:2400], writes=[Rwin[1]])
            for b in range(NB):
                cols = slice(b * BW, (b + 1) * BW)
                for kc2, np_, bank in ((0, 128, 0), (1, 64, 1)):
                    def emit(kc2=kc2, np_=np_, bank=bank):
                        inst = None
                        for kc in range(8):
                            inst = nc.tensor.matmul(PS[0:np_, bank, 0:BW], lhsT=WQ[:, 0, kc, kc2 * 128:kc2 * 128 + np_], rhs=HT[:, kc, cols],
                                                    start=(kc == 0), stop=(kc == 7))
                        return inst
                    S.op("pe", emit, reads=[Rwin[0], Rht], writes=[RPS[bank]])
                    S.op("act", lambda kc2=kc2, np_=np_, bank=bank: nc.scalar.activation(out=SQ[0:np_, kc2, 0:BW], in_=PS[0:np_, bank, 0:BW], func=AF.Square),
                         reads=[RPS[bank]], writes=[Rsq])

                def emit_ss():
                    nc.tensor.matmul(PS[:, 2, 0:BW], lhsT=onesb[:, :], rhs=SQ[:, 0, 0:BW], start=True, stop=False)
                    return nc.tensor.matmul(PS[:, 2, 0:BW], lhsT=onesb[0:64, :], rhs=SQ[0:64, 1, 0:BW], start=False, stop=True)
                S.op("pe", emit_ss, reads=[Rsq, Rones], writes=[RPS[2]])
                rstd_from_ps(2, 192.0)
                for kc2, np_, bank in ((0, 128, 0), (1, 64, 1)):
                    S.op("dve", lambda kc2=kc2, np_=np_, bank=bank: nc.vector.tensor_tensor(out=CQ[0:np_, kc2, cols], in0=PS[0:np_, bank, 0:BW], in1=RST[0:np_, 0:BW], op=ALU.mult),
                         reads=[RPS[bank], Rrst], writes=[Rcq])
                def emit_kv():
                    inst = None
                    for kc in range(8):
                        inst = nc.tensor.matmul(PS[:, 3, 0:BW], lhsT=WQ[:, 1, kc, 0:128], rhs=HT[:, kc, cols], start=(kc == 0), stop=(kc == 7))
                    return inst
                S.op("pe", emit_kv, reads=[Rwin[1], Rht], writes=[RPS[3]])
                S.op("act", lambda: nc.scalar.activation(out=SQ[:, 0, 0:BW], in_=PS[:, 3, 0:BW], func=AF.Square), reads=[RPS[3]], writes=[Rsq])
                S.op("pe", lambda: nc.tensor.matmul(PS[:, 2, 0:BW], lhsT=onesb[:, :], rhs=SQ[:, 0, 0:BW], start=True, stop=True), reads=[Rsq, Rones], writes=[RPS[2]])
                rstd_from_ps(2, 128.0)
                S.op("dve", lambda: nc.vector.scalar_tensor_tensor(out=CKVN[:, cols], in0=PS[:, 3, 0:BW], scalar=PV[:, 12:13], in1=RST[:, 0:BW], op0=ALU.mult, op1=ALU.mult),
                     reads=[RPS[3], Rpv, Rrst], writes=[Rckvn])
                def emit_kr():
                    inst = None
                    for kc in range(8):
                        inst = nc.tensor.matmul(PS[0:32, 4, 0:BW], lhsT=WQ[:, 1, kc, 128:160], rhs=HT[:, kc, cols], start=(kc == 0), stop=(kc == 7))
                    return inst
                S.op("pe", emit_kr, reads=[Rwin[1], Rht], writes=[RPS[4]])
                if sample:
                    def emit_krs():
                        inst = None
                        for kc in range(8):
                            inst = nc.tensor.matmul(PS[0:32, 5, 0:BW], lhsT=WKRS[:, kc, :], rhs=HT[:, kc, cols], start=(kc == 0), stop=(kc == 7))
                        return inst
                    S.op("pe", emit_krs, reads=[Rw, Rht], writes=[RPS[5]])
                    S.dma("sp", TB[0:32, 0, 0:BW], c_rope_mla[0, :, cols], writes=[Rtb])
                    S.dma("sp", TB[0:32, 1, 0:BW], c_rope_mla[1, :, cols], writes=[Rtb])
                    S.op("dve", lambda: nc.vector.tensor_tensor(out=T1[0:32, 0, 0:BW], in0=PS[0:32, 4, 0:BW], in1=TB[0:32, 0, 0:BW], op=ALU.mult),
                         reads=[RPS[4], Rtb], writes=[Rt1])
                    S.op("dve", lambda: nc.vector.tensor_tensor(out=T1[0:32, 1, 0:BW], in0=PS[0:32, 5, 0:BW], in1=TB[0:32, 1, 0:BW], op=ALU.mult),
                         reads=[RPS[5], Rtb], writes=[Rt1])
                    S.op("dve", lambda: nc.vector.tensor_tensor(out=KR[0:32, cols], in0=T1[0:32, 0, 0:BW], in1=T1[0:32, 1, 0:BW], op=ALU.add),
                         reads=[Rt1], writes=[Rkr])
                else:
                    S.op("act", lambda: nc.scalar.copy(out=KR[0:32, cols], in_=PS[0:32, 4, 0:BW]), reads=[RPS[4]], writes=[Rkr])
            if sample:
                CT, _ = view(oB + 3072, [128, 4, 160])
                Rct = Reg("m_CT")
                S.barrier()
                S.dma("sp", CT[:, :, :], ctx_mla[l].rearrange("(i p) f -> p i f", p=128), writes=[Rct])
                for i in range(4):
                    S.op("pe", lambda i=i: nc.tensor.transpose(out=PS[:, 6, 0:128], in_=CT[:, i, 0:128], identity=ident[:]), reads=[Rct, Rid], writes=[RPS[6]])
                    S.op("act", lambda i=i: nc.scalar.copy(out=CKVN[:, T + i * 128:T + (i + 1) * 128], in_=PS[:, 6, 0:128]), reads=[RPS[6]], writes=[Rckvn])
                    S.op("pe", lambda i=i: nc.tensor.transpose(out=PS[0:32, 7, 0:128], in_=CT[:, i, 128:160], identity=ident[:]), reads=[Rct, Rid], writes=[RPS[7]])
                    S.op("act", lambda i=i: nc.scalar.copy(out=KR[0:32, T + i * 128:T + (i + 1) * 128], in_=PS[0:32, 7, 0:128]), reads=[RPS[7]], writes=[Rkr])
            S.barrier()
            o = oB
            KN, o = view(o, [128, 2560], BF16)
            QN, o = view(o, [128, 2048], BF16)
            QR, o = view(o, [128, 2048], BF16)
            VA, o = view(o, [128, 20, 66], BF16)
            Rkn, Rqn, Rqr, Rva = Reg("m_KN"), Reg("m_QN"), Reg("m_QR"), Reg("m_VA")
            ow = WIN0
            TB2, ow2 = view(ow, [128, 2, 512])
            T2, ow2 = view(ow2, [128, 2, 512])
            PT, ow3 = view(ow, [128, 2, 512], BF16)
            OS, ow3 = view(ow3, [128, 512])
            OT, ow3 = view(ow3, [128, 512], BF16)
            Rtb2, Rt2, Rpt, Ros, Rot = Reg("m_TB2"), Reg("m_T2"), [Reg("m_PT0"), Reg("m_PT1")], Reg("m_OS"), Reg("m_OT")
            S.op("dve", lambda: nc.vector.memset(VA[:, :, 64:66], 1.0), writes=[Rva])
            KBW = 512
            for h in range(4):
                for kb in range((Skeys + KBW - 1) // KBW):
                    w = min(KBW, Skeys - kb * KBW)
                    bank = mc["pb"] % 4
                    mc["pb"] += 1
                    S.op("pe", lambda kb=kb, w=w, bank=bank: nc.tensor.matmul(PS[0:64, bank, 0:w], lhsT=WUKV[:, h * 128:h * 128 + 64], rhs=CKVN[:, kb * KBW:kb * KBW + w],
                                                                              start=True, stop=True), reads=[Rw, Rckvn], writes=[RPS[bank]])
                    S.op("act", lambda kb=kb, w=w, bank=bank: nc.scalar.copy(out=KN[0:64, kb * KBW:kb * KBW + w], in_=PS[0:64, bank, 0:w]), reads=[RPS[bank]], writes=[Rkn])
                for kt in range(NKT):
                    bank = mc["pb"] % 4
                    mc["pb"] += 1
                    S.op("pe", lambda kt=kt, bank=bank: nc.tensor.matmul(PS[:, bank, 0:64], lhsT=CKVN[:, kt * 128:(kt + 1) * 128], rhs=WUKV[:, h * 128 + 64:h * 128 + 128],
                                                                          start=True, stop=True), reads=[Rw, Rckvn], writes=[RPS[bank]])
                    S.op("dve", lambda kt=kt, bank=bank: nc.vector.tensor_copy(out=VA[:, kt, 0:64], in_=PS[:, bank, 0:64]), reads=[RPS[bank]], writes=[Rva])
                for b in range(NB):
                    cols = slice(b * BW, (b + 1) * BW)
                    bank = mc["pb"] % 4
                    mc["pb"] += 1

                    def emit_q(c0, m, bank, wt=None):
                        def f():
                            if wt is None:
                                nc.tensor.matmul(PS[0:m, bank, 0:BW], lhsT=WUQ[:, 0, c0:c0 + m], rhs=CQ[:, 0, cols], start=True, stop=False)
                                return nc.tensor.matmul(PS[0:m, bank, 0:BW], lhsT=WUQ[0:64, 1, c0:c0 + m], rhs=CQ[0:64, 1, cols], start=False, stop=True)
                            nc.tensor.matmul(PS[0:m, bank, 0:BW], lhsT=WUQS[:, 0, h, :], rhs=CQ[:, 0, cols], start=True, stop=False)
                            return nc.tensor.matmul(PS[0:m, bank, 0:BW], lhsT=WUQS[0:64, 1, h, :], rhs=CQ[0:64, 1, cols], start=False, stop=True)
                        return f
                    S.op("pe", emit_q(h * 96, 64, bank), reads=[Rw, Rcq], writes=[RPS[bank]])
                    S.op("act", lambda bank=bank: nc.scalar.copy(out=QN[0:64, cols], in_=PS[0:64, bank, 0:BW]), reads=[RPS[bank]], writes=[Rqn])
                    bank2 = mc["pb"] % 4
                    mc["pb"] += 1
                    S.op("pe", emit_q(h * 96 + 64, 32, bank2), reads=[Rw, Rcq], writes=[RPS[bank2]])
                    if sample:
                        bank3 = mc["pb"] % 4
                        mc["pb"] += 1
                        S.op("pe", emit_q(0, 32, bank3, wt=1), reads=[Rw, Rcq], writes=[RPS[bank3]])
                        S.dma("sp", TB2[0:32, 0, 0:BW], c_rope_mla[0, :, cols], writes=[Rtb2])
                        S.dma("sp", TB2[0:32, 1, 0:BW], c_rope_mla[1, :, cols], writes=[Rtb2])
                        S.op("dve", lambda: nc.vector.tensor_tensor(out=T2[0:32, 0, 0:BW], in0=PS[0:32, bank2, 0:BW], in1=TB2[0:32, 0, 0:BW], op=ALU.mult),
                             reads=[RPS[bank2], Rtb2], writes=[Rt2])
                        S.op("dve", lambda: nc.vector.tensor_tensor(out=T2[0:32, 1, 0:BW], in0=PS[0:32, bank3, 0:BW], in1=TB2[0:32, 1, 0:BW], op=ALU.mult),
                             reads=[RPS[bank3], Rtb2], writes=[Rt2])
                        S.op("dve", lambda: nc.vector.tensor_tensor(out=QR[0:32, cols], in0=T2[0:32, 0, 0:BW], in1=T2[0:32, 1, 0:BW], op=ALU.add),
                             reads=[Rt2], writes=[Rqr])
                    else:
                        S.op("act", lambda: nc.scalar.copy(out=QR[0:32, cols], in_=PS[0:32, bank2, 0:BW]), reads=[RPS[bank2]], writes=[Rqr])
                S.barrier()
                for b in range(NB):
                    cols = slice(b * BW, (b + 1) * BW)
                    ob = 4 + (b % 2)
                    for kt in range(NKT):
                        bank = mc["pb"] % 4
                        mc["pb"] += 1
                        pi = kt % 2

                        def emit_s(kt=kt, bank=bank):
                            nc.tensor.matmul(PS[:, bank, 0:BW], lhsT=KN[0:64, kt * 128:(kt + 1) * 128], rhs=QN[0:64, cols], start=True, stop=False)
                            return nc.tensor.matmul(PS[:, bank, 0:BW], lhsT=KR[0:32, kt * 128:(kt + 1) * 128], rhs=QR[0:32, cols], start=False, stop=True)
                        S.op("pe", emit_s, reads=[Rkn, Rqn, Rkr, Rqr], writes=[RPS[bank]])
                        S.op("act", lambda bank=bank, pi=pi: nc.scalar.activation(out=PT[:, pi, 0:BW], in_=PS[:, bank, 0:BW], func=AF.Exp, scale=ATT_SCALE),
                             reads=[RPS[bank]], writes=[Rpt[pi]])
                        S.op("pe", lambda kt=kt, pi=pi: nc.tensor.matmul(PS[0:65, ob, 0:BW], lhsT=VA[:, kt, 0:65], rhs=PT[:, pi, 0:BW], start=(kt == 0), stop=(kt == NKT - 1)),
                             reads=[Rva, Rpt[pi]], writes=[RPS[ob]])
                    S.op("act", lambda: nc.scalar.copy(out=OS[0:65, 0:BW], in_=PS[0:65, ob, 0:BW]), reads=[RPS[ob]], writes=[Ros])
                    S.op("dve", lambda: nc.vector.reciprocal(out=OS[64:65, 0:BW], in_=OS[64:65, 0:BW]), reads=[Ros], writes=[Ros])
                    S.op("pe", lambda: nc.tensor.matmul(PS[0:64, 6, 0:BW], lhsT=ones32[64:65, 0:64], rhs=OS[64:65, 0:BW], start=True, stop=True),
                         reads=[Ros, Rones], writes=[RPS[6]])
                    S.op("dve", lambda: nc.vector.tensor_tensor(out=OT[0:64, 0:BW], in0=PS[0:64, 6, 0:BW], in1=OS[0:64, 0:BW], op=ALU.mult),
                         reads=[RPS[6], Ros], writes=[Rot])
                    S.dma("sp", BR[(h % 2) * 64:(h % 2) * 64 + 64, 6 + h // 2, cols], OT[0:64, 0:BW], reads=[Rot], writes=[Rbr[6 + h // 2]], key=Reg("m_OTd"))
                S.barrier()

        def mla_cache_out(l, t0, ntile, pi):
            o = BS0
            CA, o = view(o, [128, 2, 160])
            KVB, o = view(o, [128, 128])
            Rca, Rkvb = [Reg("m_CA0"), Reg("m_CA1")], Reg("m_KVB")
            winv = w_in[l].rearrange("(kc p) n -> p kc n", p=128)
            S.dma("pool", WIN[:, 0, :, 0:160], winv[:, :, 2240:2400], writes=[Rwin[0]])
            S.dma("sp", KVB[:, :], mla_kv_norm[l:l + 1, :].broadcast_to([128, 128]), writes=[Rkvb])
            for tt in range(ntile):
                bank = mc["pb"] % 4
                mc["pb"] += 1
                smi = rot["sm"] % 4
                rot["sm"] += 1
                sm = small[:, smi, :]

                def emit(tt=tt, bank=bank):
                    inst = None
                    for kc in range(8):
                        inst = nc.tensor.matmul(PS[:, bank, 0:160], lhsT=HT[:, kc, tt * 128:(tt + 1) * 128], rhs=WIN[:, 0, kc, 0:160], start=(kc == 0), stop=(kc == 7))
                    return inst
                S.op("pe", emit, reads=[Rwin[0], Rht], writes=[RPS[bank]])
                S.op("act", lambda: nc.scalar.activation(out=junk[:, 0:128], in_=PS[:, bank, 0:128], func=AF.Square, accum_out=sm[:, 0:1]),
                     reads=[RPS[bank]], writes=[Rjunk, Rsmall[smi]])
                S.op("act", lambda: nc.scalar.activation(out=sm[:, 1:2], in_=sm[:, 0:1], func=AF.Sqrt, scale=1.0 / 128, bias=epsb[:, 0:1]),
                     reads=[Rsmall[smi], Reps], writes=[Rsmall[smi]])
                S.op("dve", lambda: nc.vector.reciprocal(out=sm[:, 2:3], in_=sm[:, 1:2]), reads=[Rsmall[smi]], writes=[Rsmall[smi]])
                ci_ = tt % 2
                S.op("dve", lambda: nc.vector.scalar_tensor_tensor(out=CA[:, ci_, 0:128], in0=PS[:, bank, 0:128], scalar=sm[:, 2:3], in1=KVB[:, :], op0=ALU.mult, op1=ALU.mult),
                     reads=[RPS[bank], Rsmall[smi], Rkvb], writes=[Rca[ci_]])
                S.op("dve", lambda: nc.vector.tensor_copy(out=CA[:, ci_, 128:160], in_=PS[:, bank, 128:160]), reads=[RPS[bank]], writes=[Rca[ci_]])
                OUT_EVS.append(S.dma("sp", o_mla[pi, l, tt * 128:(tt + 1) * 128, :], CA[:, ci_, :], reads=[Rca[ci_]], key=Reg("o_mla_d%d" % ci_)))

        def branch_ret(l, t0, T, kind):
            sample = kind == "sample"
            n = T // 128
            o = BS0
            WRb, o = view(o, [128, 8, 512], BF16)
            SBst, o = view(o, [128, 16, 256], BF16)
            DM, o = view(o, [128, 4, 128], BF16)
            QD, o = view(o, [128, 2, 4, 128], BF16)
            CD, o = view(o, [128, 2, 256])
            LG, o = view(o, [128, 8])
            KD, o = view(o, [128, 2, 4])
            SF, o = view(o, [128, 256])
            SB, o = view(o, [128, 256])
            SFb, o = view(o, [128, 256], BF16)
            oT = o
            WRa, _ = view(WIN0, [128, 8, 512], BF16)
            Rwr, Rsbst, Rtab, Rsf, Rsb, Rsfb = Reg("r_WR"), Reg("r_SBst"), Reg("r_TAB"), Reg("r_SF"), Reg("r_SB"), Reg("r_SFb")
            winv = w_in[l].rearrange("(kc p) n -> p kc n", p=128)
            S.dma("pool", WRa[:, :, :], winv[:, :, 256:768], writes=[Rwr])
            S.dma("pool", WRb[:, :, :], winv[:, :, 768:1280], writes=[Rwr])
            CT6, o2 = view(oT, [128, 6, 128])
            E1, o2 = view(o2, [128, 2, 128])
            PIDX, o2 = view(o2, [128, 2])
            C128, o2 = view(o2, [128, 64])
            Rc6, Re1 = Reg("r_C6"), Reg("r_E1")
            S.dma("sp", CT6[:, :, :], c_ret.rearrange("k p i -> p k i"), writes=[Rc6])
            S.dma("sp", PIDX[:, :], c_pidx[:, :], writes=[Rc6])
            S.dma("sp", LG[:, :], ret_decay[l:l + 1].rearrange("o d h -> o (d h)").broadcast_to([128, 8]), writes=[Rtab])
            S.op("dve", lambda: nc.vector.memset(C128[:, :], 128.0), writes=[Rc6])
            S.op("act", lambda: nc.scalar.activation(out=LG[:, :], in_=LG[:, :], func=AF.Sigmoid), reads=[Rtab], writes=[Rtab])
            S.op("act", lambda: nc.scalar.activation(out=LG[:, :], in_=LG[:, :], func=AF.Ln), reads=[Rtab], writes=[Rtab])
            for h in range(4):
                S.op("act", lambda h=h: nc.scalar.activation(out=E1[:, 0, :], in_=CT6[:, 0, :], func=AF.Exp, scale=LG[:, h:h + 1]), reads=[Rc6, Rtab], writes=[Re1])
                S.op("act", lambda h=h: nc.scalar.activation(out=E1[:, 1, :], in_=CT6[:, 1, :], func=AF.Exp, scale=LG[:, 4 + h:5 + h]), reads=[Rc6, Rtab], writes=[Re1])
                S.op("dve", lambda h=h: nc.vector.tensor_tensor(out=E1[:, :, :], in0=E1[:, :, :], in1=CT6[:, 2:4, :], op=ALU.mult), reads=[Re1, Rc6], writes=[Re1])
                S.op("dve", lambda h=h: nc.vector.tensor_tensor(out=DM[:, h, :], in0=E1[:, 0, :], in1=E1[:, 1, :], op=ALU.add), reads=[Re1], writes=[Rtab])
                for d in range(2):
                    S.op("act", lambda h=h, d=d: nc.scalar.activation(out=QD[:, d, h, :], in_=CT6[:, 4 + d, :], func=AF.Exp, scale=LG[:, d * 4 + h:d * 4 + h + 1]),
                         reads=[Rc6, Rtab], writes=[Rtab])
                    S.op("act", lambda h=h, d=d: nc.scalar.activation(out=CD[:, d, h * 64:(h + 1) * 64], in_=C128[:, :], func=AF.Exp, scale=LG[:, d * 4 + h:d * 4 + h + 1]),
                         reads=[Rc6, Rtab], writes=[Rtab])
                    S.op("act", lambda h=h, d=d: nc.scalar.activation(out=KD[:, d, h:h + 1], in_=PIDX[:, d:d + 1], func=AF.Exp, scale=LG[:, d * 4 + h:d * 4 + h + 1]),
                         reads=[Rc6, Rtab], writes=[Rtab])
            S.op("dve", lambda: nc.vector.tensor_scalar(out=KD[:, :, :], in0=KD[:, :, :], scalar1=0.125, scalar2=None, op0=ALU.mult), reads=[Rtab], writes=[Rtab])
            if sample:
                S.dma("sp", SF[0:64, :].rearrange("d (h e) -> d h e", h=4), st_ret[l, 0].rearrange("h d e -> d h e"), writes=[Rsf])
                S.dma("sp", SB[0:64, :].rearrange("d (h e) -> d h e", h=4), st_ret[l, 1].rearrange("h d e -> d h e"), writes=[Rsb])
            else:
                S.op("dve", lambda: nc.vector.memset(SF[0:64, :], 0.0), writes=[Rsf])
                S.op("dve", lambda: nc.vector.memset(SB[0:64, :], 0.0), writes=[Rsb])
            S.barrier()
            if cfg.get("ret_stop") == "tables":
                return
            o3 = oT
            QK, o3 = view(o3, [128, 512], BF16)
            TA, o3 = view(o3, [128, 256])
            TBt, o3 = view(o3, [128, 256])
            RT, o3 = view(o3, [128, 2, 32])
            KDt, o3 = view(o3, [128, 256], BF16)
            VTc, o3 = view(o3, [128, 256], BF16)
            SRG, o3 = view(o3, [128, 256], BF16)
            QT, o3 = view(o3, [128, 3, 512], BF16)
            KT, o3 = view(o3, [128, 512], BF16)
            AM, o3 = view(o3, [128, 512], BF16)
            CEN, o3 = view(o3, [128, 256])
            SQr, o3 = view(o3, [128, 256])
            NRo, o3 = view(o3, [128, 256], BF16)
            MS, o3 = view(o3, [128, 8])
            Rqk, Rta, Rrt, Rkd, Rvt, Rsrg, Rqt, Rkt, Ram, Rcen, Rsq, Rnro, Rms = (Reg("r_" + x) for x in
                ("QK", "TA", "RT", "KDt", "VTc", "SRG", "QT", "KT", "AM", "CEN", "SQ", "NRo", "MS"))
            PSb = lambda bank: PS[:, bank, :].bitcast(BF16)

            def proj(c, bank, WRx, c0, ncol):
                def emit():
                    inst = None
                    for kc in range(8):
                        inst = nc.tensor.matmul(PS[:, bank, 0:ncol], lhsT=HT[:, kc, c * 128:(c + 1) * 128], rhs=WRx[:, kc, c0:c0 + ncol], start=(kc == 0), stop=(kc == 7))
                    return inst
                S.op("pe", emit, reads=[Rwr, Rht], writes=[RPS[bank]])

            def rope(c, bank, col0, ng, dst):
                src = PS[:, bank, col0:col0 + ng * 64].rearrange("p (g t e) -> p g t e", g=ng, t=2)
                dv = dst.rearrange("p (g t e) -> p g t e", g=ng, t=2)
                if not sample:
                    S.op("act", lambda: nc.scalar.copy(out=dst, in_=PS[:, bank, col0:col0 + ng * 64]), reads=[RPS[bank]], writes=[Rqk])
                    return
                S.dma("sp", RT[:, 0, :], c_rope_ret[0, c * 128:(c + 1) * 128, :], writes=[Rrt])
                S.dma("sp", RT[:, 1, :], c_rope_ret[1, c * 128:(c + 1) * 128, :], writes=[Rrt])
                cosb = RT[:, 0, :].unsqueeze(1).broadcast_to([128, ng, 32])
                sinb = RT[:, 1, :].unsqueeze(1).broadcast_to([128, ng, 32])
                ta = TA[:, 0:ng * 32].rearrange("p (g e) -> p g e", g=ng)
                tb = TBt[:, 0:ng * 32].rearrange("p (g e) -> p g e", g=ng)
                S.op("dve", lambda: nc.vector.tensor_tensor(out=ta, in0=src[:, :, 0, :], in1=cosb, op=ALU.mult), reads=[RPS[bank], Rrt], writes=[Rta])
                S.op("dve", lambda: nc.vector.tensor_tensor(out=tb, in0=src[:, :, 1, :], in1=sinb, op=ALU.mult), reads=[RPS[bank], Rrt], writes=[Rta])
                S.op("dve", lambda: nc.vector.tensor_tensor(out=dv[:, :, 0, :], in0=ta, in1=tb, op=ALU.subtract), reads=[Rta], writes=[Rqk])
                S.op("dve", lambda: nc.vector.tensor_tensor(out=ta, in0=src[:, :, 0, :], in1=sinb, op=ALU.mult), reads=[RPS[bank], Rrt], writes=[Rta])
                S.op("dve", lambda: nc.vector.tensor_tensor(out=tb, in0=src[:, :, 1, :], in1=cosb, op=ALU.mult), reads=[RPS[bank], Rrt], writes=[Rta])
                S.op("dve", lambda: nc.vector.tensor_tensor(out=dv[:, :, 1, :], in0=ta, in1=tb, op=ALU.add), reads=[Rta], writes=[Rqk])

            def kdec_mul(d, ksrc):
                S.op("dve", lambda: nc.vector.tensor_tensor(out=KDt[:, :].rearrange("p (h e) -> p h e", h=4), in0=ksrc.rearrange("p (h e) -> p h e", h=4),
                                                            in1=KD[:, d, :].unsqueeze(2).broadcast_to([128, 4, 64]), op=ALU.mult), reads=[Rqk, Rtab], writes=[Rkd])

            def umat(bank):
                def emit():
                    inst = None
                    for h in range(4):
                        inst = nc.tensor.matmul(PS[0:64, bank, h * 64:(h + 1) * 64], lhsT=KDt[:, h * 64:(h + 1) * 64], rhs=VTc[:, h * 64:(h + 1) * 64], start=True, stop=True)
                    return inst
                S.op("pe", emit, reads=[Rkd, Rvt], writes=[RPS[bank]])

            def state_update(St, Rst, d, bank):
                S.op("dve", lambda: nc.vector.tensor_tensor(out=St[0:64, :], in0=St[0:64, :], in1=CD[0:64, d, :], op=ALU.mult), reads=[Rst, Rtab], writes=[Rst])
                S.op("dve", lambda: nc.vector.tensor_tensor(out=St[0:64, :], in0=St[0:64, :], in1=PS[0:64, bank, 0:256], op=ALU.add), reads=[Rst, RPS[bank]], writes=[Rst])

            for c in range(n - 1, -1, -1):
                proj(c, 0, WRa, 256, 256)
                proj(c, 1, WRb, 0, 256)
                rope(c, 0, 0, 4, QK[:, 0:256])
                S.op("act", lambda: nc.scalar.copy(out=VTc[:, :], in_=PS[:, 1, 0:256]), reads=[RPS[1]], writes=[Rvt])
                kdec_mul(1, QK[:, 0:256])
                umat(5)
                S.op("act", lambda c=c: nc.scalar.copy(out=SBst[0:64, c, :], in_=SB[0:64, :]), reads=[Rsb], writes=[Rsbst])
                state_update(SB, Rsb, 1, 5)
            S.op("act", lambda: nc.scalar.copy(out=SFb[0:64, :], in_=SF[0:64, :]), reads=[Rsf], writes=[Rsfb])
            if cfg.get("ret_stop") == "pass1":
                return
            for c in range(n):
                proj(c, 0, WRa, 0, 512)
                proj(c, 1, WRb, 0, 512)
                rope(c, 0, 0, 8, QK[:, :])
                S.op("act", lambda: nc.scalar.copy(out=VTc[:, :], in_=PS[:, 1, 0:256]), reads=[RPS[1]], writes=[Rvt])
                S.op("act", lambda: nc.scalar.activation(out=SRG[:, :], in_=PS[:, 1, 256:512], func=AF.Silu), reads=[RPS[1]], writes=[Rsrg])
                kdec_mul(0, QK[:, 256:512])

                def emit_t():
                    inst = None
                    for g in range(8):
                        inst = nc.tensor.transpose(out=PSb(2)[0:64, g * 128:(g + 1) * 128], in_=QK[:, g * 64:(g + 1) * 64], identity=identb[:])
                    return inst
                S.op("pe", emit_t, reads=[Rqk, Rid], writes=[RPS[2]])
                S.op("dve", lambda: nc.vector.tensor_copy(out=QT[0:64, 0, :], in_=PSb(2)[0:64, 0:512]), reads=[RPS[2]], writes=[Rqt])
                for d in range(2):
                    S.op("dve", lambda d=d: nc.vector.tensor_tensor(out=QT[0:64, 1 + d, :], in0=PSb(2)[0:64, 0:512], in1=QD[0:64, d, :, :].rearrange("p h i -> p (h i)"), op=ALU.mult),
                         reads=[RPS[2], Rtab], writes=[Rqt])
                S.op("dve", lambda: nc.vector.tensor_copy(out=KT[0:64, :], in_=PSb(2)[0:64, 512:1024]), reads=[RPS[2]], writes=[Rkt])
                if cfg.get("ret_stop") == "p2a":
                    continue

                def emit_a():
                    inst = None
                    for h in range(4):
                        inst = nc.tensor.matmul(PS[:, 3, h * 128:(h + 1) * 128], lhsT=KT[0:64, h * 128:(h + 1) * 128], rhs=QT[0:64, 0, h * 128:(h + 1) * 128], start=True, stop=True)
                    return inst
                S.op("pe", emit_a, reads=[Rkt, Rqt], writes=[RPS[3]])
                S.op("dve", lambda: nc.vector.tensor_tensor(out=AM[:, :], in0=PS[:, 3, :], in1=DM[:, :, :].rearrange("p h i -> p (h i)"), op=ALU.mult),
                     reads=[RPS[3], Rtab], writes=[Ram])
                if cfg.get("ret_stop") == "p2b":
                    continue

                def emit_o(c=c):
                    inst = None
                    for h in range(4):
                        oc = PS[:, 4, h * 64:(h + 1) * 64]
                        nc.tensor.matmul(oc, lhsT=AM[:, h * 128:(h + 1) * 128], rhs=VTc[:, h * 64:(h + 1) * 64], start=True, stop=False)
                        nc.tensor.matmul(oc, lhsT=QT[0:64, 1, h * 128:(h + 1) * 128], rhs=SFb[0:64, h * 64:(h + 1) * 64], start=False, stop=False)
                        inst = nc.tensor.matmul(oc, lhsT=QT[0:64, 2, h * 128:(h + 1) * 128], rhs=SBst[0:64, c, h * 64:(h + 1) * 64], start=False, stop=True)
                    return inst
                S.op("pe", emit_o, reads=[Ram, Rvt, Rqt, Rsfb, Rsbst], writes=[RPS[4]])
                umat(5)
                state_update(SF, Rsf, 0, 5)
                S.op("act", lambda: nc.scalar.copy(out=SFb[0:64, :], in_=SF[0:64, :]), reads=[Rsf], writes=[Rsfb])
                if cfg.get("ret_stop") == "p2c":
                    continue
                ov = PS[:, 4, 0:256].rearrange("p (h e) -> p h e", h=4)
                S.op("dve", lambda: nc.vector.tensor_reduce(out=MS[:, 0:4], in_=ov, axis=AX.X, op=ALU.add), reads=[RPS[4]], writes=[Rms])
                S.op("dve", lambda: nc.vector.tensor_scalar(out=MS[:, 0:4], in0=MS[:, 0:4], scalar1=-1.0 / 64, scalar2=None, op0=ALU.mult), reads=[Rms], writes=[Rms])
                cv = CEN[:, :].rearrange("p (h e) -> p h e", h=4)
                S.op("dve", lambda: nc.vector.tensor_tensor(out=cv, in0=ov, in1=MS[:, 0:4].unsqueeze(2).broadcast_to([128, 4, 64]), op=ALU.add),
                     reads=[RPS[4], Rms], writes=[Rcen])
                S.op("dve", lambda: nc.vector.tensor_tensor(out=SQr[:, :], in0=CEN[:, :], in1=CEN[:, :], op=ALU.mult), reads=[Rcen], writes=[Rsq])
                S.op("dve", lambda: nc.vector.tensor_reduce(out=MS[:, 4:8], in_=SQr[:, :].rearrange("p (h e) -> p h e", h=4), axis=AX.X, op=ALU.add), reads=[Rsq], writes=[Rms])
                S.op("act", lambda: nc.scalar.activation(out=MS[:, 4:8], in_=MS[:, 4:8], func=AF.Sqrt, scale=1.0 / 64, bias=epsb[:, 0:1]), reads=[Rms, Reps], writes=[Rms])
                S.op("dve", lambda: nc.vector.reciprocal(out=MS[:, 4:8], in_=MS[:, 4:8]), reads=[Rms], writes=[Rms])
                S.op("dve", lambda: nc.vector.tensor_tensor(out=cv, in0=cv, in1=MS[:, 4:8].unsqueeze(2).broadcast_to([128, 4, 64]), op=ALU.mult), reads=[Rcen, Rms], writes=[Rcen])
                S.op("dve", lambda: nc.vector.tensor_tensor(out=NRo[:, :], in0=CEN[:, :], in1=SRG[:, :], op=ALU.mult), reads=[Rcen, Rsrg], writes=[Rnro])

                if cfg.get("ret_stop") == "p2d":
                    continue

                def emit_t2():
                    inst = None
                    for cc in range(2):
                        inst = nc.tensor.transpose(out=PSb(6)[:, cc * 128:(cc + 1) * 128], in_=NRo[:, cc * 128:(cc + 1) * 128], identity=identb[:])
                    return inst
                S.op("pe", emit_t2, reads=[Rnro, Rid], writes=[RPS[6]])
                for cc in range(2):
                    S.op("dve", lambda cc=cc, c=c: nc.vector.tensor_scalar(out=BR[:, 2 + cc, c * 128:(c + 1) * 128], in0=PSb(6)[:, cc * 128:(cc + 1) * 128],
                                                                          scalar1=PV[:, 8 + cc:9 + cc], scalar2=None, op0=ALU.mult), reads=[RPS[6], Rpv], writes=[Rbr[2 + cc]])
            if not sample:
                pi = 0 if kind == "pA" else 1
                OUT_EVS.append(S.dma("sp", o_ret[pi, l, 0].rearrange("h d e -> d h e"), SF[0:64, :].rearrange("d (h e) -> d h e", h=4), reads=[Rsf], key=Reg("o_ret_d")))
                OUT_EVS.append(S.dma("sp", o_ret[pi, l, 1].rearrange("h d e -> d h e"), SB[0:64, :].rearrange("d (h e) -> d h e", h=4), reads=[Rsb], key=Reg("o_ret_d")))

        def branch_s5(l, t0, T, NB, BW, kind):
            sample = kind == "sample"
            o = BS0
            UT, o = view(o, [128, 2, 2048], BF16)
            YS, o = view(o, [128, 2, 2048], BF16)
            YFp, o = view(o, [128, 2048], BF16)
            oZ = o
            TRI, o = view(o, [128, 2, 512])
            TRIb, o = view(o, [128, 2, 512], BF16)
            BZ, o = view(o, [128, 2, 512], BF16)
            oTT = o
            TT, o = view(o, [128, 2, 1024], BF16)
            TTf, _ = view(oTT, [128, 2, 512])
            oS = o
            ow = WIN0
            SBb, ow = view(ow, [128, 2, 512], BF16)
            OTs, ow = view(ow, [128, 512], BF16)
            BW_, ow = view(ow, [128, 2, 2, 128], BF16)
            BBR, ow = view(ow, [128, 16, 16])
            BBI, ow = view(ow, [128, 16, 16])
            CW, ow = view(ow, [128, 8, 2, 32], BF16)
            WGL, ow = view(ow, [128, 2, 256], BF16)
            YV, _ = view(WIN0, [128, 512])
            Rut, Rys, Rsfs, Rtri, Rbz, Rtt, Rsbb, Rots, Rrb, Rbw = (Reg("s_" + x) for x in ("UT", "YS", "YFp", "TRI", "BZ", "TT", "SBb", "OTs", "RB", "BW"))
            def sm_(shape, dt=F32):
                nonlocal o
                v, o = view(o, shape, dt)
                return v
            o1 = [oTT]

            def ot_(shape, dt=F32):
                v, o1[0] = view(o1[0], shape, dt)
                return v
            LRE, LIM, LDT, AR, AI, FR, FI = (ot_([128, 16]) for _ in range(7))
            BRE, BIM = ot_([128, 16, 16]), ot_([128, 16, 16])
            CNAT = ot_([128, 2, 64])
            MAG, UR, UI, W1, W2_, W3 = (sm_([128, 16]) for _ in range(6))
            UBR, UBI = sm_([128, 16]), sm_([128, 16])
            TA_, TBs = sm_([128, 2, 16, 16]), sm_([128, 2, 16, 32])
            PW = sm_([128, 2, 16])
            S0t = sm_([128, 16, 2])
            INI = sm_([128, 2, 2])
            FIN = sm_([128, 16, 2])
            WP = sm_([128, 128])
            Rsu = Reg("s_setup")
            Rini, Rfin, Rwp, Rcn = Reg("s_INI"), Reg("s_FIN"), Reg("s_WP"), Reg("s_CN")
            Rbu = Reg("s_BU")
            V = nc.vector
            dbgon = cfg.get("s5dbg") == kind
            if dbgon:
                dbg2 = nc.dram_tensor("dbg2", [128, 4096], F32, kind="ExternalOutput").ap()

            def dbg(ap, c0, n, regs):
                if dbgon:
                    S.dma("sp", dbg2[:, c0:c0 + n], ap, reads=regs, key=Reg("dbg2"))

            def dv(fn, reads, writes):
                S.op("dve", fn, reads=reads, writes=writes)

            def tt(out, a, b, op, reads=(Rsu,), writes=(Rsu,)):
                dv(lambda: V.tensor_tensor(out=out, in0=a, in1=b, op=op), list(reads), list(writes))

            def cmul(orr, oi, ar, ai, br, bi, t1, t2, reads=(Rsu,), writes=(Rsu,)):
                tt(t1, ar, br, ALU.mult, reads, writes)
                tt(t2, ai, bi, ALU.mult, reads, writes)
                tt(t2, t1, t2, ALU.subtract, reads, writes)
                tt(t1, ar, bi, ALU.mult, reads, writes)
                tt(oi, ai, br, ALU.mult, reads, writes)
                tt(oi, t1, oi, ALU.add, reads, writes)
                tt(orr, t2, t2, ALU.max, reads, writes)

            for cc in range(2):
                proj_fm(l, cc * 128, 128, NB, BW, lambda b, bank, cc=cc: S.op(
                    "act", lambda: nc.scalar.copy(out=UT[:, cc, b * BW:(b + 1) * BW], in_=PS[:, bank, 0:BW]), reads=[RPS[bank]], writes=[Rut]))
            S.barrier()
            for d in range(2):
                for dst, src in ((LRE, s5_lam_re), (LIM, s5_lam_im)):
                    S.dma("sp", dst[:, d::2], src[l, d].rearrange("(m g) p -> (g p) m", g=2), writes=[Rsu], slow=True)
                for g2 in range(2):
                    S.dma("sp", LDT[g2 * 64:(g2 + 1) * 64, d::2], s5_log_dt[l, d:d + 1, g2::2].broadcast_to([64, 8]), writes=[Rsu], slow=True)
                for dst, src in ((BRE, s5_b_re), (BIM, s5_b_im)):
                    S.dma("sp", dst[:, d::2, :], src[l, d].rearrange("(m g) p h -> (g p) m h", g=2), writes=[Rsu])
                if sample:
                    S.dma("sp", S0t[:, d::2, :], st_s5[l, d].rearrange("(m g) p r -> (g p) m r", g=2), writes=[Rsu], slow=True)
            S.dma("pool", WGL[:, :, :], s5_w_glu[l].rearrange("(kc p) n -> p kc n", p=128), writes=[Rsu])
            S.op("act", lambda: nc.scalar.activation(out=LDT[:, :], in_=LDT[:, :], func=AF.Exp), reads=[Rsu], writes=[Rsu])
            tt(W1[:, :], LRE[:, :], LDT[:, :], ALU.mult)
            S.op("act", lambda: nc.scalar.activation(out=MAG[:, :], in_=W1[:, :], func=AF.Exp), reads=[Rsu], writes=[Rsu])
            tt(W1[:, :], LIM[:, :], LDT[:, :], ALU.mult)
            S.op("act", lambda: nc.scalar.activation(out=UI[:, :], in_=W1[:, :], func=AF.Sin, scale=1.0 / 64), reads=[Rsu], writes=[Rsu])
            S.op("act", lambda: nc.scalar.activation(out=UR[:, :], in_=W1[:, :], func=AF.Sin, scale=1.0 / 64, bias=halfpi[:, 0:1]), reads=[Rsu, Reps], writes=[Rsu])
            for _ in range(6):
                tt(W1[:, :], UR[:, :], UR[:, :], ALU.mult)
                tt(W2_[:, :], UI[:, :], UI[:, :], ALU.mult)
                tt(W3[:, :], UR[:, :], UI[:, :], ALU.mult)
                tt(UR[:, :], W1[:, :], W2_[:, :], ALU.subtract)
                tt(UI[:, :], W3[:, :], W3[:, :], ALU.add)
            tt(AR[:, :], MAG[:, :], UR[:, :], ALU.mult)
            tt(AI[:, :], MAG[:, :], UI[:, :], ALU.mult)
            tt(W1[:, :], LRE[:, :], LRE[:, :], ALU.mult)
            tt(W2_[:, :], LIM[:, :], LIM[:, :], ALU.mult)
            tt(W1[:, :], W1[:, :], W2_[:, :], ALU.add)
            dv(lambda: V.reciprocal(out=W1[:, :], in_=W1[:, :]), [Rsu], [Rsu])
            dv(lambda: V.tensor_scalar(out=W2_[:, :], in0=AR[:, :], scalar1=-1.0, scalar2=None, op0=ALU.add), [Rsu], [Rsu])
            tt(FR[:, :], W2_[:, :], LRE[:, :], ALU.mult)
            tt(W3[:, :], AI[:, :], LIM[:, :], ALU.mult)
            tt(FR[:, :], FR[:, :], W3[:, :], ALU.add)
            tt(FR[:, :], FR[:, :], W1[:, :], ALU.mult)
            tt(FI[:, :], AI[:, :], LRE[:, :], ALU.mult)
            tt(W3[:, :], W2_[:, :], LIM[:, :], ALU.mult)
            tt(FI[:, :], FI[:, :], W3[:, :], ALU.subtract)
            tt(FI[:, :], FI[:, :], W1[:, :], ALU.mult)
            dbg(MAG[:, :], 0, 16, [Rsu]); dbg(UR[:, :], 16, 16, [Rsu]); dbg(UI[:, :], 32, 16, [Rsu]); dbg(FR[:, :], 48, 16, [Rsu]); dbg(FI[:, :], 64, 16, [Rsu])
            frb = FR[:, :].unsqueeze(2).broadcast_to([128, 16, 16])
            fib = FI[:, :].unsqueeze(2).broadcast_to([128, 16, 16])
            tt(BBR[:, :, :], BRE[:, :, :], frb, ALU.mult)
            tt(BBI[:, :, :], BIM[:, :, :], fib, ALU.mult)
            tt(BBR[:, :, :], BBR[:, :, :], BBI[:, :, :], ALU.subtract)
            tt(BBI[:, :, :], BRE[:, :, :], fib, ALU.mult)
            tt(BRE[:, :, :], BIM[:, :, :], frb, ALU.mult)
            tt(BBI[:, :, :], BBI[:, :, :], BRE[:, :, :], ALU.add)
            dv(lambda: V.memset(CW[:, :, :, :], 0.0), [], [Rsu])
            for ri, src in ((0, s5_c_re), (1, s5_c_im)):
                S.dma("sp", CNAT[:, :, :], src[l].rearrange("(c g) h p -> (g h) c p", c=2), writes=[Rcn])
                CNB = TTf[:, 1, 0:64].bitcast(BF16)
                dv(lambda: V.tensor_copy(out=CNB.rearrange("p (c k) -> p c k", c=2), in_=CNAT[:, :, :]), [Rcn, Rtt], [Rtt])
                for c in range(2):
                    for half in range(2):
                        S.op("pe", lambda c=c, half=half: nc.tensor.matmul(PS[half * 64:(half + 1) * 64, 6, c * 128:(c + 1) * 128], lhsT=CNB[:, c * 64:(c + 1) * 64], rhs=identb[:, :],
                                                                           start=True, stop=True), reads=[Rtt, Rid], writes=[RPS[6]])
                ctv = PS[:, 6, 0:256].rearrange("q (m g h) -> q m g h", m=8, g=2)
                sc = 1.0 if ri == 0 else -1.0
                dv(lambda ri=ri, sc=sc: V.tensor_scalar(out=CW[0:64, :, ri, 0:16], in0=ctv[0:64, :, 0, :], scalar1=sc, scalar2=None, op0=ALU.mult), [RPS[6]], [Rsu])
                dv(lambda ri=ri, sc=sc: V.tensor_scalar(out=CW[64:128, :, ri, 16:32], in0=ctv[64:128, :, 1, :], scalar1=sc, scalar2=None, op0=ALU.mult), [RPS[6]], [Rsu])
            S.barrier()
            def build_pows(TAB, nent, base_r, base_i):
                dv(lambda: V.memset(TAB[:, 0, :, 0:1], 1.0), [], [Rsu])
                dv(lambda: V.memset(TAB[:, 1, :, 0:1], 0.0), [], [Rsu])
                tt(PW[:, 0, :], base_r, base_r, ALU.max)
                tt(PW[:, 1, :], base_i, base_i, ALU.max)
                nn = 1
                while nn < nent:
                    pr = PW[:, 0, :].unsqueeze(2).broadcast_to([128, 16, nn])
                    pi_ = PW[:, 1, :].unsqueeze(2).broadcast_to([128, 16, nn])
                    t1 = TTf[:, 0, 0:16 * nn].rearrange("p (k j) -> p k j", k=16)
                    t2 = TTf[:, 1, 0:16 * nn].rearrange("p (k j) -> p k j", k=16)
                    rr, ri = Rsu, Rtt
                    tt(t1, TAB[:, 0, :, 0:nn], pr, ALU.mult, (rr, ri), (ri,))
                    tt(t2, TAB[:, 1, :, 0:nn], pi_, ALU.mult, (rr, ri), (ri,))
                    tt(TAB[:, 0, :, nn:2 * nn], t1, t2, ALU.subtract, (rr, ri), (rr,))
                    tt(t1, TAB[:, 0, :, 0:nn], pi_, ALU.mult, (rr, ri), (ri,))
                    tt(t2, TAB[:, 1, :, 0:nn], pr, ALU.mult, (rr, ri), (ri,))
                    tt(TAB[:, 1, :, nn:2 * nn], t1, t2, ALU.add, (rr, ri), (rr,))
                    tt(W1[:, :], PW[:, 0, :], PW[:, 0, :], ALU.mult)
                    tt(W2_[:, :], PW[:, 1, :], PW[:, 1, :], ALU.mult)
                    tt(W3[:, :], PW[:, 0, :], PW[:, 1, :], ALU.mult)
                    tt(PW[:, 0, :], W1[:, :], W2_[:, :], ALU.subtract)
                    tt(PW[:, 1, :], W3[:, :], W3[:, :], ALU.add)
                    nn *= 2
            build_pows(TBs, 32, UR[:, :], UI[:, :])
            tt(W1[:, :], PW[:, 0, :], PW[:, 0, :], ALU.max)
            tt(W2_[:, :], PW[:, 1, :], PW[:, 1, :], ALU.max)
            tt(UBR[:, :], PW[:, 0, :], PW[:, 0, :], ALU.max)
            tt(UBI[:, :], PW[:, 1, :], PW[:, 1, :], ALU.max)
            build_pows(TA_, 16, UBR[:, :], UBI[:, :])
            if BW == 512:
                tt(UBR[:, :], PW[:, 0, :], PW[:, 0, :], ALU.max)
                tt(UBI[:, :], PW[:, 1, :], PW[:, 1, :], ALU.max)
            else:
                tt(UBR[:, :], TA_[:, 0, :, 8], TA_[:, 0, :, 8], ALU.max)
                tt(UBI[:, :], TA_[:, 1, :, 8], TA_[:, 1, :, 8], ALU.max)
            S.barrier()
            mcb = [0]
            for m in range(8):
                cc, m4 = m // 4, m % 4
                for d in range(2):
                    k = m * 2 + d
                    for ri, BB in ((0, BBR), (1, BBI)):
                        dv(lambda: V.memset(WP[:, :], 0.0), [Rwp], [Rwp])
                        dv(lambda BB=BB, k=k: V.tensor_copy(out=WP[0:64, m4 * 32:m4 * 32 + 16], in_=BB[0:64, k, :]), [Rsu, Rwp], [Rwp])
                        dv(lambda BB=BB, k=k: V.tensor_copy(out=WP[64:128, m4 * 32 + 16:m4 * 32 + 32], in_=BB[64:128, k, :]), [Rsu, Rwp], [Rwp])
                        S.op("pe", lambda: nc.tensor.transpose(out=PS[:, 7, 0:128], in_=WP[:, :], identity=ident[:]), reads=[Rwp, Rid], writes=[RPS[7]])
                        S.op("act", lambda d=d, ri=ri: nc.scalar.copy(out=BW_[:, d, ri, :], in_=PS[:, 7, 0:128]), reads=[RPS[7]], writes=[Rbw])
                for d in range(2):
                    k = m * 2 + d
                    rev = d == 1
                    ar = TA_[:, 0, k, :].unsqueeze(2).broadcast_to([128, 16, 32])
                    ai = TA_[:, 1, k, :].unsqueeze(2).broadcast_to([128, 16, 32])
                    br = TBs[:, 0, k, :].unsqueeze(1).broadcast_to([128, 16, 32])
                    bi = TBs[:, 1, k, :].unsqueeze(1).broadcast_to([128, 16, 32])
                    trv = TRI[:, 0, :].rearrange("p (q j) -> p q j", q=16)
                    tiv = TRI[:, 1, :].rearrange("p (q j) -> p q j", q=16)
                    t1 = TTf[:, 0, :].rearrange("p (q j) -> p q j", q=16)
                    t2 = TTf[:, 1, :].rearrange("p (q j) -> p q j", q=16)
                    rw = (Rsu, Rtt, Rtri)
                    tt(t1, ar, br, ALU.mult, rw, (Rtt,))
                    tt(t2, ai, bi, ALU.mult, rw, (Rtt,))
                    tt(trv, t1, t2, ALU.subtract, rw, (Rtri,))
                    tt(t1, ar, bi, ALU.mult, rw, (Rtt,))
                    tt(t2, ai, br, ALU.mult, rw, (Rtt,))
                    tt(tiv, t1, t2, ALU.add, rw, (Rtri,))
                    dv(lambda: V.tensor_copy(out=TRIb[:, :, :], in_=TRI[:, :, :]), [Rtri], [Rtri])
                    if k == 0:
                        dbg(TRI[:, 0, :], 128, 512, [Rtri]); dbg(TRI[:, 1, :], 640, 512, [Rtri])
                    ib = 0
                    if sample:
                        cmul(INI[:, 0, ib:ib + 1], INI[:, 1, ib:ib + 1], UR[:, k:k + 1], UI[:, k:k + 1], S0t[:, k, 0:1], S0t[:, k, 1:2], W1[:, 0:1], W2_[:, 0:1], (Rsu, Rini), (Rsu, Rini))
                    else:
                        dv(lambda: V.memset(INI[:, :, 0:1], 0.0), [Rini], [Rini])
                    blocks = list(range(NB - 1, -1, -1)) if rev else list(range(NB))
                    for bi_, b in enumerate(blocks):
                        cols = slice(b * BW, (b + 1) * BW)
                        bk = 2 * (mcb[0] % 2)
                        mcb[0] += 1
                        for ri in range(2):
                            S.op("pe", lambda ri=ri: nc.tensor.matmul(PS[:, bk + ri, 0:BW], lhsT=BW_[:, d, ri, :], rhs=UT[:, cc, cols], start=True, stop=True),
                                 reads=[Rbw, Rut], writes=[RPS[bk + ri]])
                        if rev:
                            trr, tri = TRI[:, 0, BW - 1::-1] if BW == 512 else TRI[:, 0, BW - 1::-1], TRI[:, 1, BW - 1::-1]
                            trr = TRI[:, 0, 0:BW][:, ::-1]
                            tri = TRI[:, 1, 0:BW][:, ::-1]
                            trrb = TRIb[:, 0, 0:BW][:, ::-1]
                            trib = TRIb[:, 1, 0:BW][:, ::-1]
                        else:
                            trr, tri = TRI[:, 0, 0:BW], TRI[:, 1, 0:BW]
                            trrb, trib = TRIb[:, 0, 0:BW], TRIb[:, 1, 0:BW]
                        for ri in range(2):
                            S.op("act", lambda ri=ri: nc.scalar.copy(out=SBb[:, ri, 0:BW], in_=PS[:, bk + ri, 0:BW]), reads=[RPS[bk + ri]], writes=[Rsbb])
                        pre, pim = SBb[:, 0, 0:BW], SBb[:, 1, 0:BW]
                        rr = (Rtri, Rsbb, Rtt, Rbz)
                        tt(TT[:, 0, 0:BW], pre, trrb, ALU.mult, rr, (Rtt,))
                        tt(TT[:, 1, 0:BW], pim, trib, ALU.mult, rr, (Rtt,))
                        tt(BZ[:, 0, 0:BW], TT[:, 0, 0:BW], TT[:, 1, 0:BW], ALU.add, rr, (Rbz,))
                        tt(TT[:, 0, 0:BW], pim, trrb, ALU.mult, rr, (Rtt,))
                        tt(TT[:, 1, 0:BW], pre, trib, ALU.mult, rr, (Rtt,))
                        tt(BZ[:, 1, 0:BW], TT[:, 0, 0:BW], TT[:, 1, 0:BW], ALU.subtract, rr, (Rbz,))
                        if k == 0 and bi_ == 0:
                            dbg(BZ[:, 0, 0:BW], 1152, BW, [Rbz]); dbg(BZ[:, 1, 0:BW], 1664, BW, [Rbz])
                        for ri in range(2):
                            zo = TT[:, ri, 0:BW]
                            zin = BZ[:, ri, 0:BW]
                            if rev:
                                zo, zin = zo[:, ::-1], zin[:, ::-1]
                            dv(lambda zo=zo, zin=zin, ri=ri: V.tensor_tensor_scan(out=zo, data0=MAG[:, k:k + 1].broadcast_to([128, BW]), data1=zin, initial=INI[:, ri, ib:ib + 1], op0=ALU.mult, op1=ALU.add),
                               [Rsu, Rbz, Rini, Rtt], [Rtt])
                        if k == 0 and bi_ == 0:
                            dbg(TT[:, 0, 0:BW], 2176, BW, [Rtt]); dbg(TT[:, 1, 0:BW], 2688, BW, [Rtt])
                        zl = 0 if rev else BW - 1
                        zr_, zi_ = TT[:, 0, zl:zl + 1], TT[:, 1, zl:zl + 1]
                        last_blk = bi_ == NB - 1
                        if not last_blk:
                            ib2 = 1 - ib
                            rc, wc = [Rsu, Rini, Rtt], [Rsu, Rini]
                            dv(lambda: V.tensor_scalar(out=W1[:, 0:1], in0=zi_, scalar1=UBI[:, k:k + 1], scalar2=None, op0=ALU.mult), rc, wc)
                            dv(lambda: V.scalar_tensor_tensor(out=INI[:, 0, ib2:ib2 + 1], in0=zr_, scalar=UBR[:, k:k + 1], in1=W1[:, 0:1], op0=ALU.mult, op1=ALU.subtract), rc, wc)
                            dv(lambda: V.tensor_scalar(out=W2_[:, 0:1], in0=zr_, scalar1=UBI[:, k:k + 1], scalar2=None, op0=ALU.mult), rc, wc)
                            dv(lambda: V.scalar_tensor_tensor(out=INI[:, 1, ib2:ib2 + 1], in0=zi_, scalar=UBR[:, k:k + 1], in1=W2_[:, 0:1], op0=ALU.mult, op1=ALU.add), rc, wc)
                            ib = ib2
                        elif not sample:
                            cmul(FIN[:, k, 0:1], FIN[:, k, 1:2], TRI[:, 0, BW - 1:BW], TRI[:, 1, BW - 1:BW], zr_, zi_, W1[:, 0:1], W2_[:, 0:1], (Rsu, Rtri, Rtt, Rfin), (Rsu, Rfin))
                        dstS = SBb[:, :, 0:BW]
                        Rdst = Rsbb
                        rr2 = (Rtri, Rtt, Rbz)
                        tt(BZ[:, 0, 0:BW], TT[:, 0, 0:BW], trrb, ALU.mult, rr2, (Rbz,))
                        tt(BZ[:, 1, 0:BW], TT[:, 1, 0:BW], trib, ALU.mult, rr2, (Rbz,))
                        tt(dstS[:, 0, :], BZ[:, 0, 0:BW], BZ[:, 1, 0:BW], ALU.subtract, (Rbz, Rdst), (Rdst,))
                        tt(BZ[:, 0, 0:BW], TT[:, 1, 0:BW], trrb, ALU.mult, rr2, (Rbz,))
                        tt(BZ[:, 1, 0:BW], TT[:, 0, 0:BW], trib, ALU.mult, rr2, (Rbz,))
                        tt(dstS[:, 1, :], BZ[:, 0, 0:BW], BZ[:, 1, 0:BW], ALU.add, (Rbz, Rdst), (Rdst,))
                        def emit_y():
                            nc.tensor.matmul(PS[0:32, 4, 0:BW], lhsT=CW[:, m, 0, :], rhs=SBb[:, 0, 0:BW], start=True, stop=False)
                            return nc.tensor.matmul(PS[0:32, 4, 0:BW], lhsT=CW[:, m, 1, :], rhs=SBb[:, 1, 0:BW], start=False, stop=True)
                        S.op("pe", emit_y, reads=[Rsu, Rsbb], writes=[RPS[4]])
                        if not rev:
                            S.op("act", lambda: nc.scalar.copy(out=YFp[0:32, cols], in_=PS[0:32, 4, 0:BW]), reads=[RPS[4]], writes=[Rsfs])
                        else:
                            tt(OTs[0:32, 0:BW], PS[0:32, 4, 0:BW], YFp[0:32, cols], ALU.add, (RPS[4], Rsfs, Rots), (Rots,))
                            S.dma("sp", YS[m4 * 32:(m4 + 1) * 32, cc, cols], OTs[0:32, 0:BW], reads=[Rots], writes=[Rys], key=Reg("s_OTd"))
            if not sample:
                pi = 0 if kind == "pA" else 1
                for d in range(2):
                    OUT_EVS.append(S.dma("sp", o_s5[pi, l, d].rearrange("(m g) p r -> (g p) m r", g=2), FIN[:, d::2, :], reads=[Rfin], key=Reg("o_s5_d"), slow=True))
            S.barrier()
            Z, _ = view(oZ, [128, 2, 2048], BF16)
            Rz_ = Reg("s_Z")
            for b in range(NB):
                cols = slice(b * BW, (b + 1) * BW)
                for cc in range(2):
                    yv_ = YV[:, 0:BW]
                    dv(lambda: V.scalar_tensor_tensor(out=yv_, in0=UT[:, cc, cols], scalar=PV[:, 10 + cc:11 + cc], in1=YS[:, cc, cols], op0=ALU.mult, op1=ALU.add),
                       [Rut, Rpv, Rys, Rbz], [Rbz])
                    tt(TTf[:, 0, 0:BW], yv_, yv_, ALU.mult, (Rbz, Rtt), (Rtt,))
                    dv(lambda: V.tensor_scalar(out=TTf[:, 0, 0:BW], in0=TTf[:, 0, 0:BW], scalar1=0.044715, scalar2=1.0, op0=ALU.mult, op1=ALU.add), [Rtt], [Rtt])
                    tt(TTf[:, 0, 0:BW], TTf[:, 0, 0:BW], yv_, ALU.mult, (Rbz, Rtt), (Rtt,))
                    S.op("act", lambda: nc.scalar.activation(out=TTf[:, 1, 0:BW], in_=TTf[:, 0, 0:BW], func=AF.Sigmoid, scale=1.5957691216057308), reads=[Rtt], writes=[Rtt])
                    tt(Z[:, cc, cols], TTf[:, 1, 0:BW], yv_, ALU.mult, (Rbz, Rtt, Rz_), (Rz_,))
                for co in range(2):
                    def emit_g(co=co):
                        nc.tensor.matmul(PS[:, 5, 0:BW], lhsT=WGL[:, 0, co * 128:(co + 1) * 128], rhs=Z[:, 0, cols], start=True, stop=False)
                        return nc.tensor.matmul(PS[:, 5, 0:BW], lhsT=WGL[:, 1, co * 128:(co + 1) * 128], rhs=Z[:, 1, cols], start=False, stop=True)
                    S.op("pe", emit_g, reads=[Rsu, Rz_], writes=[RPS[5]])
                    S.op("act", lambda: nc.scalar.activation(out=OTs[:, 0:BW], in_=PS[:, 5, 0:BW], func=AF.Sigmoid), reads=[RPS[5]], writes=[Rots])
                    tt(BR[:, co, cols], Z[:, co, cols], OTs[:, 0:BW], ALU.mult, (Rz_, Rots), (Rbr[co],))

        DBG = {}

        def mixer(l):
            make_gate_bcast(1)
            mixer_params(l)
            for (t0, ntile, ci, kind) in SEQS:
                T = ntile * 128
                BW = min(512, T)
                NB = T // BW
                S.barrier()
                cur["XN"], _ = view(BS0, [128, 2, D])
                for tt in range(ntile):
                    prenorm_tile(t0 + tt, ci, 1, HT[:, :, tt * 128:(tt + 1) * 128], Rht, (4 + 2 * (tt % 2), 5 + 2 * (tt % 2)))
                S.barrier()
                if kind != "sample":
                    mla_cache_out(l, t0, ntile, 0 if kind == "pA" else 1)
                    S.barrier()
                zero = []
                for nm, chs in (("s5", (0, 1)), ("ret", (2, 3)), ("conv", (4, 5)), ("mla", (6, 7))):
                    if not cfg.get(nm, True):
                        zero += list(chs)
                for i in zero:
                    S.op("dve", lambda i=i: nc.vector.memset(BR[:, i, 0:T], 0.0), writes=[Rbr[i]])
                if cfg.get("conv", True):
                    branch_conv(l, T, NB, BW)
                    S.barrier()
                if cfg.get("mla", True):
                    branch_mla(l, t0, T, NB, BW, kind)
                    S.barrier()
                if cfg.get("ret", True):
                    branch_ret(l, t0, T, kind)
                    S.barrier()
                if cfg.get("s5", True):
                    branch_s5(l, t0, T, NB, BW, kind)
                    S.barrier()
                if tuple(cfg.get("dump_br", ())) == (l, kind):
                    dbg = nc.dram_tensor("dbg_br", [128, 8, 2048], BF16, kind="ExternalOutput").ap()
                    S.dma("sp", dbg[:, :, 0:T], BR[:, :, 0:T], reads=Rbr, key=Reg("dbg"))
                gate_stage(l, t0, ntile, ci, T, NB, BW)
                S.barrier()
            cur["XN"], cur["TMP"] = XN, TMP

        for l in range(LAYERS):
            S.barrier()
            compute_mod(l)
            S.barrier()
            if cfg.get("ffn1", True):
                ffn(l, 0, 0)
            S.barrier()
            if cfg.get("mixer", True):
                mixer(l)
            S.barrier()
            if cfg.get("ffn2", True):
                ffn(l, 1, 2)

        yv = y.rearrange("(t p) d -> p t d", p=128)
        Ryout = Reg("yout")
        evs = []
        for t in range(NT):
            evs.append(S.dma("sp", yv[:, t, :], X[:, t, :], reads=[RX[t]], key=Ryout))
        S._wait("sp", set([evs[-1]] + OUT_EVS))
        S.barrier()
    return nc


def _axial(T, dim):
    rows = T // 64
    row = np.repeat(np.arange(rows, dtype=np.float32), 64)
    col = np.tile(np.arange(64, dtype=np.float32), rows)
    quarter = dim // 4
    inv = (np.float32(10000.0) ** (-np.arange(quarter, dtype=np.float32) / np.float32(quarter))).astype(np.float32)
    ang = np.concatenate([row[:, None] * inv, col[:, None] * inv], axis=-1).astype(np.float32)
    return np.cos(ang).astype(np.float32), np.sin(ang).astype(np.float32)


def _rope_tables():
    c, s = _axial(2048, 32)
    mla = np.stack([np.concatenate([c.T, c.T], 0), np.concatenate([s.T, s.T], 0)], 0)
    c2, s2 = _axial(2048, 64)
    ret = np.stack([c2, s2], 0)
    return np.ascontiguousarray(mla, dtype=np.float32), np.ascontiguousarray(ret, dtype=np.float32)


def _prep_inputs(inputs):
    f = lambda a: np.ascontiguousarray(np.asarray(a, dtype=np.float32))
    shared = {k: f(inputs[k]) for k in (
        "w_mod", "b_mod", "norm_pre", "norm_post", "ffn_w1", "ffn_w3", "ffn_w2", "w_in", "s5_lam_re", "s5_lam_im", "s5_log_dt",
        "s5_b_re", "s5_b_im", "s5_c_re", "s5_c_im", "s5_d", "s5_w_glu", "ret_decay", "ret_gn", "conv_w", "conv_b",
        "mla_q_norm", "mla_w_uq", "mla_kv_norm", "mla_w_ukv", "w_branch", "w_gate", "b_gate", "w_o")}
    shared["c_ident"] = np.eye(128, dtype=np.float32)
    shared["c_rope_mla"], shared["c_rope_ret"] = _rope_tables()
    jj = np.arange(128, dtype=np.float32)[:, None]
    ii = np.arange(128, dtype=np.float32)[None, :]
    diff = ii - jj
    shared["c_ret"] = np.ascontiguousarray(np.stack([
        np.maximum(diff, 0), np.maximum(-diff, 0), 0.125 * (diff >= 0), 0.125 * (diff < 0),
        np.broadcast_to(ii + 1.0, (128, 128)), np.broadcast_to(128.0 - ii, (128, 128))], 0), dtype=np.float32)
    shared["c_pidx"] = np.ascontiguousarray(np.concatenate([127.0 - jj, jj], 1), dtype=np.float32)
    xp = f(inputs["x_prompt"])
    xs = f(inputs["x_sample"])
    c = f(inputs["c"])
    cctx = f(inputs["c_ctx"])
    maps = []
    for i in range(NCORES):
        m = dict(shared)
        m["xin"] = np.ascontiguousarray(np.concatenate([xs[i], xp[2 * i], xp[2 * i + 1]], axis=0))
        m["cond2"] = np.ascontiguousarray(np.stack([c[i], cctx], axis=0))
        m["st_s5"] = f(inputs["state_s5"][i])
        m["st_ret"] = f(inputs["state_ret"][i])
        m["ctx_mla"] = f(inputs["cache_mla"][i])
        maps.append(m)
    return maps


def _gather(res):
    ys = np.stack([r["y"][:2048] for r in res], axis=0)
    yp = np.stack([r["y"][2048 + 256 * j:2048 + 256 * (j + 1)] for r in res for j in range(2)], axis=0)
    s5 = np.concatenate([r["o_s5"] for r in res], axis=0)
    ret = np.concatenate([r["o_ret"] for r in res], axis=0)
    mla = np.concatenate([r["o_mla"] for r in res], axis=0)
    return (yp.astype(np.float32), ys.astype(np.float32), s5.astype(np.float32), ret.astype(np.float32), mla.astype(np.float32))


CFG = {}


def kernel(**inputs):
    nc = build(CFG)
    maps = _prep_inputs(inputs)
    res = run_bass_kernel_spmd(nc, maps, core_ids=list(range(NCORES)))
    return _gather(res.results)
```

```python
import numpy as np
import concourse.bass as bass
import concourse.mybir as mybir
from concourse.bass_utils import run_bass_kernel_spmd
from contextlib import ExitStack

F32 = mybir.dt.float32
BF16 = mybir.dt.bfloat16
AF = mybir.ActivationFunctionType
ALU = mybir.AluOpType
AX = mybir.AxisListType

D = 1024
DFF = 2816
NFF = 22
TOK = 2560
NT = 20
EPS = 1e-6
NCORES = 8
INC = 2400

SAME_ENG_SYNC = True


_REGS = {}


def Reg(name):
    if name not in _REGS:
        _REGS[name] = _Reg(name)
    return _REGS[name]


class _Reg:
    __slots__ = ("name", "w", "r", "dsem", "dcnt")

    def __init__(self, name):
        self.name = name
        self.w = None
        self.r = []
        self.dsem = None
        self.dcnt = 0


class Sched:
    def __init__(self, nc, es):
        self.nc = nc
        self.es = es
        self.eng = {"pe": nc.tensor, "act": nc.scalar, "dve": nc.vector, "pool": nc.gpsimd, "sp": nc.sync}
        self.sem = {e: es.enter_context(nc.semaphore("s_" + e)) for e in self.eng}
        self.cnt = {e: 0 for e in self.eng}
        self.seen = {e: {} for e in self.eng}
        self.seen_d = {e: {} for e in self.eng}
        self.nsem = 0
        self.out_events = []
        self.pending_reads = {}

    def _wait(self, e, deps, raw=None):
        best = {}
        bestd = {}
        for d in deps:
            if d[0] == "e":
                _, e2, c = d
                if e2 == e and (e == "pe" or not SAME_ENG_SYNC or (raw is not None and d not in raw)):
                    continue
                if c > best.get(e2, 0):
                    best[e2] = c
            else:
                _, sem, tgt, key = d
                if tgt > bestd.get(key, (None, 0))[1]:
                    bestd[key] = (sem, tgt)
        E = self.eng[e]
        for e2, c in best.items():
            if self.seen[e].get(e2, 0) >= c:
                continue
            E.wait_ge(self.sem[e2], c)
            self.seen[e][e2] = c
        for key, (sem, tgt) in bestd.items():
            if self.seen_d[e].get(key, 0) >= tgt:
                continue
            E.wait_ge(sem, tgt)
            self.seen_d[e][key] = tgt

    def _deps(self, reads, writes):
        deps = set()
        for r in reads:
            if r.w is not None:
                deps.add(r.w)
        for w in writes:
            if w.w is not None:
                deps.add(w.w)
            deps.update(w.r)
        return deps

    def op(self, e, emit, reads=(), writes=()):
        raw = set(r.w for r in reads if r.w is not None)
        self._wait(e, self._deps(reads, writes), raw)
        inst = emit()
        self.cnt[e] += 1
        inst.then_inc(self.sem[e], 1)
        ev = ("e", e, self.cnt[e])
        for r in reads:
            r.r.append(ev)
        for w in writes:
            w.w = ev
            w.r = []
        return ev

    def dma(self, e, out, in_, reads=(), writes=(), key=None, slow=False):
        key = key or (list(writes) + list(reads))[0]
        if key.dsem is None:
            key.dsem = self.es.enter_context(self.nc.semaphore("d%d" % self.nsem))
            self.nsem += 1
        deps = set(d for d in self._deps(reads, writes) if not (d[0] == "d" and d[3] == key.name))
        self._wait(e, deps)
        inst = self.eng[e].dma_start(out=out, in_=in_, allow_slow_non_contiguous=True) if slow else self.eng[e].dma_start(out=out, in_=in_)
        key.dcnt += 16
        inst.then_inc(key.dsem, 16)
        ev = ("d", key.dsem, key.dcnt, key.name)
        if reads:
            self.pending_reads[key.name] = ev
        for r in reads:
            r.r.append(ev)
        for w in writes:
            w.w = ev
            w.r = []
        return ev

    def barrier(self):
        pend = set(self.pending_reads.values())
        for e in self.eng:
            deps = set(("e", e2, self.cnt[e2]) for e2 in self.eng if e2 != e and self.cnt[e2] > 0)
            self._wait(e, deps | pend)
        self.pending_reads = {}


def build(cfg):
    _REGS.clear()
    nc = bass.Bass("TRN2", target_bir_lowering=False)
    LAYERS = cfg.get("layers", 2)

    def din(name, shape):
        return nc.dram_tensor(name, list(shape), F32, kind="ExternalInput").ap()

    def dout(name, shape):
        return nc.dram_tensor(name, list(shape), F32, kind="ExternalOutput").ap()

    xin = din("xin", [TOK, D])
    cond2 = din("cond2", [2, D])
    st_s5 = din("st_s5", [2, 2, 16, 64, 2])
    st_ret = din("st_ret", [2, 2, 4, 64, 64])
    ctx_mla = din("ctx_mla", [2, 512, 160])
    w_mod = din("w_mod", [2, D, 9 * D])
    b_mod = din("b_mod", [2, 9 * D])
    norm_pre = din("norm_pre", [2, 3, D])
    norm_post = din("norm_post", [2, 3, D])
    ffn_w1 = din("ffn_w1", [2, 2, D, DFF])
    ffn_w3 = din("ffn_w3", [2, 2, D, DFF])
    ffn_w2 = din("ffn_w2", [2, 2, DFF, D])
    w_in = din("w_in", [2, D, INC])
    s5_lam_re = din("s5_lam_re", [2, 2, 16, 64])
    s5_lam_im = din("s5_lam_im", [2, 2, 16, 64])
    s5_log_dt = din("s5_log_dt", [2, 2, 16])
    s5_b_re = din("s5_b_re", [2, 2, 16, 64, 16])
    s5_b_im = din("s5_b_im", [2, 2, 16, 64, 16])
    s5_c_re = din("s5_c_re", [2, 16, 16, 64])
    s5_c_im = din("s5_c_im", [2, 16, 16, 64])
    s5_d = din("s5_d", [2, 256])
    s5_w_glu = din("s5_w_glu", [2, 256, 256])
    ret_decay = din("ret_decay", [2, 2, 4])
    ret_gn = din("ret_gn", [2, 256])
    conv_w = din("conv_w", [2, 3, 256])
    conv_b = din("conv_b", [2, 256])
    mla_q_norm = din("mla_q_norm", [2, 192])
    mla_w_uq = din("mla_w_uq", [2, 192, 384])
    mla_kv_norm = din("mla_kv_norm", [2, 128])
    mla_w_ukv = din("mla_w_ukv", [2, 128, 512])
    w_branch = din("w_branch", [2, 4, 256, D])
    w_gate = din("w_gate", [2, D, 4 * D])
    b_gate = din("b_gate", [2, 4 * D])
    w_o = din("w_o", [2, D, D])
    c_ident = din("c_ident", [128, 128])
    c_rope_mla = din("c_rope_mla", [2, 32, 2048])
    c_rope_ret = din("c_rope_ret", [2, 2048, 32])
    c_ret = din("c_ret", [6, 128, 128])
    c_pidx = din("c_pidx", [128, 2])

    y = dout("y", [TOK, D])
    o_s5 = dout("o_s5", [2, 2, 2, 16, 64, 2])
    o_ret = dout("o_ret", [2, 2, 2, 4, 64, 64])
    o_mla = dout("o_mla", [2, 2, 256, 160])

    es = ExitStack()
    with es:
        S = Sched(nc, es)

        def sb(name, shape, dt=F32):
            return es.enter_context(nc.sbuf_tensor(name, list(shape), dt))

        X = sb("X", [128, NT, D])
        RX = [Reg("X%d" % t) for t in range(NT)]
        PS = es.enter_context(nc.psum_tensor("PS", [128, 8, 512], F32))
        RPS = [Reg("PS%d" % b) for b in range(8)]
        ident = sb("ident", [128, 128])
        identb = sb("identb", [128, 128], BF16)
        Rid = Reg("ident")
        VEC = sb("VEC", [128, 2, 72])
        Rvec = Reg("VEC")
        NRM = sb("NRM", [128, 48])
        Rnrm = Reg("NRM")
        SV = sb("SV", [128, 2, 3, 8])
        Rsv = Reg("SV")
        GV = sb("GV", [128, 2, 3, 8])
        Rgv = Reg("GV")
        GB = sb("GB", [128, 2, D])
        Rgb = [Reg("GB0"), Reg("GB1")]
        DG = sb("DG", [128, 2, 128])
        Rdg = [Reg("DG0"), Reg("DG1")]
        small = sb("small", [128, 4, 4])
        Rsmall = [Reg("sm%d" % i) for i in range(4)]
        junk = sb("junk", [128, D], BF16)
        Rjunk = Reg("junk")
        ACOLS = 28672
        ARENA = sb("ARENA", [128, ACOLS])

        def view(off, shape, dt=F32):
            n = int(np.prod(shape[1:]))
            nbytes = n * (2 if dt == BF16 else 4)
            assert off % 4 == 0 and nbytes % 4 == 0 and off + nbytes <= ACOLS * 4, (off, shape)
            ap = ARENA[:, off // 4:(off + nbytes) // 4]
            if dt == BF16:
                ap = ap.bitcast(BF16)
            if len(shape) > 2:
                names = "abcdef"[:len(shape) - 1]
                pat = "p (" + " ".join(names) + ") -> p " + " ".join(names)
                ap = ap.rearrange(pat, **{names[i]: shape[i + 1] for i in range(len(names) - 1)})
            return ap, off + nbytes

        HTB, _o = view(0, [128, 2, 8, 512], BF16)
        GT, _o = view(_o, [128, NFF, 512], BF16)
        W13, _o = view(_o, [128, 3, 2, 8, 256], BF16)
        W2, _o = view(_o, [128, 3, 2, D], BF16)
        SIL, _o = view(_o, [128, 2, 512], BF16)
        XN, _o = view(_o, [128, 2, D])
        TMP, _o = view(_o, [128, 2, D])
        WM, _ = view(0, [128, 2, 8, 512], BF16)
        Rxn = [Reg("XN0"), Reg("XN1")]

        xv = xin.rearrange("(t p) d -> p t d", p=128)
        Rxall = Reg("xall")
        for t in range(NT):
            S.dma("sp", X[:, t, :], xv[:, t, :], writes=[RX[t]], key=Rxall)
        for t in range(NT):
            RX[t].w = ("d", Rxall.dsem, Rxall.dcnt, Rxall.name)
        S.dma("sp", ident[:], c_ident[:, :], writes=[Rid])
        S.op("dve", lambda: nc.vector.tensor_copy(out=identb[:], in_=ident[:]), reads=[Rid], writes=[Rid])

        rot = {"ps": 0, "sm": 0, "xn": 0, "dg": 0}
        OUT_EVS = []

        stage = sb("stage", [128, 128])
        Rstage = Reg("stage")

        def load_T(dst_ap, src_ap, rows, dst_reg, bank=7):
            S.dma("sp", stage[0:rows, :], src_ap, writes=[Rstage])
            S.op("pe", lambda: nc.tensor.transpose(out=PS[:, bank, 0:rows], in_=stage[0:rows, :], identity=ident[0:rows, 0:rows]),
                 reads=[Rstage, Rid], writes=[RPS[bank]])
            S.op("dve", lambda: nc.vector.tensor_copy(out=dst_ap, in_=PS[:, bank, 0:rows]), reads=[RPS[bank]], writes=[dst_reg])

        SCT = sb("SCT", [128, 8, 2], BF16)
        Rsct = Reg("SCT")
        sct32 = sb("sct32", [128, 16])
        load_T(sct32[:, :], cond2.rearrange("c (k p) -> (c k) p", p=128), 16, Rsct)
        S.op("act", lambda: nc.scalar.activation(out=SCT[:].rearrange("p k c -> p c k"), in_=sct32[:].rearrange("p (c k) -> p c k", c=2), func=AF.Silu),
             reads=[Rsct], writes=[Rsct])

        Rwm = [Reg("WM0"), Reg("WM1")]
        BM = sb("BM", [128, 72])
        Rbm = Reg("BM")

        def compute_mod(l):
            load_T(BM[:, :], b_mod[l].rearrange("(c p) -> c p", p=128), 72, Rbm)
            load_T(NRM[:, 0:24], norm_pre[l].rearrange("s (c p) -> (s c) p", p=128), 24, Rnrm)
            load_T(NRM[:, 24:48], norm_post[l].rearrange("s (c p) -> (s c) p", p=128), 24, Rnrm)
            wv = w_mod[l].rearrange("(kc p) n -> p kc n", p=128)
            for cb in range(18):
                sl = cb % 2
                S.dma("pool", WM[:, sl, :, :], wv[:, :, cb * 512:(cb + 1) * 512], writes=[Rwm[sl]])
                bank = 6

                def emit(cb=cb, sl=sl):
                    inst = None
                    for cc in range(4):
                        for kc in range(8):
                            inst = nc.tensor.matmul(PS[:, bank, cc * 2:cc * 2 + 2], lhsT=WM[:, sl, kc, cc * 128:(cc + 1) * 128],
                                                    rhs=SCT[:, kc, :], start=(kc == 0), stop=(kc == 7))
                    return inst
                S.op("pe", emit, reads=[Rwm[sl], Rsct], writes=[RPS[bank]])
                S.op("dve", lambda cb=cb: nc.vector.tensor_tensor(
                    out=VEC[:, :, cb * 4:(cb + 1) * 4].rearrange("p c j -> p j c"),
                    in0=PS[:, bank, 0:8].rearrange("p (j c) -> p j c", c=2),
                    in1=BM[:, cb * 4:(cb + 1) * 4].unsqueeze(2).broadcast_to([128, 4, 2]), op=ALU.add),
                    reads=[RPS[bank], Rbm], writes=[Rvec])
            for ci in range(2):
                for s in range(3):
                    S.op("dve", lambda ci=ci, s=s: nc.vector.scalar_tensor_tensor(
                        out=SV[:, ci, s, :], in0=VEC[:, ci, (3 * s + 1) * 8:(3 * s + 2) * 8], scalar=1.0,
                        in1=NRM[:, s * 8:(s + 1) * 8], op0=ALU.add, op1=ALU.mult), reads=[Rvec, Rnrm], writes=[Rsv])
                    fac = 1.0 if s == 1 else 0.5
                    S.op("dve", lambda ci=ci, s=s, fac=fac: nc.vector.scalar_tensor_tensor(
                        out=GV[:, ci, s, :], in0=VEC[:, ci, (3 * s + 2) * 8:(3 * s + 3) * 8], scalar=fac,
                        in1=NRM[:, 24 + s * 8:24 + (s + 1) * 8], op0=ALU.mult, op1=ALU.mult), reads=[Rvec, Rnrm], writes=[Rgv])

        def make_gate_bcast(s):
            for ci in range(2):
                bank0 = 4 + 2 * ci
                for c in range(8):
                    dgi = rot["dg"] % 2
                    rot["dg"] += 1
                    S.op("dve", lambda c=c, ci=ci, dgi=dgi: nc.vector.tensor_scalar(
                        out=DG[:, dgi, :], in0=ident[:], scalar1=GV[:, ci, s, c:c + 1], scalar2=None, op0=ALU.mult),
                        reads=[Rid, Rgv], writes=[Rdg[dgi]])
                    bank = bank0 + c // 4
                    S.op("pe", lambda c=c, dgi=dgi, bank=bank: nc.tensor.matmul(
                        PS[:, bank, (c % 4) * 128:(c % 4 + 1) * 128], lhsT=ones32[:], rhs=DG[:, dgi, :], start=True, stop=True),
                        reads=[Rdg[dgi], Rones], writes=[RPS[bank]])
                S.op("dve", lambda ci=ci, bank0=bank0: nc.vector.tensor_copy(
                    out=GB[:, ci, :], in_=PS[:, bank0:bank0 + 2, :].rearrange("p b n -> p (b n)")),
                    reads=[RPS[bank0], RPS[bank0 + 1]], writes=[Rgb[ci]])

        ones32 = sb("ones32", [128, 128])
        Rones = Reg("ones")
        S.op("dve", lambda: nc.vector.memset(ones32[:], 1.0), writes=[Rones])

        cur = {"XN": XN, "TMP": TMP, "HT": None, "BR": None}

        def prenorm_p1(t):
            XN = cur["XN"]
            smi = rot["sm"] % 4
            rot["sm"] += 1
            xi = rot["xn"] % 2
            rot["xn"] += 1
            sm = small[:, smi, :]
            S.op("act", lambda: nc.scalar.activation(out=junk[:], in_=X[:, t, :], func=AF.Square, accum_out=sm[:, 0:1]),
                 reads=[RX[t]], writes=[Rjunk, Rsmall[smi]])
            S.op("act", lambda: nc.scalar.activation(out=sm[:, 1:2], in_=sm[:, 0:1], func=AF.Sqrt, scale=1.0 / D, bias=epsb[:, 0:1]),
                 reads=[Rsmall[smi], Reps], writes=[Rsmall[smi]])
            S.op("dve", lambda: nc.vector.reciprocal(out=sm[:, 2:3], in_=sm[:, 1:2]), reads=[Rsmall[smi]], writes=[Rsmall[smi]])
            S.op("dve", lambda: nc.vector.tensor_scalar(out=XN[:, xi, :], in0=X[:, t, :], scalar1=sm[:, 2:3], scalar2=None, op0=ALU.mult),
                 reads=[RX[t], Rsmall[smi]], writes=[Rxn[xi]])
            return xi

        def prenorm_p2(xi, ci, s, dst, dst_reg, banks):
            XN = cur["XN"]
            b0, b1 = banks

            def emit():
                inst = None
                for c in range(8):
                    bk = b0 if c < 4 else b1
                    inst = nc.tensor.transpose(out=PS[:, bk, (c % 4) * 128:(c % 4 + 1) * 128], in_=XN[:, xi, c * 128:(c + 1) * 128], identity=ident[:])
                return inst
            S.op("pe", emit, reads=[Rxn[xi], Rid], writes=[RPS[b0], RPS[b1]])
            for c in range(8):
                bk = b0 if c < 4 else b1
                S.op("act", lambda c=c, bk=bk: nc.scalar.activation(
                    out=dst[:, c, :], in_=PS[:, bk, (c % 4) * 128:(c % 4 + 1) * 128], func=AF.Identity,
                    scale=SV[:, ci, s, c:c + 1], bias=VEC[:, ci, 3 * s * 8 + c:3 * s * 8 + c + 1]),
                    reads=[RPS[bk], Rsv, Rvec], writes=[dst_reg])

        def prenorm_tile(t, ci, s, dst, dst_reg, banks):
            prenorm_p2(prenorm_p1(t), ci, s, dst, dst_reg, banks)

        epsb = sb("epsb", [128, 1])
        halfpi = sb("halfpi", [128, 1])
        Reps = Reg("eps")
        S.op("dve", lambda: nc.vector.memset(epsb[:], EPS), writes=[Reps])
        S.op("dve", lambda: nc.vector.memset(halfpi[:], float(np.pi / 2)), writes=[Reps])

        Rtmp = [Reg("TMP0"), Reg("TMP1")]

        def postnorm_tile(t, ci, b0):
            TMP = cur["TMP"]
            smi = rot["sm"] % 4
            rot["sm"] += 1
            ti = rot["xn"] % 2
            rot["xn"] += 1
            sm = small[:, smi, :]
            fin = PS[:, b0:b0 + 2, :].rearrange("p b n -> p (b n)")
            S.op("act", lambda: nc.scalar.activation(out=junk[:], in_=fin, func=AF.Square, accum_out=sm[:, 0:1]),
                 reads=[RPS[b0], RPS[b0 + 1]], writes=[Rjunk, Rsmall[smi]])
            S.op("act", lambda: nc.scalar.activation(out=sm[:, 1:2], in_=sm[:, 0:1], func=AF.Sqrt, scale=1.0 / D, bias=epsb[:, 0:1]),
                 reads=[Rsmall[smi], Reps], writes=[Rsmall[smi]])
            S.op("dve", lambda: nc.vector.reciprocal(out=sm[:, 2:3], in_=sm[:, 1:2]), reads=[Rsmall[smi]], writes=[Rsmall[smi]])
            S.op("dve", lambda: nc.vector.scalar_tensor_tensor(out=TMP[:, ti, :], in0=fin, scalar=sm[:, 2:3], in1=GB[:, ci, :],
                                                               op0=ALU.mult, op1=ALU.mult),
                 reads=[RPS[b0], RPS[b0 + 1], Rsmall[smi], Rgb[ci]], writes=[Rtmp[ti]])
            S.op("dve", lambda: nc.vector.tensor_tensor(out=X[:, t, :], in0=X[:, t, :], in1=TMP[:, ti, :], op=ALU.add),
                 reads=[RX[t], Rtmp[ti]], writes=[RX[t]])

        Rhtb = [Reg("HTB0"), Reg("HTB1")]
        Rgt = [Reg("GT%d" % j) for j in range(NFF)]
        Rw13 = [Reg("W13_%d" % i) for i in range(3)]
        Rw2 = [Reg("W2_%d" % i) for i in range(3)]
        Rsil = [Reg("SIL0"), Reg("SIL1")]
        cnts = {"w13": 0, "w2": 0, "htb": 0, "sil": 0, "pa": 0}

        def ffn(l, f, s):
            make_gate_bcast(s)
            w1v = ffn_w1[l, f].rearrange("(kc p) n -> p kc n", p=128)
            w3v = ffn_w3[l, f].rearrange("(kc p) n -> p kc n", p=128)
            w2v = ffn_w2[l, f].rearrange("(j p) n -> p j n", p=128)
            hb0 = cnts["htb"]
            cnts["htb"] += 5

            def prenorm_block(blk_):
                hb_ = (hb0 + blk_) % 2
                ci_ = 0 if blk_ < 4 else 1
                for tt in range(4):
                    prenorm_tile(blk_ * 4 + tt, ci_, s, HTB[:, hb_, :, tt * 128:(tt + 1) * 128], Rhtb[hb_], (4 + 2 * (tt % 2), 5 + 2 * (tt % 2)))
            prenorm_block(0)
            pend = [None] * 4
            for blk in range(5):
                ci = 0 if blk < 4 else 1
                hb = (hb0 + blk) % 2
                for j2 in range(NFF // 2):
                    if blk + 1 < 5:
                        nb_, hbn, cin = blk + 1, (hb0 + blk + 1) % 2, (0 if blk + 1 < 4 else 1)
                        if 4 <= j2 <= 7:
                            tt_ = j2 - 4
                            prenorm_p2(pend[tt_], cin, s, HTB[:, hbn, :, tt_ * 128:(tt_ + 1) * 128], Rhtb[hbn], (4 + 2 * (tt_ % 2), 5 + 2 * (tt_ % 2)))
                        if 2 <= j2 <= 5:
                            pend[j2 - 2] = prenorm_p1(nb_ * 4 + (j2 - 2))
                    sl = cnts["w13"] % 3
                    cnts["w13"] += 1
                    S.dma("pool", W13[:, sl, 0, :, :], w1v[:, :, j2 * 256:(j2 + 1) * 256], writes=[Rw13[sl]])
                    S.dma("pool", W13[:, sl, 1, :, :], w3v[:, :, j2 * 256:(j2 + 1) * 256], writes=[Rw13[sl]])
                    for jj in range(2):
                        j = 2 * j2 + jj
                        pa = cnts["pa"] % 2
                        cnts["pa"] += 1
                        b1, b3 = 2 * pa, 2 * pa + 1
                        for (m, bk) in ((0, b1), (1, b3)):
                            def emit(m=m, bk=bk, jj=jj, sl=sl):
                                inst = None
                                for kc in range(8):
                                    inst = nc.tensor.matmul(PS[:, bk, :], lhsT=W13[:, sl, m, kc, jj * 128:(jj + 1) * 128],
                                                            rhs=HTB[:, hb, kc, :], start=(kc == 0), stop=(kc == 7))
                                return inst
                            S.op("pe", emit, reads=[Rw13[sl], Rhtb[hb]], writes=[RPS[bk]])
                        si = cnts["sil"] % 2
                        cnts["sil"] += 1
                        S.op("act", lambda b1=b1, si=si: nc.scalar.activation(out=SIL[:, si, :], in_=PS[:, b1, :], func=AF.Silu),
                             reads=[RPS[b1]], writes=[Rsil[si]])
                        S.op("dve", lambda b3=b3, si=si, j=j: nc.vector.tensor_tensor(out=GT[:, j, :], in0=PS[:, b3, :], in1=SIL[:, si, :], op=ALU.mult),
                             reads=[RPS[b3], Rsil[si]], writes=[Rgt[j]])
                for j2 in range(NFF // 2):
                    sl = cnts["w2"] % 3
                    cnts["w2"] += 1
                    S.dma("pool", W2[:, sl, :, :], w2v[:, 2 * j2:2 * j2 + 2, :], writes=[Rw2[sl]])
                    for jj in range(2):
                        j = 2 * j2 + jj

                        def emit(j=j, jj=jj, sl=sl):
                            inst = None
                            for tt in range(4):
                                for half in range(2):
                                    inst = nc.tensor.matmul(PS[:, 2 * tt + half, :], lhsT=GT[:, j, tt * 128:(tt + 1) * 128],
                                                            rhs=W2[:, sl, jj, half * 512:(half + 1) * 512], start=(j == 0), stop=(j == NFF - 1))
                            return inst
                        S.op("pe", emit, reads=[Rgt[j], Rw2[sl]], writes=RPS)
                for tt in range(4):
                    postnorm_tile(blk * 4 + tt, ci, 2 * tt)

        MOFF = 0
        HT, MOFF = view(MOFF, [128, 8, 2048], BF16)
        BR, MOFF = view(MOFF, [128, 8, 2048], BF16)
        WIN0 = MOFF
        WIN, MOFF = view(MOFF, [128, 2, 8, 256], BF16)
        BS0 = MOFF
        Rht = Reg("HT")
        Rbr = [Reg("BR%d" % i) for i in range(8)]
        Rwin = [Reg("WIN0"), Reg("WIN1")]
        PV = sb("PV", [128, 64])
        Rpv = Reg("PV")
        BG = sb("BG", [128, 32])
        Rbg = Reg("BG")
        mc = {"win": 0, "pb": 0, "wg": 0, "wo": 0, "sg": 0}
        SEQS = [(0, 16, 0, "sample"), (16, 2, 1, "pA"), (18, 2, 1, "pB")]

        def mixer_params(l):
            load_T(PV[:, 0:6], conv_w[l].rearrange("j (c p) -> (j c) p", p=128), 6, Rpv)
            load_T(PV[:, 6:8], conv_b[l].rearrange("(c p) -> c p", p=128), 2, Rpv)
            load_T(PV[:, 8:10], ret_gn[l].rearrange("(c p) -> c p", p=128), 2, Rpv)
            load_T(PV[:, 10:12], s5_d[l].rearrange("(c p) -> c p", p=128), 2, Rpv)
            load_T(PV[:, 12:13], mla_kv_norm[l].rearrange("(c p) -> c p", p=128), 1, Rpv)
            load_T(BG[:, :], b_gate[l].rearrange("(c p) -> c p", p=128), 32, Rbg)

        def proj_fm(l, col0, ncols, NB, BW, evac, extra_reads=()):
            winv = w_in[l].rearrange("(kc p) n -> p kc n", p=128)
            sl = mc["win"] % 2
            mc["win"] += 1
            S.dma("pool", WIN[:, sl, :, 0:ncols], winv[:, :, col0:col0 + ncols], writes=[Rwin[sl]])
            for b in range(NB):
                bank = mc["pb"] % 4
                mc["pb"] += 1

                def emit(b=b, bank=bank):
                    inst = None
                    for kc in range(8):
                        inst = nc.tensor.matmul(PS[0:ncols, bank, 0:BW], lhsT=WIN[:, sl, kc, 0:ncols], rhs=cur["HT"][:, kc, b * BW:(b + 1) * BW],
                                                start=(kc == 0), stop=(kc == 7))
                    return inst
                S.op("pe", emit, reads=[Rwin[sl], Rht], writes=[RPS[bank]])
                evac(b, bank)

        def branch_conv(l, T, NB, BW):
            o = BS0
            Z, o = view(o, [128, 2056])
            CX, o = view(o, [128, 2048])
            Y, o = view(o, [128, 2048])
            CB, o = view(o, [128, 2048], BF16)
            Rz, Rcx, Ry, Rcb = Reg("Z"), Reg("CX"), Reg("Y"), Reg("CB")
            for cc in range(2):
                S.op("dve", lambda: nc.vector.memset(Z[:, 0:1], 0.0), writes=[Rz])
                S.op("dve", lambda: nc.vector.memset(Z[:, T + 1:T + 2], 0.0), writes=[Rz])
                proj_fm(l, 1280 + cc * 128, 128, NB, BW, lambda b, bank: S.op(
                    "act", lambda: nc.scalar.copy(out=CX[:, b * BW:(b + 1) * BW], in_=PS[:, bank, 0:BW]), reads=[RPS[bank]], writes=[Rcx]))
                proj_fm(l, 1792 + cc * 128, 128, NB, BW, lambda b, bank: S.op(
                    "dve", lambda: nc.vector.tensor_tensor(out=Z[:, 1 + b * BW:1 + (b + 1) * BW], in0=PS[:, bank, 0:BW], in1=CX[:, b * BW:(b + 1) * BW], op=ALU.mult),
                    reads=[RPS[bank], Rcx], writes=[Rz]))
                proj_fm(l, 1536 + cc * 128, 128, NB, BW, lambda b, bank: S.op(
                    "act", lambda: nc.scalar.copy(out=CB[:, b * BW:(b + 1) * BW], in_=PS[:, bank, 0:BW]), reads=[RPS[bank]], writes=[Rcb]))
                S.op("dve", lambda: nc.vector.tensor_scalar(out=Y[:, 0:T], in0=Z[:, 1:T + 1], scalar1=PV[:, 2 + cc:3 + cc], scalar2=PV[:, 6 + cc:7 + cc],
                                                            op0=ALU.mult, op1=ALU.add), reads=[Rz, Rpv], writes=[Ry])
                S.op("dve", lambda: nc.vector.scalar_tensor_tensor(out=Y[:, 0:T], in0=Z[:, 0:T], scalar=PV[:, 0 + cc:1 + cc], in1=Y[:, 0:T],
                                                                   op0=ALU.mult, op1=ALU.add), reads=[Rz, Rpv, Ry], writes=[Ry])
                S.op("dve", lambda: nc.vector.scalar_tensor_tensor(out=Y[:, 0:T], in0=Z[:, 2:T + 2], scalar=PV[:, 4 + cc:5 + cc], in1=Y[:, 0:T],
                                                                   op0=ALU.mult, op1=ALU.add), reads=[Rz, Rpv, Ry], writes=[Ry])
                S.op("dve", lambda: nc.vector.tensor_tensor(out=cur["BR"][:, 4 + cc, 0:T], in0=Y[:, 0:T], in1=CB[:, 0:T], op=ALU.mult),
                     reads=[Ry, Rcb], writes=[Rbr[4 + cc]])

        def gate_stage(l, t0, ntile, ci, T, NB, BW):
            o = WIN0
            WG, o = view(o, [128, 2, 8, 4, 128], BF16)
            WB, o = view(o, [128, 2, 2, 4, 128], BF16)
            MG, o = view(o, [128, 8, 512], BF16)
            SG, o = view(o, [128, 4, 512], BF16)
            WO, o = view(o, [128, 2, D], BF16)
            ACC, o = view(o, [128, 2, 512])
            cur["TMP"], o = view(o, [128, 2, D])
            Rwg = [Reg("WG0"), Reg("WG1")]
            Rmg = [Reg("MG%d" % c) for c in range(8)]
            Rsg = [Reg("SG%d" % n) for n in range(4)]
            Rwo = [Reg("WO0"), Reg("WO1")]
            Racc = [Reg("ACC0"), Reg("ACC1")]
            wgv = w_gate[l].rearrange("(kc p) (n d) -> p kc n d", p=128, n=4)
            wbv = w_branch[l].rearrange("n (kc p) d -> p kc n d", p=128)
            tpb = BW // 128
            for b in range(NB):
                for c in range(8):
                    sl = mc["wg"] % 2
                    mc["wg"] += 1
                    for n in range(4):
                        S.dma("pool", WG[:, sl, :, n, :], wgv[:, :, n, c * 128:(c + 1) * 128], writes=[Rwg[sl]])
                    for n in range(4):
                        S.dma("pool", WB[:, sl, :, n, :], wbv[:, :, n, c * 128:(c + 1) * 128], writes=[Rwg[sl]])
                    for n in range(4):
                        def emit_g(n=n, sl=sl):
                            inst = None
                            for kc in range(8):
                                inst = nc.tensor.matmul(PS[:, n, 0:BW], lhsT=WG[:, sl, kc, n, :], rhs=cur["HT"][:, kc, b * BW:(b + 1) * BW],
                                                        start=(kc == 0), stop=(kc == 7))
                            return inst
                        S.op("pe", emit_g, reads=[Rwg[sl], Rht], writes=[RPS[n]])

                        def emit_p(n=n, sl=sl):
                            inst = None
                            for kc in range(2):
                                inst = nc.tensor.matmul(PS[:, 4 + n, 0:BW], lhsT=WB[:, sl, kc, n, :], rhs=cur["BR"][:, 2 * n + kc, b * BW:(b + 1) * BW],
                                                        start=(kc == 0), stop=(kc == 1))
                            return inst
                        S.op("pe", emit_p, reads=[Rwg[sl], Rbr[2 * n], Rbr[2 * n + 1]], writes=[RPS[4 + n]])
                        S.op("act", lambda n=n: nc.scalar.activation(out=SG[:, n, 0:BW], in_=PS[:, n, 0:BW], func=AF.Sigmoid,
                                                                     bias=BG[:, n * 8 + c:n * 8 + c + 1]), reads=[RPS[n], Rbg], writes=[Rsg[n]])
                    S.op("dve", lambda: nc.vector.tensor_tensor(out=ACC[:, 0, 0:BW], in0=PS[:, 4, 0:BW], in1=SG[:, 0, 0:BW], op=ALU.mult),
                         reads=[RPS[4], Rsg[0]], writes=[Racc[0]])
                    for n in range(1, 4):
                        S.op("dve", lambda n=n: nc.vector.tensor_tensor(out=ACC[:, 1, 0:BW], in0=PS[:, 4 + n, 0:BW], in1=SG[:, n, 0:BW], op=ALU.mult),
                             reads=[RPS[4 + n], Rsg[n]], writes=[Racc[1]])
                        if n < 3:
                            S.op("dve", lambda: nc.vector.tensor_tensor(out=ACC[:, 0, 0:BW], in0=ACC[:, 0, 0:BW], in1=ACC[:, 1, 0:BW], op=ALU.add),
                                 reads=[Racc[0], Racc[1]], writes=[Racc[0]])
                        else:
                            S.op("dve", lambda: nc.vector.tensor_tensor(out=MG[:, c, 0:BW], in0=ACC[:, 0, 0:BW], in1=ACC[:, 1, 0:BW], op=ALU.add),
                                 reads=[Racc[0], Racc[1]], writes=[Rmg[c]])
                for c in range(8):
                    sl = mc["wo"] % 2
                    mc["wo"] += 1
                    S.dma("pool", WO[:, sl, :], w_o[l, c * 128:(c + 1) * 128, :], writes=[Rwo[sl]])

                    def emit_o(c=c, sl=sl):
                        inst = None
                        for tt in range(tpb):
                            for half in range(2):
                                inst = nc.tensor.matmul(PS[:, 2 * tt + half, :], lhsT=MG[:, c, tt * 128:(tt + 1) * 128],
                                                        rhs=WO[:, sl, half * 512:(half + 1) * 512], start=(c == 0), stop=(c == 7))
                        return inst
                    S.op("pe", emit_o, reads=[Rmg[c], Rwo[sl]], writes=RPS[0:2 * tpb])
                for tt in range(tpb):
                    postnorm_tile(t0 + b * tpb + tt, ci, 2 * tt)

        onesb = sb("onesb", [128, 128], BF16)
        S.op("dve", lambda: nc.vector.memset(onesb[:], 1.0), writes=[Rones])
        ATT_SCALE = float(96 ** -0.5)

        def branch_mla(l, t0, T, NB, BW, kind):
            sample = kind == "sample"
            Skeys = T + (512 if sample else 0)
            NKT = Skeys // 128
            o = BS0
            CQ, o = view(o, [128, 2, 2048], BF16)
            CKVN, o = view(o, [128, 2560], BF16)
            KR, o = view(o, [128, 2560], BF16)
            WUQ, o = view(o, [128, 2, 384], BF16)
            WUQS, o = view(o, [128, 2, 4, 32], BF16)
            WUKV, o = view(o, [128, 512], BF16)
            WKRS, o = view(o, [128, 8, 32], BF16)
            QNV, o = view(o, [128, 2])
            oB = o
            Rcq, Rckvn, Rkr, Rw = Reg("m_CQ"), Reg("m_CKVN"), Reg("m_KR"), Reg("m_W")
            W32, oo = view(oB, [128, 2, 384])
            Rw32 = Reg("m_W32")
            S.dma("sp", W32[:, 0, :], mla_w_uq[l, 0:128, :], writes=[Rw32])
            S.dma("sp", W32[0:64, 1, :], mla_w_uq[l, 128:192, :], writes=[Rw32])
            S.dma("sp", QNV[:, 0:1], mla_q_norm[l, 0:128].unsqueeze(1), writes=[Rw])
            S.dma("sp", QNV[0:64, 1:2], mla_q_norm[l, 128:192].unsqueeze(1), writes=[Rw])
            S.dma("pool", WUKV[:, :], mla_w_ukv[l, :, :], writes=[Rw])
            for kc, np_ in ((0, 128), (1, 64)):
                S.op("dve", lambda kc=kc, np_=np_: nc.vector.tensor_scalar(out=WUQ[0:np_, kc, :], in0=W32[0:np_, kc, :], scalar1=QNV[0:np_, kc:kc + 1],
                                                                          scalar2=None, op0=ALU.mult), reads=[Rw32, Rw], writes=[Rw])
                if sample:
                    wv = WUQ[0:np_, kc, :].rearrange("p (h e) -> p h e", h=4)
                    S.op("dve", lambda wv=wv, kc=kc, np_=np_: nc.vector.tensor_scalar(out=WUQS[0:np_, kc, :, 0:16], in0=wv[:, :, 80:96], scalar1=-1.0,
                                                                                   scalar2=None, op0=ALU.mult), reads=[Rw], writes=[Rw])
                    S.op("dve", lambda wv=wv, kc=kc, np_=np_: nc.vector.tensor_copy(out=WUQS[0:np_, kc, :, 16:32], in_=wv[:, :, 64:80]), reads=[Rw], writes=[Rw])
            if sample:
                winv = w_in[l].rearrange("(kc p) n -> p kc n", p=128)
                S.dma("pool", WKRS[:, :, 0:16], winv[:, :, 2384:2400], writes=[Rw])
                S.dma("pool", WKRS[:, :, 16:32], winv[:, :, 2368:2384], writes=[Rw])
                S.op("dve", lambda: nc.vector.tensor_scalar(out=WKRS[:, :, 0:16], in0=WKRS[:, :, 0:16], scalar1=-1.0, scalar2=None, op0=ALU.mult),
                     reads=[Rw], writes=[Rw])
            SQ, oo = view(oo, [128, 2, 512], BF16)
            RST, oo = view(oo, [128, 512])
            TB, oo = view(oo, [128, 2, 512])
            T1, oo = view(oo, [128, 2, 512])
            Rsq, Rrst, Rtb, Rt1 = Reg("m_SQ"), Reg("m_RST"), Reg("m_TB"), Reg("m_T1")

            def rstd_from_ps(bank, parts):
                S.op("act", lambda: nc.scalar.activation(out=RST[:, 0:BW], in_=PS[:, bank, 0:BW], func=AF.Sqrt, scale=1.0 / parts, bias=epsb[:, 0:1]),
                     reads=[RPS[bank], Reps], writes=[Rrst])
                S.op("dve", lambda: nc.vector.reciprocal(out=RST[:, 0:BW], in_=RST[:, 0:BW]), reads=[Rrst], writes=[Rrst])

            winv = w_in[l].rearrange("(kc p) n -> p kc n", p=128)
            WQ = WIN
            S.dma("pool", WQ[:, 0, :, 0:192], winv[:, :, 2048:2240], writes=[Rwin[0]])
            S.dma("pool", WQ[:, 1, :, 0:160], winv[:, :, 2240:2400], writes=[Rwin[1]])
            for b in range(NB):
                cols = slice(b * BW, (b + 1) * BW)
                for kc2, np_, bank in ((0, 128, 0), (1, 64, 1)):
                    def emit(kc2=kc2, np_=np_, bank=bank):
                        inst = None
                        for kc in range(8):
                            inst = nc.tensor.matmul(PS[0:np_, bank, 0:BW], lhsT=WQ[:, 0, kc, kc2 * 128:kc2 * 128 + np_], rhs=cur["HT"][:, kc, cols],
                                                    start=(kc == 0), stop=(kc == 7))
                        return inst
                    S.op("pe", emit, reads=[Rwin[0], Rht], writes=[RPS[bank]])
                    S.op("act", lambda kc2=kc2, np_=np_, bank=bank: nc.scalar.activation(out=SQ[0:np_, kc2, 0:BW], in_=PS[0:np_, bank, 0:BW], func=AF.Square),
                         reads=[RPS[bank]], writes=[Rsq])

                def emit_ss():
                    nc.tensor.matmul(PS[:, 2, 0:BW], lhsT=onesb[:, :], rhs=SQ[:, 0, 0:BW], start=True, stop=False)
                    return nc.tensor.matmul(PS[:, 2, 0:BW], lhsT=onesb[0:64, :], rhs=SQ[0:64, 1, 0:BW], start=False, stop=True)
                S.op("pe", emit_ss, reads=[Rsq, Rones], writes=[RPS[2]])
                rstd_from_ps(2, 192.0)
                for kc2, np_, bank in ((0, 128, 0), (1, 64, 1)):
                    S.op("dve", lambda kc2=kc2, np_=np_, bank=bank: nc.vector.tensor_tensor(out=CQ[0:np_, kc2, cols], in0=PS[0:np_, bank, 0:BW], in1=RST[0:np_, 0:BW], op=ALU.mult),
                         reads=[RPS[bank], Rrst], writes=[Rcq])
                def emit_kv():
                    inst = None
                    for kc in range(8):
                        inst = nc.tensor.matmul(PS[:, 3, 0:BW], lhsT=WQ[:, 1, kc, 0:128], rhs=cur["HT"][:, kc, cols], start=(kc == 0), stop=(kc == 7))
                    return inst
                S.op("pe", emit_kv, reads=[Rwin[1], Rht], writes=[RPS[3]])
                S.op("act", lambda: nc.scalar.activation(out=SQ[:, 0, 0:BW], in_=PS[:, 3, 0:BW], func=AF.Square), reads=[RPS[3]], writes=[Rsq])
                S.op("pe", lambda: nc.tensor.matmul(PS[:, 2, 0:BW], lhsT=onesb[:, :], rhs=SQ[:, 0, 0:BW], start=True, stop=True), reads=[Rsq, Rones], writes=[RPS[2]])
                rstd_from_ps(2, 128.0)
                S.op("dve", lambda: nc.vector.scalar_tensor_tensor(out=CKVN[:, cols], in0=PS[:, 3, 0:BW], scalar=PV[:, 12:13], in1=RST[:, 0:BW], op0=ALU.mult, op1=ALU.mult),
                     reads=[RPS[3], Rpv, Rrst], writes=[Rckvn])
                def emit_kr():
                    inst = None
                    for kc in range(8):
                        inst = nc.tensor.matmul(PS[0:32, 4, 0:BW], lhsT=WQ[:, 1, kc, 128:160], rhs=cur["HT"][:, kc, cols], start=(kc == 0), stop=(kc == 7))
                    return inst
                S.op("pe", emit_kr, reads=[Rwin[1], Rht], writes=[RPS[4]])
                if sample:
                    def emit_krs():
                        inst = None
                        for kc in range(8):
                            inst = nc.tensor.matmul(PS[0:32, 5, 0:BW], lhsT=WKRS[:, kc, :], rhs=cur["HT"][:, kc, cols], start=(kc == 0), stop=(kc == 7))
                        return inst
                    S.op("pe", emit_krs, reads=[Rw, Rht], writes=[RPS[5]])
                    S.dma("sp", TB[0:32, 0, 0:BW], c_rope_mla[0, :, cols], writes=[Rtb])
                    S.dma("sp", TB[0:32, 1, 0:BW], c_rope_mla[1, :, cols], writes=[Rtb])
                    S.op("dve", lambda: nc.vector.tensor_tensor(out=T1[0:32, 0, 0:BW], in0=PS[0:32, 4, 0:BW], in1=TB[0:32, 0, 0:BW], op=ALU.mult),
                         reads=[RPS[4], Rtb], writes=[Rt1])
                    S.op("dve", lambda: nc.vector.tensor_tensor(out=T1[0:32, 1, 0:BW], in0=PS[0:32, 5, 0:BW], in1=TB[0:32, 1, 0:BW], op=ALU.mult),
                         reads=[RPS[5], Rtb], writes=[Rt1])
                    S.op("dve", lambda: nc.vector.tensor_tensor(out=KR[0:32, cols], in0=T1[0:32, 0, 0:BW], in1=T1[0:32, 1, 0:BW], op=ALU.add),
                         reads=[Rt1], writes=[Rkr])
                else:
                    S.op("act", lambda: nc.scalar.copy(out=KR[0:32, cols], in_=PS[0:32, 4, 0:BW]), reads=[RPS[4]], writes=[Rkr])
            if sample:
                CT, _ = view(oB + 3072, [128, 4, 160])
                Rct = Reg("m_CT")
                S.barrier()
                S.dma("sp", CT[:, :, :], ctx_mla[l].rearrange("(i p) f -> p i f", p=128), writes=[Rct])
                for i in range(4):
                    S.op("pe", lambda i=i: nc.tensor.transpose(out=PS[:, 6, 0:128], in_=CT[:, i, 0:128], identity=ident[:]), reads=[Rct, Rid], writes=[RPS[6]])
                    S.op("act", lambda i=i: nc.scalar.copy(out=CKVN[:, T + i * 128:T + (i + 1) * 128], in_=PS[:, 6, 0:128]), reads=[RPS[6]], writes=[Rckvn])
                    S.op("pe", lambda i=i: nc.tensor.transpose(out=PS[0:32, 7, 0:128], in_=CT[:, i, 128:160], identity=ident[:]), reads=[Rct, Rid], writes=[RPS[7]])
                    S.op("act", lambda i=i: nc.scalar.copy(out=KR[0:32, T + i * 128:T + (i + 1) * 128], in_=PS[0:32, 7, 0:128]), reads=[RPS[7]], writes=[Rkr])
            S.barrier()
            o = oB
            KN, o = view(o, [128, 2560], BF16)
            QN, o = view(o, [128, 2048], BF16)
            QR, o = view(o, [128, 2048], BF16)
            VA, o = view(o, [128, 20, 66], BF16)
            Rkn, Rqn, Rqr, Rva = Reg("m_KN"), Reg("m_QN"), Reg("m_QR"), Reg("m_VA")
            ow = WIN0
            TB2, ow2 = view(ow, [128, 2, 512])
            T2, ow2 = view(ow2, [128, 2, 512])
            PT, ow3 = view(ow, [128, 2, 512], BF16)
            OS, ow3 = view(ow3, [128, 512])
            OT, ow3 = view(ow3, [128, 512], BF16)
            Rtb2, Rt2, Rpt, Ros, Rot = Reg("m_TB2"), Reg("m_T2"), [Reg("m_PT0"), Reg("m_PT1")], Reg("m_OS"), Reg("m_OT")
            S.op("dve", lambda: nc.vector.memset(VA[:, :, 64:66], 1.0), writes=[Rva])
            KBW = 512
            for h in range(4):
                for kb in range((Skeys + KBW - 1) // KBW):
                    w = min(KBW, Skeys - kb * KBW)
                    bank = mc["pb"] % 4
                    mc["pb"] += 1
                    S.op("pe", lambda kb=kb, w=w, bank=bank: nc.tensor.matmul(PS[0:64, bank, 0:w], lhsT=WUKV[:, h * 128:h * 128 + 64], rhs=CKVN[:, kb * KBW:kb * KBW + w],
                                                                              start=True, stop=True), reads=[Rw, Rckvn], writes=[RPS[bank]])
                    S.op("act", lambda kb=kb, w=w, bank=bank: nc.scalar.copy(out=KN[0:64, kb * KBW:kb * KBW + w], in_=PS[0:64, bank, 0:w]), reads=[RPS[bank]], writes=[Rkn])
                for kt in range(NKT):
                    bank = mc["pb"] % 4
                    mc["pb"] += 1
                    S.op("pe", lambda kt=kt, bank=bank: nc.tensor.matmul(PS[:, bank, 0:64], lhsT=CKVN[:, kt * 128:(kt + 1) * 128], rhs=WUKV[:, h * 128 + 64:h * 128 + 128],
                                                                          start=True, stop=True), reads=[Rw, Rckvn], writes=[RPS[bank]])
                    S.op("dve", lambda kt=kt, bank=bank: nc.vector.tensor_copy(out=VA[:, kt, 0:64], in_=PS[:, bank, 0:64]), reads=[RPS[bank]], writes=[Rva])
                for b in range(NB):
                    cols = slice(b * BW, (b + 1) * BW)
                    bank = mc["pb"] % 4
                    mc["pb"] += 1

                    def emit_q(c0, m, bank, wt=None):
                        def f():
                            if wt is None:
                                nc.tensor.matmul(PS[0:m, bank, 0:BW], lhsT=WUQ[:, 0, c0:c0 + m], rhs=CQ[:, 0, cols], start=True, stop=False)
                                return nc.tensor.matmul(PS[0:m, bank, 0:BW], lhsT=WUQ[0:64, 1, c0:c0 + m], rhs=CQ[0:64, 1, cols], start=False, stop=True)
                            nc.tensor.matmul(PS[0:m, bank, 0:BW], lhsT=WUQS[:, 0, h, :], rhs=CQ[:, 0, cols], start=True, stop=False)
                            return nc.tensor.matmul(PS[0:m, bank, 0:BW], lhsT=WUQS[0:64, 1, h, :], rhs=CQ[0:64, 1, cols], start=False, stop=True)
                        return f
                    S.op("pe", emit_q(h * 96, 64, bank), reads=[Rw, Rcq], writes=[RPS[bank]])
                    S.op("act", lambda bank=bank: nc.scalar.copy(out=QN[0:64, cols], in_=PS[0:64, bank, 0:BW]), reads=[RPS[bank]], writes=[Rqn])
                    bank2 = mc["pb"] % 4
                    mc["pb"] += 1
                    S.op("pe", emit_q(h * 96 + 64, 32, bank2), reads=[Rw, Rcq], writes=[RPS[bank2]])
                    if sample:
                        bank3 = mc["pb"] % 4
                        mc["pb"] += 1
                        S.op("pe", emit_q(0, 32, bank3, wt=1), reads=[Rw, Rcq], writes=[RPS[bank3]])
                        S.dma("sp", TB2[0:32, 0, 0:BW], c_rope_mla[0, :, cols], writes=[Rtb2])
                        S.dma("sp", TB2[0:32, 1, 0:BW], c_rope_mla[1, :, cols], writes=[Rtb2])
                        S.op("dve", lambda: nc.vector.tensor_tensor(out=T2[0:32, 0, 0:BW], in0=PS[0:32, bank2, 0:BW], in1=TB2[0:32, 0, 0:BW], op=ALU.mult),
                             reads=[RPS[bank2], Rtb2], writes=[Rt2])
                        S.op("dve", lambda: nc.vector.tensor_tensor(out=T2[0:32, 1, 0:BW], in0=PS[0:32, bank3, 0:BW], in1=TB2[0:32, 1, 0:BW], op=ALU.mult),
                             reads=[RPS[bank3], Rtb2], writes=[Rt2])
                        S.op("dve", lambda: nc.vector.tensor_tensor(out=QR[0:32, cols], in0=T2[0:32, 0, 0:BW], in1=T2[0:32, 1, 0:BW], op=ALU.add),
                             reads=[Rt2], writes=[Rqr])
                    else:
                        S.op("act", lambda: nc.scalar.copy(out=QR[0:32, cols], in_=PS[0:32, bank2, 0:BW]), reads=[RPS[bank2]], writes=[Rqr])
                S.barrier()
                for b in range(NB):
                    cols = slice(b * BW, (b + 1) * BW)
                    ob = 4 + (b % 2)
                    for kt in range(NKT):
                        bank = mc["pb"] % 4
                        mc["pb"] += 1
                        pi = kt % 2

                        def emit_s(kt=kt, bank=bank):
                            nc.tensor.matmul(PS[:, bank, 0:BW], lhsT=KN[0:64, kt * 128:(kt + 1) * 128], rhs=QN[0:64, cols], start=True, stop=False)
                            return nc.tensor.matmul(PS[:, bank, 0:BW], lhsT=KR[0:32, kt * 128:(kt + 1) * 128], rhs=QR[0:32, cols], start=False, stop=True)
                        S.op("pe", emit_s, reads=[Rkn, Rqn, Rkr, Rqr], writes=[RPS[bank]])
                        S.op("act", lambda bank=bank, pi=pi: nc.scalar.activation(out=PT[:, pi, 0:BW], in_=PS[:, bank, 0:BW], func=AF.Exp, scale=ATT_SCALE),
                             reads=[RPS[bank]], writes=[Rpt[pi]])
                        S.op("pe", lambda kt=kt, pi=pi: nc.tensor.matmul(PS[0:65, ob, 0:BW], lhsT=VA[:, kt, 0:65], rhs=PT[:, pi, 0:BW], start=(kt == 0), stop=(kt == NKT - 1)),
                             reads=[Rva, Rpt[pi]], writes=[RPS[ob]])
                    S.op("act", lambda: nc.scalar.copy(out=OS[0:65, 0:BW], in_=PS[0:65, ob, 0:BW]), reads=[RPS[ob]], writes=[Ros])
                    S.op("dve", lambda: nc.vector.reciprocal(out=OS[64:65, 0:BW], in_=OS[64:65, 0:BW]), reads=[Ros], writes=[Ros])
                    S.op("pe", lambda: nc.tensor.matmul(PS[0:64, 6, 0:BW], lhsT=ones32[64:65, 0:64], rhs=OS[64:65, 0:BW], start=True, stop=True),
                         reads=[Ros, Rones], writes=[RPS[6]])
                    S.op("dve", lambda: nc.vector.tensor_tensor(out=OT[0:64, 0:BW], in0=PS[0:64, 6, 0:BW], in1=OS[0:64, 0:BW], op=ALU.mult),
                         reads=[RPS[6], Ros], writes=[Rot])
                    S.dma("sp", cur["BR"][(h % 2) * 64:(h % 2) * 64 + 64, 6 + h // 2, cols], OT[0:64, 0:BW], reads=[Rot], writes=[Rbr[6 + h // 2]], key=Reg("m_OTd"))
                S.barrier()

        def mla_cache_out(l, t0, ntile, pi):
            o = BS0
            CA, o = view(o, [128, 2, 160])
            KVB, o = view(o, [128, 128])
            Rca, Rkvb = [Reg("m_CA0"), Reg("m_CA1")], Reg("m_KVB")
            winv = w_in[l].rearrange("(kc p) n -> p kc n", p=128)
            S.dma("pool", WIN[:, 0, :, 0:160], winv[:, :, 2240:2400], writes=[Rwin[0]])
            S.dma("sp", KVB[:, :], mla_kv_norm[l:l + 1, :].broadcast_to([128, 128]), writes=[Rkvb])
            for tt in range(ntile):
                bank = mc["pb"] % 4
                mc["pb"] += 1
                smi = rot["sm"] % 4
                rot["sm"] += 1
                sm = small[:, smi, :]

                def emit(tt=tt, bank=bank):
                    inst = None
                    for kc in range(8):
                        inst = nc.tensor.matmul(PS[:, bank, 0:160], lhsT=cur["HT"][:, kc, tt * 128:(tt + 1) * 128], rhs=WIN[:, 0, kc, 0:160], start=(kc == 0), stop=(kc == 7))
                    return inst
                S.op("pe", emit, reads=[Rwin[0], Rht], writes=[RPS[bank]])
                S.op("act", lambda: nc.scalar.activation(out=junk[:, 0:128], in_=PS[:, bank, 0:128], func=AF.Square, accum_out=sm[:, 0:1]),
                     reads=[RPS[bank]], writes=[Rjunk, Rsmall[smi]])
                S.op("act", lambda: nc.scalar.activation(out=sm[:, 1:2], in_=sm[:, 0:1], func=AF.Sqrt, scale=1.0 / 128, bias=epsb[:, 0:1]),
                     reads=[Rsmall[smi], Reps], writes=[Rsmall[smi]])
                S.op("dve", lambda: nc.vector.reciprocal(out=sm[:, 2:3], in_=sm[:, 1:2]), reads=[Rsmall[smi]], writes=[Rsmall[smi]])
                ci_ = tt % 2
                S.op("dve", lambda: nc.vector.scalar_tensor_tensor(out=CA[:, ci_, 0:128], in0=PS[:, bank, 0:128], scalar=sm[:, 2:3], in1=KVB[:, :], op0=ALU.mult, op1=ALU.mult),
                     reads=[RPS[bank], Rsmall[smi], Rkvb], writes=[Rca[ci_]])
                S.op("dve", lambda: nc.vector.tensor_copy(out=CA[:, ci_, 128:160], in_=PS[:, bank, 128:160]), reads=[RPS[bank]], writes=[Rca[ci_]])
                OUT_EVS.append(S.dma("sp", o_mla[pi, l, tt * 128:(tt + 1) * 128, :], CA[:, ci_, :], reads=[Rca[ci_]], key=Reg("o_mla_d%d" % ci_)))

        def branch_ret(l, t0, T, kind):
            sample = kind == "sample"
            n = T // 128
            o = BS0
            WRb, o = view(o, [128, 8, 512], BF16)
            SBst, o = view(o, [128, 16, 256], BF16)
            DM, o = view(o, [128, 4, 128], BF16)
            QD, o = view(o, [128, 2, 4, 128], BF16)
            CD, o = view(o, [128, 2, 256])
            LG, o = view(o, [128, 8])
            KD, o = view(o, [128, 2, 4])
            SF, o = view(o, [128, 256])
            SB, o = view(o, [128, 256])
            SFb, o = view(o, [128, 256], BF16)
            oT = o
            WRa, _ = view(WIN0, [128, 8, 512], BF16)
            Rwr, Rsbst, Rtab, Rsf, Rsb, Rsfb = Reg("r_WR"), Reg("r_SBst"), Reg("r_TAB"), Reg("r_SF"), Reg("r_SB"), Reg("r_SFb")
            winv = w_in[l].rearrange("(kc p) n -> p kc n", p=128)
            S.dma("pool", WRa[:, :, :], winv[:, :, 256:768], writes=[Rwr])
            S.dma("pool", WRb[:, :, :], winv[:, :, 768:1280], writes=[Rwr])
            CT6, o2 = view(oT, [128, 6, 128])
            E1, o2 = view(o2, [128, 2, 128])
            PIDX, o2 = view(o2, [128, 2])
            C128, o2 = view(o2, [128, 64])
            Rc6, Re1 = Reg("r_C6"), Reg("r_E1")
            S.dma("sp", CT6[:, :, :], c_ret.rearrange("k p i -> p k i"), writes=[Rc6])
            S.dma("sp", PIDX[:, :], c_pidx[:, :], writes=[Rc6])
            S.dma("sp", LG[:, :], ret_decay[l:l + 1].rearrange("o d h -> o (d h)").broadcast_to([128, 8]), writes=[Rtab])
            S.op("dve", lambda: nc.vector.memset(C128[:, :], 128.0), writes=[Rc6])
            S.op("act", lambda: nc.scalar.activation(out=LG[:, :], in_=LG[:, :], func=AF.Sigmoid), reads=[Rtab], writes=[Rtab])
            S.op("act", lambda: nc.scalar.activation(out=LG[:, :], in_=LG[:, :], func=AF.Ln), reads=[Rtab], writes=[Rtab])
            for h in range(4):
                S.op("act", lambda h=h: nc.scalar.activation(out=E1[:, 0, :], in_=CT6[:, 0, :], func=AF.Exp, scale=LG[:, h:h + 1]), reads=[Rc6, Rtab], writes=[Re1])
                S.op("act", lambda h=h: nc.scalar.activation(out=E1[:, 1, :], in_=CT6[:, 1, :], func=AF.Exp, scale=LG[:, 4 + h:5 + h]), reads=[Rc6, Rtab], writes=[Re1])
                S.op("dve", lambda h=h: nc.vector.tensor_tensor(out=E1[:, :, :], in0=E1[:, :, :], in1=CT6[:, 2:4, :], op=ALU.mult), reads=[Re1, Rc6], writes=[Re1])
                S.op("dve", lambda h=h: nc.vector.tensor_tensor(out=DM[:, h, :], in0=E1[:, 0, :], in1=E1[:, 1, :], op=ALU.add), reads=[Re1], writes=[Rtab])
                for d in range(2):
                    S.op("act", lambda h=h, d=d: nc.scalar.activation(out=QD[:, d, h, :], in_=CT6[:, 4 + d, :], func=AF.Exp, scale=LG[:, d * 4 + h:d * 4 + h + 1]),
                         reads=[Rc6, Rtab], writes=[Rtab])
                    S.op("act", lambda h=h, d=d: nc.scalar.activation(out=CD[:, d, h * 64:(h + 1) * 64], in_=C128[:, :], func=AF.Exp, scale=LG[:, d * 4 + h:d * 4 + h + 1]),
                         reads=[Rc6, Rtab], writes=[Rtab])
                    S.op("act", lambda h=h, d=d: nc.scalar.activation(out=KD[:, d, h:h + 1], in_=PIDX[:, d:d + 1], func=AF.Exp, scale=LG[:, d * 4 + h:d * 4 + h + 1]),
                         reads=[Rc6, Rtab], writes=[Rtab])
            S.op("dve", lambda: nc.vector.tensor_scalar(out=KD[:, :, :], in0=KD[:, :, :], scalar1=0.125, scalar2=None, op0=ALU.mult), reads=[Rtab], writes=[Rtab])
            if sample:
                S.dma("sp", SF[0:64, :].rearrange("d (h e) -> d h e", h=4), st_ret[l, 0].rearrange("h d e -> d h e"), writes=[Rsf])
                S.dma("sp", SB[0:64, :].rearrange("d (h e) -> d h e", h=4), st_ret[l, 1].rearrange("h d e -> d h e"), writes=[Rsb])
            else:
                S.op("dve", lambda: nc.vector.memset(SF[0:64, :], 0.0), writes=[Rsf])
                S.op("dve", lambda: nc.vector.memset(SB[0:64, :], 0.0), writes=[Rsb])
            S.barrier()
            if cfg.get("ret_stop") == "tables":
                return
            o3 = oT
            QK, o3 = view(o3, [128, 512], BF16)
            TA, o3 = view(o3, [128, 256])
            TBt, o3 = view(o3, [128, 256])
            RT, o3 = view(o3, [128, 2, 32])
            KDt, o3 = view(o3, [128, 256], BF16)
            VTc, o3 = view(o3, [128, 256], BF16)
            SRG, o3 = view(o3, [128, 256], BF16)
            QT, o3 = view(o3, [128, 3, 512], BF16)
            KT, o3 = view(o3, [128, 512], BF16)
            AM, o3 = view(o3, [128, 512], BF16)
            CEN, o3 = view(o3, [128, 256])
            SQr, o3 = view(o3, [128, 256])
            NRo, o3 = view(o3, [128, 256], BF16)
            MS, o3 = view(o3, [128, 8])
            Rqk, Rta, Rrt, Rkd, Rvt, Rsrg, Rqt, Rkt, Ram, Rcen, Rsq, Rnro, Rms = (Reg("r_" + x) for x in
                ("QK", "TA", "RT", "KDt", "VTc", "SRG", "QT", "KT", "AM", "CEN", "SQ", "NRo", "MS"))
            PSb = lambda bank: PS[:, bank, :].bitcast(BF16)

            def proj(c, bank, WRx, c0, ncol):
                def emit():
                    inst = None
                    for kc in range(8):
                        inst = nc.tensor.matmul(PS[:, bank, 0:ncol], lhsT=cur["HT"][:, kc, c * 128:(c + 1) * 128], rhs=WRx[:, kc, c0:c0 + ncol], start=(kc == 0), stop=(kc == 7))
                    return inst
                S.op("pe", emit, reads=[Rwr, Rht], writes=[RPS[bank]])

            def rope(c, bank, col0, ng, dst):
                src = PS[:, bank, col0:col0 + ng * 64].rearrange("p (g t e) -> p g t e", g=ng, t=2)
                dv = dst.rearrange("p (g t e) -> p g t e", g=ng, t=2)
                if not sample:
                    S.op("act", lambda: nc.scalar.copy(out=dst, in_=PS[:, bank, col0:col0 + ng * 64]), reads=[RPS[bank]], writes=[Rqk])
                    return
                S.dma("sp", RT[:, 0, :], c_rope_ret[0, c * 128:(c + 1) * 128, :], writes=[Rrt])
                S.dma("sp", RT[:, 1, :], c_rope_ret[1, c * 128:(c + 1) * 128, :], writes=[Rrt])
                cosb = RT[:, 0, :].unsqueeze(1).broadcast_to([128, ng, 32])
                sinb = RT[:, 1, :].unsqueeze(1).broadcast_to([128, ng, 32])
                ta = TA[:, 0:ng * 32].rearrange("p (g e) -> p g e", g=ng)
                tb = TBt[:, 0:ng * 32].rearrange("p (g e) -> p g e", g=ng)
                S.op("dve", lambda: nc.vector.tensor_tensor(out=ta, in0=src[:, :, 0, :], in1=cosb, op=ALU.mult), reads=[RPS[bank], Rrt], writes=[Rta])
                S.op("dve", lambda: nc.vector.tensor_tensor(out=tb, in0=src[:, :, 1, :], in1=sinb, op=ALU.mult), reads=[RPS[bank], Rrt], writes=[Rta])
                S.op("dve", lambda: nc.vector.tensor_tensor(out=dv[:, :, 0, :], in0=ta, in1=tb, op=ALU.subtract), reads=[Rta], writes=[Rqk])
                S.op("dve", lambda: nc.vector.tensor_tensor(out=ta, in0=src[:, :, 0, :], in1=sinb, op=ALU.mult), reads=[RPS[bank], Rrt], writes=[Rta])
                S.op("dve", lambda: nc.vector.tensor_tensor(out=tb, in0=src[:, :, 1, :], in1=cosb, op=ALU.mult), reads=[RPS[bank], Rrt], writes=[Rta])
                S.op("dve", lambda: nc.vector.tensor_tensor(out=dv[:, :, 1, :], in0=ta, in1=tb, op=ALU.add), reads=[Rta], writes=[Rqk])

            def kdec_mul(d, ksrc):
                S.op("dve", lambda: nc.vector.tensor_tensor(out=KDt[:, :].rearrange("p (h e) -> p h e", h=4), in0=ksrc.rearrange("p (h e) -> p h e", h=4),
                                                            in1=KD[:, d, :].unsqueeze(2).broadcast_to([128, 4, 64]), op=ALU.mult), reads=[Rqk, Rtab], writes=[Rkd])

            def umat(bank):
                def emit():
                    inst = None
                    for h in range(4):
                        inst = nc.tensor.matmul(PS[0:64, bank, h * 64:(h + 1) * 64], lhsT=KDt[:, h * 64:(h + 1) * 64], rhs=VTc[:, h * 64:(h + 1) * 64], start=True, stop=True)
                    return inst
                S.op("pe", emit, reads=[Rkd, Rvt], writes=[RPS[bank]])

            def state_update(St, Rst, d, bank):
                S.op("dve", lambda: nc.vector.tensor_tensor(out=St[0:64, :], in0=St[0:64, :], in1=CD[0:64, d, :], op=ALU.mult), reads=[Rst, Rtab], writes=[Rst])
                S.op("dve", lambda: nc.vector.tensor_tensor(out=St[0:64, :], in0=St[0:64, :], in1=PS[0:64, bank, 0:256], op=ALU.add), reads=[Rst, RPS[bank]], writes=[Rst])

            for c in range(n - 1, -1, -1):
                proj(c, 0, WRa, 256, 256)
                proj(c, 1, WRb, 0, 256)
                rope(c, 0, 0, 4, QK[:, 0:256])
                S.op("act", lambda: nc.scalar.copy(out=VTc[:, :], in_=PS[:, 1, 0:256]), reads=[RPS[1]], writes=[Rvt])
                kdec_mul(1, QK[:, 0:256])
                umat(5)
                S.op("act", lambda c=c: nc.scalar.copy(out=SBst[0:64, c, :], in_=SB[0:64, :]), reads=[Rsb], writes=[Rsbst])
                state_update(SB, Rsb, 1, 5)
            S.op("act", lambda: nc.scalar.copy(out=SFb[0:64, :], in_=SF[0:64, :]), reads=[Rsf], writes=[Rsfb])
            if cfg.get("ret_stop") == "pass1":
                return
            for c in range(n):
                proj(c, 0, WRa, 0, 512)
                proj(c, 1, WRb, 0, 512)
                rope(c, 0, 0, 8, QK[:, :])
                S.op("act", lambda: nc.scalar.copy(out=VTc[:, :], in_=PS[:, 1, 0:256]), reads=[RPS[1]], writes=[Rvt])
                S.op("act", lambda: nc.scalar.activation(out=SRG[:, :], in_=PS[:, 1, 256:512], func=AF.Silu), reads=[RPS[1]], writes=[Rsrg])
                kdec_mul(0, QK[:, 256:512])

                def emit_t():
                    inst = None
                    for g in range(8):
                        inst = nc.tensor.transpose(out=PSb(2)[0:64, g * 128:(g + 1) * 128], in_=QK[:, g * 64:(g + 1) * 64], identity=identb[:])
                    return inst
                S.op("pe", emit_t, reads=[Rqk, Rid], writes=[RPS[2]])
                S.op("dve", lambda: nc.vector.tensor_copy(out=QT[0:64, 0, :], in_=PSb(2)[0:64, 0:512]), reads=[RPS[2]], writes=[Rqt])
                for d in range(2):
                    S.op("dve", lambda d=d: nc.vector.tensor_tensor(out=QT[0:64, 1 + d, :], in0=PSb(2)[0:64, 0:512], in1=QD[0:64, d, :, :].rearrange("p h i -> p (h i)"), op=ALU.mult),
                         reads=[RPS[2], Rtab], writes=[Rqt])
                S.op("dve", lambda: nc.vector.tensor_copy(out=KT[0:64, :], in_=PSb(2)[0:64, 512:1024]), reads=[RPS[2]], writes=[Rkt])
                if cfg.get("ret_stop") == "p2a":
                    continue

                def emit_a():
                    inst = None
                    for h in range(4):
                        inst = nc.tensor.matmul(PS[:, 3, h * 128:(h + 1) * 128], lhsT=KT[0:64, h * 128:(h + 1) * 128], rhs=QT[0:64, 0, h * 128:(h + 1) * 128], start=True, stop=True)
                    return inst
                S.op("pe", emit_a, reads=[Rkt, Rqt], writes=[RPS[3]])
                S.op("dve", lambda: nc.vector.tensor_tensor(out=AM[:, :], in0=PS[:, 3, :], in1=DM[:, :, :].rearrange("p h i -> p (h i)"), op=ALU.mult),
                     reads=[RPS[3], Rtab], writes=[Ram])
                if cfg.get("ret_stop") == "p2b":
                    continue

                def emit_o(c=c):
                    inst = None
                    for h in range(4):
                        oc = PS[:, 4, h * 64:(h + 1) * 64]
                        nc.tensor.matmul(oc, lhsT=AM[:, h * 128:(h + 1) * 128], rhs=VTc[:, h * 64:(h + 1) * 64], start=True, stop=False)
                        nc.tensor.matmul(oc, lhsT=QT[0:64, 1, h * 128:(h + 1) * 128], rhs=SFb[0:64, h * 64:(h + 1) * 64], start=False, stop=False)
                        inst = nc.tensor.matmul(oc, lhsT=QT[0:64, 2, h * 128:(h + 1) * 128], rhs=SBst[0:64, c, h * 64:(h + 1) * 64], start=False, stop=True)
                    return inst
                S.op("pe", emit_o, reads=[Ram, Rvt, Rqt, Rsfb, Rsbst], writes=[RPS[4]])
                umat(5)
                state_update(SF, Rsf, 0, 5)
                S.op("act", lambda: nc.scalar.copy(out=SFb[0:64, :], in_=SF[0:64, :]), reads=[Rsf], writes=[Rsfb])
                if cfg.get("ret_stop") == "p2c":
                    continue
                ov = PS[:, 4, 0:256].rearrange("p (h e) -> p h e", h=4)
                S.op("dve", lambda: nc.vector.tensor_reduce(out=MS[:, 0:4], in_=ov, axis=AX.X, op=ALU.add), reads=[RPS[4]], writes=[Rms])
                S.op("dve", lambda: nc.vector.tensor_scalar(out=MS[:, 0:4], in0=MS[:, 0:4], scalar1=-1.0 / 64, scalar2=None, op0=ALU.mult), reads=[Rms], writes=[Rms])
                cv = CEN[:, :].rearrange("p (h e) -> p h e", h=4)
                S.op("dve", lambda: nc.vector.tensor_tensor(out=cv, in0=ov, in1=MS[:, 0:4].unsqueeze(2).broadcast_to([128, 4, 64]), op=ALU.add),
                     reads=[RPS[4], Rms], writes=[Rcen])
                S.op("dve", lambda: nc.vector.tensor_tensor(out=SQr[:, :], in0=CEN[:, :], in1=CEN[:, :], op=ALU.mult), reads=[Rcen], writes=[Rsq])
                S.op("dve", lambda: nc.vector.tensor_reduce(out=MS[:, 4:8], in_=SQr[:, :].rearrange("p (h e) -> p h e", h=4), axis=AX.X, op=ALU.add), reads=[Rsq], writes=[Rms])
                S.op("act", lambda: nc.scalar.activation(out=MS[:, 4:8], in_=MS[:, 4:8], func=AF.Sqrt, scale=1.0 / 64, bias=epsb[:, 0:1]), reads=[Rms, Reps], writes=[Rms])
                S.op("dve", lambda: nc.vector.reciprocal(out=MS[:, 4:8], in_=MS[:, 4:8]), reads=[Rms], writes=[Rms])
                S.op("dve", lambda: nc.vector.tensor_tensor(out=cv, in0=cv, in1=MS[:, 4:8].unsqueeze(2).broadcast_to([128, 4, 64]), op=ALU.mult), reads=[Rcen, Rms], writes=[Rcen])
                S.op("dve", lambda: nc.vector.tensor_tensor(out=NRo[:, :], in0=CEN[:, :], in1=SRG[:, :], op=ALU.mult), reads=[Rcen, Rsrg], writes=[Rnro])

                if cfg.get("ret_stop") == "p2d":
                    continue

                def emit_t2():
                    inst = None
                    for cc in range(2):
                        inst = nc.tensor.transpose(out=PSb(6)[:, cc * 128:(cc + 1) * 128], in_=NRo[:, cc * 128:(cc + 1) * 128], identity=identb[:])
                    return inst
                S.op("pe", emit_t2, reads=[Rnro, Rid], writes=[RPS[6]])
                for cc in range(2):
                    S.op("dve", lambda cc=cc, c=c: nc.vector.tensor_scalar(out=cur["BR"][:, 2 + cc, c * 128:(c + 1) * 128], in0=PSb(6)[:, cc * 128:(cc + 1) * 128],
                                                                          scalar1=PV[:, 8 + cc:9 + cc], scalar2=None, op0=ALU.mult), reads=[RPS[6], Rpv], writes=[Rbr[2 + cc]])
            if not sample:
                pi = 0 if kind == "pA" else 1
                OUT_EVS.append(S.dma("sp", o_ret[pi, l, 0].rearrange("h d e -> d h e"), SF[0:64, :].rearrange("d (h e) -> d h e", h=4), reads=[Rsf], key=Reg("o_ret_d")))
                OUT_EVS.append(S.dma("sp", o_ret[pi, l, 1].rearrange("h d e -> d h e"), SB[0:64, :].rearrange("d (h e) -> d h e", h=4), reads=[Rsb], key=Reg("o_ret_d")))

        def branch_s5(l, t0, T, NB, BW, kind, multi=False):
            sample = kind == "sample"
            o = BS0
            UT, o = view(o, [128, 2, 2048], BF16)
            YS, o = view(o, [128, 2, 2048], BF16)
            YFp, o = view(o, [128, 2048], BF16)
            oZ = o
            TRI, o = view(o, [128, 2, 512])
            TRIb, o = view(o, [128, 2, 512], BF16)
            BZ, o = view(o, [128, 2, 512], BF16)
            oTT = o
            TT, o = view(o, [128, 2, 1024], BF16)
            TTf, _ = view(oTT, [128, 2, 512])
            oS = o
            ow = WIN0
            SBb, ow = view(ow, [128, 2, 512], BF16)
            OTs, ow = view(ow, [128, 512], BF16)
            BW_, ow = view(ow, [128, 2, 2, 128], BF16)
            BBR, ow = view(ow, [128, 16, 16])
            BBI, ow = view(ow, [128, 16, 16])
            CW, ow = view(ow, [128, 8, 2, 32], BF16)
            WGL, ow = view(ow, [128, 2, 256], BF16)
            YV, _ = view(WIN0, [128, 512])
            Rut, Rys, Rsfs, Rtri, Rbz, Rtt, Rsbb, Rots, Rrb, Rbw = (Reg("s_" + x) for x in ("UT", "YS", "YFp", "TRI", "BZ", "TT", "SBb", "OTs", "RB", "BW"))
            def sm_(shape, dt=F32):
                nonlocal o
                v, o = view(o, shape, dt)
                return v
            o1 = [oTT]

            def ot_(shape, dt=F32):
                v, o1[0] = view(o1[0], shape, dt)
                return v
            LRE, LIM, LDT, AR, AI, FR, FI = (ot_([128, 16]) for _ in range(7))
            BRE, BIM = ot_([128, 16, 16]), ot_([128, 16, 16])
            CNAT = ot_([128, 2, 64])
            MAG, UR, UI, W1, W2_, W3 = (sm_([128, 16]) for _ in range(6))
            UBR, UBI = sm_([128, 16]), sm_([128, 16])
            TA_, TBs = sm_([128, 2, 16, 16]), sm_([128, 2, 16, 32])
            PW = sm_([128, 2, 16])
            S0t = sm_([128, 16, 2])
            INI = sm_([128, 2, 2])
            FIN = sm_([128, 2, 16, 2])
            WP = sm_([128, 128])
            Rsu = Reg("s_setup")
            Rini, Rfin, Rwp, Rcn = Reg("s_INI"), Reg("s_FIN"), Reg("s_WP"), Reg("s_CN")
            Rbu = Reg("s_BU")
            V = nc.vector
            dbgon = cfg.get("s5dbg") == kind
            if dbgon:
                dbg2 = nc.dram_tensor("dbg2", [128, 4096], F32, kind="ExternalOutput").ap()

            def dbg(ap, c0, n, regs):
                if dbgon:
                    S.dma("sp", dbg2[:, c0:c0 + n], ap, reads=regs, key=Reg("dbg2"))

            def dv(fn, reads, writes):
                S.op("dve", fn, reads=reads, writes=writes)

            def tt(out, a, b, op, reads=(Rsu,), writes=(Rsu,)):
                dv(lambda: V.tensor_tensor(out=out, in0=a, in1=b, op=op), list(reads), list(writes))

            def cmul(orr, oi, ar, ai, br, bi, t1, t2, reads=(Rsu,), writes=(Rsu,)):
                tt(t1, ar, br, ALU.mult, reads, writes)
                tt(t2, ai, bi, ALU.mult, reads, writes)
                tt(t2, t1, t2, ALU.subtract, reads, writes)
                tt(t1, ar, bi, ALU.mult, reads, writes)
                tt(oi, ai, br, ALU.mult, reads, writes)
                tt(oi, t1, oi, ALU.add, reads, writes)
                tt(orr, t2, t2, ALU.max, reads, writes)

            for cc in range(2):
                proj_fm(l, cc * 128, 128, NB, BW, lambda b, bank, cc=cc: S.op(
                    "act", lambda: nc.scalar.copy(out=UT[:, cc, b * BW:(b + 1) * BW], in_=PS[:, bank, 0:BW]), reads=[RPS[bank]], writes=[Rut]))
            S.barrier()
            for d in range(2):
                for dst, src in ((LRE, s5_lam_re), (LIM, s5_lam_im)):
                    S.dma("sp", dst[:, d::2], src[l, d].rearrange("(m g) p -> (g p) m", g=2), writes=[Rsu], slow=True)
                for g2 in range(2):
                    S.dma("sp", LDT[g2 * 64:(g2 + 1) * 64, d::2], s5_log_dt[l, d:d + 1, g2::2].broadcast_to([64, 8]), writes=[Rsu], slow=True)
                for dst, src in ((BRE, s5_b_re), (BIM, s5_b_im)):
                    S.dma("sp", dst[:, d::2, :], src[l, d].rearrange("(m g) p h -> (g p) m h", g=2), writes=[Rsu])
                if sample:
                    S.dma("sp", S0t[:, d::2, :], st_s5[l, d].rearrange("(m g) p r -> (g p) m r", g=2), writes=[Rsu], slow=True)
            S.dma("pool", WGL[:, :, :], s5_w_glu[l].rearrange("(kc p) n -> p kc n", p=128), writes=[Rsu])
            S.op("act", lambda: nc.scalar.activation(out=LDT[:, :], in_=LDT[:, :], func=AF.Exp), reads=[Rsu], writes=[Rsu])
            tt(W1[:, :], LRE[:, :], LDT[:, :], ALU.mult)
            S.op("act", lambda: nc.scalar.activation(out=MAG[:, :], in_=W1[:, :], func=AF.Exp), reads=[Rsu], writes=[Rsu])
            tt(W1[:, :], LIM[:, :], LDT[:, :], ALU.mult)
            S.op("act", lambda: nc.scalar.activation(out=UI[:, :], in_=W1[:, :], func=AF.Sin, scale=1.0 / 64), reads=[Rsu], writes=[Rsu])
            S.op("act", lambda: nc.scalar.activation(out=UR[:, :], in_=W1[:, :], func=AF.Sin, scale=1.0 / 64, bias=halfpi[:, 0:1]), reads=[Rsu, Reps], writes=[Rsu])
            for _ in range(6):
                tt(W1[:, :], UR[:, :], UR[:, :], ALU.mult)
                tt(W2_[:, :], UI[:, :], UI[:, :], ALU.mult)
                tt(W3[:, :], UR[:, :], UI[:, :], ALU.mult)
                tt(UR[:, :], W1[:, :], W2_[:, :], ALU.subtract)
                tt(UI[:, :], W3[:, :], W3[:, :], ALU.add)
            tt(AR[:, :], MAG[:, :], UR[:, :], ALU.mult)
            tt(AI[:, :], MAG[:, :], UI[:, :], ALU.mult)
            tt(W1[:, :], LRE[:, :], LRE[:, :], ALU.mult)
            tt(W2_[:, :], LIM[:, :], LIM[:, :], ALU.mult)
            tt(W1[:, :], W1[:, :], W2_[:, :], ALU.add)
            dv(lambda: V.reciprocal(out=W1[:, :], in_=W1[:, :]), [Rsu], [Rsu])
            dv(lambda: V.tensor_scalar(out=W2_[:, :], in0=AR[:, :], scalar1=-1.0, scalar2=None, op0=ALU.add), [Rsu], [Rsu])
            tt(FR[:, :], W2_[:, :], LRE[:, :], ALU.mult)
            tt(W3[:, :], AI[:, :], LIM[:, :], ALU.mult)
            tt(FR[:, :], FR[:, :], W3[:, :], ALU.add)
            tt(FR[:, :], FR[:, :], W1[:, :], ALU.mult)
            tt(FI[:, :], AI[:, :], LRE[:, :], ALU.mult)
            tt(W3[:, :], W2_[:, :], LIM[:, :], ALU.mult)
            tt(FI[:, :], FI[:, :], W3[:, :], ALU.subtract)
            tt(FI[:, :], FI[:, :], W1[:, :], ALU.mult)
            dbg(MAG[:, :], 0, 16, [Rsu]); dbg(UR[:, :], 16, 16, [Rsu]); dbg(UI[:, :], 32, 16, [Rsu]); dbg(FR[:, :], 48, 16, [Rsu]); dbg(FI[:, :], 64, 16, [Rsu])
            frb = FR[:, :].unsqueeze(2).broadcast_to([128, 16, 16])
            fib = FI[:, :].unsqueeze(2).broadcast_to([128, 16, 16])
            tt(BBR[:, :, :], BRE[:, :, :], frb, ALU.mult)
            tt(BBI[:, :, :], BIM[:, :, :], fib, ALU.mult)
            tt(BBR[:, :, :], BBR[:, :, :], BBI[:, :, :], ALU.subtract)
            tt(BBI[:, :, :], BRE[:, :, :], fib, ALU.mult)
            tt(BRE[:, :, :], BIM[:, :, :], frb, ALU.mult)
            tt(BBI[:, :, :], BBI[:, :, :], BRE[:, :, :], ALU.add)
            dv(lambda: V.memset(CW[:, :, :, :], 0.0), [], [Rsu])
            for ri, src in ((0, s5_c_re), (1, s5_c_im)):
                S.dma("sp", CNAT[:, :, :], src[l].rearrange("(c g) h p -> (g h) c p", c=2), writes=[Rcn])
                CNB = TTf[:, 1, 0:64].bitcast(BF16)
                dv(lambda: V.tensor_copy(out=CNB.rearrange("p (c k) -> p c k", c=2), in_=CNAT[:, :, :]), [Rcn, Rtt], [Rtt])
                for c in range(2):
                    for half in range(2):
                        S.op("pe", lambda c=c, half=half: nc.tensor.matmul(PS[half * 64:(half + 1) * 64, 6, c * 128:(c + 1) * 128], lhsT=CNB[:, c * 64:(c + 1) * 64], rhs=identb[:, :],
                                                                           start=True, stop=True), reads=[Rtt, Rid], writes=[RPS[6]])
                ctv = PS[:, 6, 0:256].rearrange("q (m g h) -> q m g h", m=8, g=2)
                sc = 1.0 if ri == 0 else -1.0
                dv(lambda ri=ri, sc=sc: V.tensor_scalar(out=CW[0:64, :, ri, 0:16], in0=ctv[0:64, :, 0, :], scalar1=sc, scalar2=None, op0=ALU.mult), [RPS[6]], [Rsu])
                dv(lambda ri=ri, sc=sc: V.tensor_scalar(out=CW[64:128, :, ri, 16:32], in0=ctv[64:128, :, 1, :], scalar1=sc, scalar2=None, op0=ALU.mult), [RPS[6]], [Rsu])
            S.barrier()
            def build_pows(TAB, nent, base_r, base_i):
                dv(lambda: V.memset(TAB[:, 0, :, 0:1], 1.0), [], [Rsu])
                dv(lambda: V.memset(TAB[:, 1, :, 0:1], 0.0), [], [Rsu])
                tt(PW[:, 0, :], base_r, base_r, ALU.max)
                tt(PW[:, 1, :], base_i, base_i, ALU.max)
                nn = 1
                while nn < nent:
                    pr = PW[:, 0, :].unsqueeze(2).broadcast_to([128, 16, nn])
                    pi_ = PW[:, 1, :].unsqueeze(2).broadcast_to([128, 16, nn])
                    t1 = TTf[:, 0, 0:16 * nn].rearrange("p (k j) -> p k j", k=16)
                    t2 = TTf[:, 1, 0:16 * nn].rearrange("p (k j) -> p k j", k=16)
                    rr, ri = Rsu, Rtt
                    tt(t1, TAB[:, 0, :, 0:nn], pr, ALU.mult, (rr, ri), (ri,))
                    tt(t2, TAB[:, 1, :, 0:nn], pi_, ALU.mult, (rr, ri), (ri,))
                    tt(TAB[:, 0, :, nn:2 * nn], t1, t2, ALU.subtract, (rr, ri), (rr,))
                    tt(t1, TAB[:, 0, :, 0:nn], pi_, ALU.mult, (rr, ri), (ri,))
                    tt(t2, TAB[:, 1, :, 0:nn], pr, ALU.mult, (rr, ri), (ri,))
                    tt(TAB[:, 1, :, nn:2 * nn], t1, t2, ALU.add, (rr, ri), (rr,))
                    tt(W1[:, :], PW[:, 0, :], PW[:, 0, :], ALU.mult)
                    tt(W2_[:, :], PW[:, 1, :], PW[:, 1, :], ALU.mult)
                    tt(W3[:, :], PW[:, 0, :], PW[:, 1, :], ALU.mult)
                    tt(PW[:, 0, :], W1[:, :], W2_[:, :], ALU.subtract)
                    tt(PW[:, 1, :], W3[:, :], W3[:, :], ALU.add)
                    nn *= 2
            build_pows(TBs, 32, UR[:, :], UI[:, :])
            tt(W1[:, :], PW[:, 0, :], PW[:, 0, :], ALU.max)
            tt(W2_[:, :], PW[:, 1, :], PW[:, 1, :], ALU.max)
            tt(UBR[:, :], PW[:, 0, :], PW[:, 0, :], ALU.max)
            tt(UBI[:, :], PW[:, 1, :], PW[:, 1, :], ALU.max)
            build_pows(TA_, 16, UBR[:, :], UBI[:, :])
            if BW == 512:
                tt(UBR[:, :], PW[:, 0, :], PW[:, 0, :], ALU.max)
                tt(UBI[:, :], PW[:, 1, :], PW[:, 1, :], ALU.max)
            else:
                tt(UBR[:, :], TA_[:, 0, :, 8], TA_[:, 0, :, 8], ALU.max)
                tt(UBI[:, :], TA_[:, 1, :, 8], TA_[:, 1, :, 8], ALU.max)
            S.barrier()
            mcb = [0]
            for m in range(8):
                cc, m4 = m // 4, m % 4
                for d in range(2):
                    k = m * 2 + d
                    for ri, BB in ((0, BBR), (1, BBI)):
                        dv(lambda: V.memset(WP[:, :], 0.0), [Rwp], [Rwp])
                        dv(lambda BB=BB, k=k: V.tensor_copy(out=WP[0:64, m4 * 32:m4 * 32 + 16], in_=BB[0:64, k, :]), [Rsu, Rwp], [Rwp])
                        dv(lambda BB=BB, k=k: V.tensor_copy(out=WP[64:128, m4 * 32 + 16:m4 * 32 + 32], in_=BB[64:128, k, :]), [Rsu, Rwp], [Rwp])
                        S.op("pe", lambda: nc.tensor.transpose(out=PS[:, 7, 0:128], in_=WP[:, :], identity=ident[:]), reads=[Rwp, Rid], writes=[RPS[7]])
                        S.op("act", lambda d=d, ri=ri: nc.scalar.copy(out=BW_[:, d, ri, :], in_=PS[:, 7, 0:128]), reads=[RPS[7]], writes=[Rbw])
                for d in range(2):
                    k = m * 2 + d
                    rev = d == 1
                    ar = TA_[:, 0, k, :].unsqueeze(2).broadcast_to([128, 16, 32])
                    ai = TA_[:, 1, k, :].unsqueeze(2).broadcast_to([128, 16, 32])
                    br = TBs[:, 0, k, :].unsqueeze(1).broadcast_to([128, 16, 32])
                    bi = TBs[:, 1, k, :].unsqueeze(1).broadcast_to([128, 16, 32])
                    trv = TRI[:, 0, :].rearrange("p (q j) -> p q j", q=16)
                    tiv = TRI[:, 1, :].rearrange("p (q j) -> p q j", q=16)
                    t1 = TTf[:, 0, :].rearrange("p (q j) -> p q j", q=16)
                    t2 = TTf[:, 1, :].rearrange("p (q j) -> p q j", q=16)
                    rw = (Rsu, Rtt, Rtri)
                    tt(t1, ar, br, ALU.mult, rw, (Rtt,))
                    tt(t2, ai, bi, ALU.mult, rw, (Rtt,))
                    tt(trv, t1, t2, ALU.subtract, rw, (Rtri,))
                    tt(t1, ar, bi, ALU.mult, rw, (Rtt,))
                    tt(t2, ai, br, ALU.mult, rw, (Rtt,))
                    tt(tiv, t1, t2, ALU.add, rw, (Rtri,))
                    dv(lambda: V.tensor_copy(out=TRIb[:, :, :], in_=TRI[:, :, :]), [Rtri], [Rtri])
                    if k == 0:
                        dbg(TRI[:, 0, :], 128, 512, [Rtri]); dbg(TRI[:, 1, :], 640, 512, [Rtri])
                    ib = 0
                    if sample:
                        cmul(INI[:, 0, ib:ib + 1], INI[:, 1, ib:ib + 1], UR[:, k:k + 1], UI[:, k:k + 1], S0t[:, k, 0:1], S0t[:, k, 1:2], W1[:, 0:1], W2_[:, 0:1], (Rsu, Rini), (Rsu, Rini))
                    else:
                        dv(lambda: V.memset(INI[:, :, 0:1], 0.0), [Rini], [Rini])
                    blocks = list(range(NB - 1, -1, -1)) if rev else list(range(NB))
                    for bi_, b in enumerate(blocks):
                        cols = slice(b * BW, (b + 1) * BW)
                        bk = 2 * (mcb[0] % 2)
                        mcb[0] += 1
                        for ri in range(2):
                            S.op("pe", lambda ri=ri: nc.tensor.matmul(PS[:, bk + ri, 0:BW], lhsT=BW_[:, d, ri, :], rhs=UT[:, cc, cols], start=True, stop=True),
                                 reads=[Rbw, Rut], writes=[RPS[bk + ri]])
                        if rev:
                            trr, tri = TRI[:, 0, BW - 1::-1] if BW == 512 else TRI[:, 0, BW - 1::-1], TRI[:, 1, BW - 1::-1]
                            trr = TRI[:, 0, 0:BW][:, ::-1]
                            tri = TRI[:, 1, 0:BW][:, ::-1]
                            trrb = TRIb[:, 0, 0:BW][:, ::-1]
                            trib = TRIb[:, 1, 0:BW][:, ::-1]
                        else:
                            trr, tri = TRI[:, 0, 0:BW], TRI[:, 1, 0:BW]
                            trrb, trib = TRIb[:, 0, 0:BW], TRIb[:, 1, 0:BW]
                        for ri in range(2):
                            S.op("act", lambda ri=ri: nc.scalar.copy(out=SBb[:, ri, 0:BW], in_=PS[:, bk + ri, 0:BW]), reads=[RPS[bk + ri]], writes=[Rsbb])
                        pre, pim = SBb[:, 0, 0:BW], SBb[:, 1, 0:BW]
                        rr = (Rtri, Rsbb, Rtt, Rbz)
                        tt(TT[:, 0, 0:BW], pre, trrb, ALU.mult, rr, (Rtt,))
                        tt(TT[:, 1, 0:BW], pim, trib, ALU.mult, rr, (Rtt,))
                        tt(BZ[:, 0, 0:BW], TT[:, 0, 0:BW], TT[:, 1, 0:BW], ALU.add, rr, (Rbz,))
                        tt(TT[:, 0, 0:BW], pim, trrb, ALU.mult, rr, (Rtt,))
                        tt(TT[:, 1, 0:BW], pre, trib, ALU.mult, rr, (Rtt,))
                        tt(BZ[:, 1, 0:BW], TT[:, 0, 0:BW], TT[:, 1, 0:BW], ALU.subtract, rr, (Rbz,))
                        if k == 0 and bi_ == 0:
                            dbg(BZ[:, 0, 0:BW], 1152, BW, [Rbz]); dbg(BZ[:, 1, 0:BW], 1664, BW, [Rbz])
                        for ri in range(2):
                            zo = TT[:, ri, 0:BW]
                            zin = BZ[:, ri, 0:BW]
                            if rev:
                                zo, zin = zo[:, ::-1], zin[:, ::-1]
                            dv(lambda zo=zo, zin=zin, ri=ri: V.tensor_tensor_scan(out=zo, data0=MAG[:, k:k + 1].broadcast_to([128, BW]), data1=zin, initial=INI[:, ri, ib:ib + 1], op0=ALU.mult, op1=ALU.add),
                               [Rsu, Rbz, Rini, Rtt], [Rtt])
                        if k == 0 and bi_ == 0:
                            dbg(TT[:, 0, 0:BW], 2176, BW, [Rtt]); dbg(TT[:, 1, 0:BW], 2688, BW, [Rtt])
                        zl = 0 if rev else BW - 1
                        zr_, zi_ = TT[:, 0, zl:zl + 1], TT[:, 1, zl:zl + 1]
                        last_blk = bi_ == NB - 1 or multi
                        if multi and bi_ != NB - 1:
                            dv(lambda: V.memset(INI[:, :, 1 - ib:2 - ib], 0.0), [Rini], [Rini])
                        if not last_blk:
                            ib2 = 1 - ib
                            rc, wc = [Rsu, Rini, Rtt], [Rsu, Rini]
                            dv(lambda: V.tensor_scalar(out=W1[:, 0:1], in0=zi_, scalar1=UBI[:, k:k + 1], scalar2=None, op0=ALU.mult), rc, wc)
                            dv(lambda: V.scalar_tensor_tensor(out=INI[:, 0, ib2:ib2 + 1], in0=zr_, scalar=UBR[:, k:k + 1], in1=W1[:, 0:1], op0=ALU.mult, op1=ALU.subtract), rc, wc)
                            dv(lambda: V.tensor_scalar(out=W2_[:, 0:1], in0=zr_, scalar1=UBI[:, k:k + 1], scalar2=None, op0=ALU.mult), rc, wc)
                            dv(lambda: V.scalar_tensor_tensor(out=INI[:, 1, ib2:ib2 + 1], in0=zi_, scalar=UBR[:, k:k + 1], in1=W2_[:, 0:1], op0=ALU.mult, op1=ALU.add), rc, wc)
                            ib = ib2
                        elif not sample:
                            pf = b if multi else 0
                            cmul(FIN[:, pf, k, 0:1], FIN[:, pf, k, 1:2], TRI[:, 0, BW - 1:BW], TRI[:, 1, BW - 1:BW], zr_, zi_, W1[:, 0:1], W2_[:, 0:1], (Rsu, Rtri, Rtt, Rfin), (Rsu, Rfin))
                        if multi and bi_ != NB - 1:
                            ib = 1 - ib
                        dstS = SBb[:, :, 0:BW]
                        Rdst = Rsbb
                        rr2 = (Rtri, Rtt, Rbz)
                        tt(BZ[:, 0, 0:BW], TT[:, 0, 0:BW], trrb, ALU.mult, rr2, (Rbz,))
                        tt(BZ[:, 1, 0:BW], TT[:, 1, 0:BW], trib, ALU.mult, rr2, (Rbz,))
                        tt(dstS[:, 0, :], BZ[:, 0, 0:BW], BZ[:, 1, 0:BW], ALU.subtract, (Rbz, Rdst), (Rdst,))
                        tt(BZ[:, 0, 0:BW], TT[:, 1, 0:BW], trrb, ALU.mult, rr2, (Rbz,))
                        tt(BZ[:, 1, 0:BW], TT[:, 0, 0:BW], trib, ALU.mult, rr2, (Rbz,))
                        tt(dstS[:, 1, :], BZ[:, 0, 0:BW], BZ[:, 1, 0:BW], ALU.add, (Rbz, Rdst), (Rdst,))
                        def emit_y():
                            nc.tensor.matmul(PS[0:32, 4, 0:BW], lhsT=CW[:, m, 0, :], rhs=SBb[:, 0, 0:BW], start=True, stop=False)
                            return nc.tensor.matmul(PS[0:32, 4, 0:BW], lhsT=CW[:, m, 1, :], rhs=SBb[:, 1, 0:BW], start=False, stop=True)
                        S.op("pe", emit_y, reads=[Rsu, Rsbb], writes=[RPS[4]])
                        if not rev:
                            S.op("act", lambda: nc.scalar.copy(out=YFp[0:32, cols], in_=PS[0:32, 4, 0:BW]), reads=[RPS[4]], writes=[Rsfs])
                        else:
                            tt(OTs[0:32, 0:BW], PS[0:32, 4, 0:BW], YFp[0:32, cols], ALU.add, (RPS[4], Rsfs, Rots), (Rots,))
                            S.dma("sp", YS[m4 * 32:(m4 + 1) * 32, cc, cols], OTs[0:32, 0:BW], reads=[Rots], writes=[Rys], key=Reg("s_OTd"))
            if not sample:
                for pi in ((0, 1) if multi else ((0 if kind == "pA" else 1),)):
                    pf = pi if multi else 0
                    for d in range(2):
                        OUT_EVS.append(S.dma("sp", o_s5[pi, l, d].rearrange("(m g) p r -> (g p) m r", g=2), FIN[:, pf, d::2, :], reads=[Rfin], key=Reg("o_s5_d"), slow=True))
            S.barrier()
            Z, _ = view(oZ, [128, 2, 2048], BF16)
            Rz_ = Reg("s_Z")
            for b in range(NB):
                cols = slice(b * BW, (b + 1) * BW)
                for cc in range(2):
                    yv_ = YV[:, 0:BW]
                    dv(lambda: V.scalar_tensor_tensor(out=yv_, in0=UT[:, cc, cols], scalar=PV[:, 10 + cc:11 + cc], in1=YS[:, cc, cols], op0=ALU.mult, op1=ALU.add),
                       [Rut, Rpv, Rys, Rbz], [Rbz])
                    tt(TTf[:, 0, 0:BW], yv_, yv_, ALU.mult, (Rbz, Rtt), (Rtt,))
                    dv(lambda: V.tensor_scalar(out=TTf[:, 0, 0:BW], in0=TTf[:, 0, 0:BW], scalar1=0.044715, scalar2=1.0, op0=ALU.mult, op1=ALU.add), [Rtt], [Rtt])
                    tt(TTf[:, 0, 0:BW], TTf[:, 0, 0:BW], yv_, ALU.mult, (Rbz, Rtt), (Rtt,))
                    S.op("act", lambda: nc.scalar.activation(out=TTf[:, 1, 0:BW], in_=TTf[:, 0, 0:BW], func=AF.Sigmoid, scale=1.5957691216057308), reads=[Rtt], writes=[Rtt])
                    tt(Z[:, cc, cols], TTf[:, 1, 0:BW], yv_, ALU.mult, (Rbz, Rtt, Rz_), (Rz_,))
                for co in range(2):
                    def emit_g(co=co):
                        nc.tensor.matmul(PS[:, 5, 0:BW], lhsT=WGL[:, 0, co * 128:(co + 1) * 128], rhs=Z[:, 0, cols], start=True, stop=False)
                        return nc.tensor.matmul(PS[:, 5, 0:BW], lhsT=WGL[:, 1, co * 128:(co + 1) * 128], rhs=Z[:, 1, cols], start=False, stop=True)
                    S.op("pe", emit_g, reads=[Rsu, Rz_], writes=[RPS[5]])
                    S.op("act", lambda: nc.scalar.activation(out=OTs[:, 0:BW], in_=PS[:, 5, 0:BW], func=AF.Sigmoid), reads=[RPS[5]], writes=[Rots])
                    tt(cur["BR"][:, co, cols], Z[:, co, cols], OTs[:, 0:BW], ALU.mult, (Rz_, Rots), (Rbr[co],))

        DBG = {}

        def mixer(l):
            make_gate_bcast(1)
            mixer_params(l)

            def build_ht(t0, ntile, ci):
                S.barrier()
                cur["XN"], _ = view(BS0, [128, 2, D])
                for tt in range(ntile):
                    prenorm_tile(t0 + tt, ci, 1, HT[:, :, tt * 128:(tt + 1) * 128], Rht, (4 + 2 * (tt % 2), 5 + 2 * (tt % 2)))
                S.barrier()

            def zero_disabled(T):
                for nm, chs in (("s5", (0, 1)), ("ret", (2, 3)), ("conv", (4, 5)), ("mla", (6, 7))):
                    if not cfg.get(nm, True):
                        for i in chs:
                            S.op("dve", lambda i=i: nc.vector.memset(BR[:, i, 0:T], 0.0), writes=[Rbr[i]])

            def seq_branches(t0, T, NB, BW, kind):
                if cfg.get("conv", True):
                    branch_conv(l, T, NB, BW)
                    S.barrier()
                if cfg.get("mla", True):
                    branch_mla(l, t0, T, NB, BW, kind)
                    S.barrier()
                if cfg.get("ret", True):
                    branch_ret(l, t0, T, kind)
                    S.barrier()

            cur["HT"], cur["BR"] = HT, BR
            build_ht(0, 16, 0)
            zero_disabled(2048)
            seq_branches(0, 2048, 4, 512, "sample")
            if cfg.get("s5", True):
                branch_s5(l, 0, 2048, 4, 512, "sample")
                S.barrier()
            if tuple(cfg.get("dump_br", ())) == (l, "sample"):
                dbg = nc.dram_tensor("dbg_br", [128, 8, 2048], BF16, kind="ExternalOutput").ap()
                S.dma("sp", dbg[:, :, 0:2048], BR[:, :, 0:2048], reads=Rbr, key=Reg("dbg"))
            gate_stage(l, 0, 16, 0, 2048, 4, 512)
            S.barrier()
            build_ht(16, 4, 1)
            zero_disabled(512)
            for pi, kind in ((0, "pA"), (1, "pB")):
                cur["HT"], cur["BR"] = HT[:, :, pi * 256:(pi + 1) * 256], BR[:, :, pi * 256:(pi + 1) * 256]
                mla_cache_out(l, 16 + 2 * pi, 2, pi)
                S.barrier()
                seq_branches(16 + 2 * pi, 256, 1, 256, kind)
            cur["HT"], cur["BR"] = HT, BR
            if cfg.get("s5", True):
                branch_s5(l, 16, 512, 2, 256, "pAB", multi=True)
                S.barrier()
            gate_stage(l, 16, 4, 1, 512, 1, 512)
            S.barrier()
            cur["XN"], cur["TMP"] = XN, TMP

        for l in range(LAYERS):
            S.barrier()
            compute_mod(l)
            S.barrier()
            if cfg.get("ffn1", True):
                ffn(l, 0, 0)
            S.barrier()
            if cfg.get("mixer", True):
                mixer(l)
            S.barrier()
            if cfg.get("ffn2", True):
                ffn(l, 1, 2)

        yv = y.rearrange("(t p) d -> p t d", p=128)
        Ryout = Reg("yout")
        evs = []
        for t in range(NT):
            evs.append(S.dma("sp", yv[:, t, :], X[:, t, :], reads=[RX[t]], key=Ryout))
        S._wait("sp", set([evs[-1]] + OUT_EVS))
        S.barrier()
    return nc


def _axial(T, dim):
    rows = T // 64
    row = np.repeat(np.arange(rows, dtype=np.float32), 64)
    col = np.tile(np.arange(64, dtype=np.float32), rows)
    quarter = dim // 4
    inv = (np.float32(10000.0) ** (-np.arange(quarter, dtype=np.float32) / np.float32(quarter))).astype(np.float32)
    ang = np.concatenate([row[:, None] * inv, col[:, None] * inv], axis=-1).astype(np.float32)
    return np.cos(ang).astype(np.float32), np.sin(ang).astype(np.float32)


def _rope_tables():
    c, s = _axial(2048, 32)
    mla = np.stack([np.concatenate([c.T, c.T], 0), np.concatenate([s.T, s.T], 0)], 0)
    c2, s2 = _axial(2048, 64)
    ret = np.stack([c2, s2], 0)
    return np.ascontiguousarray(mla, dtype=np.float32), np.ascontiguousarray(ret, dtype=np.float32)


def _prep_inputs(inputs):
    f = lambda a: np.ascontiguousarray(np.asarray(a, dtype=np.float32))
    shared = {k: f(inputs[k]) for k in (
        "w_mod", "b_mod", "norm_pre", "norm_post", "ffn_w1", "ffn_w3", "ffn_w2", "w_in", "s5_lam_re", "s5_lam_im", "s5_log_dt",
        "s5_b_re", "s5_b_im", "s5_c_re", "s5_c_im", "s5_d", "s5_w_glu", "ret_decay", "ret_gn", "conv_w", "conv_b",
        "mla_q_norm", "mla_w_uq", "mla_kv_norm", "mla_w_ukv", "w_branch", "w_gate", "b_gate", "w_o")}
    shared["c_ident"] = np.eye(128, dtype=np.float32)
    shared["c_rope_mla"], shared["c_rope_ret"] = _rope_tables()
    jj = np.arange(128, dtype=np.float32)[:, None]
    ii = np.arange(128, dtype=np.float32)[None, :]
    diff = ii - jj
    shared["c_ret"] = np.ascontiguousarray(np.stack([
        np.maximum(diff, 0), np.maximum(-diff, 0), 0.125 * (diff >= 0), 0.125 * (diff < 0),
        np.broadcast_to(ii + 1.0, (128, 128)), np.broadcast_to(128.0 - ii, (128, 128))], 0), dtype=np.float32)
    shared["c_pidx"] = np.ascontiguousarray(np.concatenate([127.0 - jj, jj], 1), dtype=np.float32)
    xp = f(inputs["x_prompt"])
    xs = f(inputs["x_sample"])
    c = f(inputs["c"])
    cctx = f(inputs["c_ctx"])
    maps = []
    for i in range(NCORES):
        m = dict(shared)
        m["xin"] = np.ascontiguousarray(np.concatenate([xs[i], xp[2 * i], xp[2 * i + 1]], axis=0))
        m["cond2"] = np.ascontiguousarray(np.stack([c[i], cctx], axis=0))
        m["st_s5"] = f(inputs["state_s5"][i])
        m["st_ret"] = f(inputs["state_ret"][i])
        m["ctx_mla"] = f(inputs["cache_mla"][i])
        maps.append(m)
    return maps


def _gather(res):
    ys = np.stack([r["y"][:2048] for r in res], axis=0)
    yp = np.stack([r["y"][2048 + 256 * j:2048 + 256 * (j + 1)] for r in res for j in range(2)], axis=0)
    s5 = np.concatenate([r["o_s5"] for r in res], axis=0)
    ret = np.concatenate([r["o_ret"] for r in res], axis=0)
    mla = np.concatenate([r["o_mla"] for r in res], axis=0)
    return (yp.astype(np.float32), ys.astype(np.float32), s5.astype(np.float32), ret.astype(np.float32), mla.astype(np.float32))


CFG = {}


def kernel(**inputs):
    nc = build(CFG)
    maps = _prep_inputs(inputs)
    res = run_bass_kernel_spmd(nc, maps, core_ids=list(range(NCORES)))
    return _gather(res.results)
```

```python
import numpy as np
import concourse.bass as bass
import concourse.mybir as mybir
from concourse.bass_utils import run_bass_kernel_spmd
from contextlib import ExitStack

F32 = mybir.dt.float32
BF16 = mybir.dt.bfloat16
AF = mybir.ActivationFunctionType
ALU = mybir.AluOpType
AX = mybir.AxisListType

D = 1024
DFF = 2816
NFF = 22
TOK = 2560
NT = 20
EPS = 1e-6
NCORES = 8
INC = 2400

SAME_ENG_SYNC = True


_REGS = {}


def Reg(name):
    if name not in _REGS:
        _REGS[name] = _Reg(name)
    return _REGS[name]


class _Reg:
    __slots__ = ("name", "w", "r", "dsem", "dcnt")

    def __init__(self, name):
        self.name = name
        self.w = None
        self.r = []
        self.dsem = None
        self.dcnt = 0


class Sched:
    def __init__(self, nc, es):
        self.nc = nc
        self.es = es
        self.eng = {"pe": nc.tensor, "act": nc.scalar, "dve": nc.vector, "pool": nc.gpsimd, "sp": nc.sync}
        self.sem = {e: es.enter_context(nc.semaphore("s_" + e)) for e in self.eng}
        self.cnt = {e: 0 for e in self.eng}
        self.seen = {e: {} for e in self.eng}
        self.seen_d = {e: {} for e in self.eng}
        self.nsem = 0
        self.out_events = []
        self.pending_reads = {}

    def _wait(self, e, deps, raw=None):
        best = {}
        bestd = {}
        for d in deps:
            if d[0] == "e":
                _, e2, c = d
                if e2 == e and (e == "pe" or not SAME_ENG_SYNC or (raw is not None and d not in raw)):
                    continue
                if c > best.get(e2, 0):
                    best[e2] = c
            else:
                _, sem, tgt, key = d
                if tgt > bestd.get(key, (None, 0))[1]:
                    bestd[key] = (sem, tgt)
        E = self.eng[e]
        for e2, c in best.items():
            if self.seen[e].get(e2, 0) >= c:
                continue
            E.wait_ge(self.sem[e2], c)
            self.seen[e][e2] = c
        for key, (sem, tgt) in bestd.items():
            if self.seen_d[e].get(key, 0) >= tgt:
                continue
            E.wait_ge(sem, tgt)
            self.seen_d[e][key] = tgt

    def _deps(self, reads, writes):
        deps = set()
        for r in reads:
            if r.w is not None:
                deps.add(r.w)
        for w in writes:
            if w.w is not None:
                deps.add(w.w)
            deps.update(w.r)
        return deps

    def op(self, e, emit, reads=(), writes=()):
        raw = set(r.w for r in reads if r.w is not None)
        self._wait(e, self._deps(reads, writes), raw)
        inst = emit()
        self.cnt[e] += 1
        inst.then_inc(self.sem[e], 1)
        ev = ("e", e, self.cnt[e])
        for r in reads:
            r.r.append(ev)
        for w in writes:
            w.w = ev
            w.r = []
        return ev

    def dma(self, e, out, in_, reads=(), writes=(), key=None, slow=False):
        key = key or (list(writes) + list(reads))[0]
        if key.dsem is None:
            key.dsem = self.es.enter_context(self.nc.semaphore("d%d" % self.nsem))
            self.nsem += 1
        deps = set(d for d in self._deps(reads, writes) if not (d[0] == "d" and d[3] == key.name))
        self._wait(e, deps)
        inst = self.eng[e].dma_start(out=out, in_=in_, allow_slow_non_contiguous=True) if slow else self.eng[e].dma_start(out=out, in_=in_)
        key.dcnt += 16
        inst.then_inc(key.dsem, 16)
        ev = ("d", key.dsem, key.dcnt, key.name)
        if reads:
            self.pending_reads[key.name] = ev
        for r in reads:
            r.r.append(ev)
        for w in writes:
            w.w = ev
            w.r = []
        return ev

    def barrier(self):
        pend = set(self.pending_reads.values())
        for e in self.eng:
            deps = set(("e", e2, self.cnt[e2]) for e2 in self.eng if e2 != e and self.cnt[e2] > 0)
            self._wait(e, deps | pend)
        self.pending_reads = {}


def build(cfg):
    _REGS.clear()
    nc = bass.Bass("TRN2", target_bir_lowering=False)
    LAYERS = cfg.get("layers", 2)

    def din(name, shape):
        return nc.dram_tensor(name, list(shape), F32, kind="ExternalInput").ap()

    def dout(name, shape):
        return nc.dram_tensor(name, list(shape), F32, kind="ExternalOutput").ap()

    xin = din("xin", [TOK, D])
    cond2 = din("cond2", [2, D])
    st_s5 = din("st_s5", [2, 2, 16, 64, 2])
    st_ret = din("st_ret", [2, 2, 4, 64, 64])
    ctx_mla = din("ctx_mla", [2, 512, 160])
    w_mod = din("w_mod", [2, D, 9 * D])
    b_mod = din("b_mod", [2, 9 * D])
    norm_pre = din("norm_pre", [2, 3, D])
    norm_post = din("norm_post", [2, 3, D])
    ffn_w1 = din("ffn_w1", [2, 2, D, DFF])
    ffn_w3 = din("ffn_w3", [2, 2, D, DFF])
    ffn_w2 = din("ffn_w2", [2, 2, DFF, D])
    w_in = din("w_in", [2, D, INC])
    s5_lam_re = din("s5_lam_re", [2, 2, 16, 64])
    s5_lam_im = din("s5_lam_im", [2, 2, 16, 64])
    s5_log_dt = din("s5_log_dt", [2, 2, 16])
    s5_b_re = din("s5_b_re", [2, 2, 16, 64, 16])
    s5_b_im = din("s5_b_im", [2, 2, 16, 64, 16])
    s5_c_re = din("s5_c_re", [2, 16, 16, 64])
    s5_c_im = din("s5_c_im", [2, 16, 16, 64])
    s5_d = din("s5_d", [2, 256])
    s5_w_glu = din("s5_w_glu", [2, 256, 256])
    ret_decay = din("ret_decay", [2, 2, 4])
    ret_gn = din("ret_gn", [2, 256])
    conv_w = din("conv_w", [2, 3, 256])
    conv_b = din("conv_b", [2, 256])
    mla_q_norm = din("mla_q_norm", [2, 192])
    mla_w_uq = din("mla_w_uq", [2, 192, 384])
    mla_kv_norm = din("mla_kv_norm", [2, 128])
    mla_w_ukv = din("mla_w_ukv", [2, 128, 512])
    w_branch = din("w_branch", [2, 4, 256, D])
    w_gate = din("w_gate", [2, D, 4 * D])
    b_gate = din("b_gate", [2, 4 * D])
    w_o = din("w_o", [2, D, D])
    c_ident = din("c_ident", [128, 128])
    c_rope_mla = din("c_rope_mla", [2, 32, 2048])
    c_rope_ret = din("c_rope_ret", [2, 2048, 32])
    c_ret = din("c_ret", [6, 128, 128])
    c_pidx = din("c_pidx", [128, 2])

    y = dout("y", [TOK, D])
    o_s5 = dout("o_s5", [2, 2, 2, 16, 64, 2])
    o_ret = dout("o_ret", [2, 2, 2, 4, 64, 64])
    o_mla = dout("o_mla", [2, 2, 256, 160])

    es = ExitStack()
    with es:
        S = Sched(nc, es)

        def sb(name, shape, dt=F32):
            return es.enter_context(nc.sbuf_tensor(name, list(shape), dt))

        X = sb("X", [128, NT, D])
        RX = [Reg("X%d" % t) for t in range(NT)]
        PS = es.enter_context(nc.psum_tensor("PS", [128, 8, 512], F32))
        RPS = [Reg("PS%d" % b) for b in range(8)]
        ident = sb("ident", [128, 128])
        identb = sb("identb", [128, 128], BF16)
        Rid = Reg("ident")
        VEC = sb("VEC", [128, 2, 72])
        Rvec = Reg("VEC")
        NRM = sb("NRM", [128, 48])
        Rnrm = Reg("NRM")
        SV = sb("SV", [128, 2, 3, 8])
        Rsv = Reg("SV")
        GV = sb("GV", [128, 2, 3, 8])
        Rgv = Reg("GV")
        GB = sb("GB", [128, 2, D])
        Rgb = [Reg("GB0"), Reg("GB1")]
        DG = sb("DG", [128, 2, 128])
        Rdg = [Reg("DG0"), Reg("DG1")]
        small = sb("small", [128, 4, 4])
        Rsmall = [Reg("sm%d" % i) for i in range(4)]
        junk = sb("junk", [128, D], BF16)
        Rjunk = Reg("junk")
        ACOLS = 28672
        ARENA = sb("ARENA", [128, ACOLS])

        def view(off, shape, dt=F32):
            n = int(np.prod(shape[1:]))
            nbytes = n * (2 if dt == BF16 else 4)
            assert off % 4 == 0 and nbytes % 4 == 0 and off + nbytes <= ACOLS * 4, (off, shape)
            ap = ARENA[:, off // 4:(off + nbytes) // 4]
            if dt == BF16:
                ap = ap.bitcast(BF16)
            if len(shape) > 2:
                names = "abcdef"[:len(shape) - 1]
                pat = "p (" + " ".join(names) + ") -> p " + " ".join(names)
                ap = ap.rearrange(pat, **{names[i]: shape[i + 1] for i in range(len(names) - 1)})
            return ap, off + nbytes

        HTB, _o = view(0, [128, 2, 8, 512], BF16)
        GT, _o = view(_o, [128, NFF, 512], BF16)
        W13, _o = view(_o, [128, 3, 2, 8, 256], BF16)
        W2, _o = view(_o, [128, 3, 2, D], BF16)
        SIL, _o = view(_o, [128, 2, 512], BF16)
        XN, _o = view(_o, [128, 2, D])
        TMP, _o = view(_o, [128, 2, D])
        WM, _ = view(0, [128, 2, 8, 512], BF16)
        Rxn = [Reg("XN0"), Reg("XN1")]

        xv = xin.rearrange("(t p) d -> p t d", p=128)
        Rxall = Reg("xall")
        for t in range(NT):
            S.dma("sp", X[:, t, :], xv[:, t, :], writes=[RX[t]], key=Rxall)
        for t in range(NT):
            RX[t].w = ("d", Rxall.dsem, Rxall.dcnt, Rxall.name)
        S.dma("sp", ident[:], c_ident[:, :], writes=[Rid])
        S.op("dve", lambda: nc.vector.tensor_copy(out=identb[:], in_=ident[:]), reads=[Rid], writes=[Rid])

        rot = {"ps": 0, "sm": 0, "xn": 0, "dg": 0}
        OUT_EVS = []

        stage = sb("stage", [128, 128])
        Rstage = Reg("stage")

        def load_T(dst_ap, src_ap, rows, dst_reg, bank=7):
            S.dma("sp", stage[0:rows, :], src_ap, writes=[Rstage])
            S.op("pe", lambda: nc.tensor.transpose(out=PS[:, bank, 0:rows], in_=stage[0:rows, :], identity=ident[0:rows, 0:rows]),
                 reads=[Rstage, Rid], writes=[RPS[bank]])
            S.op("dve", lambda: nc.vector.tensor_copy(out=dst_ap, in_=PS[:, bank, 0:rows]), reads=[RPS[bank]], writes=[dst_reg])

        SCT = sb("SCT", [128, 8, 2], BF16)
        Rsct = Reg("SCT")
        sct32 = sb("sct32", [128, 16])
        load_T(sct32[:, :], cond2.rearrange("c (k p) -> (c k) p", p=128), 16, Rsct)
        S.op("act", lambda: nc.scalar.activation(out=SCT[:].rearrange("p k c -> p c k"), in_=sct32[:].rearrange("p (c k) -> p c k", c=2), func=AF.Silu),
             reads=[Rsct], writes=[Rsct])

        Rwm = [Reg("WM0"), Reg("WM1")]
        BM = sb("BM", [128, 72])
        Rbm = Reg("BM")

        def compute_mod(l):
            load_T(BM[:, :], b_mod[l].rearrange("(c p) -> c p", p=128), 72, Rbm)
            load_T(NRM[:, 0:24], norm_pre[l].rearrange("s (c p) -> (s c) p", p=128), 24, Rnrm)
            load_T(NRM[:, 24:48], norm_post[l].rearrange("s (c p) -> (s c) p", p=128), 24, Rnrm)
            wv = w_mod[l].rearrange("(kc p) n -> p kc n", p=128)
            for cb in range(18):
                sl = cb % 2
                S.dma("pool", WM[:, sl, :, :], wv[:, :, cb * 512:(cb + 1) * 512], writes=[Rwm[sl]])
                bank = 6

                def emit(cb=cb, sl=sl):
                    inst = None
                    for cc in range(4):
                        for kc in range(8):
                            inst = nc.tensor.matmul(PS[:, bank, cc * 2:cc * 2 + 2], lhsT=WM[:, sl, kc, cc * 128:(cc + 1) * 128],
                                                    rhs=SCT[:, kc, :], start=(kc == 0), stop=(kc == 7))
                    return inst
                S.op("pe", emit, reads=[Rwm[sl], Rsct], writes=[RPS[bank]])
                S.op("dve", lambda cb=cb: nc.vector.tensor_tensor(
                    out=VEC[:, :, cb * 4:(cb + 1) * 4].rearrange("p c j -> p j c"),
                    in0=PS[:, bank, 0:8].rearrange("p (j c) -> p j c", c=2),
                    in1=BM[:, cb * 4:(cb + 1) * 4].unsqueeze(2).broadcast_to([128, 4, 2]), op=ALU.add),
                    reads=[RPS[bank], Rbm], writes=[Rvec])
            for ci in range(2):
                for s in range(3):
                    S.op("dve", lambda ci=ci, s=s: nc.vector.scalar_tensor_tensor(
                        out=SV[:, ci, s, :], in0=VEC[:, ci, (3 * s + 1) * 8:(3 * s + 2) * 8], scalar=1.0,
                        in1=NRM[:, s * 8:(s + 1) * 8], op0=ALU.add, op1=ALU.mult), reads=[Rvec, Rnrm], writes=[Rsv])
                    fac = 1.0 if s == 1 else 0.5
                    S.op("dve", lambda ci=ci, s=s, fac=fac: nc.vector.scalar_tensor_tensor(
                        out=GV[:, ci, s, :], in0=VEC[:, ci, (3 * s + 2) * 8:(3 * s + 3) * 8], scalar=fac,
                        in1=NRM[:, 24 + s * 8:24 + (s + 1) * 8], op0=ALU.mult, op1=ALU.mult), reads=[Rvec, Rnrm], writes=[Rgv])

        def make_gate_bcast(s):
            for ci in range(2):
                bank0 = 4 + 2 * ci
                for c in range(8):
                    dgi = rot["dg"] % 2
                    rot["dg"] += 1
                    S.op("dve", lambda c=c, ci=ci, dgi=dgi: nc.vector.tensor_scalar(
                        out=DG[:, dgi, :], in0=ident[:], scalar1=GV[:, ci, s, c:c + 1], scalar2=None, op0=ALU.mult),
                        reads=[Rid, Rgv], writes=[Rdg[dgi]])
                    bank = bank0 + c // 4
                    S.op("pe", lambda c=c, dgi=dgi, bank=bank: nc.tensor.matmul(
                        PS[:, bank, (c % 4) * 128:(c % 4 + 1) * 128], lhsT=ones32[:], rhs=DG[:, dgi, :], start=True, stop=True),
                        reads=[Rdg[dgi], Rones], writes=[RPS[bank]])
                S.op("dve", lambda ci=ci, bank0=bank0: nc.vector.tensor_copy(
                    out=GB[:, ci, :], in_=PS[:, bank0:bank0 + 2, :].rearrange("p b n -> p (b n)")),
                    reads=[RPS[bank0], RPS[bank0 + 1]], writes=[Rgb[ci]])

        ones32 = sb("ones32", [128, 128])
        Rones = Reg("ones")
        S.op("dve", lambda: nc.vector.memset(ones32[:], 1.0), writes=[Rones])

        cur = {"XN": XN, "TMP": TMP, "HT": None, "BR": None}

        def prenorm_p1(t):
            XN = cur["XN"]
            smi = rot["sm"] % 4
            rot["sm"] += 1
            xi = rot["xn"] % 2
            rot["xn"] += 1
            sm = small[:, smi, :]
            S.op("act", lambda: nc.scalar.activation(out=junk[:], in_=X[:, t, :], func=AF.Square, accum_out=sm[:, 0:1]),
                 reads=[RX[t]], writes=[Rjunk, Rsmall[smi]])
            S.op("act", lambda: nc.scalar.activation(out=sm[:, 1:2], in_=sm[:, 0:1], func=AF.Sqrt, scale=1.0 / D, bias=epsb[:, 0:1]),
                 reads=[Rsmall[smi], Reps], writes=[Rsmall[smi]])
            S.op("dve", lambda: nc.vector.reciprocal(out=sm[:, 2:3], in_=sm[:, 1:2]), reads=[Rsmall[smi]], writes=[Rsmall[smi]])
            S.op("dve", lambda: nc.vector.tensor_scalar(out=XN[:, xi, :], in0=X[:, t, :], scalar1=sm[:, 2:3], scalar2=None, op0=ALU.mult),
                 reads=[RX[t], Rsmall[smi]], writes=[Rxn[xi]])
            return xi

        def prenorm_p2(xi, ci, s, dst, dst_reg, banks):
            XN = cur["XN"]
            b0, b1 = banks

            def emit():
                inst = None
                for c in range(8):
                    bk = b0 if c < 4 else b1
                    inst = nc.tensor.transpose(out=PS[:, bk, (c % 4) * 128:(c % 4 + 1) * 128], in_=XN[:, xi, c * 128:(c + 1) * 128], identity=ident[:])
                return inst
            S.op("pe", emit, reads=[Rxn[xi], Rid], writes=[RPS[b0], RPS[b1]])
            for c in range(8):
                bk = b0 if c < 4 else b1
                S.op("act", lambda c=c, bk=bk: nc.scalar.activation(
                    out=dst[:, c, :], in_=PS[:, bk, (c % 4) * 128:(c % 4 + 1) * 128], func=AF.Identity,
                    scale=SV[:, ci, s, c:c + 1], bias=VEC[:, ci, 3 * s * 8 + c:3 * s * 8 + c + 1]),
                    reads=[RPS[bk], Rsv, Rvec], writes=[dst_reg])

        def prenorm_tile(t, ci, s, dst, dst_reg, banks):
            prenorm_p2(prenorm_p1(t), ci, s, dst, dst_reg, banks)

        epsb = sb("epsb", [128, 1])
        halfpi = sb("halfpi", [128, 1])
        Reps = Reg("eps")
        S.op("dve", lambda: nc.vector.memset(epsb[:], EPS), writes=[Reps])
        S.op("dve", lambda: nc.vector.memset(halfpi[:], float(np.pi / 2)), writes=[Reps])

        Rtmp = [Reg("TMP0"), Reg("TMP1")]

        def postnorm_tile(t, ci, b0):
            TMP = cur["TMP"]
            smi = rot["sm"] % 4
            rot["sm"] += 1
            ti = rot["xn"] % 2
            rot["xn"] += 1
            sm = small[:, smi, :]
            fin = PS[:, b0:b0 + 2, :].rearrange("p b n -> p (b n)")
            S.op("act", lambda: nc.scalar.activation(out=junk[:], in_=fin, func=AF.Square, accum_out=sm[:, 0:1]),
                 reads=[RPS[b0], RPS[b0 + 1]], writes=[Rjunk, Rsmall[smi]])
            S.op("act", lambda: nc.scalar.activation(out=sm[:, 1:2], in_=sm[:, 0:1], func=AF.Sqrt, scale=1.0 / D, bias=epsb[:, 0:1]),
                 reads=[Rsmall[smi], Reps], writes=[Rsmall[smi]])
            S.op("dve", lambda: nc.vector.reciprocal(out=sm[:, 2:3], in_=sm[:, 1:2]), reads=[Rsmall[smi]], writes=[Rsmall[smi]])
            S.op("dve", lambda: nc.vector.scalar_tensor_tensor(out=TMP[:, ti, :], in0=fin, scalar=sm[:, 2:3], in1=GB[:, ci, :],
                                                               op0=ALU.mult, op1=ALU.mult),
                 reads=[RPS[b0], RPS[b0 + 1], Rsmall[smi], Rgb[ci]], writes=[Rtmp[ti]])
            S.op("dve", lambda: nc.vector.tensor_tensor(out=X[:, t, :], in0=X[:, t, :], in1=TMP[:, ti, :], op=ALU.add),
                 reads=[RX[t], Rtmp[ti]], writes=[RX[t]])

        Rhtb = [Reg("HTB0"), Reg("HTB1")]
        Rgt = [Reg("GT%d" % j) for j in range(NFF)]
        Rw13 = [Reg("W13_%d" % i) for i in range(3)]
        Rw2 = [Reg("W2_%d" % i) for i in range(3)]
        Rsil = [Reg("SIL0"), Reg("SIL1")]
        cnts = {"w13": 0, "w2": 0, "htb": 0, "sil": 0, "pa": 0}

        def ffn(l, f, s):
            make_gate_bcast(s)
            w1v = ffn_w1[l, f].rearrange("(kc p) n -> p kc n", p=128)
            w3v = ffn_w3[l, f].rearrange("(kc p) n -> p kc n", p=128)
            w2v = ffn_w2[l, f].rearrange("(j p) n -> p j n", p=128)
            hb0 = cnts["htb"]
            cnts["htb"] += 5

            def prenorm_block(blk_):
                hb_ = (hb0 + blk_) % 2
                ci_ = 0 if blk_ < 4 else 1
                for tt in range(4):
                    prenorm_tile(blk_ * 4 + tt, ci_, s, HTB[:, hb_, :, tt * 128:(tt + 1) * 128], Rhtb[hb_], (4 + 2 * (tt % 2), 5 + 2 * (tt % 2)))
            prenorm_block(0)
            pend = [None] * 4
            for blk in range(5):
                ci = 0 if blk < 4 else 1
                hb = (hb0 + blk) % 2
                for j2 in range(NFF // 2):
                    if blk + 1 < 5:
                        nb_, hbn, cin = blk + 1, (hb0 + blk + 1) % 2, (0 if blk + 1 < 4 else 1)
                        if 4 <= j2 <= 7:
                            tt_ = j2 - 4
                            prenorm_p2(pend[tt_], cin, s, HTB[:, hbn, :, tt_ * 128:(tt_ + 1) * 128], Rhtb[hbn], (4 + 2 * (tt_ % 2), 5 + 2 * (tt_ % 2)))
                        if 2 <= j2 <= 5:
                            pend[j2 - 2] = prenorm_p1(nb_ * 4 + (j2 - 2))
                    sl = cnts["w13"] % 3
                    cnts["w13"] += 1
                    S.dma("pool", W13[:, sl, 0, :, :], w1v[:, :, j2 * 256:(j2 + 1) * 256], writes=[Rw13[sl]])
                    S.dma("pool", W13[:, sl, 1, :, :], w3v[:, :, j2 * 256:(j2 + 1) * 256], writes=[Rw13[sl]])
                    for jj in range(2):
                        j = 2 * j2 + jj
                        pa = cnts["pa"] % 2
                        cnts["pa"] += 1
                        b1, b3 = 2 * pa, 2 * pa + 1
                        for (m, bk) in ((0, b1), (1, b3)):
                            def emit(m=m, bk=bk, jj=jj, sl=sl):
                                inst = None
                                for kc in range(8):
                                    inst = nc.tensor.matmul(PS[:, bk, :], lhsT=W13[:, sl, m, kc, jj * 128:(jj + 1) * 128],
                                                            rhs=HTB[:, hb, kc, :], start=(kc == 0), stop=(kc == 7))
                                return inst
                            S.op("pe", emit, reads=[Rw13[sl], Rhtb[hb]], writes=[RPS[bk]])
                        si = cnts["sil"] % 2
                        cnts["sil"] += 1
                        S.op("act", lambda b1=b1, si=si: nc.scalar.activation(out=SIL[:, si, :], in_=PS[:, b1, :], func=AF.Silu),
                             reads=[RPS[b1]], writes=[Rsil[si]])
                        S.op("dve", lambda b3=b3, si=si, j=j: nc.vector.tensor_tensor(out=GT[:, j, :], in0=PS[:, b3, :], in1=SIL[:, si, :], op=ALU.mult),
                             reads=[RPS[b3], Rsil[si]], writes=[Rgt[j]])
                for j2 in range(NFF // 2):
                    sl = cnts["w2"] % 3
                    cnts["w2"] += 1
                    S.dma("pool", W2[:, sl, :, :], w2v[:, 2 * j2:2 * j2 + 2, :], writes=[Rw2[sl]])
                    for jj in range(2):
                        j = 2 * j2 + jj

                        def emit(j=j, jj=jj, sl=sl):
                            inst = None
                            for tt in range(4):
                                for half in range(2):
                                    inst = nc.tensor.matmul(PS[:, 2 * tt + half, :], lhsT=GT[:, j, tt * 128:(tt + 1) * 128],
                                                            rhs=W2[:, sl, jj, half * 512:(half + 1) * 512], start=(j == 0), stop=(j == NFF - 1))
                            return inst
                        S.op("pe", emit, reads=[Rgt[j], Rw2[sl]], writes=RPS)
                for tt in range(4):
                    postnorm_tile(blk * 4 + tt, ci, 2 * tt)

        MOFF = 0
        HT, MOFF = view(MOFF, [128, 8, 2048], BF16)
        BR, MOFF = view(MOFF, [128, 8, 2048], BF16)
        WIN0 = MOFF
        WIN, MOFF = view(MOFF, [128, 2, 8, 256], BF16)
        BS0 = MOFF
        Rht = Reg("HT")
        Rbr = [Reg("BR%d" % i) for i in range(8)]
        Rwin = [Reg("WIN0"), Reg("WIN1")]
        PV = sb("PV", [128, 64])
        Rpv = Reg("PV")
        BG = sb("BG", [128, 32])
        Rbg = Reg("BG")
        mc = {"win": 0, "pb": 0, "wg": 0, "wo": 0, "sg": 0}
        SEQS = [(0, 16, 0, "sample"), (16, 2, 1, "pA"), (18, 2, 1, "pB")]

        def mixer_params(l):
            load_T(PV[:, 0:6], conv_w[l].rearrange("j (c p) -> (j c) p", p=128), 6, Rpv)
            load_T(PV[:, 6:8], conv_b[l].rearrange("(c p) -> c p", p=128), 2, Rpv)
            load_T(PV[:, 8:10], ret_gn[l].rearrange("(c p) -> c p", p=128), 2, Rpv)
            load_T(PV[:, 10:12], s5_d[l].rearrange("(c p) -> c p", p=128), 2, Rpv)
            load_T(PV[:, 12:13], mla_kv_norm[l].rearrange("(c p) -> c p", p=128), 1, Rpv)
            load_T(BG[:, :], b_gate[l].rearrange("(c p) -> c p", p=128), 32, Rbg)

        def proj_fm(l, col0, ncols, NB, BW, evac, extra_reads=()):
            winv = w_in[l].rearrange("(kc p) n -> p kc n", p=128)
            sl = mc["win"] % 2
            mc["win"] += 1
            S.dma("pool", WIN[:, sl, :, 0:ncols], winv[:, :, col0:col0 + ncols], writes=[Rwin[sl]])
            for b in range(NB):
                bank = mc["pb"] % 4
                mc["pb"] += 1

                def emit(b=b, bank=bank):
                    inst = None
                    for kc in range(8):
                        inst = nc.tensor.matmul(PS[0:ncols, bank, 0:BW], lhsT=WIN[:, sl, kc, 0:ncols], rhs=cur["HT"][:, kc, b * BW:(b + 1) * BW],
                                                start=(kc == 0), stop=(kc == 7))
                    return inst
                S.op("pe", emit, reads=[Rwin[sl], Rht], writes=[RPS[bank]])
                evac(b, bank)

        def branch_conv(l, T, NB, BW):
            o = BS0
            Z, o = view(o, [128, 2056])
            CX, o = view(o, [128, 2048])
            Y, o = view(o, [128, 2048])
            CB, o = view(o, [128, 2048], BF16)
            Rz, Rcx, Ry, Rcb = Reg("Z"), Reg("CX"), Reg("Y"), Reg("CB")
            for cc in range(2):
                S.op("dve", lambda: nc.vector.memset(Z[:, 0:1], 0.0), writes=[Rz])
                S.op("dve", lambda: nc.vector.memset(Z[:, T + 1:T + 2], 0.0), writes=[Rz])
                proj_fm(l, 1280 + cc * 128, 128, NB, BW, lambda b, bank: S.op(
                    "act", lambda: nc.scalar.copy(out=CX[:, b * BW:(b + 1) * BW], in_=PS[:, bank, 0:BW]), reads=[RPS[bank]], writes=[Rcx]))
                proj_fm(l, 1792 + cc * 128, 128, NB, BW, lambda b, bank: S.op(
                    "dve", lambda: nc.vector.tensor_tensor(out=Z[:, 1 + b * BW:1 + (b + 1) * BW], in0=PS[:, bank, 0:BW], in1=CX[:, b * BW:(b + 1) * BW], op=ALU.mult),
                    reads=[RPS[bank], Rcx], writes=[Rz]))
                proj_fm(l, 1536 + cc * 128, 128, NB, BW, lambda b, bank: S.op(
                    "act", lambda: nc.scalar.copy(out=CB[:, b * BW:(b + 1) * BW], in_=PS[:, bank, 0:BW]), reads=[RPS[bank]], writes=[Rcb]))
                S.op("dve", lambda: nc.vector.tensor_scalar(out=Y[:, 0:T], in0=Z[:, 1:T + 1], scalar1=PV[:, 2 + cc:3 + cc], scalar2=PV[:, 6 + cc:7 + cc],
                                                            op0=ALU.mult, op1=ALU.add), reads=[Rz, Rpv], writes=[Ry])
                S.op("dve", lambda: nc.vector.scalar_tensor_tensor(out=Y[:, 0:T], in0=Z[:, 0:T], scalar=PV[:, 0 + cc:1 + cc], in1=Y[:, 0:T],
                                                                   op0=ALU.mult, op1=ALU.add), reads=[Rz, Rpv, Ry], writes=[Ry])
                S.op("dve", lambda: nc.vector.scalar_tensor_tensor(out=Y[:, 0:T], in0=Z[:, 2:T + 2], scalar=PV[:, 4 + cc:5 + cc], in1=Y[:, 0:T],
                                                                   op0=ALU.mult, op1=ALU.add), reads=[Rz, Rpv, Ry], writes=[Ry])
                S.op("dve", lambda: nc.vector.tensor_tensor(out=cur["BR"][:, 4 + cc, 0:T], in0=Y[:, 0:T], in1=CB[:, 0:T], op=ALU.mult),
                     reads=[Ry, Rcb], writes=[Rbr[4 + cc]])

        def gate_stage(l, t0, ntile, ci, T, NB, BW):
            o = WIN0
            WG, o = view(o, [128, 2, 8, 4, 128], BF16)
            WB, o = view(o, [128, 2, 2, 4, 128], BF16)
            MG, o = view(o, [128, 8, 512], BF16)
            SG, o = view(o, [128, 4, 512], BF16)
            WO, o = view(o, [128, 2, D], BF16)
            ACC, o = view(o, [128, 2, 512])
            cur["TMP"], o = view(o, [128, 2, D])
            Rwg = [Reg("WG0"), Reg("WG1")]
            Rmg = [Reg("MG%d" % c) for c in range(8)]
            Rsg = [Reg("SG%d" % n) for n in range(4)]
            Rwo = [Reg("WO0"), Reg("WO1")]
            Racc = [Reg("ACC0"), Reg("ACC1")]
            wgv = w_gate[l].rearrange("(kc p) (n d) -> p kc n d", p=128, n=4)
            wbv = w_branch[l].rearrange("n (kc p) d -> p kc n d", p=128)
            tpb = BW // 128
            for b in range(NB):
                for c in range(8):
                    sl = mc["wg"] % 2
                    mc["wg"] += 1
                    for n in range(4):
                        S.dma("pool", WG[:, sl, :, n, :], wgv[:, :, n, c * 128:(c + 1) * 128], writes=[Rwg[sl]])
                    for n in range(4):
                        S.dma("pool", WB[:, sl, :, n, :], wbv[:, :, n, c * 128:(c + 1) * 128], writes=[Rwg[sl]])
                    for n in range(4):
                        def emit_g(n=n, sl=sl):
                            inst = None
                            for kc in range(8):
                                inst = nc.tensor.matmul(PS[:, n, 0:BW], lhsT=WG[:, sl, kc, n, :], rhs=cur["HT"][:, kc, b * BW:(b + 1) * BW],
                                                        start=(kc == 0), stop=(kc == 7))
                            return inst
                        S.op("pe", emit_g, reads=[Rwg[sl], Rht], writes=[RPS[n]])

                        def emit_p(n=n, sl=sl):
                            inst = None
                            for kc in range(2):
                                inst = nc.tensor.matmul(PS[:, 4 + n, 0:BW], lhsT=WB[:, sl, kc, n, :], rhs=cur["BR"][:, 2 * n + kc, b * BW:(b + 1) * BW],
                                                        start=(kc == 0), stop=(kc == 1))
                            return inst
                        S.op("pe", emit_p, reads=[Rwg[sl], Rbr[2 * n], Rbr[2 * n + 1]], writes=[RPS[4 + n]])
                        S.op("act", lambda n=n: nc.scalar.activation(out=SG[:, n, 0:BW], in_=PS[:, n, 0:BW], func=AF.Sigmoid,
                                                                     bias=BG[:, n * 8 + c:n * 8 + c + 1]), reads=[RPS[n], Rbg], writes=[Rsg[n]])
                    S.op("dve", lambda: nc.vector.tensor_tensor(out=ACC[:, 0, 0:BW], in0=PS[:, 4, 0:BW], in1=SG[:, 0, 0:BW], op=ALU.mult),
                         reads=[RPS[4], Rsg[0]], writes=[Racc[0]])
                    for n in range(1, 4):
                        S.op("dve", lambda n=n: nc.vector.tensor_tensor(out=ACC[:, 1, 0:BW], in0=PS[:, 4 + n, 0:BW], in1=SG[:, n, 0:BW], op=ALU.mult),
                             reads=[RPS[4 + n], Rsg[n]], writes=[Racc[1]])
                        if n < 3:
                            S.op("dve", lambda: nc.vector.tensor_tensor(out=ACC[:, 0, 0:BW], in0=ACC[:, 0, 0:BW], in1=ACC[:, 1, 0:BW], op=ALU.add),
                                 reads=[Racc[0], Racc[1]], writes=[Racc[0]])
                        else:
                            S.op("dve", lambda: nc.vector.tensor_tensor(out=MG[:, c, 0:BW], in0=ACC[:, 0, 0:BW], in1=ACC[:, 1, 0:BW], op=ALU.add),
                                 reads=[Racc[0], Racc[1]], writes=[Rmg[c]])
                for c in range(8):
                    sl = mc["wo"] % 2
                    mc["wo"] += 1
                    S.dma("pool", WO[:, sl, :], w_o[l, c * 128:(c + 1) * 128, :], writes=[Rwo[sl]])

                    def emit_o(c=c, sl=sl):
                        inst = None
                        for tt in range(tpb):
                            for half in range(2):
                                inst = nc.tensor.matmul(PS[:, 2 * tt + half, :], lhsT=MG[:, c, tt * 128:(tt + 1) * 128],
                                                        rhs=WO[:, sl, half * 512:(half + 1) * 512], start=(c == 0), stop=(c == 7))
                        return inst
                    S.op("pe", emit_o, reads=[Rmg[c], Rwo[sl]], writes=RPS[0:2 * tpb])
                for tt in range(tpb):
                    postnorm_tile(t0 + b * tpb + tt, ci, 2 * tt)

        onesb = sb("onesb", [128, 128], BF16)
        S.op("dve", lambda: nc.vector.memset(onesb[:], 1.0), writes=[Rones])
        ATT_SCALE = float(96 ** -0.5)

        def branch_mla(l, t0, T, NB, BW, kind):
            sample = kind == "sample"
            Skeys = T + (512 if sample else 0)
            NKT = Skeys // 128
            o = BS0
            CQ, o = view(o, [128, 2, 2048], BF16)
            CKVN, o = view(o, [128, 2560], BF16)
            KR, o = view(o, [128, 2560], BF16)
            WUQ, o = view(o, [128, 2, 384], BF16)
            WUQS, o = view(o, [128, 2, 4, 32], BF16)
            WUKV, o = view(o, [128, 512], BF16)
            WKRS, o = view(o, [128, 8, 32], BF16)
            QNV, o = view(o, [128, 2])
            oB = o
            Rcq, Rckvn, Rkr, Rw = Reg("m_CQ"), Reg("m_CKVN"), Reg("m_KR"), Reg("m_W")
            W32, oo = view(oB, [128, 2, 384])
            Rw32 = Reg("m_W32")
            S.dma("sp", W32[:, 0, :], mla_w_uq[l, 0:128, :], writes=[Rw32])
            S.dma("sp", W32[0:64, 1, :], mla_w_uq[l, 128:192, :], writes=[Rw32])
            S.dma("sp", QNV[:, 0:1], mla_q_norm[l, 0:128].unsqueeze(1), writes=[Rw])
            S.dma("sp", QNV[0:64, 1:2], mla_q_norm[l, 128:192].unsqueeze(1), writes=[Rw])
            S.dma("pool", WUKV[:, :], mla_w_ukv[l, :, :], writes=[Rw])
            for kc, np_ in ((0, 128), (1, 64)):
                S.op("dve", lambda kc=kc, np_=np_: nc.vector.tensor_scalar(out=WUQ[0:np_, kc, :], in0=W32[0:np_, kc, :], scalar1=QNV[0:np_, kc:kc + 1],
                                                                          scalar2=None, op0=ALU.mult), reads=[Rw32, Rw], writes=[Rw])
                if sample:
                    wv = WUQ[0:np_, kc, :].rearrange("p (h e) -> p h e", h=4)
                    S.op("dve", lambda wv=wv, kc=kc, np_=np_: nc.vector.tensor_scalar(out=WUQS[0:np_, kc, :, 0:16], in0=wv[:, :, 80:96], scalar1=-1.0,
                                                                                   scalar2=None, op0=ALU.mult), reads=[Rw], writes=[Rw])
                    S.op("dve", lambda wv=wv, kc=kc, np_=np_: nc.vector.tensor_copy(out=WUQS[0:np_, kc, :, 16:32], in_=wv[:, :, 64:80]), reads=[Rw], writes=[Rw])
            if sample:
                winv = w_in[l].rearrange("(kc p) n -> p kc n", p=128)
                S.dma("pool", WKRS[:, :, 0:16], winv[:, :, 2384:2400], writes=[Rw])
                S.dma("pool", WKRS[:, :, 16:32], winv[:, :, 2368:2384], writes=[Rw])
                S.op("dve", lambda: nc.vector.tensor_scalar(out=WKRS[:, :, 0:16], in0=WKRS[:, :, 0:16], scalar1=-1.0, scalar2=None, op0=ALU.mult),
                     reads=[Rw], writes=[Rw])
            SQ, oo = view(oo, [128, 2, 512], BF16)
            RST, oo = view(oo, [128, 512])
            TB, oo = view(oo, [128, 2, 512])
            T1, oo = view(oo, [128, 2, 512])
            Rsq, Rrst, Rtb, Rt1 = Reg("m_SQ"), Reg("m_RST"), Reg("m_TB"), Reg("m_T1")

            def rstd_from_ps(bank, parts):
                S.op("act", lambda: nc.scalar.activation(out=RST[:, 0:BW], in_=PS[:, bank, 0:BW], func=AF.Sqrt, scale=1.0 / parts, bias=epsb[:, 0:1]),
                     reads=[RPS[bank], Reps], writes=[Rrst])
                S.op("dve", lambda: nc.vector.reciprocal(out=RST[:, 0:BW], in_=RST[:, 0:BW]), reads=[Rrst], writes=[Rrst])

            winv = w_in[l].rearrange("(kc p) n -> p kc n", p=128)
            WQ = WIN
            S.dma("pool", WQ[:, 0, :, 0:192], winv[:, :, 2048:2240], writes=[Rwin[0]])
            S.dma("pool", WQ[:, 1, :, 0:160], winv[:, :, 2240:2400], writes=[Rwin[1]])
            for b in range(NB):
                cols = slice(b * BW, (b + 1) * BW)
                for kc2, np_, bank in ((0, 128, 0), (1, 64, 1)):
                    def emit(kc2=kc2, np_=np_, bank=bank):
                        inst = None
                        for kc in range(8):
                            inst = nc.tensor.matmul(PS[0:np_, bank, 0:BW], lhsT=WQ[:, 0, kc, kc2 * 128:kc2 * 128 + np_], rhs=cur["HT"][:, kc, cols],
                                                    start=(kc == 0), stop=(kc == 7))
                        return inst
                    S.op("pe", emit, reads=[Rwin[0], Rht], writes=[RPS[bank]])
                    S.op("act", lambda kc2=kc2, np_=np_, bank=bank: nc.scalar.activation(out=SQ[0:np_, kc2, 0:BW], in_=PS[0:np_, bank, 0:BW], func=AF.Square),
                         reads=[RPS[bank]], writes=[Rsq])

                def emit_ss():
                    nc.tensor.matmul(PS[:, 2, 0:BW], lhsT=onesb[:, :], rhs=SQ[:, 0, 0:BW], start=True, stop=False)
                    return nc.tensor.matmul(PS[:, 2, 0:BW], lhsT=onesb[0:64, :], rhs=SQ[0:64, 1, 0:BW], start=False, stop=True)
                S.op("pe", emit_ss, reads=[Rsq, Rones], writes=[RPS[2]])
                rstd_from_ps(2, 192.0)
                for kc2, np_, bank in ((0, 128, 0), (1, 64, 1)):
                    S.op("dve", lambda kc2=kc2, np_=np_, bank=bank: nc.vector.tensor_tensor(out=CQ[0:np_, kc2, cols], in0=PS[0:np_, bank, 0:BW], in1=RST[0:np_, 0:BW], op=ALU.mult),
                         reads=[RPS[bank], Rrst], writes=[Rcq])
                def emit_kv():
                    inst = None
                    for kc in range(8):
                        inst = nc.tensor.matmul(PS[:, 3, 0:BW], lhsT=WQ[:, 1, kc, 0:128], rhs=cur["HT"][:, kc, cols], start=(kc == 0), stop=(kc == 7))
                    return inst
                S.op("pe", emit_kv, reads=[Rwin[1], Rht], writes=[RPS[3]])
                S.op("act", lambda: nc.scalar.activation(out=SQ[:, 0, 0:BW], in_=PS[:, 3, 0:BW], func=AF.Square), reads=[RPS[3]], writes=[Rsq])
                S.op("pe", lambda: nc.tensor.matmul(PS[:, 2, 0:BW], lhsT=onesb[:, :], rhs=SQ[:, 0, 0:BW], start=True, stop=True), reads=[Rsq, Rones], writes=[RPS[2]])
                rstd_from_ps(2, 128.0)
                S.op("dve", lambda: nc.vector.scalar_tensor_tensor(out=CKVN[:, cols], in0=PS[:, 3, 0:BW], scalar=PV[:, 12:13], in1=RST[:, 0:BW], op0=ALU.mult, op1=ALU.mult),
                     reads=[RPS[3], Rpv, Rrst], writes=[Rckvn])
                def emit_kr():
                    inst = None
                    for kc in range(8):
                        inst = nc.tensor.matmul(PS[0:32, 4, 0:BW], lhsT=WQ[:, 1, kc, 128:160], rhs=cur["HT"][:, kc, cols], start=(kc == 0), stop=(kc == 7))
                    return inst
                S.op("pe", emit_kr, reads=[Rwin[1], Rht], writes=[RPS[4]])
                if sample:
                    def emit_krs():
                        inst = None
                        for kc in range(8):
                            inst = nc.tensor.matmul(PS[0:32, 5, 0:BW], lhsT=WKRS[:, kc, :], rhs=cur["HT"][:, kc, cols], start=(kc == 0), stop=(kc == 7))
                        return inst
                    S.op("pe", emit_krs, reads=[Rw, Rht], writes=[RPS[5]])
                    S.dma("sp", TB[0:32, 0, 0:BW], c_rope_mla[0, :, cols], writes=[Rtb])
                    S.dma("sp", TB[0:32, 1, 0:BW], c_rope_mla[1, :, cols], writes=[Rtb])
                    S.op("dve", lambda: nc.vector.tensor_tensor(out=T1[0:32, 0, 0:BW], in0=PS[0:32, 4, 0:BW], in1=TB[0:32, 0, 0:BW], op=ALU.mult),
                         reads=[RPS[4], Rtb], writes=[Rt1])
                    S.op("dve", lambda: nc.vector.tensor_tensor(out=T1[0:32, 1, 0:BW], in0=PS[0:32, 5, 0:BW], in1=TB[0:32, 1, 0:BW], op=ALU.mult),
                         reads=[RPS[5], Rtb], writes=[Rt1])
                    S.op("dve", lambda: nc.vector.tensor_tensor(out=KR[0:32, cols], in0=T1[0:32, 0, 0:BW], in1=T1[0:32, 1, 0:BW], op=ALU.add),
                         reads=[Rt1], writes=[Rkr])
                else:
                    S.op("act", lambda: nc.scalar.copy(out=KR[0:32, cols], in_=PS[0:32, 4, 0:BW]), reads=[RPS[4]], writes=[Rkr])
            if sample:
                CT, _ = view(oB + 3072, [128, 4, 160])
                Rct = Reg("m_CT")
                S.barrier()
                S.dma("sp", CT[:, :, :], ctx_mla[l].rearrange("(i p) f -> p i f", p=128), writes=[Rct])
                for i in range(4):
                    S.op("pe", lambda i=i: nc.tensor.transpose(out=PS[:, 6, 0:128], in_=CT[:, i, 0:128], identity=ident[:]), reads=[Rct, Rid], writes=[RPS[6]])
                    S.op("act", lambda i=i: nc.scalar.copy(out=CKVN[:, T + i * 128:T + (i + 1) * 128], in_=PS[:, 6, 0:128]), reads=[RPS[6]], writes=[Rckvn])
                    S.op("pe", lambda i=i: nc.tensor.transpose(out=PS[0:32, 7, 0:128], in_=CT[:, i, 128:160], identity=ident[:]), reads=[Rct, Rid], writes=[RPS[7]])
                    S.op("act", lambda i=i: nc.scalar.copy(out=KR[0:32, T + i * 128:T + (i + 1) * 128], in_=PS[0:32, 7, 0:128]), reads=[RPS[7]], writes=[Rkr])
            S.barrier()
            o = oB
            KN, o = view(o, [128, 2560], BF16)
            QN, o = view(o, [128, 2048], BF16)
            QR, o = view(o, [128, 2048], BF16)
            VA, o = view(o, [128, 20, 66], BF16)
            Rkn, Rqn, Rqr, Rva = Reg("m_KN"), Reg("m_QN"), Reg("m_QR"), Reg("m_VA")
            ow = WIN0
            TB2, ow2 = view(ow, [128, 2, 512])
            T2, ow2 = view(ow2, [128, 2, 512])
            PT, ow3 = view(ow, [128, 2, 512], BF16)
            OS, ow3 = view(ow3, [128, 512])
            OT, ow3 = view(ow3, [128, 512], BF16)
            Rtb2, Rt2, Rpt, Ros, Rot = Reg("m_TB2"), Reg("m_T2"), [Reg("m_PT0"), Reg("m_PT1")], Reg("m_OS"), Reg("m_OT")
            S.op("dve", lambda: nc.vector.memset(VA[:, :, 64:66], 1.0), writes=[Rva])
            KBW = 512
            for h in range(4):
                for kb in range((Skeys + KBW - 1) // KBW):
                    w = min(KBW, Skeys - kb * KBW)
                    bank = mc["pb"] % 4
                    mc["pb"] += 1
                    S.op("pe", lambda kb=kb, w=w, bank=bank: nc.tensor.matmul(PS[0:64, bank, 0:w], lhsT=WUKV[:, h * 128:h * 128 + 64], rhs=CKVN[:, kb * KBW:kb * KBW + w],
                                                                              start=True, stop=True), reads=[Rw, Rckvn], writes=[RPS[bank]])
                    S.op("act", lambda kb=kb, w=w, bank=bank: nc.scalar.copy(out=KN[0:64, kb * KBW:kb * KBW + w], in_=PS[0:64, bank, 0:w]), reads=[RPS[bank]], writes=[Rkn])
                for kt in range(NKT):
                    bank = mc["pb"] % 4
                    mc["pb"] += 1
                    S.op("pe", lambda kt=kt, bank=bank: nc.tensor.matmul(PS[:, bank, 0:64], lhsT=CKVN[:, kt * 128:(kt + 1) * 128], rhs=WUKV[:, h * 128 + 64:h * 128 + 128],
                                                                          start=True, stop=True), reads=[Rw, Rckvn], writes=[RPS[bank]])
                    S.op("dve", lambda kt=kt, bank=bank: nc.vector.tensor_copy(out=VA[:, kt, 0:64], in_=PS[:, bank, 0:64]), reads=[RPS[bank]], writes=[Rva])
                for b in range(NB):
                    cols = slice(b * BW, (b + 1) * BW)
                    bank = mc["pb"] % 4
                    mc["pb"] += 1

                    def emit_q(c0, m, bank, wt=None):
                        def f():
                            if wt is None:
                                nc.tensor.matmul(PS[0:m, bank, 0:BW], lhsT=WUQ[:, 0, c0:c0 + m], rhs=CQ[:, 0, cols], start=True, stop=False)
                                return nc.tensor.matmul(PS[0:m, bank, 0:BW], lhsT=WUQ[0:64, 1, c0:c0 + m], rhs=CQ[0:64, 1, cols], start=False, stop=True)
                            nc.tensor.matmul(PS[0:m, bank, 0:BW], lhsT=WUQS[:, 0, h, :], rhs=CQ[:, 0, cols], start=True, stop=False)
                            return nc.tensor.matmul(PS[0:m, bank, 0:BW], lhsT=WUQS[0:64, 1, h, :], rhs=CQ[0:64, 1, cols], start=False, stop=True)
                        return f
                    S.op("pe", emit_q(h * 96, 64, bank), reads=[Rw, Rcq], writes=[RPS[bank]])
                    S.op("act", lambda bank=bank: nc.scalar.copy(out=QN[0:64, cols], in_=PS[0:64, bank, 0:BW]), reads=[RPS[bank]], writes=[Rqn])
                    bank2 = mc["pb"] % 4
                    mc["pb"] += 1
                    S.op("pe", emit_q(h * 96 + 64, 32, bank2), reads=[Rw, Rcq], writes=[RPS[bank2]])
                    if sample:
                        bank3 = mc["pb"] % 4
                        mc["pb"] += 1
                        S.op("pe", emit_q(0, 32, bank3, wt=1), reads=[Rw, Rcq], writes=[RPS[bank3]])
                        S.dma("sp", TB2[0:32, 0, 0:BW], c_rope_mla[0, :, cols], writes=[Rtb2])
                        S.dma("sp", TB2[0:32, 1, 0:BW], c_rope_mla[1, :, cols], writes=[Rtb2])
                        S.op("dve", lambda: nc.vector.tensor_tensor(out=T2[0:32, 0, 0:BW], in0=PS[0:32, bank2, 0:BW], in1=TB2[0:32, 0, 0:BW], op=ALU.mult),
                             reads=[RPS[bank2], Rtb2], writes=[Rt2])
                        S.op("dve", lambda: nc.vector.tensor_tensor(out=T2[0:32, 1, 0:BW], in0=PS[0:32, bank3, 0:BW], in1=TB2[0:32, 1, 0:BW], op=ALU.mult),
                             reads=[RPS[bank3], Rtb2], writes=[Rt2])
                        S.op("dve", lambda: nc.vector.tensor_tensor(out=QR[0:32, cols], in0=T2[0:32, 0, 0:BW], in1=T2[0:32, 1, 0:BW], op=ALU.add),
                             reads=[Rt2], writes=[Rqr])
                    else:
                        S.op("act", lambda: nc.scalar.copy(out=QR[0:32, cols], in_=PS[0:32, bank2, 0:BW]), reads=[RPS[bank2]], writes=[Rqr])
                S.barrier()
                for b in range(NB):
                    cols = slice(b * BW, (b + 1) * BW)
                    ob = 4 + (b % 2)
                    for kt in range(NKT):
                        bank = mc["pb"] % 4
                        mc["pb"] += 1
                        pi = kt % 2

                        def emit_s(kt=kt, bank=bank):
                            nc.tensor.matmul(PS[:, bank, 0:BW], lhsT=KN[0:64, kt * 128:(kt + 1) * 128], rhs=QN[0:64, cols], start=True, stop=False)
                            return nc.tensor.matmul(PS[:, bank, 0:BW], lhsT=KR[0:32, kt * 128:(kt + 1) * 128], rhs=QR[0:32, cols], start=False, stop=True)
                        S.op("pe", emit_s, reads=[Rkn, Rqn, Rkr, Rqr], writes=[RPS[bank]])
                        S.op("act", lambda bank=bank, pi=pi: nc.scalar.activation(out=PT[:, pi, 0:BW], in_=PS[:, bank, 0:BW], func=AF.Exp, scale=ATT_SCALE),
                             reads=[RPS[bank]], writes=[Rpt[pi]])
                        S.op("pe", lambda kt=kt, pi=pi: nc.tensor.matmul(PS[0:65, ob, 0:BW], lhsT=VA[:, kt, 0:65], rhs=PT[:, pi, 0:BW], start=(kt == 0), stop=(kt == NKT - 1)),
                             reads=[Rva, Rpt[pi]], writes=[RPS[ob]])
                    S.op("act", lambda: nc.scalar.copy(out=OS[0:65, 0:BW], in_=PS[0:65, ob, 0:BW]), reads=[RPS[ob]], writes=[Ros])
                    S.op("dve", lambda: nc.vector.reciprocal(out=OS[64:65, 0:BW], in_=OS[64:65, 0:BW]), reads=[Ros], writes=[Ros])
                    S.op("pe", lambda: nc.tensor.matmul(PS[0:64, 6, 0:BW], lhsT=ones32[64:65, 0:64], rhs=OS[64:65, 0:BW], start=True, stop=True),
                         reads=[Ros, Rones], writes=[RPS[6]])
                    S.op("dve", lambda: nc.vector.tensor_tensor(out=OT[0:64, 0:BW], in0=PS[0:64, 6, 0:BW], in1=OS[0:64, 0:BW], op=ALU.mult),
                         reads=[RPS[6], Ros], writes=[Rot])
                    S.dma("sp", cur["BR"][(h % 2) * 64:(h % 2) * 64 + 64, 6 + h // 2, cols], OT[0:64, 0:BW], reads=[Rot], writes=[Rbr[6 + h // 2]], key=Reg("m_OTd"))
                S.barrier()

        def mla_cache_out(l, t0, ntile, pi):
            o = BS0
            CA, o = view(o, [128, 2, 160])
            KVB, o = view(o, [128, 128])
            Rca, Rkvb = [Reg("m_CA0"), Reg("m_CA1")], Reg("m_KVB")
            winv = w_in[l].rearrange("(kc p) n -> p kc n", p=128)
            S.dma("pool", WIN[:, 0, :, 0:160], winv[:, :, 2240:2400], writes=[Rwin[0]])
            S.dma("sp", KVB[:, :], mla_kv_norm[l:l + 1, :].broadcast_to([128, 128]), writes=[Rkvb])
            for tt in range(ntile):
                bank = mc["pb"] % 4
                mc["pb"] += 1
                smi = rot["sm"] % 4
                rot["sm"] += 1
                sm = small[:, smi, :]

                def emit(tt=tt, bank=bank):
                    inst = None
                    for kc in range(8):
                        inst = nc.tensor.matmul(PS[:, bank, 0:160], lhsT=cur["HT"][:, kc, tt * 128:(tt + 1) * 128], rhs=WIN[:, 0, kc, 0:160], start=(kc == 0), stop=(kc == 7))
                    return inst
                S.op("pe", emit, reads=[Rwin[0], Rht], writes=[RPS[bank]])
                S.op("act", lambda: nc.scalar.activation(out=junk[:, 0:128], in_=PS[:, bank, 0:128], func=AF.Square, accum_out=sm[:, 0:1]),
                     reads=[RPS[bank]], writes=[Rjunk, Rsmall[smi]])
                S.op("act", lambda: nc.scalar.activation(out=sm[:, 1:2], in_=sm[:, 0:1], func=AF.Sqrt, scale=1.0 / 128, bias=epsb[:, 0:1]),
                     reads=[Rsmall[smi], Reps], writes=[Rsmall[smi]])
                S.op("dve", lambda: nc.vector.reciprocal(out=sm[:, 2:3], in_=sm[:, 1:2]), reads=[Rsmall[smi]], writes=[Rsmall[smi]])
                ci_ = tt % 2
                S.op("dve", lambda: nc.vector.scalar_tensor_tensor(out=CA[:, ci_, 0:128], in0=PS[:, bank, 0:128], scalar=sm[:, 2:3], in1=KVB[:, :], op0=ALU.mult, op1=ALU.mult),
                     reads=[RPS[bank], Rsmall[smi], Rkvb], writes=[Rca[ci_]])
                S.op("dve", lambda: nc.vector.tensor_copy(out=CA[:, ci_, 128:160], in_=PS[:, bank, 128:160]), reads=[RPS[bank]], writes=[Rca[ci_]])
                OUT_EVS.append(S.dma("sp", o_mla[pi, l, tt * 128:(tt + 1) * 128, :], CA[:, ci_, :], reads=[Rca[ci_]], key=Reg("o_mla_d%d" % ci_)))

        def branch_ret(l, t0, T, kind):
            sample = kind == "sample"
            n = T // 128
            o = BS0
            WRb, o = view(o, [128, 8, 512], BF16)
            SBst, o = view(o, [128, 16, 256], BF16)
            DM, o = view(o, [128, 4, 128], BF16)
            QD, o = view(o, [128, 2, 4, 128], BF16)
            CD, o = view(o, [128, 2, 256])
            LG, o = view(o, [128, 8])
            KD, o = view(o, [128, 2, 4])
            SF, o = view(o, [128, 256])
            SB, o = view(o, [128, 256])
            SFb, o = view(o, [128, 256], BF16)
            oT = o
            WRa, _ = view(WIN0, [128, 8, 512], BF16)
            Rwr, Rsbst, Rtab, Rsf, Rsb, Rsfb = Reg("r_WR"), Reg("r_SBst"), Reg("r_TAB"), Reg("r_SF"), Reg("r_SB"), Reg("r_SFb")
            winv = w_in[l].rearrange("(kc p) n -> p kc n", p=128)
            S.dma("pool", WRa[:, :, :], winv[:, :, 256:768], writes=[Rwr])
            S.dma("pool", WRb[:, :, :], winv[:, :, 768:1280], writes=[Rwr])
            CT6, o2 = view(oT, [128, 6, 128])
            E1, o2 = view(o2, [128, 2, 128])
            PIDX, o2 = view(o2, [128, 2])
            C128, o2 = view(o2, [128, 64])
            Rc6, Re1 = Reg("r_C6"), Reg("r_E1")
            S.dma("sp", CT6[:, :, :], c_ret.rearrange("k p i -> p k i"), writes=[Rc6])
            S.dma("sp", PIDX[:, :], c_pidx[:, :], writes=[Rc6])
            S.dma("sp", LG[:, :], ret_decay[l:l + 1].rearrange("o d h -> o (d h)").broadcast_to([128, 8]), writes=[Rtab])
            S.op("dve", lambda: nc.vector.memset(C128[:, :], 128.0), writes=[Rc6])
            S.op("act", lambda: nc.scalar.activation(out=LG[:, :], in_=LG[:, :], func=AF.Sigmoid), reads=[Rtab], writes=[Rtab])
            S.op("act", lambda: nc.scalar.activation(out=LG[:, :], in_=LG[:, :], func=AF.Ln), reads=[Rtab], writes=[Rtab])
            for h in range(4):
                S.op("act", lambda h=h: nc.scalar.activation(out=E1[:, 0, :], in_=CT6[:, 0, :], func=AF.Exp, scale=LG[:, h:h + 1]), reads=[Rc6, Rtab], writes=[Re1])
                S.op("act", lambda h=h: nc.scalar.activation(out=E1[:, 1, :], in_=CT6[:, 1, :], func=AF.Exp, scale=LG[:, 4 + h:5 + h]), reads=[Rc6, Rtab], writes=[Re1])
                S.op("dve", lambda h=h: nc.vector.tensor_tensor(out=E1[:, :, :], in0=E1[:, :, :], in1=CT6[:, 2:4, :], op=ALU.mult), reads=[Re1, Rc6], writes=[Re1])
                S.op("dve", lambda h=h: nc.vector.tensor_tensor(out=DM[:, h, :], in0=E1[:, 0, :], in1=E1[:, 1, :], op=ALU.add), reads=[Re1], writes=[Rtab])
                for d in range(2):
                    S.op("act", lambda h=h, d=d: nc.scalar.activation(out=QD[:, d, h, :], in_=CT6[:, 4 + d, :], func=AF.Exp, scale=LG[:, d * 4 + h:d * 4 + h + 1]),
                         reads=[Rc6, Rtab], writes=[Rtab])
                    S.op("act", lambda h=h, d=d: nc.scalar.activation(out=CD[:, d, h * 64:(h + 1) * 64], in_=C128[:, :], func=AF.Exp, scale=LG[:, d * 4 + h:d * 4 + h + 1]),
                         reads=[Rc6, Rtab], writes=[Rtab])
                    S.op("act", lambda h=h, d=d: nc.scalar.activation(out=KD[:, d, h:h + 1], in_=PIDX[:, d:d + 1], func=AF.Exp, scale=LG[:, d * 4 + h:d * 4 + h + 1]),
                         reads=[Rc6, Rtab], writes=[Rtab])
            S.op("dve", lambda: nc.vector.tensor_scalar(out=KD[:, :, :], in0=KD[:, :, :], scalar1=0.125, scalar2=None, op0=ALU.mult), reads=[Rtab], writes=[Rtab])
            if sample:
                S.dma("sp", SF[0:64, :].rearrange("d (h e) -> d h e", h=4), st_ret[l, 0].rearrange("h d e -> d h e"), writes=[Rsf])
                S.dma("sp", SB[0:64, :].rearrange("d (h e) -> d h e", h=4), st_ret[l, 1].rearrange("h d e -> d h e"), writes=[Rsb])
            else:
                S.op("dve", lambda: nc.vector.memset(SF[0:64, :], 0.0), writes=[Rsf])
                S.op("dve", lambda: nc.vector.memset(SB[0:64, :], 0.0), writes=[Rsb])
            S.barrier()
            if cfg.get("ret_stop") == "tables":
                return
            o3 = oT
            QK, o3 = view(o3, [128, 512], BF16)
            TA, o3 = view(o3, [128, 256])
            TBt, o3 = view(o3, [128, 256])
            RT, o3 = view(o3, [128, 2, 32])
            KDt, o3 = view(o3, [128, 256], BF16)
            VTc, o3 = view(o3, [128, 256], BF16)
            SRG, o3 = view(o3, [128, 256], BF16)
            QT, o3 = view(o3, [128, 3, 512], BF16)
            KT, o3 = view(o3, [128, 512], BF16)
            AM, o3 = view(o3, [128, 512], BF16)
            CEN, o3 = view(o3, [128, 256])
            SQr, o3 = view(o3, [128, 256])
            NRo, o3 = view(o3, [128, 256], BF16)
            MS, o3 = view(o3, [128, 8])
            Rqk, Rta, Rrt, Rkd, Rvt, Rsrg, Rqt, Rkt, Ram, Rcen, Rsq, Rnro, Rms = (Reg("r_" + x) for x in
                ("QK", "TA", "RT", "KDt", "VTc", "SRG", "QT", "KT", "AM", "CEN", "SQ", "NRo", "MS"))
            PSb = lambda bank: PS[:, bank, :].bitcast(BF16)

            def proj(c, bank, WRx, c0, ncol):
                def emit():
                    inst = None
                    for kc in range(8):
                        inst = nc.tensor.matmul(PS[:, bank, 0:ncol], lhsT=cur["HT"][:, kc, c * 128:(c + 1) * 128], rhs=WRx[:, kc, c0:c0 + ncol], start=(kc == 0), stop=(kc == 7))
                    return inst
                S.op("pe", emit, reads=[Rwr, Rht], writes=[RPS[bank]])

            def rope(c, bank, col0, ng, dst):
                src = PS[:, bank, col0:col0 + ng * 64].rearrange("p (g t e) -> p g t e", g=ng, t=2)
                dv = dst.rearrange("p (g t e) -> p g t e", g=ng, t=2)
                if not sample:
                    S.op("act", lambda: nc.scalar.copy(out=dst, in_=PS[:, bank, col0:col0 + ng * 64]), reads=[RPS[bank]], writes=[Rqk])
                    return
                S.dma("sp", RT[:, 0, :], c_rope_ret[0, c * 128:(c + 1) * 128, :], writes=[Rrt])
                S.dma("sp", RT[:, 1, :], c_rope_ret[1, c * 128:(c + 1) * 128, :], writes=[Rrt])
                cosb = RT[:, 0, :].unsqueeze(1).broadcast_to([128, ng, 32])
                sinb = RT[:, 1, :].unsqueeze(1).broadcast_to([128, ng, 32])
                ta = TA[:, 0:ng * 32].rearrange("p (g e) -> p g e", g=ng)
                tb = TBt[:, 0:ng * 32].rearrange("p (g e) -> p g e", g=ng)
                S.op("dve", lambda: nc.vector.tensor_tensor(out=ta, in0=src[:, :, 0, :], in1=cosb, op=ALU.mult), reads=[RPS[bank], Rrt], writes=[Rta])
                S.op("dve", lambda: nc.vector.tensor_tensor(out=tb, in0=src[:, :, 1, :], in1=sinb, op=ALU.mult), reads=[RPS[bank], Rrt], writes=[Rta])
                S.op("dve", lambda: nc.vector.tensor_tensor(out=dv[:, :, 0, :], in0=ta, in1=tb, op=ALU.subtract), reads=[Rta], writes=[Rqk])
                S.op("dve", lambda: nc.vector.tensor_tensor(out=ta, in0=src[:, :, 0, :], in1=sinb, op=ALU.mult), reads=[RPS[bank], Rrt], writes=[Rta])
                S.op("dve", lambda: nc.vector.tensor_tensor(out=tb, in0=src[:, :, 1, :], in1=cosb, op=ALU.mult), reads=[RPS[bank], Rrt], writes=[Rta])
                S.op("dve", lambda: nc.vector.tensor_tensor(out=dv[:, :, 1, :], in0=ta, in1=tb, op=ALU.add), reads=[Rta], writes=[Rqk])

            def kdec_mul(d, ksrc):
                S.op("dve", lambda: nc.vector.tensor_tensor(out=KDt[:, :].rearrange("p (h e) -> p h e", h=4), in0=ksrc.rearrange("p (h e) -> p h e", h=4),
                                                            in1=KD[:, d, :].unsqueeze(2).broadcast_to([128, 4, 64]), op=ALU.mult), reads=[Rqk, Rtab], writes=[Rkd])

            def umat(bank):
                def emit():
                    inst = None
                    for h in range(4):
                        inst = nc.tensor.matmul(PS[0:64, bank, h * 64:(h + 1) * 64], lhsT=KDt[:, h * 64:(h + 1) * 64], rhs=VTc[:, h * 64:(h + 1) * 64], start=True, stop=True)
                    return inst
                S.op("pe", emit, reads=[Rkd, Rvt], writes=[RPS[bank]])

            def state_update(St, Rst, d, bank):
                S.op("dve", lambda: nc.vector.tensor_tensor(out=St[0:64, :], in0=St[0:64, :], in1=CD[0:64, d, :], op=ALU.mult), reads=[Rst, Rtab], writes=[Rst])
                S.op("dve", lambda: nc.vector.tensor_tensor(out=St[0:64, :], in0=St[0:64, :], in1=PS[0:64, bank, 0:256], op=ALU.add), reads=[Rst, RPS[bank]], writes=[Rst])

            for c in range(n - 1, -1, -1):
                proj(c, 0, WRa, 256, 256)
                proj(c, 1, WRb, 0, 256)
                rope(c, 0, 0, 4, QK[:, 0:256])
                S.op("act", lambda: nc.scalar.copy(out=VTc[:, :], in_=PS[:, 1, 0:256]), reads=[RPS[1]], writes=[Rvt])
                kdec_mul(1, QK[:, 0:256])
                umat(5)
                S.op("act", lambda c=c: nc.scalar.copy(out=SBst[0:64, c, :], in_=SB[0:64, :]), reads=[Rsb], writes=[Rsbst])
                state_update(SB, Rsb, 1, 5)
            S.op("act", lambda: nc.scalar.copy(out=SFb[0:64, :], in_=SF[0:64, :]), reads=[Rsf], writes=[Rsfb])
            if cfg.get("ret_stop") == "pass1":
                return
            for c in range(n):
                proj(c, 0, WRa, 0, 512)
                proj(c, 1, WRb, 0, 512)
                rope(c, 0, 0, 8, QK[:, :])
                S.op("act", lambda: nc.scalar.copy(out=VTc[:, :], in_=PS[:, 1, 0:256]), reads=[RPS[1]], writes=[Rvt])
                S.op("act", lambda: nc.scalar.activation(out=SRG[:, :], in_=PS[:, 1, 256:512], func=AF.Silu), reads=[RPS[1]], writes=[Rsrg])
                kdec_mul(0, QK[:, 256:512])

                def emit_t():
                    inst = None
                    for g in range(8):
                        inst = nc.tensor.transpose(out=PSb(2)[0:64, g * 128:(g + 1) * 128], in_=QK[:, g * 64:(g + 1) * 64], identity=identb[:])
                    return inst
                S.op("pe", emit_t, reads=[Rqk, Rid], writes=[RPS[2]])
                S.op("dve", lambda: nc.vector.tensor_copy(out=QT[0:64, 0, :], in_=PSb(2)[0:64, 0:512]), reads=[RPS[2]], writes=[Rqt])
                for d in range(2):
                    S.op("dve", lambda d=d: nc.vector.tensor_tensor(out=QT[0:64, 1 + d, :], in0=PSb(2)[0:64, 0:512], in1=QD[0:64, d, :, :].rearrange("p h i -> p (h i)"), op=ALU.mult),
                         reads=[RPS[2], Rtab], writes=[Rqt])
                S.op("dve", lambda: nc.vector.tensor_copy(out=KT[0:64, :], in_=PSb(2)[0:64, 512:1024]), reads=[RPS[2]], writes=[Rkt])
                if cfg.get("ret_stop") == "p2a":
                    continue

                def emit_a():
                    inst = None
                    for h in range(4):
                        inst = nc.tensor.matmul(PS[:, 3, h * 128:(h + 1) * 128], lhsT=KT[0:64, h * 128:(h + 1) * 128], rhs=QT[0:64, 0, h * 128:(h + 1) * 128], start=True, stop=True)
                    return inst
                S.op("pe", emit_a, reads=[Rkt, Rqt], writes=[RPS[3]])
                S.op("dve", lambda: nc.vector.tensor_tensor(out=AM[:, :], in0=PS[:, 3, :], in1=DM[:, :, :].rearrange("p h i -> p (h i)"), op=ALU.mult),
                     reads=[RPS[3], Rtab], writes=[Ram])
                if cfg.get("ret_stop") == "p2b":
                    continue

                def emit_o(c=c):
                    inst = None
                    for h in range(4):
                        oc = PS[:, 4, h * 64:(h + 1) * 64]
                        nc.tensor.matmul(oc, lhsT=AM[:, h * 128:(h + 1) * 128], rhs=VTc[:, h * 64:(h + 1) * 64], start=True, stop=False)
                        nc.tensor.matmul(oc, lhsT=QT[0:64, 1, h * 128:(h + 1) * 128], rhs=SFb[0:64, h * 64:(h + 1) * 64], start=False, stop=False)
                        inst = nc.tensor.matmul(oc, lhsT=QT[0:64, 2, h * 128:(h + 1) * 128], rhs=SBst[0:64, c, h * 64:(h + 1) * 64], start=False, stop=True)
                    return inst
                S.op("pe", emit_o, reads=[Ram, Rvt, Rqt, Rsfb, Rsbst], writes=[RPS[4]])
                umat(5)
                state_update(SF, Rsf, 0, 5)
                S.op("act", lambda: nc.scalar.copy(out=SFb[0:64, :], in_=SF[0:64, :]), reads=[Rsf], writes=[Rsfb])
                if cfg.get("ret_stop") == "p2c":
                    continue
                ov = PS[:, 4, 0:256].rearrange("p (h e) -> p h e", h=4)
                S.op("dve", lambda: nc.vector.tensor_reduce(out=MS[:, 0:4], in_=ov, axis=AX.X, op=ALU.add), reads=[RPS[4]], writes=[Rms])
                S.op("dve", lambda: nc.vector.tensor_scalar(out=MS[:, 0:4], in0=MS[:, 0:4], scalar1=-1.0 / 64, scalar2=None, op0=ALU.mult), reads=[Rms], writes=[Rms])
                cv = CEN[:, :].rearrange("p (h e) -> p h e", h=4)
                S.op("dve", lambda: nc.vector.tensor_tensor(out=cv, in0=ov, in1=MS[:, 0:4].unsqueeze(2).broadcast_to([128, 4, 64]), op=ALU.add),
                     reads=[RPS[4], Rms], writes=[Rcen])
                S.op("dve", lambda: nc.vector.tensor_tensor(out=SQr[:, :], in0=CEN[:, :], in1=CEN[:, :], op=ALU.mult), reads=[Rcen], writes=[Rsq])
                S.op("dve", lambda: nc.vector.tensor_reduce(out=MS[:, 4:8], in_=SQr[:, :].rearrange("p (h e) -> p h e", h=4), axis=AX.X, op=ALU.add), reads=[Rsq], writes=[Rms])
                S.op("act", lambda: nc.scalar.activation(out=MS[:, 4:8], in_=MS[:, 4:8], func=AF.Sqrt, scale=1.0 / 64, bias=epsb[:, 0:1]), reads=[Rms, Reps], writes=[Rms])
                S.op("dve", lambda: nc.vector.reciprocal(out=MS[:, 4:8], in_=MS[:, 4:8]), reads=[Rms], writes=[Rms])
                S.op("dve", lambda: nc.vector.tensor_tensor(out=cv, in0=cv, in1=MS[:, 4:8].unsqueeze(2).broadcast_to([128, 4, 64]), op=ALU.mult), reads=[Rcen, Rms], writes=[Rcen])
                S.op("dve", lambda: nc.vector.tensor_tensor(out=NRo[:, :], in0=CEN[:, :], in1=SRG[:, :], op=ALU.mult), reads=[Rcen, Rsrg], writes=[Rnro])

                if cfg.get("ret_stop") == "p2d":
                    continue

                def emit_t2():
                    inst = None
                    for cc in range(2):
                        inst = nc.tensor.transpose(out=PSb(6)[:, cc * 128:(cc + 1) * 128], in_=NRo[:, cc * 128:(cc + 1) * 128], identity=identb[:])
                    return inst
                S.op("pe", emit_t2, reads=[Rnro, Rid], writes=[RPS[6]])
                for cc in range(2):
                    S.op("dve", lambda cc=cc, c=c: nc.vector.tensor_scalar(out=cur["BR"][:, 2 + cc, c * 128:(c + 1) * 128], in0=PSb(6)[:, cc * 128:(cc + 1) * 128],
                                                                          scalar1=PV[:, 8 + cc:9 + cc], scalar2=None, op0=ALU.mult), reads=[RPS[6], Rpv], writes=[Rbr[2 + cc]])
            if not sample:
                pi = 0 if kind == "pA" else 1
                OUT_EVS.append(S.dma("sp", o_ret[pi, l, 0].rearrange("h d e -> d h e"), SF[0:64, :].rearrange("d (h e) -> d h e", h=4), reads=[Rsf], key=Reg("o_ret_d")))
                OUT_EVS.append(S.dma("sp", o_ret[pi, l, 1].rearrange("h d e -> d h e"), SB[0:64, :].rearrange("d (h e) -> d h e", h=4), reads=[Rsb], key=Reg("o_ret_d")))

        def branch_s5(l, t0, T, NB, BW, kind, multi=False):
            sample = kind == "sample"
            o = BS0
            UT, o = view(o, [128, 2, 2048], BF16)
            YS, o = view(o, [128, 2, 2048], BF16)
            YFp, o = view(o, [128, 2048], BF16)
            oZ = o
            TRI, o = view(o, [128, 2, 512])
            TRIb, o = view(o, [128, 2, 512], BF16)
            BZ, o = view(o, [128, 2, 512], BF16)
            oTT = o
            TT, o = view(o, [128, 2, 1024], BF16)
            TTf, _ = view(oTT, [128, 2, 512])
            oS = o
            ow = WIN0
            SBb, ow = view(ow, [128, 2, 512], BF16)
            OTs, ow = view(ow, [128, 512], BF16)
            BW_, ow = view(ow, [128, 2, 2, 128], BF16)
            BBR, ow = view(ow, [128, 16, 16])
            BBI, ow = view(ow, [128, 16, 16])
            CW, ow = view(ow, [128, 8, 2, 32], BF16)
            WGL, ow = view(ow, [128, 2, 256], BF16)
            YV, _ = view(WIN0, [128, 512])
            Rut, Rys, Rsfs, Rtri, Rbz, Rtt, Rsbb, Rots, Rrb, Rbw = (Reg("s_" + x) for x in ("UT", "YS", "YFp", "TRI", "BZ", "TT", "SBb", "OTs", "RB", "BW"))
            def sm_(shape, dt=F32):
                nonlocal o
                v, o = view(o, shape, dt)
                return v
            o1 = [oTT]

            def ot_(shape, dt=F32):
                v, o1[0] = view(o1[0], shape, dt)
                return v
            LRE, LIM, LDT, AR, AI, FR, FI = (ot_([128, 16]) for _ in range(7))
            BRE, BIM = ot_([128, 16, 16]), ot_([128, 16, 16])
            CNAT = ot_([128, 2, 64])
            MAG, UR, UI, W1, W2_, W3 = (sm_([128, 16]) for _ in range(6))
            UBR, UBI = sm_([128, 16]), sm_([128, 16])
            TA_, TBs = sm_([128, 2, 16, 16]), sm_([128, 2, 16, 32])
            PW = sm_([128, 2, 16])
            S0t = sm_([128, 16, 2])
            INI = sm_([128, 2, 2])
            FIN = sm_([128, 2, 16, 2])
            WP = sm_([128, 128])
            Rsu = Reg("s_setup")
            Rini, Rfin, Rwp, Rcn = Reg("s_INI"), Reg("s_FIN"), Reg("s_WP"), Reg("s_CN")
            Rbu = Reg("s_BU")
            Rt4 = [Reg("s_T0"), Reg("s_T1"), Reg("s_P0"), Reg("s_P1")]
            Rbz2 = [Reg("s_BZ0"), Reg("s_BZ1")]
            Rw12 = [Reg("s_W1"), Reg("s_W2")]
            V = nc.vector
            dbgon = cfg.get("s5dbg") == kind
            if dbgon:
                dbg2 = nc.dram_tensor("dbg2", [128, 4096], F32, kind="ExternalOutput").ap()

            def dbg(ap, c0, n, regs):
                if dbgon:
                    S.dma("sp", dbg2[:, c0:c0 + n], ap, reads=regs, key=Reg("dbg2"))

            def dv(fn, reads, writes):
                S.op("dve", fn, reads=reads, writes=writes)

            def tt(out, a, b, op, reads=(Rsu,), writes=(Rsu,)):
                dv(lambda: V.tensor_tensor(out=out, in0=a, in1=b, op=op), list(reads), list(writes))

            def cmul(orr, oi, ar, ai, br, bi, t1, t2, reads=(Rsu,), writes=(Rsu,)):
                tt(t1, ar, br, ALU.mult, reads, writes)
                tt(t2, ai, bi, ALU.mult, reads, writes)
                tt(t2, t1, t2, ALU.subtract, reads, writes)
                tt(t1, ar, bi, ALU.mult, reads, writes)
                tt(oi, ai, br, ALU.mult, reads, writes)
                tt(oi, t1, oi, ALU.add, reads, writes)
                tt(orr, t2, t2, ALU.max, reads, writes)

            for cc in range(2):
                proj_fm(l, cc * 128, 128, NB, BW, lambda b, bank, cc=cc: S.op(
                    "act", lambda: nc.scalar.copy(out=UT[:, cc, b * BW:(b + 1) * BW], in_=PS[:, bank, 0:BW]), reads=[RPS[bank]], writes=[Rut]))
            S.barrier()
            for d in range(2):
                for dst, src in ((LRE, s5_lam_re), (LIM, s5_lam_im)):
                    S.dma("sp", dst[:, d::2], src[l, d].rearrange("(m g) p -> (g p) m", g=2), writes=[Rsu], slow=True)
                for g2 in range(2):
                    S.dma("sp", LDT[g2 * 64:(g2 + 1) * 64, d::2], s5_log_dt[l, d:d + 1, g2::2].broadcast_to([64, 8]), writes=[Rsu], slow=True)
                for dst, src in ((BRE, s5_b_re), (BIM, s5_b_im)):
                    S.dma("sp", dst[:, d::2, :], src[l, d].rearrange("(m g) p h -> (g p) m h", g=2), writes=[Rsu])
                if sample:
                    S.dma("sp", S0t[:, d::2, :], st_s5[l, d].rearrange("(m g) p r -> (g p) m r", g=2), writes=[Rsu], slow=True)
            S.dma("pool", WGL[:, :, :], s5_w_glu[l].rearrange("(kc p) n -> p kc n", p=128), writes=[Rsu])
            S.op("act", lambda: nc.scalar.activation(out=LDT[:, :], in_=LDT[:, :], func=AF.Exp), reads=[Rsu], writes=[Rsu])
            tt(W1[:, :], LRE[:, :], LDT[:, :], ALU.mult)
            S.op("act", lambda: nc.scalar.activation(out=MAG[:, :], in_=W1[:, :], func=AF.Exp), reads=[Rsu], writes=[Rsu])
            tt(W1[:, :], LIM[:, :], LDT[:, :], ALU.mult)
            S.op("act", lambda: nc.scalar.activation(out=UI[:, :], in_=W1[:, :], func=AF.Sin, scale=1.0 / 64), reads=[Rsu], writes=[Rsu])
            S.op("act", lambda: nc.scalar.activation(out=UR[:, :], in_=W1[:, :], func=AF.Sin, scale=1.0 / 64, bias=halfpi[:, 0:1]), reads=[Rsu, Reps], writes=[Rsu])
            for _ in range(6):
                tt(W1[:, :], UR[:, :], UR[:, :], ALU.mult)
                tt(W2_[:, :], UI[:, :], UI[:, :], ALU.mult)
                tt(W3[:, :], UR[:, :], UI[:, :], ALU.mult)
                tt(UR[:, :], W1[:, :], W2_[:, :], ALU.subtract)
                tt(UI[:, :], W3[:, :], W3[:, :], ALU.add)
            tt(AR[:, :], MAG[:, :], UR[:, :], ALU.mult)
            tt(AI[:, :], MAG[:, :], UI[:, :], ALU.mult)
            tt(W1[:, :], LRE[:, :], LRE[:, :], ALU.mult)
            tt(W2_[:, :], LIM[:, :], LIM[:, :], ALU.mult)
            tt(W1[:, :], W1[:, :], W2_[:, :], ALU.add)
            dv(lambda: V.reciprocal(out=W1[:, :], in_=W1[:, :]), [Rsu], [Rsu])
            dv(lambda: V.tensor_scalar(out=W2_[:, :], in0=AR[:, :], scalar1=-1.0, scalar2=None, op0=ALU.add), [Rsu], [Rsu])
            tt(FR[:, :], W2_[:, :], LRE[:, :], ALU.mult)
            tt(W3[:, :], AI[:, :], LIM[:, :], ALU.mult)
            tt(FR[:, :], FR[:, :], W3[:, :], ALU.add)
            tt(FR[:, :], FR[:, :], W1[:, :], ALU.mult)
            tt(FI[:, :], AI[:, :], LRE[:, :], ALU.mult)
            tt(W3[:, :], W2_[:, :], LIM[:, :], ALU.mult)
            tt(FI[:, :], FI[:, :], W3[:, :], ALU.subtract)
            tt(FI[:, :], FI[:, :], W1[:, :], ALU.mult)
            dbg(MAG[:, :], 0, 16, [Rsu]); dbg(UR[:, :], 16, 16, [Rsu]); dbg(UI[:, :], 32, 16, [Rsu]); dbg(FR[:, :], 48, 16, [Rsu]); dbg(FI[:, :], 64, 16, [Rsu])
            frb = FR[:, :].unsqueeze(2).broadcast_to([128, 16, 16])
            fib = FI[:, :].unsqueeze(2).broadcast_to([128, 16, 16])
            tt(BBR[:, :, :], BRE[:, :, :], frb, ALU.mult)
            tt(BBI[:, :, :], BIM[:, :, :], fib, ALU.mult)
            tt(BBR[:, :, :], BBR[:, :, :], BBI[:, :, :], ALU.subtract)
            tt(BBI[:, :, :], BRE[:, :, :], fib, ALU.mult)
            tt(BRE[:, :, :], BIM[:, :, :], frb, ALU.mult)
            tt(BBI[:, :, :], BBI[:, :, :], BRE[:, :, :], ALU.add)
            dv(lambda: V.memset(CW[:, :, :, :], 0.0), [], [Rsu])
            for ri, src in ((0, s5_c_re), (1, s5_c_im)):
                S.dma("sp", CNAT[:, :, :], src[l].rearrange("(c g) h p -> (g h) c p", c=2), writes=[Rcn])
                CNB = TTf[:, 1, 0:64].bitcast(BF16)
                dv(lambda: V.tensor_copy(out=CNB.rearrange("p (c k) -> p c k", c=2), in_=CNAT[:, :, :]), [Rcn, Rtt], [Rtt])
                for c in range(2):
                    for half in range(2):
                        S.op("pe", lambda c=c, half=half: nc.tensor.matmul(PS[half * 64:(half + 1) * 64, 6, c * 128:(c + 1) * 128], lhsT=CNB[:, c * 64:(c + 1) * 64], rhs=identb[:, :],
                                                                           start=True, stop=True), reads=[Rtt, Rid], writes=[RPS[6]])
                ctv = PS[:, 6, 0:256].rearrange("q (m g h) -> q m g h", m=8, g=2)
                sc = 1.0 if ri == 0 else -1.0
                dv(lambda ri=ri, sc=sc: V.tensor_scalar(out=CW[0:64, :, ri, 0:16], in0=ctv[0:64, :, 0, :], scalar1=sc, scalar2=None, op0=ALU.mult), [RPS[6]], [Rsu])
                dv(lambda ri=ri, sc=sc: V.tensor_scalar(out=CW[64:128, :, ri, 16:32], in0=ctv[64:128, :, 1, :], scalar1=sc, scalar2=None, op0=ALU.mult), [RPS[6]], [Rsu])
            S.barrier()
            def build_pows(TAB, nent, base_r, base_i):
                dv(lambda: V.memset(TAB[:, 0, :, 0:1], 1.0), [], [Rsu])
                dv(lambda: V.memset(TAB[:, 1, :, 0:1], 0.0), [], [Rsu])
                tt(PW[:, 0, :], base_r, base_r, ALU.max)
                tt(PW[:, 1, :], base_i, base_i, ALU.max)
                nn = 1
                while nn < nent:
                    pr = PW[:, 0, :].unsqueeze(2).broadcast_to([128, 16, nn])
                    pi_ = PW[:, 1, :].unsqueeze(2).broadcast_to([128, 16, nn])
                    t1 = TTf[:, 0, 0:16 * nn].rearrange("p (k j) -> p k j", k=16)
                    t2 = TTf[:, 1, 0:16 * nn].rearrange("p (k j) -> p k j", k=16)
                    rr, ri = Rsu, Rtt
                    tt(t1, TAB[:, 0, :, 0:nn], pr, ALU.mult, (rr, ri), (ri,))
                    tt(t2, TAB[:, 1, :, 0:nn], pi_, ALU.mult, (rr, ri), (ri,))
                    tt(TAB[:, 0, :, nn:2 * nn], t1, t2, ALU.subtract, (rr, ri), (rr,))
                    tt(t1, TAB[:, 0, :, 0:nn], pi_, ALU.mult, (rr, ri), (ri,))
                    tt(t2, TAB[:, 1, :, 0:nn], pr, ALU.mult, (rr, ri), (ri,))
                    tt(TAB[:, 1, :, nn:2 * nn], t1, t2, ALU.add, (rr, ri), (rr,))
                    tt(W1[:, :], PW[:, 0, :], PW[:, 0, :], ALU.mult)
                    tt(W2_[:, :], PW[:, 1, :], PW[:, 1, :], ALU.mult)
                    tt(W3[:, :], PW[:, 0, :], PW[:, 1, :], ALU.mult)
                    tt(PW[:, 0, :], W1[:, :], W2_[:, :], ALU.subtract)
                    tt(PW[:, 1, :], W3[:, :], W3[:, :], ALU.add)
                    nn *= 2
            build_pows(TBs, 32, UR[:, :], UI[:, :])
            tt(W1[:, :], PW[:, 0, :], PW[:, 0, :], ALU.max)
            tt(W2_[:, :], PW[:, 1, :], PW[:, 1, :], ALU.max)
            tt(UBR[:, :], PW[:, 0, :], PW[:, 0, :], ALU.max)
            tt(UBI[:, :], PW[:, 1, :], PW[:, 1, :], ALU.max)
            build_pows(TA_, 16, UBR[:, :], UBI[:, :])
            if BW == 512:
                tt(UBR[:, :], PW[:, 0, :], PW[:, 0, :], ALU.max)
                tt(UBI[:, :], PW[:, 1, :], PW[:, 1, :], ALU.max)
            else:
                tt(UBR[:, :], TA_[:, 0, :, 8], TA_[:, 0, :, 8], ALU.max)
                tt(UBI[:, :], TA_[:, 1, :, 8], TA_[:, 1, :, 8], ALU.max)
            S.barrier()
            mcb = [0]
            for m in range(8):
                cc, m4 = m // 4, m % 4
                for d in range(2):
                    k = m * 2 + d
                    for ri, BB in ((0, BBR), (1, BBI)):
                        dv(lambda: V.memset(WP[:, :], 0.0), [Rwp], [Rwp])
                        dv(lambda BB=BB, k=k: V.tensor_copy(out=WP[0:64, m4 * 32:m4 * 32 + 16], in_=BB[0:64, k, :]), [Rsu, Rwp], [Rwp])
                        dv(lambda BB=BB, k=k: V.tensor_copy(out=WP[64:128, m4 * 32 + 16:m4 * 32 + 32], in_=BB[64:128, k, :]), [Rsu, Rwp], [Rwp])
                        S.op("pe", lambda: nc.tensor.transpose(out=PS[:, 7, 0:128], in_=WP[:, :], identity=ident[:]), reads=[Rwp, Rid], writes=[RPS[7]])
                        S.op("act", lambda d=d, ri=ri: nc.scalar.copy(out=BW_[:, d, ri, :], in_=PS[:, 7, 0:128]), reads=[RPS[7]], writes=[Rbw])
                for d in range(2):
                    k = m * 2 + d
                    rev = d == 1
                    ar = TA_[:, 0, k, :].unsqueeze(2).broadcast_to([128, 16, 32])
                    ai = TA_[:, 1, k, :].unsqueeze(2).broadcast_to([128, 16, 32])
                    br = TBs[:, 0, k, :].unsqueeze(1).broadcast_to([128, 16, 32])
                    bi = TBs[:, 1, k, :].unsqueeze(1).broadcast_to([128, 16, 32])
                    trv = TRI[:, 0, :].rearrange("p (q j) -> p q j", q=16)
                    tiv = TRI[:, 1, :].rearrange("p (q j) -> p q j", q=16)
                    t1 = TTf[:, 0, :].rearrange("p (q j) -> p q j", q=16)
                    t2 = TTf[:, 1, :].rearrange("p (q j) -> p q j", q=16)
                    rw = (Rsu, Rtt, Rtri) + tuple(Rt4)
                    tt(t1, ar, br, ALU.mult, rw, (Rtt,) + tuple(Rt4))
                    tt(t2, ai, bi, ALU.mult, rw, (Rtt,) + tuple(Rt4))
                    tt(trv, t1, t2, ALU.subtract, rw, (Rtri,))
                    tt(t1, ar, bi, ALU.mult, rw, (Rtt,) + tuple(Rt4))
                    tt(t2, ai, br, ALU.mult, rw, (Rtt,) + tuple(Rt4))
                    tt(tiv, t1, t2, ALU.add, rw, (Rtri,))
                    dv(lambda: V.tensor_copy(out=TRIb[:, :, :], in_=TRI[:, :, :]), [Rtri], [Rtri])
                    if k == 0:
                        dbg(TRI[:, 0, :], 128, 512, [Rtri]); dbg(TRI[:, 1, :], 640, 512, [Rtri])
                    ib = 0
                    if sample:
                        cmul(INI[:, 0, ib:ib + 1], INI[:, 1, ib:ib + 1], UR[:, k:k + 1], UI[:, k:k + 1], S0t[:, k, 0:1], S0t[:, k, 1:2], W1[:, 0:1], W2_[:, 0:1], (Rsu, Rini), (Rsu, Rini))
                    else:
                        dv(lambda: V.memset(INI[:, :, 0:1], 0.0), [Rini], [Rini])
                    blocks = list(range(NB - 1, -1, -1)) if rev else list(range(NB))
                    for bi_, b in enumerate(blocks):
                        cols = slice(b * BW, (b + 1) * BW)
                        bk = 2 * (mcb[0] % 2)
                        mcb[0] += 1
                        for ri in range(2):
                            S.op("pe", lambda ri=ri: nc.tensor.matmul(PS[:, bk + ri, 0:BW], lhsT=BW_[:, d, ri, :], rhs=UT[:, cc, cols], start=True, stop=True),
                                 reads=[Rbw, Rut], writes=[RPS[bk + ri]])
                        if rev:
                            trr, tri = TRI[:, 0, BW - 1::-1] if BW == 512 else TRI[:, 0, BW - 1::-1], TRI[:, 1, BW - 1::-1]
                            trr = TRI[:, 0, 0:BW][:, ::-1]
                            tri = TRI[:, 1, 0:BW][:, ::-1]
                            trrb = TRIb[:, 0, 0:BW][:, ::-1]
                            trib = TRIb[:, 1, 0:BW][:, ::-1]
                        else:
                            trr, tri = TRI[:, 0, 0:BW], TRI[:, 1, 0:BW]
                            trrb, trib = TRIb[:, 0, 0:BW], TRIb[:, 1, 0:BW]
                        for ri in range(2):
                            S.op("act", lambda ri=ri: nc.scalar.copy(out=SBb[:, ri, 0:BW], in_=PS[:, bk + ri, 0:BW]), reads=[RPS[bk + ri]], writes=[Rsbb])
                        pre, pim = SBb[:, 0, 0:BW], SBb[:, 1, 0:BW]
                        T0_, T1_ = TT[:, 0, 0:BW], TT[:, 1, 0:BW]
                        P0_, P1_ = TT[:, 0, 512:512 + BW], TT[:, 1, 512:512 + BW]
                        B0_, B1_ = BZ[:, 0, 0:BW], BZ[:, 1, 0:BW]
                        tt(T0_, pre, trrb, ALU.mult, (Rtri, Rsbb, Rt4[0]), (Rt4[0],))
                        tt(T1_, pim, trib, ALU.mult, (Rtri, Rsbb, Rt4[1]), (Rt4[1],))
                        tt(P0_, pim, trrb, ALU.mult, (Rtri, Rsbb, Rt4[2]), (Rt4[2],))
                        tt(P1_, pre, trib, ALU.mult, (Rtri, Rsbb, Rt4[3]), (Rt4[3],))
                        tt(B0_, T0_, T1_, ALU.add, (Rt4[0], Rt4[1], Rbz2[0]), (Rbz2[0],))
                        tt(B1_, P0_, P1_, ALU.subtract, (Rt4[2], Rt4[3], Rbz2[1]), (Rbz2[1],))
                        for ri in range(2):
                            zo = TT[:, ri, 0:BW]
                            zin = BZ[:, ri, 0:BW]
                            if rev:
                                zo, zin = zo[:, ::-1], zin[:, ::-1]
                            dv(lambda zo=zo, zin=zin, ri=ri: V.tensor_tensor_scan(out=zo, data0=MAG[:, k:k + 1].broadcast_to([128, BW]), data1=zin, initial=INI[:, ri, ib:ib + 1], op0=ALU.mult, op1=ALU.add),
                               [Rsu, Rbz2[ri], Rini, Rt4[ri]], [Rt4[ri]])
                        zl = 0 if rev else BW - 1
                        zr_, zi_ = TT[:, 0, zl:zl + 1], TT[:, 1, zl:zl + 1]
                        dstS = SBb[:, :, 0:BW]
                        Rdst = Rsbb
                        tt(B0_, T0_, trrb, ALU.mult, (Rtri, Rt4[0], Rbz2[0]), (Rbz2[0],))
                        tt(B1_, T1_, trib, ALU.mult, (Rtri, Rt4[1], Rbz2[1]), (Rbz2[1],))
                        tt(P0_, T1_, trrb, ALU.mult, (Rtri, Rt4[1], Rt4[2]), (Rt4[2],))
                        tt(P1_, T0_, trib, ALU.mult, (Rtri, Rt4[0], Rt4[3]), (Rt4[3],))
                        tt(dstS[:, 0, :], B0_, B1_, ALU.subtract, (Rbz2[0], Rbz2[1], Rdst), (Rdst,))
                        tt(dstS[:, 1, :], P0_, P1_, ALU.add, (Rt4[2], Rt4[3], Rdst), (Rdst,))
                        last_blk = bi_ == NB - 1 or multi
                        if multi and bi_ != NB - 1:
                            dv(lambda: V.memset(INI[:, :, 1 - ib:2 - ib], 0.0), [Rini], [Rini])
                        if not last_blk:
                            ib2 = 1 - ib
                            rc, wc = [Rsu, Rini, Rt4[0], Rt4[1]], [Rini]
                            dv(lambda: V.tensor_scalar(out=W1[:, 0:1], in0=zi_, scalar1=UBI[:, k:k + 1], scalar2=None, op0=ALU.mult), [Rsu, Rt4[1], Rw12[0]], [Rw12[0]])
                            dv(lambda: V.tensor_scalar(out=W2_[:, 0:1], in0=zr_, scalar1=UBI[:, k:k + 1], scalar2=None, op0=ALU.mult), [Rsu, Rt4[0], Rw12[1]], [Rw12[1]])
                            dv(lambda: V.scalar_tensor_tensor(out=INI[:, 0, ib2:ib2 + 1], in0=zr_, scalar=UBR[:, k:k + 1], in1=W1[:, 0:1], op0=ALU.mult, op1=ALU.subtract), [Rsu, Rt4[0], Rw12[0], Rini], [Rini])
                            dv(lambda: V.scalar_tensor_tensor(out=INI[:, 1, ib2:ib2 + 1], in0=zi_, scalar=UBR[:, k:k + 1], in1=W2_[:, 0:1], op0=ALU.mult, op1=ALU.add), [Rsu, Rt4[1], Rw12[1], Rini], [Rini])
                            ib = ib2
                        elif not sample:
                            pf = b if multi else 0
                            cmul(FIN[:, pf, k, 0:1], FIN[:, pf, k, 1:2], TRI[:, 0, BW - 1:BW], TRI[:, 1, BW - 1:BW], zr_, zi_, W1[:, 0:1], W2_[:, 0:1], (Rsu, Rtri, Rt4[0], Rt4[1], Rfin), (Rsu, Rfin))
                        if multi and bi_ != NB - 1:
                            ib = 1 - ib
                        def emit_y():
                            nc.tensor.matmul(PS[0:32, 4, 0:BW], lhsT=CW[:, m, 0, :], rhs=SBb[:, 0, 0:BW], start=True, stop=False)
                            return nc.tensor.matmul(PS[0:32, 4, 0:BW], lhsT=CW[:, m, 1, :], rhs=SBb[:, 1, 0:BW], start=False, stop=True)
                        S.op("pe", emit_y, reads=[Rsu, Rsbb], writes=[RPS[4]])
                        if not rev:
                            S.op("act", lambda: nc.scalar.copy(out=YFp[0:32, cols], in_=PS[0:32, 4, 0:BW]), reads=[RPS[4]], writes=[Rsfs])
                        else:
                            tt(OTs[0:32, 0:BW], PS[0:32, 4, 0:BW], YFp[0:32, cols], ALU.add, (RPS[4], Rsfs, Rots), (Rots,))
                            S.dma("sp", YS[m4 * 32:(m4 + 1) * 32, cc, cols], OTs[0:32, 0:BW], reads=[Rots], writes=[Rys], key=Reg("s_OTd"))
            if not sample:
                for pi in ((0, 1) if multi else ((0 if kind == "pA" else 1),)):
                    pf = pi if multi else 0
                    for d in range(2):
                        OUT_EVS.append(S.dma("sp", o_s5[pi, l, d].rearrange("(m g) p r -> (g p) m r", g=2), FIN[:, pf, d::2, :], reads=[Rfin], key=Reg("o_s5_d"), slow=True))
            S.barrier()
            Z, _ = view(oZ, [128, 2, 2048], BF16)
            Rz_ = Reg("s_Z")
            for b in range(NB):
                cols = slice(b * BW, (b + 1) * BW)
                for cc in range(2):
                    yv_ = YV[:, 0:BW]
                    dv(lambda: V.scalar_tensor_tensor(out=yv_, in0=UT[:, cc, cols], scalar=PV[:, 10 + cc:11 + cc], in1=YS[:, cc, cols], op0=ALU.mult, op1=ALU.add),
                       [Rut, Rpv, Rys, Rbz], [Rbz])
                    tt(TTf[:, 0, 0:BW], yv_, yv_, ALU.mult, (Rbz, Rtt), (Rtt,))
                    dv(lambda: V.tensor_scalar(out=TTf[:, 0, 0:BW], in0=TTf[:, 0, 0:BW], scalar1=0.044715, scalar2=1.0, op0=ALU.mult, op1=ALU.add), [Rtt], [Rtt])
                    tt(TTf[:, 0, 0:BW], TTf[:, 0, 0:BW], yv_, ALU.mult, (Rbz, Rtt), (Rtt,))
                    S.op("act", lambda: nc.scalar.activation(out=TTf[:, 1, 0:BW], in_=TTf[:, 0, 0:BW], func=AF.Sigmoid, scale=1.5957691216057308), reads=[Rtt], writes=[Rtt])
                    tt(Z[:, cc, cols], TTf[:, 1, 0:BW], yv_, ALU.mult, (Rbz, Rtt, Rz_), (Rz_,))
                for co in range(2):
                    def emit_g(co=co):
                        nc.tensor.matmul(PS[:, 5, 0:BW], lhsT=WGL[:, 0, co * 128:(co + 1) * 128], rhs=Z[:, 0, cols], start=True, stop=False)
                        return nc.tensor.matmul(PS[:, 5, 0:BW], lhsT=WGL[:, 1, co * 128:(co + 1) * 128], rhs=Z[:, 1, cols], start=False, stop=True)
                    S.op("pe", emit_g, reads=[Rsu, Rz_], writes=[RPS[5]])
                    S.op("act", lambda: nc.scalar.activation(out=OTs[:, 0:BW], in_=PS[:, 5, 0:BW], func=AF.Sigmoid), reads=[RPS[5]], writes=[Rots])
                    tt(cur["BR"][:, co, cols], Z[:, co, cols], OTs[:, 0:BW], ALU.mult, (Rz_, Rots), (Rbr[co],))

        DBG = {}

        def mixer(l):
            make_gate_bcast(1)
            mixer_params(l)

            def build_ht(t0, ntile, ci):
                S.barrier()
                cur["XN"], _ = view(BS0, [128, 2, D])
                for tt in range(ntile):
                    prenorm_tile(t0 + tt, ci, 1, HT[:, :, tt * 128:(tt + 1) * 128], Rht, (4 + 2 * (tt % 2), 5 + 2 * (tt % 2)))
                S.barrier()

            def zero_disabled(T):
                for nm, chs in (("s5", (0, 1)), ("ret", (2, 3)), ("conv", (4, 5)), ("mla", (6, 7))):
                    if not cfg.get(nm, True):
                        for i in chs:
                            S.op("dve", lambda i=i: nc.vector.memset(BR[:, i, 0:T], 0.0), writes=[Rbr[i]])

            def seq_branches(t0, T, NB, BW, kind):
                if cfg.get("conv", True):
                    branch_conv(l, T, NB, BW)
                    S.barrier()
                if cfg.get("mla", True):
                    branch_mla(l, t0, T, NB, BW, kind)
                    S.barrier()
                if cfg.get("ret", True):
                    branch_ret(l, t0, T, kind)
                    S.barrier()

            cur["HT"], cur["BR"] = HT, BR
            build_ht(0, 16, 0)
            zero_disabled(2048)
            seq_branches(0, 2048, 4, 512, "sample")
            if cfg.get("s5", True):
                branch_s5(l, 0, 2048, 4, 512, "sample")
                S.barrier()
            if tuple(cfg.get("dump_br", ())) == (l, "sample"):
                dbg = nc.dram_tensor("dbg_br", [128, 8, 2048], BF16, kind="ExternalOutput").ap()
                S.dma("sp", dbg[:, :, 0:2048], BR[:, :, 0:2048], reads=Rbr, key=Reg("dbg"))
            gate_stage(l, 0, 16, 0, 2048, 4, 512)
            S.barrier()
            build_ht(16, 4, 1)
            zero_disabled(512)
            for pi, kind in ((0, "pA"), (1, "pB")):
                cur["HT"], cur["BR"] = HT[:, :, pi * 256:(pi + 1) * 256], BR[:, :, pi * 256:(pi + 1) * 256]
                mla_cache_out(l, 16 + 2 * pi, 2, pi)
                S.barrier()
                seq_branches(16 + 2 * pi, 256, 1, 256, kind)
            cur["HT"], cur["BR"] = HT, BR
            if cfg.get("s5", True):
                branch_s5(l, 16, 512, 2, 256, "pAB", multi=True)
                S.barrier()
            gate_stage(l, 16, 4, 1, 512, 1, 512)
            S.barrier()
            cur["XN"], cur["TMP"] = XN, TMP

        for l in range(LAYERS):
            S.barrier()
            compute_mod(l)
            S.barrier()
            if cfg.get("ffn1", True):
                ffn(l, 0, 0)
            S.barrier()
            if cfg.get("mixer", True):
                mixer(l)
            S.barrier()
            if cfg.get("ffn2", True):
                ffn(l, 1, 2)

        yv = y.rearrange("(t p) d -> p t d", p=128)
        Ryout = Reg("yout")
        evs = []
        for t in range(NT):
            evs.append(S.dma("sp", yv[:, t, :], X[:, t, :], reads=[RX[t]], key=Ryout))
        S._wait("sp", set([evs[-1]] + OUT_EVS))
        S.barrier()
    return nc


def _axial(T, dim):
    rows = T // 64
    row = np.repeat(np.arange(rows, dtype=np.float32), 64)
    col = np.tile(np.arange(64, dtype=np.float32), rows)
    quarter = dim // 4
    inv = (np.float32(10000.0) ** (-np.arange(quarter, dtype=np.float32) / np.float32(quarter))).astype(np.float32)
    ang = np.concatenate([row[:, None] * inv, col[:, None] * inv], axis=-1).astype(np.float32)
    return np.cos(ang).astype(np.float32), np.sin(ang).astype(np.float32)


def _rope_tables():
    c, s = _axial(2048, 32)
    mla = np.stack([np.concatenate([c.T, c.T], 0), np.concatenate([s.T, s.T], 0)], 0)
    c2, s2 = _axial(2048, 64)
    ret = np.stack([c2, s2], 0)
    return np.ascontiguousarray(mla, dtype=np.float32), np.ascontiguousarray(ret, dtype=np.float32)


def _prep_inputs(inputs):
    f = lambda a: np.ascontiguousarray(np.asarray(a, dtype=np.float32))
    shared = {k: f(inputs[k]) for k in (
        "w_mod", "b_mod", "norm_pre", "norm_post", "ffn_w1", "ffn_w3", "ffn_w2", "w_in", "s5_lam_re", "s5_lam_im", "s5_log_dt",
        "s5_b_re", "s5_b_im", "s5_c_re", "s5_c_im", "s5_d", "s5_w_glu", "ret_decay", "ret_gn", "conv_w", "conv_b",
        "mla_q_norm", "mla_w_uq", "mla_kv_norm", "mla_w_ukv", "w_branch", "w_gate", "b_gate", "w_o")}
    shared["c_ident"] = np.eye(128, dtype=np.float32)
    shared["c_rope_mla"], shared["c_rope_ret"] = _rope_tables()
    jj = np.arange(128, dtype=np.float32)[:, None]
    ii = np.arange(128, dtype=np.float32)[None, :]
    diff = ii - jj
    shared["c_ret"] = np.ascontiguousarray(np.stack([
        np.maximum(diff, 0), np.maximum(-diff, 0), 0.125 * (diff >= 0), 0.125 * (diff < 0),
        np.broadcast_to(ii + 1.0, (128, 128)), np.broadcast_to(128.0 - ii, (128, 128))], 0), dtype=np.float32)
    shared["c_pidx"] = np.ascontiguousarray(np.concatenate([127.0 - jj, jj], 1), dtype=np.float32)
    xp = f(inputs["x_prompt"])
    xs = f(inputs["x_sample"])
    c = f(inputs["c"])
    cctx = f(inputs["c_ctx"])
    maps = []
    for i in range(NCORES):
        m = dict(shared)
        m["xin"] = np.ascontiguousarray(np.concatenate([xs[i], xp[2 * i], xp[2 * i + 1]], axis=0))
        m["cond2"] = np.ascontiguousarray(np.stack([c[i], cctx], axis=0))
        m["st_s5"] = f(inputs["state_s5"][i])
        m["st_ret"] = f(inputs["state_ret"][i])
        m["ctx_mla"] = f(inputs["cache_mla"][i])
        maps.append(m)
    return maps


def _gather(res):
    ys = np.stack([r["y"][:2048] for r in res], axis=0)
    yp = np.stack([r["y"][2048 + 256 * j:2048 + 256 * (j + 1)] for r in res for j in range(2)], axis=0)
    s5 = np.concatenate([r["o_s5"] for r in res], axis=0)
    ret = np.concatenate([r["o_ret"] for r in res], axis=0)
    mla = np.concatenate([r["o_mla"] for r in res], axis=0)
    return (yp.astype(np.float32), ys.astype(np.float32), s5.astype(np.float32), ret.astype(np.float32), mla.astype(np.float32))


CFG = {}


def kernel(**inputs):
    nc = build(CFG)
    maps = _prep_inputs(inputs)
    res = run_bass_kernel_spmd(nc, maps, core_ids=list(range(NCORES)))
    return _gather(res.results)
```

```python
import numpy as np
import concourse.bass as bass
import concourse.mybir as mybir
from concourse.bass_utils import run_bass_kernel_spmd
from contextlib import ExitStack

F32 = mybir.dt.float32
BF16 = mybir.dt.bfloat16
AF = mybir.ActivationFunctionType
ALU = mybir.AluOpType
AX = mybir.AxisListType

D = 1024
DFF = 2816
NFF = 22
TOK = 2560
NT = 20
EPS = 1e-6
NCORES = 8
INC = 2400

SAME_ENG_SYNC = True


_REGS = {}


def Reg(name):
    if name not in _REGS:
        _REGS[name] = _Reg(name)
    return _REGS[name]


class _Reg:
    __slots__ = ("name", "w", "r", "dsem", "dcnt")

    def __init__(self, name):
        self.name = name
        self.w = None
        self.r = []
        self.dsem = None
        self.dcnt = 0


class Sched:
    def __init__(self, nc, es):
        self.nc = nc
        self.es = es
        self.eng = {"pe": nc.tensor, "act": nc.scalar, "dve": nc.vector, "pool": nc.gpsimd, "sp": nc.sync}
        self.sem = {e: es.enter_context(nc.semaphore("s_" + e)) for e in self.eng}
        self.cnt = {e: 0 for e in self.eng}
        self.seen = {e: {} for e in self.eng}
        self.seen_d = {e: {} for e in self.eng}
        self.nsem = 0
        self.out_events = []
        self.pending_reads = {}

    def _wait(self, e, deps, raw=None):
        best = {}
        bestd = {}
        for d in deps:
            if d[0] == "e":
                _, e2, c = d
                if e2 == e and (e == "pe" or not SAME_ENG_SYNC or (raw is not None and d not in raw)):
                    continue
                if c > best.get(e2, 0):
                    best[e2] = c
            else:
                _, sem, tgt, key = d
                if tgt > bestd.get(key, (None, 0))[1]:
                    bestd[key] = (sem, tgt)
        E = self.eng[e]
        for e2, c in best.items():
            if self.seen[e].get(e2, 0) >= c:
                continue
            E.wait_ge(self.sem[e2], c)
            self.seen[e][e2] = c
        for key, (sem, tgt) in bestd.items():
            if self.seen_d[e].get(key, 0) >= tgt:
                continue
            E.wait_ge(sem, tgt)
            self.seen_d[e][key] = tgt

    def _deps(self, reads, writes):
        deps = set()
        for r in reads:
            if r.w is not None:
                deps.add(r.w)
        for w in writes:
            if w.w is not None:
                deps.add(w.w)
            deps.update(w.r)
        return deps

    def op(self, e, emit, reads=(), writes=()):
        raw = set(r.w for r in reads if r.w is not None)
        self._wait(e, self._deps(reads, writes), raw)
        inst = emit()
        self.cnt[e] += 1
        inst.then_inc(self.sem[e], 1)
        ev = ("e", e, self.cnt[e])
        for r in reads:
            r.r.append(ev)
        for w in writes:
            w.w = ev
            w.r = []
        return ev

    def dma(self, e, out, in_, reads=(), writes=(), key=None, slow=False):
        key = key or (list(writes) + list(reads))[0]
        if key.dsem is None:
            key.dsem = self.es.enter_context(self.nc.semaphore("d%d" % self.nsem))
            self.nsem += 1
        deps = set(d for d in self._deps(reads, writes) if not (d[0] == "d" and d[3] == key.name))
        self._wait(e, deps)
        inst = self.eng[e].dma_start(out=out, in_=in_, allow_slow_non_contiguous=True) if slow else self.eng[e].dma_start(out=out, in_=in_)
        key.dcnt += 16
        inst.then_inc(key.dsem, 16)
        ev = ("d", key.dsem, key.dcnt, key.name)
        if reads:
            self.pending_reads[key.name] = ev
        for r in reads:
            r.r.append(ev)
        for w in writes:
            w.w = ev
            w.r = []
        return ev

    def barrier(self):
        pend = set(self.pending_reads.values())
        for e in self.eng:
            deps = set(("e", e2, self.cnt[e2]) for e2 in self.eng if e2 != e and self.cnt[e2] > 0)
            self._wait(e, deps | pend)
        self.pending_reads = {}


def build(cfg):
    _REGS.clear()
    nc = bass.Bass("TRN2", target_bir_lowering=False)
    LAYERS = cfg.get("layers", 2)

    def din(name, shape):
        return nc.dram_tensor(name, list(shape), F32, kind="ExternalInput").ap()

    def dout(name, shape):
        return nc.dram_tensor(name, list(shape), F32, kind="ExternalOutput").ap()

    xin = din("xin", [TOK, D])
    cond2 = din("cond2", [2, D])
    st_s5 = din("st_s5", [2, 2, 16, 64, 2])
    st_ret = din("st_ret", [2, 2, 4, 64, 64])
    ctx_mla = din("ctx_mla", [2, 512, 160])
    w_mod = din("w_mod", [2, D, 9 * D])
    b_mod = din("b_mod", [2, 9 * D])
    norm_pre = din("norm_pre", [2, 3, D])
    norm_post = din("norm_post", [2, 3, D])
    ffn_w1 = din("ffn_w1", [2, 2, D, DFF])
    ffn_w3 = din("ffn_w3", [2, 2, D, DFF])
    ffn_w2 = din("ffn_w2", [2, 2, DFF, D])
    w_in = din("w_in", [2, D, INC])
    s5_lam_re = din("s5_lam_re", [2, 2, 16, 64])
    s5_lam_im = din("s5_lam_im", [2, 2, 16, 64])
    s5_log_dt = din("s5_log_dt", [2, 2, 16])
    s5_b_re = din("s5_b_re", [2, 2, 16, 64, 16])
    s5_b_im = din("s5_b_im", [2, 2, 16, 64, 16])
    s5_c_re = din("s5_c_re", [2, 16, 16, 64])
    s5_c_im = din("s5_c_im", [2, 16, 16, 64])
    s5_d = din("s5_d", [2, 256])
    s5_w_glu = din("s5_w_glu", [2, 256, 256])
    ret_decay = din("ret_decay", [2, 2, 4])
    ret_gn = din("ret_gn", [2, 256])
    conv_w = din("conv_w", [2, 3, 256])
    conv_b = din("conv_b", [2, 256])
    mla_q_norm = din("mla_q_norm", [2, 192])
    mla_w_uq = din("mla_w_uq", [2, 192, 384])
    mla_kv_norm = din("mla_kv_norm", [2, 128])
    mla_w_ukv = din("mla_w_ukv", [2, 128, 512])
    w_branch = din("w_branch", [2, 4, 256, D])
    w_gate = din("w_gate", [2, D, 4 * D])
    b_gate = din("b_gate", [2, 4 * D])
    w_o = din("w_o", [2, D, D])
    c_ident = din("c_ident", [128, 128])
    c_rope_mla = din("c_rope_mla", [2, 32, 2048])
    c_rope_ret = din("c_rope_ret", [2, 2048, 32])
    c_ret = din("c_ret", [6, 128, 128])
    c_pidx = din("c_pidx", [128, 2])

    y = dout("y", [TOK, D])
    o_s5 = dout("o_s5", [2, 2, 2, 16, 64, 2])
    o_ret = dout("o_ret", [2, 2, 2, 4, 64, 64])
    o_mla = dout("o_mla", [2, 2, 256, 160])

    es = ExitStack()
    with es:
        S = Sched(nc, es)

        def sb(name, shape, dt=F32):
            return es.enter_context(nc.sbuf_tensor(name, list(shape), dt))

        X = sb("X", [128, NT, D])
        RX = [Reg("X%d" % t) for t in range(NT)]
        PS = es.enter_context(nc.psum_tensor("PS", [128, 8, 512], F32))
        RPS = [Reg("PS%d" % b) for b in range(8)]
        ident = sb("ident", [128, 128])
        identb = sb("identb", [128, 128], BF16)
        Rid = Reg("ident")
        VEC = sb("VEC", [128, 2, 72])
        Rvec = Reg("VEC")
        NRM = sb("NRM", [128, 48])
        Rnrm = Reg("NRM")
        SV = sb("SV", [128, 2, 3, 8])
        Rsv = Reg("SV")
        GV = sb("GV", [128, 2, 3, 8])
        Rgv = Reg("GV")
        GB = sb("GB", [128, 2, D])
        Rgb = [Reg("GB0"), Reg("GB1")]
        DG = sb("DG", [128, 2, 128])
        Rdg = [Reg("DG0"), Reg("DG1")]
        small = sb("small", [128, 4, 4])
        Rsmall = [Reg("sm%d" % i) for i in range(4)]
        junk = sb("junk", [128, D], BF16)
        Rjunk = Reg("junk")
        ACOLS = 28672
        ARENA = sb("ARENA", [128, ACOLS])

        def view(off, shape, dt=F32):
            n = int(np.prod(shape[1:]))
            nbytes = n * (2 if dt == BF16 else 4)
            assert off % 4 == 0 and nbytes % 4 == 0 and off + nbytes <= ACOLS * 4, (off, shape)
            ap = ARENA[:, off // 4:(off + nbytes) // 4]
            if dt == BF16:
                ap = ap.bitcast(BF16)
            if len(shape) > 2:
                names = "abcdef"[:len(shape) - 1]
                pat = "p (" + " ".join(names) + ") -> p " + " ".join(names)
                ap = ap.rearrange(pat, **{names[i]: shape[i + 1] for i in range(len(names) - 1)})
            return ap, off + nbytes

        HTB, _o = view(0, [128, 2, 8, 512], BF16)
        GT, _o = view(_o, [128, NFF, 512], BF16)
        W13, _o = view(_o, [128, 3, 2, 8, 256], BF16)
        W2, _o = view(_o, [128, 3, 2, D], BF16)
        SIL, _o = view(_o, [128, 2, 512], BF16)
        XN, _o = view(_o, [128, 2, D])
        TMP, _o = view(_o, [128, 2, D])
        WM, _ = view(0, [128, 2, 8, 512], BF16)
        Rxn = [Reg("XN0"), Reg("XN1")]

        xv = xin.rearrange("(t p) d -> p t d", p=128)
        Rxall = Reg("xall")
        for t in range(NT):
            S.dma("sp", X[:, t, :], xv[:, t, :], writes=[RX[t]], key=Rxall)
        for t in range(NT):
            RX[t].w = ("d", Rxall.dsem, Rxall.dcnt, Rxall.name)
        S.dma("sp", ident[:], c_ident[:, :], writes=[Rid])
        S.op("dve", lambda: nc.vector.tensor_copy(out=identb[:], in_=ident[:]), reads=[Rid], writes=[Rid])

        rot = {"ps": 0, "sm": 0, "xn": 0, "dg": 0}
        OUT_EVS = []

        stage = sb("stage", [128, 128])
        Rstage = Reg("stage")

        def load_T(dst_ap, src_ap, rows, dst_reg, bank=7):
            S.dma("sp", stage[0:rows, :], src_ap, writes=[Rstage])
            S.op("pe", lambda: nc.tensor.transpose(out=PS[:, bank, 0:rows], in_=stage[0:rows, :], identity=ident[0:rows, 0:rows]),
                 reads=[Rstage, Rid], writes=[RPS[bank]])
            S.op("dve", lambda: nc.vector.tensor_copy(out=dst_ap, in_=PS[:, bank, 0:rows]), reads=[RPS[bank]], writes=[dst_reg])

        SCT = sb("SCT", [128, 8, 2], BF16)
        Rsct = Reg("SCT")
        sct32 = sb("sct32", [128, 16])
        load_T(sct32[:, :], cond2.rearrange("c (k p) -> (c k) p", p=128), 16, Rsct)
        S.op("act", lambda: nc.scalar.activation(out=SCT[:].rearrange("p k c -> p c k"), in_=sct32[:].rearrange("p (c k) -> p c k", c=2), func=AF.Silu),
             reads=[Rsct], writes=[Rsct])

        Rwm = [Reg("WM0"), Reg("WM1")]
        BM = sb("BM", [128, 72])
        Rbm = Reg("BM")

        def compute_mod(l):
            load_T(BM[:, :], b_mod[l].rearrange("(c p) -> c p", p=128), 72, Rbm)
            load_T(NRM[:, 0:24], norm_pre[l].rearrange("s (c p) -> (s c) p", p=128), 24, Rnrm)
            load_T(NRM[:, 24:48], norm_post[l].rearrange("s (c p) -> (s c) p", p=128), 24, Rnrm)
            wv = w_mod[l].rearrange("(kc p) n -> p kc n", p=128)
            for cb in range(18):
                sl = cb % 2
                S.dma("pool", WM[:, sl, :, :], wv[:, :, cb * 512:(cb + 1) * 512], writes=[Rwm[sl]])
                bank = 6

                def emit(cb=cb, sl=sl):
                    inst = None
                    for cc in range(4):
                        for kc in range(8):
                            inst = nc.tensor.matmul(PS[:, bank, cc * 2:cc * 2 + 2], lhsT=WM[:, sl, kc, cc * 128:(cc + 1) * 128],
                                                    rhs=SCT[:, kc, :], start=(kc == 0), stop=(kc == 7))
                    return inst
                S.op("pe", emit, reads=[Rwm[sl], Rsct], writes=[RPS[bank]])
                S.op("dve", lambda cb=cb: nc.vector.tensor_tensor(
                    out=VEC[:, :, cb * 4:(cb + 1) * 4].rearrange("p c j -> p j c"),
                    in0=PS[:, bank, 0:8].rearrange("p (j c) -> p j c", c=2),
                    in1=BM[:, cb * 4:(cb + 1) * 4].unsqueeze(2).broadcast_to([128, 4, 2]), op=ALU.add),
                    reads=[RPS[bank], Rbm], writes=[Rvec])
            for ci in range(2):
                for s in range(3):
                    S.op("dve", lambda ci=ci, s=s: nc.vector.scalar_tensor_tensor(
                        out=SV[:, ci, s, :], in0=VEC[:, ci, (3 * s + 1) * 8:(3 * s + 2) * 8], scalar=1.0,
                        in1=NRM[:, s * 8:(s + 1) * 8], op0=ALU.add, op1=ALU.mult), reads=[Rvec, Rnrm], writes=[Rsv])
                    fac = 1.0 if s == 1 else 0.5
                    S.op("dve", lambda ci=ci, s=s, fac=fac: nc.vector.scalar_tensor_tensor(
                        out=GV[:, ci, s, :], in0=VEC[:, ci, (3 * s + 2) * 8:(3 * s + 3) * 8], scalar=fac,
                        in1=NRM[:, 24 + s * 8:24 + (s + 1) * 8], op0=ALU.mult, op1=ALU.mult), reads=[Rvec, Rnrm], writes=[Rgv])

        def make_gate_bcast(s):
            for ci in range(2):
                bank0 = 4 + 2 * ci
                for c in range(8):
                    dgi = rot["dg"] % 2
                    rot["dg"] += 1
                    S.op("dve", lambda c=c, ci=ci, dgi=dgi: nc.vector.tensor_scalar(
                        out=DG[:, dgi, :], in0=ident[:], scalar1=GV[:, ci, s, c:c + 1], scalar2=None, op0=ALU.mult),
                        reads=[Rid, Rgv], writes=[Rdg[dgi]])
                    bank = bank0 + c // 4
                    S.op("pe", lambda c=c, dgi=dgi, bank=bank: nc.tensor.matmul(
                        PS[:, bank, (c % 4) * 128:(c % 4 + 1) * 128], lhsT=ones32[:], rhs=DG[:, dgi, :], start=True, stop=True),
                        reads=[Rdg[dgi], Rones], writes=[RPS[bank]])
                S.op("dve", lambda ci=ci, bank0=bank0: nc.vector.tensor_copy(
                    out=GB[:, ci, :], in_=PS[:, bank0:bank0 + 2, :].rearrange("p b n -> p (b n)")),
                    reads=[RPS[bank0], RPS[bank0 + 1]], writes=[Rgb[ci]])

        ones32 = sb("ones32", [128, 128])
        Rones = Reg("ones")
        S.op("dve", lambda: nc.vector.memset(ones32[:], 1.0), writes=[Rones])

        cur = {"XN": XN, "TMP": TMP, "HT": None, "BR": None}

        def prenorm_p1(t):
            XN = cur["XN"]
            smi = rot["sm"] % 4
            rot["sm"] += 1
            xi = rot["xn"] % 2
            rot["xn"] += 1
            sm = small[:, smi, :]
            S.op("act", lambda: nc.scalar.activation(out=junk[:], in_=X[:, t, :], func=AF.Square, accum_out=sm[:, 0:1]),
                 reads=[RX[t]], writes=[Rjunk, Rsmall[smi]])
            S.op("act", lambda: nc.scalar.activation(out=sm[:, 1:2], in_=sm[:, 0:1], func=AF.Sqrt, scale=1.0 / D, bias=epsb[:, 0:1]),
                 reads=[Rsmall[smi], Reps], writes=[Rsmall[smi]])
            S.op("dve", lambda: nc.vector.reciprocal(out=sm[:, 2:3], in_=sm[:, 1:2]), reads=[Rsmall[smi]], writes=[Rsmall[smi]])
            S.op("dve", lambda: nc.vector.tensor_scalar(out=XN[:, xi, :], in0=X[:, t, :], scalar1=sm[:, 2:3], scalar2=None, op0=ALU.mult),
                 reads=[RX[t], Rsmall[smi]], writes=[Rxn[xi]])
            return xi

        def prenorm_p2(xi, ci, s, dst, dst_reg, banks):
            XN = cur["XN"]
            b0, b1 = banks

            def emit():
                inst = None
                for c in range(8):
                    bk = b0 if c < 4 else b1
                    inst = nc.tensor.transpose(out=PS[:, bk, (c % 4) * 128:(c % 4 + 1) * 128], in_=XN[:, xi, c * 128:(c + 1) * 128], identity=ident[:])
                return inst
            S.op("pe", emit, reads=[Rxn[xi], Rid], writes=[RPS[b0], RPS[b1]])
            for c in range(8):
                bk = b0 if c < 4 else b1
                S.op("act", lambda c=c, bk=bk: nc.scalar.activation(
                    out=dst[:, c, :], in_=PS[:, bk, (c % 4) * 128:(c % 4 + 1) * 128], func=AF.Identity,
                    scale=SV[:, ci, s, c:c + 1], bias=VEC[:, ci, 3 * s * 8 + c:3 * s * 8 + c + 1]),
                    reads=[RPS[bk], Rsv, Rvec], writes=[dst_reg])

        def prenorm_tile(t, ci, s, dst, dst_reg, banks):
            prenorm_p2(prenorm_p1(t), ci, s, dst, dst_reg, banks)

        epsb = sb("epsb", [128, 1])
        halfpi = sb("halfpi", [128, 1])
        Reps = Reg("eps")
        S.op("dve", lambda: nc.vector.memset(epsb[:], EPS), writes=[Reps])
        S.op("dve", lambda: nc.vector.memset(halfpi[:], float(np.pi / 2)), writes=[Reps])

        Rtmp = [Reg("TMP0"), Reg("TMP1")]

        def postnorm_tile(t, ci, b0):
            TMP = cur["TMP"]
            smi = rot["sm"] % 4
            rot["sm"] += 1
            ti = rot["xn"] % 2
            rot["xn"] += 1
            sm = small[:, smi, :]
            fin = PS[:, b0:b0 + 2, :].rearrange("p b n -> p (b n)")
            S.op("act", lambda: nc.scalar.activation(out=junk[:], in_=fin, func=AF.Square, accum_out=sm[:, 0:1]),
                 reads=[RPS[b0], RPS[b0 + 1]], writes=[Rjunk, Rsmall[smi]])
            S.op("act", lambda: nc.scalar.activation(out=sm[:, 1:2], in_=sm[:, 0:1], func=AF.Sqrt, scale=1.0 / D, bias=epsb[:, 0:1]),
                 reads=[Rsmall[smi], Reps], writes=[Rsmall[smi]])
            S.op("dve", lambda: nc.vector.reciprocal(out=sm[:, 2:3], in_=sm[:, 1:2]), reads=[Rsmall[smi]], writes=[Rsmall[smi]])
            S.op("dve", lambda: nc.vector.scalar_tensor_tensor(out=TMP[:, ti, :], in0=fin, scalar=sm[:, 2:3], in1=GB[:, ci, :],
                                                               op0=ALU.mult, op1=ALU.mult),
                 reads=[RPS[b0], RPS[b0 + 1], Rsmall[smi], Rgb[ci]], writes=[Rtmp[ti]])
            S.op("dve", lambda: nc.vector.tensor_tensor(out=X[:, t, :], in0=X[:, t, :], in1=TMP[:, ti, :], op=ALU.add),
                 reads=[RX[t], Rtmp[ti]], writes=[RX[t]])

        Rhtb = [Reg("HTB0"), Reg("HTB1")]
        Rgt = [Reg("GT%d" % j) for j in range(NFF)]
        Rw13 = [Reg("W13_%d" % i) for i in range(3)]
        Rw2 = [Reg("W2_%d" % i) for i in range(3)]
        Rsil = [Reg("SIL0"), Reg("SIL1")]
        cnts = {"w13": 0, "w2": 0, "htb": 0, "sil": 0, "pa": 0}

        def ffn(l, f, s):
            make_gate_bcast(s)
            w1v = ffn_w1[l, f].rearrange("(kc p) n -> p kc n", p=128)
            w3v = ffn_w3[l, f].rearrange("(kc p) n -> p kc n", p=128)
            w2v = ffn_w2[l, f].rearrange("(j p) n -> p j n", p=128)
            hb0 = cnts["htb"]
            cnts["htb"] += 5

            def prenorm_block(blk_):
                hb_ = (hb0 + blk_) % 2
                ci_ = 0 if blk_ < 4 else 1
                for tt in range(4):
                    prenorm_tile(blk_ * 4 + tt, ci_, s, HTB[:, hb_, :, tt * 128:(tt + 1) * 128], Rhtb[hb_], (4 + 2 * (tt % 2), 5 + 2 * (tt % 2)))
            prenorm_block(0)
            pend = [None] * 4
            for blk in range(5):
                ci = 0 if blk < 4 else 1
                hb = (hb0 + blk) % 2
                for j2 in range(NFF // 2):
                    if blk + 1 < 5:
                        nb_, hbn, cin = blk + 1, (hb0 + blk + 1) % 2, (0 if blk + 1 < 4 else 1)
                        if 4 <= j2 <= 7:
                            tt_ = j2 - 4
                            prenorm_p2(pend[tt_], cin, s, HTB[:, hbn, :, tt_ * 128:(tt_ + 1) * 128], Rhtb[hbn], (4 + 2 * (tt_ % 2), 5 + 2 * (tt_ % 2)))
                        if 2 <= j2 <= 5:
                            pend[j2 - 2] = prenorm_p1(nb_ * 4 + (j2 - 2))
                    sl = cnts["w13"] % 3
                    cnts["w13"] += 1
                    S.dma("pool", W13[:, sl, 0, :, :], w1v[:, :, j2 * 256:(j2 + 1) * 256], writes=[Rw13[sl]])
                    S.dma("pool", W13[:, sl, 1, :, :], w3v[:, :, j2 * 256:(j2 + 1) * 256], writes=[Rw13[sl]])
                    for jj in range(2):
                        j = 2 * j2 + jj
                        pa = cnts["pa"] % 2
                        cnts["pa"] += 1
                        b1, b3 = 2 * pa, 2 * pa + 1
                        for (m, bk) in ((0, b1), (1, b3)):
                            def emit(m=m, bk=bk, jj=jj, sl=sl):
                                inst = None
                                for kc in range(8):
                                    inst = nc.tensor.matmul(PS[:, bk, :], lhsT=W13[:, sl, m, kc, jj * 128:(jj + 1) * 128],
                                                            rhs=HTB[:, hb, kc, :], start=(kc == 0), stop=(kc == 7))
                                return inst
                            S.op("pe", emit, reads=[Rw13[sl], Rhtb[hb]], writes=[RPS[bk]])
                        si = cnts["sil"] % 2
                        cnts["sil"] += 1
                        S.op("act", lambda b1=b1, si=si: nc.scalar.activation(out=SIL[:, si, :], in_=PS[:, b1, :], func=AF.Silu),
                             reads=[RPS[b1]], writes=[Rsil[si]])
                        S.op("dve", lambda b3=b3, si=si, j=j: nc.vector.tensor_tensor(out=GT[:, j, :], in0=PS[:, b3, :], in1=SIL[:, si, :], op=ALU.mult),
                             reads=[RPS[b3], Rsil[si]], writes=[Rgt[j]])
                for j2 in range(NFF // 2):
                    sl = cnts["w2"] % 3
                    cnts["w2"] += 1
                    S.dma("pool", W2[:, sl, :, :], w2v[:, 2 * j2:2 * j2 + 2, :], writes=[Rw2[sl]])
                    for jj in range(2):
                        j = 2 * j2 + jj

                        def emit(j=j, jj=jj, sl=sl):
                            inst = None
                            for tt in range(4):
                                for half in range(2):
                                    inst = nc.tensor.matmul(PS[:, 2 * tt + half, :], lhsT=GT[:, j, tt * 128:(tt + 1) * 128],
                                                            rhs=W2[:, sl, jj, half * 512:(half + 1) * 512], start=(j == 0), stop=(j == NFF - 1))
                            return inst
                        S.op("pe", emit, reads=[Rgt[j], Rw2[sl]], writes=RPS)
                for tt in range(4):
                    postnorm_tile(blk * 4 + tt, ci, 2 * tt)

        MOFF = 0
        HT, MOFF = view(MOFF, [128, 8, 2048], BF16)
        BR, MOFF = view(MOFF, [128, 8, 2048], BF16)
        WIN0 = MOFF
        WIN, MOFF = view(MOFF, [128, 2, 8, 256], BF16)
        BS0 = MOFF
        Rht = Reg("HT")
        Rbr = [Reg("BR%d" % i) for i in range(8)]
        Rwin = [Reg("WIN0"), Reg("WIN1")]
        PV = sb("PV", [128, 64])
        Rpv = Reg("PV")
        BG = sb("BG", [128, 32])
        Rbg = Reg("BG")
        mc = {"win": 0, "pb": 0, "wg": 0, "wo": 0, "sg": 0}
        SEQS = [(0, 16, 0, "sample"), (16, 2, 1, "pA"), (18, 2, 1, "pB")]

        def mixer_params(l):
            load_T(PV[:, 0:6], conv_w[l].rearrange("j (c p) -> (j c) p", p=128), 6, Rpv)
            load_T(PV[:, 6:8], conv_b[l].rearrange("(c p) -> c p", p=128), 2, Rpv)
            load_T(PV[:, 8:10], ret_gn[l].rearrange("(c p) -> c p", p=128), 2, Rpv)
            load_T(PV[:, 10:12], s5_d[l].rearrange("(c p) -> c p", p=128), 2, Rpv)
            load_T(PV[:, 12:13], mla_kv_norm[l].rearrange("(c p) -> c p", p=128), 1, Rpv)
            load_T(BG[:, :], b_gate[l].rearrange("(c p) -> c p", p=128), 32, Rbg)

        def proj_fm(l, col0, ncols, NB, BW, evac, extra_reads=()):
            winv = w_in[l].rearrange("(kc p) n -> p kc n", p=128)
            sl = mc["win"] % 2
            mc["win"] += 1
            S.dma("pool", WIN[:, sl, :, 0:ncols], winv[:, :, col0:col0 + ncols], writes=[Rwin[sl]])
            for b in range(NB):
                bank = mc["pb"] % 4
                mc["pb"] += 1

                def emit(b=b, bank=bank):
                    inst = None
                    for kc in range(8):
                        inst = nc.tensor.matmul(PS[0:ncols, bank, 0:BW], lhsT=WIN[:, sl, kc, 0:ncols], rhs=cur["HT"][:, kc, b * BW:(b + 1) * BW],
                                                start=(kc == 0), stop=(kc == 7))
                    return inst
                S.op("pe", emit, reads=[Rwin[sl], Rht], writes=[RPS[bank]])
                evac(b, bank)

        def branch_conv(l, T, NB, BW):
            o = BS0
            Z, o = view(o, [128, 2056])
            CX, o = view(o, [128, 2048])
            Y, o = view(o, [128, 2048])
            CB, o = view(o, [128, 2048], BF16)
            Rz, Rcx, Ry, Rcb = Reg("Z"), Reg("CX"), Reg("Y"), Reg("CB")
            for cc in range(2):
                S.op("dve", lambda: nc.vector.memset(Z[:, 0:1], 0.0), writes=[Rz])
                S.op("dve", lambda: nc.vector.memset(Z[:, T + 1:T + 2], 0.0), writes=[Rz])
                proj_fm(l, 1280 + cc * 128, 128, NB, BW, lambda b, bank: S.op(
                    "act", lambda: nc.scalar.copy(out=CX[:, b * BW:(b + 1) * BW], in_=PS[:, bank, 0:BW]), reads=[RPS[bank]], writes=[Rcx]))
                proj_fm(l, 1792 + cc * 128, 128, NB, BW, lambda b, bank: S.op(
                    "dve", lambda: nc.vector.tensor_tensor(out=Z[:, 1 + b * BW:1 + (b + 1) * BW], in0=PS[:, bank, 0:BW], in1=CX[:, b * BW:(b + 1) * BW], op=ALU.mult),
                    reads=[RPS[bank], Rcx], writes=[Rz]))
                proj_fm(l, 1536 + cc * 128, 128, NB, BW, lambda b, bank: S.op(
                    "act", lambda: nc.scalar.copy(out=CB[:, b * BW:(b + 1) * BW], in_=PS[:, bank, 0:BW]), reads=[RPS[bank]], writes=[Rcb]))
                S.op("dve", lambda: nc.vector.tensor_scalar(out=Y[:, 0:T], in0=Z[:, 1:T + 1], scalar1=PV[:, 2 + cc:3 + cc], scalar2=PV[:, 6 + cc:7 + cc],
                                                            op0=ALU.mult, op1=ALU.add), reads=[Rz, Rpv], writes=[Ry])
                S.op("dve", lambda: nc.vector.scalar_tensor_tensor(out=Y[:, 0:T], in0=Z[:, 0:T], scalar=PV[:, 0 + cc:1 + cc], in1=Y[:, 0:T],
                                                                   op0=ALU.mult, op1=ALU.add), reads=[Rz, Rpv, Ry], writes=[Ry])
                S.op("dve", lambda: nc.vector.scalar_tensor_tensor(out=Y[:, 0:T], in0=Z[:, 2:T + 2], scalar=PV[:, 4 + cc:5 + cc], in1=Y[:, 0:T],
                                                                   op0=ALU.mult, op1=ALU.add), reads=[Rz, Rpv, Ry], writes=[Ry])
                S.op("dve", lambda: nc.vector.tensor_tensor(out=cur["BR"][:, 4 + cc, 0:T], in0=Y[:, 0:T], in1=CB[:, 0:T], op=ALU.mult),
                     reads=[Ry, Rcb], writes=[Rbr[4 + cc]])

        def gate_stage(l, t0, ntile, ci, T, NB, BW):
            o = WIN0
            WG, o = view(o, [128, 2, 8, 4, 128], BF16)
            WB, o = view(o, [128, 2, 2, 4, 128], BF16)
            MG, o = view(o, [128, 8, 512], BF16)
            SG, o = view(o, [128, 4, 512], BF16)
            WO, o = view(o, [128, 2, D], BF16)
            ACC, o = view(o, [128, 2, 512])
            cur["TMP"], o = view(o, [128, 2, D])
            Rwg = [Reg("WG0"), Reg("WG1")]
            Rmg = [Reg("MG%d" % c) for c in range(8)]
            Rsg = [Reg("SG%d" % n) for n in range(4)]
            Rwo = [Reg("WO0"), Reg("WO1")]
            Racc = [Reg("ACC0"), Reg("ACC1")]
            wgv = w_gate[l].rearrange("(kc p) (n d) -> p kc n d", p=128, n=4)
            wbv = w_branch[l].rearrange("n (kc p) d -> p kc n d", p=128)
            tpb = BW // 128
            for b in range(NB):
                for c in range(8):
                    sl = mc["wg"] % 2
                    mc["wg"] += 1
                    for n in range(4):
                        S.dma("pool", WG[:, sl, :, n, :], wgv[:, :, n, c * 128:(c + 1) * 128], writes=[Rwg[sl]])
                    for n in range(4):
                        S.dma("pool", WB[:, sl, :, n, :], wbv[:, :, n, c * 128:(c + 1) * 128], writes=[Rwg[sl]])
                    for n in range(4):
                        def emit_g(n=n, sl=sl):
                            inst = None
                            for kc in range(8):
                                inst = nc.tensor.matmul(PS[:, n, 0:BW], lhsT=WG[:, sl, kc, n, :], rhs=cur["HT"][:, kc, b * BW:(b + 1) * BW],
                                                        start=(kc == 0), stop=(kc == 7))
                            return inst
                        S.op("pe", emit_g, reads=[Rwg[sl], Rht], writes=[RPS[n]])

                        def emit_p(n=n, sl=sl):
                            inst = None
                            for kc in range(2):
                                inst = nc.tensor.matmul(PS[:, 4 + n, 0:BW], lhsT=WB[:, sl, kc, n, :], rhs=cur["BR"][:, 2 * n + kc, b * BW:(b + 1) * BW],
                                                        start=(kc == 0), stop=(kc == 1))
                            return inst
                        S.op("pe", emit_p, reads=[Rwg[sl], Rbr[2 * n], Rbr[2 * n + 1]], writes=[RPS[4 + n]])
                        S.op("act", lambda n=n: nc.scalar.activation(out=SG[:, n, 0:BW], in_=PS[:, n, 0:BW], func=AF.Sigmoid,
                                                                     bias=BG[:, n * 8 + c:n * 8 + c + 1]), reads=[RPS[n], Rbg], writes=[Rsg[n]])
                    S.op("dve", lambda: nc.vector.tensor_tensor(out=ACC[:, 0, 0:BW], in0=PS[:, 4, 0:BW], in1=SG[:, 0, 0:BW], op=ALU.mult),
                         reads=[RPS[4], Rsg[0]], writes=[Racc[0]])
                    for n in range(1, 4):
                        S.op("dve", lambda n=n: nc.vector.tensor_tensor(out=ACC[:, 1, 0:BW], in0=PS[:, 4 + n, 0:BW], in1=SG[:, n, 0:BW], op=ALU.mult),
                             reads=[RPS[4 + n], Rsg[n]], writes=[Racc[1]])
                        if n < 3:
                            S.op("dve", lambda: nc.vector.tensor_tensor(out=ACC[:, 0, 0:BW], in0=ACC[:, 0, 0:BW], in1=ACC[:, 1, 0:BW], op=ALU.add),
                                 reads=[Racc[0], Racc[1]], writes=[Racc[0]])
                        else:
                            S.op("dve", lambda: nc.vector.tensor_tensor(out=MG[:, c, 0:BW], in0=ACC[:, 0, 0:BW], in1=ACC[:, 1, 0:BW], op=ALU.add),
                                 reads=[Racc[0], Racc[1]], writes=[Rmg[c]])
                for c in range(8):
                    sl = mc["wo"] % 2
                    mc["wo"] += 1
                    S.dma("pool", WO[:, sl, :], w_o[l, c * 128:(c + 1) * 128, :], writes=[Rwo[sl]])

                    def emit_o(c=c, sl=sl):
                        inst = None
                        for tt in range(tpb):
                            for half in range(2):
                                inst = nc.tensor.matmul(PS[:, 2 * tt + half, :], lhsT=MG[:, c, tt * 128:(tt + 1) * 128],
                                                        rhs=WO[:, sl, half * 512:(half + 1) * 512], start=(c == 0), stop=(c == 7))
                        return inst
                    S.op("pe", emit_o, reads=[Rmg[c], Rwo[sl]], writes=RPS[0:2 * tpb])
                for tt in range(tpb):
                    postnorm_tile(t0 + b * tpb + tt, ci, 2 * tt)

        onesb = sb("onesb", [128, 128], BF16)
        S.op("dve", lambda: nc.vector.memset(onesb[:], 1.0), writes=[Rones])
        ATT_SCALE = float(96 ** -0.5)

        def branch_mla(l, t0, T, NB, BW, kind):
            sample = kind == "sample"
            Skeys = T + (512 if sample else 0)
            NKT = Skeys // 128
            o = BS0
            CQ, o = view(o, [128, 2, 2048], BF16)
            CKVN, o = view(o, [128, 2560], BF16)
            KR, o = view(o, [128, 2560], BF16)
            WUQ, o = view(o, [128, 2, 384], BF16)
            WUQS, o = view(o, [128, 2, 4, 32], BF16)
            WUKV, o = view(o, [128, 512], BF16)
            WKRS, o = view(o, [128, 8, 32], BF16)
            QNV, o = view(o, [128, 2])
            oB = o
            Rcq, Rckvn, Rkr, Rw = Reg("m_CQ"), Reg("m_CKVN"), Reg("m_KR"), Reg("m_W")
            W32, oo = view(oB, [128, 2, 384])
            Rw32 = Reg("m_W32")
            S.dma("sp", W32[:, 0, :], mla_w_uq[l, 0:128, :], writes=[Rw32])
            S.dma("sp", W32[0:64, 1, :], mla_w_uq[l, 128:192, :], writes=[Rw32])
            S.dma("sp", QNV[:, 0:1], mla_q_norm[l, 0:128].unsqueeze(1), writes=[Rw])
            S.dma("sp", QNV[0:64, 1:2], mla_q_norm[l, 128:192].unsqueeze(1), writes=[Rw])
            S.dma("pool", WUKV[:, :], mla_w_ukv[l, :, :], writes=[Rw])
            for kc, np_ in ((0, 128), (1, 64)):
                S.op("dve", lambda kc=kc, np_=np_: nc.vector.tensor_scalar(out=WUQ[0:np_, kc, :], in0=W32[0:np_, kc, :], scalar1=QNV[0:np_, kc:kc + 1],
                                                                          scalar2=None, op0=ALU.mult), reads=[Rw32, Rw], writes=[Rw])
                if sample:
                    wv = WUQ[0:np_, kc, :].rearrange("p (h e) -> p h e", h=4)
                    S.op("dve", lambda wv=wv, kc=kc, np_=np_: nc.vector.tensor_scalar(out=WUQS[0:np_, kc, :, 0:16], in0=wv[:, :, 80:96], scalar1=-1.0,
                                                                                   scalar2=None, op0=ALU.mult), reads=[Rw], writes=[Rw])
                    S.op("dve", lambda wv=wv, kc=kc, np_=np_: nc.vector.tensor_copy(out=WUQS[0:np_, kc, :, 16:32], in_=wv[:, :, 64:80]), reads=[Rw], writes=[Rw])
            if sample:
                winv = w_in[l].rearrange("(kc p) n -> p kc n", p=128)
                S.dma("pool", WKRS[:, :, 0:16], winv[:, :, 2384:2400], writes=[Rw])
                S.dma("pool", WKRS[:, :, 16:32], winv[:, :, 2368:2384], writes=[Rw])
                S.op("dve", lambda: nc.vector.tensor_scalar(out=WKRS[:, :, 0:16], in0=WKRS[:, :, 0:16], scalar1=-1.0, scalar2=None, op0=ALU.mult),
                     reads=[Rw], writes=[Rw])
            SQ, oo = view(oo, [128, 2, 512], BF16)
            RST, oo = view(oo, [128, 512])
            TB, oo = view(oo, [128, 2, 512])
            T1, oo = view(oo, [128, 2, 512])
            Rsq, Rrst, Rtb, Rt1 = Reg("m_SQ"), Reg("m_RST"), Reg("m_TB"), Reg("m_T1")

            def rstd_from_ps(bank, parts):
                S.op("act", lambda: nc.scalar.activation(out=RST[:, 0:BW], in_=PS[:, bank, 0:BW], func=AF.Sqrt, scale=1.0 / parts, bias=epsb[:, 0:1]),
                     reads=[RPS[bank], Reps], writes=[Rrst])
                S.op("dve", lambda: nc.vector.reciprocal(out=RST[:, 0:BW], in_=RST[:, 0:BW]), reads=[Rrst], writes=[Rrst])

            winv = w_in[l].rearrange("(kc p) n -> p kc n", p=128)
            WQ = WIN
            S.dma("pool", WQ[:, 0, :, 0:192], winv[:, :, 2048:2240], writes=[Rwin[0]])
            S.dma("pool", WQ[:, 1, :, 0:160], winv[:, :, 2240:2400], writes=[Rwin[1]])
            for b in range(NB):
                cols = slice(b * BW, (b + 1) * BW)
                for kc2, np_, bank in ((0, 128, 0), (1, 64, 1)):
                    def emit(kc2=kc2, np_=np_, bank=bank):
                        inst = None
                        for kc in range(8):
                            inst = nc.tensor.matmul(PS[0:np_, bank, 0:BW], lhsT=WQ[:, 0, kc, kc2 * 128:kc2 * 128 + np_], rhs=cur["HT"][:, kc, cols],
                                                    start=(kc == 0), stop=(kc == 7))
                        return inst
                    S.op("pe", emit, reads=[Rwin[0], Rht], writes=[RPS[bank]])
                    S.op("act", lambda kc2=kc2, np_=np_, bank=bank: nc.scalar.activation(out=SQ[0:np_, kc2, 0:BW], in_=PS[0:np_, bank, 0:BW], func=AF.Square),
                         reads=[RPS[bank]], writes=[Rsq])

                def emit_ss():
                    nc.tensor.matmul(PS[:, 2, 0:BW], lhsT=onesb[:, :], rhs=SQ[:, 0, 0:BW], start=True, stop=False)
                    return nc.tensor.matmul(PS[:, 2, 0:BW], lhsT=onesb[0:64, :], rhs=SQ[0:64, 1, 0:BW], start=False, stop=True)
                S.op("pe", emit_ss, reads=[Rsq, Rones], writes=[RPS[2]])
                rstd_from_ps(2, 192.0)
                for kc2, np_, bank in ((0, 128, 0), (1, 64, 1)):
                    S.op("dve", lambda kc2=kc2, np_=np_, bank=bank: nc.vector.tensor_tensor(out=CQ[0:np_, kc2, cols], in0=PS[0:np_, bank, 0:BW], in1=RST[0:np_, 0:BW], op=ALU.mult),
                         reads=[RPS[bank], Rrst], writes=[Rcq])
                def emit_kv():
                    inst = None
                    for kc in range(8):
                        inst = nc.tensor.matmul(PS[:, 3, 0:BW], lhsT=WQ[:, 1, kc, 0:128], rhs=cur["HT"][:, kc, cols], start=(kc == 0), stop=(kc == 7))
                    return inst
                S.op("pe", emit_kv, reads=[Rwin[1], Rht], writes=[RPS[3]])
                S.op("act", lambda: nc.scalar.activation(out=SQ[:, 0, 0:BW], in_=PS[:, 3, 0:BW], func=AF.Square), reads=[RPS[3]], writes=[Rsq])
                S.op("pe", lambda: nc.tensor.matmul(PS[:, 2, 0:BW], lhsT=onesb[:, :], rhs=SQ[:, 0, 0:BW], start=True, stop=True), reads=[Rsq, Rones], writes=[RPS[2]])
                rstd_from_ps(2, 128.0)
                S.op("dve", lambda: nc.vector.scalar_tensor_tensor(out=CKVN[:, cols], in0=PS[:, 3, 0:BW], scalar=PV[:, 12:13], in1=RST[:, 0:BW], op0=ALU.mult, op1=ALU.mult),
                     reads=[RPS[3], Rpv, Rrst], writes=[Rckvn])
                def emit_kr():
                    inst = None
                    for kc in range(8):
                        inst = nc.tensor.matmul(PS[0:32, 4, 0:BW], lhsT=WQ[:, 1, kc, 128:160], rhs=cur["HT"][:, kc, cols], start=(kc == 0), stop=(kc == 7))
                    return inst
                S.op("pe", emit_kr, reads=[Rwin[1], Rht], writes=[RPS[4]])
                if sample:
                    def emit_krs():
                        inst = None
                        for kc in range(8):
                            inst = nc.tensor.matmul(PS[0:32, 5, 0:BW], lhsT=WKRS[:, kc, :], rhs=cur["HT"][:, kc, cols], start=(kc == 0), stop=(kc == 7))
                        return inst
                    S.op("pe", emit_krs, reads=[Rw, Rht], writes=[RPS[5]])
                    S.dma("sp", TB[0:32, 0, 0:BW], c_rope_mla[0, :, cols], writes=[Rtb])
                    S.dma("sp", TB[0:32, 1, 0:BW], c_rope_mla[1, :, cols], writes=[Rtb])
                    S.op("dve", lambda: nc.vector.tensor_tensor(out=T1[0:32, 0, 0:BW], in0=PS[0:32, 4, 0:BW], in1=TB[0:32, 0, 0:BW], op=ALU.mult),
                         reads=[RPS[4], Rtb], writes=[Rt1])
                    S.op("dve", lambda: nc.vector.tensor_tensor(out=T1[0:32, 1, 0:BW], in0=PS[0:32, 5, 0:BW], in1=TB[0:32, 1, 0:BW], op=ALU.mult),
                         reads=[RPS[5], Rtb], writes=[Rt1])
                    S.op("dve", lambda: nc.vector.tensor_tensor(out=KR[0:32, cols], in0=T1[0:32, 0, 0:BW], in1=T1[0:32, 1, 0:BW], op=ALU.add),
                         reads=[Rt1], writes=[Rkr])
                else:
                    S.op("act", lambda: nc.scalar.copy(out=KR[0:32, cols], in_=PS[0:32, 4, 0:BW]), reads=[RPS[4]], writes=[Rkr])
            if sample:
                CT, _ = view(oB + 3072, [128, 4, 160])
                Rct = Reg("m_CT")
                S.barrier()
                S.dma("sp", CT[:, :, :], ctx_mla[l].rearrange("(i p) f -> p i f", p=128), writes=[Rct])
                for i in range(4):
                    S.op("pe", lambda i=i: nc.tensor.transpose(out=PS[:, 6, 0:128], in_=CT[:, i, 0:128], identity=ident[:]), reads=[Rct, Rid], writes=[RPS[6]])
                    S.op("act", lambda i=i: nc.scalar.copy(out=CKVN[:, T + i * 128:T + (i + 1) * 128], in_=PS[:, 6, 0:128]), reads=[RPS[6]], writes=[Rckvn])
                    S.op("pe", lambda i=i: nc.tensor.transpose(out=PS[0:32, 7, 0:128], in_=CT[:, i, 128:160], identity=ident[:]), reads=[Rct, Rid], writes=[RPS[7]])
                    S.op("act", lambda i=i: nc.scalar.copy(out=KR[0:32, T + i * 128:T + (i + 1) * 128], in_=PS[0:32, 7, 0:128]), reads=[RPS[7]], writes=[Rkr])
            S.barrier()
            o = oB
            KN, o = view(o, [128, 2560], BF16)
            QN, o = view(o, [128, 2048], BF16)
            QR, o = view(o, [128, 2048], BF16)
            VA, o = view(o, [128, 20, 66], BF16)
            Rkn, Rqn, Rqr, Rva = Reg("m_KN"), Reg("m_QN"), Reg("m_QR"), Reg("m_VA")
            ow = WIN0
            TB2, ow2 = view(ow, [128, 2, 512])
            T2, ow2 = view(ow2, [128, 2, 512])
            PT, ow3 = view(ow, [128, 2, 512], BF16)
            OS, ow3 = view(ow3, [128, 512])
            OT, ow3 = view(ow3, [128, 512], BF16)
            Rtb2, Rt2, Rpt, Ros, Rot = Reg("m_TB2"), Reg("m_T2"), [Reg("m_PT0"), Reg("m_PT1")], Reg("m_OS"), Reg("m_OT")
            S.op("dve", lambda: nc.vector.memset(VA[:, :, 64:66], 1.0), writes=[Rva])
            S.dma("sp", KN[64:96, 0:Skeys], KR[0:32, 0:Skeys], reads=[Rkr], writes=[Rkn], key=Reg("m_KRd"))
            KBW = 512
            for h in range(4):
                for kb in range((Skeys + KBW - 1) // KBW):
                    w = min(KBW, Skeys - kb * KBW)
                    bank = mc["pb"] % 4
                    mc["pb"] += 1
                    S.op("pe", lambda kb=kb, w=w, bank=bank: nc.tensor.matmul(PS[0:64, bank, 0:w], lhsT=WUKV[:, h * 128:h * 128 + 64], rhs=CKVN[:, kb * KBW:kb * KBW + w],
                                                                              start=True, stop=True), reads=[Rw, Rckvn], writes=[RPS[bank]])
                    S.op("act", lambda kb=kb, w=w, bank=bank: nc.scalar.copy(out=KN[0:64, kb * KBW:kb * KBW + w], in_=PS[0:64, bank, 0:w]), reads=[RPS[bank]], writes=[Rkn])
                for kt in range(NKT):
                    bank = mc["pb"] % 4
                    mc["pb"] += 1
                    S.op("pe", lambda kt=kt, bank=bank: nc.tensor.matmul(PS[:, bank, 0:64], lhsT=CKVN[:, kt * 128:(kt + 1) * 128], rhs=WUKV[:, h * 128 + 64:h * 128 + 128],
                                                                          start=True, stop=True), reads=[Rw, Rckvn], writes=[RPS[bank]])
                    S.op("dve", lambda kt=kt, bank=bank: nc.vector.tensor_copy(out=VA[:, kt, 0:64], in_=PS[:, bank, 0:64]), reads=[RPS[bank]], writes=[Rva])
                for b in range(NB):
                    cols = slice(b * BW, (b + 1) * BW)
                    bank = mc["pb"] % 4
                    mc["pb"] += 1

                    def emit_q(c0, m, bank, wt=None, p0=0):
                        def f():
                            po = PS[p0:p0 + m, bank, 0:BW]
                            if wt is None:
                                nc.tensor.matmul(po, lhsT=WUQ[:, 0, c0:c0 + m], rhs=CQ[:, 0, cols], start=True, stop=False)
                                return nc.tensor.matmul(po, lhsT=WUQ[0:64, 1, c0:c0 + m], rhs=CQ[0:64, 1, cols], start=False, stop=True)
                            nc.tensor.matmul(po, lhsT=WUQS[:, 0, h, :], rhs=CQ[:, 0, cols], start=True, stop=False)
                            return nc.tensor.matmul(po, lhsT=WUQS[0:64, 1, h, :], rhs=CQ[0:64, 1, cols], start=False, stop=True)
                        return f
                    S.op("pe", emit_q(h * 96, 64, bank), reads=[Rw, Rcq], writes=[RPS[bank]])
                    S.op("act", lambda bank=bank: nc.scalar.copy(out=QN[0:64, cols], in_=PS[0:64, bank, 0:BW]), reads=[RPS[bank]], writes=[Rqn])
                    bank2 = mc["pb"] % 4
                    mc["pb"] += 1
                    S.op("pe", emit_q(h * 96 + 64, 32, bank2, p0=64), reads=[Rw, Rcq], writes=[RPS[bank2]])
                    if sample:
                        bank3 = mc["pb"] % 4
                        mc["pb"] += 1
                        S.op("pe", emit_q(0, 32, bank3, wt=1, p0=64), reads=[Rw, Rcq], writes=[RPS[bank3]])
                        S.dma("sp", TB2[64:96, 0, 0:BW], c_rope_mla[0, :, cols], writes=[Rtb2])
                        S.dma("sp", TB2[64:96, 1, 0:BW], c_rope_mla[1, :, cols], writes=[Rtb2])
                        S.op("dve", lambda: nc.vector.tensor_tensor(out=T2[64:96, 0, 0:BW], in0=PS[64:96, bank2, 0:BW], in1=TB2[64:96, 0, 0:BW], op=ALU.mult),
                             reads=[RPS[bank2], Rtb2], writes=[Rt2])
                        S.op("dve", lambda: nc.vector.tensor_tensor(out=T2[64:96, 1, 0:BW], in0=PS[64:96, bank3, 0:BW], in1=TB2[64:96, 1, 0:BW], op=ALU.mult),
                             reads=[RPS[bank3], Rtb2], writes=[Rt2])
                        S.op("dve", lambda: nc.vector.tensor_tensor(out=QN[64:96, cols], in0=T2[64:96, 0, 0:BW], in1=T2[64:96, 1, 0:BW], op=ALU.add),
                             reads=[Rt2], writes=[Rqn])
                    else:
                        S.op("act", lambda: nc.scalar.copy(out=QN[64:96, cols], in_=PS[64:96, bank2, 0:BW]), reads=[RPS[bank2]], writes=[Rqn])
                S.barrier()
                for b in range(NB):
                    cols = slice(b * BW, (b + 1) * BW)
                    ob = 4 + (b % 2)
                    for kt in range(NKT):
                        bank = mc["pb"] % 4
                        mc["pb"] += 1
                        pi = kt % 2

                        S.op("pe", lambda kt=kt, bank=bank: nc.tensor.matmul(PS[:, bank, 0:BW], lhsT=KN[0:96, kt * 128:(kt + 1) * 128], rhs=QN[0:96, cols], start=True, stop=True),
                             reads=[Rkn, Rqn], writes=[RPS[bank]])
                        S.op("act", lambda bank=bank, pi=pi: nc.scalar.activation(out=PT[:, pi, 0:BW], in_=PS[:, bank, 0:BW], func=AF.Exp, scale=ATT_SCALE),
                             reads=[RPS[bank]], writes=[Rpt[pi]])
                        S.op("pe", lambda kt=kt, pi=pi: nc.tensor.matmul(PS[0:65, ob, 0:BW], lhsT=VA[:, kt, 0:65], rhs=PT[:, pi, 0:BW], start=(kt == 0), stop=(kt == NKT - 1)),
                             reads=[Rva, Rpt[pi]], writes=[RPS[ob]])
                    S.op("act", lambda: nc.scalar.copy(out=OS[0:65, 0:BW], in_=PS[0:65, ob, 0:BW]), reads=[RPS[ob]], writes=[Ros])
                    S.op("dve", lambda: nc.vector.reciprocal(out=OS[64:65, 0:BW], in_=OS[64:65, 0:BW]), reads=[Ros], writes=[Ros])
                    S.op("pe", lambda: nc.tensor.matmul(PS[0:64, 6, 0:BW], lhsT=ones32[64:65, 0:64], rhs=OS[64:65, 0:BW], start=True, stop=True),
                         reads=[Ros, Rones], writes=[RPS[6]])
                    S.op("dve", lambda: nc.vector.tensor_tensor(out=OT[0:64, 0:BW], in0=PS[0:64, 6, 0:BW], in1=OS[0:64, 0:BW], op=ALU.mult),
                         reads=[RPS[6], Ros], writes=[Rot])
                    S.dma("sp", cur["BR"][(h % 2) * 64:(h % 2) * 64 + 64, 6 + h // 2, cols], OT[0:64, 0:BW], reads=[Rot], writes=[Rbr[6 + h // 2]], key=Reg("m_OTd"))
                S.barrier()

        def mla_cache_out(l, t0, ntile, pi):
            o = BS0
            CA, o = view(o, [128, 2, 160])
            KVB, o = view(o, [128, 128])
            Rca, Rkvb = [Reg("m_CA0"), Reg("m_CA1")], Reg("m_KVB")
            winv = w_in[l].rearrange("(kc p) n -> p kc n", p=128)
            S.dma("pool", WIN[:, 0, :, 0:160], winv[:, :, 2240:2400], writes=[Rwin[0]])
            S.dma("sp", KVB[:, :], mla_kv_norm[l:l + 1, :].broadcast_to([128, 128]), writes=[Rkvb])
            for tt in range(ntile):
                bank = mc["pb"] % 4
                mc["pb"] += 1
                smi = rot["sm"] % 4
                rot["sm"] += 1
                sm = small[:, smi, :]

                def emit(tt=tt, bank=bank):
                    inst = None
                    for kc in range(8):
                        inst = nc.tensor.matmul(PS[:, bank, 0:160], lhsT=cur["HT"][:, kc, tt * 128:(tt + 1) * 128], rhs=WIN[:, 0, kc, 0:160], start=(kc == 0), stop=(kc == 7))
                    return inst
                S.op("pe", emit, reads=[Rwin[0], Rht], writes=[RPS[bank]])
                S.op("act", lambda: nc.scalar.activation(out=junk[:, 0:128], in_=PS[:, bank, 0:128], func=AF.Square, accum_out=sm[:, 0:1]),
                     reads=[RPS[bank]], writes=[Rjunk, Rsmall[smi]])
                S.op("act", lambda: nc.scalar.activation(out=sm[:, 1:2], in_=sm[:, 0:1], func=AF.Sqrt, scale=1.0 / 128, bias=epsb[:, 0:1]),
                     reads=[Rsmall[smi], Reps], writes=[Rsmall[smi]])
                S.op("dve", lambda: nc.vector.reciprocal(out=sm[:, 2:3], in_=sm[:, 1:2]), reads=[Rsmall[smi]], writes=[Rsmall[smi]])
                ci_ = tt % 2
                S.op("dve", lambda: nc.vector.scalar_tensor_tensor(out=CA[:, ci_, 0:128], in0=PS[:, bank, 0:128], scalar=sm[:, 2:3], in1=KVB[:, :], op0=ALU.mult, op1=ALU.mult),
                     reads=[RPS[bank], Rsmall[smi], Rkvb], writes=[Rca[ci_]])
                S.op("dve", lambda: nc.vector.tensor_copy(out=CA[:, ci_, 128:160], in_=PS[:, bank, 128:160]), reads=[RPS[bank]], writes=[Rca[ci_]])
                OUT_EVS.append(S.dma("sp", o_mla[pi, l, tt * 128:(tt + 1) * 128, :], CA[:, ci_, :], reads=[Rca[ci_]], key=Reg("o_mla_d%d" % ci_)))

        def branch_ret(l, t0, T, kind):
            sample = kind == "sample"
            n = T // 128
            o = BS0
            WRb, o = view(o, [128, 8, 512], BF16)
            SBst, o = view(o, [128, 16, 256], BF16)
            DM, o = view(o, [128, 4, 128], BF16)
            QD, o = view(o, [128, 2, 4, 128], BF16)
            CD, o = view(o, [128, 2, 256])
            LG, o = view(o, [128, 8])
            KD, o = view(o, [128, 2, 4])
            SF, o = view(o, [128, 256])
            SB, o = view(o, [128, 256])
            SFb, o = view(o, [128, 256], BF16)
            oT = o
            WRa, _ = view(WIN0, [128, 8, 512], BF16)
            Rwr, Rsbst, Rtab, Rsf, Rsb, Rsfb = Reg("r_WR"), Reg("r_SBst"), Reg("r_TAB"), Reg("r_SF"), Reg("r_SB"), Reg("r_SFb")
            winv = w_in[l].rearrange("(kc p) n -> p kc n", p=128)
            S.dma("pool", WRa[:, :, :], winv[:, :, 256:768], writes=[Rwr])
            S.dma("pool", WRb[:, :, :], winv[:, :, 768:1280], writes=[Rwr])
            CT6, o2 = view(oT, [128, 6, 128])
            E1, o2 = view(o2, [128, 2, 128])
            PIDX, o2 = view(o2, [128, 2])
            C128, o2 = view(o2, [128, 64])
            Rc6, Re1 = Reg("r_C6"), Reg("r_E1")
            S.dma("sp", CT6[:, :, :], c_ret.rearrange("k p i -> p k i"), writes=[Rc6])
            S.dma("sp", PIDX[:, :], c_pidx[:, :], writes=[Rc6])
            S.dma("sp", LG[:, :], ret_decay[l:l + 1].rearrange("o d h -> o (d h)").broadcast_to([128, 8]), writes=[Rtab])
            S.op("dve", lambda: nc.vector.memset(C128[:, :], 128.0), writes=[Rc6])
            S.op("act", lambda: nc.scalar.activation(out=LG[:, :], in_=LG[:, :], func=AF.Sigmoid), reads=[Rtab], writes=[Rtab])
            S.op("act", lambda: nc.scalar.activation(out=LG[:, :], in_=LG[:, :], func=AF.Ln), reads=[Rtab], writes=[Rtab])
            for h in range(4):
                S.op("act", lambda h=h: nc.scalar.activation(out=E1[:, 0, :], in_=CT6[:, 0, :], func=AF.Exp, scale=LG[:, h:h + 1]), reads=[Rc6, Rtab], writes=[Re1])
                S.op("act", lambda h=h: nc.scalar.activation(out=E1[:, 1, :], in_=CT6[:, 1, :], func=AF.Exp, scale=LG[:, 4 + h:5 + h]), reads=[Rc6, Rtab], writes=[Re1])
                S.op("dve", lambda h=h: nc.vector.tensor_tensor(out=E1[:, :, :], in0=E1[:, :, :], in1=CT6[:, 2:4, :], op=ALU.mult), reads=[Re1, Rc6], writes=[Re1])
                S.op("dve", lambda h=h: nc.vector.tensor_tensor(out=DM[:, h, :], in0=E1[:, 0, :], in1=E1[:, 1, :], op=ALU.add), reads=[Re1], writes=[Rtab])
                for d in range(2):
                    S.op("act", lambda h=h, d=d: nc.scalar.activation(out=QD[:, d, h, :], in_=CT6[:, 4 + d, :], func=AF.Exp, scale=LG[:, d * 4 + h:d * 4 + h + 1]),
                         reads=[Rc6, Rtab], writes=[Rtab])
                    S.op("act", lambda h=h, d=d: nc.scalar.activation(out=CD[:, d, h * 64:(h + 1) * 64], in_=C128[:, :], func=AF.Exp, scale=LG[:, d * 4 + h:d * 4 + h + 1]),
                         reads=[Rc6, Rtab], writes=[Rtab])
                    S.op("act", lambda h=h, d=d: nc.scalar.activation(out=KD[:, d, h:h + 1], in_=PIDX[:, d:d + 1], func=AF.Exp, scale=LG[:, d * 4 + h:d * 4 + h + 1]),
                         reads=[Rc6, Rtab], writes=[Rtab])
            S.op("dve", lambda: nc.vector.tensor_scalar(out=KD[:, :, :], in0=KD[:, :, :], scalar1=0.125, scalar2=None, op0=ALU.mult), reads=[Rtab], writes=[Rtab])
            if sample:
                S.dma("sp", SF[0:64, :].rearrange("d (h e) -> d h e", h=4), st_ret[l, 0].rearrange("h d e -> d h e"), writes=[Rsf])
                S.dma("sp", SB[0:64, :].rearrange("d (h e) -> d h e", h=4), st_ret[l, 1].rearrange("h d e -> d h e"), writes=[Rsb])
            else:
                S.op("dve", lambda: nc.vector.memset(SF[0:64, :], 0.0), writes=[Rsf])
                S.op("dve", lambda: nc.vector.memset(SB[0:64, :], 0.0), writes=[Rsb])
            S.barrier()
            if cfg.get("ret_stop") == "tables":
                return
            o3 = oT
            QK, o3 = view(o3, [128, 512], BF16)
            TA, o3 = view(o3, [128, 256])
            TBt, o3 = view(o3, [128, 256])
            RT, o3 = view(o3, [128, 2, 32])
            KDt, o3 = view(o3, [128, 256], BF16)
            VTc, o3 = view(o3, [128, 256], BF16)
            SRG, o3 = view(o3, [128, 256], BF16)
            QT, o3 = view(o3, [128, 3, 512], BF16)
            KT, o3 = view(o3, [128, 512], BF16)
            AM, o3 = view(o3, [128, 512], BF16)
            CEN, o3 = view(o3, [128, 256])
            SQr, o3 = view(o3, [128, 256])
            NRo, o3 = view(o3, [128, 256], BF16)
            MS, o3 = view(o3, [128, 8])
            Rqk, Rta, Rrt, Rkd, Rvt, Rsrg, Rqt, Rkt, Ram, Rcen, Rsq, Rnro, Rms = (Reg("r_" + x) for x in
                ("QK", "TA", "RT", "KDt", "VTc", "SRG", "QT", "KT", "AM", "CEN", "SQ", "NRo", "MS"))
            PSb = lambda bank: PS[:, bank, :].bitcast(BF16)

            def proj(c, bank, WRx, c0, ncol):
                def emit():
                    inst = None
                    for kc in range(8):
                        inst = nc.tensor.matmul(PS[:, bank, 0:ncol], lhsT=cur["HT"][:, kc, c * 128:(c + 1) * 128], rhs=WRx[:, kc, c0:c0 + ncol], start=(kc == 0), stop=(kc == 7))
                    return inst
                S.op("pe", emit, reads=[Rwr, Rht], writes=[RPS[bank]])

            def rope(c, bank, col0, ng, dst):
                src = PS[:, bank, col0:col0 + ng * 64].rearrange("p (g t e) -> p g t e", g=ng, t=2)
                dv = dst.rearrange("p (g t e) -> p g t e", g=ng, t=2)
                if not sample:
                    S.op("act", lambda: nc.scalar.copy(out=dst, in_=PS[:, bank, col0:col0 + ng * 64]), reads=[RPS[bank]], writes=[Rqk])
                    return
                S.dma("sp", RT[:, 0, :], c_rope_ret[0, c * 128:(c + 1) * 128, :], writes=[Rrt])
                S.dma("sp", RT[:, 1, :], c_rope_ret[1, c * 128:(c + 1) * 128, :], writes=[Rrt])
                cosb = RT[:, 0, :].unsqueeze(1).broadcast_to([128, ng, 32])
                sinb = RT[:, 1, :].unsqueeze(1).broadcast_to([128, ng, 32])
                ta = TA[:, 0:ng * 32].rearrange("p (g e) -> p g e", g=ng)
                tb = TBt[:, 0:ng * 32].rearrange("p (g e) -> p g e", g=ng)
                S.op("dve", lambda: nc.vector.tensor_tensor(out=ta, in0=src[:, :, 0, :], in1=cosb, op=ALU.mult), reads=[RPS[bank], Rrt], writes=[Rta])
                S.op("dve", lambda: nc.vector.tensor_tensor(out=tb, in0=src[:, :, 1, :], in1=sinb, op=ALU.mult), reads=[RPS[bank], Rrt], writes=[Rta])
                S.op("dve", lambda: nc.vector.tensor_tensor(out=dv[:, :, 0, :], in0=ta, in1=tb, op=ALU.subtract), reads=[Rta], writes=[Rqk])
                S.op("dve", lambda: nc.vector.tensor_tensor(out=ta, in0=src[:, :, 0, :], in1=sinb, op=ALU.mult), reads=[RPS[bank], Rrt], writes=[Rta])
                S.op("dve", lambda: nc.vector.tensor_tensor(out=tb, in0=src[:, :, 1, :], in1=cosb, op=ALU.mult), reads=[RPS[bank], Rrt], writes=[Rta])
                S.op("dve", lambda: nc.vector.tensor_tensor(out=dv[:, :, 1, :], in0=ta, in1=tb, op=ALU.add), reads=[Rta], writes=[Rqk])

            def kdec_mul(d, ksrc):
                S.op("dve", lambda: nc.vector.tensor_tensor(out=KDt[:, :].rearrange("p (h e) -> p h e", h=4), in0=ksrc.rearrange("p (h e) -> p h e", h=4),
                                                            in1=KD[:, d, :].unsqueeze(2).broadcast_to([128, 4, 64]), op=ALU.mult), reads=[Rqk, Rtab], writes=[Rkd])

            def umat(bank):
                def emit():
                    inst = None
                    for h in range(4):
                        inst = nc.tensor.matmul(PS[0:64, bank, h * 64:(h + 1) * 64], lhsT=KDt[:, h * 64:(h + 1) * 64], rhs=VTc[:, h * 64:(h + 1) * 64], start=True, stop=True)
                    return inst
                S.op("pe", emit, reads=[Rkd, Rvt], writes=[RPS[bank]])

            def state_update(St, Rst, d, bank):
                S.op("dve", lambda: nc.vector.tensor_tensor(out=St[0:64, :], in0=St[0:64, :], in1=CD[0:64, d, :], op=ALU.mult), reads=[Rst, Rtab], writes=[Rst])
                S.op("dve", lambda: nc.vector.tensor_tensor(out=St[0:64, :], in0=St[0:64, :], in1=PS[0:64, bank, 0:256], op=ALU.add), reads=[Rst, RPS[bank]], writes=[Rst])

            for c in range(n - 1, -1, -1):
                proj(c, 0, WRa, 256, 256)
                proj(c, 1, WRb, 0, 256)
                rope(c, 0, 0, 4, QK[:, 0:256])
                S.op("act", lambda: nc.scalar.copy(out=VTc[:, :], in_=PS[:, 1, 0:256]), reads=[RPS[1]], writes=[Rvt])
                kdec_mul(1, QK[:, 0:256])
                umat(5)
                S.op("act", lambda c=c: nc.scalar.copy(out=SBst[0:64, c, :], in_=SB[0:64, :]), reads=[Rsb], writes=[Rsbst])
                state_update(SB, Rsb, 1, 5)
            S.op("act", lambda: nc.scalar.copy(out=SFb[0:64, :], in_=SF[0:64, :]), reads=[Rsf], writes=[Rsfb])
            if cfg.get("ret_stop") == "pass1":
                return
            for c in range(n):
                proj(c, 0, WRa, 0, 512)
                proj(c, 1, WRb, 0, 512)
                rope(c, 0, 0, 8, QK[:, :])
                S.op("act", lambda: nc.scalar.copy(out=VTc[:, :], in_=PS[:, 1, 0:256]), reads=[RPS[1]], writes=[Rvt])
                S.op("act", lambda: nc.scalar.activation(out=SRG[:, :], in_=PS[:, 1, 256:512], func=AF.Silu), reads=[RPS[1]], writes=[Rsrg])
                kdec_mul(0, QK[:, 256:512])

                def emit_t():
                    inst = None
                    for g in range(8):
                        inst = nc.tensor.transpose(out=PSb(2)[0:64, g * 128:(g + 1) * 128], in_=QK[:, g * 64:(g + 1) * 64], identity=identb[:])
                    return inst
                S.op("pe", emit_t, reads=[Rqk, Rid], writes=[RPS[2]])
                S.op("dve", lambda: nc.vector.tensor_copy(out=QT[0:64, 0, :], in_=PSb(2)[0:64, 0:512]), reads=[RPS[2]], writes=[Rqt])
                for d in range(2):
                    S.op("dve", lambda d=d: nc.vector.tensor_tensor(out=QT[0:64, 1 + d, :], in0=PSb(2)[0:64, 0:512], in1=QD[0:64, d, :, :].rearrange("p h i -> p (h i)"), op=ALU.mult),
                         reads=[RPS[2], Rtab], writes=[Rqt])
                S.op("dve", lambda: nc.vector.tensor_copy(out=KT[0:64, :], in_=PSb(2)[0:64, 512:1024]), reads=[RPS[2]], writes=[Rkt])
                if cfg.get("ret_stop") == "p2a":
                    continue

                def emit_a():
                    inst = None
                    for h in range(4):
                        inst = nc.tensor.matmul(PS[:, 3, h * 128:(h + 1) * 128], lhsT=KT[0:64, h * 128:(h + 1) * 128], rhs=QT[0:64, 0, h * 128:(h + 1) * 128], start=True, stop=True)
                    return inst
                S.op("pe", emit_a, reads=[Rkt, Rqt], writes=[RPS[3]])
                S.op("dve", lambda: nc.vector.tensor_tensor(out=AM[:, :], in0=PS[:, 3, :], in1=DM[:, :, :].rearrange("p h i -> p (h i)"), op=ALU.mult),
                     reads=[RPS[3], Rtab], writes=[Ram])
                if cfg.get("ret_stop") == "p2b":
                    continue

                def emit_o(c=c):
                    inst = None
                    for h in range(4):
                        oc = PS[:, 4, h * 64:(h + 1) * 64]
                        nc.tensor.matmul(oc, lhsT=AM[:, h * 128:(h + 1) * 128], rhs=VTc[:, h * 64:(h + 1) * 64], start=True, stop=False)
                        nc.tensor.matmul(oc, lhsT=QT[0:64, 1, h * 128:(h + 1) * 128], rhs=SFb[0:64, h * 64:(h + 1) * 64], start=False, stop=False)
                        inst = nc.tensor.matmul(oc, lhsT=QT[0:64, 2, h * 128:(h + 1) * 128], rhs=SBst[0:64, c, h * 64:(h + 1) * 64], start=False, stop=True)
                    return inst
                S.op("pe", emit_o, reads=[Ram, Rvt, Rqt, Rsfb, Rsbst], writes=[RPS[4]])
                umat(5)
                state_update(SF, Rsf, 0, 5)
                S.op("act", lambda: nc.scalar.copy(out=SFb[0:64, :], in_=SF[0:64, :]), reads=[Rsf], writes=[Rsfb])
                if cfg.get("ret_stop") == "p2c":
                    continue
                ov = PS[:, 4, 0:256].rearrange("p (h e) -> p h e", h=4)
                S.op("dve", lambda: nc.vector.tensor_reduce(out=MS[:, 0:4], in_=ov, axis=AX.X, op=ALU.add), reads=[RPS[4]], writes=[Rms])
                S.op("dve", lambda: nc.vector.tensor_scalar(out=MS[:, 0:4], in0=MS[:, 0:4], scalar1=-1.0 / 64, scalar2=None, op0=ALU.mult), reads=[Rms], writes=[Rms])
                cv = CEN[:, :].rearrange("p (h e) -> p h e", h=4)
                S.op("dve", lambda: nc.vector.tensor_tensor(out=cv, in0=ov, in1=MS[:, 0:4].unsqueeze(2).broadcast_to([128, 4, 64]), op=ALU.add),
                     reads=[RPS[4], Rms], writes=[Rcen])
                S.op("dve", lambda: nc.vector.tensor_tensor(out=SQr[:, :], in0=CEN[:, :], in1=CEN[:, :], op=ALU.mult), reads=[Rcen], writes=[Rsq])
                S.op("dve", lambda: nc.vector.tensor_reduce(out=MS[:, 4:8], in_=SQr[:, :].rearrange("p (h e) -> p h e", h=4), axis=AX.X, op=ALU.add), reads=[Rsq], writes=[Rms])
                S.op("act", lambda: nc.scalar.activation(out=MS[:, 4:8], in_=MS[:, 4:8], func=AF.Sqrt, scale=1.0 / 64, bias=epsb[:, 0:1]), reads=[Rms, Reps], writes=[Rms])
                S.op("dve", lambda: nc.vector.reciprocal(out=MS[:, 4:8], in_=MS[:, 4:8]), reads=[Rms], writes=[Rms])
                S.op("dve", lambda: nc.vector.tensor_tensor(out=cv, in0=cv, in1=MS[:, 4:8].unsqueeze(2).broadcast_to([128, 4, 64]), op=ALU.mult), reads=[Rcen, Rms], writes=[Rcen])
                S.op("dve", lambda: nc.vector.tensor_tensor(out=NRo[:, :], in0=CEN[:, :], in1=SRG[:, :], op=ALU.mult), reads=[Rcen, Rsrg], writes=[Rnro])

                if cfg.get("ret_stop") == "p2d":
                    continue

                def emit_t2():
                    inst = None
                    for cc in range(2):
                        inst = nc.tensor.transpose(out=PSb(6)[:, cc * 128:(cc + 1) * 128], in_=NRo[:, cc * 128:(cc + 1) * 128], identity=identb[:])
                    return inst
                S.op("pe", emit_t2, reads=[Rnro, Rid], writes=[RPS[6]])
                for cc in range(2):
                    S.op("dve", lambda cc=cc, c=c: nc.vector.tensor_scalar(out=cur["BR"][:, 2 + cc, c * 128:(c + 1) * 128], in0=PSb(6)[:, cc * 128:(cc + 1) * 128],
                                                                          scalar1=PV[:, 8 + cc:9 + cc], scalar2=None, op0=ALU.mult), reads=[RPS[6], Rpv], writes=[Rbr[2 + cc]])
            if not sample:
                pi = 0 if kind == "pA" else 1
                OUT_EVS.append(S.dma("sp", o_ret[pi, l, 0].rearrange("h d e -> d h e"), SF[0:64, :].rearrange("d (h e) -> d h e", h=4), reads=[Rsf], key=Reg("o_ret_d")))
                OUT_EVS.append(S.dma("sp", o_ret[pi, l, 1].rearrange("h d e -> d h e"), SB[0:64, :].rearrange("d (h e) -> d h e", h=4), reads=[Rsb], key=Reg("o_ret_d")))

        def branch_s5(l, t0, T, NB, BW, kind, multi=False):
            sample = kind == "sample"
            o = BS0
            UT, o = view(o, [128, 2, 2048], BF16)
            YS, o = view(o, [128, 2, 2048], BF16)
            YFp, o = view(o, [128, 2048], BF16)
            oZ = o
            TRI, o = view(o, [128, 2, 512])
            TRIb, o = view(o, [128, 2, 512], BF16)
            BZ, o = view(o, [128, 2, 512], BF16)
            oTT = o
            TT, o = view(o, [128, 2, 1024], BF16)
            TTf, _ = view(oTT, [128, 2, 512])
            oS = o
            ow = WIN0
            SBb, ow = view(ow, [128, 2, 512], BF16)
            OTs, ow = view(ow, [128, 512], BF16)
            BW_, ow = view(ow, [128, 2, 2, 128], BF16)
            BBR, ow = view(ow, [128, 16, 16])
            BBI, ow = view(ow, [128, 16, 16])
            CW, ow = view(ow, [128, 8, 2, 32], BF16)
            WGL, ow = view(ow, [128, 2, 256], BF16)
            YV, _ = view(WIN0, [128, 512])
            Rut, Rys, Rsfs, Rtri, Rbz, Rtt, Rsbb, Rots, Rrb, Rbw = (Reg("s_" + x) for x in ("UT", "YS", "YFp", "TRI", "BZ", "TT", "SBb", "OTs", "RB", "BW"))
            def sm_(shape, dt=F32):
                nonlocal o
                v, o = view(o, shape, dt)
                return v
            o1 = [oTT]

            def ot_(shape, dt=F32):
                v, o1[0] = view(o1[0], shape, dt)
                return v
            LRE, LIM, LDT, AR, AI, FR, FI = (ot_([128, 16]) for _ in range(7))
            BRE, BIM = ot_([128, 16, 16]), ot_([128, 16, 16])
            CNAT = ot_([128, 2, 64])
            MAG, UR, UI, W1, W2_, W3 = (sm_([128, 16]) for _ in range(6))
            UBR, UBI = sm_([128, 16]), sm_([128, 16])
            TA_, TBs = sm_([128, 2, 16, 16]), sm_([128, 2, 16, 32])
            PW = sm_([128, 2, 16])
            S0t = sm_([128, 16, 2])
            INI = sm_([128, 2, 2])
            FIN = sm_([128, 2, 16, 2])
            WP = sm_([128, 128])
            Rsu = Reg("s_setup")
            Rini, Rfin, Rwp, Rcn = Reg("s_INI"), Reg("s_FIN"), Reg("s_WP"), Reg("s_CN")
            Rbu = Reg("s_BU")
            Rt4 = [Reg("s_T0"), Reg("s_T1"), Reg("s_P0"), Reg("s_P1")]
            Rbz2 = [Reg("s_BZ0"), Reg("s_BZ1")]
            Rw12 = [Reg("s_W1"), Reg("s_W2")]
            V = nc.vector
            dbgon = cfg.get("s5dbg") == kind
            if dbgon:
                dbg2 = nc.dram_tensor("dbg2", [128, 4096], F32, kind="ExternalOutput").ap()

            def dbg(ap, c0, n, regs):
                if dbgon:
                    S.dma("sp", dbg2[:, c0:c0 + n], ap, reads=regs, key=Reg("dbg2"))

            def dv(fn, reads, writes):
                S.op("dve", fn, reads=reads, writes=writes)

            def tt(out, a, b, op, reads=(Rsu,), writes=(Rsu,)):
                dv(lambda: V.tensor_tensor(out=out, in0=a, in1=b, op=op), list(reads), list(writes))

            def cmul(orr, oi, ar, ai, br, bi, t1, t2, reads=(Rsu,), writes=(Rsu,)):
                tt(t1, ar, br, ALU.mult, reads, writes)
                tt(t2, ai, bi, ALU.mult, reads, writes)
                tt(t2, t1, t2, ALU.subtract, reads, writes)
                tt(t1, ar, bi, ALU.mult, reads, writes)
                tt(oi, ai, br, ALU.mult, reads, writes)
                tt(oi, t1, oi, ALU.add, reads, writes)
                tt(orr, t2, t2, ALU.max, reads, writes)

            for cc in range(2):
                proj_fm(l, cc * 128, 128, NB, BW, lambda b, bank, cc=cc: S.op(
                    "act", lambda: nc.scalar.copy(out=UT[:, cc, b * BW:(b + 1) * BW], in_=PS[:, bank, 0:BW]), reads=[RPS[bank]], writes=[Rut]))
            S.barrier()
            for d in range(2):
                for dst, src in ((LRE, s5_lam_re), (LIM, s5_lam_im)):
                    S.dma("sp", dst[:, d::2], src[l, d].rearrange("(m g) p -> (g p) m", g=2), writes=[Rsu], slow=True)
                for g2 in range(2):
                    S.dma("sp", LDT[g2 * 64:(g2 + 1) * 64, d::2], s5_log_dt[l, d:d + 1, g2::2].broadcast_to([64, 8]), writes=[Rsu], slow=True)
                for dst, src in ((BRE, s5_b_re), (BIM, s5_b_im)):
                    S.dma("sp", dst[:, d::2, :], src[l, d].rearrange("(m g) p h -> (g p) m h", g=2), writes=[Rsu])
                if sample:
                    S.dma("sp", S0t[:, d::2, :], st_s5[l, d].rearrange("(m g) p r -> (g p) m r", g=2), writes=[Rsu], slow=True)
            S.dma("pool", WGL[:, :, :], s5_w_glu[l].rearrange("(kc p) n -> p kc n", p=128), writes=[Rsu])
            S.op("act", lambda: nc.scalar.activation(out=LDT[:, :], in_=LDT[:, :], func=AF.Exp), reads=[Rsu], writes=[Rsu])
            tt(W1[:, :], LRE[:, :], LDT[:, :], ALU.mult)
            S.op("act", lambda: nc.scalar.activation(out=MAG[:, :], in_=W1[:, :], func=AF.Exp), reads=[Rsu], writes=[Rsu])
            tt(W1[:, :], LIM[:, :], LDT[:, :], ALU.mult)
            S.op("act", lambda: nc.scalar.activation(out=UI[:, :], in_=W1[:, :], func=AF.Sin, scale=1.0 / 64), reads=[Rsu], writes=[Rsu])
            S.op("act", lambda: nc.scalar.activation(out=UR[:, :], in_=W1[:, :], func=AF.Sin, scale=1.0 / 64, bias=halfpi[:, 0:1]), reads=[Rsu, Reps], writes=[Rsu])
            for _ in range(6):
                tt(W1[:, :], UR[:, :], UR[:, :], ALU.mult)
                tt(W2_[:, :], UI[:, :], UI[:, :], ALU.mult)
                tt(W3[:, :], UR[:, :], UI[:, :], ALU.mult)
                tt(UR[:, :], W1[:, :], W2_[:, :], ALU.subtract)
                tt(UI[:, :], W3[:, :], W3[:, :], ALU.add)
            tt(AR[:, :], MAG[:, :], UR[:, :], ALU.mult)
            tt(AI[:, :], MAG[:, :], UI[:, :], ALU.mult)
            tt(W1[:, :], LRE[:, :], LRE[:, :], ALU.mult)
            tt(W2_[:, :], LIM[:, :], LIM[:, :], ALU.mult)
            tt(W1[:, :], W1[:, :], W2_[:, :], ALU.add)
            dv(lambda: V.reciprocal(out=W1[:, :], in_=W1[:, :]), [Rsu], [Rsu])
            dv(lambda: V.tensor_scalar(out=W2_[:, :], in0=AR[:, :], scalar1=-1.0, scalar2=None, op0=ALU.add), [Rsu], [Rsu])
            tt(FR[:, :], W2_[:, :], LRE[:, :], ALU.mult)
            tt(W3[:, :], AI[:, :], LIM[:, :], ALU.mult)
            tt(FR[:, :], FR[:, :], W3[:, :], ALU.add)
            tt(FR[:, :], FR[:, :], W1[:, :], ALU.mult)
            tt(FI[:, :], AI[:, :], LRE[:, :], ALU.mult)
            tt(W3[:, :], W2_[:, :], LIM[:, :], ALU.mult)
            tt(FI[:, :], FI[:, :], W3[:, :], ALU.subtract)
            tt(FI[:, :], FI[:, :], W1[:, :], ALU.mult)
            dbg(MAG[:, :], 0, 16, [Rsu]); dbg(UR[:, :], 16, 16, [Rsu]); dbg(UI[:, :], 32, 16, [Rsu]); dbg(FR[:, :], 48, 16, [Rsu]); dbg(FI[:, :], 64, 16, [Rsu])
            frb = FR[:, :].unsqueeze(2).broadcast_to([128, 16, 16])
            fib = FI[:, :].unsqueeze(2).broadcast_to([128, 16, 16])
            tt(BBR[:, :, :], BRE[:, :, :], frb, ALU.mult)
            tt(BBI[:, :, :], BIM[:, :, :], fib, ALU.mult)
            tt(BBR[:, :, :], BBR[:, :, :], BBI[:, :, :], ALU.subtract)
            tt(BBI[:, :, :], BRE[:, :, :], fib, ALU.mult)
            tt(BRE[:, :, :], BIM[:, :, :], frb, ALU.mult)
            tt(BBI[:, :, :], BBI[:, :, :], BRE[:, :, :], ALU.add)
            dv(lambda: V.memset(CW[:, :, :, :], 0.0), [], [Rsu])
            for ri, src in ((0, s5_c_re), (1, s5_c_im)):
                S.dma("sp", CNAT[:, :, :], src[l].rearrange("(c g) h p -> (g h) c p", c=2), writes=[Rcn])
                CNB = TTf[:, 1, 0:64].bitcast(BF16)
                dv(lambda: V.tensor_copy(out=CNB.rearrange("p (c k) -> p c k", c=2), in_=CNAT[:, :, :]), [Rcn, Rtt], [Rtt])
                for c in range(2):
                    for half in range(2):
                        S.op("pe", lambda c=c, half=half: nc.tensor.matmul(PS[half * 64:(half + 1) * 64, 6, c * 128:(c + 1) * 128], lhsT=CNB[:, c * 64:(c + 1) * 64], rhs=identb[:, :],
                                                                           start=True, stop=True), reads=[Rtt, Rid], writes=[RPS[6]])
                ctv = PS[:, 6, 0:256].rearrange("q (m g h) -> q m g h", m=8, g=2)
                sc = 1.0 if ri == 0 else -1.0
                dv(lambda ri=ri, sc=sc: V.tensor_scalar(out=CW[0:64, :, ri, 0:16], in0=ctv[0:64, :, 0, :], scalar1=sc, scalar2=None, op0=ALU.mult), [RPS[6]], [Rsu])
                dv(lambda ri=ri, sc=sc: V.tensor_scalar(out=CW[64:128, :, ri, 16:32], in0=ctv[64:128, :, 1, :], scalar1=sc, scalar2=None, op0=ALU.mult), [RPS[6]], [Rsu])
            S.barrier()
            def build_pows(TAB, nent, base_r, base_i):
                dv(lambda: V.memset(TAB[:, 0, :, 0:1], 1.0), [], [Rsu])
                dv(lambda: V.memset(TAB[:, 1, :, 0:1], 0.0), [], [Rsu])
                tt(PW[:, 0, :], base_r, base_r, ALU.max)
                tt(PW[:, 1, :], base_i, base_i, ALU.max)
                nn = 1
                while nn < nent:
                    pr = PW[:, 0, :].unsqueeze(2).broadcast_to([128, 16, nn])
                    pi_ = PW[:, 1, :].unsqueeze(2).broadcast_to([128, 16, nn])
                    t1 = TTf[:, 0, 0:16 * nn].rearrange("p (k j) -> p k j", k=16)
                    t2 = TTf[:, 1, 0:16 * nn].rearrange("p (k j) -> p k j", k=16)
                    rr, ri = Rsu, Rtt
                    tt(t1, TAB[:, 0, :, 0:nn], pr, ALU.mult, (rr, ri), (ri,))
                    tt(t2, TAB[:, 1, :, 0:nn], pi_, ALU.mult, (rr, ri), (ri,))
                    tt(TAB[:, 0, :, nn:2 * nn], t1, t2, ALU.subtract, (rr, ri), (rr,))
                    tt(t1, TAB[:, 0, :, 0:nn], pi_, ALU.mult, (rr, ri), (ri,))
                    tt(t2, TAB[:, 1, :, 0:nn], pr, ALU.mult, (rr, ri), (ri,))
                    tt(TAB[:, 1, :, nn:2 * nn], t1, t2, ALU.add, (rr, ri), (rr,))
                    tt(W1[:, :], PW[:, 0, :], PW[:, 0, :], ALU.mult)
                    tt(W2_[:, :], PW[:, 1, :], PW[:, 1, :], ALU.mult)
                    tt(W3[:, :], PW[:, 0, :], PW[:, 1, :], ALU.mult)
                    tt(PW[:, 0, :], W1[:, :], W2_[:, :], ALU.subtract)
                    tt(PW[:, 1, :], W3[:, :], W3[:, :], ALU.add)
                    nn *= 2
            build_pows(TBs, 32, UR[:, :], UI[:, :])
            tt(W1[:, :], PW[:, 0, :], PW[:, 0, :], ALU.max)
            tt(W2_[:, :], PW[:, 1, :], PW[:, 1, :], ALU.max)
            tt(UBR[:, :], PW[:, 0, :], PW[:, 0, :], ALU.max)
            tt(UBI[:, :], PW[:, 1, :], PW[:, 1, :], ALU.max)
            build_pows(TA_, 16, UBR[:, :], UBI[:, :])
            if BW == 512:
                tt(UBR[:, :], PW[:, 0, :], PW[:, 0, :], ALU.max)
                tt(UBI[:, :], PW[:, 1, :], PW[:, 1, :], ALU.max)
            else:
                tt(UBR[:, :], TA_[:, 0, :, 8], TA_[:, 0, :, 8], ALU.max)
                tt(UBI[:, :], TA_[:, 1, :, 8], TA_[:, 1, :, 8], ALU.max)
            S.barrier()
            mcb = [0]
            for m in range(8):
                cc, m4 = m // 4, m % 4
                for d in range(2):
                    k = m * 2 + d
                    for ri, BB in ((0, BBR), (1, BBI)):
                        dv(lambda: V.memset(WP[:, :], 0.0), [Rwp], [Rwp])
                        dv(lambda BB=BB, k=k: V.tensor_copy(out=WP[0:64, m4 * 32:m4 * 32 + 16], in_=BB[0:64, k, :]), [Rsu, Rwp], [Rwp])
                        dv(lambda BB=BB, k=k: V.tensor_copy(out=WP[64:128, m4 * 32 + 16:m4 * 32 + 32], in_=BB[64:128, k, :]), [Rsu, Rwp], [Rwp])
                        S.op("pe", lambda: nc.tensor.transpose(out=PS[:, 7, 0:128], in_=WP[:, :], identity=ident[:]), reads=[Rwp, Rid], writes=[RPS[7]])
                        S.op("act", lambda d=d, ri=ri: nc.scalar.copy(out=BW_[:, d, ri, :], in_=PS[:, 7, 0:128]), reads=[RPS[7]], writes=[Rbw])
                for d in range(2):
                    k = m * 2 + d
                    rev = d == 1
                    ar = TA_[:, 0, k, :].unsqueeze(2).broadcast_to([128, 16, 32])
                    ai = TA_[:, 1, k, :].unsqueeze(2).broadcast_to([128, 16, 32])
                    br = TBs[:, 0, k, :].unsqueeze(1).broadcast_to([128, 16, 32])
                    bi = TBs[:, 1, k, :].unsqueeze(1).broadcast_to([128, 16, 32])
                    trv = TRI[:, 0, :].rearrange("p (q j) -> p q j", q=16)
                    tiv = TRI[:, 1, :].rearrange("p (q j) -> p q j", q=16)
                    t1 = TTf[:, 0, :].rearrange("p (q j) -> p q j", q=16)
                    t2 = TTf[:, 1, :].rearrange("p (q j) -> p q j", q=16)
                    rw = (Rsu, Rtt, Rtri) + tuple(Rt4)
                    tt(t1, ar, br, ALU.mult, rw, (Rtt,) + tuple(Rt4))
                    tt(t2, ai, bi, ALU.mult, rw, (Rtt,) + tuple(Rt4))
                    tt(trv, t1, t2, ALU.subtract, rw, (Rtri,))
                    tt(t1, ar, bi, ALU.mult, rw, (Rtt,) + tuple(Rt4))
                    tt(t2, ai, br, ALU.mult, rw, (Rtt,) + tuple(Rt4))
                    tt(tiv, t1, t2, ALU.add, rw, (Rtri,))
                    dv(lambda: V.tensor_copy(out=TRIb[:, :, :], in_=TRI[:, :, :]), [Rtri], [Rtri])
                    if k == 0:
                        dbg(TRI[:, 0, :], 128, 512, [Rtri]); dbg(TRI[:, 1, :], 640, 512, [Rtri])
                    ib = 0
                    if sample:
                        cmul(INI[:, 0, ib:ib + 1], INI[:, 1, ib:ib + 1], UR[:, k:k + 1], UI[:, k:k + 1], S0t[:, k, 0:1], S0t[:, k, 1:2], W1[:, 0:1], W2_[:, 0:1], (Rsu, Rini), (Rsu, Rini))
                    else:
                        dv(lambda: V.memset(INI[:, :, 0:1], 0.0), [Rini], [Rini])
                    blocks = list(range(NB - 1, -1, -1)) if rev else list(range(NB))
                    for bi_, b in enumerate(blocks):
                        cols = slice(b * BW, (b + 1) * BW)
                        bk = 2 * (mcb[0] % 2)
                        mcb[0] += 1
                        for ri in range(2):
                            S.op("pe", lambda ri=ri: nc.tensor.matmul(PS[:, bk + ri, 0:BW], lhsT=BW_[:, d, ri, :], rhs=UT[:, cc, cols], start=True, stop=True),
                                 reads=[Rbw, Rut], writes=[RPS[bk + ri]])
                        if rev:
                            trr, tri = TRI[:, 0, BW - 1::-1] if BW == 512 else TRI[:, 0, BW - 1::-1], TRI[:, 1, BW - 1::-1]
                            trr = TRI[:, 0, 0:BW][:, ::-1]
                            tri = TRI[:, 1, 0:BW][:, ::-1]
                            trrb = TRIb[:, 0, 0:BW][:, ::-1]
                            trib = TRIb[:, 1, 0:BW][:, ::-1]
                        else:
                            trr, tri = TRI[:, 0, 0:BW], TRI[:, 1, 0:BW]
                            trrb, trib = TRIb[:, 0, 0:BW], TRIb[:, 1, 0:BW]
                        for ri in range(2):
                            S.op("act", lambda ri=ri: nc.scalar.copy(out=SBb[:, ri, 0:BW], in_=PS[:, bk + ri, 0:BW]), reads=[RPS[bk + ri]], writes=[Rsbb])
                        pre, pim = SBb[:, 0, 0:BW], SBb[:, 1, 0:BW]
                        T0_, T1_ = TT[:, 0, 0:BW], TT[:, 1, 0:BW]
                        P0_, P1_ = TT[:, 0, 512:512 + BW], TT[:, 1, 512:512 + BW]
                        B0_, B1_ = BZ[:, 0, 0:BW], BZ[:, 1, 0:BW]
                        tt(T0_, pre, trrb, ALU.mult, (Rtri, Rsbb, Rt4[0]), (Rt4[0],))
                        tt(T1_, pim, trib, ALU.mult, (Rtri, Rsbb, Rt4[1]), (Rt4[1],))
                        tt(P0_, pim, trrb, ALU.mult, (Rtri, Rsbb, Rt4[2]), (Rt4[2],))
                        tt(P1_, pre, trib, ALU.mult, (Rtri, Rsbb, Rt4[3]), (Rt4[3],))
                        tt(B0_, T0_, T1_, ALU.add, (Rt4[0], Rt4[1], Rbz2[0]), (Rbz2[0],))
                        tt(B1_, P0_, P1_, ALU.subtract, (Rt4[2], Rt4[3], Rbz2[1]), (Rbz2[1],))
                        for ri in range(2):
                            zo = TT[:, ri, 0:BW]
                            zin = BZ[:, ri, 0:BW]
                            if rev:
                                zo, zin = zo[:, ::-1], zin[:, ::-1]
                            dv(lambda zo=zo, zin=zin, ri=ri: V.tensor_tensor_scan(out=zo, data0=MAG[:, k:k + 1].broadcast_to([128, BW]), data1=zin, initial=INI[:, ri, ib:ib + 1], op0=ALU.mult, op1=ALU.add),
                               [Rsu, Rbz2[ri], Rini, Rt4[ri]], [Rt4[ri]])
                        zl = 0 if rev else BW - 1
                        zr_, zi_ = TT[:, 0, zl:zl + 1], TT[:, 1, zl:zl + 1]
                        dstS = SBb[:, :, 0:BW]
                        Rdst = Rsbb
                        tt(B0_, T0_, trrb, ALU.mult, (Rtri, Rt4[0], Rbz2[0]), (Rbz2[0],))
                        tt(B1_, T1_, trib, ALU.mult, (Rtri, Rt4[1], Rbz2[1]), (Rbz2[1],))
                        tt(P0_, T1_, trrb, ALU.mult, (Rtri, Rt4[1], Rt4[2]), (Rt4[2],))
                        tt(P1_, T0_, trib, ALU.mult, (Rtri, Rt4[0], Rt4[3]), (Rt4[3],))
                        tt(dstS[:, 0, :], B0_, B1_, ALU.subtract, (Rbz2[0], Rbz2[1], Rdst), (Rdst,))
                        tt(dstS[:, 1, :], P0_, P1_, ALU.add, (Rt4[2], Rt4[3], Rdst), (Rdst,))
                        last_blk = bi_ == NB - 1 or multi
                        if multi and bi_ != NB - 1:
                            dv(lambda: V.memset(INI[:, :, 1 - ib:2 - ib], 0.0), [Rini], [Rini])
                        if not last_blk:
                            ib2 = 1 - ib
                            rc, wc = [Rsu, Rini, Rt4[0], Rt4[1]], [Rini]
                            dv(lambda: V.tensor_scalar(out=W1[:, 0:1], in0=zi_, scalar1=UBI[:, k:k + 1], scalar2=None, op0=ALU.mult), [Rsu, Rt4[1], Rw12[0]], [Rw12[0]])
                            dv(lambda: V.tensor_scalar(out=W2_[:, 0:1], in0=zr_, scalar1=UBI[:, k:k + 1], scalar2=None, op0=ALU.mult), [Rsu, Rt4[0], Rw12[1]], [Rw12[1]])
                            dv(lambda: V.scalar_tensor_tensor(out=INI[:, 0, ib2:ib2 + 1], in0=zr_, scalar=UBR[:, k:k + 1], in1=W1[:, 0:1], op0=ALU.mult, op1=ALU.subtract), [Rsu, Rt4[0], Rw12[0], Rini], [Rini])
                            dv(lambda: V.scalar_tensor_tensor(out=INI[:, 1, ib2:ib2 + 1], in0=zi_, scalar=UBR[:, k:k + 1], in1=W2_[:, 0:1], op0=ALU.mult, op1=ALU.add), [Rsu, Rt4[1], Rw12[1], Rini], [Rini])
                            ib = ib2
                        elif not sample:
                            pf = b if multi else 0
                            cmul(FIN[:, pf, k, 0:1], FIN[:, pf, k, 1:2], TRI[:, 0, BW - 1:BW], TRI[:, 1, BW - 1:BW], zr_, zi_, W1[:, 0:1], W2_[:, 0:1], (Rsu, Rtri, Rt4[0], Rt4[1], Rfin), (Rsu, Rfin))
                        if multi and bi_ != NB - 1:
                            ib = 1 - ib
                        def emit_y():
                            nc.tensor.matmul(PS[0:32, 4, 0:BW], lhsT=CW[:, m, 0, :], rhs=SBb[:, 0, 0:BW], start=True, stop=False)
                            return nc.tensor.matmul(PS[0:32, 4, 0:BW], lhsT=CW[:, m, 1, :], rhs=SBb[:, 1, 0:BW], start=False, stop=True)
                        S.op("pe", emit_y, reads=[Rsu, Rsbb], writes=[RPS[4]])
                        if not rev:
                            S.op("act", lambda: nc.scalar.copy(out=YFp[0:32, cols], in_=PS[0:32, 4, 0:BW]), reads=[RPS[4]], writes=[Rsfs])
                        else:
                            tt(OTs[0:32, 0:BW], PS[0:32, 4, 0:BW], YFp[0:32, cols], ALU.add, (RPS[4], Rsfs, Rots), (Rots,))
                            S.dma("sp", YS[m4 * 32:(m4 + 1) * 32, cc, cols], OTs[0:32, 0:BW], reads=[Rots], writes=[Rys], key=Reg("s_OTd"))
            if not sample:
                for pi in ((0, 1) if multi else ((0 if kind == "pA" else 1),)):
                    pf = pi if multi else 0
                    for d in range(2):
                        OUT_EVS.append(S.dma("sp", o_s5[pi, l, d].rearrange("(m g) p r -> (g p) m r", g=2), FIN[:, pf, d::2, :], reads=[Rfin], key=Reg("o_s5_d"), slow=True))
            S.barrier()
            Z, _ = view(oZ, [128, 2, 2048], BF16)
            Rz_ = Reg("s_Z")
            for b in range(NB):
                cols = slice(b * BW, (b + 1) * BW)
                for cc in range(2):
                    yv_ = YV[:, 0:BW]
                    dv(lambda: V.scalar_tensor_tensor(out=yv_, in0=UT[:, cc, cols], scalar=PV[:, 10 + cc:11 + cc], in1=YS[:, cc, cols], op0=ALU.mult, op1=ALU.add),
                       [Rut, Rpv, Rys, Rbz], [Rbz])
                    tt(TTf[:, 0, 0:BW], yv_, yv_, ALU.mult, (Rbz, Rtt), (Rtt,))
                    dv(lambda: V.tensor_scalar(out=TTf[:, 0, 0:BW], in0=TTf[:, 0, 0:BW], scalar1=0.044715, scalar2=1.0, op0=ALU.mult, op1=ALU.add), [Rtt], [Rtt])
                    tt(TTf[:, 0, 0:BW], TTf[:, 0, 0:BW], yv_, ALU.mult, (Rbz, Rtt), (Rtt,))
                    S.op("act", lambda: nc.scalar.activation(out=TTf[:, 1, 0:BW], in_=TTf[:, 0, 0:BW], func=AF.Sigmoid, scale=1.5957691216057308), reads=[Rtt], writes=[Rtt])
                    tt(Z[:, cc, cols], TTf[:, 1, 0:BW], yv_, ALU.mult, (Rbz, Rtt, Rz_), (Rz_,))
                for co in range(2):
                    def emit_g(co=co):
                        nc.tensor.matmul(PS[:, 5, 0:BW], lhsT=WGL[:, 0, co * 128:(co + 1) * 128], rhs=Z[:, 0, cols], start=True, stop=False)
                        return nc.tensor.matmul(PS[:, 5, 0:BW], lhsT=WGL[:, 1, co * 128:(co + 1) * 128], rhs=Z[:, 1, cols], start=False, stop=True)
                    S.op("pe", emit_g, reads=[Rsu, Rz_], writes=[RPS[5]])
                    S.op("act", lambda: nc.scalar.activation(out=OTs[:, 0:BW], in_=PS[:, 5, 0:BW], func=AF.Sigmoid), reads=[RPS[5]], writes=[Rots])
                    tt(cur["BR"][:, co, cols], Z[:, co, cols], OTs[:, 0:BW], ALU.mult, (Rz_, Rots), (Rbr[co],))

        DBG = {}

        def mixer(l):
            make_gate_bcast(1)
            mixer_params(l)

            def build_ht(t0, ntile, ci):
                S.barrier()
                cur["XN"], _ = view(BS0, [128, 2, D])
                for tt in range(ntile):
                    prenorm_tile(t0 + tt, ci, 1, HT[:, :, tt * 128:(tt + 1) * 128], Rht, (4 + 2 * (tt % 2), 5 + 2 * (tt % 2)))
                S.barrier()

            def zero_disabled(T):
                for nm, chs in (("s5", (0, 1)), ("ret", (2, 3)), ("conv", (4, 5)), ("mla", (6, 7))):
                    if not cfg.get(nm, True):
                        for i in chs:
                            S.op("dve", lambda i=i: nc.vector.memset(BR[:, i, 0:T], 0.0), writes=[Rbr[i]])

            def seq_branches(t0, T, NB, BW, kind):
                if cfg.get("conv", True):
                    branch_conv(l, T, NB, BW)
                    S.barrier()
                if cfg.get("mla", True):
                    branch_mla(l, t0, T, NB, BW, kind)
                    S.barrier()
                if cfg.get("ret", True):
                    branch_ret(l, t0, T, kind)
                    S.barrier()

            cur["HT"], cur["BR"] = HT, BR
            build_ht(0, 16, 0)
            zero_disabled(2048)
            seq_branches(0, 2048, 4, 512, "sample")
            if cfg.get("s5", True):
                branch_s5(l, 0, 2048, 4, 512, "sample")
                S.barrier()
            if tuple(cfg.get("dump_br", ())) == (l, "sample"):
                dbg = nc.dram_tensor("dbg_br", [128, 8, 2048], BF16, kind="ExternalOutput").ap()
                S.dma("sp", dbg[:, :, 0:2048], BR[:, :, 0:2048], reads=Rbr, key=Reg("dbg"))
            gate_stage(l, 0, 16, 0, 2048, 4, 512)
            S.barrier()
            build_ht(16, 4, 1)
            zero_disabled(512)
            for pi, kind in ((0, "pA"), (1, "pB")):
                cur["HT"], cur["BR"] = HT[:, :, pi * 256:(pi + 1) * 256], BR[:, :, pi * 256:(pi + 1) * 256]
                mla_cache_out(l, 16 + 2 * pi, 2, pi)
                S.barrier()
                seq_branches(16 + 2 * pi, 256, 1, 256, kind)
            cur["HT"], cur["BR"] = HT, BR
            if cfg.get("s5", True):
                branch_s5(l, 16, 512, 2, 256, "pAB", multi=True)
                S.barrier()
            gate_stage(l, 16, 4, 1, 512, 1, 512)
            S.barrier()
            cur["XN"], cur["TMP"] = XN, TMP

        for l in range(LAYERS):
            S.barrier()
            compute_mod(l)
            S.barrier()
            if cfg.get("ffn1", True):
                ffn(l, 0, 0)
            S.barrier()
            if cfg.get("mixer", True):
                mixer(l)
            S.barrier()
            if cfg.get("ffn2", True):
                ffn(l, 1, 2)

        yv = y.rearrange("(t p) d -> p t d", p=128)
        Ryout = Reg("yout")
        evs = []
        for t in range(NT):
            evs.append(S.dma("sp", yv[:, t, :], X[:, t, :], reads=[RX[t]], key=Ryout))
        S._wait("sp", set([evs[-1]] + OUT_EVS))
        S.barrier()
    return nc


def _axial(T, dim):
    rows = T // 64
    row = np.repeat(np.arange(rows, dtype=np.float32), 64)
    col = np.tile(np.arange(64, dtype=np.float32), rows)
    quarter = dim // 4
    inv = (np.float32(10000.0) ** (-np.arange(quarter, dtype=np.float32) / np.float32(quarter))).astype(np.float32)
    ang = np.concatenate([row[:, None] * inv, col[:, None] * inv], axis=-1).astype(np.float32)
    return np.cos(ang).astype(np.float32), np.sin(ang).astype(np.float32)


def _rope_tables():
    c, s = _axial(2048, 32)
    mla = np.stack([np.concatenate([c.T, c.T], 0), np.concatenate([s.T, s.T], 0)], 0)
    c2, s2 = _axial(2048, 64)
    ret = np.stack([c2, s2], 0)
    return np.ascontiguousarray(mla, dtype=np.float32), np.ascontiguousarray(ret, dtype=np.float32)


def _prep_inputs(inputs):
    f = lambda a: np.ascontiguousarray(np.asarray(a, dtype=np.float32))
    shared = {k: f(inputs[k]) for k in (
        "w_mod", "b_mod", "norm_pre", "norm_post", "ffn_w1", "ffn_w3", "ffn_w2", "w_in", "s5_lam_re", "s5_lam_im", "s5_log_dt",
        "s5_b_re", "s5_b_im", "s5_c_re", "s5_c_im", "s5_d", "s5_w_glu", "ret_decay", "ret_gn", "conv_w", "conv_b",
        "mla_q_norm", "mla_w_uq", "mla_kv_norm", "mla_w_ukv", "w_branch", "w_gate", "b_gate", "w_o")}
    shared["c_ident"] = np.eye(128, dtype=np.float32)
    shared["c_rope_mla"], shared["c_rope_ret"] = _rope_tables()
    jj = np.arange(128, dtype=np.float32)[:, None]
    ii = np.arange(128, dtype=np.float32)[None, :]
    diff = ii - jj
    shared["c_ret"] = np.ascontiguousarray(np.stack([
        np.maximum(diff, 0), np.maximum(-diff, 0), 0.125 * (diff >= 0), 0.125 * (diff < 0),
        np.broadcast_to(ii + 1.0, (128, 128)), np.broadcast_to(128.0 - ii, (128, 128))], 0), dtype=np.float32)
    shared["c_pidx"] = np.ascontiguousarray(np.concatenate([127.0 - jj, jj], 1), dtype=np.float32)
    xp = f(inputs["x_prompt"])
    xs = f(inputs["x_sample"])
    c = f(inputs["c"])
    cctx = f(inputs["c_ctx"])
    maps = []
    for i in range(NCORES):
        m = dict(shared)
        m["xin"] = np.ascontiguousarray(np.concatenate([xs[i], xp[2 * i], xp[2 * i + 1]], axis=0))
        m["cond2"] = np.ascontiguousarray(np.stack([c[i], cctx], axis=0))
        m["st_s5"] = f(inputs["state_s5"][i])
        m["st_ret"] = f(inputs["state_ret"][i])
        m["ctx_mla"] = f(inputs["cache_mla"][i])
        maps.append(m)
    return maps


def _gather(res):
    ys = np.stack([r["y"][:2048] for r in res], axis=0)
    yp = np.stack([r["y"][2048 + 256 * j:2048 + 256 * (j + 1)] for r in res for j in range(2)], axis=0)
    s5 = np.concatenate([r["o_s5"] for r in res], axis=0)
    ret = np.concatenate([r["o_ret"] for r in res], axis=0)
    mla = np.concatenate([r["o_mla"] for r in res], axis=0)
    return (yp.astype(np.float32), ys.astype(np.float32), s5.astype(np.float32), ret.astype(np.float32), mla.astype(np.float32))


CFG = {}


def kernel(**inputs):
    nc = build(CFG)
    maps = _prep_inputs(inputs)
    res = run_bass_kernel_spmd(nc, maps, core_ids=list(range(NCORES)))
    return _gather(res.results)
```

```python
import numpy as np
import concourse.bass as bass
import concourse.mybir as mybir
from concourse.bass_utils import run_bass_kernel_spmd
from contextlib import ExitStack

F32 = mybir.dt.float32
BF16 = mybir.dt.bfloat16
AF = mybir.ActivationFunctionType
ALU = mybir.AluOpType
AX = mybir.AxisListType

D = 1024
DFF = 2816
NFF = 22
TOK = 2560
NT = 20
EPS = 1e-6
NCORES = 8
INC = 2400

SAME_ENG_SYNC = True


_REGS = {}


def Reg(name):
    if name not in _REGS:
        _REGS[name] = _Reg(name)
    return _REGS[name]


class _Reg:
    __slots__ = ("name", "w", "r", "dsem", "dcnt")

    def __init__(self, name):
        self.name = name
        self.w = None
        self.r = []
        self.dsem = None
        self.dcnt = 0


class Sched:
    def __init__(self, nc, es):
        self.nc = nc
        self.es = es
        self.eng = {"pe": nc.tensor, "act": nc.scalar, "dve": nc.vector, "pool": nc.gpsimd, "sp": nc.sync}
        self.sem = {e: es.enter_context(nc.semaphore("s_" + e)) for e in self.eng}
        self.cnt = {e: 0 for e in self.eng}
        self.seen = {e: {} for e in self.eng}
        self.seen_d = {e: {} for e in self.eng}
        self.nsem = 0
        self.out_events = []
        self.pending_reads = {}

    def _wait(self, e, deps, raw=None):
        best = {}
        bestd = {}
        for d in deps:
            if d[0] == "e":
                _, e2, c = d
                if e2 == e and (e == "pe" or not SAME_ENG_SYNC or (raw is not None and d not in raw)):
                    continue
                if c > best.get(e2, 0):
                    best[e2] = c
            else:
                _, sem, tgt, key = d
                if tgt > bestd.get(key, (None, 0))[1]:
                    bestd[key] = (sem, tgt)
        E = self.eng[e]
        for e2, c in best.items():
            if self.seen[e].get(e2, 0) >= c:
                continue
            E.wait_ge(self.sem[e2], c)
            self.seen[e][e2] = c
        for key, (sem, tgt) in bestd.items():
            if self.seen_d[e].get(key, 0) >= tgt:
                continue
            E.wait_ge(sem, tgt)
            self.seen_d[e][key] = tgt

    def _deps(self, reads, writes):
        deps = set()
        for r in reads:
            if r.w is not None:
                deps.add(r.w)
        for w in writes:
            if w.w is not None:
                deps.add(w.w)
            deps.update(w.r)
        return deps

    def op(self, e, emit, reads=(), writes=()):
        raw = set(r.w for r in reads if r.w is not None)
        self._wait(e, self._deps(reads, writes), raw)
        inst = emit()
        self.cnt[e] += 1
        inst.then_inc(self.sem[e], 1)
        ev = ("e", e, self.cnt[e])
        for r in reads:
            r.r.append(ev)
        for w in writes:
            w.w = ev
            w.r = []
        return ev

    def dma(self, e, out, in_, reads=(), writes=(), key=None, slow=False):
        key = key or (list(writes) + list(reads))[0]
        if key.dsem is None:
            key.dsem = self.es.enter_context(self.nc.semaphore("d%d" % self.nsem))
            self.nsem += 1
        deps = set(d for d in self._deps(reads, writes) if not (d[0] == "d" and d[3] == key.name))
        self._wait(e, deps)
        inst = self.eng[e].dma_start(out=out, in_=in_, allow_slow_non_contiguous=True) if slow else self.eng[e].dma_start(out=out, in_=in_)
        key.dcnt += 16
        inst.then_inc(key.dsem, 16)
        ev = ("d", key.dsem, key.dcnt, key.name)
        if reads:
            self.pending_reads[key.name] = ev
        for r in reads:
            r.r.append(ev)
        for w in writes:
            w.w = ev
            w.r = []
        return ev

    def barrier(self):
        pend = set(self.pending_reads.values())
        for e in self.eng:
            deps = set(("e", e2, self.cnt[e2]) for e2 in self.eng if e2 != e and self.cnt[e2] > 0)
            self._wait(e, deps | pend)
        self.pending_reads = {}


def build(cfg):
    _REGS.clear()
    nc = bass.Bass("TRN2", target_bir_lowering=False)
    LAYERS = cfg.get("layers", 2)

    def din(name, shape):
        return nc.dram_tensor(name, list(shape), F32, kind="ExternalInput").ap()

    def dout(name, shape):
        return nc.dram_tensor(name, list(shape), F32, kind="ExternalOutput").ap()

    xin = din("xin", [TOK, D])
    cond2 = din("cond2", [2, D])
    st_s5 = din("st_s5", [2, 2, 16, 64, 2])
    st_ret = din("st_ret", [2, 2, 4, 64, 64])
    ctx_mla = din("ctx_mla", [2, 512, 160])
    w_mod = din("w_mod", [2, D, 9 * D])
    b_mod = din("b_mod", [2, 9 * D])
    norm_pre = din("norm_pre", [2, 3, D])
    norm_post = din("norm_post", [2, 3, D])
    ffn_w1 = din("ffn_w1", [2, 2, D, DFF])
    ffn_w3 = din("ffn_w3", [2, 2, D, DFF])
    ffn_w2 = din("ffn_w2", [2, 2, DFF, D])
    w_in = din("w_in", [2, D, INC])
    s5_lam_re = din("s5_lam_re", [2, 2, 16, 64])
    s5_lam_im = din("s5_lam_im", [2, 2, 16, 64])
    s5_log_dt = din("s5_log_dt", [2, 2, 16])
    s5_b_re = din("s5_b_re", [2, 2, 16, 64, 16])
    s5_b_im = din("s5_b_im", [2, 2, 16, 64, 16])
    s5_c_re = din("s5_c_re", [2, 16, 16, 64])
    s5_c_im = din("s5_c_im", [2, 16, 16, 64])
    s5_d = din("s5_d", [2, 256])
    s5_w_glu = din("s5_w_glu", [2, 256, 256])
    ret_decay = din("ret_decay", [2, 2, 4])
    ret_gn = din("ret_gn", [2, 256])
    conv_w = din("conv_w", [2, 3, 256])
    conv_b = din("conv_b", [2, 256])
    mla_q_norm = din("mla_q_norm", [2, 192])
    mla_w_uq = din("mla_w_uq", [2, 192, 384])
    mla_kv_norm = din("mla_kv_norm", [2, 128])
    mla_w_ukv = din("mla_w_ukv", [2, 128, 512])
    w_branch = din("w_branch", [2, 4, 256, D])
    w_gate = din("w_gate", [2, D, 4 * D])
    b_gate = din("b_gate", [2, 4 * D])
    w_o = din("w_o", [2, D, D])
    c_ident = din("c_ident", [128, 128])
    c_rope_mla = din("c_rope_mla", [2, 32, 2048])
    c_rope_ret = din("c_rope_ret", [2, 2048, 32])
    c_ret = din("c_ret", [6, 128, 128])
    c_pidx = din("c_pidx", [128, 2])

    y = dout("y", [TOK, D])
    o_s5 = dout("o_s5", [2, 2, 2, 16, 64, 2])
    o_ret = dout("o_ret", [2, 2, 2, 4, 64, 64])
    o_mla = dout("o_mla", [2, 2, 256, 160])

    es = ExitStack()
    with es:
        S = Sched(nc, es)

        def sb(name, shape, dt=F32):
            return es.enter_context(nc.sbuf_tensor(name, list(shape), dt))

        X = sb("X", [128, NT, D])
        RX = [Reg("X%d" % t) for t in range(NT)]
        PS = es.enter_context(nc.psum_tensor("PS", [128, 8, 512], F32))
        RPS = [Reg("PS%d" % b) for b in range(8)]
        ident = sb("ident", [128, 128])
        identb = sb("identb", [128, 128], BF16)
        Rid = Reg("ident")
        VEC = sb("VEC", [128, 2, 72])
        Rvec = Reg("VEC")
        NRM = sb("NRM", [128, 48])
        Rnrm = Reg("NRM")
        SV = sb("SV", [128, 2, 3, 8])
        Rsv = Reg("SV")
        GV = sb("GV", [128, 2, 3, 8])
        Rgv = Reg("GV")
        GB = sb("GB", [128, 2, D])
        Rgb = [Reg("GB0"), Reg("GB1")]
        DG = sb("DG", [128, 2, 128])
        Rdg = [Reg("DG0"), Reg("DG1")]
        small = sb("small", [128, 4, 4])
        Rsmall = [Reg("sm%d" % i) for i in range(4)]
        junk = sb("junk", [128, D], BF16)
        Rjunk = Reg("junk")
        ACOLS = 28672
        ARENA = sb("ARENA", [128, ACOLS])

        def view(off, shape, dt=F32):
            n = int(np.prod(shape[1:]))
            nbytes = n * (2 if dt == BF16 else 4)
            assert off % 4 == 0 and nbytes % 4 == 0 and off + nbytes <= ACOLS * 4, (off, shape)
            ap = ARENA[:, off // 4:(off + nbytes) // 4]
            if dt == BF16:
                ap = ap.bitcast(BF16)
            if len(shape) > 2:
                names = "abcdef"[:len(shape) - 1]
                pat = "p (" + " ".join(names) + ") -> p " + " ".join(names)
                ap = ap.rearrange(pat, **{names[i]: shape[i + 1] for i in range(len(names) - 1)})
            return ap, off + nbytes

        HTB, _o = view(0, [128, 2, 8, 512], BF16)
        GT, _o = view(_o, [128, NFF, 512], BF16)
        W13, _o = view(_o, [128, 3, 2, 8, 256], BF16)
        W2, _o = view(_o, [128, 3, 2, D], BF16)
        SIL, _o = view(_o, [128, 2, 512], BF16)
        XN, _o = view(_o, [128, 2, D])
        TMP, _o = view(_o, [128, 2, D])
        WM, _ = view(0, [128, 2, 8, 512], BF16)
        Rxn = [Reg("XN0"), Reg("XN1")]

        xv = xin.rearrange("(t p) d -> p t d", p=128)
        Rxall = Reg("xall")
        for t in range(NT):
            S.dma("sp", X[:, t, :], xv[:, t, :], writes=[RX[t]], key=Rxall)
        for t in range(NT):
            RX[t].w = ("d", Rxall.dsem, Rxall.dcnt, Rxall.name)
        S.dma("sp", ident[:], c_ident[:, :], writes=[Rid])
        S.op("dve", lambda: nc.vector.tensor_copy(out=identb[:], in_=ident[:]), reads=[Rid], writes=[Rid])

        rot = {"ps": 0, "sm": 0, "xn": 0, "dg": 0}
        OUT_EVS = []

        stage = sb("stage", [128, 128])
        Rstage = Reg("stage")

        def load_T(dst_ap, src_ap, rows, dst_reg, bank=7):
            S.dma("sp", stage[0:rows, :], src_ap, writes=[Rstage])
            S.op("pe", lambda: nc.tensor.transpose(out=PS[:, bank, 0:rows], in_=stage[0:rows, :], identity=ident[0:rows, 0:rows]),
                 reads=[Rstage, Rid], writes=[RPS[bank]])
            S.op("dve", lambda: nc.vector.tensor_copy(out=dst_ap, in_=PS[:, bank, 0:rows]), reads=[RPS[bank]], writes=[dst_reg])

        SCT = sb("SCT", [128, 8, 2], BF16)
        Rsct = Reg("SCT")
        sct32 = sb("sct32", [128, 16])
        load_T(sct32[:, :], cond2.rearrange("c (k p) -> (c k) p", p=128), 16, Rsct)
        S.op("act", lambda: nc.scalar.activation(out=SCT[:].rearrange("p k c -> p c k"), in_=sct32[:].rearrange("p (c k) -> p c k", c=2), func=AF.Silu),
             reads=[Rsct], writes=[Rsct])

        Rwm = [Reg("WM0"), Reg("WM1")]
        BM = sb("BM", [128, 72])
        Rbm = Reg("BM")

        def compute_mod(l):
            load_T(BM[:, :], b_mod[l].rearrange("(c p) -> c p", p=128), 72, Rbm)
            load_T(NRM[:, 0:24], norm_pre[l].rearrange("s (c p) -> (s c) p", p=128), 24, Rnrm)
            load_T(NRM[:, 24:48], norm_post[l].rearrange("s (c p) -> (s c) p", p=128), 24, Rnrm)
            wv = w_mod[l].rearrange("(kc p) n -> p kc n", p=128)
            for cb in range(18):
                sl = cb % 2
                S.dma("pool", WM[:, sl, :, :], wv[:, :, cb * 512:(cb + 1) * 512], writes=[Rwm[sl]])
                bank = 6

                def emit(cb=cb, sl=sl):
                    inst = None
                    for cc in range(4):
                        for kc in range(8):
                            inst = nc.tensor.matmul(PS[:, bank, cc * 2:cc * 2 + 2], lhsT=WM[:, sl, kc, cc * 128:(cc + 1) * 128],
                                                    rhs=SCT[:, kc, :], start=(kc == 0), stop=(kc == 7))
                    return inst
                S.op("pe", emit, reads=[Rwm[sl], Rsct], writes=[RPS[bank]])
                S.op("dve", lambda cb=cb: nc.vector.tensor_tensor(
                    out=VEC[:, :, cb * 4:(cb + 1) * 4].rearrange("p c j -> p j c"),
                    in0=PS[:, bank, 0:8].rearrange("p (j c) -> p j c", c=2),
                    in1=BM[:, cb * 4:(cb + 1) * 4].unsqueeze(2).broadcast_to([128, 4, 2]), op=ALU.add),
                    reads=[RPS[bank], Rbm], writes=[Rvec])
            for ci in range(2):
                for s in range(3):
                    S.op("dve", lambda ci=ci, s=s: nc.vector.scalar_tensor_tensor(
                        out=SV[:, ci, s, :], in0=VEC[:, ci, (3 * s + 1) * 8:(3 * s + 2) * 8], scalar=1.0,
                        in1=NRM[:, s * 8:(s + 1) * 8], op0=ALU.add, op1=ALU.mult), reads=[Rvec, Rnrm], writes=[Rsv])
                    fac = 1.0 if s == 1 else 0.5
                    S.op("dve", lambda ci=ci, s=s, fac=fac: nc.vector.scalar_tensor_tensor(
                        out=GV[:, ci, s, :], in0=VEC[:, ci, (3 * s + 2) * 8:(3 * s + 3) * 8], scalar=fac,
                        in1=NRM[:, 24 + s * 8:24 + (s + 1) * 8], op0=ALU.mult, op1=ALU.mult), reads=[Rvec, Rnrm], writes=[Rgv])

        def make_gate_bcast(s):
            for ci in range(2):
                bank0 = 4 + 2 * ci
                for c in range(8):
                    dgi = rot["dg"] % 2
                    rot["dg"] += 1
                    S.op("dve", lambda c=c, ci=ci, dgi=dgi: nc.vector.tensor_scalar(
                        out=DG[:, dgi, :], in0=ident[:], scalar1=GV[:, ci, s, c:c + 1], scalar2=None, op0=ALU.mult),
                        reads=[Rid, Rgv], writes=[Rdg[dgi]])
                    bank = bank0 + c // 4
                    S.op("pe", lambda c=c, dgi=dgi, bank=bank: nc.tensor.matmul(
                        PS[:, bank, (c % 4) * 128:(c % 4 + 1) * 128], lhsT=ones32[:], rhs=DG[:, dgi, :], start=True, stop=True),
                        reads=[Rdg[dgi], Rones], writes=[RPS[bank]])
                S.op("dve", lambda ci=ci, bank0=bank0: nc.vector.tensor_copy(
                    out=GB[:, ci, :], in_=PS[:, bank0:bank0 + 2, :].rearrange("p b n -> p (b n)")),
                    reads=[RPS[bank0], RPS[bank0 + 1]], writes=[Rgb[ci]])

        ones32 = sb("ones32", [128, 128])
        Rones = Reg("ones")
        S.op("dve", lambda: nc.vector.memset(ones32[:], 1.0), writes=[Rones])

        cur = {"XN": XN, "TMP": TMP, "HT": None, "BR": None}

        def prenorm_p1(t):
            XN = cur["XN"]
            smi = rot["sm"] % 4
            rot["sm"] += 1
            xi = rot["xn"] % 2
            rot["xn"] += 1
            sm = small[:, smi, :]
            S.op("act", lambda: nc.scalar.activation(out=junk[:], in_=X[:, t, :], func=AF.Square, accum_out=sm[:, 0:1]),
                 reads=[RX[t]], writes=[Rjunk, Rsmall[smi]])
            S.op("act", lambda: nc.scalar.activation(out=sm[:, 1:2], in_=sm[:, 0:1], func=AF.Sqrt, scale=1.0 / D, bias=epsb[:, 0:1]),
                 reads=[Rsmall[smi], Reps], writes=[Rsmall[smi]])
            S.op("dve", lambda: nc.vector.reciprocal(out=sm[:, 2:3], in_=sm[:, 1:2]), reads=[Rsmall[smi]], writes=[Rsmall[smi]])
            S.op("dve", lambda: nc.vector.tensor_scalar(out=XN[:, xi, :], in0=X[:, t, :], scalar1=sm[:, 2:3], scalar2=None, op0=ALU.mult),
                 reads=[RX[t], Rsmall[smi]], writes=[Rxn[xi]])
            return xi

        def prenorm_p2(xi, ci, s, dst, dst_reg, banks):
            XN = cur["XN"]
            b0, b1 = banks

            def emit():
                inst = None
                for c in range(8):
                    bk = b0 if c < 4 else b1
                    inst = nc.tensor.transpose(out=PS[:, bk, (c % 4) * 128:(c % 4 + 1) * 128], in_=XN[:, xi, c * 128:(c + 1) * 128], identity=ident[:])
                return inst
            S.op("pe", emit, reads=[Rxn[xi], Rid], writes=[RPS[b0], RPS[b1]])
            for c in range(8):
                bk = b0 if c < 4 else b1
                S.op("act", lambda c=c, bk=bk: nc.scalar.activation(
                    out=dst[:, c, :], in_=PS[:, bk, (c % 4) * 128:(c % 4 + 1) * 128], func=AF.Identity,
                    scale=SV[:, ci, s, c:c + 1], bias=VEC[:, ci, 3 * s * 8 + c:3 * s * 8 + c + 1]),
                    reads=[RPS[bk], Rsv, Rvec], writes=[dst_reg])

        def prenorm_tile(t, ci, s, dst, dst_reg, banks):
            prenorm_p2(prenorm_p1(t), ci, s, dst, dst_reg, banks)

        epsb = sb("epsb", [128, 1])
        halfpi = sb("halfpi", [128, 1])
        Reps = Reg("eps")
        S.op("dve", lambda: nc.vector.memset(epsb[:], EPS), writes=[Reps])
        S.op("dve", lambda: nc.vector.memset(halfpi[:], float(np.pi / 2)), writes=[Reps])

        Rtmp = [Reg("TMP0"), Reg("TMP1")]

        def postnorm_tile(t, ci, b0):
            TMP = cur["TMP"]
            smi = rot["sm"] % 4
            rot["sm"] += 1
            ti = rot["xn"] % 2
            rot["xn"] += 1
            sm = small[:, smi, :]
            fin = PS[:, b0:b0 + 2, :].rearrange("p b n -> p (b n)")
            S.op("act", lambda: nc.scalar.activation(out=junk[:], in_=fin, func=AF.Square, accum_out=sm[:, 0:1]),
                 reads=[RPS[b0], RPS[b0 + 1]], writes=[Rjunk, Rsmall[smi]])
            S.op("act", lambda: nc.scalar.activation(out=sm[:, 1:2], in_=sm[:, 0:1], func=AF.Sqrt, scale=1.0 / D, bias=epsb[:, 0:1]),
                 reads=[Rsmall[smi], Reps], writes=[Rsmall[smi]])
            S.op("dve", lambda: nc.vector.reciprocal(out=sm[:, 2:3], in_=sm[:, 1:2]), reads=[Rsmall[smi]], writes=[Rsmall[smi]])
            S.op("dve", lambda: nc.vector.scalar_tensor_tensor(out=TMP[:, ti, :], in0=fin, scalar=sm[:, 2:3], in1=GB[:, ci, :],
                                                               op0=ALU.mult, op1=ALU.mult),
                 reads=[RPS[b0], RPS[b0 + 1], Rsmall[smi], Rgb[ci]], writes=[Rtmp[ti]])
            S.op("dve", lambda: nc.vector.tensor_tensor(out=X[:, t, :], in0=X[:, t, :], in1=TMP[:, ti, :], op=ALU.add),
                 reads=[RX[t], Rtmp[ti]], writes=[RX[t]])

        Rhtb = [Reg("HTB0"), Reg("HTB1")]
        Rgt = [Reg("GT%d" % j) for j in range(NFF)]
        Rw13 = [Reg("W13_%d" % i) for i in range(3)]
        Rw2 = [Reg("W2_%d" % i) for i in range(3)]
        Rsil = [Reg("SIL0"), Reg("SIL1")]
        cnts = {"w13": 0, "w2": 0, "htb": 0, "sil": 0, "pa": 0}

        def ffn(l, f, s):
            make_gate_bcast(s)
            w1v = ffn_w1[l, f].rearrange("(kc p) n -> p kc n", p=128)
            w3v = ffn_w3[l, f].rearrange("(kc p) n -> p kc n", p=128)
            w2v = ffn_w2[l, f].rearrange("(j p) n -> p j n", p=128)
            hb0 = cnts["htb"]
            cnts["htb"] += 5

            def prenorm_block(blk_):
                hb_ = (hb0 + blk_) % 2
                ci_ = 0 if blk_ < 4 else 1
                for tt in range(4):
                    prenorm_tile(blk_ * 4 + tt, ci_, s, HTB[:, hb_, :, tt * 128:(tt + 1) * 128], Rhtb[hb_], (4 + 2 * (tt % 2), 5 + 2 * (tt % 2)))
            prenorm_block(0)
            pend = [None] * 4
            for blk in range(5):
                ci = 0 if blk < 4 else 1
                hb = (hb0 + blk) % 2
                for j2 in range(NFF // 2):
                    if blk + 1 < 5:
                        nb_, hbn, cin = blk + 1, (hb0 + blk + 1) % 2, (0 if blk + 1 < 4 else 1)
                        if 4 <= j2 <= 7:
                            tt_ = j2 - 4
                            prenorm_p2(pend[tt_], cin, s, HTB[:, hbn, :, tt_ * 128:(tt_ + 1) * 128], Rhtb[hbn], (4 + 2 * (tt_ % 2), 5 + 2 * (tt_ % 2)))
                        if 2 <= j2 <= 5:
                            pend[j2 - 2] = prenorm_p1(nb_ * 4 + (j2 - 2))
                    sl = cnts["w13"] % 3
                    cnts["w13"] += 1
                    S.dma("pool", W13[:, sl, 0, :, :], w1v[:, :, j2 * 256:(j2 + 1) * 256], writes=[Rw13[sl]])
                    S.dma("pool", W13[:, sl, 1, :, :], w3v[:, :, j2 * 256:(j2 + 1) * 256], writes=[Rw13[sl]])
                    for jj in range(2):
                        j = 2 * j2 + jj
                        pa = cnts["pa"] % 2
                        cnts["pa"] += 1
                        b1, b3 = 2 * pa, 2 * pa + 1
                        for (m, bk) in ((0, b1), (1, b3)):
                            def emit(m=m, bk=bk, jj=jj, sl=sl):
                                inst = None
                                for kc in range(8):
                                    inst = nc.tensor.matmul(PS[:, bk, :], lhsT=W13[:, sl, m, kc, jj * 128:(jj + 1) * 128],
                                                            rhs=HTB[:, hb, kc, :], start=(kc == 0), stop=(kc == 7))
                                return inst
                            S.op("pe", emit, reads=[Rw13[sl], Rhtb[hb]], writes=[RPS[bk]])
                        si = cnts["sil"] % 2
                        cnts["sil"] += 1
                        S.op("act", lambda b1=b1, si=si: nc.scalar.activation(out=SIL[:, si, :], in_=PS[:, b1, :], func=AF.Silu),
                             reads=[RPS[b1]], writes=[Rsil[si]])
                        S.op("dve", lambda b3=b3, si=si, j=j: nc.vector.tensor_tensor(out=GT[:, j, :], in0=PS[:, b3, :], in1=SIL[:, si, :], op=ALU.mult),
                             reads=[RPS[b3], Rsil[si]], writes=[Rgt[j]])
                for j2 in range(NFF // 2):
                    sl = cnts["w2"] % 3
                    cnts["w2"] += 1
                    S.dma("pool", W2[:, sl, :, :], w2v[:, 2 * j2:2 * j2 + 2, :], writes=[Rw2[sl]])
                    for jj in range(2):
                        j = 2 * j2 + jj

                        def emit(j=j, jj=jj, sl=sl):
                            inst = None
                            for tt in range(4):
                                for half in range(2):
                                    inst = nc.tensor.matmul(PS[:, 2 * tt + half, :], lhsT=GT[:, j, tt * 128:(tt + 1) * 128],
                                                            rhs=W2[:, sl, jj, half * 512:(half + 1) * 512], start=(j == 0), stop=(j == NFF - 1))
                            return inst
                        S.op("pe", emit, reads=[Rgt[j], Rw2[sl]], writes=RPS)
                for tt in range(4):
                    postnorm_tile(blk * 4 + tt, ci, 2 * tt)

        MOFF = 0
        HT, MOFF = view(MOFF, [128, 8, 2048], BF16)
        BR, MOFF = view(MOFF, [128, 8, 2048], BF16)
        WIN0 = MOFF
        WIN, MOFF = view(MOFF, [128, 2, 8, 256], BF16)
        BS0 = MOFF
        Rht = Reg("HT")
        Rbr = [Reg("BR%d" % i) for i in range(8)]
        Rwin = [Reg("WIN0"), Reg("WIN1")]
        PV = sb("PV", [128, 64])
        Rpv = Reg("PV")
        BG = sb("BG", [128, 32])
        Rbg = Reg("BG")
        mc = {"win": 0, "pb": 0, "wg": 0, "wo": 0, "sg": 0}
        SEQS = [(0, 16, 0, "sample"), (16, 2, 1, "pA"), (18, 2, 1, "pB")]

        def mixer_params(l):
            load_T(PV[:, 0:6], conv_w[l].rearrange("j (c p) -> (j c) p", p=128), 6, Rpv)
            load_T(PV[:, 6:8], conv_b[l].rearrange("(c p) -> c p", p=128), 2, Rpv)
            load_T(PV[:, 8:10], ret_gn[l].rearrange("(c p) -> c p", p=128), 2, Rpv)
            load_T(PV[:, 10:12], s5_d[l].rearrange("(c p) -> c p", p=128), 2, Rpv)
            load_T(PV[:, 12:13], mla_kv_norm[l].rearrange("(c p) -> c p", p=128), 1, Rpv)
            load_T(BG[:, :], b_gate[l].rearrange("(c p) -> c p", p=128), 32, Rbg)

        def proj_fm(l, col0, ncols, NB, BW, evac, extra_reads=()):
            winv = w_in[l].rearrange("(kc p) n -> p kc n", p=128)
            sl = mc["win"] % 2
            mc["win"] += 1
            S.dma("pool", WIN[:, sl, :, 0:ncols], winv[:, :, col0:col0 + ncols], writes=[Rwin[sl]])
            for b in range(NB):
                bank = mc["pb"] % 4
                mc["pb"] += 1

                def emit(b=b, bank=bank):
                    inst = None
                    for kc in range(8):
                        inst = nc.tensor.matmul(PS[0:ncols, bank, 0:BW], lhsT=WIN[:, sl, kc, 0:ncols], rhs=cur["HT"][:, kc, b * BW:(b + 1) * BW],
                                                start=(kc == 0), stop=(kc == 7))
                    return inst
                S.op("pe", emit, reads=[Rwin[sl], Rht], writes=[RPS[bank]])
                evac(b, bank)

        def branch_conv(l, T, NB, BW):
            o = BS0
            Z, o = view(o, [128, 2056])
            CX, o = view(o, [128, 2048])
            Y, o = view(o, [128, 2048])
            CB, o = view(o, [128, 2048], BF16)
            Rz, Rcx, Ry, Rcb = Reg("Z"), Reg("CX"), Reg("Y"), Reg("CB")
            for cc in range(2):
                S.op("dve", lambda: nc.vector.memset(Z[:, 0:1], 0.0), writes=[Rz])
                S.op("dve", lambda: nc.vector.memset(Z[:, T + 1:T + 2], 0.0), writes=[Rz])
                proj_fm(l, 1280 + cc * 128, 128, NB, BW, lambda b, bank: S.op(
                    "act", lambda: nc.scalar.copy(out=CX[:, b * BW:(b + 1) * BW], in_=PS[:, bank, 0:BW]), reads=[RPS[bank]], writes=[Rcx]))
                proj_fm(l, 1792 + cc * 128, 128, NB, BW, lambda b, bank: S.op(
                    "dve", lambda: nc.vector.tensor_tensor(out=Z[:, 1 + b * BW:1 + (b + 1) * BW], in0=PS[:, bank, 0:BW], in1=CX[:, b * BW:(b + 1) * BW], op=ALU.mult),
                    reads=[RPS[bank], Rcx], writes=[Rz]))
                proj_fm(l, 1536 + cc * 128, 128, NB, BW, lambda b, bank: S.op(
                    "act", lambda: nc.scalar.copy(out=CB[:, b * BW:(b + 1) * BW], in_=PS[:, bank, 0:BW]), reads=[RPS[bank]], writes=[Rcb]))
                S.op("dve", lambda: nc.vector.tensor_scalar(out=Y[:, 0:T], in0=Z[:, 1:T + 1], scalar1=PV[:, 2 + cc:3 + cc], scalar2=PV[:, 6 + cc:7 + cc],
                                                            op0=ALU.mult, op1=ALU.add), reads=[Rz, Rpv], writes=[Ry])
                S.op("dve", lambda: nc.vector.scalar_tensor_tensor(out=Y[:, 0:T], in0=Z[:, 0:T], scalar=PV[:, 0 + cc:1 + cc], in1=Y[:, 0:T],
                                                                   op0=ALU.mult, op1=ALU.add), reads=[Rz, Rpv, Ry], writes=[Ry])
                S.op("dve", lambda: nc.vector.scalar_tensor_tensor(out=Y[:, 0:T], in0=Z[:, 2:T + 2], scalar=PV[:, 4 + cc:5 + cc], in1=Y[:, 0:T],
                                                                   op0=ALU.mult, op1=ALU.add), reads=[Rz, Rpv, Ry], writes=[Ry])
                S.op("dve", lambda: nc.vector.tensor_tensor(out=cur["BR"][:, 4 + cc, 0:T], in0=Y[:, 0:T], in1=CB[:, 0:T], op=ALU.mult),
                     reads=[Ry, Rcb], writes=[Rbr[4 + cc]])

        def gate_stage(l, t0, ntile, ci, T, NB, BW):
            o = WIN0
            WG, o = view(o, [128, 2, 8, 4, 128], BF16)
            WB, o = view(o, [128, 2, 2, 4, 128], BF16)
            MG, o = view(o, [128, 8, 512], BF16)
            SG, o = view(o, [128, 4, 512], BF16)
            WO, o = view(o, [128, 2, D], BF16)
            ACC, o = view(o, [128, 2, 512])
            cur["TMP"], o = view(o, [128, 2, D])
            Rwg = [Reg("WG0"), Reg("WG1")]
            Rmg = [Reg("MG%d" % c) for c in range(8)]
            Rsg = [Reg("SG%d" % n) for n in range(4)]
            Rwo = [Reg("WO0"), Reg("WO1")]
            Racc = [Reg("ACC0"), Reg("ACC1")]
            wgv = w_gate[l].rearrange("(kc p) (n d) -> p kc n d", p=128, n=4)
            wbv = w_branch[l].rearrange("n (kc p) d -> p kc n d", p=128)
            tpb = BW // 128
            for b in range(NB):
                for c in range(8):
                    sl = mc["wg"] % 2
                    mc["wg"] += 1
                    for n in range(4):
                        S.dma("pool", WG[:, sl, :, n, :], wgv[:, :, n, c * 128:(c + 1) * 128], writes=[Rwg[sl]])
                    for n in range(4):
                        S.dma("pool", WB[:, sl, :, n, :], wbv[:, :, n, c * 128:(c + 1) * 128], writes=[Rwg[sl]])
                    for n in range(4):
                        def emit_g(n=n, sl=sl):
                            inst = None
                            for kc in range(8):
                                inst = nc.tensor.matmul(PS[:, n, 0:BW], lhsT=WG[:, sl, kc, n, :], rhs=cur["HT"][:, kc, b * BW:(b + 1) * BW],
                                                        start=(kc == 0), stop=(kc == 7))
                            return inst
                        S.op("pe", emit_g, reads=[Rwg[sl], Rht], writes=[RPS[n]])

                        def emit_p(n=n, sl=sl):
                            inst = None
                            for kc in range(2):
                                inst = nc.tensor.matmul(PS[:, 4 + n, 0:BW], lhsT=WB[:, sl, kc, n, :], rhs=cur["BR"][:, 2 * n + kc, b * BW:(b + 1) * BW],
                                                        start=(kc == 0), stop=(kc == 1))
                            return inst
                        S.op("pe", emit_p, reads=[Rwg[sl], Rbr[2 * n], Rbr[2 * n + 1]], writes=[RPS[4 + n]])
                        S.op("act", lambda n=n: nc.scalar.activation(out=SG[:, n, 0:BW], in_=PS[:, n, 0:BW], func=AF.Sigmoid,
                                                                     bias=BG[:, n * 8 + c:n * 8 + c + 1]), reads=[RPS[n], Rbg], writes=[Rsg[n]])
                    S.op("dve", lambda: nc.vector.tensor_tensor(out=ACC[:, 0, 0:BW], in0=PS[:, 4, 0:BW], in1=SG[:, 0, 0:BW], op=ALU.mult),
                         reads=[RPS[4], Rsg[0]], writes=[Racc[0]])
                    for n in range(1, 4):
                        S.op("dve", lambda n=n: nc.vector.tensor_tensor(out=ACC[:, 1, 0:BW], in0=PS[:, 4 + n, 0:BW], in1=SG[:, n, 0:BW], op=ALU.mult),
                             reads=[RPS[4 + n], Rsg[n]], writes=[Racc[1]])
                        if n < 3:
                            S.op("dve", lambda: nc.vector.tensor_tensor(out=ACC[:, 0, 0:BW], in0=ACC[:, 0, 0:BW], in1=ACC[:, 1, 0:BW], op=ALU.add),
                                 reads=[Racc[0], Racc[1]], writes=[Racc[0]])
                        else:
                            S.op("dve", lambda: nc.vector.tensor_tensor(out=MG[:, c, 0:BW], in0=ACC[:, 0, 0:BW], in1=ACC[:, 1, 0:BW], op=ALU.add),
                                 reads=[Racc[0], Racc[1]], writes=[Rmg[c]])
                for c in range(8):
                    sl = mc["wo"] % 2
                    mc["wo"] += 1
                    S.dma("pool", WO[:, sl, :], w_o[l, c * 128:(c + 1) * 128, :], writes=[Rwo[sl]])

                    def emit_o(c=c, sl=sl):
                        inst = None
                        for tt in range(tpb):
                            for half in range(2):
                                inst = nc.tensor.matmul(PS[:, 2 * tt + half, :], lhsT=MG[:, c, tt * 128:(tt + 1) * 128],
                                                        rhs=WO[:, sl, half * 512:(half + 1) * 512], start=(c == 0), stop=(c == 7))
                        return inst
                    S.op("pe", emit_o, reads=[Rmg[c], Rwo[sl]], writes=RPS[0:2 * tpb])
                for tt in range(tpb):
                    postnorm_tile(t0 + b * tpb + tt, ci, 2 * tt)

        onesb = sb("onesb", [128, 128], BF16)
        S.op("dve", lambda: nc.vector.memset(onesb[:], 1.0), writes=[Rones])
        ATT_SCALE = float(96 ** -0.5)

        def branch_mla(l, t0, T, NB, BW, kind):
            sample = kind == "sample"
            Skeys = T + (512 if sample else 0)
            NKT = Skeys // 128
            o = BS0
            CQ, o = view(o, [128, 2, 2048], BF16)
            CKVN, o = view(o, [128, 2560], BF16)
            KR, o = view(o, [128, 2560], BF16)
            WUQ, o = view(o, [128, 2, 384], BF16)
            WUQS, o = view(o, [128, 2, 4, 32], BF16)
            WUKV, o = view(o, [128, 512], BF16)
            WKRS, o = view(o, [128, 8, 32], BF16)
            QNV, o = view(o, [128, 2])
            oB = o
            Rcq, Rckvn, Rkr, Rw = Reg("m_CQ"), Reg("m_CKVN"), Reg("m_KR"), Reg("m_W")
            W32, oo = view(oB, [128, 2, 384])
            Rw32 = Reg("m_W32")
            S.dma("sp", W32[:, 0, :], mla_w_uq[l, 0:128, :], writes=[Rw32])
            S.dma("sp", W32[0:64, 1, :], mla_w_uq[l, 128:192, :], writes=[Rw32])
            S.dma("sp", QNV[:, 0:1], mla_q_norm[l, 0:128].unsqueeze(1), writes=[Rw])
            S.dma("sp", QNV[0:64, 1:2], mla_q_norm[l, 128:192].unsqueeze(1), writes=[Rw])
            S.dma("pool", WUKV[:, :], mla_w_ukv[l, :, :], writes=[Rw])
            for kc, np_ in ((0, 128), (1, 64)):
                S.op("dve", lambda kc=kc, np_=np_: nc.vector.tensor_scalar(out=WUQ[0:np_, kc, :], in0=W32[0:np_, kc, :], scalar1=QNV[0:np_, kc:kc + 1],
                                                                          scalar2=None, op0=ALU.mult), reads=[Rw32, Rw], writes=[Rw])
                if sample:
                    wv = WUQ[0:np_, kc, :].rearrange("p (h e) -> p h e", h=4)
                    S.op("dve", lambda wv=wv, kc=kc, np_=np_: nc.vector.tensor_scalar(out=WUQS[0:np_, kc, :, 0:16], in0=wv[:, :, 80:96], scalar1=-1.0,
                                                                                   scalar2=None, op0=ALU.mult), reads=[Rw], writes=[Rw])
                    S.op("dve", lambda wv=wv, kc=kc, np_=np_: nc.vector.tensor_copy(out=WUQS[0:np_, kc, :, 16:32], in_=wv[:, :, 64:80]), reads=[Rw], writes=[Rw])
            if sample:
                winv = w_in[l].rearrange("(kc p) n -> p kc n", p=128)
                S.dma("pool", WKRS[:, :, 0:16], winv[:, :, 2384:2400], writes=[Rw])
                S.dma("pool", WKRS[:, :, 16:32], winv[:, :, 2368:2384], writes=[Rw])
                S.op("dve", lambda: nc.vector.tensor_scalar(out=WKRS[:, :, 0:16], in0=WKRS[:, :, 0:16], scalar1=-1.0, scalar2=None, op0=ALU.mult),
                     reads=[Rw], writes=[Rw])
            SQ, oo = view(oo, [128, 2, 512], BF16)
            RST, oo = view(oo, [128, 512])
            TB, oo = view(oo, [128, 2, 512])
            T1, oo = view(oo, [128, 2, 512])
            Rsq, Rrst, Rtb, Rt1 = Reg("m_SQ"), Reg("m_RST"), Reg("m_TB"), Reg("m_T1")

            def rstd_from_ps(bank, parts):
                S.op("act", lambda: nc.scalar.activation(out=RST[:, 0:BW], in_=PS[:, bank, 0:BW], func=AF.Sqrt, scale=1.0 / parts, bias=epsb[:, 0:1]),
                     reads=[RPS[bank], Reps], writes=[Rrst])
                S.op("dve", lambda: nc.vector.reciprocal(out=RST[:, 0:BW], in_=RST[:, 0:BW]), reads=[Rrst], writes=[Rrst])

            winv = w_in[l].rearrange("(kc p) n -> p kc n", p=128)
            WQ = WIN
            S.dma("pool", WQ[:, 0, :, 0:192], winv[:, :, 2048:2240], writes=[Rwin[0]])
            S.dma("pool", WQ[:, 1, :, 0:160], winv[:, :, 2240:2400], writes=[Rwin[1]])
            for b in range(NB):
                cols = slice(b * BW, (b + 1) * BW)
                for kc2, np_, bank in ((0, 128, 0), (1, 64, 1)):
                    def emit(kc2=kc2, np_=np_, bank=bank):
                        inst = None
                        for kc in range(8):
                            inst = nc.tensor.matmul(PS[0:np_, bank, 0:BW], lhsT=WQ[:, 0, kc, kc2 * 128:kc2 * 128 + np_], rhs=cur["HT"][:, kc, cols],
                                                    start=(kc == 0), stop=(kc == 7))
                        return inst
                    S.op("pe", emit, reads=[Rwin[0], Rht], writes=[RPS[bank]])
                    S.op("act", lambda kc2=kc2, np_=np_, bank=bank: nc.scalar.activation(out=SQ[0:np_, kc2, 0:BW], in_=PS[0:np_, bank, 0:BW], func=AF.Square),
                         reads=[RPS[bank]], writes=[Rsq])

                def emit_ss():
                    nc.tensor.matmul(PS[:, 2, 0:BW], lhsT=onesb[:, :], rhs=SQ[:, 0, 0:BW], start=True, stop=False)
                    return nc.tensor.matmul(PS[:, 2, 0:BW], lhsT=onesb[0:64, :], rhs=SQ[0:64, 1, 0:BW], start=False, stop=True)
                S.op("pe", emit_ss, reads=[Rsq, Rones], writes=[RPS[2]])
                rstd_from_ps(2, 192.0)
                for kc2, np_, bank in ((0, 128, 0), (1, 64, 1)):
                    S.op("dve", lambda kc2=kc2, np_=np_, bank=bank: nc.vector.tensor_tensor(out=CQ[0:np_, kc2, cols], in0=PS[0:np_, bank, 0:BW], in1=RST[0:np_, 0:BW], op=ALU.mult),
                         reads=[RPS[bank], Rrst], writes=[Rcq])
                def emit_kv():
                    inst = None
                    for kc in range(8):
                        inst = nc.tensor.matmul(PS[:, 3, 0:BW], lhsT=WQ[:, 1, kc, 0:128], rhs=cur["HT"][:, kc, cols], start=(kc == 0), stop=(kc == 7))
                    return inst
                S.op("pe", emit_kv, reads=[Rwin[1], Rht], writes=[RPS[3]])
                S.op("act", lambda: nc.scalar.activation(out=SQ[:, 0, 0:BW], in_=PS[:, 3, 0:BW], func=AF.Square), reads=[RPS[3]], writes=[Rsq])
                S.op("pe", lambda: nc.tensor.matmul(PS[:, 2, 0:BW], lhsT=onesb[:, :], rhs=SQ[:, 0, 0:BW], start=True, stop=True), reads=[Rsq, Rones], writes=[RPS[2]])
                rstd_from_ps(2, 128.0)
                S.op("dve", lambda: nc.vector.scalar_tensor_tensor(out=CKVN[:, cols], in0=PS[:, 3, 0:BW], scalar=PV[:, 12:13], in1=RST[:, 0:BW], op0=ALU.mult, op1=ALU.mult),
                     reads=[RPS[3], Rpv, Rrst], writes=[Rckvn])
                def emit_kr():
                    inst = None
                    for kc in range(8):
                        inst = nc.tensor.matmul(PS[0:32, 4, 0:BW], lhsT=WQ[:, 1, kc, 128:160], rhs=cur["HT"][:, kc, cols], start=(kc == 0), stop=(kc == 7))
                    return inst
                S.op("pe", emit_kr, reads=[Rwin[1], Rht], writes=[RPS[4]])
                if sample:
                    def emit_krs():
                        inst = None
                        for kc in range(8):
                            inst = nc.tensor.matmul(PS[0:32, 5, 0:BW], lhsT=WKRS[:, kc, :], rhs=cur["HT"][:, kc, cols], start=(kc == 0), stop=(kc == 7))
                        return inst
                    S.op("pe", emit_krs, reads=[Rw, Rht], writes=[RPS[5]])
                    S.dma("sp", TB[0:32, 0, 0:BW], c_rope_mla[0, :, cols], writes=[Rtb])
                    S.dma("sp", TB[0:32, 1, 0:BW], c_rope_mla[1, :, cols], writes=[Rtb])
                    S.op("dve", lambda: nc.vector.tensor_tensor(out=T1[0:32, 0, 0:BW], in0=PS[0:32, 4, 0:BW], in1=TB[0:32, 0, 0:BW], op=ALU.mult),
                         reads=[RPS[4], Rtb], writes=[Rt1])
                    S.op("dve", lambda: nc.vector.tensor_tensor(out=T1[0:32, 1, 0:BW], in0=PS[0:32, 5, 0:BW], in1=TB[0:32, 1, 0:BW], op=ALU.mult),
                         reads=[RPS[5], Rtb], writes=[Rt1])
                    S.op("dve", lambda: nc.vector.tensor_tensor(out=KR[0:32, cols], in0=T1[0:32, 0, 0:BW], in1=T1[0:32, 1, 0:BW], op=ALU.add),
                         reads=[Rt1], writes=[Rkr])
                else:
                    S.op("act", lambda: nc.scalar.copy(out=KR[0:32, cols], in_=PS[0:32, 4, 0:BW]), reads=[RPS[4]], writes=[Rkr])
            if sample:
                CT, _ = view(oB + 3072, [128, 4, 160])
                Rct = Reg("m_CT")
                S.barrier()
                S.dma("sp", CT[:, :, :], ctx_mla[l].rearrange("(i p) f -> p i f", p=128), writes=[Rct])
                for i in range(4):
                    S.op("pe", lambda i=i: nc.tensor.transpose(out=PS[:, 6, 0:128], in_=CT[:, i, 0:128], identity=ident[:]), reads=[Rct, Rid], writes=[RPS[6]])
                    S.op("act", lambda i=i: nc.scalar.copy(out=CKVN[:, T + i * 128:T + (i + 1) * 128], in_=PS[:, 6, 0:128]), reads=[RPS[6]], writes=[Rckvn])
                    S.op("pe", lambda i=i: nc.tensor.transpose(out=PS[0:32, 7, 0:128], in_=CT[:, i, 128:160], identity=ident[:]), reads=[Rct, Rid], writes=[RPS[7]])
                    S.op("act", lambda i=i: nc.scalar.copy(out=KR[0:32, T + i * 128:T + (i + 1) * 128], in_=PS[0:32, 7, 0:128]), reads=[RPS[7]], writes=[Rkr])
            S.barrier()
            o = oB
            KN, o = view(o, [128, 2560], BF16)
            QN, o = view(o, [128, 2048], BF16)
            QR, o = view(o, [128, 2048], BF16)
            VA, o = view(o, [128, 20, 66], BF16)
            Rkn, Rqn, Rqr, Rva = Reg("m_KN"), Reg("m_QN"), Reg("m_QR"), Reg("m_VA")
            ow = WIN0
            TB2, ow2 = view(ow, [128, 2, 512])
            T2, ow2 = view(ow2, [128, 2, 512])
            PT, ow3 = view(ow, [128, 2, 512], BF16)
            OS, ow3 = view(ow3, [128, 512])
            OT, ow3 = view(ow3, [128, 512], BF16)
            Rtb2, Rt2, Rpt, Ros, Rot = Reg("m_TB2"), Reg("m_T2"), [Reg("m_PT0"), Reg("m_PT1")], Reg("m_OS"), Reg("m_OT")
            S.op("dve", lambda: nc.vector.memset(VA[:, :, 64:66], 1.0), writes=[Rva])
            S.dma("sp", KN[64:96, 0:Skeys], KR[0:32, 0:Skeys], reads=[Rkr], writes=[Rkn], key=Reg("m_KRd"))
            KBW = 512
            for h in range(4):
                for kb in range((Skeys + KBW - 1) // KBW):
                    w = min(KBW, Skeys - kb * KBW)
                    bank = mc["pb"] % 4
                    mc["pb"] += 1
                    S.op("pe", lambda kb=kb, w=w, bank=bank: nc.tensor.matmul(PS[0:64, bank, 0:w], lhsT=WUKV[:, h * 128:h * 128 + 64], rhs=CKVN[:, kb * KBW:kb * KBW + w],
                                                                              start=True, stop=True), reads=[Rw, Rckvn], writes=[RPS[bank]])
                    S.op("act", lambda kb=kb, w=w, bank=bank: nc.scalar.copy(out=KN[0:64, kb * KBW:kb * KBW + w], in_=PS[0:64, bank, 0:w]), reads=[RPS[bank]], writes=[Rkn])
                for kt in range(NKT):
                    bank = mc["pb"] % 4
                    mc["pb"] += 1
                    S.op("pe", lambda kt=kt, bank=bank: nc.tensor.matmul(PS[:, bank, 0:64], lhsT=CKVN[:, kt * 128:(kt + 1) * 128], rhs=WUKV[:, h * 128 + 64:h * 128 + 128],
                                                                          start=True, stop=True), reads=[Rw, Rckvn], writes=[RPS[bank]])
                    S.op("dve", lambda kt=kt, bank=bank: nc.vector.tensor_copy(out=VA[:, kt, 0:64], in_=PS[:, bank, 0:64]), reads=[RPS[bank]], writes=[Rva])
                for b in range(NB):
                    cols = slice(b * BW, (b + 1) * BW)
                    bank = mc["pb"] % 4
                    mc["pb"] += 1

                    def emit_q(c0, m, bank, wt=None, p0=0):
                        def f():
                            po = PS[p0:p0 + m, bank, 0:BW]
                            if wt is None:
                                nc.tensor.matmul(po, lhsT=WUQ[:, 0, c0:c0 + m], rhs=CQ[:, 0, cols], start=True, stop=False)
                                return nc.tensor.matmul(po, lhsT=WUQ[0:64, 1, c0:c0 + m], rhs=CQ[0:64, 1, cols], start=False, stop=True)
                            nc.tensor.matmul(po, lhsT=WUQS[:, 0, h, :], rhs=CQ[:, 0, cols], start=True, stop=False)
                            return nc.tensor.matmul(po, lhsT=WUQS[0:64, 1, h, :], rhs=CQ[0:64, 1, cols], start=False, stop=True)
                        return f
                    S.op("pe", emit_q(h * 96, 64, bank), reads=[Rw, Rcq], writes=[RPS[bank]])
                    S.op("act", lambda bank=bank: nc.scalar.copy(out=QN[0:64, cols], in_=PS[0:64, bank, 0:BW]), reads=[RPS[bank]], writes=[Rqn])
                    bank2 = mc["pb"] % 4
                    mc["pb"] += 1
                    S.op("pe", emit_q(h * 96 + 64, 32, bank2, p0=64), reads=[Rw, Rcq], writes=[RPS[bank2]])
                    if sample:
                        bank3 = mc["pb"] % 4
                        mc["pb"] += 1
                        S.op("pe", emit_q(0, 32, bank3, wt=1, p0=64), reads=[Rw, Rcq], writes=[RPS[bank3]])
                        S.dma("sp", TB2[64:96, 0, 0:BW], c_rope_mla[0, :, cols], writes=[Rtb2])
                        S.dma("sp", TB2[64:96, 1, 0:BW], c_rope_mla[1, :, cols], writes=[Rtb2])
                        S.op("dve", lambda: nc.vector.tensor_tensor(out=T2[64:96, 0, 0:BW], in0=PS[64:96, bank2, 0:BW], in1=TB2[64:96, 0, 0:BW], op=ALU.mult),
                             reads=[RPS[bank2], Rtb2], writes=[Rt2])
                        S.op("dve", lambda: nc.vector.tensor_tensor(out=T2[64:96, 1, 0:BW], in0=PS[64:96, bank3, 0:BW], in1=TB2[64:96, 1, 0:BW], op=ALU.mult),
                             reads=[RPS[bank3], Rtb2], writes=[Rt2])
                        S.op("dve", lambda: nc.vector.tensor_tensor(out=QN[64:96, cols], in0=T2[64:96, 0, 0:BW], in1=T2[64:96, 1, 0:BW], op=ALU.add),
                             reads=[Rt2], writes=[Rqn])
                    else:
                        S.op("act", lambda: nc.scalar.copy(out=QN[64:96, cols], in_=PS[64:96, bank2, 0:BW]), reads=[RPS[bank2]], writes=[Rqn])
                S.barrier()
                for b in range(NB):
                    cols = slice(b * BW, (b + 1) * BW)
                    ob = 4 + (b % 2)
                    for kt in range(NKT):
                        bank = mc["pb"] % 4
                        mc["pb"] += 1
                        pi = kt % 2

                        S.op("pe", lambda kt=kt, bank=bank: nc.tensor.matmul(PS[:, bank, 0:BW], lhsT=KN[0:96, kt * 128:(kt + 1) * 128], rhs=QN[0:96, cols], start=True, stop=True),
                             reads=[Rkn, Rqn], writes=[RPS[bank]])
                        S.op("act", lambda bank=bank, pi=pi: nc.scalar.activation(out=PT[:, pi, 0:BW], in_=PS[:, bank, 0:BW], func=AF.Exp, scale=ATT_SCALE),
                             reads=[RPS[bank]], writes=[Rpt[pi]])
                        S.op("pe", lambda kt=kt, pi=pi: nc.tensor.matmul(PS[0:65, ob, 0:BW], lhsT=VA[:, kt, 0:65], rhs=PT[:, pi, 0:BW], start=(kt == 0), stop=(kt == NKT - 1)),
                             reads=[Rva, Rpt[pi]], writes=[RPS[ob]])
                    S.op("act", lambda: nc.scalar.copy(out=OS[0:65, 0:BW], in_=PS[0:65, ob, 0:BW]), reads=[RPS[ob]], writes=[Ros])
                    S.op("dve", lambda: nc.vector.reciprocal(out=OS[64:65, 0:BW], in_=OS[64:65, 0:BW]), reads=[Ros], writes=[Ros])
                    S.op("pe", lambda: nc.tensor.matmul(PS[0:64, 6, 0:BW], lhsT=ones32[64:65, 0:64], rhs=OS[64:65, 0:BW], start=True, stop=True),
                         reads=[Ros, Rones], writes=[RPS[6]])
                    S.op("dve", lambda: nc.vector.tensor_tensor(out=OT[0:64, 0:BW], in0=PS[0:64, 6, 0:BW], in1=OS[0:64, 0:BW], op=ALU.mult),
                         reads=[RPS[6], Ros], writes=[Rot])
                    S.dma("sp", cur["BR"][(h % 2) * 64:(h % 2) * 64 + 64, 6 + h // 2, cols], OT[0:64, 0:BW], reads=[Rot], writes=[Rbr[6 + h // 2]], key=Reg("m_OTd"))
                S.barrier()

        def mla_cache_out(l, t0, ntile, pi):
            o = BS0
            CA, o = view(o, [128, 2, 160])
            KVB, o = view(o, [128, 128])
            Rca, Rkvb = [Reg("m_CA0"), Reg("m_CA1")], Reg("m_KVB")
            winv = w_in[l].rearrange("(kc p) n -> p kc n", p=128)
            S.dma("pool", WIN[:, 0, :, 0:160], winv[:, :, 2240:2400], writes=[Rwin[0]])
            S.dma("sp", KVB[:, :], mla_kv_norm[l:l + 1, :].broadcast_to([128, 128]), writes=[Rkvb])
            for tt in range(ntile):
                bank = mc["pb"] % 4
                mc["pb"] += 1
                smi = rot["sm"] % 4
                rot["sm"] += 1
                sm = small[:, smi, :]

                def emit(tt=tt, bank=bank):
                    inst = None
                    for kc in range(8):
                        inst = nc.tensor.matmul(PS[:, bank, 0:160], lhsT=cur["HT"][:, kc, tt * 128:(tt + 1) * 128], rhs=WIN[:, 0, kc, 0:160], start=(kc == 0), stop=(kc == 7))
                    return inst
                S.op("pe", emit, reads=[Rwin[0], Rht], writes=[RPS[bank]])
                S.op("act", lambda: nc.scalar.activation(out=junk[:, 0:128], in_=PS[:, bank, 0:128], func=AF.Square, accum_out=sm[:, 0:1]),
                     reads=[RPS[bank]], writes=[Rjunk, Rsmall[smi]])
                S.op("act", lambda: nc.scalar.activation(out=sm[:, 1:2], in_=sm[:, 0:1], func=AF.Sqrt, scale=1.0 / 128, bias=epsb[:, 0:1]),
                     reads=[Rsmall[smi], Reps], writes=[Rsmall[smi]])
                S.op("dve", lambda: nc.vector.reciprocal(out=sm[:, 2:3], in_=sm[:, 1:2]), reads=[Rsmall[smi]], writes=[Rsmall[smi]])
                ci_ = tt % 2
                S.op("dve", lambda: nc.vector.scalar_tensor_tensor(out=CA[:, ci_, 0:128], in0=PS[:, bank, 0:128], scalar=sm[:, 2:3], in1=KVB[:, :], op0=ALU.mult, op1=ALU.mult),
                     reads=[RPS[bank], Rsmall[smi], Rkvb], writes=[Rca[ci_]])
                S.op("dve", lambda: nc.vector.tensor_copy(out=CA[:, ci_, 128:160], in_=PS[:, bank, 128:160]), reads=[RPS[bank]], writes=[Rca[ci_]])
                OUT_EVS.append(S.dma("sp", o_mla[pi, l, tt * 128:(tt + 1) * 128, :], CA[:, ci_, :], reads=[Rca[ci_]], key=Reg("o_mla_d%d" % ci_)))

        def branch_ret(l, t0, T, kind):
            sample = kind == "sample"
            n = T // 128
            o = BS0
            WRb, o = view(o, [128, 8, 512], BF16)
            SBst, o = view(o, [128, 16, 256], BF16)
            DM, o = view(o, [128, 4, 128], BF16)
            QD, o = view(o, [128, 2, 4, 128], BF16)
            CD, o = view(o, [128, 2, 256])
            LG, o = view(o, [128, 8])
            KD, o = view(o, [128, 2, 4])
            SF, o = view(o, [128, 256])
            SB, o = view(o, [128, 256])
            SFb, o = view(o, [128, 256], BF16)
            oT = o
            WRa, _ = view(WIN0, [128, 8, 512], BF16)
            Rwr, Rsbst, Rtab, Rsf, Rsb, Rsfb = Reg("r_WR"), Reg("r_SBst"), Reg("r_TAB"), Reg("r_SF"), Reg("r_SB"), Reg("r_SFb")
            winv = w_in[l].rearrange("(kc p) n -> p kc n", p=128)
            S.dma("pool", WRa[:, :, :], winv[:, :, 256:768], writes=[Rwr])
            S.dma("pool", WRb[:, :, :], winv[:, :, 768:1280], writes=[Rwr])
            CT6, o2 = view(oT, [128, 6, 128])
            E1, o2 = view(o2, [128, 2, 128])
            PIDX, o2 = view(o2, [128, 2])
            C128, o2 = view(o2, [128, 64])
            Rc6, Re1 = Reg("r_C6"), Reg("r_E1")
            S.dma("sp", CT6[:, :, :], c_ret.rearrange("k p i -> p k i"), writes=[Rc6])
            S.dma("sp", PIDX[:, :], c_pidx[:, :], writes=[Rc6])
            S.dma("sp", LG[:, :], ret_decay[l:l + 1].rearrange("o d h -> o (d h)").broadcast_to([128, 8]), writes=[Rtab])
            S.op("dve", lambda: nc.vector.memset(C128[:, :], 128.0), writes=[Rc6])
            S.op("act", lambda: nc.scalar.activation(out=LG[:, :], in_=LG[:, :], func=AF.Sigmoid), reads=[Rtab], writes=[Rtab])
            S.op("act", lambda: nc.scalar.activation(out=LG[:, :], in_=LG[:, :], func=AF.Ln), reads=[Rtab], writes=[Rtab])
            for h in range(4):
                S.op("act", lambda h=h: nc.scalar.activation(out=E1[:, 0, :], in_=CT6[:, 0, :], func=AF.Exp, scale=LG[:, h:h + 1]), reads=[Rc6, Rtab], writes=[Re1])
                S.op("act", lambda h=h: nc.scalar.activation(out=E1[:, 1, :], in_=CT6[:, 1, :], func=AF.Exp, scale=LG[:, 4 + h:5 + h]), reads=[Rc6, Rtab], writes=[Re1])
                S.op("dve", lambda h=h: nc.vector.tensor_tensor(out=E1[:, :, :], in0=E1[:, :, :], in1=CT6[:, 2:4, :], op=ALU.mult), reads=[Re1, Rc6], writes=[Re1])
                S.op("dve", lambda h=h: nc.vector.tensor_tensor(out=DM[:, h, :], in0=E1[:, 0, :], in1=E1[:, 1, :], op=ALU.add), reads=[Re1], writes=[Rtab])
                for d in range(2):
                    S.op("act", lambda h=h, d=d: nc.scalar.activation(out=QD[:, d, h, :], in_=CT6[:, 4 + d, :], func=AF.Exp, scale=LG[:, d * 4 + h:d * 4 + h + 1]),
                         reads=[Rc6, Rtab], writes=[Rtab])
                    S.op("act", lambda h=h, d=d: nc.scalar.activation(out=CD[:, d, h * 64:(h + 1) * 64], in_=C128[:, :], func=AF.Exp, scale=LG[:, d * 4 + h:d * 4 + h + 1]),
                         reads=[Rc6, Rtab], writes=[Rtab])
                    S.op("act", lambda h=h, d=d: nc.scalar.activation(out=KD[:, d, h:h + 1], in_=PIDX[:, d:d + 1], func=AF.Exp, scale=LG[:, d * 4 + h:d * 4 + h + 1]),
                         reads=[Rc6, Rtab], writes=[Rtab])
            S.op("dve", lambda: nc.vector.tensor_scalar(out=KD[:, :, :], in0=KD[:, :, :], scalar1=0.125, scalar2=None, op0=ALU.mult), reads=[Rtab], writes=[Rtab])
            if sample:
                S.dma("sp", SF[0:64, :].rearrange("d (h e) -> d h e", h=4), st_ret[l, 0].rearrange("h d e -> d h e"), writes=[Rsf])
                S.dma("sp", SB[0:64, :].rearrange("d (h e) -> d h e", h=4), st_ret[l, 1].rearrange("h d e -> d h e"), writes=[Rsb])
            else:
                S.op("dve", lambda: nc.vector.memset(SF[0:64, :], 0.0), writes=[Rsf])
                S.op("dve", lambda: nc.vector.memset(SB[0:64, :], 0.0), writes=[Rsb])
            S.barrier()
            if cfg.get("ret_stop") == "tables":
                return
            o3 = oT
            QK, o3 = view(o3, [128, 512], BF16)
            TA, o3 = view(o3, [128, 256])
            TBt, o3 = view(o3, [128, 256])
            RT, o3 = view(o3, [128, 2, 32])
            KDt, o3 = view(o3, [128, 256], BF16)
            VTc, o3 = view(o3, [128, 256], BF16)
            SRG, o3 = view(o3, [128, 256], BF16)
            QT, o3 = view(o3, [128, 3, 512], BF16)
            KT, o3 = view(o3, [128, 512], BF16)
            AM, o3 = view(o3, [128, 512], BF16)
            CEN, o3 = view(o3, [128, 256])
            SQr, o3 = view(o3, [128, 256])
            NRo, o3 = view(o3, [128, 256], BF16)
            MS, o3 = view(o3, [128, 8])
            Rqk, Rta, Rrt, Rkd, Rvt, Rsrg, Rqt, Rkt, Ram, Rcen, Rsq, Rnro, Rms = (Reg("r_" + x) for x in
                ("QK", "TA", "RT", "KDt", "VTc", "SRG", "QT", "KT", "AM", "CEN", "SQ", "NRo", "MS"))
            PSb = lambda bank: PS[:, bank, :].bitcast(BF16)
            S.op("dve", lambda: nc.vector.memset(QT[64:128, :, :], 0.0), writes=[Rqt])
            S.op("dve", lambda: nc.vector.memset(KT[64:128, :], 0.0), writes=[Rkt])
            S.op("dve", lambda: nc.vector.memset(SFb[64:128, :], 0.0), writes=[Rsfb])
            S.op("dve", lambda: nc.vector.memset(SBst[64:128, :, :], 0.0), writes=[Rsbst])

            def proj(c, bank, WRx, c0, ncol):
                def emit():
                    inst = None
                    for kc in range(8):
                        inst = nc.tensor.matmul(PS[:, bank, 0:ncol], lhsT=cur["HT"][:, kc, c * 128:(c + 1) * 128], rhs=WRx[:, kc, c0:c0 + ncol], start=(kc == 0), stop=(kc == 7))
                    return inst
                S.op("pe", emit, reads=[Rwr, Rht], writes=[RPS[bank]])

            def rope(c, bank, col0, ng, dst):
                src = PS[:, bank, col0:col0 + ng * 64].rearrange("p (g t e) -> p g t e", g=ng, t=2)
                dv = dst.rearrange("p (g t e) -> p g t e", g=ng, t=2)
                if not sample:
                    S.op("act", lambda: nc.scalar.copy(out=dst, in_=PS[:, bank, col0:col0 + ng * 64]), reads=[RPS[bank]], writes=[Rqk])
                    return
                S.dma("sp", RT[:, 0, :], c_rope_ret[0, c * 128:(c + 1) * 128, :], writes=[Rrt])
                S.dma("sp", RT[:, 1, :], c_rope_ret[1, c * 128:(c + 1) * 128, :], writes=[Rrt])
                cosb = RT[:, 0, :].unsqueeze(1).broadcast_to([128, ng, 32])
                sinb = RT[:, 1, :].unsqueeze(1).broadcast_to([128, ng, 32])
                ta = TA[:, 0:ng * 32].rearrange("p (g e) -> p g e", g=ng)
                tb = TBt[:, 0:ng * 32].rearrange("p (g e) -> p g e", g=ng)
                S.op("dve", lambda: nc.vector.tensor_tensor(out=ta, in0=src[:, :, 0, :], in1=cosb, op=ALU.mult), reads=[RPS[bank], Rrt], writes=[Rta])
                S.op("dve", lambda: nc.vector.tensor_tensor(out=tb, in0=src[:, :, 1, :], in1=sinb, op=ALU.mult), reads=[RPS[bank], Rrt], writes=[Rta])
                S.op("dve", lambda: nc.vector.tensor_tensor(out=dv[:, :, 0, :], in0=ta, in1=tb, op=ALU.subtract), reads=[Rta], writes=[Rqk])
                S.op("dve", lambda: nc.vector.tensor_tensor(out=ta, in0=src[:, :, 0, :], in1=sinb, op=ALU.mult), reads=[RPS[bank], Rrt], writes=[Rta])
                S.op("dve", lambda: nc.vector.tensor_tensor(out=tb, in0=src[:, :, 1, :], in1=cosb, op=ALU.mult), reads=[RPS[bank], Rrt], writes=[Rta])
                S.op("dve", lambda: nc.vector.tensor_tensor(out=dv[:, :, 1, :], in0=ta, in1=tb, op=ALU.add), reads=[Rta], writes=[Rqk])

            def kdec_mul(d, ksrc):
                S.op("dve", lambda: nc.vector.tensor_tensor(out=KDt[:, :].rearrange("p (h e) -> p h e", h=4), in0=ksrc.rearrange("p (h e) -> p h e", h=4),
                                                            in1=KD[:, d, :].unsqueeze(2).broadcast_to([128, 4, 64]), op=ALU.mult), reads=[Rqk, Rtab], writes=[Rkd])

            def umat(bank):
                def emit():
                    inst = None
                    for h in range(4):
                        inst = nc.tensor.matmul(PS[0:64, bank, h * 64:(h + 1) * 64], lhsT=KDt[:, h * 64:(h + 1) * 64], rhs=VTc[:, h * 64:(h + 1) * 64], start=True, stop=True)
                    return inst
                S.op("pe", emit, reads=[Rkd, Rvt], writes=[RPS[bank]])

            def state_update(St, Rst, d, bank):
                S.op("dve", lambda: nc.vector.tensor_tensor(out=St[0:64, :], in0=St[0:64, :], in1=CD[0:64, d, :], op=ALU.mult), reads=[Rst, Rtab], writes=[Rst])
                S.op("dve", lambda: nc.vector.tensor_tensor(out=St[0:64, :], in0=St[0:64, :], in1=PS[0:64, bank, 0:256], op=ALU.add), reads=[Rst, RPS[bank]], writes=[Rst])

            for c in range(n - 1, -1, -1):
                proj(c, 0, WRa, 256, 256)
                proj(c, 1, WRb, 0, 256)
                rope(c, 0, 0, 4, QK[:, 0:256])
                S.op("act", lambda: nc.scalar.copy(out=VTc[:, :], in_=PS[:, 1, 0:256]), reads=[RPS[1]], writes=[Rvt])
                kdec_mul(1, QK[:, 0:256])
                umat(5)
                S.op("act", lambda c=c: nc.scalar.copy(out=SBst[0:64, c, :], in_=SB[0:64, :]), reads=[Rsb], writes=[Rsbst])
                state_update(SB, Rsb, 1, 5)
            S.op("act", lambda: nc.scalar.copy(out=SFb[0:64, :], in_=SF[0:64, :]), reads=[Rsf], writes=[Rsfb])
            if cfg.get("ret_stop") == "pass1":
                return
            for c in range(n):
                proj(c, 0, WRa, 0, 512)
                proj(c, 1, WRb, 0, 512)
                rope(c, 0, 0, 8, QK[:, :])
                S.op("act", lambda: nc.scalar.copy(out=VTc[:, :], in_=PS[:, 1, 0:256]), reads=[RPS[1]], writes=[Rvt])
                S.op("act", lambda: nc.scalar.activation(out=SRG[:, :], in_=PS[:, 1, 256:512], func=AF.Silu), reads=[RPS[1]], writes=[Rsrg])
                kdec_mul(0, QK[:, 256:512])

                def emit_t():
                    inst = None
                    for g in range(8):
                        inst = nc.tensor.transpose(out=PSb(2)[0:64, g * 128:(g + 1) * 128], in_=QK[:, g * 64:(g + 1) * 64], identity=identb[:])
                    return inst
                S.op("pe", emit_t, reads=[Rqk, Rid], writes=[RPS[2]])
                S.op("dve", lambda: nc.vector.tensor_copy(out=QT[0:64, 0, :], in_=PSb(2)[0:64, 0:512]), reads=[RPS[2]], writes=[Rqt])
                for d in range(2):
                    S.op("dve", lambda d=d: nc.vector.tensor_tensor(out=QT[0:64, 1 + d, :], in0=PSb(2)[0:64, 0:512], in1=QD[0:64, d, :, :].rearrange("p h i -> p (h i)"), op=ALU.mult),
                         reads=[RPS[2], Rtab], writes=[Rqt])
                S.op("dve", lambda: nc.vector.tensor_copy(out=KT[0:64, :], in_=PSb(2)[0:64, 512:1024]), reads=[RPS[2]], writes=[Rkt])
                if cfg.get("ret_stop") == "p2a":
                    continue

                def emit_a():
                    inst = None
                    for h in range(4):
                        inst = nc.tensor.matmul(PS[:, 3, h * 128:(h + 1) * 128], lhsT=KT[:, h * 128:(h + 1) * 128], rhs=QT[:, 0, h * 128:(h + 1) * 128], start=True, stop=True)
                    return inst
                S.op("pe", emit_a, reads=[Rkt, Rqt], writes=[RPS[3]])
                S.op("dve", lambda: nc.vector.tensor_tensor(out=AM[:, :], in0=PS[:, 3, :], in1=DM[:, :, :].rearrange("p h i -> p (h i)"), op=ALU.mult),
                     reads=[RPS[3], Rtab], writes=[Ram])
                if cfg.get("ret_stop") == "p2b":
                    continue

                def emit_o(c=c):
                    inst = None
                    for h in range(4):
                        oc = PS[:, 4, h * 64:(h + 1) * 64]
                        nc.tensor.matmul(oc, lhsT=AM[:, h * 128:(h + 1) * 128], rhs=VTc[:, h * 64:(h + 1) * 64], start=True, stop=False)
                        nc.tensor.matmul(oc, lhsT=QT[:, 1, h * 128:(h + 1) * 128], rhs=SFb[:, h * 64:(h + 1) * 64], start=False, stop=False)
                        inst = nc.tensor.matmul(oc, lhsT=QT[:, 2, h * 128:(h + 1) * 128], rhs=SBst[:, c, h * 64:(h + 1) * 64], start=False, stop=True)
                    return inst
                S.op("pe", emit_o, reads=[Ram, Rvt, Rqt, Rsfb, Rsbst], writes=[RPS[4]])
                umat(5)
                state_update(SF, Rsf, 0, 5)
                S.op("act", lambda: nc.scalar.copy(out=SFb[0:64, :], in_=SF[0:64, :]), reads=[Rsf], writes=[Rsfb])
                if cfg.get("ret_stop") == "p2c":
                    continue
                ov = PS[:, 4, 0:256].rearrange("p (h e) -> p h e", h=4)
                S.op("dve", lambda: nc.vector.tensor_reduce(out=MS[:, 0:4], in_=ov, axis=AX.X, op=ALU.add), reads=[RPS[4]], writes=[Rms])
                S.op("dve", lambda: nc.vector.tensor_scalar(out=MS[:, 0:4], in0=MS[:, 0:4], scalar1=-1.0 / 64, scalar2=None, op0=ALU.mult), reads=[Rms], writes=[Rms])
                cv = CEN[:, :].rearrange("p (h e) -> p h e", h=4)
                S.op("dve", lambda: nc.vector.tensor_tensor(out=cv, in0=ov, in1=MS[:, 0:4].unsqueeze(2).broadcast_to([128, 4, 64]), op=ALU.add),
                     reads=[RPS[4], Rms], writes=[Rcen])
                S.op("dve", lambda: nc.vector.tensor_tensor(out=SQr[:, :], in0=CEN[:, :], in1=CEN[:, :], op=ALU.mult), reads=[Rcen], writes=[Rsq])
                S.op("dve", lambda: nc.vector.tensor_reduce(out=MS[:, 4:8], in_=SQr[:, :].rearrange("p (h e) -> p h e", h=4), axis=AX.X, op=ALU.add), reads=[Rsq], writes=[Rms])
                S.op("act", lambda: nc.scalar.activation(out=MS[:, 4:8], in_=MS[:, 4:8], func=AF.Sqrt, scale=1.0 / 64, bias=epsb[:, 0:1]), reads=[Rms, Reps], writes=[Rms])
                S.op("dve", lambda: nc.vector.reciprocal(out=MS[:, 4:8], in_=MS[:, 4:8]), reads=[Rms], writes=[Rms])
                S.op("dve", lambda: nc.vector.tensor_tensor(out=cv, in0=cv, in1=MS[:, 4:8].unsqueeze(2).broadcast_to([128, 4, 64]), op=ALU.mult), reads=[Rcen, Rms], writes=[Rcen])
                S.op("dve", lambda: nc.vector.tensor_tensor(out=NRo[:, :], in0=CEN[:, :], in1=SRG[:, :], op=ALU.mult), reads=[Rcen, Rsrg], writes=[Rnro])

                if cfg.get("ret_stop") == "p2d":
                    continue

                def emit_t2():
                    inst = None
                    for cc in range(2):
                        inst = nc.tensor.transpose(out=PSb(6)[:, cc * 128:(cc + 1) * 128], in_=NRo[:, cc * 128:(cc + 1) * 128], identity=identb[:])
                    return inst
                S.op("pe", emit_t2, reads=[Rnro, Rid], writes=[RPS[6]])
                for cc in range(2):
                    S.op("dve", lambda cc=cc, c=c: nc.vector.tensor_scalar(out=cur["BR"][:, 2 + cc, c * 128:(c + 1) * 128], in0=PSb(6)[:, cc * 128:(cc + 1) * 128],
                                                                          scalar1=PV[:, 8 + cc:9 + cc], scalar2=None, op0=ALU.mult), reads=[RPS[6], Rpv], writes=[Rbr[2 + cc]])
            if not sample:
                pi = 0 if kind == "pA" else 1
                OUT_EVS.append(S.dma("sp", o_ret[pi, l, 0].rearrange("h d e -> d h e"), SF[0:64, :].rearrange("d (h e) -> d h e", h=4), reads=[Rsf], key=Reg("o_ret_d")))
                OUT_EVS.append(S.dma("sp", o_ret[pi, l, 1].rearrange("h d e -> d h e"), SB[0:64, :].rearrange("d (h e) -> d h e", h=4), reads=[Rsb], key=Reg("o_ret_d")))

        def branch_s5(l, t0, T, NB, BW, kind, multi=False):
            sample = kind == "sample"
            o = BS0
            UT, o = view(o, [128, 2, 2048], BF16)
            YS, o = view(o, [128, 2, 2048], BF16)
            YFp, o = view(o, [128, 2048], BF16)
            oZ = o
            TRI, o = view(o, [128, 2, 512])
            TRIb, o = view(o, [128, 2, 512], BF16)
            BZ, o = view(o, [128, 2, 512], BF16)
            oTT = o
            TT, o = view(o, [128, 2, 1024], BF16)
            TTf, _ = view(oTT, [128, 2, 512])
            oS = o
            ow = WIN0
            SBb, ow = view(ow, [128, 2, 512], BF16)
            OTs, ow = view(ow, [128, 512], BF16)
            BW_, ow = view(ow, [128, 2, 2, 128], BF16)
            BBR, ow = view(ow, [128, 16, 16])
            BBI, ow = view(ow, [128, 16, 16])
            CW, ow = view(ow, [128, 8, 2, 32], BF16)
            WGL, ow = view(ow, [128, 2, 256], BF16)
            YV, _ = view(WIN0, [128, 512])
            Rut, Rys, Rsfs, Rtri, Rbz, Rtt, Rsbb, Rots, Rrb, Rbw = (Reg("s_" + x) for x in ("UT", "YS", "YFp", "TRI", "BZ", "TT", "SBb", "OTs", "RB", "BW"))
            def sm_(shape, dt=F32):
                nonlocal o
                v, o = view(o, shape, dt)
                return v
            o1 = [oTT]

            def ot_(shape, dt=F32):
                v, o1[0] = view(o1[0], shape, dt)
                return v
            LRE, LIM, LDT, AR, AI, FR, FI = (ot_([128, 16]) for _ in range(7))
            BRE, BIM = ot_([128, 16, 16]), ot_([128, 16, 16])
            CNAT = ot_([128, 2, 64])
            MAG, UR, UI, W1, W2_, W3 = (sm_([128, 16]) for _ in range(6))
            UBR, UBI = sm_([128, 16]), sm_([128, 16])
            TA_, TBs = sm_([128, 2, 16, 16]), sm_([128, 2, 16, 32])
            PW = sm_([128, 2, 16])
            S0t = sm_([128, 16, 2])
            INI = sm_([128, 2, 2])
            FIN = sm_([128, 2, 16, 2])
            WP = sm_([128, 128])
            Rsu = Reg("s_setup")
            Rini, Rfin, Rwp, Rcn = Reg("s_INI"), Reg("s_FIN"), Reg("s_WP"), Reg("s_CN")
            Rbu = Reg("s_BU")
            Rt4 = [Reg("s_T0"), Reg("s_T1"), Reg("s_P0"), Reg("s_P1")]
            Rbz2 = [Reg("s_BZ0"), Reg("s_BZ1")]
            Rw12 = [Reg("s_W1"), Reg("s_W2")]
            V = nc.vector
            dbgon = cfg.get("s5dbg") == kind
            if dbgon:
                dbg2 = nc.dram_tensor("dbg2", [128, 4096], F32, kind="ExternalOutput").ap()

            def dbg(ap, c0, n, regs):
                if dbgon:
                    S.dma("sp", dbg2[:, c0:c0 + n], ap, reads=regs, key=Reg("dbg2"))

            def dv(fn, reads, writes):
                S.op("dve", fn, reads=reads, writes=writes)

            def tt(out, a, b, op, reads=(Rsu,), writes=(Rsu,)):
                dv(lambda: V.tensor_tensor(out=out, in0=a, in1=b, op=op), list(reads), list(writes))

            def cmul(orr, oi, ar, ai, br, bi, t1, t2, reads=(Rsu,), writes=(Rsu,)):
                tt(t1, ar, br, ALU.mult, reads, writes)
                tt(t2, ai, bi, ALU.mult, reads, writes)
                tt(t2, t1, t2, ALU.subtract, reads, writes)
                tt(t1, ar, bi, ALU.mult, reads, writes)
                tt(oi, ai, br, ALU.mult, reads, writes)
                tt(oi, t1, oi, ALU.add, reads, writes)
                tt(orr, t2, t2, ALU.max, reads, writes)

            for cc in range(2):
                proj_fm(l, cc * 128, 128, NB, BW, lambda b, bank, cc=cc: S.op(
                    "act", lambda: nc.scalar.copy(out=UT[:, cc, b * BW:(b + 1) * BW], in_=PS[:, bank, 0:BW]), reads=[RPS[bank]], writes=[Rut]))
            S.barrier()
            for d in range(2):
                for dst, src in ((LRE, s5_lam_re), (LIM, s5_lam_im)):
                    S.dma("sp", dst[:, d::2], src[l, d].rearrange("(m g) p -> (g p) m", g=2), writes=[Rsu], slow=True)
                for g2 in range(2):
                    S.dma("sp", LDT[g2 * 64:(g2 + 1) * 64, d::2], s5_log_dt[l, d:d + 1, g2::2].broadcast_to([64, 8]), writes=[Rsu], slow=True)
                for dst, src in ((BRE, s5_b_re), (BIM, s5_b_im)):
                    S.dma("sp", dst[:, d::2, :], src[l, d].rearrange("(m g) p h -> (g p) m h", g=2), writes=[Rsu])
                if sample:
                    S.dma("sp", S0t[:, d::2, :], st_s5[l, d].rearrange("(m g) p r -> (g p) m r", g=2), writes=[Rsu], slow=True)
            S.dma("pool", WGL[:, :, :], s5_w_glu[l].rearrange("(kc p) n -> p kc n", p=128), writes=[Rsu])
            S.op("act", lambda: nc.scalar.activation(out=LDT[:, :], in_=LDT[:, :], func=AF.Exp), reads=[Rsu], writes=[Rsu])
            tt(W1[:, :], LRE[:, :], LDT[:, :], ALU.mult)
            S.op("act", lambda: nc.scalar.activation(out=MAG[:, :], in_=W1[:, :], func=AF.Exp), reads=[Rsu], writes=[Rsu])
            tt(W1[:, :], LIM[:, :], LDT[:, :], ALU.mult)
            S.op("act", lambda: nc.scalar.activation(out=UI[:, :], in_=W1[:, :], func=AF.Sin, scale=1.0 / 64), reads=[Rsu], writes=[Rsu])
            S.op("act", lambda: nc.scalar.activation(out=UR[:, :], in_=W1[:, :], func=AF.Sin, scale=1.0 / 64, bias=halfpi[:, 0:1]), reads=[Rsu, Reps], writes=[Rsu])
            for _ in range(6):
                tt(W1[:, :], UR[:, :], UR[:, :], ALU.mult)
                tt(W2_[:, :], UI[:, :], UI[:, :], ALU.mult)
                tt(W3[:, :], UR[:, :], UI[:, :], ALU.mult)
                tt(UR[:, :], W1[:, :], W2_[:, :], ALU.subtract)
                tt(UI[:, :], W3[:, :], W3[:, :], ALU.add)
            tt(AR[:, :], MAG[:, :], UR[:, :], ALU.mult)
            tt(AI[:, :], MAG[:, :], UI[:, :], ALU.mult)
            tt(W1[:, :], LRE[:, :], LRE[:, :], ALU.mult)
            tt(W2_[:, :], LIM[:, :], LIM[:, :], ALU.mult)
            tt(W1[:, :], W1[:, :], W2_[:, :], ALU.add)
            dv(lambda: V.reciprocal(out=W1[:, :], in_=W1[:, :]), [Rsu], [Rsu])
            dv(lambda: V.tensor_scalar(out=W2_[:, :], in0=AR[:, :], scalar1=-1.0, scalar2=None, op0=ALU.add), [Rsu], [Rsu])
            tt(FR[:, :], W2_[:, :], LRE[:, :], ALU.mult)
            tt(W3[:, :], AI[:, :], LIM[:, :], ALU.mult)
            tt(FR[:, :], FR[:, :], W3[:, :], ALU.add)
            tt(FR[:, :], FR[:, :], W1[:, :], ALU.mult)
            tt(FI[:, :], AI[:, :], LRE[:, :], ALU.mult)
            tt(W3[:, :], W2_[:, :], LIM[:, :], ALU.mult)
            tt(FI[:, :], FI[:, :], W3[:, :], ALU.subtract)
            tt(FI[:, :], FI[:, :], W1[:, :], ALU.mult)
            dbg(MAG[:, :], 0, 16, [Rsu]); dbg(UR[:, :], 16, 16, [Rsu]); dbg(UI[:, :], 32, 16, [Rsu]); dbg(FR[:, :], 48, 16, [Rsu]); dbg(FI[:, :], 64, 16, [Rsu])
            frb = FR[:, :].unsqueeze(2).broadcast_to([128, 16, 16])
            fib = FI[:, :].unsqueeze(2).broadcast_to([128, 16, 16])
            tt(BBR[:, :, :], BRE[:, :, :], frb, ALU.mult)
            tt(BBI[:, :, :], BIM[:, :, :], fib, ALU.mult)
            tt(BBR[:, :, :], BBR[:, :, :], BBI[:, :, :], ALU.subtract)
            tt(BBI[:, :, :], BRE[:, :, :], fib, ALU.mult)
            tt(BRE[:, :, :], BIM[:, :, :], frb, ALU.mult)
            tt(BBI[:, :, :], BBI[:, :, :], BRE[:, :, :], ALU.add)
            dv(lambda: V.memset(CW[:, :, :, :], 0.0), [], [Rsu])
            for ri, src in ((0, s5_c_re), (1, s5_c_im)):
                S.dma("sp", CNAT[:, :, :], src[l].rearrange("(c g) h p -> (g h) c p", c=2), writes=[Rcn])
                CNB = TTf[:, 1, 0:64].bitcast(BF16)
                dv(lambda: V.tensor_copy(out=CNB.rearrange("p (c k) -> p c k", c=2), in_=CNAT[:, :, :]), [Rcn, Rtt], [Rtt])
                for c in range(2):
                    for half in range(2):
                        S.op("pe", lambda c=c, half=half: nc.tensor.matmul(PS[half * 64:(half + 1) * 64, 6, c * 128:(c + 1) * 128], lhsT=CNB[:, c * 64:(c + 1) * 64], rhs=identb[:, :],
                                                                           start=True, stop=True), reads=[Rtt, Rid], writes=[RPS[6]])
                ctv = PS[:, 6, 0:256].rearrange("q (m g h) -> q m g h", m=8, g=2)
                sc = 1.0 if ri == 0 else -1.0
                dv(lambda ri=ri, sc=sc: V.tensor_scalar(out=CW[0:64, :, ri, 0:16], in0=ctv[0:64, :, 0, :], scalar1=sc, scalar2=None, op0=ALU.mult), [RPS[6]], [Rsu])
                dv(lambda ri=ri, sc=sc: V.tensor_scalar(out=CW[64:128, :, ri, 16:32], in0=ctv[64:128, :, 1, :], scalar1=sc, scalar2=None, op0=ALU.mult), [RPS[6]], [Rsu])
            S.barrier()
            def build_pows(TAB, nent, base_r, base_i):
                dv(lambda: V.memset(TAB[:, 0, :, 0:1], 1.0), [], [Rsu])
                dv(lambda: V.memset(TAB[:, 1, :, 0:1], 0.0), [], [Rsu])
                tt(PW[:, 0, :], base_r, base_r, ALU.max)
                tt(PW[:, 1, :], base_i, base_i, ALU.max)
                nn = 1
                while nn < nent:
                    pr = PW[:, 0, :].unsqueeze(2).broadcast_to([128, 16, nn])
                    pi_ = PW[:, 1, :].unsqueeze(2).broadcast_to([128, 16, nn])
                    t1 = TTf[:, 0, 0:16 * nn].rearrange("p (k j) -> p k j", k=16)
                    t2 = TTf[:, 1, 0:16 * nn].rearrange("p (k j) -> p k j", k=16)
                    rr, ri = Rsu, Rtt
                    tt(t1, TAB[:, 0, :, 0:nn], pr, ALU.mult, (rr, ri), (ri,))
                    tt(t2, TAB[:, 1, :, 0:nn], pi_, ALU.mult, (rr, ri), (ri,))
                    tt(TAB[:, 0, :, nn:2 * nn], t1, t2, ALU.subtract, (rr, ri), (rr,))
                    tt(t1, TAB[:, 0, :, 0:nn], pi_, ALU.mult, (rr, ri), (ri,))
                    tt(t2, TAB[:, 1, :, 0:nn], pr, ALU.mult, (rr, ri), (ri,))
                    tt(TAB[:, 1, :, nn:2 * nn], t1, t2, ALU.add, (rr, ri), (rr,))
                    tt(W1[:, :], PW[:, 0, :], PW[:, 0, :], ALU.mult)
                    tt(W2_[:, :], PW[:, 1, :], PW[:, 1, :], ALU.mult)
                    tt(W3[:, :], PW[:, 0, :], PW[:, 1, :], ALU.mult)
                    tt(PW[:, 0, :], W1[:, :], W2_[:, :], ALU.subtract)
                    tt(PW[:, 1, :], W3[:, :], W3[:, :], ALU.add)
                    nn *= 2
            build_pows(TBs, 32, UR[:, :], UI[:, :])
            tt(W1[:, :], PW[:, 0, :], PW[:, 0, :], ALU.max)
            tt(W2_[:, :], PW[:, 1, :], PW[:, 1, :], ALU.max)
            tt(UBR[:, :], PW[:, 0, :], PW[:, 0, :], ALU.max)
            tt(UBI[:, :], PW[:, 1, :], PW[:, 1, :], ALU.max)
            build_pows(TA_, 16, UBR[:, :], UBI[:, :])
            if BW == 512:
                tt(UBR[:, :], PW[:, 0, :], PW[:, 0, :], ALU.max)
                tt(UBI[:, :], PW[:, 1, :], PW[:, 1, :], ALU.max)
            else:
                tt(UBR[:, :], TA_[:, 0, :, 8], TA_[:, 0, :, 8], ALU.max)
                tt(UBI[:, :], TA_[:, 1, :, 8], TA_[:, 1, :, 8], ALU.max)
            S.barrier()
            mcb = [0]
            for m in range(8):
                cc, m4 = m // 4, m % 4
                for d in range(2):
                    k = m * 2 + d
                    for ri, BB in ((0, BBR), (1, BBI)):
                        dv(lambda: V.memset(WP[:, :], 0.0), [Rwp], [Rwp])
                        dv(lambda BB=BB, k=k: V.tensor_copy(out=WP[0:64, m4 * 32:m4 * 32 + 16], in_=BB[0:64, k, :]), [Rsu, Rwp], [Rwp])
                        dv(lambda BB=BB, k=k: V.tensor_copy(out=WP[64:128, m4 * 32 + 16:m4 * 32 + 32], in_=BB[64:128, k, :]), [Rsu, Rwp], [Rwp])
                        S.op("pe", lambda: nc.tensor.transpose(out=PS[:, 7, 0:128], in_=WP[:, :], identity=ident[:]), reads=[Rwp, Rid], writes=[RPS[7]])
                        S.op("act", lambda d=d, ri=ri: nc.scalar.copy(out=BW_[:, d, ri, :], in_=PS[:, 7, 0:128]), reads=[RPS[7]], writes=[Rbw])
                for d in range(2):
                    k = m * 2 + d
                    rev = d == 1
                    ar = TA_[:, 0, k, :].unsqueeze(2).broadcast_to([128, 16, 32])
                    ai = TA_[:, 1, k, :].unsqueeze(2).broadcast_to([128, 16, 32])
                    br = TBs[:, 0, k, :].unsqueeze(1).broadcast_to([128, 16, 32])
                    bi = TBs[:, 1, k, :].unsqueeze(1).broadcast_to([128, 16, 32])
                    trv = TRI[:, 0, :].rearrange("p (q j) -> p q j", q=16)
                    tiv = TRI[:, 1, :].rearrange("p (q j) -> p q j", q=16)
                    t1 = TTf[:, 0, :].rearrange("p (q j) -> p q j", q=16)
                    t2 = TTf[:, 1, :].rearrange("p (q j) -> p q j", q=16)
                    rw = (Rsu, Rtt, Rtri) + tuple(Rt4)
                    tt(t1, ar, br, ALU.mult, rw, (Rtt,) + tuple(Rt4))
                    tt(t2, ai, bi, ALU.mult, rw, (Rtt,) + tuple(Rt4))
                    tt(trv, t1, t2, ALU.subtract, rw, (Rtri,))
                    tt(t1, ar, bi, ALU.mult, rw, (Rtt,) + tuple(Rt4))
                    tt(t2, ai, br, ALU.mult, rw, (Rtt,) + tuple(Rt4))
                    tt(tiv, t1, t2, ALU.add, rw, (Rtri,))
                    dv(lambda: V.tensor_copy(out=TRIb[:, :, :], in_=TRI[:, :, :]), [Rtri], [Rtri])
                    if k == 0:
                        dbg(TRI[:, 0, :], 128, 512, [Rtri]); dbg(TRI[:, 1, :], 640, 512, [Rtri])
                    ib = 0
                    if sample:
                        cmul(INI[:, 0, ib:ib + 1], INI[:, 1, ib:ib + 1], UR[:, k:k + 1], UI[:, k:k + 1], S0t[:, k, 0:1], S0t[:, k, 1:2], W1[:, 0:1], W2_[:, 0:1], (Rsu, Rini), (Rsu, Rini))
                    else:
                        dv(lambda: V.memset(INI[:, :, 0:1], 0.0), [Rini], [Rini])
                    blocks = list(range(NB - 1, -1, -1)) if rev else list(range(NB))
                    for bi_, b in enumerate(blocks):
                        cols = slice(b * BW, (b + 1) * BW)
                        bk = 2 * (mcb[0] % 2)
                        mcb[0] += 1
                        for ri in range(2):
                            S.op("pe", lambda ri=ri: nc.tensor.matmul(PS[:, bk + ri, 0:BW], lhsT=BW_[:, d, ri, :], rhs=UT[:, cc, cols], start=True, stop=True),
                                 reads=[Rbw, Rut], writes=[RPS[bk + ri]])
                        if rev:
                            trr, tri = TRI[:, 0, BW - 1::-1] if BW == 512 else TRI[:, 0, BW - 1::-1], TRI[:, 1, BW - 1::-1]
                            trr = TRI[:, 0, 0:BW][:, ::-1]
                            tri = TRI[:, 1, 0:BW][:, ::-1]
                            trrb = TRIb[:, 0, 0:BW][:, ::-1]
                            trib = TRIb[:, 1, 0:BW][:, ::-1]
                        else:
                            trr, tri = TRI[:, 0, 0:BW], TRI[:, 1, 0:BW]
                            trrb, trib = TRIb[:, 0, 0:BW], TRIb[:, 1, 0:BW]
                        for ri in range(2):
                            S.op("act", lambda ri=ri: nc.scalar.copy(out=SBb[:, ri, 0:BW], in_=PS[:, bk + ri, 0:BW]), reads=[RPS[bk + ri]], writes=[Rsbb])
                        pre, pim = SBb[:, 0, 0:BW], SBb[:, 1, 0:BW]
                        T0_, T1_ = TT[:, 0, 0:BW], TT[:, 1, 0:BW]
                        P0_, P1_ = TT[:, 0, 512:512 + BW], TT[:, 1, 512:512 + BW]
                        B0_, B1_ = BZ[:, 0, 0:BW], BZ[:, 1, 0:BW]
                        tt(T0_, pre, trrb, ALU.mult, (Rtri, Rsbb, Rt4[0]), (Rt4[0],))
                        tt(T1_, pim, trib, ALU.mult, (Rtri, Rsbb, Rt4[1]), (Rt4[1],))
                        tt(P0_, pim, trrb, ALU.mult, (Rtri, Rsbb, Rt4[2]), (Rt4[2],))
                        tt(P1_, pre, trib, ALU.mult, (Rtri, Rsbb, Rt4[3]), (Rt4[3],))
                        tt(B0_, T0_, T1_, ALU.add, (Rt4[0], Rt4[1], Rbz2[0]), (Rbz2[0],))
                        tt(B1_, P0_, P1_, ALU.subtract, (Rt4[2], Rt4[3], Rbz2[1]), (Rbz2[1],))
                        for ri in range(2):
                            zo = TT[:, ri, 0:BW]
                            zin = BZ[:, ri, 0:BW]
                            if rev:
                                zo, zin = zo[:, ::-1], zin[:, ::-1]
                            dv(lambda zo=zo, zin=zin, ri=ri: V.tensor_tensor_scan(out=zo, data0=MAG[:, k:k + 1].broadcast_to([128, BW]), data1=zin, initial=INI[:, ri, ib:ib + 1], op0=ALU.mult, op1=ALU.add),
                               [Rsu, Rbz2[ri], Rini, Rt4[ri]], [Rt4[ri]])
                        zl = 0 if rev else BW - 1
                        zr_, zi_ = TT[:, 0, zl:zl + 1], TT[:, 1, zl:zl + 1]
                        dstS = SBb[:, :, 0:BW]
                        Rdst = Rsbb
                        tt(B0_, T0_, trrb, ALU.mult, (Rtri, Rt4[0], Rbz2[0]), (Rbz2[0],))
                        tt(B1_, T1_, trib, ALU.mult, (Rtri, Rt4[1], Rbz2[1]), (Rbz2[1],))
                        tt(P0_, T1_, trrb, ALU.mult, (Rtri, Rt4[1], Rt4[2]), (Rt4[2],))
                        tt(P1_, T0_, trib, ALU.mult, (Rtri, Rt4[0], Rt4[3]), (Rt4[3],))
                        tt(dstS[:, 0, :], B0_, B1_, ALU.subtract, (Rbz2[0], Rbz2[1], Rdst), (Rdst,))
                        tt(dstS[:, 1, :], P0_, P1_, ALU.add, (Rt4[2], Rt4[3], Rdst), (Rdst,))
                        last_blk = bi_ == NB - 1 or multi
                        if multi and bi_ != NB - 1:
                            dv(lambda: V.memset(INI[:, :, 1 - ib:2 - ib], 0.0), [Rini], [Rini])
                        if not last_blk:
                            ib2 = 1 - ib
                            rc, wc = [Rsu, Rini, Rt4[0], Rt4[1]], [Rini]
                            dv(lambda: V.tensor_scalar(out=W1[:, 0:1], in0=zi_, scalar1=UBI[:, k:k + 1], scalar2=None, op0=ALU.mult), [Rsu, Rt4[1], Rw12[0]], [Rw12[0]])
                            dv(lambda: V.tensor_scalar(out=W2_[:, 0:1], in0=zr_, scalar1=UBI[:, k:k + 1], scalar2=None, op0=ALU.mult), [Rsu, Rt4[0], Rw12[1]], [Rw12[1]])
                            dv(lambda: V.scalar_tensor_tensor(out=INI[:, 0, ib2:ib2 + 1], in0=zr_, scalar=UBR[:, k:k + 1], in1=W1[:, 0:1], op0=ALU.mult, op1=ALU.subtract), [Rsu, Rt4[0], Rw12[0], Rini], [Rini])
                            dv(lambda: V.scalar_tensor_tensor(out=INI[:, 1, ib2:ib2 + 1], in0=zi_, scalar=UBR[:, k:k + 1], in1=W2_[:, 0:1], op0=ALU.mult, op1=ALU.add), [Rsu, Rt4[1], Rw12[1], Rini], [Rini])
                            ib = ib2
                        elif not sample:
                            pf = b if multi else 0
                            cmul(FIN[:, pf, k, 0:1], FIN[:, pf, k, 1:2], TRI[:, 0, BW - 1:BW], TRI[:, 1, BW - 1:BW], zr_, zi_, W1[:, 0:1], W2_[:, 0:1], (Rsu, Rtri, Rt4[0], Rt4[1], Rfin), (Rsu, Rfin))
                        if multi and bi_ != NB - 1:
                            ib = 1 - ib
                        def emit_y():
                            nc.tensor.matmul(PS[0:32, 4, 0:BW], lhsT=CW[:, m, 0, :], rhs=SBb[:, 0, 0:BW], start=True, stop=False)
                            return nc.tensor.matmul(PS[0:32, 4, 0:BW], lhsT=CW[:, m, 1, :], rhs=SBb[:, 1, 0:BW], start=False, stop=True)
                        S.op("pe", emit_y, reads=[Rsu, Rsbb], writes=[RPS[4]])
                        if not rev:
                            S.op("act", lambda: nc.scalar.copy(out=YFp[0:32, cols], in_=PS[0:32, 4, 0:BW]), reads=[RPS[4]], writes=[Rsfs])
                        else:
                            tt(OTs[0:32, 0:BW], PS[0:32, 4, 0:BW], YFp[0:32, cols], ALU.add, (RPS[4], Rsfs, Rots), (Rots,))
                            S.dma("sp", YS[m4 * 32:(m4 + 1) * 32, cc, cols], OTs[0:32, 0:BW], reads=[Rots], writes=[Rys], key=Reg("s_OTd"))
            if not sample:
                for pi in ((0, 1) if multi else ((0 if kind == "pA" else 1),)):
                    pf = pi if multi else 0
                    for d in range(2):
                        OUT_EVS.append(S.dma("sp", o_s5[pi, l, d].rearrange("(m g) p r -> (g p) m r", g=2), FIN[:, pf, d::2, :], reads=[Rfin], key=Reg("o_s5_d"), slow=True))
            S.barrier()
            Z, _ = view(oZ, [128, 2, 2048], BF16)
            Rz_ = Reg("s_Z")
            for b in range(NB):
                cols = slice(b * BW, (b + 1) * BW)
                for cc in range(2):
                    yv_ = YV[:, 0:BW]
                    dv(lambda: V.scalar_tensor_tensor(out=yv_, in0=UT[:, cc, cols], scalar=PV[:, 10 + cc:11 + cc], in1=YS[:, cc, cols], op0=ALU.mult, op1=ALU.add),
                       [Rut, Rpv, Rys, Rbz], [Rbz])
                    tt(TTf[:, 0, 0:BW], yv_, yv_, ALU.mult, (Rbz, Rtt), (Rtt,))
                    dv(lambda: V.tensor_scalar(out=TTf[:, 0, 0:BW], in0=TTf[:, 0, 0:BW], scalar1=0.044715, scalar2=1.0, op0=ALU.mult, op1=ALU.add), [Rtt], [Rtt])
                    tt(TTf[:, 0, 0:BW], TTf[:, 0, 0:BW], yv_, ALU.mult, (Rbz, Rtt), (Rtt,))
                    S.op("act", lambda: nc.scalar.activation(out=TTf[:, 1, 0:BW], in_=TTf[:, 0, 0:BW], func=AF.Sigmoid, scale=1.5957691216057308), reads=[Rtt], writes=[Rtt])
                    tt(Z[:, cc, cols], TTf[:, 1, 0:BW], yv_, ALU.mult, (Rbz, Rtt, Rz_), (Rz_,))
                for co in range(2):
                    def emit_g(co=co):
                        nc.tensor.matmul(PS[:, 5, 0:BW], lhsT=WGL[:, 0, co * 128:(co + 1) * 128], rhs=Z[:, 0, cols], start=True, stop=False)
                        return nc.tensor.matmul(PS[:, 5, 0:BW], lhsT=WGL[:, 1, co * 128:(co + 1) * 128], rhs=Z[:, 1, cols], start=False, stop=True)
                    S.op("pe", emit_g, reads=[Rsu, Rz_], writes=[RPS[5]])
                    S.op("act", lambda: nc.scalar.activation(out=OTs[:, 0:BW], in_=PS[:, 5, 0:BW], func=AF.Sigmoid), reads=[RPS[5]], writes=[Rots])
                    tt(cur["BR"][:, co, cols], Z[:, co, cols], OTs[:, 0:BW], ALU.mult, (Rz_, Rots), (Rbr[co],))

        DBG = {}

        def mixer(l):
            make_gate_bcast(1)
            mixer_params(l)

            def build_ht(t0, ntile, ci):
                S.barrier()
                cur["XN"], _ = view(BS0, [128, 2, D])
                for tt in range(ntile):
                    prenorm_tile(t0 + tt, ci, 1, HT[:, :, tt * 128:(tt + 1) * 128], Rht, (4 + 2 * (tt % 2), 5 + 2 * (tt % 2)))
                S.barrier()

            def zero_disabled(T):
                for nm, chs in (("s5", (0, 1)), ("ret", (2, 3)), ("conv", (4, 5)), ("mla", (6, 7))):
                    if not cfg.get(nm, True):
                        for i in chs:
                            S.op("dve", lambda i=i: nc.vector.memset(BR[:, i, 0:T], 0.0), writes=[Rbr[i]])

            def seq_branches(t0, T, NB, BW, kind):
                if cfg.get("conv", True):
                    branch_conv(l, T, NB, BW)
                    S.barrier()
                if cfg.get("mla", True):
                    branch_mla(l, t0, T, NB, BW, kind)
                    S.barrier()
                if cfg.get("ret", True):
                    branch_ret(l, t0, T, kind)
                    S.barrier()

            cur["HT"], cur["BR"] = HT, BR
            build_ht(0, 16, 0)
            zero_disabled(2048)
            seq_branches(0, 2048, 4, 512, "sample")
            if cfg.get("s5", True):
                branch_s5(l, 0, 2048, 4, 512, "sample")
                S.barrier()
            if tuple(cfg.get("dump_br", ())) == (l, "sample"):
                dbg = nc.dram_tensor("dbg_br", [128, 8, 2048], BF16, kind="ExternalOutput").ap()
                S.dma("sp", dbg[:, :, 0:2048], BR[:, :, 0:2048], reads=Rbr, key=Reg("dbg"))
            gate_stage(l, 0, 16, 0, 2048, 4, 512)
            S.barrier()
            build_ht(16, 4, 1)
            zero_disabled(512)
            for pi, kind in ((0, "pA"), (1, "pB")):
                cur["HT"], cur["BR"] = HT[:, :, pi * 256:(pi + 1) * 256], BR[:, :, pi * 256:(pi + 1) * 256]
                mla_cache_out(l, 16 + 2 * pi, 2, pi)
                S.barrier()
                seq_branches(16 + 2 * pi, 256, 1, 256, kind)
            cur["HT"], cur["BR"] = HT, BR
            if cfg.get("s5", True):
                branch_s5(l, 16, 512, 2, 256, "pAB", multi=True)
                S.barrier()
            gate_stage(l, 16, 4, 1, 512, 1, 512)
            S.barrier()
            cur["XN"], cur["TMP"] = XN, TMP

        for l in range(LAYERS):
            S.barrier()
            compute_mod(l)
            S.barrier()
            if cfg.get("ffn1", True):
                ffn(l, 0, 0)
            S.barrier()
            if cfg.get("mixer", True):
                mixer(l)
            S.barrier()
            if cfg.get("ffn2", True):
                ffn(l, 1, 2)

        yv = y.rearrange("(t p) d -> p t d", p=128)
        Ryout = Reg("yout")
        evs = []
        for t in range(NT):
            evs.append(S.dma("sp", yv[:, t, :], X[:, t, :], reads=[RX[t]], key=Ryout))
        S._wait("sp", set([evs[-1]] + OUT_EVS))
        S.barrier()
    return nc


def _axial(T, dim):
    rows = T // 64
    row = np.repeat(np.arange(rows, dtype=np.float32), 64)
    col = np.tile(np.arange(64, dtype=np.float32), rows)
    quarter = dim // 4
    inv = (np.float32(10000.0) ** (-np.arange(quarter, dtype=np.float32) / np.float32(quarter))).astype(np.float32)
    ang = np.concatenate([row[:, None] * inv, col[:, None] * inv], axis=-1).astype(np.float32)
    return np.cos(ang).astype(np.float32), np.sin(ang).astype(np.float32)


def _rope_tables():
    c, s = _axial(2048, 32)
    mla = np.stack([np.concatenate([c.T, c.T], 0), np.concatenate([s.T, s.T], 0)], 0)
    c2, s2 = _axial(2048, 64)
    ret = np.stack([c2, s2], 0)
    return np.ascontiguousarray(mla, dtype=np.float32), np.ascontiguousarray(ret, dtype=np.float32)


def _prep_inputs(inputs):
    f = lambda a: np.ascontiguousarray(np.asarray(a, dtype=np.float32))
    shared = {k: f(inputs[k]) for k in (
        "w_mod", "b_mod", "norm_pre", "norm_post", "ffn_w1", "ffn_w3", "ffn_w2", "w_in", "s5_lam_re", "s5_lam_im", "s5_log_dt",
        "s5_b_re", "s5_b_im", "s5_c_re", "s5_c_im", "s5_d", "s5_w_glu", "ret_decay", "ret_gn", "conv_w", "conv_b",
        "mla_q_norm", "mla_w_uq", "mla_kv_norm", "mla_w_ukv", "w_branch", "w_gate", "b_gate", "w_o")}
    shared["c_ident"] = np.eye(128, dtype=np.float32)
    shared["c_rope_mla"], shared["c_rope_ret"] = _rope_tables()
    jj = np.arange(128, dtype=np.float32)[:, None]
    ii = np.arange(128, dtype=np.float32)[None, :]
    diff = ii - jj
    shared["c_ret"] = np.ascontiguousarray(np.stack([
        np.maximum(diff, 0), np.maximum(-diff, 0), 0.125 * (diff >= 0), 0.125 * (diff < 0),
        np.broadcast_to(ii + 1.0, (128, 128)), np.broadcast_to(128.0 - ii, (128, 128))], 0), dtype=np.float32)
    shared["c_pidx"] = np.ascontiguousarray(np.concatenate([127.0 - jj, jj], 1), dtype=np.float32)
    xp = f(inputs["x_prompt"])
    xs = f(inputs["x_sample"])
    c = f(inputs["c"])
    cctx = f(inputs["c_ctx"])
    maps = []
    for i in range(NCORES):
        m = dict(shared)
        m["xin"] = np.ascontiguousarray(np.concatenate([xs[i], xp[2 * i], xp[2 * i + 1]], axis=0))
        m["cond2"] = np.ascontiguousarray(np.stack([c[i], cctx], axis=0))
        m["st_s5"] = f(inputs["state_s5"][i])
        m["st_ret"] = f(inputs["state_ret"][i])
        m["ctx_mla"] = f(inputs["cache_mla"][i])
        maps.append(m)
    return maps


def _gather(res):
    ys = np.stack([r["y"][:2048] for r in res], axis=0)
    yp = np.stack([r["y"][2048 + 256 * j:2048 + 256 * (j + 1)] for r in res for j in range(2)], axis=0)
    s5 = np.concatenate([r["o_s5"] for r in res], axis=0)
    ret = np.concatenate([r["o_ret"] for r in res], axis=0)
    mla = np.concatenate([r["o_mla"] for r in res], axis=0)
    return (yp.astype(np.float32), ys.astype(np.float32), s5.astype(np.float32), ret.astype(np.float32), mla.astype(np.float32))


CFG = {}


def kernel(**inputs):
    nc = build(CFG)
    maps = _prep_inputs(inputs)
    res = run_bass_kernel_spmd(nc, maps, core_ids=list(range(NCORES)))
    return _gather(res.results)
```

```python
import numpy as np
import concourse.bass as bass
import concourse.mybir as mybir
from concourse.bass_utils import run_bass_kernel_spmd
from contextlib import ExitStack

F32 = mybir.dt.float32
BF16 = mybir.dt.bfloat16
AF = mybir.ActivationFunctionType
ALU = mybir.AluOpType
AX = mybir.AxisListType

D = 1024
DFF = 2816
NFF = 22
TOK = 2560
NT = 20
EPS = 1e-6
NCORES = 8
INC = 2400

SAME_ENG_SYNC = True


_REGS = {}


def Reg(name):
    if name not in _REGS:
        _REGS[name] = _Reg(name)
    return _REGS[name]


class _Reg:
    __slots__ = ("name", "w", "r", "dsem", "dcnt")

    def __init__(self, name):
        self.name = name
        self.w = None
        self.r = []
        self.dsem = None
        self.dcnt = 0


class Sched:
    def __init__(self, nc, es):
        self.nc = nc
        self.es = es
        self.eng = {"pe": nc.tensor, "act": nc.scalar, "dve": nc.vector, "pool": nc.gpsimd, "sp": nc.sync}
        self.sem = {e: es.enter_context(nc.semaphore("s_" + e)) for e in self.eng}
        self.cnt = {e: 0 for e in self.eng}
        self.seen = {e: {} for e in self.eng}
        self.seen_d = {e: {} for e in self.eng}
        self.nsem = 0
        self.out_events = []
        self.pending_reads = {}

    def _wait(self, e, deps, raw=None):
        best = {}
        bestd = {}
        for d in deps:
            if d[0] == "e":
                _, e2, c = d
                if e2 == e and (e == "pe" or not SAME_ENG_SYNC or (raw is not None and d not in raw)):
                    continue
                if c > best.get(e2, 0):
                    best[e2] = c
            else:
                _, sem, tgt, key = d
                if tgt > bestd.get(key, (None, 0))[1]:
                    bestd[key] = (sem, tgt)
        E = self.eng[e]
        for e2, c in best.items():
            if self.seen[e].get(e2, 0) >= c:
                continue
            E.wait_ge(self.sem[e2], c)
            self.seen[e][e2] = c
        for key, (sem, tgt) in bestd.items():
            if self.seen_d[e].get(key, 0) >= tgt:
                continue
            E.wait_ge(sem, tgt)
            self.seen_d[e][key] = tgt

    def _deps(self, reads, writes):
        deps = set()
        for r in reads:
            if r.w is not None:
                deps.add(r.w)
        for w in writes:
            if w.w is not None:
                deps.add(w.w)
            deps.update(w.r)
        return deps

    def op(self, e, emit, reads=(), writes=()):
        raw = set(r.w for r in reads if r.w is not None)
        self._wait(e, self._deps(reads, writes), raw)
        inst = emit()
        self.cnt[e] += 1
        inst.then_inc(self.sem[e], 1)
        ev = ("e", e, self.cnt[e])
        for r in reads:
            r.r.append(ev)
        for w in writes:
            w.w = ev
            w.r = []
        return ev

    def dma(self, e, out, in_, reads=(), writes=(), key=None, slow=False):
        key = key or (list(writes) + list(reads))[0]
        if key.dsem is None:
            key.dsem = self.es.enter_context(self.nc.semaphore("d%d" % self.nsem))
            self.nsem += 1
        deps = set(d for d in self._deps(reads, writes) if not (d[0] == "d" and d[3] == key.name))
        self._wait(e, deps)
        inst = self.eng[e].dma_start(out=out, in_=in_, allow_slow_non_contiguous=True) if slow else self.eng[e].dma_start(out=out, in_=in_)
        key.dcnt += 16
        inst.then_inc(key.dsem, 16)
        ev = ("d", key.dsem, key.dcnt, key.name)
        if reads:
            self.pending_reads[key.name] = ev
        for r in reads:
            r.r.append(ev)
        for w in writes:
            w.w = ev
            w.r = []
        return ev

    def barrier(self):
        pend = set(self.pending_reads.values())
        for e in self.eng:
            deps = set(("e", e2, self.cnt[e2]) for e2 in self.eng if e2 != e and self.cnt[e2] > 0)
            self._wait(e, deps | pend)
        self.pending_reads = {}


def build(cfg):
    _REGS.clear()
    nc = bass.Bass("TRN2", target_bir_lowering=False)
    LAYERS = cfg.get("layers", 2)

    def din(name, shape):
        return nc.dram_tensor(name, list(shape), F32, kind="ExternalInput").ap()

    def dout(name, shape):
        return nc.dram_tensor(name, list(shape), F32, kind="ExternalOutput").ap()

    xin = din("xin", [TOK, D])
    cond2 = din("cond2", [2, D])
    st_s5 = din("st_s5", [2, 2, 16, 64, 2])
    st_ret = din("st_ret", [2, 2, 4, 64, 64])
    ctx_mla = din("ctx_mla", [2, 512, 160])
    w_mod = din("w_mod", [2, D, 9 * D])
    b_mod = din("b_mod", [2, 9 * D])
    norm_pre = din("norm_pre", [2, 3, D])
    norm_post = din("norm_post", [2, 3, D])
    ffn_w1 = din("ffn_w1", [2, 2, D, DFF])
    ffn_w3 = din("ffn_w3", [2, 2, D, DFF])
    ffn_w2 = din("ffn_w2", [2, 2, DFF, D])
    w_in = din("w_in", [2, D, INC])
    s5_lam_re = din("s5_lam_re", [2, 2, 16, 64])
    s5_lam_im = din("s5_lam_im", [2, 2, 16, 64])
    s5_log_dt = din("s5_log_dt", [2, 2, 16])
    s5_b_re = din("s5_b_re", [2, 2, 16, 64, 16])
    s5_b_im = din("s5_b_im", [2, 2, 16, 64, 16])
    s5_c_re = din("s5_c_re", [2, 16, 16, 64])
    s5_c_im = din("s5_c_im", [2, 16, 16, 64])
    s5_d = din("s5_d", [2, 256])
    s5_w_glu = din("s5_w_glu", [2, 256, 256])
    ret_decay = din("ret_decay", [2, 2, 4])
    ret_gn = din("ret_gn", [2, 256])
    conv_w = din("conv_w", [2, 3, 256])
    conv_b = din("conv_b", [2, 256])
    mla_q_norm = din("mla_q_norm", [2, 192])
    mla_w_uq = din("mla_w_uq", [2, 192, 384])
    mla_kv_norm = din("mla_kv_norm", [2, 128])
    mla_w_ukv = din("mla_w_ukv", [2, 128, 512])
    w_branch = din("w_branch", [2, 4, 256, D])
    w_gate = din("w_gate", [2, D, 4 * D])
    b_gate = din("b_gate", [2, 4 * D])
    w_o = din("w_o", [2, D, D])
    c_ident = din("c_ident", [128, 128])
    c_rope_mla = din("c_rope_mla", [2, 32, 2048])
    c_rope_ret = din("c_rope_ret", [2, 2048, 32])
    c_ret = din("c_ret", [6, 128, 128])
    c_pidx = din("c_pidx", [128, 2])

    y = dout("y", [TOK, D])
    o_s5 = dout("o_s5", [2, 2, 2, 16, 64, 2])
    o_ret = dout("o_ret", [2, 2, 2, 4, 64, 64])
    o_mla = dout("o_mla", [2, 2, 256, 160])

    es = ExitStack()
    with es:
        S = Sched(nc, es)

        def sb(name, shape, dt=F32):
            return es.enter_context(nc.sbuf_tensor(name, list(shape), dt))

        X = sb("X", [128, NT, D])
        RX = [Reg("X%d" % t) for t in range(NT)]
        PS = es.enter_context(nc.psum_tensor("PS", [128, 8, 512], F32))
        RPS = [Reg("PS%d" % b) for b in range(8)]
        ident = sb("ident", [128, 128])
        identb = sb("identb", [128, 128], BF16)
        Rid = Reg("ident")
        VEC = sb("VEC", [128, 2, 72])
        Rvec = Reg("VEC")
        NRM = sb("NRM", [128, 48])
        Rnrm = Reg("NRM")
        SV = sb("SV", [128, 2, 3, 8])
        Rsv = Reg("SV")
        GV = sb("GV", [128, 2, 3, 8])
        Rgv = Reg("GV")
        GB = sb("GB", [128, 2, D])
        Rgb = [Reg("GB0"), Reg("GB1")]
        DG = sb("DG", [128, 2, 128])
        Rdg = [Reg("DG0"), Reg("DG1")]
        small = sb("small", [128, 4, 4])
        Rsmall = [Reg("sm%d" % i) for i in range(4)]
        junk = sb("junk", [128, D], BF16)
        Rjunk = Reg("junk")
        ACOLS = 28672
        ARENA = sb("ARENA", [128, ACOLS])

        def view(off, shape, dt=F32):
            n = int(np.prod(shape[1:]))
            nbytes = n * (2 if dt == BF16 else 4)
            assert off % 4 == 0 and nbytes % 4 == 0 and off + nbytes <= ACOLS * 4, (off, shape)
            ap = ARENA[:, off // 4:(off + nbytes) // 4]
            if dt == BF16:
                ap = ap.bitcast(BF16)
            if len(shape) > 2:
                names = "abcdef"[:len(shape) - 1]
                pat = "p (" + " ".join(names) + ") -> p " + " ".join(names)
                ap = ap.rearrange(pat, **{names[i]: shape[i + 1] for i in range(len(names) - 1)})
            return ap, off + nbytes

        HTB, _o = view(0, [128, 2, 8, 512], BF16)
        GT, _o = view(_o, [128, NFF, 512], BF16)
        W13, _o = view(_o, [128, 3, 2, 8, 256], BF16)
        W2, _o = view(_o, [128, 3, 2, D], BF16)
        SIL, _o = view(_o, [128, 2, 512], BF16)
        XN, _o = view(_o, [128, 2, D])
        TMP, _o = view(_o, [128, 2, D])
        WM, _ = view(0, [128, 2, 8, 512], BF16)
        Rxn = [Reg("XN0"), Reg("XN1")]

        xv = xin.rearrange("(t p) d -> p t d", p=128)
        Rxall = Reg("xall")
        for t in range(NT):
            S.dma("sp", X[:, t, :], xv[:, t, :], writes=[RX[t]], key=Rxall)
        for t in range(NT):
            RX[t].w = ("d", Rxall.dsem, Rxall.dcnt, Rxall.name)
        S.dma("sp", ident[:], c_ident[:, :], writes=[Rid])
        S.op("dve", lambda: nc.vector.tensor_copy(out=identb[:], in_=ident[:]), reads=[Rid], writes=[Rid])

        rot = {"ps": 0, "sm": 0, "xn": 0, "dg": 0}
        OUT_EVS = []

        stage = sb("stage", [128, 128])
        Rstage = Reg("stage")

        def load_T(dst_ap, src_ap, rows, dst_reg, bank=7):
            S.dma("sp", stage[0:rows, :], src_ap, writes=[Rstage])
            S.op("pe", lambda: nc.tensor.transpose(out=PS[:, bank, 0:rows], in_=stage[0:rows, :], identity=ident[0:rows, 0:rows]),
                 reads=[Rstage, Rid], writes=[RPS[bank]])
            S.op("dve", lambda: nc.vector.tensor_copy(out=dst_ap, in_=PS[:, bank, 0:rows]), reads=[RPS[bank]], writes=[dst_reg])

        SCT = sb("SCT", [128, 8, 2], BF16)
        Rsct = Reg("SCT")
        sct32 = sb("sct32", [128, 16])
        load_T(sct32[:, :], cond2.rearrange("c (k p) -> (c k) p", p=128), 16, Rsct)
        S.op("act", lambda: nc.scalar.activation(out=SCT[:].rearrange("p k c -> p c k"), in_=sct32[:].rearrange("p (c k) -> p c k", c=2), func=AF.Silu),
             reads=[Rsct], writes=[Rsct])

        Rwm = [Reg("WM0"), Reg("WM1")]
        BM = sb("BM", [128, 72])
        Rbm = Reg("BM")

        def compute_mod(l):
            load_T(BM[:, :], b_mod[l].rearrange("(c p) -> c p", p=128), 72, Rbm)
            load_T(NRM[:, 0:24], norm_pre[l].rearrange("s (c p) -> (s c) p", p=128), 24, Rnrm)
            load_T(NRM[:, 24:48], norm_post[l].rearrange("s (c p) -> (s c) p", p=128), 24, Rnrm)
            wv = w_mod[l].rearrange("(kc p) n -> p kc n", p=128)
            for cb in range(18):
                sl = cb % 2
                S.dma("pool", WM[:, sl, :, :], wv[:, :, cb * 512:(cb + 1) * 512], writes=[Rwm[sl]])
                bank = 6

                def emit(cb=cb, sl=sl):
                    inst = None
                    for cc in range(4):
                        for kc in range(8):
                            inst = nc.tensor.matmul(PS[:, bank, cc * 2:cc * 2 + 2], lhsT=WM[:, sl, kc, cc * 128:(cc + 1) * 128],
                                                    rhs=SCT[:, kc, :], start=(kc == 0), stop=(kc == 7))
                    return inst
                S.op("pe", emit, reads=[Rwm[sl], Rsct], writes=[RPS[bank]])
                S.op("dve", lambda cb=cb: nc.vector.tensor_tensor(
                    out=VEC[:, :, cb * 4:(cb + 1) * 4].rearrange("p c j -> p j c"),
                    in0=PS[:, bank, 0:8].rearrange("p (j c) -> p j c", c=2),
                    in1=BM[:, cb * 4:(cb + 1) * 4].unsqueeze(2).broadcast_to([128, 4, 2]), op=ALU.add),
                    reads=[RPS[bank], Rbm], writes=[Rvec])
            for ci in range(2):
                for s in range(3):
                    S.op("dve", lambda ci=ci, s=s: nc.vector.scalar_tensor_tensor(
                        out=SV[:, ci, s, :], in0=VEC[:, ci, (3 * s + 1) * 8:(3 * s + 2) * 8], scalar=1.0,
                        in1=NRM[:, s * 8:(s + 1) * 8], op0=ALU.add, op1=ALU.mult), reads=[Rvec, Rnrm], writes=[Rsv])
                    fac = 1.0 if s == 1 else 0.5
                    S.op("dve", lambda ci=ci, s=s, fac=fac: nc.vector.scalar_tensor_tensor(
                        out=GV[:, ci, s, :], in0=VEC[:, ci, (3 * s + 2) * 8:(3 * s + 3) * 8], scalar=fac,
                        in1=NRM[:, 24 + s * 8:24 + (s + 1) * 8], op0=ALU.mult, op1=ALU.mult), reads=[Rvec, Rnrm], writes=[Rgv])

        def make_gate_bcast(s):
            for ci in range(2):
                bank0 = 4 + 2 * ci
                for c in range(8):
                    dgi = rot["dg"] % 2
                    rot["dg"] += 1
                    S.op("dve", lambda c=c, ci=ci, dgi=dgi: nc.vector.tensor_scalar(
                        out=DG[:, dgi, :], in0=ident[:], scalar1=GV[:, ci, s, c:c + 1], scalar2=None, op0=ALU.mult),
                        reads=[Rid, Rgv], writes=[Rdg[dgi]])
                    bank = bank0 + c // 4
                    S.op("pe", lambda c=c, dgi=dgi, bank=bank: nc.tensor.matmul(
                        PS[:, bank, (c % 4) * 128:(c % 4 + 1) * 128], lhsT=ones32[:], rhs=DG[:, dgi, :], start=True, stop=True),
                        reads=[Rdg[dgi], Rones], writes=[RPS[bank]])
                S.op("dve", lambda ci=ci, bank0=bank0: nc.vector.tensor_copy(
                    out=GB[:, ci, :], in_=PS[:, bank0:bank0 + 2, :].rearrange("p b n -> p (b n)")),
                    reads=[RPS[bank0], RPS[bank0 + 1]], writes=[Rgb[ci]])

        ones32 = sb("ones32", [128, 128])
        Rones = Reg("ones")
        S.op("dve", lambda: nc.vector.memset(ones32[:], 1.0), writes=[Rones])

        cur = {"XN": XN, "TMP": TMP, "HT": None, "BR": None}

        def prenorm_p1(t):
            XN = cur["XN"]
            smi = rot["sm"] % 4
            rot["sm"] += 1
            xi = rot["xn"] % 2
            rot["xn"] += 1
            sm = small[:, smi, :]
            S.op("act", lambda: nc.scalar.activation(out=junk[:], in_=X[:, t, :], func=AF.Square, accum_out=sm[:, 0:1]),
                 reads=[RX[t]], writes=[Rjunk, Rsmall[smi]])
            S.op("act", lambda: nc.scalar.activation(out=sm[:, 1:2], in_=sm[:, 0:1], func=AF.Sqrt, scale=1.0 / D, bias=epsb[:, 0:1]),
                 reads=[Rsmall[smi], Reps], writes=[Rsmall[smi]])
            S.op("dve", lambda: nc.vector.reciprocal(out=sm[:, 2:3], in_=sm[:, 1:2]), reads=[Rsmall[smi]], writes=[Rsmall[smi]])
            S.op("dve", lambda: nc.vector.tensor_scalar(out=XN[:, xi, :], in0=X[:, t, :], scalar1=sm[:, 2:3], scalar2=None, op0=ALU.mult),
                 reads=[RX[t], Rsmall[smi]], writes=[Rxn[xi]])
            return xi

        def prenorm_p2(xi, ci, s, dst, dst_reg, banks):
            XN = cur["XN"]
            b0, b1 = banks

            def emit():
                inst = None
                for c in range(8):
                    bk = b0 if c < 4 else b1
                    inst = nc.tensor.transpose(out=PS[:, bk, (c % 4) * 128:(c % 4 + 1) * 128], in_=XN[:, xi, c * 128:(c + 1) * 128], identity=ident[:])
                return inst
            S.op("pe", emit, reads=[Rxn[xi], Rid], writes=[RPS[b0], RPS[b1]])
            for c in range(8):
                bk = b0 if c < 4 else b1
                S.op("act", lambda c=c, bk=bk: nc.scalar.activation(
                    out=dst[:, c, :], in_=PS[:, bk, (c % 4) * 128:(c % 4 + 1) * 128], func=AF.Identity,
                    scale=SV[:, ci, s, c:c + 1], bias=VEC[:, ci, 3 * s * 8 + c:3 * s * 8 + c + 1]),
                    reads=[RPS[bk], Rsv, Rvec], writes=[dst_reg])

        def prenorm_tile(t, ci, s, dst, dst_reg, banks):
            prenorm_p2(prenorm_p1(t), ci, s, dst, dst_reg, banks)

        epsb = sb("epsb", [128, 1])
        halfpi = sb("halfpi", [128, 1])
        Reps = Reg("eps")
        S.op("dve", lambda: nc.vector.memset(epsb[:], EPS), writes=[Reps])
        S.op("dve", lambda: nc.vector.memset(halfpi[:], float(np.pi / 2)), writes=[Reps])

        Rtmp = [Reg("TMP0"), Reg("TMP1")]

        def postnorm_tile(t, ci, b0):
            TMP = cur["TMP"]
            smi = rot["sm"] % 4
            rot["sm"] += 1
            ti = rot["xn"] % 2
            rot["xn"] += 1
            sm = small[:, smi, :]
            fin = PS[:, b0:b0 + 2, :].rearrange("p b n -> p (b n)")
            S.op("act", lambda: nc.scalar.activation(out=junk[:], in_=fin, func=AF.Square, accum_out=sm[:, 0:1]),
                 reads=[RPS[b0], RPS[b0 + 1]], writes=[Rjunk, Rsmall[smi]])
            S.op("act", lambda: nc.scalar.activation(out=sm[:, 1:2], in_=sm[:, 0:1], func=AF.Sqrt, scale=1.0 / D, bias=epsb[:, 0:1]),
                 reads=[Rsmall[smi], Reps], writes=[Rsmall[smi]])
            S.op("dve", lambda: nc.vector.reciprocal(out=sm[:, 2:3], in_=sm[:, 1:2]), reads=[Rsmall[smi]], writes=[Rsmall[smi]])
            S.op("dve", lambda: nc.vector.scalar_tensor_tensor(out=TMP[:, ti, :], in0=fin, scalar=sm[:, 2:3], in1=GB[:, ci, :],
                                                               op0=ALU.mult, op1=ALU.mult),
                 reads=[RPS[b0], RPS[b0 + 1], Rsmall[smi], Rgb[ci]], writes=[Rtmp[ti]])
            S.op("dve", lambda: nc.vector.tensor_tensor(out=X[:, t, :], in0=X[:, t, :], in1=TMP[:, ti, :], op=ALU.add),
                 reads=[RX[t], Rtmp[ti]], writes=[RX[t]])

        Rhtb = [Reg("HTB0"), Reg("HTB1")]
        Rgt = [Reg("GT%d" % j) for j in range(NFF)]
        Rw13 = [Reg("W13_%d" % i) for i in range(3)]
        Rw2 = [Reg("W2_%d" % i) for i in range(3)]
        Rsil = [Reg("SIL0"), Reg("SIL1")]
        cnts = {"w13": 0, "w2": 0, "htb": 0, "sil": 0, "pa": 0}

        def ffn(l, f, s):
            make_gate_bcast(s)
            w1v = ffn_w1[l, f].rearrange("(kc p) n -> p kc n", p=128)
            w3v = ffn_w3[l, f].rearrange("(kc p) n -> p kc n", p=128)
            w2v = ffn_w2[l, f].rearrange("(j p) n -> p j n", p=128)
            hb0 = cnts["htb"]
            cnts["htb"] += 5

            def prenorm_block(blk_):
                hb_ = (hb0 + blk_) % 2
                ci_ = 0 if blk_ < 4 else 1
                for tt in range(4):
                    prenorm_tile(blk_ * 4 + tt, ci_, s, HTB[:, hb_, :, tt * 128:(tt + 1) * 128], Rhtb[hb_], (4 + 2 * (tt % 2), 5 + 2 * (tt % 2)))
            prenorm_block(0)
            pend = [None] * 4
            for blk in range(5):
                ci = 0 if blk < 4 else 1
                hb = (hb0 + blk) % 2
                for j2 in range(NFF // 2):
                    if blk + 1 < 5:
                        nb_, hbn, cin = blk + 1, (hb0 + blk + 1) % 2, (0 if blk + 1 < 4 else 1)
                        if 4 <= j2 <= 7:
                            tt_ = j2 - 4
                            prenorm_p2(pend[tt_], cin, s, HTB[:, hbn, :, tt_ * 128:(tt_ + 1) * 128], Rhtb[hbn], (4 + 2 * (tt_ % 2), 5 + 2 * (tt_ % 2)))
                        if 2 <= j2 <= 5:
                            pend[j2 - 2] = prenorm_p1(nb_ * 4 + (j2 - 2))
                    sl = cnts["w13"] % 3
                    cnts["w13"] += 1
                    S.dma("pool", W13[:, sl, 0, :, :], w1v[:, :, j2 * 256:(j2 + 1) * 256], writes=[Rw13[sl]])
                    S.dma("pool", W13[:, sl, 1, :, :], w3v[:, :, j2 * 256:(j2 + 1) * 256], writes=[Rw13[sl]])
                    for jj in range(2):
                        j = 2 * j2 + jj
                        pa = cnts["pa"] % 2
                        cnts["pa"] += 1
                        b1, b3 = 2 * pa, 2 * pa + 1
                        for (m, bk) in ((0, b1), (1, b3)):
                            def emit(m=m, bk=bk, jj=jj, sl=sl):
                                inst = None
                                for kc in range(8):
                                    inst = nc.tensor.matmul(PS[:, bk, :], lhsT=W13[:, sl, m, kc, jj * 128:(jj + 1) * 128],
                                                            rhs=HTB[:, hb, kc, :], start=(kc == 0), stop=(kc == 7))
                                return inst
                            S.op("pe", emit, reads=[Rw13[sl], Rhtb[hb]], writes=[RPS[bk]])
                        si = cnts["sil"] % 2
                        cnts["sil"] += 1
                        S.op("act", lambda b1=b1, si=si: nc.scalar.activation(out=SIL[:, si, :], in_=PS[:, b1, :], func=AF.Silu),
                             reads=[RPS[b1]], writes=[Rsil[si]])
                        S.op("dve", lambda b3=b3, si=si, j=j: nc.vector.tensor_tensor(out=GT[:, j, :], in0=PS[:, b3, :], in1=SIL[:, si, :], op=ALU.mult),
                             reads=[RPS[b3], Rsil[si]], writes=[Rgt[j]])
                for j2 in range(NFF // 2):
                    sl = cnts["w2"] % 3
                    cnts["w2"] += 1
                    S.dma("pool", W2[:, sl, :, :], w2v[:, 2 * j2:2 * j2 + 2, :], writes=[Rw2[sl]])
                    for jj in range(2):
                        j = 2 * j2 + jj

                        def emit(j=j, jj=jj, sl=sl):
                            inst = None
                            for tt in range(4):
                                for half in range(2):
                                    inst = nc.tensor.matmul(PS[:, 2 * tt + half, :], lhsT=GT[:, j, tt * 128:(tt + 1) * 128],
                                                            rhs=W2[:, sl, jj, half * 512:(half + 1) * 512], start=(j == 0), stop=(j == NFF - 1))
                            return inst
                        S.op("pe", emit, reads=[Rgt[j], Rw2[sl]], writes=RPS)
                for tt in range(4):
                    postnorm_tile(blk * 4 + tt, ci, 2 * tt)

        MOFF = 0
        HT, MOFF = view(MOFF, [128, 8, 2048], BF16)
        BR, MOFF = view(MOFF, [128, 8, 2048], BF16)
        WIN0 = MOFF
        WIN, MOFF = view(MOFF, [128, 2, 8, 256], BF16)
        BS0 = MOFF
        Rht = Reg("HT")
        Rbr = [Reg("BR%d" % i) for i in range(8)]
        Rwin = [Reg("WIN0"), Reg("WIN1")]
        PV = sb("PV", [128, 64])
        Rpv = Reg("PV")
        BG = sb("BG", [128, 32])
        Rbg = Reg("BG")
        mc = {"win": 0, "pb": 0, "wg": 0, "wo": 0, "sg": 0}
        SEQS = [(0, 16, 0, "sample"), (16, 2, 1, "pA"), (18, 2, 1, "pB")]

        def mixer_params(l):
            load_T(PV[:, 0:6], conv_w[l].rearrange("j (c p) -> (j c) p", p=128), 6, Rpv)
            load_T(PV[:, 6:8], conv_b[l].rearrange("(c p) -> c p", p=128), 2, Rpv)
            load_T(PV[:, 8:10], ret_gn[l].rearrange("(c p) -> c p", p=128), 2, Rpv)
            load_T(PV[:, 10:12], s5_d[l].rearrange("(c p) -> c p", p=128), 2, Rpv)
            load_T(PV[:, 12:13], mla_kv_norm[l].rearrange("(c p) -> c p", p=128), 1, Rpv)
            load_T(BG[:, :], b_gate[l].rearrange("(c p) -> c p", p=128), 32, Rbg)

        def proj_fm(l, col0, ncols, NB, BW, evac, extra_reads=()):
            winv = w_in[l].rearrange("(kc p) n -> p kc n", p=128)
            sl = mc["win"] % 2
            mc["win"] += 1
            S.dma("pool", WIN[:, sl, :, 0:ncols], winv[:, :, col0:col0 + ncols], writes=[Rwin[sl]])
            for b in range(NB):
                bank = mc["pb"] % 4
                mc["pb"] += 1

                def emit(b=b, bank=bank):
                    inst = None
                    for kc in range(8):
                        inst = nc.tensor.matmul(PS[0:ncols, bank, 0:BW], lhsT=WIN[:, sl, kc, 0:ncols], rhs=cur["HT"][:, kc, b * BW:(b + 1) * BW],
                                                start=(kc == 0), stop=(kc == 7))
                    return inst
                S.op("pe", emit, reads=[Rwin[sl], Rht], writes=[RPS[bank]])
                evac(b, bank)

        def branch_conv(l, T, NB, BW):
            o = BS0
            Z, o = view(o, [128, 2056])
            CX, o = view(o, [128, 2048])
            Y, o = view(o, [128, 2048])
            CB, o = view(o, [128, 2048], BF16)
            Rz, Rcx, Ry, Rcb = Reg("Z"), Reg("CX"), Reg("Y"), Reg("CB")
            for cc in range(2):
                S.op("dve", lambda: nc.vector.memset(Z[:, 0:1], 0.0), writes=[Rz])
                S.op("dve", lambda: nc.vector.memset(Z[:, T + 1:T + 2], 0.0), writes=[Rz])
                proj_fm(l, 1280 + cc * 128, 128, NB, BW, lambda b, bank: S.op(
                    "act", lambda: nc.scalar.copy(out=CX[:, b * BW:(b + 1) * BW], in_=PS[:, bank, 0:BW]), reads=[RPS[bank]], writes=[Rcx]))
                proj_fm(l, 1792 + cc * 128, 128, NB, BW, lambda b, bank: S.op(
                    "dve", lambda: nc.vector.tensor_tensor(out=Z[:, 1 + b * BW:1 + (b + 1) * BW], in0=PS[:, bank, 0:BW], in1=CX[:, b * BW:(b + 1) * BW], op=ALU.mult),
                    reads=[RPS[bank], Rcx], writes=[Rz]))
                proj_fm(l, 1536 + cc * 128, 128, NB, BW, lambda b, bank: S.op(
                    "act", lambda: nc.scalar.copy(out=CB[:, b * BW:(b + 1) * BW], in_=PS[:, bank, 0:BW]), reads=[RPS[bank]], writes=[Rcb]))
                S.op("dve", lambda: nc.vector.tensor_scalar(out=Y[:, 0:T], in0=Z[:, 1:T + 1], scalar1=PV[:, 2 + cc:3 + cc], scalar2=PV[:, 6 + cc:7 + cc],
                                                            op0=ALU.mult, op1=ALU.add), reads=[Rz, Rpv], writes=[Ry])
                S.op("dve", lambda: nc.vector.scalar_tensor_tensor(out=Y[:, 0:T], in0=Z[:, 0:T], scalar=PV[:, 0 + cc:1 + cc], in1=Y[:, 0:T],
                                                                   op0=ALU.mult, op1=ALU.add), reads=[Rz, Rpv, Ry], writes=[Ry])
                S.op("dve", lambda: nc.vector.scalar_tensor_tensor(out=Y[:, 0:T], in0=Z[:, 2:T + 2], scalar=PV[:, 4 + cc:5 + cc], in1=Y[:, 0:T],
                                                                   op0=ALU.mult, op1=ALU.add), reads=[Rz, Rpv, Ry], writes=[Ry])
                S.op("dve", lambda: nc.vector.tensor_tensor(out=cur["BR"][:, 4 + cc, 0:T], in0=Y[:, 0:T], in1=CB[:, 0:T], op=ALU.mult),
                     reads=[Ry, Rcb], writes=[Rbr[4 + cc]])

        def gate_stage(l, t0, ntile, ci, T, NB, BW):
            o = WIN0
            WG, o = view(o, [128, 2, 8, 4, 128], BF16)
            WB, o = view(o, [128, 2, 2, 4, 128], BF16)
            MG, o = view(o, [128, 8, 512], BF16)
            SG, o = view(o, [128, 4, 512], BF16)
            WO, o = view(o, [128, 2, D], BF16)
            ACC, o = view(o, [128, 2, 512])
            cur["TMP"], o = view(o, [128, 2, D])
            Rwg = [Reg("WG0"), Reg("WG1")]
            Rmg = [Reg("MG%d" % c) for c in range(8)]
            Rsg = [Reg("SG%d" % n) for n in range(4)]
            Rwo = [Reg("WO0"), Reg("WO1")]
            Racc = [Reg("ACC0"), Reg("ACC1")]
            wgv = w_gate[l].rearrange("(kc p) (n d) -> p kc n d", p=128, n=4)
            wbv = w_branch[l].rearrange("n (kc p) d -> p kc n d", p=128)
            tpb = BW // 128
            for b in range(NB):
                for c in range(8):
                    sl = mc["wg"] % 2
                    mc["wg"] += 1
                    for n in range(4):
                        S.dma("pool", WG[:, sl, :, n, :], wgv[:, :, n, c * 128:(c + 1) * 128], writes=[Rwg[sl]])
                    for n in range(4):
                        S.dma("pool", WB[:, sl, :, n, :], wbv[:, :, n, c * 128:(c + 1) * 128], writes=[Rwg[sl]])
                    for n in range(4):
                        def emit_g(n=n, sl=sl):
                            inst = None
                            for kc in range(8):
                                inst = nc.tensor.matmul(PS[:, n, 0:BW], lhsT=WG[:, sl, kc, n, :], rhs=cur["HT"][:, kc, b * BW:(b + 1) * BW],
                                                        start=(kc == 0), stop=(kc == 7))
                            return inst
                        S.op("pe", emit_g, reads=[Rwg[sl], Rht], writes=[RPS[n]])

                        def emit_p(n=n, sl=sl):
                            inst = None
                            for kc in range(2):
                                inst = nc.tensor.matmul(PS[:, 4 + n, 0:BW], lhsT=WB[:, sl, kc, n, :], rhs=cur["BR"][:, 2 * n + kc, b * BW:(b + 1) * BW],
                                                        start=(kc == 0), stop=(kc == 1))
                            return inst
                        S.op("pe", emit_p, reads=[Rwg[sl], Rbr[2 * n], Rbr[2 * n + 1]], writes=[RPS[4 + n]])
                        S.op("act", lambda n=n: nc.scalar.activation(out=SG[:, n, 0:BW], in_=PS[:, n, 0:BW], func=AF.Sigmoid,
                                                                     bias=BG[:, n * 8 + c:n * 8 + c + 1]), reads=[RPS[n], Rbg], writes=[Rsg[n]])
                    S.op("dve", lambda: nc.vector.tensor_tensor(out=ACC[:, 0, 0:BW], in0=PS[:, 4, 0:BW], in1=SG[:, 0, 0:BW], op=ALU.mult),
                         reads=[RPS[4], Rsg[0]], writes=[Racc[0]])
                    for n in range(1, 4):
                        S.op("dve", lambda n=n: nc.vector.tensor_tensor(out=ACC[:, 1, 0:BW], in0=PS[:, 4 + n, 0:BW], in1=SG[:, n, 0:BW], op=ALU.mult),
                             reads=[RPS[4 + n], Rsg[n]], writes=[Racc[1]])
                        if n < 3:
                            S.op("dve", lambda: nc.vector.tensor_tensor(out=ACC[:, 0, 0:BW], in0=ACC[:, 0, 0:BW], in1=ACC[:, 1, 0:BW], op=ALU.add),
                                 reads=[Racc[0], Racc[1]], writes=[Racc[0]])
                        else:
                            S.op("dve", lambda: nc.vector.tensor_tensor(out=MG[:, c, 0:BW], in0=ACC[:, 0, 0:BW], in1=ACC[:, 1, 0:BW], op=ALU.add),
                                 reads=[Racc[0], Racc[1]], writes=[Rmg[c]])
                for c in range(8):
                    sl = mc["wo"] % 2
                    mc["wo"] += 1
                    S.dma("pool", WO[:, sl, :], w_o[l, c * 128:(c + 1) * 128, :], writes=[Rwo[sl]])

                    def emit_o(c=c, sl=sl):
                        inst = None
                        for tt in range(tpb):
                            for half in range(2):
                                inst = nc.tensor.matmul(PS[:, 2 * tt + half, :], lhsT=MG[:, c, tt * 128:(tt + 1) * 128],
                                                        rhs=WO[:, sl, half * 512:(half + 1) * 512], start=(c == 0), stop=(c == 7))
                        return inst
                    S.op("pe", emit_o, reads=[Rmg[c], Rwo[sl]], writes=RPS[0:2 * tpb])
                for tt in range(tpb):
                    postnorm_tile(t0 + b * tpb + tt, ci, 2 * tt)

        onesb = sb("onesb", [128, 128], BF16)
        S.op("dve", lambda: nc.vector.memset(onesb[:], 1.0), writes=[Rones])
        ATT_SCALE = float(96 ** -0.5)

        def branch_mla(l, t0, T, NB, BW, kind):
            sample = kind == "sample"
            Skeys = T + (512 if sample else 0)
            NKT = Skeys // 128
            o = BS0
            CQ, o = view(o, [128, 2, 2048], BF16)
            CKVN, o = view(o, [128, 2560], BF16)
            KR, o = view(o, [128, 2560], BF16)
            WUQ, o = view(o, [128, 2, 384], BF16)
            WUQS, o = view(o, [128, 2, 4, 32], BF16)
            WUKV, o = view(o, [128, 512], BF16)
            WKRS, o = view(o, [128, 8, 32], BF16)
            QNV, o = view(o, [128, 2])
            oB = o
            Rcq, Rckvn, Rkr, Rw = Reg("m_CQ"), Reg("m_CKVN"), Reg("m_KR"), Reg("m_W")
            W32, oo = view(oB, [128, 2, 384])
            Rw32 = Reg("m_W32")
            S.dma("sp", W32[:, 0, :], mla_w_uq[l, 0:128, :], writes=[Rw32])
            S.dma("sp", W32[0:64, 1, :], mla_w_uq[l, 128:192, :], writes=[Rw32])
            S.dma("pool", QNV[:, 0:1], mla_q_norm[l, 0:128].unsqueeze(1), writes=[Rw])
            S.dma("pool", QNV[0:64, 1:2], mla_q_norm[l, 128:192].unsqueeze(1), writes=[Rw])
            S.dma("pool", WUKV[:, :], mla_w_ukv[l, :, :], writes=[Rw])
            for kc, np_ in ((0, 128), (1, 64)):
                S.op("dve", lambda kc=kc, np_=np_: nc.vector.tensor_scalar(out=WUQ[0:np_, kc, :], in0=W32[0:np_, kc, :], scalar1=QNV[0:np_, kc:kc + 1],
                                                                          scalar2=None, op0=ALU.mult), reads=[Rw32, Rw], writes=[Rw])
                if sample:
                    wv = WUQ[0:np_, kc, :].rearrange("p (h e) -> p h e", h=4)
                    S.op("dve", lambda wv=wv, kc=kc, np_=np_: nc.vector.tensor_scalar(out=WUQS[0:np_, kc, :, 0:16], in0=wv[:, :, 80:96], scalar1=-1.0,
                                                                                   scalar2=None, op0=ALU.mult), reads=[Rw], writes=[Rw])
                    S.op("dve", lambda wv=wv, kc=kc, np_=np_: nc.vector.tensor_copy(out=WUQS[0:np_, kc, :, 16:32], in_=wv[:, :, 64:80]), reads=[Rw], writes=[Rw])
            if sample:
                winv = w_in[l].rearrange("(kc p) n -> p kc n", p=128)
                S.dma("pool", WKRS[:, :, 0:16], winv[:, :, 2384:2400], writes=[Rw])
                S.dma("pool", WKRS[:, :, 16:32], winv[:, :, 2368:2384], writes=[Rw])
                S.op("dve", lambda: nc.vector.tensor_scalar(out=WKRS[:, :, 0:16], in0=WKRS[:, :, 0:16], scalar1=-1.0, scalar2=None, op0=ALU.mult),
                     reads=[Rw], writes=[Rw])
            SQ, oo = view(oo, [128, 2, 512], BF16)
            RST, oo = view(oo, [128, 512])
            TB, oo = view(oo, [128, 2, 512])
            T1, oo = view(oo, [128, 2, 512])
            Rsq, Rrst, Rtb, Rt1 = Reg("m_SQ"), Reg("m_RST"), Reg("m_TB"), Reg("m_T1")

            def rstd_from_ps(bank, parts):
                S.op("act", lambda: nc.scalar.activation(out=RST[:, 0:BW], in_=PS[:, bank, 0:BW], func=AF.Sqrt, scale=1.0 / parts, bias=epsb[:, 0:1]),
                     reads=[RPS[bank], Reps], writes=[Rrst])
                S.op("dve", lambda: nc.vector.reciprocal(out=RST[:, 0:BW], in_=RST[:, 0:BW]), reads=[Rrst], writes=[Rrst])

            winv = w_in[l].rearrange("(kc p) n -> p kc n", p=128)
            WQ = WIN
            S.dma("pool", WQ[:, 0, :, 0:192], winv[:, :, 2048:2240], writes=[Rwin[0]])
            S.dma("pool", WQ[:, 1, :, 0:160], winv[:, :, 2240:2400], writes=[Rwin[1]])
            for b in range(NB):
                cols = slice(b * BW, (b + 1) * BW)
                for kc2, np_, bank in ((0, 128, 0), (1, 64, 1)):
                    def emit(kc2=kc2, np_=np_, bank=bank):
                        inst = None
                        for kc in range(8):
                            inst = nc.tensor.matmul(PS[0:np_, bank, 0:BW], lhsT=WQ[:, 0, kc, kc2 * 128:kc2 * 128 + np_], rhs=cur["HT"][:, kc, cols],
                                                    start=(kc == 0), stop=(kc == 7))
                        return inst
                    S.op("pe", emit, reads=[Rwin[0], Rht], writes=[RPS[bank]])
                    S.op("act", lambda kc2=kc2, np_=np_, bank=bank: nc.scalar.activation(out=SQ[0:np_, kc2, 0:BW], in_=PS[0:np_, bank, 0:BW], func=AF.Square),
                         reads=[RPS[bank]], writes=[Rsq])

                def emit_ss():
                    nc.tensor.matmul(PS[:, 2, 0:BW], lhsT=onesb[:, :], rhs=SQ[:, 0, 0:BW], start=True, stop=False)
                    return nc.tensor.matmul(PS[:, 2, 0:BW], lhsT=onesb[0:64, :], rhs=SQ[0:64, 1, 0:BW], start=False, stop=True)
                S.op("pe", emit_ss, reads=[Rsq, Rones], writes=[RPS[2]])
                rstd_from_ps(2, 192.0)
                for kc2, np_, bank in ((0, 128, 0), (1, 64, 1)):
                    S.op("dve", lambda kc2=kc2, np_=np_, bank=bank: nc.vector.tensor_tensor(out=CQ[0:np_, kc2, cols], in0=PS[0:np_, bank, 0:BW], in1=RST[0:np_, 0:BW], op=ALU.mult),
                         reads=[RPS[bank], Rrst], writes=[Rcq])
                def emit_kv():
                    inst = None
                    for kc in range(8):
                        inst = nc.tensor.matmul(PS[:, 3, 0:BW], lhsT=WQ[:, 1, kc, 0:128], rhs=cur["HT"][:, kc, cols], start=(kc == 0), stop=(kc == 7))
                    return inst
                S.op("pe", emit_kv, reads=[Rwin[1], Rht], writes=[RPS[3]])
                S.op("act", lambda: nc.scalar.activation(out=SQ[:, 0, 0:BW], in_=PS[:, 3, 0:BW], func=AF.Square), reads=[RPS[3]], writes=[Rsq])
                S.op("pe", lambda: nc.tensor.matmul(PS[:, 2, 0:BW], lhsT=onesb[:, :], rhs=SQ[:, 0, 0:BW], start=True, stop=True), reads=[Rsq, Rones], writes=[RPS[2]])
                rstd_from_ps(2, 128.0)
                S.op("dve", lambda: nc.vector.scalar_tensor_tensor(out=CKVN[:, cols], in0=PS[:, 3, 0:BW], scalar=PV[:, 12:13], in1=RST[:, 0:BW], op0=ALU.mult, op1=ALU.mult),
                     reads=[RPS[3], Rpv, Rrst], writes=[Rckvn])
                def emit_kr():
                    inst = None
                    for kc in range(8):
                        inst = nc.tensor.matmul(PS[0:32, 4, 0:BW], lhsT=WQ[:, 1, kc, 128:160], rhs=cur["HT"][:, kc, cols], start=(kc == 0), stop=(kc == 7))
                    return inst
                S.op("pe", emit_kr, reads=[Rwin[1], Rht], writes=[RPS[4]])
                if sample:
                    def emit_krs():
                        inst = None
                        for kc in range(8):
                            inst = nc.tensor.matmul(PS[0:32, 5, 0:BW], lhsT=WKRS[:, kc, :], rhs=cur["HT"][:, kc, cols], start=(kc == 0), stop=(kc == 7))
                        return inst
                    S.op("pe", emit_krs, reads=[Rw, Rht], writes=[RPS[5]])
                    S.dma("sp", TB[0:32, 0, 0:BW], c_rope_mla[0, :, cols], writes=[Rtb])
                    S.dma("sp", TB[0:32, 1, 0:BW], c_rope_mla[1, :, cols], writes=[Rtb])
                    S.op("dve", lambda: nc.vector.tensor_tensor(out=T1[0:32, 0, 0:BW], in0=PS[0:32, 4, 0:BW], in1=TB[0:32, 0, 0:BW], op=ALU.mult),
                         reads=[RPS[4], Rtb], writes=[Rt1])
                    S.op("dve", lambda: nc.vector.tensor_tensor(out=T1[0:32, 1, 0:BW], in0=PS[0:32, 5, 0:BW], in1=TB[0:32, 1, 0:BW], op=ALU.mult),
                         reads=[RPS[5], Rtb], writes=[Rt1])
                    S.op("dve", lambda: nc.vector.tensor_tensor(out=KR[0:32, cols], in0=T1[0:32, 0, 0:BW], in1=T1[0:32, 1, 0:BW], op=ALU.add),
                         reads=[Rt1], writes=[Rkr])
                else:
                    S.op("act", lambda: nc.scalar.copy(out=KR[0:32, cols], in_=PS[0:32, 4, 0:BW]), reads=[RPS[4]], writes=[Rkr])
            if sample:
                CT, _ = view(oB + 3072, [128, 4, 160])
                Rct = Reg("m_CT")
                S.barrier()
                S.dma("sp", CT[:, :, :], ctx_mla[l].rearrange("(i p) f -> p i f", p=128), writes=[Rct])
                for i in range(4):
                    S.op("pe", lambda i=i: nc.tensor.transpose(out=PS[:, 6, 0:128], in_=CT[:, i, 0:128], identity=ident[:]), reads=[Rct, Rid], writes=[RPS[6]])
                    S.op("act", lambda i=i: nc.scalar.copy(out=CKVN[:, T + i * 128:T + (i + 1) * 128], in_=PS[:, 6, 0:128]), reads=[RPS[6]], writes=[Rckvn])
                    S.op("pe", lambda i=i: nc.tensor.transpose(out=PS[0:32, 7, 0:128], in_=CT[:, i, 128:160], identity=ident[:]), reads=[Rct, Rid], writes=[RPS[7]])
                    S.op("act", lambda i=i: nc.scalar.copy(out=KR[0:32, T + i * 128:T + (i + 1) * 128], in_=PS[0:32, 7, 0:128]), reads=[RPS[7]], writes=[Rkr])
            S.barrier()
            o = oB
            KN, o = view(o, [128, 2560], BF16)
            QN, o = view(o, [128, 2048], BF16)
            QR, o = view(o, [128, 2048], BF16)
            VA, o = view(o, [128, 20, 66], BF16)
            Rkn, Rqn, Rqr, Rva = Reg("m_KN"), Reg("m_QN"), Reg("m_QR"), Reg("m_VA")
            ow = WIN0
            TB2, ow2 = view(ow, [128, 2, 512])
            T2, ow2 = view(ow2, [128, 2, 512])
            PT, ow3 = view(ow, [128, 2, 512], BF16)
            OS, ow3 = view(ow3, [128, 512])
            OT, ow3 = view(ow3, [128, 512], BF16)
            Rtb2, Rt2, Rpt, Ros, Rot = Reg("m_TB2"), Reg("m_T2"), [Reg("m_PT0"), Reg("m_PT1")], Reg("m_OS"), Reg("m_OT")
            S.op("dve", lambda: nc.vector.memset(VA[:, :, 64:66], 1.0), writes=[Rva])
            S.dma("sp", KN[64:96, 0:Skeys], KR[0:32, 0:Skeys], reads=[Rkr], writes=[Rkn], key=Reg("m_KRd"))
            KBW = 512
            for h in range(4):
                for kb in range((Skeys + KBW - 1) // KBW):
                    w = min(KBW, Skeys - kb * KBW)
                    bank = mc["pb"] % 4
                    mc["pb"] += 1
                    S.op("pe", lambda kb=kb, w=w, bank=bank: nc.tensor.matmul(PS[0:64, bank, 0:w], lhsT=WUKV[:, h * 128:h * 128 + 64], rhs=CKVN[:, kb * KBW:kb * KBW + w],
                                                                              start=True, stop=True), reads=[Rw, Rckvn], writes=[RPS[bank]])
                    S.op("act", lambda kb=kb, w=w, bank=bank: nc.scalar.copy(out=KN[0:64, kb * KBW:kb * KBW + w], in_=PS[0:64, bank, 0:w]), reads=[RPS[bank]], writes=[Rkn])
                for kt in range(NKT):
                    bank = mc["pb"] % 4
                    mc["pb"] += 1
                    S.op("pe", lambda kt=kt, bank=bank: nc.tensor.matmul(PS[:, bank, 0:64], lhsT=CKVN[:, kt * 128:(kt + 1) * 128], rhs=WUKV[:, h * 128 + 64:h * 128 + 128],
                                                                          start=True, stop=True), reads=[Rw, Rckvn], writes=[RPS[bank]])
                    S.op("dve", lambda kt=kt, bank=bank: nc.vector.tensor_copy(out=VA[:, kt, 0:64], in_=PS[:, bank, 0:64]), reads=[RPS[bank]], writes=[Rva])
                for b in range(NB):
                    cols = slice(b * BW, (b + 1) * BW)
                    bank = mc["pb"] % 4
                    mc["pb"] += 1

                    def emit_q(c0, m, bank, wt=None, p0=0):
                        def f():
                            po = PS[p0:p0 + m, bank, 0:BW]
                            if wt is None:
                                nc.tensor.matmul(po, lhsT=WUQ[:, 0, c0:c0 + m], rhs=CQ[:, 0, cols], start=True, stop=False)
                                return nc.tensor.matmul(po, lhsT=WUQ[0:64, 1, c0:c0 + m], rhs=CQ[0:64, 1, cols], start=False, stop=True)
                            nc.tensor.matmul(po, lhsT=WUQS[:, 0, h, :], rhs=CQ[:, 0, cols], start=True, stop=False)
                            return nc.tensor.matmul(po, lhsT=WUQS[0:64, 1, h, :], rhs=CQ[0:64, 1, cols], start=False, stop=True)
                        return f
                    S.op("pe", emit_q(h * 96, 64, bank), reads=[Rw, Rcq], writes=[RPS[bank]])
                    S.op("act", lambda bank=bank: nc.scalar.copy(out=QN[0:64, cols], in_=PS[0:64, bank, 0:BW]), reads=[RPS[bank]], writes=[Rqn])
                    bank2 = mc["pb"] % 4
                    mc["pb"] += 1
                    S.op("pe", emit_q(h * 96 + 64, 32, bank2, p0=64), reads=[Rw, Rcq], writes=[RPS[bank2]])
                    if sample:
                        bank3 = mc["pb"] % 4
                        mc["pb"] += 1
                        S.op("pe", emit_q(0, 32, bank3, wt=1, p0=64), reads=[Rw, Rcq], writes=[RPS[bank3]])
                        S.dma("sp", TB2[64:96, 0, 0:BW], c_rope_mla[0, :, cols], writes=[Rtb2])
                        S.dma("sp", TB2[64:96, 1, 0:BW], c_rope_mla[1, :, cols], writes=[Rtb2])
                        S.op("dve", lambda: nc.vector.tensor_tensor(out=T2[64:96, 0, 0:BW], in0=PS[64:96, bank2, 0:BW], in1=TB2[64:96, 0, 0:BW], op=ALU.mult),
                             reads=[RPS[bank2], Rtb2], writes=[Rt2])
                        S.op("dve", lambda: nc.vector.tensor_tensor(out=T2[64:96, 1, 0:BW], in0=PS[64:96, bank3, 0:BW], in1=TB2[64:96, 1, 0:BW], op=ALU.mult),
                             reads=[RPS[bank3], Rtb2], writes=[Rt2])
                        S.op("dve", lambda: nc.vector.tensor_tensor(out=QN[64:96, cols], in0=T2[64:96, 0, 0:BW], in1=T2[64:96, 1, 0:BW], op=ALU.add),
                             reads=[Rt2], writes=[Rqn])
                    else:
                        S.op("act", lambda: nc.scalar.copy(out=QN[64:96, cols], in_=PS[64:96, bank2, 0:BW]), reads=[RPS[bank2]], writes=[Rqn])
                S.barrier()
                for b in range(NB):
                    cols = slice(b * BW, (b + 1) * BW)
                    ob = 4 + (b % 2)
                    for kt in range(NKT):
                        bank = mc["pb"] % 4
                        mc["pb"] += 1
                        pi = kt % 2

                        S.op("pe", lambda kt=kt, bank=bank: nc.tensor.matmul(PS[:, bank, 0:BW], lhsT=KN[0:96, kt * 128:(kt + 1) * 128], rhs=QN[0:96, cols], start=True, stop=True),
                             reads=[Rkn, Rqn], writes=[RPS[bank]])
                        S.op("act", lambda bank=bank, pi=pi: nc.scalar.activation(out=PT[:, pi, 0:BW], in_=PS[:, bank, 0:BW], func=AF.Exp, scale=ATT_SCALE),
                             reads=[RPS[bank]], writes=[Rpt[pi]])
                        S.op("pe", lambda kt=kt, pi=pi: nc.tensor.matmul(PS[0:65, ob, 0:BW], lhsT=VA[:, kt, 0:65], rhs=PT[:, pi, 0:BW], start=(kt == 0), stop=(kt == NKT - 1)),
                             reads=[Rva, Rpt[pi]], writes=[RPS[ob]])
                    S.op("act", lambda: nc.scalar.copy(out=OS[0:65, 0:BW], in_=PS[0:65, ob, 0:BW]), reads=[RPS[ob]], writes=[Ros])
                    S.op("dve", lambda: nc.vector.reciprocal(out=OS[64:65, 0:BW], in_=OS[64:65, 0:BW]), reads=[Ros], writes=[Ros])
                    S.op("pe", lambda: nc.tensor.matmul(PS[0:64, 6, 0:BW], lhsT=ones32[64:65, 0:64], rhs=OS[64:65, 0:BW], start=True, stop=True),
                         reads=[Ros, Rones], writes=[RPS[6]])
                    S.op("dve", lambda: nc.vector.tensor_tensor(out=OT[0:64, 0:BW], in0=PS[0:64, 6, 0:BW], in1=OS[0:64, 0:BW], op=ALU.mult),
                         reads=[RPS[6], Ros], writes=[Rot])
                    S.dma("sp", cur["BR"][(h % 2) * 64:(h % 2) * 64 + 64, 6 + h // 2, cols], OT[0:64, 0:BW], reads=[Rot], writes=[Rbr[6 + h // 2]], key=Reg("m_OTd"))
                S.barrier()

        def mla_cache_out(l, t0, ntile, pi):
            o = BS0
            CA, o = view(o, [128, 2, 160])
            KVB, o = view(o, [128, 128])
            Rca, Rkvb = [Reg("m_CA0"), Reg("m_CA1")], Reg("m_KVB")
            winv = w_in[l].rearrange("(kc p) n -> p kc n", p=128)
            S.dma("pool", WIN[:, 0, :, 0:160], winv[:, :, 2240:2400], writes=[Rwin[0]])
            S.dma("sp", KVB[:, :], mla_kv_norm[l:l + 1, :].broadcast_to([128, 128]), writes=[Rkvb])
            for tt in range(ntile):
                bank = mc["pb"] % 4
                mc["pb"] += 1
                smi = rot["sm"] % 4
                rot["sm"] += 1
                sm = small[:, smi, :]

                def emit(tt=tt, bank=bank):
                    inst = None
                    for kc in range(8):
                        inst = nc.tensor.matmul(PS[:, bank, 0:160], lhsT=cur["HT"][:, kc, tt * 128:(tt + 1) * 128], rhs=WIN[:, 0, kc, 0:160], start=(kc == 0), stop=(kc == 7))
                    return inst
                S.op("pe", emit, reads=[Rwin[0], Rht], writes=[RPS[bank]])
                S.op("act", lambda: nc.scalar.activation(out=junk[:, 0:128], in_=PS[:, bank, 0:128], func=AF.Square, accum_out=sm[:, 0:1]),
                     reads=[RPS[bank]], writes=[Rjunk, Rsmall[smi]])
                S.op("act", lambda: nc.scalar.activation(out=sm[:, 1:2], in_=sm[:, 0:1], func=AF.Sqrt, scale=1.0 / 128, bias=epsb[:, 0:1]),
                     reads=[Rsmall[smi], Reps], writes=[Rsmall[smi]])
                S.op("dve", lambda: nc.vector.reciprocal(out=sm[:, 2:3], in_=sm[:, 1:2]), reads=[Rsmall[smi]], writes=[Rsmall[smi]])
                ci_ = tt % 2
                S.op("dve", lambda: nc.vector.scalar_tensor_tensor(out=CA[:, ci_, 0:128], in0=PS[:, bank, 0:128], scalar=sm[:, 2:3], in1=KVB[:, :], op0=ALU.mult, op1=ALU.mult),
                     reads=[RPS[bank], Rsmall[smi], Rkvb], writes=[Rca[ci_]])
                S.op("dve", lambda: nc.vector.tensor_copy(out=CA[:, ci_, 128:160], in_=PS[:, bank, 128:160]), reads=[RPS[bank]], writes=[Rca[ci_]])
                OUT_EVS.append(S.dma("sp", o_mla[pi, l, tt * 128:(tt + 1) * 128, :], CA[:, ci_, :], reads=[Rca[ci_]], key=Reg("o_mla_d%d" % ci_)))

        def branch_ret(l, t0, T, kind):
            sample = kind == "sample"
            n = T // 128
            o = BS0
            WRb, o = view(o, [128, 8, 512], BF16)
            SBst, o = view(o, [128, 16, 256], BF16)
            DM, o = view(o, [128, 4, 128], BF16)
            QD, o = view(o, [128, 2, 4, 128], BF16)
            CD, o = view(o, [128, 2, 256])
            LG, o = view(o, [128, 8])
            KD, o = view(o, [128, 2, 4])
            SF, o = view(o, [128, 256])
            SB, o = view(o, [128, 256])
            SFb, o = view(o, [128, 256], BF16)
            oT = o
            WRa, _ = view(WIN0, [128, 8, 512], BF16)
            Rwr, Rsbst, Rtab, Rsf, Rsb, Rsfb = Reg("r_WR"), Reg("r_SBst"), Reg("r_TAB"), Reg("r_SF"), Reg("r_SB"), Reg("r_SFb")
            winv = w_in[l].rearrange("(kc p) n -> p kc n", p=128)
            S.dma("pool", WRa[:, :, :], winv[:, :, 256:768], writes=[Rwr])
            S.dma("pool", WRb[:, :, :], winv[:, :, 768:1280], writes=[Rwr])
            CT6, o2 = view(oT, [128, 6, 128])
            E1, o2 = view(o2, [128, 2, 128])
            PIDX, o2 = view(o2, [128, 2])
            C128, o2 = view(o2, [128, 64])
            Rc6, Re1 = Reg("r_C6"), Reg("r_E1")
            S.dma("sp", CT6[:, :, :], c_ret.rearrange("k p i -> p k i"), writes=[Rc6])
            S.dma("sp", PIDX[:, :], c_pidx[:, :], writes=[Rc6])
            S.dma("sp", LG[:, :], ret_decay[l:l + 1].rearrange("o d h -> o (d h)").broadcast_to([128, 8]), writes=[Rtab])
            S.op("dve", lambda: nc.vector.memset(C128[:, :], 128.0), writes=[Rc6])
            S.op("act", lambda: nc.scalar.activation(out=LG[:, :], in_=LG[:, :], func=AF.Sigmoid), reads=[Rtab], writes=[Rtab])
            S.op("act", lambda: nc.scalar.activation(out=LG[:, :], in_=LG[:, :], func=AF.Ln), reads=[Rtab], writes=[Rtab])
            for h in range(4):
                S.op("act", lambda h=h: nc.scalar.activation(out=E1[:, 0, :], in_=CT6[:, 0, :], func=AF.Exp, scale=LG[:, h:h + 1]), reads=[Rc6, Rtab], writes=[Re1])
                S.op("act", lambda h=h: nc.scalar.activation(out=E1[:, 1, :], in_=CT6[:, 1, :], func=AF.Exp, scale=LG[:, 4 + h:5 + h]), reads=[Rc6, Rtab], writes=[Re1])
                S.op("dve", lambda h=h: nc.vector.tensor_tensor(out=E1[:, :, :], in0=E1[:, :, :], in1=CT6[:, 2:4, :], op=ALU.mult), reads=[Re1, Rc6], writes=[Re1])
                S.op("dve", lambda h=h: nc.vector.tensor_tensor(out=DM[:, h, :], in0=E1[:, 0, :], in1=E1[:, 1, :], op=ALU.add), reads=[Re1], writes=[Rtab])
                for d in range(2):
                    S.op("act", lambda h=h, d=d: nc.scalar.activation(out=QD[:, d, h, :], in_=CT6[:, 4 + d, :], func=AF.Exp, scale=LG[:, d * 4 + h:d * 4 + h + 1]),
                         reads=[Rc6, Rtab], writes=[Rtab])
                    S.op("act", lambda h=h, d=d: nc.scalar.activation(out=CD[:, d, h * 64:(h + 1) * 64], in_=C128[:, :], func=AF.Exp, scale=LG[:, d * 4 + h:d * 4 + h + 1]),
                         reads=[Rc6, Rtab], writes=[Rtab])
                    S.op("act", lambda h=h, d=d: nc.scalar.activation(out=KD[:, d, h:h + 1], in_=PIDX[:, d:d + 1], func=AF.Exp, scale=LG[:, d * 4 + h:d * 4 + h + 1]),
                         reads=[Rc6, Rtab], writes=[Rtab])
            S.op("dve", lambda: nc.vector.tensor_scalar(out=KD[:, :, :], in0=KD[:, :, :], scalar1=0.125, scalar2=None, op0=ALU.mult), reads=[Rtab], writes=[Rtab])
            if sample:
                S.dma("sp", SF[0:64, :].rearrange("d (h e) -> d h e", h=4), st_ret[l, 0].rearrange("h d e -> d h e"), writes=[Rsf])
                S.dma("sp", SB[0:64, :].rearrange("d (h e) -> d h e", h=4), st_ret[l, 1].rearrange("h d e -> d h e"), writes=[Rsb])
            else:
                S.op("dve", lambda: nc.vector.memset(SF[0:64, :], 0.0), writes=[Rsf])
                S.op("dve", lambda: nc.vector.memset(SB[0:64, :], 0.0), writes=[Rsb])
            S.barrier()
            if cfg.get("ret_stop") == "tables":
                return
            o3 = oT
            QK, o3 = view(o3, [128, 512], BF16)
            TA, o3 = view(o3, [128, 256])
            TBt, o3 = view(o3, [128, 256])
            RT, o3 = view(o3, [128, 2, 32])
            KDt, o3 = view(o3, [128, 256], BF16)
            VTc, o3 = view(o3, [128, 256], BF16)
            SRG, o3 = view(o3, [128, 256], BF16)
            QT, o3 = view(o3, [128, 3, 512], BF16)
            KT, o3 = view(o3, [128, 512], BF16)
            AM, o3 = view(o3, [128, 512], BF16)
            CEN, o3 = view(o3, [128, 256])
            SQr, o3 = view(o3, [128, 256])
            NRo, o3 = view(o3, [128, 256], BF16)
            MS, o3 = view(o3, [128, 8])
            Rqk, Rta, Rrt, Rkd, Rvt, Rsrg, Rqt, Rkt, Ram, Rcen, Rsq, Rnro, Rms = (Reg("r_" + x) for x in
                ("QK", "TA", "RT", "KDt", "VTc", "SRG", "QT", "KT", "AM", "CEN", "SQ", "NRo", "MS"))
            PSb = lambda bank: PS[:, bank, :].bitcast(BF16)

            def proj(c, bank, WRx, c0, ncol):
                def emit():
                    inst = None
                    for kc in range(8):
                        inst = nc.tensor.matmul(PS[:, bank, 0:ncol], lhsT=cur["HT"][:, kc, c * 128:(c + 1) * 128], rhs=WRx[:, kc, c0:c0 + ncol], start=(kc == 0), stop=(kc == 7))
                    return inst
                S.op("pe", emit, reads=[Rwr, Rht], writes=[RPS[bank]])

            def rope(c, bank, col0, ng, dst):
                src = PS[:, bank, col0:col0 + ng * 64].rearrange("p (g t e) -> p g t e", g=ng, t=2)
                dv = dst.rearrange("p (g t e) -> p g t e", g=ng, t=2)
                if not sample:
                    S.op("act", lambda: nc.scalar.copy(out=dst, in_=PS[:, bank, col0:col0 + ng * 64]), reads=[RPS[bank]], writes=[Rqk])
                    return
                S.dma("sp", RT[:, 0, :], c_rope_ret[0, c * 128:(c + 1) * 128, :], writes=[Rrt])
                S.dma("sp", RT[:, 1, :], c_rope_ret[1, c * 128:(c + 1) * 128, :], writes=[Rrt])
                cosb = RT[:, 0, :].unsqueeze(1).broadcast_to([128, ng, 32])
                sinb = RT[:, 1, :].unsqueeze(1).broadcast_to([128, ng, 32])
                ta = TA[:, 0:ng * 32].rearrange("p (g e) -> p g e", g=ng)
                tb = TBt[:, 0:ng * 32].rearrange("p (g e) -> p g e", g=ng)
                S.op("dve", lambda: nc.vector.tensor_tensor(out=ta, in0=src[:, :, 0, :], in1=cosb, op=ALU.mult), reads=[RPS[bank], Rrt], writes=[Rta])
                S.op("dve", lambda: nc.vector.tensor_tensor(out=tb, in0=src[:, :, 1, :], in1=sinb, op=ALU.mult), reads=[RPS[bank], Rrt], writes=[Rta])
                S.op("dve", lambda: nc.vector.tensor_tensor(out=dv[:, :, 0, :], in0=ta, in1=tb, op=ALU.subtract), reads=[Rta], writes=[Rqk])
                S.op("dve", lambda: nc.vector.tensor_tensor(out=ta, in0=src[:, :, 0, :], in1=sinb, op=ALU.mult), reads=[RPS[bank], Rrt], writes=[Rta])
                S.op("dve", lambda: nc.vector.tensor_tensor(out=tb, in0=src[:, :, 1, :], in1=cosb, op=ALU.mult), reads=[RPS[bank], Rrt], writes=[Rta])
                S.op("dve", lambda: nc.vector.tensor_tensor(out=dv[:, :, 1, :], in0=ta, in1=tb, op=ALU.add), reads=[Rta], writes=[Rqk])

            def kdec_mul(d, ksrc):
                S.op("dve", lambda: nc.vector.tensor_tensor(out=KDt[:, :].rearrange("p (h e) -> p h e", h=4), in0=ksrc.rearrange("p (h e) -> p h e", h=4),
                                                            in1=KD[:, d, :].unsqueeze(2).broadcast_to([128, 4, 64]), op=ALU.mult), reads=[Rqk, Rtab], writes=[Rkd])

            def umat(bank):
                def emit():
                    inst = None
                    for h in range(4):
                        inst = nc.tensor.matmul(PS[0:64, bank, h * 64:(h + 1) * 64], lhsT=KDt[:, h * 64:(h + 1) * 64], rhs=VTc[:, h * 64:(h + 1) * 64], start=True, stop=True)
                    return inst
                S.op("pe", emit, reads=[Rkd, Rvt], writes=[RPS[bank]])

            def state_update(St, Rst, d, bank):
                S.op("dve", lambda: nc.vector.tensor_tensor(out=St[0:64, :], in0=St[0:64, :], in1=CD[0:64, d, :], op=ALU.mult), reads=[Rst, Rtab], writes=[Rst])
                S.op("dve", lambda: nc.vector.tensor_tensor(out=St[0:64, :], in0=St[0:64, :], in1=PS[0:64, bank, 0:256], op=ALU.add), reads=[Rst, RPS[bank]], writes=[Rst])

            for c in range(n - 1, -1, -1):
                proj(c, 0, WRa, 256, 256)
                proj(c, 1, WRb, 0, 256)
                rope(c, 0, 0, 4, QK[:, 0:256])
                S.op("act", lambda: nc.scalar.copy(out=VTc[:, :], in_=PS[:, 1, 0:256]), reads=[RPS[1]], writes=[Rvt])
                kdec_mul(1, QK[:, 0:256])
                umat(5)
                S.op("act", lambda c=c: nc.scalar.copy(out=SBst[0:64, c, :], in_=SB[0:64, :]), reads=[Rsb], writes=[Rsbst])
                state_update(SB, Rsb, 1, 5)
            S.op("act", lambda: nc.scalar.copy(out=SFb[0:64, :], in_=SF[0:64, :]), reads=[Rsf], writes=[Rsfb])
            if cfg.get("ret_stop") == "pass1":
                return
            for c in range(n):
                proj(c, 0, WRa, 0, 512)
                proj(c, 1, WRb, 0, 512)
                rope(c, 0, 0, 8, QK[:, :])
                S.op("act", lambda: nc.scalar.copy(out=VTc[:, :], in_=PS[:, 1, 0:256]), reads=[RPS[1]], writes=[Rvt])
                S.op("act", lambda: nc.scalar.activation(out=SRG[:, :], in_=PS[:, 1, 256:512], func=AF.Silu), reads=[RPS[1]], writes=[Rsrg])
                kdec_mul(0, QK[:, 256:512])

                def emit_t():
                    inst = None
                    for g in range(8):
                        inst = nc.tensor.transpose(out=PSb(2)[0:64, g * 128:(g + 1) * 128], in_=QK[:, g * 64:(g + 1) * 64], identity=identb[:])
                    return inst
                S.op("pe", emit_t, reads=[Rqk, Rid], writes=[RPS[2]])
                S.op("dve", lambda: nc.vector.tensor_copy(out=QT[0:64, 0, :], in_=PSb(2)[0:64, 0:512]), reads=[RPS[2]], writes=[Rqt])
                for d in range(2):
                    S.op("dve", lambda d=d: nc.vector.tensor_tensor(out=QT[0:64, 1 + d, :], in0=PSb(2)[0:64, 0:512], in1=QD[0:64, d, :, :].rearrange("p h i -> p (h i)"), op=ALU.mult),
                         reads=[RPS[2], Rtab], writes=[Rqt])
                S.op("dve", lambda: nc.vector.tensor_copy(out=KT[0:64, :], in_=PSb(2)[0:64, 512:1024]), reads=[RPS[2]], writes=[Rkt])
                if cfg.get("ret_stop") == "p2a":
                    continue

                def emit_a():
                    inst = None
                    for h in range(4):
                        inst = nc.tensor.matmul(PS[:, 3, h * 128:(h + 1) * 128], lhsT=KT[0:64, h * 128:(h + 1) * 128], rhs=QT[0:64, 0, h * 128:(h + 1) * 128], start=True, stop=True)
                    return inst
                S.op("pe", emit_a, reads=[Rkt, Rqt], writes=[RPS[3]])
                S.op("dve", lambda: nc.vector.tensor_tensor(out=AM[:, :], in0=PS[:, 3, :], in1=DM[:, :, :].rearrange("p h i -> p (h i)"), op=ALU.mult),
                     reads=[RPS[3], Rtab], writes=[Ram])
                if cfg.get("ret_stop") == "p2b":
                    continue

                def emit_o(c=c):
                    inst = None
                    for h in range(4):
                        oc = PS[:, 4, h * 64:(h + 1) * 64]
                        nc.tensor.matmul(oc, lhsT=AM[:, h * 128:(h + 1) * 128], rhs=VTc[:, h * 64:(h + 1) * 64], start=True, stop=False)
                        nc.tensor.matmul(oc, lhsT=QT[0:64, 1, h * 128:(h + 1) * 128], rhs=SFb[0:64, h * 64:(h + 1) * 64], start=False, stop=False)
                        inst = nc.tensor.matmul(oc, lhsT=QT[0:64, 2, h * 128:(h + 1) * 128], rhs=SBst[0:64, c, h * 64:(h + 1) * 64], start=False, stop=True)
                    return inst
                S.op("pe", emit_o, reads=[Ram, Rvt, Rqt, Rsfb, Rsbst], writes=[RPS[4]])
                umat(5)
                state_update(SF, Rsf, 0, 5)
                S.op("act", lambda: nc.scalar.copy(out=SFb[0:64, :], in_=SF[0:64, :]), reads=[Rsf], writes=[Rsfb])
                if cfg.get("ret_stop") == "p2c":
                    continue
                ov = PS[:, 4, 0:256].rearrange("p (h e) -> p h e", h=4)
                S.op("dve", lambda: nc.vector.tensor_reduce(out=MS[:, 0:4], in_=ov, axis=AX.X, op=ALU.add), reads=[RPS[4]], writes=[Rms])
                S.op("dve", lambda: nc.vector.tensor_scalar(out=MS[:, 0:4], in0=MS[:, 0:4], scalar1=-1.0 / 64, scalar2=None, op0=ALU.mult), reads=[Rms], writes=[Rms])
                cv = CEN[:, :].rearrange("p (h e) -> p h e", h=4)
                S.op("dve", lambda: nc.vector.tensor_tensor(out=cv, in0=ov, in1=MS[:, 0:4].unsqueeze(2).broadcast_to([128, 4, 64]), op=ALU.add),
                     reads=[RPS[4], Rms], writes=[Rcen])
                S.op("dve", lambda: nc.vector.tensor_tensor(out=SQr[:, :], in0=CEN[:, :], in1=CEN[:, :], op=ALU.mult), reads=[Rcen], writes=[Rsq])
                S.op("dve", lambda: nc.vector.tensor_reduce(out=MS[:, 4:8], in_=SQr[:, :].rearrange("p (h e) -> p h e", h=4), axis=AX.X, op=ALU.add), reads=[Rsq], writes=[Rms])
                S.op("act", lambda: nc.scalar.activation(out=MS[:, 4:8], in_=MS[:, 4:8], func=AF.Sqrt, scale=1.0 / 64, bias=epsb[:, 0:1]), reads=[Rms, Reps], writes=[Rms])
                S.op("dve", lambda: nc.vector.reciprocal(out=MS[:, 4:8], in_=MS[:, 4:8]), reads=[Rms], writes=[Rms])
                S.op("dve", lambda: nc.vector.tensor_tensor(out=cv, in0=cv, in1=MS[:, 4:8].unsqueeze(2).broadcast_to([128, 4, 64]), op=ALU.mult), reads=[Rcen, Rms], writes=[Rcen])
                S.op("dve", lambda: nc.vector.tensor_tensor(out=NRo[:, :], in0=CEN[:, :], in1=SRG[:, :], op=ALU.mult), reads=[Rcen, Rsrg], writes=[Rnro])

                if cfg.get("ret_stop") == "p2d":
                    continue

                def emit_t2():
                    inst = None
                    for cc in range(2):
                        inst = nc.tensor.transpose(out=PSb(6)[:, cc * 128:(cc + 1) * 128], in_=NRo[:, cc * 128:(cc + 1) * 128], identity=identb[:])
                    return inst
                S.op("pe", emit_t2, reads=[Rnro, Rid], writes=[RPS[6]])
                for cc in range(2):
                    S.op("dve", lambda cc=cc, c=c: nc.vector.tensor_scalar(out=cur["BR"][:, 2 + cc, c * 128:(c + 1) * 128], in0=PSb(6)[:, cc * 128:(cc + 1) * 128],
                                                                          scalar1=PV[:, 8 + cc:9 + cc], scalar2=None, op0=ALU.mult), reads=[RPS[6], Rpv], writes=[Rbr[2 + cc]])
            if not sample:
                pi = 0 if kind == "pA" else 1
                OUT_EVS.append(S.dma("sp", o_ret[pi, l, 0].rearrange("h d e -> d h e"), SF[0:64, :].rearrange("d (h e) -> d h e", h=4), reads=[Rsf], key=Reg("o_ret_d")))
                OUT_EVS.append(S.dma("sp", o_ret[pi, l, 1].rearrange("h d e -> d h e"), SB[0:64, :].rearrange("d (h e) -> d h e", h=4), reads=[Rsb], key=Reg("o_ret_d")))

        def branch_s5(l, t0, T, NB, BW, kind, multi=False):
            sample = kind == "sample"
            o = BS0
            UT, o = view(o, [128, 2, 2048], BF16)
            YS, o = view(o, [128, 2, 2048], BF16)
            YFp, o = view(o, [128, 2048], BF16)
            oZ = o
            TRI, o = view(o, [128, 2, 512])
            TRIb, o = view(o, [128, 2, 512], BF16)
            BZ, o = view(o, [128, 2, 512], BF16)
            oTT = o
            TT, o = view(o, [128, 2, 1024], BF16)
            TTf, _ = view(oTT, [128, 2, 512])
            oS = o
            ow = WIN0
            SBb, ow = view(ow, [128, 2, 512], BF16)
            OTs, ow = view(ow, [128, 512], BF16)
            BW_, ow = view(ow, [128, 2, 2, 128], BF16)
            BBR, ow = view(ow, [128, 16, 16])
            BBI, ow = view(ow, [128, 16, 16])
            CW, ow = view(ow, [128, 8, 2, 32], BF16)
            WGL, ow = view(ow, [128, 2, 256], BF16)
            YV, _ = view(WIN0, [128, 512])
            Rut, Rys, Rsfs, Rtri, Rbz, Rtt, Rsbb, Rots, Rrb, Rbw = (Reg("s_" + x) for x in ("UT", "YS", "YFp", "TRI", "BZ", "TT", "SBb", "OTs", "RB", "BW"))
            def sm_(shape, dt=F32):
                nonlocal o
                v, o = view(o, shape, dt)
                return v
            o1 = [oTT]

            def ot_(shape, dt=F32):
                v, o1[0] = view(o1[0], shape, dt)
                return v
            LRE, LIM, LDT, AR, AI, FR, FI = (ot_([128, 16]) for _ in range(7))
            BRE, BIM = ot_([128, 16, 16]), ot_([128, 16, 16])
            CNAT = ot_([128, 2, 64])
            MAG, UR, UI, W1, W2_, W3 = (sm_([128, 16]) for _ in range(6))
            UBR, UBI = sm_([128, 16]), sm_([128, 16])
            TA_, TBs = sm_([128, 2, 16, 16]), sm_([128, 2, 16, 32])
            PW = sm_([128, 2, 16])
            S0t = sm_([128, 16, 2])
            INI = sm_([128, 2, 2])
            FIN = sm_([128, 2, 16, 2])
            WP = sm_([128, 128])
            Rsu = Reg("s_setup")
            Rini, Rfin, Rwp, Rcn = Reg("s_INI"), Reg("s_FIN"), Reg("s_WP"), Reg("s_CN")
            Rbu = Reg("s_BU")
            Rt4 = [Reg("s_T0"), Reg("s_T1"), Reg("s_P0"), Reg("s_P1")]
            Rbz2 = [Reg("s_BZ0"), Reg("s_BZ1")]
            Rw12 = [Reg("s_W1"), Reg("s_W2")]
            V = nc.vector
            dbgon = cfg.get("s5dbg") == kind
            if dbgon:
                dbg2 = nc.dram_tensor("dbg2", [128, 4096], F32, kind="ExternalOutput").ap()

            def dbg(ap, c0, n, regs):
                if dbgon:
                    S.dma("sp", dbg2[:, c0:c0 + n], ap, reads=regs, key=Reg("dbg2"))

            def dv(fn, reads, writes):
                S.op("dve", fn, reads=reads, writes=writes)

            def tt(out, a, b, op, reads=(Rsu,), writes=(Rsu,)):
                dv(lambda: V.tensor_tensor(out=out, in0=a, in1=b, op=op), list(reads), list(writes))

            def cmul(orr, oi, ar, ai, br, bi, t1, t2, reads=(Rsu,), writes=(Rsu,)):
                tt(t1, ar, br, ALU.mult, reads, writes)
                tt(t2, ai, bi, ALU.mult, reads, writes)
                tt(t2, t1, t2, ALU.subtract, reads, writes)
                tt(t1, ar, bi, ALU.mult, reads, writes)
                tt(oi, ai, br, ALU.mult, reads, writes)
                tt(oi, t1, oi, ALU.add, reads, writes)
                tt(orr, t2, t2, ALU.max, reads, writes)

            for cc in range(2):
                proj_fm(l, cc * 128, 128, NB, BW, lambda b, bank, cc=cc: S.op(
                    "act", lambda: nc.scalar.copy(out=UT[:, cc, b * BW:(b + 1) * BW], in_=PS[:, bank, 0:BW]), reads=[RPS[bank]], writes=[Rut]))
            S.barrier()
            for d in range(2):
                for dst, src in ((LRE, s5_lam_re), (LIM, s5_lam_im)):
                    S.dma("sp", dst[:, d::2], src[l, d].rearrange("(m g) p -> (g p) m", g=2), writes=[Rsu], slow=True)
                for g2 in range(2):
                    S.dma("sp", LDT[g2 * 64:(g2 + 1) * 64, d::2], s5_log_dt[l, d:d + 1, g2::2].broadcast_to([64, 8]), writes=[Rsu], slow=True)
                for dst, src in ((BRE, s5_b_re), (BIM, s5_b_im)):
                    S.dma("sp", dst[:, d::2, :], src[l, d].rearrange("(m g) p h -> (g p) m h", g=2), writes=[Rsu])
                if sample:
                    S.dma("sp", S0t[:, d::2, :], st_s5[l, d].rearrange("(m g) p r -> (g p) m r", g=2), writes=[Rsu], slow=True)
            Rwgl = Reg("s_WGL")
            S.dma("pool", WGL[:, :, :], s5_w_glu[l].rearrange("(kc p) n -> p kc n", p=128), writes=[Rwgl])
            S.op("act", lambda: nc.scalar.activation(out=LDT[:, :], in_=LDT[:, :], func=AF.Exp), reads=[Rsu], writes=[Rsu])
            tt(W1[:, :], LRE[:, :], LDT[:, :], ALU.mult)
            S.op("act", lambda: nc.scalar.activation(out=MAG[:, :], in_=W1[:, :], func=AF.Exp), reads=[Rsu], writes=[Rsu])
            tt(W1[:, :], LIM[:, :], LDT[:, :], ALU.mult)
            S.op("act", lambda: nc.scalar.activation(out=UI[:, :], in_=W1[:, :], func=AF.Sin, scale=1.0 / 64), reads=[Rsu], writes=[Rsu])
            S.op("act", lambda: nc.scalar.activation(out=UR[:, :], in_=W1[:, :], func=AF.Sin, scale=1.0 / 64, bias=halfpi[:, 0:1]), reads=[Rsu, Reps], writes=[Rsu])
            for _ in range(6):
                tt(W1[:, :], UR[:, :], UR[:, :], ALU.mult)
                tt(W2_[:, :], UI[:, :], UI[:, :], ALU.mult)
                tt(W3[:, :], UR[:, :], UI[:, :], ALU.mult)
                tt(UR[:, :], W1[:, :], W2_[:, :], ALU.subtract)
                tt(UI[:, :], W3[:, :], W3[:, :], ALU.add)
            tt(AR[:, :], MAG[:, :], UR[:, :], ALU.mult)
            tt(AI[:, :], MAG[:, :], UI[:, :], ALU.mult)
            tt(W1[:, :], LRE[:, :], LRE[:, :], ALU.mult)
            tt(W2_[:, :], LIM[:, :], LIM[:, :], ALU.mult)
            tt(W1[:, :], W1[:, :], W2_[:, :], ALU.add)
            dv(lambda: V.reciprocal(out=W1[:, :], in_=W1[:, :]), [Rsu], [Rsu])
            dv(lambda: V.tensor_scalar(out=W2_[:, :], in0=AR[:, :], scalar1=-1.0, scalar2=None, op0=ALU.add), [Rsu], [Rsu])
            tt(FR[:, :], W2_[:, :], LRE[:, :], ALU.mult)
            tt(W3[:, :], AI[:, :], LIM[:, :], ALU.mult)
            tt(FR[:, :], FR[:, :], W3[:, :], ALU.add)
            tt(FR[:, :], FR[:, :], W1[:, :], ALU.mult)
            tt(FI[:, :], AI[:, :], LRE[:, :], ALU.mult)
            tt(W3[:, :], W2_[:, :], LIM[:, :], ALU.mult)
            tt(FI[:, :], FI[:, :], W3[:, :], ALU.subtract)
            tt(FI[:, :], FI[:, :], W1[:, :], ALU.mult)
            dbg(MAG[:, :], 0, 16, [Rsu]); dbg(UR[:, :], 16, 16, [Rsu]); dbg(UI[:, :], 32, 16, [Rsu]); dbg(FR[:, :], 48, 16, [Rsu]); dbg(FI[:, :], 64, 16, [Rsu])
            frb = FR[:, :].unsqueeze(2).broadcast_to([128, 16, 16])
            fib = FI[:, :].unsqueeze(2).broadcast_to([128, 16, 16])
            tt(BBR[:, :, :], BRE[:, :, :], frb, ALU.mult)
            tt(BBI[:, :, :], BIM[:, :, :], fib, ALU.mult)
            tt(BBR[:, :, :], BBR[:, :, :], BBI[:, :, :], ALU.subtract)
            tt(BBI[:, :, :], BRE[:, :, :], fib, ALU.mult)
            tt(BRE[:, :, :], BIM[:, :, :], frb, ALU.mult)
            tt(BBI[:, :, :], BBI[:, :, :], BRE[:, :, :], ALU.add)
            dv(lambda: V.memset(CW[:, :, :, :], 0.0), [], [Rsu])
            for ri, src in ((0, s5_c_re), (1, s5_c_im)):
                S.dma("sp", CNAT[:, :, :], src[l].rearrange("(c g) h p -> (g h) c p", c=2), writes=[Rcn])
                CNB = TTf[:, 1, 0:64].bitcast(BF16)
                dv(lambda: V.tensor_copy(out=CNB.rearrange("p (c k) -> p c k", c=2), in_=CNAT[:, :, :]), [Rcn, Rtt], [Rtt])
                for c in range(2):
                    for half in range(2):
                        S.op("pe", lambda c=c, half=half: nc.tensor.matmul(PS[half * 64:(half + 1) * 64, 6, c * 128:(c + 1) * 128], lhsT=CNB[:, c * 64:(c + 1) * 64], rhs=identb[:, :],
                                                                           start=True, stop=True), reads=[Rtt, Rid], writes=[RPS[6]])
                ctv = PS[:, 6, 0:256].rearrange("q (m g h) -> q m g h", m=8, g=2)
                sc = 1.0 if ri == 0 else -1.0
                dv(lambda ri=ri, sc=sc: V.tensor_scalar(out=CW[0:64, :, ri, 0:16], in0=ctv[0:64, :, 0, :], scalar1=sc, scalar2=None, op0=ALU.mult), [RPS[6]], [Rsu])
                dv(lambda ri=ri, sc=sc: V.tensor_scalar(out=CW[64:128, :, ri, 16:32], in0=ctv[64:128, :, 1, :], scalar1=sc, scalar2=None, op0=ALU.mult), [RPS[6]], [Rsu])
            S.barrier()
            def build_pows(TAB, nent, base_r, base_i):
                dv(lambda: V.memset(TAB[:, 0, :, 0:1], 1.0), [], [Rsu])
                dv(lambda: V.memset(TAB[:, 1, :, 0:1], 0.0), [], [Rsu])
                tt(PW[:, 0, :], base_r, base_r, ALU.max)
                tt(PW[:, 1, :], base_i, base_i, ALU.max)
                nn = 1
                while nn < nent:
                    pr = PW[:, 0, :].unsqueeze(2).broadcast_to([128, 16, nn])
                    pi_ = PW[:, 1, :].unsqueeze(2).broadcast_to([128, 16, nn])
                    t1 = TTf[:, 0, 0:16 * nn].rearrange("p (k j) -> p k j", k=16)
                    t2 = TTf[:, 1, 0:16 * nn].rearrange("p (k j) -> p k j", k=16)
                    rr, ri = Rsu, Rtt
                    tt(t1, TAB[:, 0, :, 0:nn], pr, ALU.mult, (rr, ri), (ri,))
                    tt(t2, TAB[:, 1, :, 0:nn], pi_, ALU.mult, (rr, ri), (ri,))
                    tt(TAB[:, 0, :, nn:2 * nn], t1, t2, ALU.subtract, (rr, ri), (rr,))
                    tt(t1, TAB[:, 0, :, 0:nn], pi_, ALU.mult, (rr, ri), (ri,))
                    tt(t2, TAB[:, 1, :, 0:nn], pr, ALU.mult, (rr, ri), (ri,))
                    tt(TAB[:, 1, :, nn:2 * nn], t1, t2, ALU.add, (rr, ri), (rr,))
                    tt(W1[:, :], PW[:, 0, :], PW[:, 0, :], ALU.mult)
                    tt(W2_[:, :], PW[:, 1, :], PW[:, 1, :], ALU.mult)
                    tt(W3[:, :], PW[:, 0, :], PW[:, 1, :], ALU.mult)
                    tt(PW[:, 0, :], W1[:, :], W2_[:, :], ALU.subtract)
                    tt(PW[:, 1, :], W3[:, :], W3[:, :], ALU.add)
                    nn *= 2
            build_pows(TBs, 32, UR[:, :], UI[:, :])
            tt(W1[:, :], PW[:, 0, :], PW[:, 0, :], ALU.max)
            tt(W2_[:, :], PW[:, 1, :], PW[:, 1, :], ALU.max)
            tt(UBR[:, :], PW[:, 0, :], PW[:, 0, :], ALU.max)
            tt(UBI[:, :], PW[:, 1, :], PW[:, 1, :], ALU.max)
            build_pows(TA_, 16, UBR[:, :], UBI[:, :])
            if BW == 512:
                tt(UBR[:, :], PW[:, 0, :], PW[:, 0, :], ALU.max)
                tt(UBI[:, :], PW[:, 1, :], PW[:, 1, :], ALU.max)
            else:
                tt(UBR[:, :], TA_[:, 0, :, 8], TA_[:, 0, :, 8], ALU.max)
                tt(UBI[:, :], TA_[:, 1, :, 8], TA_[:, 1, :, 8], ALU.max)
            S.barrier()
            mcb = [0]
            for m in range(8):
                cc, m4 = m // 4, m % 4
                for d in range(2):
                    k = m * 2 + d
                    for ri, BB in ((0, BBR), (1, BBI)):
                        dv(lambda: V.memset(WP[:, :], 0.0), [Rwp], [Rwp])
                        dv(lambda BB=BB, k=k: V.tensor_copy(out=WP[0:64, m4 * 32:m4 * 32 + 16], in_=BB[0:64, k, :]), [Rsu, Rwp], [Rwp])
                        dv(lambda BB=BB, k=k: V.tensor_copy(out=WP[64:128, m4 * 32 + 16:m4 * 32 + 32], in_=BB[64:128, k, :]), [Rsu, Rwp], [Rwp])
                        S.op("pe", lambda: nc.tensor.transpose(out=PS[:, 7, 0:128], in_=WP[:, :], identity=ident[:]), reads=[Rwp, Rid], writes=[RPS[7]])
                        S.op("act", lambda d=d, ri=ri: nc.scalar.copy(out=BW_[:, d, ri, :], in_=PS[:, 7, 0:128]), reads=[RPS[7]], writes=[Rbw])
                for d in range(2):
                    k = m * 2 + d
                    rev = d == 1
                    nq = BW // 32
                    ar = TA_[:, 0, k, 0:nq].unsqueeze(2).broadcast_to([128, nq, 32])
                    ai = TA_[:, 1, k, 0:nq].unsqueeze(2).broadcast_to([128, nq, 32])
                    br = TBs[:, 0, k, :].unsqueeze(1).broadcast_to([128, nq, 32])
                    bi = TBs[:, 1, k, :].unsqueeze(1).broadcast_to([128, nq, 32])
                    trv = TRI[:, 0, 0:BW].rearrange("p (q j) -> p q j", q=nq)
                    tiv = TRI[:, 1, 0:BW].rearrange("p (q j) -> p q j", q=nq)
                    t1 = TTf[:, 0, 0:BW].rearrange("p (q j) -> p q j", q=nq)
                    t2 = TTf[:, 1, 0:BW].rearrange("p (q j) -> p q j", q=nq)
                    rw = (Rsu, Rtt, Rtri) + tuple(Rt4)
                    tt(t1, ar, br, ALU.mult, rw, (Rtt,) + tuple(Rt4))
                    tt(t2, ai, bi, ALU.mult, rw, (Rtt,) + tuple(Rt4))
                    tt(trv, t1, t2, ALU.subtract, rw, (Rtri,))
                    tt(t1, ar, bi, ALU.mult, rw, (Rtt,) + tuple(Rt4))
                    tt(t2, ai, br, ALU.mult, rw, (Rtt,) + tuple(Rt4))
                    tt(tiv, t1, t2, ALU.add, rw, (Rtri,))
                    dv(lambda: V.tensor_copy(out=TRIb[:, :, 0:BW], in_=TRI[:, :, 0:BW]), [Rtri], [Rtri])
                    if k == 0:
                        dbg(TRI[:, 0, :], 128, 512, [Rtri]); dbg(TRI[:, 1, :], 640, 512, [Rtri])
                    ib = 0
                    if sample:
                        cmul(INI[:, 0, ib:ib + 1], INI[:, 1, ib:ib + 1], UR[:, k:k + 1], UI[:, k:k + 1], S0t[:, k, 0:1], S0t[:, k, 1:2], W1[:, 0:1], W2_[:, 0:1], (Rsu, Rini), (Rsu, Rini))
                    else:
                        dv(lambda: V.memset(INI[:, :, 0:1], 0.0), [Rini], [Rini])
                    blocks = list(range(NB - 1, -1, -1)) if rev else list(range(NB))
                    for bi_, b in enumerate(blocks):
                        cols = slice(b * BW, (b + 1) * BW)
                        bk = 2 * (mcb[0] % 2)
                        mcb[0] += 1
                        for ri in range(2):
                            S.op("pe", lambda ri=ri: nc.tensor.matmul(PS[:, bk + ri, 0:BW], lhsT=BW_[:, d, ri, :], rhs=UT[:, cc, cols], start=True, stop=True),
                                 reads=[Rbw, Rut], writes=[RPS[bk + ri]])
                        if rev:
                            trr, tri = TRI[:, 0, BW - 1::-1] if BW == 512 else TRI[:, 0, BW - 1::-1], TRI[:, 1, BW - 1::-1]
                            trr = TRI[:, 0, 0:BW][:, ::-1]
                            tri = TRI[:, 1, 0:BW][:, ::-1]
                            trrb = TRIb[:, 0, 0:BW][:, ::-1]
                            trib = TRIb[:, 1, 0:BW][:, ::-1]
                        else:
                            trr, tri = TRI[:, 0, 0:BW], TRI[:, 1, 0:BW]
                            trrb, trib = TRIb[:, 0, 0:BW], TRIb[:, 1, 0:BW]
                        for ri in range(2):
                            S.op("act", lambda ri=ri: nc.scalar.copy(out=SBb[:, ri, 0:BW], in_=PS[:, bk + ri, 0:BW]), reads=[RPS[bk + ri]], writes=[Rsbb])
                        pre, pim = SBb[:, 0, 0:BW], SBb[:, 1, 0:BW]
                        T0_, T1_ = TT[:, 0, 0:BW], TT[:, 1, 0:BW]
                        P0_, P1_ = TT[:, 0, 512:512 + BW], TT[:, 1, 512:512 + BW]
                        B0_, B1_ = BZ[:, 0, 0:BW], BZ[:, 1, 0:BW]
                        tt(T0_, pre, trrb, ALU.mult, (Rtri, Rsbb, Rt4[0]), (Rt4[0],))
                        tt(T1_, pim, trib, ALU.mult, (Rtri, Rsbb, Rt4[1]), (Rt4[1],))
                        tt(P0_, pim, trrb, ALU.mult, (Rtri, Rsbb, Rt4[2]), (Rt4[2],))
                        tt(P1_, pre, trib, ALU.mult, (Rtri, Rsbb, Rt4[3]), (Rt4[3],))
                        tt(B0_, T0_, T1_, ALU.add, (Rt4[0], Rt4[1], Rbz2[0]), (Rbz2[0],))
                        tt(B1_, P0_, P1_, ALU.subtract, (Rt4[2], Rt4[3], Rbz2[1]), (Rbz2[1],))
                        for ri in range(2):
                            zo = TT[:, ri, 0:BW]
                            zin = BZ[:, ri, 0:BW]
                            if rev:
                                zo, zin = zo[:, ::-1], zin[:, ::-1]
                            dv(lambda zo=zo, zin=zin, ri=ri: V.tensor_tensor_scan(out=zo, data0=MAG[:, k:k + 1].broadcast_to([128, BW]), data1=zin, initial=INI[:, ri, ib:ib + 1], op0=ALU.mult, op1=ALU.add),
                               [Rsu, Rbz2[ri], Rini, Rt4[ri]], [Rt4[ri]])
                        zl = 0 if rev else BW - 1
                        zr_, zi_ = TT[:, 0, zl:zl + 1], TT[:, 1, zl:zl + 1]
                        dstS = SBb[:, :, 0:BW]
                        Rdst = Rsbb
                        tt(B0_, T0_, trrb, ALU.mult, (Rtri, Rt4[0], Rbz2[0]), (Rbz2[0],))
                        tt(B1_, T1_, trib, ALU.mult, (Rtri, Rt4[1], Rbz2[1]), (Rbz2[1],))
                        tt(P0_, T1_, trrb, ALU.mult, (Rtri, Rt4[1], Rt4[2]), (Rt4[2],))
                        tt(P1_, T0_, trib, ALU.mult, (Rtri, Rt4[0], Rt4[3]), (Rt4[3],))
                        tt(dstS[:, 0, :], B0_, B1_, ALU.subtract, (Rbz2[0], Rbz2[1], Rdst), (Rdst,))
                        tt(dstS[:, 1, :], P0_, P1_, ALU.add, (Rt4[2], Rt4[3], Rdst), (Rdst,))
                        last_blk = bi_ == NB - 1 or multi
                        if multi and bi_ != NB - 1:
                            dv(lambda: V.memset(INI[:, :, 1 - ib:2 - ib], 0.0), [Rini], [Rini])
                        if not last_blk:
                            ib2 = 1 - ib
                            rc, wc = [Rsu, Rini, Rt4[0], Rt4[1]], [Rini]
                            dv(lambda: V.tensor_scalar(out=W1[:, 0:1], in0=zi_, scalar1=UBI[:, k:k + 1], scalar2=None, op0=ALU.mult), [Rsu, Rt4[1], Rw12[0]], [Rw12[0]])
                            dv(lambda: V.tensor_scalar(out=W2_[:, 0:1], in0=zr_, scalar1=UBI[:, k:k + 1], scalar2=None, op0=ALU.mult), [Rsu, Rt4[0], Rw12[1]], [Rw12[1]])
                            dv(lambda: V.scalar_tensor_tensor(out=INI[:, 0, ib2:ib2 + 1], in0=zr_, scalar=UBR[:, k:k + 1], in1=W1[:, 0:1], op0=ALU.mult, op1=ALU.subtract), [Rsu, Rt4[0], Rw12[0], Rini], [Rini])
                            dv(lambda: V.scalar_tensor_tensor(out=INI[:, 1, ib2:ib2 + 1], in0=zi_, scalar=UBR[:, k:k + 1], in1=W2_[:, 0:1], op0=ALU.mult, op1=ALU.add), [Rsu, Rt4[1], Rw12[1], Rini], [Rini])
                            ib = ib2
                        elif not sample:
                            pf = b if multi else 0
                            cmul(FIN[:, pf, k, 0:1], FIN[:, pf, k, 1:2], TRI[:, 0, BW - 1:BW], TRI[:, 1, BW - 1:BW], zr_, zi_, W1[:, 0:1], W2_[:, 0:1], (Rsu, Rtri, Rt4[0], Rt4[1], Rfin), (Rsu, Rfin))
                        if multi and bi_ != NB - 1:
                            ib = 1 - ib
                        def emit_y():
                            nc.tensor.matmul(PS[0:32, 4, 0:BW], lhsT=CW[:, m, 0, :], rhs=SBb[:, 0, 0:BW], start=True, stop=False)
                            return nc.tensor.matmul(PS[0:32, 4, 0:BW], lhsT=CW[:, m, 1, :], rhs=SBb[:, 1, 0:BW], start=False, stop=True)
                        S.op("pe", emit_y, reads=[Rsu, Rsbb], writes=[RPS[4]])
                        if not rev:
                            S.op("act", lambda: nc.scalar.copy(out=YFp[0:32, cols], in_=PS[0:32, 4, 0:BW]), reads=[RPS[4]], writes=[Rsfs])
                        else:
                            tt(OTs[0:32, 0:BW], PS[0:32, 4, 0:BW], YFp[0:32, cols], ALU.add, (RPS[4], Rsfs, Rots), (Rots,))
                            S.dma("sp", YS[m4 * 32:(m4 + 1) * 32, cc, cols], OTs[0:32, 0:BW], reads=[Rots], writes=[Rys], key=Reg("s_OTd"))
            if not sample:
                for pi in ((0, 1) if multi else ((0 if kind == "pA" else 1),)):
                    pf = pi if multi else 0
                    for d in range(2):
                        OUT_EVS.append(S.dma("sp", o_s5[pi, l, d].rearrange("(m g) p r -> (g p) m r", g=2), FIN[:, pf, d::2, :], reads=[Rfin], key=Reg("o_s5_d"), slow=True))
            S.barrier()
            Z, _ = view(oZ, [128, 2, 2048], BF16)
            Rz_ = Reg("s_Z")
            for b in range(NB):
                cols = slice(b * BW, (b + 1) * BW)
                for cc in range(2):
                    yv_ = YV[:, 0:BW]
                    dv(lambda: V.scalar_tensor_tensor(out=yv_, in0=UT[:, cc, cols], scalar=PV[:, 10 + cc:11 + cc], in1=YS[:, cc, cols], op0=ALU.mult, op1=ALU.add),
                       [Rut, Rpv, Rys, Rbz], [Rbz])
                    tt(TTf[:, 0, 0:BW], yv_, yv_, ALU.mult, (Rbz, Rtt), (Rtt,))
                    dv(lambda: V.tensor_scalar(out=TTf[:, 0, 0:BW], in0=TTf[:, 0, 0:BW], scalar1=0.044715, scalar2=1.0, op0=ALU.mult, op1=ALU.add), [Rtt], [Rtt])
                    tt(TTf[:, 0, 0:BW], TTf[:, 0, 0:BW], yv_, ALU.mult, (Rbz, Rtt), (Rtt,))
                    S.op("act", lambda: nc.scalar.activation(out=TTf[:, 1, 0:BW], in_=TTf[:, 0, 0:BW], func=AF.Sigmoid, scale=1.5957691216057308), reads=[Rtt], writes=[Rtt])
                    tt(Z[:, cc, cols], TTf[:, 1, 0:BW], yv_, ALU.mult, (Rbz, Rtt, Rz_), (Rz_,))
                for co in range(2):
                    def emit_g(co=co):
                        nc.tensor.matmul(PS[:, 5, 0:BW], lhsT=WGL[:, 0, co * 128:(co + 1) * 128], rhs=Z[:, 0, cols], start=True, stop=False)
                        return nc.tensor.matmul(PS[:, 5, 0:BW], lhsT=WGL[:, 1, co * 128:(co + 1) * 128], rhs=Z[:, 1, cols], start=False, stop=True)
                    S.op("pe", emit_g, reads=[Rsu, Rwgl, Rz_], writes=[RPS[5]])
                    S.op("act", lambda: nc.scalar.activation(out=OTs[:, 0:BW], in_=PS[:, 5, 0:BW], func=AF.Sigmoid), reads=[RPS[5]], writes=[Rots])
                    tt(cur["BR"][:, co, cols], Z[:, co, cols], OTs[:, 0:BW], ALU.mult, (Rz_, Rots), (Rbr[co],))

        DBG = {}

        def mixer(l):
            make_gate_bcast(1)
            mixer_params(l)

            def build_ht(t0, ntile, ci):
                S.barrier()
                cur["XN"], _ = view(BS0, [128, 2, D])
                for tt in range(ntile):
                    prenorm_tile(t0 + tt, ci, 1, HT[:, :, tt * 128:(tt + 1) * 128], Rht, (4 + 2 * (tt % 2), 5 + 2 * (tt % 2)))
                S.barrier()

            def zero_disabled(T):
                for nm, chs in (("s5", (0, 1)), ("ret", (2, 3)), ("conv", (4, 5)), ("mla", (6, 7))):
                    if not cfg.get(nm, True):
                        for i in chs:
                            S.op("dve", lambda i=i: nc.vector.memset(BR[:, i, 0:T], 0.0), writes=[Rbr[i]])

            def seq_branches(t0, T, NB, BW, kind):
                if cfg.get("conv", True):
                    branch_conv(l, T, NB, BW)
                    S.barrier()
                if cfg.get("mla", True):
                    branch_mla(l, t0, T, NB, BW, kind)
                    S.barrier()
                if cfg.get("ret", True):
                    branch_ret(l, t0, T, kind)
                    S.barrier()

            cur["HT"], cur["BR"] = HT, BR
            build_ht(0, 16, 0)
            zero_disabled(2048)
            seq_branches(0, 2048, 4, 512, "sample")
            if cfg.get("s5", True):
                branch_s5(l, 0, 2048, 4, 512, "sample")
                S.barrier()
            if tuple(cfg.get("dump_br", ())) == (l, "sample"):
                dbg = nc.dram_tensor("dbg_br", [128, 8, 2048], BF16, kind="ExternalOutput").ap()
                S.dma("sp", dbg[:, :, 0:2048], BR[:, :, 0:2048], reads=Rbr, key=Reg("dbg"))
            gate_stage(l, 0, 16, 0, 2048, 4, 512)
            S.barrier()
            build_ht(16, 4, 1)
            zero_disabled(512)
            for pi, kind in ((0, "pA"), (1, "pB")):
                cur["HT"], cur["BR"] = HT[:, :, pi * 256:(pi + 1) * 256], BR[:, :, pi * 256:(pi + 1) * 256]
                mla_cache_out(l, 16 + 2 * pi, 2, pi)
                S.barrier()
                seq_branches(16 + 2 * pi, 256, 1, 256, kind)
            cur["HT"], cur["BR"] = HT, BR
            if cfg.get("s5", True):
                branch_s5(l, 16, 512, 2, 256, "pAB", multi=True)
                S.barrier()
            gate_stage(l, 16, 4, 1, 512, 1, 512)
            S.barrier()
            cur["XN"], cur["TMP"] = XN, TMP

        for l in range(LAYERS):
            S.barrier()
            compute_mod(l)
            S.barrier()
            if cfg.get("ffn1", True):
                ffn(l, 0, 0)
            S.barrier()
            if cfg.get("mixer", True):
                mixer(l)
            S.barrier()
            if cfg.get("ffn2", True):
                ffn(l, 1, 2)

        yv = y.rearrange("(t p) d -> p t d", p=128)
        Ryout = Reg("yout")
        evs = []
        for t in range(NT):
            evs.append(S.dma("sp", yv[:, t, :], X[:, t, :], reads=[RX[t]], key=Ryout))
        S._wait("sp", set([evs[-1]] + OUT_EVS))
        S.barrier()
    return nc


def _axial(T, dim):
    rows = T // 64
    row = np.repeat(np.arange(rows, dtype=np.float32), 64)
    col = np.tile(np.arange(64, dtype=np.float32), rows)
    quarter = dim // 4
    inv = (np.float32(10000.0) ** (-np.arange(quarter, dtype=np.float32) / np.float32(quarter))).astype(np.float32)
    ang = np.concatenate([row[:, None] * inv, col[:, None] * inv], axis=-1).astype(np.float32)
    return np.cos(ang).astype(np.float32), np.sin(ang).astype(np.float32)


def _rope_tables():
    c, s = _axial(2048, 32)
    mla = np.stack([np.concatenate([c.T, c.T], 0), np.concatenate([s.T, s.T], 0)], 0)
    c2, s2 = _axial(2048, 64)
    ret = np.stack([c2, s2], 0)
    return np.ascontiguousarray(mla, dtype=np.float32), np.ascontiguousarray(ret, dtype=np.float32)


def _prep_inputs(inputs):
    f = lambda a: np.ascontiguousarray(np.asarray(a, dtype=np.float32))
    shared = {k: f(inputs[k]) for k in (
        "w_mod", "b_mod", "norm_pre", "norm_post", "ffn_w1", "ffn_w3", "ffn_w2", "w_in", "s5_lam_re", "s5_lam_im", "s5_log_dt",
        "s5_b_re", "s5_b_im", "s5_c_re", "s5_c_im", "s5_d", "s5_w_glu", "ret_decay", "ret_gn", "conv_w", "conv_b",
        "mla_q_norm", "mla_w_uq", "mla_kv_norm", "mla_w_ukv", "w_branch", "w_gate", "b_gate", "w_o")}
    shared["c_ident"] = np.eye(128, dtype=np.float32)
    shared["c_rope_mla"], shared["c_rope_ret"] = _rope_tables()
    jj = np.arange(128, dtype=np.float32)[:, None]
    ii = np.arange(128, dtype=np.float32)[None, :]
    diff = ii - jj
    shared["c_ret"] = np.ascontiguousarray(np.stack([
        np.maximum(diff, 0), np.maximum(-diff, 0), 0.125 * (diff >= 0), 0.125 * (diff < 0),
        np.broadcast_to(ii + 1.0, (128, 128)), np.broadcast_to(128.0 - ii, (128, 128))], 0), dtype=np.float32)
    shared["c_pidx"] = np.ascontiguousarray(np.concatenate([127.0 - jj, jj], 1), dtype=np.float32)
    xp = f(inputs["x_prompt"])
    xs = f(inputs["x_sample"])
    c = f(inputs["c"])
    cctx = f(inputs["c_ctx"])
    maps = []
    for i in range(NCORES):
        m = dict(shared)
        m["xin"] = np.ascontiguousarray(np.concatenate([xs[i], xp[2 * i], xp[2 * i + 1]], axis=0))
        m["cond2"] = np.ascontiguousarray(np.stack([c[i], cctx], axis=0))
        m["st_s5"] = f(inputs["state_s5"][i])
        m["st_ret"] = f(inputs["state_ret"][i])
        m["ctx_mla"] = f(inputs["cache_mla"][i])
        maps.append(m)
    return maps


def _gather(res):
    ys = np.stack([r["y"][:2048] for r in res], axis=0)
    yp = np.stack([r["y"][2048 + 256 * j:2048 + 256 * (j + 1)] for r in res for j in range(2)], axis=0)
    s5 = np.concatenate([r["o_s5"] for r in res], axis=0)
    ret = np.concatenate([r["o_ret"] for r in res], axis=0)
    mla = np.concatenate([r["o_mla"] for r in res], axis=0)
    return (yp.astype(np.float32), ys.astype(np.float32), s5.astype(np.float32), ret.astype(np.float32), mla.astype(np.float32))


CFG = {}


def kernel(**inputs):
    nc = build(CFG)
    maps = _prep_inputs(inputs)
    res = run_bass_kernel_spmd(nc, maps, core_ids=list(range(NCORES)))
    return _gather(res.results)
```
